# Optimizing a Trainium2 kernel written in Bass

```python
import math
import jax, jax.numpy as jnp
from jax import lax
import numpy as np

D_MODEL = 1024
BATCH = 4
SEQ = 4096
DEPTH = 2

MLA_HEADS = 4
MLA_NOPE = 128
MLA_ROPE = 64
MLA_VDIM = 128
MLA_Q_LORA = 256
MLA_KV_LORA = 256
MLA_WIDTH = MLA_HEADS * MLA_VDIM
ROPE_THETA = 10000.0
Q_BLOCK = 128

RWKV_HEADS = 4
RWKV_HEAD = 64
RWKV_WIDTH = RWKV_HEADS * RWKV_HEAD
RWKV_DECAY_LORA = 32
RWKV_AAA_LORA = 32
RWKV_GATE_LORA = 64
RWKV_LN_EPS = 64e-5

MLSTM_HEADS = 4
MLSTM_QK = 32
MLSTM_V = 64
MLSTM_WIDTH = MLSTM_HEADS * MLSTM_V
MLSTM_CHUNK = 64
MLSTM_CONV = 4

D_MIX = MLA_WIDTH + RWKV_WIDTH + MLSTM_WIDTH

D_FF = -(-8 * D_MODEL // (3 * 256)) * 256
NORM_EPS = 1e-6

MLA_SPLITS = (MLA_Q_LORA, MLA_KV_LORA, MLA_ROPE)
RWKV_SPLITS = (RWKV_WIDTH, RWKV_WIDTH, RWKV_WIDTH, RWKV_DECAY_LORA, RWKV_AAA_LORA, RWKV_GATE_LORA)
MLSTM_SPLITS = (MLSTM_HEADS * MLSTM_QK, MLSTM_HEADS * MLSTM_QK, MLSTM_WIDTH, MLSTM_HEADS, MLSTM_HEADS, MLSTM_WIDTH)
GROUP_SPLITS = (sum(MLA_SPLITS), sum(RWKV_SPLITS), sum(MLSTM_SPLITS))
P_IN = sum(GROUP_SPLITS)

kernel_name = "hybrid_mla_rwkv7_mlstm_block"


def _split(x, sizes):
    offs = np.cumsum(np.array(sizes))[:-1]
    return jnp.split(x, [int(o) for o in offs], axis=-1)


def _rmsnorm(x, g):
    xf = x.astype(jnp.float32)
    y = xf * lax.rsqrt(jnp.mean(xf * xf, axis=-1, keepdims=True) + NORM_EPS)
    return (y * g.astype(jnp.float32)).astype(x.dtype)


def _head_rmsnorm(x, g):
    B, S, H, dh = x.shape
    xf = x.astype(jnp.float32)
    y = xf * lax.rsqrt(jnp.mean(xf * xf, axis=-1, keepdims=True) + NORM_EPS)
    return y.reshape(B, S, H * dh) * g.astype(jnp.float32)


def _rope_tables(positions, dim):
    inv_freq = ROPE_THETA ** (-jnp.arange(0, dim, 2, dtype=jnp.float32) / dim)
    ang = positions.astype(jnp.float32)[..., None] * inv_freq
    return jnp.cos(ang), jnp.sin(ang)


def _rope(x, cos, sin):
    xf = x.astype(jnp.float32)
    x1, x2 = jnp.split(xf, 2, axis=-1)
    return jnp.concatenate([x1 * cos - x2 * sin, x1 * sin + x2 * cos], axis=-1).astype(x.dtype)


def _causal_conv(x, w, b):
    K = w.shape[0]
    S = x.shape[1]
    xp = jnp.pad(x, ((0, 0), (K - 1, 0), (0, 0)))
    out = b
    for j in range(K):
        out = out + xp[:, j:j + S] * w[j]
    return out


def _mla(c_q, c_kv, k_pe, positions, q_norm, w_uq, kv_norm, w_ukv, out_norm):
    B, S, _ = c_q.shape
    H = MLA_HEADS
    q = (_rmsnorm(c_q, q_norm) @ w_uq).reshape(B, S, H, MLA_NOPE + MLA_ROPE)
    q_nope, q_pe = q[..., :MLA_NOPE], q[..., MLA_NOPE:]
    kv = (_rmsnorm(c_kv, kv_norm) @ w_ukv).reshape(B, S, H, MLA_NOPE + MLA_VDIM)
    k_nope, v = kv[..., :MLA_NOPE], kv[..., MLA_NOPE:]
    cos, sin = _rope_tables(positions, MLA_ROPE)
    q_pe = _rope(q_pe, cos[:, :, None, :], sin[:, :, None, :])
    k_pe = _rope(k_pe, cos, sin)
    scale = (MLA_NOPE + MLA_ROPE) ** -0.5
    nb = S // Q_BLOCK
    qn_b = jnp.moveaxis(q_nope.reshape(B, nb, Q_BLOCK, H, MLA_NOPE), 1, 0)
    qp_b = jnp.moveaxis(q_pe.reshape(B, nb, Q_BLOCK, H, MLA_ROPE), 1, 0)
    key_pos = jnp.arange(S)

    def block(args):
        qn, qp, i = args
        s = (jnp.einsum('bqhd,bkhd->bhqk', qn, k_nope)
             + jnp.einsum('bqhr,bkr->bhqk', qp, k_pe)).astype(jnp.float32) * scale
        q_idx = i * Q_BLOCK + jnp.arange(Q_BLOCK)
        s = jnp.where(key_pos[None, :] <= q_idx[:, None], s, -jnp.inf)
        p = jax.nn.softmax(s, axis=-1).astype(v.dtype)
        return jnp.einsum('bhqk,bkhd->bqhd', p, v)

    o = lax.map(block, (qn_b, qp_b, jnp.arange(nb)))
    o = jnp.moveaxis(o, 0, 1).reshape(B, S, H, MLA_VDIM)
    return _head_rmsnorm(o, out_norm)


def _rwkv7(p, mu, w0, w2, a0, a2, g2, k_k, k_a, r_k, ln_w, ln_b):
    B, S, _ = p.shape
    H, N = RWKV_HEADS, RWKV_HEAD
    prev = jnp.pad(p, ((0, 0), (1, 0), (0, 0)))[:, :-1]
    p = p + (prev - p) * mu
    r, k, v, xw, xa, xg = _split(p, RWKV_SPLITS)
    w_log = -jax.nn.softplus(-(w0 + jnp.tanh(xw) @ w2).astype(jnp.float32)) - 0.5
    decay = jnp.exp(-jnp.exp(w_log))
    a = jax.nn.sigmoid((a0 + xa @ a2).astype(jnp.float32))
    g = jax.nn.sigmoid(xg) @ g2
    r = r.astype(jnp.float32).reshape(B, S, H, N)
    k = k.astype(jnp.float32)
    v = v.astype(jnp.float32).reshape(B, S, H, N)
    kk = (k * k_k.astype(jnp.float32)).reshape(B, S, H, N)
    kk = kk / jnp.maximum(jnp.sqrt(jnp.sum(kk * kk, axis=-1, keepdims=True)), 1e-12)
    k = (k * (1.0 + (a - 1.0) * k_a.astype(jnp.float32))).reshape(B, S, H, N)
    a = a.reshape(B, S, H, N)
    decay = decay.reshape(B, S, H, N)

    def step(state, inp):
        r_t, w_t, k_t, v_t, kk_t, a_t = inp
        sa = jnp.einsum('bhij,bhj->bhi', state, -kk_t)
        state = (state * w_t[:, :, None, :]
                 + sa[..., :, None] * (kk_t * a_t)[..., None, :]
                 + v_t[..., :, None] * k_t[..., None, :])
        return state, jnp.einsum('bhij,bhj->bhi', state, r_t)

    xs = tuple(jnp.moveaxis(t, 1, 0) for t in (r, decay, k, v, kk, a))
    _, y = lax.scan(step, jnp.zeros((B, H, N, N), jnp.float32), xs)
    y = jnp.moveaxis(y, 0, 1)
    mean = jnp.mean(y, axis=-1, keepdims=True)
    var = jnp.mean(jnp.square(y - mean), axis=-1, keepdims=True)
    y = ((y - mean) * lax.rsqrt(var + RWKV_LN_EPS)).reshape(B, S, RWKV_WIDTH)
    y = y * ln_w.astype(jnp.float32) + ln_b.astype(jnp.float32)
    bonus = jnp.sum(r * k * r_k.astype(jnp.float32).reshape(H, N), axis=-1, keepdims=True) * v
    y = y + bonus.reshape(B, S, RWKV_WIDTH)
    return y * g.astype(jnp.float32)


def _mlstm(p, conv_w, conv_b, i_bias, f_bias, out_norm):
    B, S, _ = p.shape
    H, DK, DV, L = MLSTM_HEADS, MLSTM_QK, MLSTM_V, MLSTM_CHUNK
    NC = S // L
    q, k, v, i_pre, f_pre, o_pre = _split(p, MLSTM_SPLITS)
    qk = jax.nn.silu(_causal_conv(jnp.concatenate([q, k], axis=-1), conv_w, conv_b))
    q, k = jnp.split(qk, 2, axis=-1)

    def heads(t, d):
        return t.astype(jnp.float32).reshape(B, NC, L, H, d).transpose(0, 3, 1, 2, 4)

    q = heads(q, DK) * DK ** -0.5
    k = heads(k, DK)
    v = heads(v, DV)
    log_i = (i_pre + i_bias).astype(jnp.float32).reshape(B, NC, L, H).transpose(0, 3, 1, 2)
    log_f = jax.nn.log_sigmoid((f_pre + f_bias).astype(jnp.float32)).reshape(B, NC, L, H).transpose(0, 3, 1, 2)
    g = jnp.cumsum(log_f, axis=-1)
    g_last = g[..., -1]
    a = g_last[..., None] - g + log_i

    def chunk_step(carry, inp):
        C, n, m = carry
        k_c, v_c, a_c, gl_c = inp
        m_new = jnp.maximum(gl_c + m, jnp.max(a_c, axis=-1))
        dec = jnp.exp(gl_c + m - m_new)
        wts = jnp.exp(a_c - m_new[..., None])
        C_new = dec[..., None, None] * C + jnp.einsum('bhl,bhld,bhle->bhde', wts, k_c, v_c)
        n_new = dec[..., None] * n + jnp.einsum('bhl,bhld->bhd', wts, k_c)
        return (C_new, n_new, m_new), (C, n, m)

    init = (jnp.zeros((B, H, DK, DV), jnp.float32), jnp.zeros((B, H, DK), jnp.float32),
            jnp.zeros((B, H), jnp.float32))
    xs = (jnp.moveaxis(k, 2, 0), jnp.moveaxis(v, 2, 0), jnp.moveaxis(a, 2, 0), jnp.moveaxis(g_last, 2, 0))
    _, (C_prev, n_prev, m_prev) = lax.scan(chunk_step, init, xs)
    C_prev = jnp.moveaxis(C_prev, 0, 2)
    n_prev = jnp.moveaxis(n_prev, 0, 2)
    m_prev = jnp.moveaxis(m_prev, 0, 2)

    causal = jnp.tril(jnp.ones((L, L), dtype=bool))
    D = jnp.where(causal, g[..., :, None] - g[..., None, :] + log_i[..., None, :], -jnp.inf)
    inter_log = g + m_prev[..., None]
    m_t = jnp.maximum(inter_log, jnp.max(D, axis=-1))
    inter_w = jnp.exp(inter_log - m_t)
    s = jnp.einsum('bhctd,bhcjd->bhctj', q, k) * jnp.exp(D - m_t[..., None])
    num = (inter_w[..., None] * jnp.einsum('bhctd,bhcde->bhcte', q, C_prev)
           + jnp.einsum('bhctj,bhcje->bhcte', s, v))
    den = inter_w * jnp.einsum('bhctd,bhcd->bhct', q, n_prev) + jnp.sum(s, axis=-1)
    den = jnp.maximum(jnp.abs(den), jnp.exp(-m_t))
    h = (num / den[..., None]).transpose(0, 2, 3, 1, 4).reshape(B, S, H, DV)
    return _head_rmsnorm(h, out_norm) * jax.nn.sigmoid(o_pre.astype(jnp.float32))


def setup_inputs(seed: int = 0) -> dict:
    key = jax.random.key(seed)
    ks = jax.random.split(key, 32)
    f32 = jnp.float32

    def nrm(k, shape, scale):
        return jax.random.normal(k, shape, f32) * scale

    def gain(k, shape):
        return 1.0 + 0.02 * jax.random.normal(k, shape, f32)

    offset = jax.random.randint(ks[1], (BATCH, 1), 0, 2048, dtype=jnp.int32)
    positions = offset + jnp.arange(SEQ, dtype=jnp.int32)[None, :]
    w0 = jnp.linspace(-6.0, -1.0, RWKV_WIDTH, dtype=f32) + 0.5
    f_b = jnp.linspace(3.0, 6.0, MLSTM_HEADS, dtype=f32)
    return {
        "x": nrm(ks[0], (BATCH, SEQ, D_MODEL), 1.0),
        "positions": positions,
        "mix_norm": gain(ks[2], (DEPTH, D_MODEL)),
        "w_in": nrm(ks[3], (DEPTH, D_MODEL, P_IN), D_MODEL ** -0.5),
        "mla_q_norm": gain(ks[4], (DEPTH, MLA_Q_LORA)),
        "mla_w_uq": nrm(ks[5], (DEPTH, MLA_Q_LORA, MLA_HEADS * (MLA_NOPE + MLA_ROPE)), MLA_Q_LORA ** -0.5),
        "mla_kv_norm": gain(ks[6], (DEPTH, MLA_KV_LORA)),
        "mla_w_ukv": nrm(ks[7], (DEPTH, MLA_KV_LORA, MLA_HEADS * (MLA_NOPE + MLA_VDIM)), MLA_KV_LORA ** -0.5),
        "mla_out_norm": gain(ks[8], (DEPTH, MLA_WIDTH)),
        "rwkv_mu": jax.random.uniform(ks[9], (DEPTH, GROUP_SPLITS[1]), f32),
        "rwkv_w0": w0[None, :] + nrm(ks[10], (DEPTH, RWKV_WIDTH), 0.1),
        "rwkv_w2": nrm(ks[11], (DEPTH, RWKV_DECAY_LORA, RWKV_WIDTH), 0.1 * RWKV_DECAY_LORA ** -0.5),
        "rwkv_a0": nrm(ks[12], (DEPTH, RWKV_WIDTH), 0.1),
        "rwkv_a2": nrm(ks[13], (DEPTH, RWKV_AAA_LORA, RWKV_WIDTH), RWKV_AAA_LORA ** -0.5),
        "rwkv_g2": nrm(ks[14], (DEPTH, RWKV_GATE_LORA, RWKV_WIDTH), RWKV_GATE_LORA ** -0.5),
        "rwkv_k_k": 0.85 + nrm(ks[15], (DEPTH, RWKV_WIDTH), 0.02),
        "rwkv_k_a": gain(ks[16], (DEPTH, RWKV_WIDTH)),
        "rwkv_r_k": nrm(ks[17], (DEPTH, RWKV_WIDTH), 0.1),
        "rwkv_ln_w": gain(ks[18], (DEPTH, RWKV_WIDTH)),
        "rwkv_ln_b": nrm(ks[19], (DEPTH, RWKV_WIDTH), 0.02),
        "mlstm_conv_w": nrm(ks[20], (DEPTH, MLSTM_CONV, 2 * MLSTM_HEADS * MLSTM_QK), MLSTM_CONV ** -0.5),
        "mlstm_conv_b": nrm(ks[21], (DEPTH, 2 * MLSTM_HEADS * MLSTM_QK), 0.02),
        "mlstm_i_bias": -1.0 + nrm(ks[22], (DEPTH, MLSTM_HEADS), 0.1),
        "mlstm_f_bias": f_b[None, :] + nrm(ks[23], (DEPTH, MLSTM_HEADS), 0.1),
        "mlstm_out_norm": gain(ks[24], (DEPTH, MLSTM_WIDTH)),
        "w_out": nrm(ks[25], (DEPTH, D_MIX, D_MODEL), D_MIX ** -0.5),
        "ffn_norm": gain(ks[26], (DEPTH, D_MODEL)),
        "w_gate": nrm(ks[27], (DEPTH, D_MODEL, D_FF), D_MODEL ** -0.5),
        "w_up": nrm(ks[28], (DEPTH, D_MODEL, D_FF), D_MODEL ** -0.5),
        "w_down": nrm(ks[29], (DEPTH, D_FF, D_MODEL), D_FF ** -0.5),
        "final_norm": gain(ks[30], (D_MODEL,)),
    }


def reference(x, positions, mix_norm, w_in, mla_q_norm, mla_w_uq, mla_kv_norm, mla_w_ukv, mla_out_norm,
              rwkv_mu, rwkv_w0, rwkv_w2, rwkv_a0, rwkv_a2, rwkv_g2, rwkv_k_k, rwkv_k_a, rwkv_r_k,
              rwkv_ln_w, rwkv_ln_b, mlstm_conv_w, mlstm_conv_b, mlstm_i_bias, mlstm_f_bias, mlstm_out_norm,
              w_out, ffn_norm, w_gate, w_up, w_down, final_norm):
    for l in range(DEPTH):
        h = _rmsnorm(x, mix_norm[l])
        p = h @ w_in[l]
        p_mla, p_rwkv, p_mlstm = _split(p, GROUP_SPLITS)
        c_q, c_kv, k_pe = _split(p_mla, MLA_SPLITS)
        y_mla = _mla(c_q, c_kv, k_pe, positions, mla_q_norm[l], mla_w_uq[l], mla_kv_norm[l],
                     mla_w_ukv[l], mla_out_norm[l])
        y_rwkv = _rwkv7(p_rwkv, rwkv_mu[l], rwkv_w0[l], rwkv_w2[l], rwkv_a0[l], rwkv_a2[l], rwkv_g2[l],
                        rwkv_k_k[l], rwkv_k_a[l], rwkv_r_k[l], rwkv_ln_w[l], rwkv_ln_b[l])
        y_mlstm = _mlstm(p_mlstm, mlstm_conv_w[l], mlstm_conv_b[l], mlstm_i_bias[l], mlstm_f_bias[l],
                         mlstm_out_norm[l])
        y = jnp.concatenate([y_mla.astype(x.dtype), y_rwkv.astype(x.dtype), y_mlstm.astype(x.dtype)], axis=-1)
        x = x + y @ w_out[l]
        h = _rmsnorm(x, ffn_norm[l])
        x = x + (jax.nn.silu(h @ w_gate[l]) * (h @ w_up[l])) @ w_down[l]
    return _rmsnorm(x, final_norm)
```

```python
from contextlib import ExitStack
import numpy as np
import concourse.bass as bass
import concourse.mybir as mybir
from concourse.bass_utils import run_bass_kernel_spmd

F32 = mybir.dt.float32
BF16 = mybir.dt.bfloat16
I32 = mybir.dt.int32
ALU = mybir.AluOpType
AF = mybir.ActivationFunctionType

D = 1024
SEQ = 4096
NTOK = 2048
DFF = 2816
NCORES = 8
EPS = 1e-6

NCH = 13
CH_ROWS = [128] * 12 + [4]
HALF_ROWS = 12 * 128 + 4
CH_OFF = [i * 128 for i in range(13)]
NCOLA = 2 * HALF_ROWS


def half_cols(hc):
    cols = []
    cols += list(range(0, 256))
    cols += list(range(256, 512))
    cols += list(range(512, 576))
    cols += list(range(544, 576)) + list(range(512, 544))
    R0 = 576
    for part in range(3):
        cols += list(range(R0 + part * 256 + hc * 128, R0 + part * 256 + hc * 128 + 128))
    cols += list(range(R0 + 768, R0 + 896))
    M0 = 576 + 896
    cols += list(range(M0 + hc * 64, M0 + hc * 64 + 64))
    cols += list(range(M0 + 128 + hc * 64, M0 + 128 + hc * 64 + 64))
    cols += list(range(M0 + 256 + hc * 128, M0 + 256 + hc * 128 + 128))
    cols += list(range(M0 + 520 + hc * 128, M0 + 520 + hc * 128 + 128))
    cols += list(range(M0 + 512 + hc * 2, M0 + 512 + hc * 2 + 2))
    cols += list(range(M0 + 516 + hc * 2, M0 + 516 + hc * 2 + 2))
    assert len(cols) == HALF_ROWS
    return cols


class Res:
    __slots__ = ("w", "r", "name")

    def __init__(self, name=""):
        self.w = None
        self.r = {}
        self.name = name


class Bld:
    NDMA = 8

    def __init__(self, nc, tag="", shared=None):
        self.nc = nc
        self.E = {"pe": nc.tensor, "dve": nc.vector, "act": nc.scalar, "pool": nc.gpsimd, "sp": nc.sync}
        self.sems = {}
        self.cnt = {}
        self.seen = {e: {} for e in self.E}
        self.touched = set()
        for e in self.E:
            self.sems[e] = nc.alloc_semaphore(name=f"s{tag}_{e}")
            self.cnt[e] = 0
        if shared is not None and "sems" in shared:
            self.sems.update(shared["sems"])
            self.cnt.update(shared["cnt"])
            self.dslot = shared["dslot"]
        else:
            dsems, dcnt = {}, {}
            self.dslot = {}
            for q in ("sp", "act", "pool"):
                for i in range(self.NDMA):
                    k = f"d{q}{i}"
                    dsems[k] = nc.alloc_semaphore(name=f"sdma_{k}")
                    dcnt[k] = 0
                self.dslot[q] = 0
            self.sems.update(dsems)
            self.cnt.update(dcnt)
            if shared is not None:
                shared["sems"] = dsems
                shared["dslot"] = self.dslot
                shared["cnt"] = {}
        self.shared = shared

    def _sync_shared(self):
        if self.shared is not None:
            for k in self.shared["sems"]:
                self.shared["cnt"][k] = self.cnt[k]

    def _wait(self, eng, deps):
        best = {}
        for k, v in deps:
            if v > best.get(k, 0):
                best[k] = v
        for k, v in best.items():
            if self.seen[eng].get(k, 0) >= v:
                continue
            self.E[eng].wait_ge(self.sems[k], v)
            self.seen[eng][k] = v

    def _deps(self, eng, reads, writes):
        deps = []
        for r in reads:
            if r.w is not None:
                if not (eng == "pe" and r.w[0] == "pe"):
                    deps.append(r.w)
        for w in writes:
            if w.w is not None and w.w[0] != eng:
                deps.append(w.w)
            for k, v in w.r.items():
                if k != eng:
                    deps.append((k, v))
        return deps

    def _mark(self, ev, reads, writes):
        self.touched.update(reads)
        self.touched.update(writes)
        for r in reads:
            if ev[1] > r.r.get(ev[0], 0):
                r.r[ev[0]] = ev[1]
        for w in writes:
            w.w = ev
            w.r = {}

    def op(self, eng, fn, reads=(), writes=()):
        self._wait(eng, self._deps(eng, reads, writes))
        ins = fn(self.E[eng])
        self.cnt[eng] += 1
        ins.then_inc(self.sems[eng], 1)
        self._mark((eng, self.cnt[eng]), reads, writes)
        return ins

    def dma(self, q, out, in_, reads=(), writes=()):
        i = self.dslot[q]
        self.dslot[q] = (i + 1) % self.NDMA
        k = f"d{q}{i}"
        deps = self._deps(k, reads, writes)
        deps.append((k, self.cnt[k]))
        self._wait(q, deps)
        ins = self.E[q].dma_start(out=out, in_=in_)
        self.cnt[k] += 16
        ins.then_inc(self.sems[k], 16)
        self._mark((k, self.cnt[k]), reads, writes)
        return ins

    def barrier(self):
        deps = [(k, v) for k, v in self.cnt.items() if v > 0]
        for e in ("sp", "pe", "dve", "act", "pool"):
            self._wait(e, deps)
        for e in ("sp", "pe", "dve", "act", "pool"):
            for k, v in deps:
                assert self.seen[e].get(k, 0) >= v
        for r in self.touched:
            r.w = None
            r.r = {}
        self.touched = set()
        self._sync_shared()

    def wait_all(self, eng, ress):
        deps = []
        for r in ress:
            if r.w is not None:
                deps.append(r.w)
        self._wait(eng, deps)


def dram_ap(t, offset, pattern):
    return bass.AP(t, offset, [list(p) for p in pattern])


def emit_rmsnorm_T(bd, x_tile, x_res, hT, hT_res, j, tmp, ident):
    ss, ss_r = tmp["ss"], tmp["ss_r"]
    junk, junk_r = tmp["junk"], tmp["junk_r"]
    xn, xn_r = tmp["xn"], tmp["xn_r"]
    pt, pt_r = tmp["pt"], tmp["pt_r"]
    bd.op("dve", lambda e: e.scalar_tensor_tensor(out=junk[:], in0=x_tile, scalar=1.0, in1=x_tile,
                                                  op0=ALU.mult, op1=ALU.mult, accum_out=ss[:, 0:1]),
          reads=[x_res], writes=[junk_r, ss_r])
    bd.op("act", lambda e: e.activation(out=ss[:, 1:2], in_=ss[:, 0:1], func=AF.Sqrt, scale=1.0 / D,
                                        bias=tmp["eps"][:, 0:1]), reads=[ss_r, tmp["eps_r"]], writes=[ss_r])
    bd.op("dve", lambda e: e.reciprocal(out=ss[:, 2:3], in_=ss[:, 1:2]), reads=[ss_r], writes=[ss_r])
    bd.op("act", lambda e: e.activation(out=xn[:], in_=x_tile, func=AF.Copy, scale=ss[:, 2:3]),
          reads=[x_res, ss_r], writes=[xn_r])
    for kc in range(8):
        bd.op("pe", lambda e, kc=kc: e.transpose(out=pt[:, kc * 128:(kc + 1) * 128],
                                                 in_=xn[:, kc * 128:(kc + 1) * 128], identity=ident[:]),
              reads=[xn_r], writes=[pt_r])
    bd.op("act", lambda e: e.activation(out=hT[:, :, j * 128:(j + 1) * 128],
                                        in_=pt[:].rearrange("p (k t) -> p k t", k=8), func=AF.Copy),
          reads=[pt_r], writes=[hT_res])


_UC = [0]


def _u(name):
    _UC[0] += 1
    return f"{name}_{_UC[0]}"


class View:
    def __init__(self, ap):
        self._ap = ap

    def ap(self):
        return self._ap


def emit_phaseA(nc, bd, es, x_d, w_d, g_d, id_d, pT_d, ntok):
    sb = lambda name, shape, dt: es.enter_context(nc.sbuf_tensor(_u(name), shape, dt))
    ps = lambda name, shape, dt: es.enter_context(nc.psum_tensor(_u(name), shape, dt))
    wb = sb("A_wb", [128, 8, NCOLA], BF16)
    wb_r = [Res() for _ in range(8)]
    stage = [sb(f"A_stage{i}", [128, NCOLA], F32) for i in range(2)]
    stage_r = [Res(), Res()]
    gcol = sb("A_gcol", [128, 8], F32)
    gcol_r = Res()
    idf = sb("A_idf", [128, 128], F32)
    ident = sb("A_ident", [128, 128], BF16)
    ident_r = Res()
    idf_r = Res()
    xt = [sb(f"A_xt{i}", [128, D], F32) for i in range(2)]
    xt_r = [Res(), Res()]
    hT = [sb(f"A_hT{i}", [128, 8, 512], BF16) for i in range(2)]
    hT_r = [Res(), Res()]
    tmp = dict(ss=sb("A_ss", [128, 4], F32), ss_r=Res(), junk=sb("A_junk", [128, D], BF16), junk_r=Res(),
               xn=sb("A_xn", [128, D], BF16), xn_r=Res(),
               pt=ps("A_pt", [128, D], BF16), pt_r=Res(), eps=sb("A_eps", [128, 1], F32), eps_r=Res())
    bd.op("dve", lambda e: e.memset(tmp["eps"][:], EPS), writes=[tmp["eps_r"]])
    NPB = 4
    pb = [ps(f"A_pb{i}", [128, 512], F32) for i in range(NPB)]
    pb_r = [Res() for _ in range(NPB)]
    ost = [sb(f"A_ost{i}", [128, 512], F32) for i in range(4)]
    ost_r = [Res() for _ in range(4)]

    bd.dma("sp", gcol[:], g_d.ap(), writes=[gcol_r])
    bd.dma("sp", idf[:], id_d.ap(), writes=[idf_r])
    bd.op("dve", lambda e: e.tensor_copy(out=ident[:], in_=idf[:]), reads=[idf_r], writes=[ident_r])
    for kc in range(8):
        s = kc % 2
        bd.dma("pool", stage[s][:], w_d.ap()[kc * 128:(kc + 1) * 128, :], writes=[stage_r[s]])
        bd.op("dve", lambda e, kc=kc, s=s: e.tensor_scalar(out=wb[:, kc, :], in0=stage[s][:],
                                                          scalar1=gcol[:, kc:kc + 1], scalar2=None, op0=ALU.mult),
              reads=[stage_r[s], gcol_r], writes=[wb_r[kc]])
    x_ap = x_d.ap()
    pT_ap = pT_d.ap()
    nblk = ntok // 512
    oi = 0
    for blk in range(nblk):
        hb = blk % 2
        for j in range(4):
            ti = blk * 4 + j
            xb = ti % 2
            bd.dma("sp", xt[xb][:], x_ap[ti * 128:(ti + 1) * 128, :], writes=[xt_r[xb]])
            tmp2 = dict(tmp)
            emit_rmsnorm_T(bd, xt[xb][:], xt_r[xb], hT[hb], hT_r[hb], j, tmp2, ident)
        for half in range(2):
            for c in range(NCH):
                m = CH_ROWS[c]
                col0 = half * HALF_ROWS + CH_OFF[c]
                pbi = oi % NPB
                for kc in range(8):
                    bd.op("pe", lambda e, kc=kc, col0=col0, m=m, pbi=pbi, hb=hb: e.matmul(
                        pb[pbi][0:m, :], lhsT=wb[:, kc, col0:col0 + m], rhs=hT[hb][:, kc, :],
                        start=(kc == 0), stop=(kc == 7)),
                        reads=[wb_r[kc], hT_r[hb]], writes=[pb_r[pbi]])
                osi = oi % 4
                eng = "act" if oi % 2 == 0 else "dve"
                if eng == "act":
                    bd.op("act", lambda e, m=m, pbi=pbi, osi=osi: e.activation(
                        out=ost[osi][0:m, :], in_=pb[pbi][0:m, :], func=AF.Copy),
                        reads=[pb_r[pbi]], writes=[ost_r[osi]])
                else:
                    bd.op("dve", lambda e, m=m, pbi=pbi, osi=osi: e.tensor_copy(
                        out=ost[osi][0:m, :], in_=pb[pbi][0:m, :]),
                        reads=[pb_r[pbi]], writes=[ost_r[osi]])
                bd.dma("sp", pT_ap[col0:col0 + m, blk * 512:(blk + 1) * 512], ost[osi][0:m, :],
                       reads=[ost_r[osi]])
                oi += 1
    bd.barrier()


def _gcol(g):
    return np.ascontiguousarray(np.asarray(g, np.float32).reshape(8, 128).T)


def prep_w_inA(w_in_l):
    cols = half_cols(0) + half_cols(1)
    return np.ascontiguousarray(w_in_l[:, cols])


TWO_PI = 6.283185307179586
(C_QN0, C_QN1, C_KVN0, C_KVN1, C_INVF, C_SGN, C_MLAO0, C_MLAO1,
 C_MU_R, C_MU_K, C_MU_V, C_MU_L, C_W0, C_A0, C_KK, C_KA, C_RK, C_LNW, C_LNB,
 C_CWQ0, C_CWQ1, C_CWQ2, C_CWQ3, C_CBQ, C_CWK0, C_CWK1, C_CWK2, C_CWK3, C_CBK,
 C_IB, C_FB, C_MLO, C_EPS, C_LNEPS, C_ONE, C_ZERO, C_IB1, C_FB1) = range(38)
NCST = 38


def emit_attention(bd, nc, es, name, heads, scale_exp, dv, wfun, out_fn, PS):
    sb = lambda nm, shape, dt: es.enter_context(nc.sbuf_tensor(_u(nm), shape, dt))
    NPT = 3
    pT = [sb(f"{name}_pT{i}", [128, 512], BF16) for i in range(NPT)]
    pT_r = [Res() for _ in range(NPT)]
    mask = PS["mask"]
    mask_r = PS["mask_r"]
    it = 0
    for h in heads:
        for qb in range(SEQ // 512):
            nkt = 4 * (qb + 1)
            for kt in range(nkt):
                st, st_r = PS["st"][it % 2]
                kp = PS["kparts"](h, kt)
                qp = PS["qparts"](h, qb)
                n = len(kp)
                for i in range(n):
                    bd.op("pe", lambda e, i=i, st=st: e.matmul(st[:, :], lhsT=kp[i][0], rhs=qp[i][0],
                                                              start=(i == 0), stop=(i == n - 1)),
                          reads=[kp[i][1], qp[i][1]], writes=[st_r])
                p, p_r = pT[it % NPT], pT_r[it % NPT]
                wfun(h, kt, qb, st, st_r, p, p_r)
                jd = kt - 4 * qb
                if jd >= 0:
                    bd.op("pool", lambda e, p=p, jd=jd: e.tensor_tensor(
                        out=p[:, jd * 128:(jd + 1) * 128], in0=p[:, jd * 128:(jd + 1) * 128], in1=mask[:],
                        op=ALU.mult), reads=[p_r, mask_r], writes=[p_r])
                v_ap, v_r = PS["v"](h, kt)
                for j in range(4):
                    qt = 4 * qb + j
                    if qt < kt:
                        continue
                    o, o_r = PS["o"][j]
                    bd.op("pe", lambda e, o=o, p=p, j=j, v_ap=v_ap, kt=kt, qt=qt: e.matmul(
                        o[:, 0:dv + 1], lhsT=p[:, j * 128:(j + 1) * 128], rhs=v_ap,
                        start=(kt == 0), stop=(kt == qt)), reads=[p_r, v_r], writes=[o_r])
                    if kt == qt:
                        out_fn(h, qt, o, o_r)
                it += 1


def emit_rope_tables(bd, nc, es, pos_d, cst, cst_r, CS, SN, tab_r):
    with ExitStack() as es2:
        sb = lambda nm, shape, dt: es2.enter_context(nc.sbuf_tensor(_u(nm), shape, dt))
        ti = sb("rt_i", [64, SEQ], I32)
        ta = sb("rt_a", [64, SEQ], F32)
        tb = sb("rt_b", [64, SEQ], F32)
        ti_r, ta_r, tb_r = Res(), Res(), Res()
        src = bass.AP(pos_d, 0, [[0, 64], [1, SEQ]])
        bd.dma("sp", ti[:], src, writes=[ti_r])
        bd.op("dve", lambda e: e.tensor_copy(out=ta[:], in_=ti[:]), reads=[ti_r], writes=[ta_r])
        bd.op("dve", lambda e: e.tensor_scalar(out=ta[:], in0=ta[:], scalar1=cst[0:64, C_INVF:C_INVF + 1],
                                               scalar2=None, op0=ALU.mult), reads=[ta_r, cst_r], writes=[ta_r])
        for which in (0, 1):
            shift = 0.0 if which == 0 else TWO_PI / 4
            bd.op("dve", lambda e: e.tensor_scalar(out=tb[:], in0=ta[:], scalar1=shift, scalar2=1.0 / TWO_PI,
                                                   op0=ALU.add, op1=ALU.mult), reads=[ta_r], writes=[tb_r])
            bd.op("dve", lambda e: e.tensor_copy(out=ti[:], in_=tb[:]), reads=[tb_r], writes=[ti_r])
            bd.op("dve", lambda e: e.tensor_copy(out=tb[:], in_=ti[:]), reads=[ti_r], writes=[tb_r])
            bd.op("dve", lambda e: e.scalar_tensor_tensor(out=tb[:], in0=tb[:], scalar=-TWO_PI, in1=ta[:],
                                                          op0=ALU.mult, op1=ALU.add),
                  reads=[tb_r, ta_r], writes=[tb_r])
            bd.op("dve", lambda e: e.tensor_scalar(out=tb[:], in0=tb[:], scalar1=shift, scalar2=TWO_PI / 2,
                                                   op0=ALU.add, op1=ALU.min), reads=[tb_r], writes=[tb_r])
            bd.op("dve", lambda e: e.tensor_scalar(out=tb[:], in0=tb[:], scalar1=-TWO_PI / 2, scalar2=None,
                                                   op0=ALU.max), reads=[tb_r], writes=[tb_r])
            if which == 0:
                bd.op("act", lambda e: e.activation(out=SN[:], in_=tb[:], func=AF.Sin,
                                                    scale=cst[0:64, C_SGN:C_SGN + 1]),
                      reads=[tb_r, cst_r], writes=[tab_r])
            else:
                bd.op("act", lambda e: e.activation(out=CS[:], in_=tb[:], func=AF.Sin),
                      reads=[tb_r], writes=[tab_r])
        bd.barrier()


def emit_mla(bd, nc, es, pT_d, pos_d, wuq_d, wukv_d, yT_d, cst, cst_r, K):
    sb = lambda nm, shape, dt: es.enter_context(nc.sbuf_tensor(_u(nm), shape, dt))
    pT = pT_d.ap()
    yT = yT_d.ap()
    CS = sb("m_CS", [64, SEQ], F32)
    SN = sb("m_SN", [64, SEQ], F32)
    tab_r = Res()
    emit_rope_tables(bd, nc, es, pos_d, cst, cst_r, CS, SN, tab_r)
    wq = sb("m_wq", [128, 2, 512], BF16)
    wkv = sb("m_wkv", [128, 2, 512], BF16)
    wq_r, wkv_r = Res(), Res()
    wst = sb("m_wst", [128, 2, 512], F32)
    wst_r = Res()
    for (wd, wt, wr) in ((wuq_d, wq, wq_r), (wukv_d, wkv, wkv_r)):
        bd.dma("sp", wst[:], wd.ap().rearrange("(k p) n -> p k n", p=128), writes=[wst_r])
        bd.op("dve", lambda e, wt=wt: e.tensor_copy(out=wt[:], in_=wst[:]), reads=[wst_r], writes=[wr])
    Qn = [sb(f"m_Qn{h}", [128, SEQ], BF16) for h in range(2)]
    Qr = [sb(f"m_Qr{h}", [64, SEQ], BF16) for h in range(2)]
    Kn = [sb(f"m_Kn{h}", [128, SEQ], BF16) for h in range(2)]
    Kr = sb("m_Kr", [64, SEQ], BF16)
    V = [sb(f"m_V{h}", [128, 32, 129], BF16) for h in range(2)]
    qk_r = Res()
    for h in range(2):
        bd.op("pool", lambda e, h=h: e.memset(V[h][:, :, 128:129], 1.0), writes=[qk_r])
    banks, banks_r, ptb, ptb_r = K["banks"], K["banks_r"], K["ptb"], K["ptb_r"]
    ones, ones_r = K["ones"], K["ones_r"]
    with ExitStack() as es2:
        sb2 = lambda nm, shape, dt: es2.enter_context(nc.sbuf_tensor(_u(nm), shape, dt))
        cf = [sb2(f"m_cf{i}", [128, 2, 512], F32) for i in range(2)]
        cf_r = [Res(), Res()]
        sq = sb2("m_sq", [128, 2, 512], F32)
        sq_r = Res()
        rs = sb2("m_rs", [128, 512], F32)
        rs_r = Res()
        cn = [sb2(f"m_cn{i}", [128, 2, 512], BF16) for i in range(2)]
        cn_r = [Res(), Res()]
        kx = sb2("m_kx", [64, 2, 512], F32)
        kx_r = Res()
        t1 = sb2("m_t1", [64, 512], F32)
        t2 = sb2("m_t2", [64, 512], F32)
        t1_r, t2_r = Res(), Res()
        bi = 0

        def nb():
            nonlocal bi
            b = bi % 6
            bi += 1
            return banks[b], banks_r[b]

        def rope(dst, x_ap, xs_ap, src_res, sl):
            bd.op("dve", lambda e: e.tensor_tensor(out=t1[:], in0=x_ap, in1=CS[:, sl], op=ALU.mult),
                  reads=src_res + [tab_r], writes=[t1_r])
            bd.op("dve", lambda e: e.tensor_tensor(out=t2[:], in0=xs_ap, in1=SN[:, sl], op=ALU.mult),
                  reads=src_res + [tab_r], writes=[t2_r])
            bd.op("dve", lambda e: e.tensor_tensor(out=dst, in0=t1[:], in1=t2[:], op=ALU.add),
                  reads=[t1_r, t2_r], writes=[qk_r])

        for tb in range(SEQ // 512):
            sl = slice(tb * 512, (tb + 1) * 512)
            for which in (0, 1):
                ci = which
                row0 = which * 256
                bd.dma("sp", cf[ci][:], pT[row0:row0 + 256, sl].rearrange("(k p) t -> p k t", p=128),
                       writes=[cf_r[ci]])
                bd.op("act", lambda e, ci=ci: e.activation(out=sq[:], in_=cf[ci][:], func=AF.Square),
                      reads=[cf_r[ci]], writes=[sq_r])
                pss, pss_r = nb()
                for c in range(2):
                    bd.op("pe", lambda e, c=c, pss=pss: e.matmul(pss[:, :], lhsT=ones[:], rhs=sq[:, c, :],
                                                                 start=(c == 0), stop=(c == 1)),
                          reads=[ones_r, sq_r], writes=[pss_r])
                bd.op("act", lambda e, pss=pss: e.activation(out=rs[:], in_=pss[:, :], func=AF.Sqrt,
                                                             scale=1.0 / 256, bias=cst[:, C_EPS:C_EPS + 1]),
                      reads=[pss_r, cst_r], writes=[rs_r])
                bd.op("dve", lambda e: e.reciprocal(out=rs[:], in_=rs[:]), reads=[rs_r], writes=[rs_r])
                gcol = C_QN0 if which == 0 else C_KVN0
                for c in range(2):
                    bd.op("dve", lambda e, c=c, ci=ci, gcol=gcol: e.scalar_tensor_tensor(
                        out=cn[ci][:, c, :], in0=cf[ci][:, c, :], scalar=cst[:, gcol + c:gcol + c + 1], in1=rs[:],
                        op0=ALU.mult, op1=ALU.mult), reads=[cf_r[ci], rs_r, cst_r], writes=[cn_r[ci]])
                if which == 0:
                    for h in range(2):
                        pq, pq_r = nb()
                        for c in range(2):
                            bd.op("pe", lambda e, c=c, h=h, pq=pq: e.matmul(
                                pq[:, :], lhsT=wq[:, c, h * 256:h * 256 + 128], rhs=cn[0][:, c, :],
                                start=(c == 0), stop=(c == 1)), reads=[wq_r, cn_r[0]], writes=[pq_r])
                        bd.op("act", lambda e, h=h, pq=pq: e.activation(out=Qn[h][:, sl], in_=pq[:, :], func=AF.Copy),
                              reads=[pq_r], writes=[qk_r])
                        pa, pa_r = nb()
                        pb_, pb_r = nb()
                        for (pp, pp_r, off) in ((pa, pa_r, 128), (pb_, pb_r, 192)):
                            for c in range(2):
                                bd.op("pe", lambda e, c=c, h=h, pp=pp, off=off: e.matmul(
                                    pp[0:64, :], lhsT=wq[:, c, h * 256 + off:h * 256 + off + 64], rhs=cn[0][:, c, :],
                                    start=(c == 0), stop=(c == 1)), reads=[wq_r, cn_r[0]], writes=[pp_r])
                        rope(Qr[h][:, sl], pa[0:64, :], pb_[0:64, :], [pa_r, pb_r], sl)
                else:
                    for h in range(2):
                        pk, pk_r = nb()
                        for c in range(2):
                            bd.op("pe", lambda e, c=c, h=h, pk=pk: e.matmul(
                                pk[:, :], lhsT=wkv[:, c, h * 256:h * 256 + 128], rhs=cn[1][:, c, :],
                                start=(c == 0), stop=(c == 1)), reads=[wkv_r, cn_r[1]], writes=[pk_r])
                        bd.op("act", lambda e, h=h, pk=pk: e.activation(out=Kn[h][:, sl], in_=pk[:, :], func=AF.Copy),
                              reads=[pk_r], writes=[qk_r])
                        pv, pv_r = nb()
                        for j in range(4):
                            for c in range(2):
                                bd.op("pe", lambda e, c=c, h=h, j=j, pv=pv: e.matmul(
                                    pv[:, j * 128:(j + 1) * 128], lhsT=cn[1][:, c, j * 128:(j + 1) * 128],
                                    rhs=wkv[:, c, h * 256 + 128:h * 256 + 256],
                                    start=(c == 0 and j == 0), stop=(c == 1)), reads=[wkv_r, cn_r[1]], writes=[pv_r])
                        bd.op("dve", lambda e, h=h, pv=pv: e.tensor_copy(
                            out=V[h][:, tb * 4:(tb + 1) * 4, 0:128],
                            in_=pv[:, :].rearrange("p (j d) -> p j d", j=4)), reads=[pv_r], writes=[qk_r])
            bd.dma("sp", kx[:], pT[512:640, sl].rearrange("(k p) t -> p k t", p=64), writes=[kx_r])
            rope(Kr[:, sl], kx[:, 0, :], kx[:, 1, :], [kx_r], sl)
        bd.barrier()
    with ExitStack() as es3:
        sb3 = lambda nm, shape, dt: es3.enter_context(nc.sbuf_tensor(_u(nm), shape, dt))
        of = sb3("m_of", [128, 132], F32)
        of_r = Res()
        onb = sb3("m_onb", [128, 128], BF16)
        onb_r = Res()
        st_ = sb3("m_stat", [128, 4], F32)
        st_r = Res()
        junk = sb3("m_junk", [128, 128], F32)
        junk_r = Res()
        yst = [sb3(f"m_yst{i}", [128, 512], BF16) for i in range(2)]
        yst_r = [Res(), Res()]
        scale = (128 + 64) ** -0.5

        def wfun(h, kt, qb, st, st_r2, p, p_r):
            bd.op("act", lambda e: e.activation(out=p[:], in_=st[:, :], func=AF.Exp, scale=scale),
                  reads=[st_r2], writes=[p_r])

        def out_fn(h, qt, o, o_r):
            bd.op("dve", lambda e: e.reciprocal(out=st_[:, 0:1], in_=o[:, 128:129]), reads=[o_r], writes=[st_r])
            bd.op("dve", lambda e: e.tensor_scalar(out=of[:, 0:128], in0=o[:, 0:128], scalar1=st_[:, 0:1],
                                                   scalar2=None, op0=ALU.mult), reads=[o_r, st_r], writes=[of_r])
            bd.op("dve", lambda e: e.scalar_tensor_tensor(out=junk[:], in0=of[:, 0:128], scalar=1.0, in1=of[:, 0:128],
                                                          op0=ALU.mult, op1=ALU.mult, accum_out=st_[:, 1:2]),
                  reads=[of_r], writes=[junk_r, st_r])
            bd.op("act", lambda e: e.activation(out=st_[:, 2:3], in_=st_[:, 1:2], func=AF.Sqrt, scale=1.0 / 128,
                                                bias=cst[:, C_EPS:C_EPS + 1]), reads=[st_r, cst_r], writes=[st_r])
            bd.op("dve", lambda e: e.reciprocal(out=st_[:, 3:4], in_=st_[:, 2:3]), reads=[st_r], writes=[st_r])
            bd.op("act", lambda e: e.activation(out=onb[:], in_=of[:, 0:128], func=AF.Copy, scale=st_[:, 3:4]),
                  reads=[of_r, st_r], writes=[onb_r])
            bd.op("pe", lambda e: e.transpose(out=ptb[:, 0:128], in_=onb[:], identity=K["identb"][:]),
                  reads=[onb_r, K["identb_r"]], writes=[ptb_r])
            ys, ys_r = yst[(qt // 4) % 2], yst_r[(qt // 4) % 2]
            j = qt % 4
            bd.op("act", lambda e: e.activation(out=ys[:, j * 128:(j + 1) * 128], in_=ptb[:, 0:128], func=AF.Copy,
                                                scale=cst[:, C_MLAO0 + h:C_MLAO0 + h + 1]),
                  reads=[ptb_r, cst_r], writes=[ys_r])
            if j == 3:
                qb = qt // 4
                bd.dma("sp", yT[h * 128:(h + 1) * 128, qb * 512:(qb + 1) * 512], ys[:], reads=[ys_r])

        PS = dict(st=[(banks[0], banks_r[0]), (banks[1], banks_r[1])],
                  o=[(banks[2 + j], banks_r[2 + j]) for j in range(4)],
                  mask=K["mask"], mask_r=K["mask_r"],
                  kparts=lambda h, kt: [(Kn[h][:, kt * 128:(kt + 1) * 128], qk_r), (Kr[:, kt * 128:(kt + 1) * 128], qk_r)],
                  qparts=lambda h, qb: [(Qn[h][:, qb * 512:(qb + 1) * 512], qk_r), (Qr[h][:, qb * 512:(qb + 1) * 512], qk_r)],
                  v=lambda h, kt: (V[h][:, kt, :], qk_r))
        emit_attention(bd, nc, es3, "mla", [0, 1], scale, 128, wfun, out_fn, PS)
        bd.barrier()


def emit_mlstm(bd, nc, es, pT_d, yT_d, cst, cst_r, K):
    sb = lambda nm, shape, dt: es.enter_context(nc.sbuf_tensor(_u(nm), shape, dt))
    pT = pT_d.ap()
    yT = yT_d.ap()
    banks, banks_r, ptb, ptb_r = K["banks"], K["banks_r"], K["ptb"], K["ptb_r"]
    misc, misc_r = banks[6], banks_r[6]
    R_Q, R_K, R_V, R_O, R_G = 1152, 1216, 1280, 1408, 1536
    Qb = sb("l_Qb", [64, SEQ], BF16)
    Kb = sb("l_Kb", [64, SEQ], BF16)
    Vm = sb("l_Vm", [128, 32, 2, 65], BF16)
    Ym = sb("l_Ym", [128, SEQ], BF16)
    uT = sb("l_uT", [128, 2, 32], F32)
    emT = sb("l_emT", [128, 2, 32], F32)
    nPb = [sb(f"l_nPb{h}", [128, SEQ], F32) for h in range(2)]
    prep_r = Res()
    ym_r = Res()
    bd.op("pool", lambda e: e.memset(Vm[:, :, :, 64:65], 1.0), writes=[prep_r])
    with ExitStack() as es2:
        sb2 = lambda nm, shape, dt: es2.enter_context(nc.sbuf_tensor(_u(nm), shape, dt))
        xin = sb2("l_xin", [64, SEQ], F32)
        A = sb2("l_A", [64, SEQ], F32)
        B = sb2("l_B", [64, SEQ], F32)
        xin_r, A_r, B_r = Res(), Res(), Res()
        for (row0, cw0, cb, dst, scl) in ((R_Q, C_CWQ0, C_CBQ, Qb, 32 ** -0.5), (R_K, C_CWK0, C_CBK, Kb, 1.0)):
            bd.dma("sp", xin[:], pT[row0:row0 + 64, :], writes=[xin_r])
            bd.op("dve", lambda e, cw0=cw0, cb=cb: e.tensor_scalar(
                out=A[:], in0=xin[:], scalar1=cst[0:64, cw0 + 3:cw0 + 4], scalar2=cst[0:64, cb:cb + 1],
                op0=ALU.mult, op1=ALU.add), reads=[xin_r, cst_r], writes=[A_r])
            src, src_r, dstt, dst_r = A, A_r, B, B_r
            for sh in (1, 2, 3):
                bd.op("dve", lambda e, sh=sh, cw0=cw0, src=src, dstt=dstt: e.scalar_tensor_tensor(
                    out=dstt[:, sh:], in0=xin[:, 0:SEQ - sh], scalar=cst[0:64, cw0 + 3 - sh:cw0 + 4 - sh],
                    in1=src[:, sh:], op0=ALU.mult, op1=ALU.add), reads=[xin_r, cst_r, src_r], writes=[dst_r])
                bd.op("dve", lambda e, sh=sh, src=src, dstt=dstt: e.tensor_copy(out=dstt[:, 0:sh], in_=src[:, 0:sh]),
                      reads=[src_r], writes=[dst_r])
                src, src_r, dstt, dst_r = dstt, dst_r, src, src_r
            bd.op("act", lambda e, src=src: e.activation(out=xin[:], in_=src[:], func=AF.Silu),
                  reads=[src_r], writes=[xin_r])
            bd.op("dve", lambda e, dst=dst, scl=scl: e.tensor_scalar(out=dst[:], in0=xin[:], scalar1=scl, scalar2=None,
                                                                     op0=ALU.mult), reads=[xin_r], writes=[prep_r])
        bd.barrier()
    with ExitStack() as es2:
        sb2 = lambda nm, shape, dt: es2.enter_context(nc.sbuf_tensor(_u(nm), shape, dt))
        t0 = sb2("l_t0", [1, SEQ], F32)
        t1 = sb2("l_t1", [1, SEQ], F32)
        t2 = sb2("l_t2", [1, SEQ], F32)
        onesrow = sb2("l_onesrow", [1, SEQ], F32)
        vin = sb2("l_vin", [128, SEQ], F32)
        t0_r, t1_r, t2_r, or_r, vin_r = Res(), Res(), Res(), Res(), Res()
        bd.op("dve", lambda e: e.memset(onesrow[:], 1.0), writes=[or_r])
        identf = K["identf"]
        for h in range(2):
            cib = C_IB if h == 0 else C_IB1
            cfb = C_FB if h == 0 else C_FB1
            bd.dma("sp", t0[:], pT[R_G + h:R_G + h + 1, :], writes=[t0_r])
            bd.dma("sp", t1[:], pT[R_G + 2 + h:R_G + 3 + h, :], writes=[t1_r])
            bd.op("dve", lambda e, cib=cib: e.tensor_scalar(out=t0[:], in0=t0[:], scalar1=cst[0:1, cib:cib + 1],
                                                            scalar2=None, op0=ALU.add), reads=[t0_r, cst_r], writes=[t0_r])
            bd.op("act", lambda e, cfb=cfb: e.activation(out=t1[:], in_=t1[:], func=AF.Sigmoid,
                                                         bias=cst[0:1, cfb:cfb + 1]), reads=[t1_r, cst_r], writes=[t1_r])
            bd.op("act", lambda e: e.activation(out=t1[:], in_=t1[:], func=AF.Ln), reads=[t1_r], writes=[t1_r])
            bd.op("dve", lambda e: e.tensor_tensor_scan(out=t2[:], data0=onesrow[:], data1=t1[:], initial=0.0,
                                                        op0=ALU.mult, op1=ALU.add), reads=[or_r, t1_r], writes=[t2_r])
            bd.op("dve", lambda e: e.tensor_tensor(out=t0[:], in0=t0[:], in1=t2[:], op=ALU.subtract),
                  reads=[t0_r, t2_r], writes=[t0_r])
            bd.op("dve", lambda e: e.tensor_tensor_scan(out=t1[:], data0=onesrow[:], data1=t0[:], initial=0.0,
                                                        op0=ALU.mult, op1=ALU.max), reads=[or_r, t0_r], writes=[t1_r])
            bd.op("dve", lambda e: e.tensor_tensor(out=t2[:], in0=t2[:], in1=t1[:], op=ALU.add),
                  reads=[t2_r, t1_r], writes=[t2_r])
            for jt in range(32):
                bd.op("pe", lambda e, jt=jt: e.transpose(out=misc[:, jt:jt + 1], in_=t0[0:1, jt * 128:(jt + 1) * 128],
                                                         identity=identf[0:1, 0:1]), reads=[t0_r, K["mats_r"]], writes=[misc_r])
                bd.op("pe", lambda e, jt=jt: e.transpose(out=misc[:, 32 + jt:33 + jt], in_=t2[0:1, jt * 128:(jt + 1) * 128],
                                                         identity=identf[0:1, 0:1]), reads=[t2_r, K["mats_r"]], writes=[misc_r])
            bd.op("dve", lambda e, h=h: e.tensor_copy(out=uT[:, h, :], in_=misc[:, 0:32]), reads=[misc_r], writes=[prep_r])
            bd.op("act", lambda e, h=h: e.activation(out=emT[:, h, :], in_=misc[:, 32:64], func=AF.Exp, scale=-1.0),
                  reads=[misc_r], writes=[prep_r])
            for tb in range(SEQ // 512):
                bd.op("pe", lambda e, tb=tb: e.matmul(misc[:, :], lhsT=K["ones"][0:1, :], rhs=t1[0:1, tb * 512:(tb + 1) * 512],
                                                      start=True, stop=True), reads=[t1_r, K["ones_r"]], writes=[misc_r])
                bd.op("act", lambda e, tb=tb, h=h: e.activation(out=nPb[h][:, tb * 512:(tb + 1) * 512], in_=misc[:, :],
                                                                func=AF.Copy, scale=-1.0), reads=[misc_r], writes=[prep_r])
        bd.dma("sp", vin[:], pT[R_V:R_V + 128, :], writes=[vin_r])
        for jt in range(32):
            bd.op("pe", lambda e, jt=jt: e.transpose(out=misc[:, 0:128], in_=vin[:, jt * 128:(jt + 1) * 128],
                                                     identity=identf), reads=[vin_r, K["mats_r"]], writes=[misc_r])
            bd.op("dve", lambda e, jt=jt: e.tensor_copy(out=Vm[:, jt, :, 0:64],
                                                        in_=misc[:, 0:128].rearrange("p (h d) -> p h d", h=2)),
                  reads=[misc_r], writes=[prep_r])
        bd.barrier()
    with ExitStack() as es3:
        sb3 = lambda nm, shape, dt: es3.enter_context(nc.sbuf_tensor(_u(nm), shape, dt))
        Wt = [sb3(f"l_W{i}", [128, 512], F32) for i in range(2)]
        Wt_r = [Res(), Res()]
        of = sb3("l_of", [128, 64], F32)
        of_r = Res()
        onb = sb3("l_onb", [128, 64], BF16)
        onb_r = Res()
        st_ = sb3("l_stat", [128, 6], F32)
        st_r = Res()
        junk = sb3("l_junk", [128, 64], F32)
        junk_r = Res()
        wi = [0]

        def wfun(h, kt, qb, st, st_r2, p, p_r):
            w, w_r = Wt[wi[0] % 2], Wt_r[wi[0] % 2]
            wi[0] += 1
            bd.op("act", lambda e: e.activation(out=w[:], in_=nPb[h][:, qb * 512:(qb + 1) * 512], func=AF.Exp,
                                                bias=uT[:, h, kt:kt + 1]), reads=[prep_r], writes=[w_r])
            bd.op("dve", lambda e: e.tensor_tensor(out=p[:], in0=st[:, :], in1=w[:], op=ALU.mult),
                  reads=[st_r2, w_r], writes=[p_r])

        def out_fn(h, qt, o, o_r):
            bd.op("act", lambda e: e.activation(out=st_[:, 0:1], in_=o[:, 64:65], func=AF.Abs), reads=[o_r], writes=[st_r])
            bd.op("dve", lambda e: e.tensor_tensor(out=st_[:, 1:2], in0=st_[:, 0:1], in1=emT[:, h, qt:qt + 1], op=ALU.max),
                  reads=[st_r, prep_r], writes=[st_r])
            bd.op("dve", lambda e: e.reciprocal(out=st_[:, 2:3], in_=st_[:, 1:2]), reads=[st_r], writes=[st_r])
            bd.op("dve", lambda e: e.tensor_scalar(out=of[:], in0=o[:, 0:64], scalar1=st_[:, 2:3], scalar2=None,
                                                   op0=ALU.mult), reads=[o_r, st_r], writes=[of_r])
            bd.op("dve", lambda e: e.scalar_tensor_tensor(out=junk[:], in0=of[:], scalar=1.0, in1=of[:],
                                                          op0=ALU.mult, op1=ALU.mult, accum_out=st_[:, 3:4]),
                  reads=[of_r], writes=[junk_r, st_r])
            bd.op("act", lambda e: e.activation(out=st_[:, 4:5], in_=st_[:, 3:4], func=AF.Sqrt, scale=1.0 / 64,
                                                bias=cst[:, C_EPS:C_EPS + 1]), reads=[st_r, cst_r], writes=[st_r])
            bd.op("dve", lambda e: e.reciprocal(out=st_[:, 5:6], in_=st_[:, 4:5]), reads=[st_r], writes=[st_r])
            bd.op("act", lambda e: e.activation(out=onb[:], in_=of[:], func=AF.Copy, scale=st_[:, 5:6]),
                  reads=[of_r, st_r], writes=[onb_r])
            bd.op("pe", lambda e: e.transpose(out=ptb[h * 64:(h + 1) * 64, 0:128], in_=onb[:], identity=K["identb"][:]),
                  reads=[onb_r, K["identb_r"]], writes=[ptb_r])
            bd.op("act", lambda e: e.activation(out=Ym[h * 64:(h + 1) * 64, qt * 128:(qt + 1) * 128],
                                                in_=ptb[h * 64:(h + 1) * 64, 0:128], func=AF.Copy,
                                                scale=cst[h * 64:(h + 1) * 64, C_MLO:C_MLO + 1]),
                  reads=[ptb_r, cst_r], writes=[ym_r])

        PS = dict(st=[(banks[0], banks_r[0]), (banks[1], banks_r[1])],
                  o=[(banks[2 + j], banks_r[2 + j]) for j in range(4)],
                  mask=K["mask"], mask_r=K["mask_r"],
                  kparts=lambda h, kt: [(Kb[h * 32:(h + 1) * 32, kt * 128:(kt + 1) * 128], prep_r)],
                  qparts=lambda h, qb: [(Qb[h * 32:(h + 1) * 32, qb * 512:(qb + 1) * 512], prep_r)],
                  v=lambda h, kt: (Vm[:, kt, h, :], prep_r))
        emit_attention(bd, nc, es3, "mls", [0, 1], 1.0, 64, wfun, out_fn, PS)
        og = sb3("l_og", [128, SEQ], F32)
        og_r = Res()
        bd.dma("sp", og[:], pT[R_O:R_O + 128, :], writes=[og_r])
        bd.op("act", lambda e: e.activation(out=og[:], in_=og[:], func=AF.Sigmoid), reads=[og_r], writes=[og_r])
        bd.op("dve", lambda e: e.tensor_tensor(out=Ym[:], in0=Ym[:], in1=og[:], op=ALU.mult),
              reads=[ym_r, og_r], writes=[ym_r])
        bd.dma("sp", yT[384:512, :], Ym[:], reads=[ym_r])
        bd.barrier()


RW_T = 16


def emit_rwkv(bd, nc, es, D_, cst, cst_r, K):
    sb = lambda nm, shape, dt: es.enter_context(nc.sbuf_tensor(_u(nm), shape, dt))
    pT = D_["pT"].ap()
    yT = D_["yT"].ap()
    scr = D_["scr"]
    banks, banks_r = K["banks"], K["banks_r"]
    bones, identf, mats_r = K["bones"], K["identf"], K["mats_r"]
    R_R, R_K, R_V, R_L = 640, 768, 896, 1024
    vS = sb("r_vS", [128, SEQ], F32)
    gS = sb("r_gS", [128, SEQ], F32)
    boS = sb("r_boS", [128, SEQ], F32)
    yS = sb("r_yS", [128, SEQ], F32)
    vS_r, gS_r, boS_r, yS_r = Res(), Res(), Res(), Res()
    wl = sb("r_wl", [128, 3, 128], F32)
    wl_r = Res()
    bd.dma("sp", wl[:], D_["wl"].ap(), writes=[wl_r])
    c2 = sb("r_c2", [128, 2], F32)
    c2_r = Res()
    bd.op("dve", lambda e: e.tensor_scalar(out=c2[:, 0:1], in0=cst[:, C_KA:C_KA + 1], scalar1=-1.0, scalar2=1.0,
                                           op0=ALU.mult, op1=ALU.add), reads=[cst_r], writes=[c2_r])
    scr_r = Res()
    with ExitStack() as es2:
        sb2 = lambda nm, shape, dt: es2.enter_context(nc.sbuf_tensor(_u(nm), shape, dt))
        rS = sb2("r_rS", [128, SEQ], F32)
        kS = sb2("r_kS", [128, SEQ], F32)
        lS = sb2("r_lS", [128, SEQ], F32)
        dd = sb2("r_dd", [128, SEQ], F32)
        rS_r, kS_r, lS_r, dd_r = Res(), Res(), Res(), Res()
        for (row0, t, t_r, mu) in ((R_R, rS, rS_r, C_MU_R), (R_K, kS, kS_r, C_MU_K), (R_V, vS, vS_r, C_MU_V),
                                   (R_L, lS, lS_r, C_MU_L)):
            bd.dma("sp", t[:], pT[row0:row0 + 128, :], writes=[t_r])
            bd.op("dve", lambda e, t=t: e.tensor_tensor(out=dd[:, 1:SEQ], in0=t[:, 0:SEQ - 1], in1=t[:, 1:SEQ],
                                                        op=ALU.subtract), reads=[t_r], writes=[dd_r])
            bd.op("dve", lambda e, t=t: e.tensor_scalar(out=dd[:, 0:1], in0=t[:, 0:1], scalar1=-1.0, scalar2=None,
                                                        op0=ALU.mult), reads=[t_r], writes=[dd_r])
            bd.op("dve", lambda e, t=t, mu=mu: e.scalar_tensor_tensor(out=t[:], in0=dd[:], scalar=cst[:, mu:mu + 1],
                                                                      in1=t[:], op0=ALU.mult, op1=ALU.add),
                  reads=[dd_r, t_r, cst_r], writes=[t_r])
        names = ["th", "sg", "sgm", "wd", "aT", "kkr", "sq", "nrm", "nkk", "bb", "t1", "km", "prod"]
        T_ = {n: sb2("r_" + n, [128, 512], F32) for n in names}
        T_r = {n: Res() for n in names}
        stg = [sb2(f"r_stg{i}", [128, 5, 128], F32) for i in range(2)]
        stg_r = [Res(), Res()]
        bi = [0]

        def nb():
            b = bi[0] % 7
            bi[0] += 1
            return banks[b], banks_r[b]

        def A(fn, reads, writes):
            bd.op("act", fn, reads=reads, writes=writes)

        def V(fn, reads, writes):
            bd.op("dve", fn, reads=reads, writes=writes)

        ti = 0
        for tb in range(SEQ // 512):
            sl = slice(tb * 512, (tb + 1) * 512)
            A(lambda e: e.activation(out=T_["th"][:], in_=lS[:, sl], func=AF.Tanh), [lS_r], [T_r["th"]])
            A(lambda e: e.activation(out=T_["sg"][:], in_=lS[:, sl], func=AF.Sigmoid), [lS_r], [T_r["sg"]])
            pw, pw_r = nb()
            bd.op("pe", lambda e: e.matmul(pw[:, :], lhsT=wl[:, 0, :], rhs=T_["th"][:], start=True, stop=True),
                  reads=[wl_r, T_r["th"]], writes=[pw_r])
            A(lambda e: e.activation(out=T_["sgm"][:], in_=pw[:, :], func=AF.Sigmoid, bias=cst[:, C_W0:C_W0 + 1]),
              [pw_r, cst_r], [T_r["sgm"]])
            A(lambda e: e.activation(out=T_["wd"][:], in_=T_["sgm"][:], func=AF.Exp, scale=-float(np.exp(-0.5))),
              [T_r["sgm"]], [T_r["wd"]])
            pa, pa_r = nb()
            bd.op("pe", lambda e: e.matmul(pa[:, :], lhsT=wl[:, 1, :], rhs=lS[:, sl], start=True, stop=True),
                  reads=[wl_r, lS_r], writes=[pa_r])
            A(lambda e: e.activation(out=T_["aT"][:], in_=pa[:, :], func=AF.Sigmoid, bias=cst[:, C_A0:C_A0 + 1]),
              [pa_r, cst_r], [T_r["aT"]])
            pg, pg_r = nb()
            bd.op("pe", lambda e: e.matmul(pg[:, :], lhsT=wl[:, 2, :], rhs=T_["sg"][:], start=True, stop=True),
                  reads=[wl_r, T_r["sg"]], writes=[pg_r])
            A(lambda e: e.activation(out=gS[:, sl], in_=pg[:, :], func=AF.Copy), [pg_r], [gS_r])
            V(lambda e: e.tensor_scalar(out=T_["kkr"][:], in0=kS[:, sl], scalar1=cst[:, C_KK:C_KK + 1], scalar2=None,
                                        op0=ALU.mult), [kS_r, cst_r], [T_r["kkr"]])
            A(lambda e: e.activation(out=T_["sq"][:], in_=T_["kkr"][:], func=AF.Square), [T_r["kkr"]], [T_r["sq"]])
            pn, pn_r = nb()
            bd.op("pe", lambda e: e.matmul(pn[:, :], lhsT=bones, rhs=T_["sq"][:], start=True, stop=True),
                  reads=[mats_r, T_r["sq"]], writes=[pn_r])
            A(lambda e: e.activation(out=T_["nrm"][:], in_=pn[:, :], func=AF.Sqrt), [pn_r], [T_r["nrm"]])
            V(lambda e: e.tensor_scalar(out=T_["nrm"][:], in0=T_["nrm"][:], scalar1=1e-12, scalar2=None, op0=ALU.max),
              [T_r["nrm"]], [T_r["nrm"]])
            V(lambda e: e.reciprocal(out=T_["nrm"][:], in_=T_["nrm"][:]), [T_r["nrm"]], [T_r["nrm"]])
            V(lambda e: e.scalar_tensor_tensor(out=T_["nkk"][:], in0=T_["kkr"][:], scalar=-1.0, in1=T_["nrm"][:],
                                               op0=ALU.mult, op1=ALU.mult), [T_r["kkr"], T_r["nrm"]], [T_r["nkk"]])
            V(lambda e: e.scalar_tensor_tensor(out=T_["bb"][:], in0=T_["nkk"][:], scalar=-1.0, in1=T_["aT"][:],
                                               op0=ALU.mult, op1=ALU.mult), [T_r["nkk"], T_r["aT"]], [T_r["bb"]])
            V(lambda e: e.tensor_scalar(out=T_["t1"][:], in0=T_["aT"][:], scalar1=cst[:, C_KA:C_KA + 1],
                                        scalar2=c2[:, 0:1], op0=ALU.mult, op1=ALU.add),
              [T_r["aT"], cst_r, c2_r], [T_r["t1"]])
            V(lambda e: e.tensor_tensor(out=T_["km"][:], in0=kS[:, sl], in1=T_["t1"][:], op=ALU.mult),
              [kS_r, T_r["t1"]], [T_r["km"]])
            V(lambda e: e.scalar_tensor_tensor(out=T_["prod"][:], in0=rS[:, sl], scalar=cst[:, C_RK:C_RK + 1],
                                               in1=T_["km"][:], op0=ALU.mult, op1=ALU.mult),
              [rS_r, cst_r, T_r["km"]], [T_r["prod"]])
            pb_, pb_r = nb()
            bd.op("pe", lambda e: e.matmul(pb_[:, :], lhsT=bones, rhs=T_["prod"][:], start=True, stop=True),
                  reads=[mats_r, T_r["prod"]], writes=[pb_r])
            V(lambda e: e.tensor_tensor(out=boS[:, sl], in0=pb_[:, :], in1=vS[:, sl], op=ALU.mult),
              [pb_r, vS_r], [boS_r])
            for j in range(4):
                t0 = tb * 512 + j * 128
                px, px_r = nb()
                py, py_r = nb()
                srcs = [(T_["nkk"][:, j * 128:(j + 1) * 128], T_r["nkk"]), (T_["wd"][:, j * 128:(j + 1) * 128], T_r["wd"]),
                        (T_["bb"][:, j * 128:(j + 1) * 128], T_r["bb"]), (T_["km"][:, j * 128:(j + 1) * 128], T_r["km"]),
                        (rS[:, t0:t0 + 128], rS_r)]
                for q, (ap_, r_) in enumerate(srcs):
                    if q < 4:
                        bd.op("pe", lambda e, q=q, ap_=ap_: e.transpose(out=px[:, q * 128:(q + 1) * 128], in_=ap_,
                                                                       identity=identf), reads=[r_, mats_r], writes=[px_r])
                    else:
                        bd.op("pe", lambda e, ap_=ap_: e.transpose(out=py[:, 0:128], in_=ap_, identity=identf),
                              reads=[r_, mats_r], writes=[py_r])
                sg_, sg_r = stg[ti % 2], stg_r[ti % 2]
                ti += 1
                A(lambda e, sg_=sg_: e.activation(out=sg_[:, 0:4, :], in_=px[:, :].rearrange("p (q c) -> p q c", q=4),
                                                  func=AF.Copy), [px_r], [sg_r])
                V(lambda e, sg_=sg_: e.tensor_copy(out=sg_[:, 4, :], in_=py[:, 0:128]), [py_r], [sg_r])
                for h in range(2):
                    dst = bass.AP(scr, h * SEQ * 320 + t0 * 320, [[320, 128], [64, 5], [1, 64]])
                    bd.dma("sp" if h == 0 else "act", dst, sg_[:, :, h * 64:(h + 1) * 64], reads=[sg_r], writes=[scr_r])
        bd.barrier()
    with ExitStack() as es3:
        sb3 = lambda nm, shape, dt: es3.enter_context(nc.sbuf_tensor(_u(nm), shape, dt))
        T = RW_T
        NB = 3
        BC = [sb3(f"r_BC{i}", [128, T, 5, 64], F32) for i in range(NB)]
        BC_r = [Res() for _ in range(NB)]
        S = sb3("r_S", [128, 64], F32)
        junk = sb3("r_junk", [128, 64], F32)
        sa = sb3("r_sa", [128, 1], F32)
        S_r, junk_r, sa_r = Res(), Res(), Res()
        bd.op("dve", lambda e: e.memset(S[:], 0.0), writes=[S_r])
        nchunk = SEQ // T

        def load(ci):
            b = ci % NB
            for h in range(2):
                src = bass.AP(scr, h * SEQ * 320 + ci * T * 320, [[0, 64], [1, T * 320]])
                bd.dma("sp" if h == 0 else "act", BC[b][h * 64:(h + 1) * 64, :, :, :].rearrange("p t q j -> p (t q j)"),
                       src, reads=[scr_r], writes=[BC_r[b]])

        load(0)
        load(1)
        for ci in range(nchunk):
            if ci + 2 < nchunk:
                load(ci + 2)
            b = ci % NB
            bc, bc_r = BC[b], BC_r[b]
            for tt in range(T):
                t = ci * T + tt
                bd.op("dve", lambda e, bc=bc, tt=tt: e.scalar_tensor_tensor(
                    out=junk[:], in0=S[:], scalar=1.0, in1=bc[:, tt, 0, :], op0=ALU.mult, op1=ALU.mult,
                    accum_out=sa[:, 0:1]), reads=[S_r, bc_r], writes=[junk_r, sa_r])
                bd.op("dve", lambda e, bc=bc, tt=tt: e.tensor_tensor(out=S[:], in0=S[:], in1=bc[:, tt, 1, :], op=ALU.mult),
                      reads=[S_r, bc_r], writes=[S_r])
                bd.op("dve", lambda e, bc=bc, tt=tt: e.scalar_tensor_tensor(
                    out=S[:], in0=bc[:, tt, 2, :], scalar=sa[:, 0:1], in1=S[:], op0=ALU.mult, op1=ALU.add),
                    reads=[S_r, bc_r, sa_r], writes=[S_r])
                bd.op("dve", lambda e, bc=bc, tt=tt, t=t: e.scalar_tensor_tensor(
                    out=S[:], in0=bc[:, tt, 3, :], scalar=vS[:, t:t + 1], in1=S[:], op0=ALU.mult, op1=ALU.add),
                    reads=[S_r, bc_r, vS_r], writes=[S_r])
                bd.op("dve", lambda e, bc=bc, tt=tt, t=t: e.scalar_tensor_tensor(
                    out=junk[:], in0=S[:], scalar=1.0, in1=bc[:, tt, 4, :], op0=ALU.mult, op1=ALU.mult,
                    accum_out=yS[:, t:t + 1]), reads=[S_r, bc_r], writes=[junk_r, yS_r])
        bd.barrier()
    with ExitStack() as es4:
        sb4 = lambda nm, shape, dt: es4.enter_context(nc.sbuf_tensor(_u(nm), shape, dt))
        yc = sb4("r_yc", [128, 512], F32)
        sq = sb4("r_sq2", [128, 512], F32)
        rs = sb4("r_rs", [128, 512], F32)
        yo = [sb4(f"r_yo{i}", [128, 512], BF16) for i in range(2)]
        yc_r, sq_r, rs_r = Res(), Res(), Res()
        yo_r = [Res(), Res()]
        for tb in range(SEQ // 512):
            sl = slice(tb * 512, (tb + 1) * 512)
            pm, pm_r = banks[tb % 2], banks_r[tb % 2]
            pv, pv_r = banks[2 + tb % 2], banks_r[2 + tb % 2]
            bd.op("pe", lambda e: e.matmul(pm[:, :], lhsT=bones, rhs=yS[:, sl], start=True, stop=True),
                  reads=[mats_r, yS_r], writes=[pm_r])
            bd.op("dve", lambda e: e.scalar_tensor_tensor(out=yc[:], in0=pm[:, :], scalar=-1.0 / 64, in1=yS[:, sl],
                                                          op0=ALU.mult, op1=ALU.add), reads=[pm_r, yS_r], writes=[yc_r])
            bd.op("act", lambda e: e.activation(out=sq[:], in_=yc[:], func=AF.Square), reads=[yc_r], writes=[sq_r])
            bd.op("pe", lambda e: e.matmul(pv[:, :], lhsT=bones, rhs=sq[:], start=True, stop=True),
                  reads=[mats_r, sq_r], writes=[pv_r])
            bd.op("act", lambda e: e.activation(out=rs[:], in_=pv[:, :], func=AF.Sqrt, scale=1.0 / 64,
                                                bias=cst[:, C_LNEPS:C_LNEPS + 1]), reads=[pv_r, cst_r], writes=[rs_r])
            bd.op("dve", lambda e: e.reciprocal(out=rs[:], in_=rs[:]), reads=[rs_r], writes=[rs_r])
            bd.op("dve", lambda e: e.tensor_tensor(out=yc[:], in0=yc[:], in1=rs[:], op=ALU.mult),
                  reads=[yc_r, rs_r], writes=[yc_r])
            bd.op("dve", lambda e: e.tensor_scalar(out=yc[:], in0=yc[:], scalar1=cst[:, C_LNW:C_LNW + 1],
                                                   scalar2=cst[:, C_LNB:C_LNB + 1], op0=ALU.mult, op1=ALU.add),
                  reads=[yc_r, cst_r], writes=[yc_r])
            bd.op("dve", lambda e: e.tensor_tensor(out=yc[:], in0=yc[:], in1=boS[:, sl], op=ALU.add),
                  reads=[yc_r, boS_r], writes=[yc_r])
            o, o_r = yo[tb % 2], yo_r[tb % 2]
            bd.op("dve", lambda e, o=o: e.tensor_tensor(out=o[:], in0=yc[:], in1=gS[:, sl], op=ALU.mult),
                  reads=[yc_r, gS_r], writes=[o_r])
            bd.dma("sp", yT[256:384, sl], o[:], reads=[o_r])
        bd.barrier()


def emit_phaseB(nc, mkbld, es, D_, mixers, tag):
    bd = mkbld(tag + "m")
    sb = lambda nm, shape, dt: es.enter_context(nc.sbuf_tensor(_u(nm), shape, dt))
    ps = lambda nm, shape, dt: es.enter_context(nc.psum_tensor(_u(nm), shape, dt))
    cst = sb("B_cst", [128, NCST], F32)
    cst_r = Res()
    bd.dma("sp", cst[:], D_["cst"].ap(), writes=[cst_r])
    mats = sb("B_mats", [128, 4, 128], F32)
    mats_r = Res()
    bd.dma("sp", mats[:], D_["mats"].ap(), writes=[mats_r])
    identb = sb("B_identb", [128, 128], BF16)
    maskb = sb("B_maskb", [128, 128], BF16)
    identb_r, maskb_r = Res(), Res()
    bd.op("dve", lambda e: e.tensor_copy(out=identb[:], in_=mats[:, 0, :]), reads=[mats_r], writes=[identb_r])
    bd.op("dve", lambda e: e.tensor_copy(out=maskb[:], in_=mats[:, 3, :]), reads=[mats_r], writes=[maskb_r])
    banks = [ps(f"B_bank{i}", [128, 512], F32) for i in range(7)]
    banks_r = [Res() for _ in range(7)]
    ptb = ps("B_ptb", [128, 1024], BF16)
    ptb_r = Res()
    K = dict(banks=banks, banks_r=banks_r, ptb=ptb, ptb_r=ptb_r, identb=identb, identb_r=identb_r,
             mask=maskb, mask_r=maskb_r, ones=mats[:, 1, :], ones_r=mats_r, identf=mats[:, 0, :],
             bones=mats[:, 2, :], mats_r=mats_r)
    if "mla" in mixers:
        with ExitStack() as es1:
            emit_mla(bd, nc, es1, D_["pT"], D_["pos"], D_["wuq"], D_["wukv"], D_["yT"], cst, cst_r, K)
    bd.barrier()
    if "mlstm" in mixers:
        with ExitStack() as es1:
            emit_mlstm(bd, nc, es1, D_["pT"], D_["yT"], cst, cst_r, K)
        bd.barrier()
    if "rwkv" in mixers:
        bd = mkbld(tag + "r")
        with ExitStack() as es1:
            emit_rwkv(bd, nc, es1, D_, cst, cst_r, K)
        bd.barrier()


def prep_phaseB_consts(P, l, hc):
    c = np.zeros((128, NCST), np.float32)
    c[:, C_QN0] = P["mla_q_norm"][l][0:128]
    c[:, C_QN1] = P["mla_q_norm"][l][128:256]
    c[:, C_KVN0] = P["mla_kv_norm"][l][0:128]
    c[:, C_KVN1] = P["mla_kv_norm"][l][128:256]
    c[:, C_INVF] = np.tile(INV_FREQ, 4)
    c[:, C_SGN] = np.tile(np.concatenate([np.full(32, -1.0, np.float32), np.full(32, 1.0, np.float32)]), 2)
    for h in range(2):
        hh = 2 * hc + h
        c[:, C_MLAO0 + h] = P["mla_out_norm"][l][hh * 128:(hh + 1) * 128]
    mu = P["rwkv_mu"][l]
    ch = slice(hc * 128, hc * 128 + 128)
    c[:, C_MU_R] = mu[0:256][ch]
    c[:, C_MU_K] = mu[256:512][ch]
    c[:, C_MU_V] = mu[512:768][ch]
    c[:, C_MU_L] = mu[768:896]
    c[:, C_W0] = P["rwkv_w0"][l][ch]
    c[:, C_A0] = P["rwkv_a0"][l][ch]
    c[:, C_KK] = P["rwkv_k_k"][l][ch]
    c[:, C_KA] = P["rwkv_k_a"][l][ch]
    c[:, C_RK] = P["rwkv_r_k"][l][ch]
    c[:, C_LNW] = P["rwkv_ln_w"][l][ch]
    c[:, C_LNB] = P["rwkv_ln_b"][l][ch]
    cw = P["mlstm_conv_w"][l]
    cb = P["mlstm_conv_b"][l]
    qs = slice(hc * 64, hc * 64 + 64)
    ks = slice(128 + hc * 64, 128 + hc * 64 + 64)
    for j in range(4):
        c[0:64, C_CWQ0 + j] = cw[j][qs]
        c[0:64, C_CWK0 + j] = cw[j][ks]
    c[0:64, C_CBQ] = cb[qs]
    c[0:64, C_CBK] = cb[ks]
    c[:, C_IB] = P["mlstm_i_bias"][l][hc * 2]
    c[:, C_FB] = P["mlstm_f_bias"][l][hc * 2]
    c[:, C_IB1] = P["mlstm_i_bias"][l][hc * 2 + 1]
    c[:, C_FB1] = P["mlstm_f_bias"][l][hc * 2 + 1]
    c[:, C_MLO] = P["mlstm_out_norm"][l][ch]
    c[:, C_EPS] = EPS
    c[:, C_LNEPS] = 64e-5
    c[:, C_ONE] = 1.0
    hq = []
    hkv = []
    for h in range(2):
        hh = 2 * hc + h
        wq = P["mla_w_uq"][l][:, hh * 192:(hh + 1) * 192]
        hq += [wq[:, 0:128], wq[:, 128:192], wq[:, 160:192], wq[:, 128:160]]
        wkv = P["mla_w_ukv"][l][:, hh * 256:(hh + 1) * 256]
        hkv += [wkv]
    wuq = np.ascontiguousarray(np.concatenate(hq, axis=1))
    wukv = np.ascontiguousarray(np.concatenate(hkv, axis=1))
    wl = np.zeros((128, 3, 128), np.float32)
    wl[0:32, 0, :] = P["rwkv_w2"][l][:, ch]
    wl[32:64, 1, :] = P["rwkv_a2"][l][:, ch]
    wl[64:128, 2, :] = P["rwkv_g2"][l][:, ch]
    return dict(cst=c, wuq=wuq, wukv=wukv, wl=wl)


INV_FREQ = (10000.0 ** (-np.arange(0, 64, 2, dtype=np.float32) / np.float32(64))).astype(np.float32)


def const_mats():
    m = np.zeros((128, 4, 128), np.float32)
    m[:, 0, :] = np.eye(128, dtype=np.float32)
    m[:, 1, :] = 1.0
    m[0:64, 2, 0:64] = 1.0
    m[64:128, 2, 64:128] = 1.0
    m[:, 3, :] = (np.arange(128)[:, None] <= np.arange(128)[None, :]).astype(np.float32)
    return m


W_OUT_PERM = (list(range(0, 256)) + list(range(512, 640)) + list(range(768, 896)) +
              list(range(256, 512)) + list(range(640, 768)) + list(range(896, 1024)))
TBC = 256


def emit_phaseC(nc, bd, es, D_, final, ntok):
    sb = lambda name, shape, dt: es.enter_context(nc.sbuf_tensor(_u(name), shape, dt))
    ps = lambda name, shape, dt: es.enter_context(nc.psum_tensor(_u(name), shape, dt))
    wo = sb("C_wo", [128, 8, D], BF16)
    wg = sb("C_wg", [128, 8, DFF], BF16)
    wu = sb("C_wu", [128, 8, DFF], BF16)
    wd = sb("C_wd", [128, 22, D], BF16)
    w_r = Res()
    HS = DFF // 2
    stage = [sb(f"C_stage{i}", [128, HS], F32) for i in range(3)]
    stage_r = [Res() for _ in range(3)]
    gcol = sb("C_gcol", [128, 8], F32)
    gcol_r = Res()
    idf = sb("C_idf", [128, 128], F32)
    ident = sb("C_ident", [128, 128], BF16)
    idf_r, ident_r = Res(), Res()
    bd.dma("sp", gcol[:], D_["g"].ap(), writes=[gcol_r])
    bd.dma("sp", idf[:], D_["ident"].ap(), writes=[idf_r])
    bd.op("dve", lambda e: e.tensor_copy(out=ident[:], in_=idf[:]), reads=[idf_r], writes=[ident_r])
    if final:
        gbc = sb("C_gbc", [128, D], F32)
        gbc_r = Res()
        bd.dma("sp", gbc[:], bass.AP(D_["gf_t"], 0, [[0, 128], [1, D]]), writes=[gbc_r])
    si = [0]
    queues = ("sp", "pool", "act")
    engs = ("dve", "pool", "act")

    def cast(dst_ap, src_ap, ncols, scale_ap=None):
        i = si[0] % 3
        si[0] += 1
        bd.dma(queues[i], stage[i][:, 0:ncols], src_ap, writes=[stage_r[i]])
        eng = engs[i] if scale_ap is None else ("dve" if i != 2 else "act")
        if eng == "act":
            if scale_ap is None:
                bd.op("act", lambda e: e.activation(out=dst_ap, in_=stage[i][:, 0:ncols], func=AF.Copy),
                      reads=[stage_r[i]], writes=[w_r])
            else:
                bd.op("act", lambda e: e.activation(out=dst_ap, in_=stage[i][:, 0:ncols], func=AF.Copy, scale=scale_ap),
                      reads=[stage_r[i], gcol_r], writes=[w_r])
        elif scale_ap is None:
            bd.op(eng, lambda e: e.tensor_copy(out=dst_ap, in_=stage[i][:, 0:ncols]), reads=[stage_r[i]], writes=[w_r])
        else:
            bd.op(eng, lambda e: e.tensor_scalar(out=dst_ap, in0=stage[i][:, 0:ncols], scalar1=scale_ap, scalar2=None,
                                                 op0=ALU.mult), reads=[stage_r[i], gcol_r], writes=[w_r])

    for kc in range(8):
        cast(wo[:, kc, :], D_["wo"].ap()[kc * 128:(kc + 1) * 128, :], D)
    for kc in range(8):
        for hh in range(2):
            cast(wg[:, kc, hh * HS:(hh + 1) * HS], D_["wg"].ap()[kc * 128:(kc + 1) * 128, hh * HS:(hh + 1) * HS], HS,
                 gcol[:, kc:kc + 1])
            cast(wu[:, kc, hh * HS:(hh + 1) * HS], D_["wu"].ap()[kc * 128:(kc + 1) * 128, hh * HS:(hh + 1) * HS], HS,
                 gcol[:, kc:kc + 1])
    for f in range(22):
        cast(wd[:, f, :], D_["wd"].ap()[f * 128:(f + 1) * 128, :], D)

    NJ = TBC // 128
    xs = sb("C_xs", [128, NJ, D], F32)
    xs_r = [Res() for _ in range(NJ)]
    yTb = sb("C_yTb", [128, 8, TBC], BF16)
    yTb_r = Res()
    hT = sb("C_hT", [128, 8, TBC], BF16)
    hT_r = Res()
    AT = sb("C_AT", [128, 22, TBC], BF16)
    AT_r = Res()
    sgt = [sb(f"C_sg{i}", [128, TBC], F32) for i in range(2)]
    sgt_r = [Res(), Res()]
    tmp = dict(ss=sb("C_ss", [128, 4], F32), ss_r=Res(), junk=sb("C_junk", [128, D], BF16), junk_r=Res(),
               xn=sb("C_xn", [128, D], BF16), xn_r=Res(),
               pt=ps("C_pt", [128, D], BF16), pt_r=Res(), eps=sb("C_eps", [128, 1], F32), eps_r=Res())
    bd.op("dve", lambda e: e.memset(tmp["eps"][:], EPS), writes=[tmp["eps_r"]])
    pb = [ps(f"C_pb{i}", [128, 512], F32) for i in range(6)]
    pb_r = [Res() for _ in range(6)]
    x_ap = D_["x"].ap()
    yT_ap = D_["yT"].ap()
    out_ap = D_["out"].ap()
    for blk in range(ntok // TBC):
        t0 = blk * TBC
        bd.dma("sp", yTb[:], yT_ap[:, t0:t0 + TBC].rearrange("(k p) t -> p k t", p=128), writes=[yTb_r])
        for j in range(NJ):
            bd.dma("pool", xs[:, j, :], x_ap[t0 + j * 128:t0 + (j + 1) * 128, :], writes=[xs_r[j]])
            for ch in range(2):
                po, po_r = pb[ch], pb_r[ch]
                for k in range(8):
                    bd.op("pe", lambda e, k=k, j=j, ch=ch, po=po: e.matmul(
                        po[:, :], lhsT=yTb[:, k, j * 128:(j + 1) * 128], rhs=wo[:, k, ch * 512:(ch + 1) * 512],
                        start=(k == 0), stop=(k == 7)), reads=[yTb_r, w_r], writes=[po_r])
                bd.op("dve", lambda e, j=j, ch=ch, po=po: e.tensor_tensor(
                    out=xs[:, j, ch * 512:(ch + 1) * 512], in0=xs[:, j, ch * 512:(ch + 1) * 512], in1=po[:, :], op=ALU.add),
                    reads=[po_r, xs_r[j]], writes=[xs_r[j]])
            emit_rmsnorm_T(bd, xs[:, j, :], xs_r[j], hT, hT_r, j, tmp, ident)
        for f in range(22):
            pg, pg_r = pb[2 + (f % 2) * 2], pb_r[2 + (f % 2) * 2]
            pu, pu_r = pb[3 + (f % 2) * 2], pb_r[3 + (f % 2) * 2]
            for (pp, pp_r, ww) in ((pg, pg_r, wg), (pu, pu_r, wu)):
                for k in range(8):
                    bd.op("pe", lambda e, k=k, f=f, pp=pp, ww=ww: e.matmul(
                        pp[:, 0:TBC], lhsT=ww[:, k, f * 128:(f + 1) * 128], rhs=hT[:, k, :],
                        start=(k == 0), stop=(k == 7)), reads=[w_r, hT_r], writes=[pp_r])
            sg, sg_r = sgt[f % 2], sgt_r[f % 2]
            bd.op("act", lambda e, sg=sg, pg=pg: e.activation(out=sg[:], in_=pg[:, 0:TBC], func=AF.Silu),
                  reads=[pg_r], writes=[sg_r])
            bd.op("dve", lambda e, sg=sg, pu=pu, f=f: e.tensor_tensor(out=AT[:, f, :], in0=sg[:], in1=pu[:, 0:TBC],
                                                                      op=ALU.mult),
                  reads=[sg_r, pu_r], writes=[AT_r])
        for j in range(NJ):
            for ch in range(2):
                pd, pd_r = pb[ch], pb_r[ch]
                for f in range(22):
                    bd.op("pe", lambda e, f=f, j=j, ch=ch, pd=pd: e.matmul(
                        pd[:, :], lhsT=AT[:, f, j * 128:(j + 1) * 128], rhs=wd[:, f, ch * 512:(ch + 1) * 512],
                        start=(f == 0), stop=(f == 21)), reads=[AT_r, w_r], writes=[pd_r])
                bd.op("dve", lambda e, j=j, ch=ch, pd=pd: e.tensor_tensor(
                    out=xs[:, j, ch * 512:(ch + 1) * 512], in0=xs[:, j, ch * 512:(ch + 1) * 512], in1=pd[:, :], op=ALU.add),
                    reads=[pd_r, xs_r[j]], writes=[xs_r[j]])
            if final:
                ss, ss_r = tmp["ss"], tmp["ss_r"]
                bd.op("dve", lambda e, j=j: e.scalar_tensor_tensor(
                    out=tmp["junk"][:], in0=xs[:, j, :], scalar=1.0, in1=xs[:, j, :], op0=ALU.mult, op1=ALU.mult,
                    accum_out=ss[:, 0:1]), reads=[xs_r[j]], writes=[tmp["junk_r"], ss_r])
                bd.op("act", lambda e: e.activation(out=ss[:, 1:2], in_=ss[:, 0:1], func=AF.Sqrt, scale=1.0 / D,
                                                    bias=tmp["eps"][:, 0:1]), reads=[ss_r, tmp["eps_r"]], writes=[ss_r])
                bd.op("dve", lambda e: e.reciprocal(out=ss[:, 2:3], in_=ss[:, 1:2]), reads=[ss_r], writes=[ss_r])
                bd.op("dve", lambda e, j=j: e.scalar_tensor_tensor(
                    out=xs[:, j, :], in0=xs[:, j, :], scalar=ss[:, 2:3], in1=gbc[:], op0=ALU.mult, op1=ALU.mult),
                    reads=[xs_r[j], ss_r, gbc_r], writes=[xs_r[j]])
            bd.dma("sp", out_ap[t0 + j * 128:t0 + (j + 1) * 128, :], xs[:, j, :], reads=[xs_r[j]])
    bd.barrier()


def build_fused(phases=None):
    nc = bass.Bass("TRN2", target_bir_lowering=False)
    dt = nc.dram_tensor
    x_d = dt("x", [SEQ, D], F32, kind="ExternalInput")
    pos_d = dt("pos", [1, SEQ], I32, kind="ExternalInput")
    wA_d = dt("wA", [2, D, NCOLA], F32, kind="ExternalInput")
    gA_d = dt("gA", [2, 128, 8], F32, kind="ExternalInput")
    id_d = dt("ident", [128, 128], F32, kind="ExternalInput")
    mats_d = dt("mats", [128, 4, 128], F32, kind="ExternalInput")
    wuq_d = dt("wuq", [2, 2, 256, 512], F32, kind="ExternalInput")
    wukv_d = dt("wukv", [2, 2, 256, 512], F32, kind="ExternalInput")
    cst_d = dt("cst", [2, 2, 128, NCST], F32, kind="ExternalInput")
    wl_d = dt("wl", [2, 2, 128, 3, 128], F32, kind="ExternalInput")
    wo_d = dt("wo", [2, D, D], F32, kind="ExternalInput")
    wg_d = dt("wg", [2, D, DFF], F32, kind="ExternalInput")
    wu_d = dt("wu", [2, D, DFF], F32, kind="ExternalInput")
    wd_d = dt("wd", [2, DFF, D], F32, kind="ExternalInput")
    gC_d = dt("gC", [2, 128, 8], F32, kind="ExternalInput")
    gf_d = dt("gf", [1, D], F32, kind="ExternalInput")
    out_d = dt("out", [SEQ, D], F32, kind="ExternalOutput")
    pT_d = dt("pT_scr", [NCOLA, SEQ], F32)
    yT_d = dt("yT_scr", [D, SEQ], BF16)
    scr_d = dt("rw_scr", [2 * SEQ * 320], F32)
    x1_d = dt("x1_scr", [SEQ, D], F32)
    shared = {}
    mkbld = lambda tag: Bld(nc, tag, shared)
    for l in range(2):
        x_in = x_d if l == 0 else x1_d
        x_out = x1_d if l == 0 else out_d
        if phases is None or f"A{l}" in phases:
          with ExitStack() as es:
            bd = mkbld(f"A{l}")
            emit_phaseA(nc, bd, es, x_in, View(wA_d.ap()[l]), View(gA_d.ap()[l]), id_d, pT_d, SEQ)
        for hc in range(2):
            if phases is not None and f"B{l}{hc}" not in phases:
                continue
            D_ = dict(pT=View(pT_d.ap()[hc * HALF_ROWS:(hc + 1) * HALF_ROWS, :]), pos=pos_d,
                      wuq=View(wuq_d.ap()[l, hc]), wukv=View(wukv_d.ap()[l, hc]), cst=View(cst_d.ap()[l, hc]),
                      wl=View(wl_d.ap()[l, hc]), mats=mats_d, yT=View(yT_d.ap()[hc * 512:(hc + 1) * 512, :]),
                      scr=scr_d)
            with ExitStack() as es:
                emit_phaseB(nc, mkbld, es, D_, ("mla", "mlstm", "rwkv"), f"B{l}{hc}")
        D_ = dict(x=x_in, yT=yT_d, wo=View(wo_d.ap()[l]), wg=View(wg_d.ap()[l]), wu=View(wu_d.ap()[l]),
                  wd=View(wd_d.ap()[l]), g=View(gC_d.ap()[l]), gf_t=gf_d, ident=id_d, out=x_out)
        if phases is None or f"C{l}" in phases:
          with ExitStack() as es:
            bd = mkbld(f"C{l}")
            emit_phaseC(nc, bd, es, D_, l == 1, SEQ)
    return nc


_CACHE = {}
_PHASES = None


def kernel(**inputs):
    P = {k: np.asarray(v) for k, v in inputs.items()}
    x = np.asarray(P["x"], np.float32)
    positions = np.asarray(P["positions"]).astype(np.int32)
    if "nc" not in _CACHE:
        _CACHE["nc"] = build_fused(_PHASES)
    nc = _CACHE["nc"]
    f32 = lambda a: np.ascontiguousarray(np.asarray(a, np.float32))
    wA = f32(np.stack([prep_w_inA(P["w_in"][l]) for l in range(2)]))
    gA = f32(np.stack([_gcol(P["mix_norm"][l]) for l in range(2)]))
    gC = f32(np.stack([_gcol(P["ffn_norm"][l]) for l in range(2)]))
    pb = [[prep_phaseB_consts(P, l, hc) for hc in range(2)] for l in range(2)]
    stk = lambda key: f32(np.stack([np.stack([pb[l][hc][key] for hc in range(2)]) for l in range(2)]))
    common = dict(
        wA=wA, gA=gA, ident=np.eye(128, dtype=np.float32), mats=const_mats(),
        wuq=stk("wuq"), wukv=stk("wukv"), cst=stk("cst"), wl=stk("wl"),
        wo=f32(np.stack([P["w_out"][l][W_OUT_PERM, :] for l in range(2)])),
        wg=f32(P["w_gate"]), wu=f32(P["w_up"]), wd=f32(P["w_down"]), gC=gC,
        gf=f32(np.asarray(P["final_norm"]).reshape(1, D)))
    in_maps = []
    for c in range(NCORES):
        b = c // 2
        m = dict(common)
        m["x"] = f32(x[b])
        m["pos"] = np.ascontiguousarray(positions[b:b + 1])
        in_maps.append(m)
    res = run_bass_kernel_spmd(nc, in_maps, core_ids=list(range(NCORES)))
    out = np.zeros((4, SEQ, D), np.float32)
    for b in range(4):
        out[b] = res.results[2 * b]["out"]
    return out
```

```python
from contextlib import ExitStack
import numpy as np
import concourse.bass as bass
import concourse.mybir as mybir
from concourse.bass_utils import run_bass_kernel_spmd

F32 = mybir.dt.float32
BF16 = mybir.dt.bfloat16
I32 = mybir.dt.int32
ALU = mybir.AluOpType
AF = mybir.ActivationFunctionType

D = 1024
SEQ = 4096
NTOK = 2048
DFF = 2816
NCORES = 8
EPS = 1e-6

NCH = 13
CH_ROWS = [128] * 12 + [4]
HALF_ROWS = 12 * 128 + 4
CH_OFF = [i * 128 for i in range(13)]
NCOLA = 2 * HALF_ROWS


def half_cols(hc):
    cols = []
    cols += list(range(0, 256))
    cols += list(range(256, 512))
    cols += list(range(512, 576))
    cols += list(range(544, 576)) + list(range(512, 544))
    R0 = 576
    for part in range(3):
        cols += list(range(R0 + part * 256 + hc * 128, R0 + part * 256 + hc * 128 + 128))
    cols += list(range(R0 + 768, R0 + 896))
    M0 = 576 + 896
    cols += list(range(M0 + hc * 64, M0 + hc * 64 + 64))
    cols += list(range(M0 + 128 + hc * 64, M0 + 128 + hc * 64 + 64))
    cols += list(range(M0 + 256 + hc * 128, M0 + 256 + hc * 128 + 128))
    cols += list(range(M0 + 520 + hc * 128, M0 + 520 + hc * 128 + 128))
    cols += list(range(M0 + 512 + hc * 2, M0 + 512 + hc * 2 + 2))
    cols += list(range(M0 + 516 + hc * 2, M0 + 516 + hc * 2 + 2))
    assert len(cols) == HALF_ROWS
    return cols


class Res:
    __slots__ = ("w", "r", "name")

    def __init__(self, name=""):
        self.w = None
        self.r = {}
        self.name = name


class Bld:
    NDMA = 8

    def __init__(self, nc, tag="", shared=None):
        self.nc = nc
        self.E = {"pe": nc.tensor, "dve": nc.vector, "act": nc.scalar, "pool": nc.gpsimd, "sp": nc.sync}
        self.sems = {}
        self.cnt = {}
        self.seen = {e: {} for e in self.E}
        self.touched = set()
        for e in self.E:
            self.sems[e] = nc.alloc_semaphore(name=f"s{tag}_{e}")
            self.cnt[e] = 0
        if shared is not None and "sems" in shared:
            self.sems.update(shared["sems"])
            self.cnt.update(shared["cnt"])
            self.dslot = shared["dslot"]
        else:
            dsems, dcnt = {}, {}
            self.dslot = {}
            for q in ("sp", "act", "pool"):
                for i in range(self.NDMA):
                    k = f"d{q}{i}"
                    dsems[k] = nc.alloc_semaphore(name=f"sdma_{k}")
                    dcnt[k] = 0
                self.dslot[q] = 0
            self.sems.update(dsems)
            self.cnt.update(dcnt)
            if shared is not None:
                shared["sems"] = dsems
                shared["dslot"] = self.dslot
                shared["cnt"] = {}
        self.shared = shared

    def _sync_shared(self):
        if self.shared is not None:
            for k in self.shared["sems"]:
                self.shared["cnt"][k] = self.cnt[k]

    def _wait(self, eng, deps):
        best = {}
        for k, v in deps:
            if v > best.get(k, 0):
                best[k] = v
        for k, v in best.items():
            if self.seen[eng].get(k, 0) >= v:
                continue
            self.E[eng].wait_ge(self.sems[k], v)
            self.seen[eng][k] = v

    def _deps(self, eng, reads, writes):
        deps = []
        for r in reads:
            if r.w is not None:
                if not (eng == "pe" and r.w[0] == "pe"):
                    deps.append(r.w)
        for w in writes:
            if w.w is not None and (w.w[0] != eng or eng != "pe"):
                deps.append(w.w)
            for k, v in w.r.items():
                if k != eng or eng != "pe":
                    deps.append((k, v))
        return deps

    def _mark(self, ev, reads, writes):
        self.touched.update(reads)
        self.touched.update(writes)
        for r in reads:
            if ev[1] > r.r.get(ev[0], 0):
                r.r[ev[0]] = ev[1]
        for w in writes:
            w.w = ev
            w.r = {}

    def op(self, eng, fn, reads=(), writes=(), ser=False):
        self._wait(eng, self._deps(eng, reads, writes))
        ins = fn(self.E[eng])
        self.cnt[eng] += 1
        ins.then_inc(self.sems[eng], 1)
        self._mark((eng, self.cnt[eng]), reads, writes)
        if ser:
            self._wait(eng, [(eng, self.cnt[eng])])
        return ins

    def dma(self, q, out, in_, reads=(), writes=()):
        i = self.dslot[q]
        self.dslot[q] = (i + 1) % self.NDMA
        k = f"d{q}{i}"
        deps = self._deps(k, reads, writes)
        deps.append((k, self.cnt[k]))
        self._wait(q, deps)
        ins = self.E[q].dma_start(out=out, in_=in_)
        self.cnt[k] += 16
        ins.then_inc(self.sems[k], 16)
        self._mark((k, self.cnt[k]), reads, writes)
        return ins

    def barrier(self):
        deps = [(k, v) for k, v in self.cnt.items() if v > 0]
        for e in ("sp", "pe", "dve", "act", "pool"):
            self._wait(e, deps)
        for e in ("sp", "pe", "dve", "act", "pool"):
            for k, v in deps:
                assert self.seen[e].get(k, 0) >= v
        for r in self.touched:
            r.w = None
            r.r = {}
        self.touched = set()
        self._sync_shared()

    def wait_all(self, eng, ress):
        deps = []
        for r in ress:
            if r.w is not None:
                deps.append(r.w)
        self._wait(eng, deps)


def dram_ap(t, offset, pattern):
    return bass.AP(t, offset, [list(p) for p in pattern])


def emit_rmsnorm_T(bd, x_tile, x_res, hT, hT_res, j, tmp, ident):
    ss, ss_r = tmp["ss"], tmp["ss_r"]
    junk, junk_r = tmp["junk"], tmp["junk_r"]
    xn, xn_r = tmp["xn"], tmp["xn_r"]
    pt, pt_r = tmp["pt"], tmp["pt_r"]
    bd.op("dve", lambda e: e.scalar_tensor_tensor(out=junk[:], in0=x_tile, scalar=1.0, in1=x_tile,
                                                  op0=ALU.mult, op1=ALU.mult, accum_out=ss[:, 0:1]),
          reads=[x_res], writes=[junk_r, ss_r])
    bd.op("act", lambda e: e.activation(out=ss[:, 1:2], in_=ss[:, 0:1], func=AF.Sqrt, scale=1.0 / D,
                                        bias=tmp["eps"][:, 0:1]), reads=[ss_r, tmp["eps_r"]], writes=[ss_r])
    bd.op("dve", lambda e: e.reciprocal(out=ss[:, 2:3], in_=ss[:, 1:2]), reads=[ss_r], writes=[ss_r])
    bd.op("act", lambda e: e.activation(out=xn[:], in_=x_tile, func=AF.Copy, scale=ss[:, 2:3]),
          reads=[x_res, ss_r], writes=[xn_r])
    for kc in range(8):
        bd.op("pe", lambda e, kc=kc: e.transpose(out=pt[:, kc * 128:(kc + 1) * 128],
                                                 in_=xn[:, kc * 128:(kc + 1) * 128], identity=ident[:]),
              reads=[xn_r], writes=[pt_r])
    bd.op("act", lambda e: e.activation(out=hT[:, :, j * 128:(j + 1) * 128],
                                        in_=pt[:].rearrange("p (k t) -> p k t", k=8), func=AF.Copy),
          reads=[pt_r], writes=[hT_res])


_UC = [0]


def _u(name):
    _UC[0] += 1
    return f"{name}_{_UC[0]}"


class View:
    def __init__(self, ap):
        self._ap = ap

    def ap(self):
        return self._ap


def emit_phaseA(nc, bd, es, x_d, w_d, g_d, id_d, pT_d, ntok):
    sb = lambda name, shape, dt: es.enter_context(nc.sbuf_tensor(_u(name), shape, dt))
    ps = lambda name, shape, dt: es.enter_context(nc.psum_tensor(_u(name), shape, dt))
    wb = sb("A_wb", [128, 8, NCOLA], BF16)
    wb_r = [Res() for _ in range(8)]
    stage = [sb(f"A_stage{i}", [128, NCOLA], F32) for i in range(2)]
    stage_r = [Res(), Res()]
    gcol = sb("A_gcol", [128, 8], F32)
    gcol_r = Res()
    idf = sb("A_idf", [128, 128], F32)
    ident = sb("A_ident", [128, 128], BF16)
    ident_r = Res()
    idf_r = Res()
    xt = [sb(f"A_xt{i}", [128, D], F32) for i in range(2)]
    xt_r = [Res(), Res()]
    hT = [sb(f"A_hT{i}", [128, 8, 512], BF16) for i in range(2)]
    hT_r = [Res(), Res()]
    tmp = dict(ss=sb("A_ss", [128, 4], F32), ss_r=Res(), junk=sb("A_junk", [128, D], BF16), junk_r=Res(),
               xn=sb("A_xn", [128, D], BF16), xn_r=Res(),
               pt=ps("A_pt", [128, D], BF16), pt_r=Res(), eps=sb("A_eps", [128, 1], F32), eps_r=Res())
    bd.op("dve", lambda e: e.memset(tmp["eps"][:], EPS), writes=[tmp["eps_r"]])
    NPB = 4
    pb = [ps(f"A_pb{i}", [128, 512], F32) for i in range(NPB)]
    pb_r = [Res() for _ in range(NPB)]
    ost = [sb(f"A_ost{i}", [128, 512], F32) for i in range(4)]
    ost_r = [Res() for _ in range(4)]

    bd.dma("sp", gcol[:], g_d.ap(), writes=[gcol_r])
    bd.dma("sp", idf[:], id_d.ap(), writes=[idf_r])
    bd.op("dve", lambda e: e.tensor_copy(out=ident[:], in_=idf[:]), reads=[idf_r], writes=[ident_r])
    for kc in range(8):
        s = kc % 2
        bd.dma("pool", stage[s][:], w_d.ap()[kc * 128:(kc + 1) * 128, :], writes=[stage_r[s]])
        bd.op("dve", lambda e, kc=kc, s=s: e.tensor_scalar(out=wb[:, kc, :], in0=stage[s][:],
                                                          scalar1=gcol[:, kc:kc + 1], scalar2=None, op0=ALU.mult),
              reads=[stage_r[s], gcol_r], writes=[wb_r[kc]])
    x_ap = x_d.ap()
    pT_ap = pT_d.ap()
    nblk = ntok // 512
    oi = 0
    for blk in range(nblk):
        hb = blk % 2
        for j in range(4):
            ti = blk * 4 + j
            xb = ti % 2
            bd.dma("sp", xt[xb][:], x_ap[ti * 128:(ti + 1) * 128, :], writes=[xt_r[xb]])
            tmp2 = dict(tmp)
            emit_rmsnorm_T(bd, xt[xb][:], xt_r[xb], hT[hb], hT_r[hb], j, tmp2, ident)
        for half in range(2):
            for c in range(NCH):
                m = CH_ROWS[c]
                col0 = half * HALF_ROWS + CH_OFF[c]
                pbi = oi % NPB
                for kc in range(8):
                    bd.op("pe", lambda e, kc=kc, col0=col0, m=m, pbi=pbi, hb=hb: e.matmul(
                        pb[pbi][0:m, :], lhsT=wb[:, kc, col0:col0 + m], rhs=hT[hb][:, kc, :],
                        start=(kc == 0), stop=(kc == 7)),
                        reads=[wb_r[kc], hT_r[hb]], writes=[pb_r[pbi]])
                osi = oi % 4
                eng = "act" if oi % 2 == 0 else "dve"
                if eng == "act":
                    bd.op("act", lambda e, m=m, pbi=pbi, osi=osi: e.activation(
                        out=ost[osi][0:m, :], in_=pb[pbi][0:m, :], func=AF.Copy),
                        reads=[pb_r[pbi]], writes=[ost_r[osi]])
                else:
                    bd.op("dve", lambda e, m=m, pbi=pbi, osi=osi: e.tensor_copy(
                        out=ost[osi][0:m, :], in_=pb[pbi][0:m, :]),
                        reads=[pb_r[pbi]], writes=[ost_r[osi]])
                bd.dma("sp", pT_ap[col0:col0 + m, blk * 512:(blk + 1) * 512], ost[osi][0:m, :],
                       reads=[ost_r[osi]])
                oi += 1
    bd.barrier()


def _gcol(g):
    return np.ascontiguousarray(np.asarray(g, np.float32).reshape(8, 128).T)


def prep_w_inA(w_in_l):
    cols = half_cols(0) + half_cols(1)
    return np.ascontiguousarray(w_in_l[:, cols])


TWO_PI = 6.283185307179586
(C_QN0, C_QN1, C_KVN0, C_KVN1, C_INVF, C_SGN, C_MLAO0, C_MLAO1,
 C_MU_R, C_MU_K, C_MU_V, C_MU_L, C_W0, C_A0, C_KK, C_KA, C_RK, C_LNW, C_LNB,
 C_CWQ0, C_CWQ1, C_CWQ2, C_CWQ3, C_CBQ, C_CWK0, C_CWK1, C_CWK2, C_CWK3, C_CBK,
 C_IB, C_FB, C_MLO, C_EPS, C_LNEPS, C_ONE, C_ZERO, C_IB1, C_FB1) = range(38)
NCST = 38


def emit_attention(bd, nc, es, name, heads, scale_exp, dv, wfun, out_fn, PS):
    sb = lambda nm, shape, dt: es.enter_context(nc.sbuf_tensor(_u(nm), shape, dt))
    NPT = 3
    pT = [sb(f"{name}_pT{i}", [128, 512], BF16) for i in range(NPT)]
    pT_r = [Res() for _ in range(NPT)]
    mask = PS["mask"]
    mask_r = PS["mask_r"]
    it = 0
    for h in heads:
        for qb in range(SEQ // 512):
            nkt = 4 * (qb + 1)
            for kt in range(nkt):
                st, st_r = PS["st"][it % 2]
                kp = PS["kparts"](h, kt)
                qp = PS["qparts"](h, qb)
                n = len(kp)
                for i in range(n):
                    bd.op("pe", lambda e, i=i, st=st: e.matmul(st[:, :], lhsT=kp[i][0], rhs=qp[i][0],
                                                              start=(i == 0), stop=(i == n - 1)),
                          reads=[kp[i][1], qp[i][1]], writes=[st_r])
                p, p_r = pT[it % NPT], pT_r[it % NPT]
                wfun(h, kt, qb, st, st_r, p, p_r)
                jd = kt - 4 * qb
                if jd >= 0:
                    bd.op("pool", lambda e, p=p, jd=jd: e.tensor_tensor(
                        out=p[:, jd * 128:(jd + 1) * 128], in0=p[:, jd * 128:(jd + 1) * 128], in1=mask[:],
                        op=ALU.mult), reads=[p_r, mask_r], writes=[p_r])
                v_ap, v_r = PS["v"](h, kt)
                for j in range(4):
                    qt = 4 * qb + j
                    if qt < kt:
                        continue
                    o, o_r = PS["o"][j]
                    bd.op("pe", lambda e, o=o, p=p, j=j, v_ap=v_ap, kt=kt, qt=qt: e.matmul(
                        o[:, 0:dv + 1], lhsT=p[:, j * 128:(j + 1) * 128], rhs=v_ap,
                        start=(kt == 0), stop=(kt == qt)), reads=[p_r, v_r], writes=[o_r])
                    if kt == qt:
                        out_fn(h, qt, o, o_r)
                it += 1


def emit_rope_tables(bd, nc, es, pos_d, cst, cst_r, CS, SN, tab_r):
    with ExitStack() as es2:
        sb = lambda nm, shape, dt: es2.enter_context(nc.sbuf_tensor(_u(nm), shape, dt))
        ti = sb("rt_i", [64, SEQ], I32)
        ta = sb("rt_a", [64, SEQ], F32)
        tb = sb("rt_b", [64, SEQ], F32)
        ti_r, ta_r, tb_r = Res(), Res(), Res()
        src = bass.AP(pos_d, 0, [[0, 64], [1, SEQ]])
        bd.dma("sp", ti[:], src, writes=[ti_r])
        bd.op("dve", lambda e: e.tensor_copy(out=ta[:], in_=ti[:]), reads=[ti_r], writes=[ta_r])
        bd.op("dve", lambda e: e.tensor_scalar(out=ta[:], in0=ta[:], scalar1=cst[0:64, C_INVF:C_INVF + 1],
                                               scalar2=None, op0=ALU.mult), reads=[ta_r, cst_r], writes=[ta_r])
        for which in (0, 1):
            shift = 0.0 if which == 0 else TWO_PI / 4
            bd.op("dve", lambda e: e.tensor_scalar(out=tb[:], in0=ta[:], scalar1=shift, scalar2=1.0 / TWO_PI,
                                                   op0=ALU.add, op1=ALU.mult), reads=[ta_r], writes=[tb_r])
            bd.op("dve", lambda e: e.tensor_copy(out=ti[:], in_=tb[:]), reads=[tb_r], writes=[ti_r])
            bd.op("dve", lambda e: e.tensor_copy(out=tb[:], in_=ti[:]), reads=[ti_r], writes=[tb_r])
            bd.op("dve", lambda e: e.scalar_tensor_tensor(out=tb[:], in0=tb[:], scalar=-TWO_PI, in1=ta[:],
                                                          op0=ALU.mult, op1=ALU.add),
                  reads=[tb_r, ta_r], writes=[tb_r])
            bd.op("dve", lambda e: e.tensor_scalar(out=tb[:], in0=tb[:], scalar1=shift, scalar2=TWO_PI / 2,
                                                   op0=ALU.add, op1=ALU.min), reads=[tb_r], writes=[tb_r])
            bd.op("dve", lambda e: e.tensor_scalar(out=tb[:], in0=tb[:], scalar1=-TWO_PI / 2, scalar2=None,
                                                   op0=ALU.max), reads=[tb_r], writes=[tb_r])
            if which == 0:
                bd.op("act", lambda e: e.activation(out=SN[:], in_=tb[:], func=AF.Sin,
                                                    scale=cst[0:64, C_SGN:C_SGN + 1]),
                      reads=[tb_r, cst_r], writes=[tab_r])
            else:
                bd.op("act", lambda e: e.activation(out=CS[:], in_=tb[:], func=AF.Sin),
                      reads=[tb_r], writes=[tab_r])
        bd.barrier()


def emit_mla(bd, nc, es, pT_d, pos_d, wuq_d, wukv_d, yT_d, cst, cst_r, K):
    sb = lambda nm, shape, dt: es.enter_context(nc.sbuf_tensor(_u(nm), shape, dt))
    pT = pT_d.ap()
    yT = yT_d.ap()
    CS = sb("m_CS", [64, SEQ], F32)
    SN = sb("m_SN", [64, SEQ], F32)
    tab_r = Res()
    emit_rope_tables(bd, nc, es, pos_d, cst, cst_r, CS, SN, tab_r)
    wq = sb("m_wq", [128, 2, 512], BF16)
    wkv = sb("m_wkv", [128, 2, 512], BF16)
    wq_r, wkv_r = Res(), Res()
    wst = sb("m_wst", [128, 2, 512], F32)
    wst_r = Res()
    for (wd, wt, wr) in ((wuq_d, wq, wq_r), (wukv_d, wkv, wkv_r)):
        bd.dma("sp", wst[:], wd.ap().rearrange("(k p) n -> p k n", p=128), writes=[wst_r])
        bd.op("dve", lambda e, wt=wt: e.tensor_copy(out=wt[:], in_=wst[:]), reads=[wst_r], writes=[wr])
    Qn = [sb(f"m_Qn{h}", [128, SEQ], BF16) for h in range(2)]
    Qr = [sb(f"m_Qr{h}", [64, SEQ], BF16) for h in range(2)]
    Kn = [sb(f"m_Kn{h}", [128, SEQ], BF16) for h in range(2)]
    Kr = sb("m_Kr", [64, SEQ], BF16)
    V = [sb(f"m_V{h}", [128, 32, 129], BF16) for h in range(2)]
    qk_r = Res()
    for h in range(2):
        bd.op("pool", lambda e, h=h: e.memset(V[h][:, :, 128:129], 1.0), writes=[qk_r])
    banks, banks_r, ptb, ptb_r = K["banks"], K["banks_r"], K["ptb"], K["ptb_r"]
    ones, ones_r = K["ones"], K["ones_r"]
    with ExitStack() as es2:
        sb2 = lambda nm, shape, dt: es2.enter_context(nc.sbuf_tensor(_u(nm), shape, dt))
        cf = [sb2(f"m_cf{i}", [128, 2, 512], F32) for i in range(2)]
        cf_r = [Res(), Res()]
        sq = sb2("m_sq", [128, 2, 512], F32)
        sq_r = Res()
        rs = sb2("m_rs", [128, 512], F32)
        rs_r = Res()
        cn = [sb2(f"m_cn{i}", [128, 2, 512], BF16) for i in range(2)]
        cn_r = [Res(), Res()]
        kx = sb2("m_kx", [64, 2, 512], F32)
        kx_r = Res()
        t1 = sb2("m_t1", [64, 512], F32)
        t2 = sb2("m_t2", [64, 512], F32)
        t1_r, t2_r = Res(), Res()
        bi = 0

        def nb():
            nonlocal bi
            b = bi % 6
            bi += 1
            return banks[b], banks_r[b]

        def rope(dst, x_ap, xs_ap, src_res, sl):
            bd.op("dve", lambda e: e.tensor_tensor(out=t1[:], in0=x_ap, in1=CS[:, sl], op=ALU.mult),
                  reads=src_res + [tab_r], writes=[t1_r])
            bd.op("dve", lambda e: e.tensor_tensor(out=t2[:], in0=xs_ap, in1=SN[:, sl], op=ALU.mult),
                  reads=src_res + [tab_r], writes=[t2_r])
            bd.op("dve", lambda e: e.tensor_tensor(out=dst, in0=t1[:], in1=t2[:], op=ALU.add),
                  reads=[t1_r, t2_r], writes=[qk_r])

        for tb in range(SEQ // 512):
            sl = slice(tb * 512, (tb + 1) * 512)
            for which in (0, 1):
                ci = which
                row0 = which * 256
                bd.dma("sp", cf[ci][:], pT[row0:row0 + 256, sl].rearrange("(k p) t -> p k t", p=128),
                       writes=[cf_r[ci]])
                bd.op("act", lambda e, ci=ci: e.activation(out=sq[:], in_=cf[ci][:], func=AF.Square),
                      reads=[cf_r[ci]], writes=[sq_r])
                pss, pss_r = nb()
                for c in range(2):
                    bd.op("pe", lambda e, c=c, pss=pss: e.matmul(pss[:, :], lhsT=ones[:], rhs=sq[:, c, :],
                                                                 start=(c == 0), stop=(c == 1)),
                          reads=[ones_r, sq_r], writes=[pss_r])
                bd.op("act", lambda e, pss=pss: e.activation(out=rs[:], in_=pss[:, :], func=AF.Sqrt,
                                                             scale=1.0 / 256, bias=cst[:, C_EPS:C_EPS + 1]),
                      reads=[pss_r, cst_r], writes=[rs_r])
                bd.op("dve", lambda e: e.reciprocal(out=rs[:], in_=rs[:]), reads=[rs_r], writes=[rs_r])
                gcol = C_QN0 if which == 0 else C_KVN0
                for c in range(2):
                    bd.op("dve", lambda e, c=c, ci=ci, gcol=gcol: e.scalar_tensor_tensor(
                        out=cn[ci][:, c, :], in0=cf[ci][:, c, :], scalar=cst[:, gcol + c:gcol + c + 1], in1=rs[:],
                        op0=ALU.mult, op1=ALU.mult), reads=[cf_r[ci], rs_r, cst_r], writes=[cn_r[ci]])
                if which == 0:
                    for h in range(2):
                        pq, pq_r = nb()
                        for c in range(2):
                            bd.op("pe", lambda e, c=c, h=h, pq=pq: e.matmul(
                                pq[:, :], lhsT=wq[:, c, h * 256:h * 256 + 128], rhs=cn[0][:, c, :],
                                start=(c == 0), stop=(c == 1)), reads=[wq_r, cn_r[0]], writes=[pq_r])
                        bd.op("act", lambda e, h=h, pq=pq: e.activation(out=Qn[h][:, sl], in_=pq[:, :], func=AF.Copy),
                              reads=[pq_r], writes=[qk_r])
                        pa, pa_r = nb()
                        pb_, pb_r = nb()
                        for (pp, pp_r, off) in ((pa, pa_r, 128), (pb_, pb_r, 192)):
                            for c in range(2):
                                bd.op("pe", lambda e, c=c, h=h, pp=pp, off=off: e.matmul(
                                    pp[0:64, :], lhsT=wq[:, c, h * 256 + off:h * 256 + off + 64], rhs=cn[0][:, c, :],
                                    start=(c == 0), stop=(c == 1)), reads=[wq_r, cn_r[0]], writes=[pp_r])
                        rope(Qr[h][:, sl], pa[0:64, :], pb_[0:64, :], [pa_r, pb_r], sl)
                else:
                    for h in range(2):
                        pk, pk_r = nb()
                        for c in range(2):
                            bd.op("pe", lambda e, c=c, h=h, pk=pk: e.matmul(
                                pk[:, :], lhsT=wkv[:, c, h * 256:h * 256 + 128], rhs=cn[1][:, c, :],
                                start=(c == 0), stop=(c == 1)), reads=[wkv_r, cn_r[1]], writes=[pk_r])
                        bd.op("act", lambda e, h=h, pk=pk: e.activation(out=Kn[h][:, sl], in_=pk[:, :], func=AF.Copy),
                              reads=[pk_r], writes=[qk_r])
                        pv, pv_r = nb()
                        for j in range(4):
                            for c in range(2):
                                bd.op("pe", lambda e, c=c, h=h, j=j, pv=pv: e.matmul(
                                    pv[:, j * 128:(j + 1) * 128], lhsT=cn[1][:, c, j * 128:(j + 1) * 128],
                                    rhs=wkv[:, c, h * 256 + 128:h * 256 + 256],
                                    start=(c == 0 and j == 0), stop=(c == 1)), reads=[wkv_r, cn_r[1]], writes=[pv_r])
                        bd.op("dve", lambda e, h=h, pv=pv: e.tensor_copy(
                            out=V[h][:, tb * 4:(tb + 1) * 4, 0:128],
                            in_=pv[:, :].rearrange("p (j d) -> p j d", j=4)), reads=[pv_r], writes=[qk_r])
            bd.dma("sp", kx[:], pT[512:640, sl].rearrange("(k p) t -> p k t", p=64), writes=[kx_r])
            rope(Kr[:, sl], kx[:, 0, :], kx[:, 1, :], [kx_r], sl)
        bd.barrier()
    with ExitStack() as es3:
        sb3 = lambda nm, shape, dt: es3.enter_context(nc.sbuf_tensor(_u(nm), shape, dt))
        of = sb3("m_of", [128, 132], F32)
        of_r = Res()
        onb = sb3("m_onb", [128, 128], BF16)
        onb_r = Res()
        st_ = sb3("m_stat", [128, 4], F32)
        st_r = Res()
        junk = sb3("m_junk", [128, 128], F32)
        junk_r = Res()
        yst = [sb3(f"m_yst{i}", [128, 512], BF16) for i in range(2)]
        yst_r = [Res(), Res()]
        scale = (128 + 64) ** -0.5

        def wfun(h, kt, qb, st, st_r2, p, p_r):
            bd.op("act", lambda e: e.activation(out=p[:], in_=st[:, :], func=AF.Exp, scale=scale),
                  reads=[st_r2], writes=[p_r])

        def out_fn(h, qt, o, o_r):
            bd.op("dve", lambda e: e.reciprocal(out=st_[:, 0:1], in_=o[:, 128:129]), reads=[o_r], writes=[st_r])
            bd.op("dve", lambda e: e.tensor_scalar(out=of[:, 0:128], in0=o[:, 0:128], scalar1=st_[:, 0:1],
                                                   scalar2=None, op0=ALU.mult), reads=[o_r, st_r], writes=[of_r])
            bd.op("dve", lambda e: e.scalar_tensor_tensor(out=junk[:], in0=of[:, 0:128], scalar=1.0, in1=of[:, 0:128],
                                                          op0=ALU.mult, op1=ALU.mult, accum_out=st_[:, 1:2]),
                  reads=[of_r], writes=[junk_r, st_r])
            bd.op("act", lambda e: e.activation(out=st_[:, 2:3], in_=st_[:, 1:2], func=AF.Sqrt, scale=1.0 / 128,
                                                bias=cst[:, C_EPS:C_EPS + 1]), reads=[st_r, cst_r], writes=[st_r])
            bd.op("dve", lambda e: e.reciprocal(out=st_[:, 3:4], in_=st_[:, 2:3]), reads=[st_r], writes=[st_r])
            bd.op("act", lambda e: e.activation(out=onb[:], in_=of[:, 0:128], func=AF.Copy, scale=st_[:, 3:4]),
                  reads=[of_r, st_r], writes=[onb_r])
            bd.op("pe", lambda e: e.transpose(out=ptb[:, 0:128], in_=onb[:], identity=K["identb"][:]),
                  reads=[onb_r, K["identb_r"]], writes=[ptb_r])
            ys, ys_r = yst[(qt // 4) % 2], yst_r[(qt // 4) % 2]
            j = qt % 4
            bd.op("act", lambda e: e.activation(out=ys[:, j * 128:(j + 1) * 128], in_=ptb[:, 0:128], func=AF.Copy,
                                                scale=cst[:, C_MLAO0 + h:C_MLAO0 + h + 1]),
                  reads=[ptb_r, cst_r], writes=[ys_r])
            if j == 3:
                qb = qt // 4
                bd.dma("sp", yT[h * 128:(h + 1) * 128, qb * 512:(qb + 1) * 512], ys[:], reads=[ys_r])

        PS = dict(st=[(banks[0], banks_r[0]), (banks[1], banks_r[1])],
                  o=[(banks[2 + j], banks_r[2 + j]) for j in range(4)],
                  mask=K["mask"], mask_r=K["mask_r"],
                  kparts=lambda h, kt: [(Kn[h][:, kt * 128:(kt + 1) * 128], qk_r), (Kr[:, kt * 128:(kt + 1) * 128], qk_r)],
                  qparts=lambda h, qb: [(Qn[h][:, qb * 512:(qb + 1) * 512], qk_r), (Qr[h][:, qb * 512:(qb + 1) * 512], qk_r)],
                  v=lambda h, kt: (V[h][:, kt, :], qk_r))
        emit_attention(bd, nc, es3, "mla", [0, 1], scale, 128, wfun, out_fn, PS)
        bd.barrier()


def emit_mlstm(bd, nc, es, pT_d, yT_d, cst, cst_r, K):
    sb = lambda nm, shape, dt: es.enter_context(nc.sbuf_tensor(_u(nm), shape, dt))
    pT = pT_d.ap()
    yT = yT_d.ap()
    banks, banks_r, ptb, ptb_r = K["banks"], K["banks_r"], K["ptb"], K["ptb_r"]
    misc, misc_r = banks[6], banks_r[6]
    R_Q, R_K, R_V, R_O, R_G = 1152, 1216, 1280, 1408, 1536
    Qb = sb("l_Qb", [64, SEQ], BF16)
    Kb = sb("l_Kb", [64, SEQ], BF16)
    Vm = sb("l_Vm", [128, 32, 2, 65], BF16)
    Ym = sb("l_Ym", [128, SEQ], BF16)
    uT = sb("l_uT", [128, 2, 32], F32)
    emT = sb("l_emT", [128, 2, 32], F32)
    nPb = [sb(f"l_nPb{h}", [128, SEQ], F32) for h in range(2)]
    prep_r = Res()
    ym_r = Res()
    bd.op("pool", lambda e: e.memset(Vm[:, :, :, 64:65], 1.0), writes=[prep_r])
    with ExitStack() as es2:
        sb2 = lambda nm, shape, dt: es2.enter_context(nc.sbuf_tensor(_u(nm), shape, dt))
        xin = sb2("l_xin", [64, SEQ], F32)
        A = sb2("l_A", [64, SEQ], F32)
        B = sb2("l_B", [64, SEQ], F32)
        xin_r, A_r, B_r = Res(), Res(), Res()
        for (row0, cw0, cb, dst, scl) in ((R_Q, C_CWQ0, C_CBQ, Qb, 32 ** -0.5), (R_K, C_CWK0, C_CBK, Kb, 1.0)):
            bd.dma("sp", xin[:], pT[row0:row0 + 64, :], writes=[xin_r])
            bd.op("dve", lambda e, cw0=cw0, cb=cb: e.tensor_scalar(
                out=A[:], in0=xin[:], scalar1=cst[0:64, cw0 + 3:cw0 + 4], scalar2=cst[0:64, cb:cb + 1],
                op0=ALU.mult, op1=ALU.add), reads=[xin_r, cst_r], writes=[A_r])
            src, src_r, dstt, dst_r = A, A_r, B, B_r
            for sh in (1, 2, 3):
                bd.op("dve", lambda e, sh=sh, cw0=cw0, src=src, dstt=dstt: e.scalar_tensor_tensor(
                    out=dstt[:, sh:], in0=xin[:, 0:SEQ - sh], scalar=cst[0:64, cw0 + 3 - sh:cw0 + 4 - sh],
                    in1=src[:, sh:], op0=ALU.mult, op1=ALU.add), reads=[xin_r, cst_r, src_r], writes=[dst_r])
                bd.op("dve", lambda e, sh=sh, src=src, dstt=dstt: e.tensor_copy(out=dstt[:, 0:sh], in_=src[:, 0:sh]),
                      reads=[src_r], writes=[dst_r])
                src, src_r, dstt, dst_r = dstt, dst_r, src, src_r
            bd.op("act", lambda e, src=src: e.activation(out=xin[:], in_=src[:], func=AF.Silu),
                  reads=[src_r], writes=[xin_r])
            bd.op("dve", lambda e, dst=dst, scl=scl: e.tensor_scalar(out=dst[:], in0=xin[:], scalar1=scl, scalar2=None,
                                                                     op0=ALU.mult), reads=[xin_r], writes=[prep_r])
        bd.barrier()
    with ExitStack() as es2:
        sb2 = lambda nm, shape, dt: es2.enter_context(nc.sbuf_tensor(_u(nm), shape, dt))
        t0 = sb2("l_t0", [1, SEQ], F32)
        t1 = sb2("l_t1", [1, SEQ], F32)
        t2 = sb2("l_t2", [1, SEQ], F32)
        onesrow = sb2("l_onesrow", [1, SEQ], F32)
        vin = sb2("l_vin", [128, SEQ], F32)
        t0_r, t1_r, t2_r, or_r, vin_r = Res(), Res(), Res(), Res(), Res()
        bd.op("dve", lambda e: e.memset(onesrow[:], 1.0), writes=[or_r])
        identf = K["identf"]
        for h in range(2):
            cib = C_IB if h == 0 else C_IB1
            cfb = C_FB if h == 0 else C_FB1
            bd.dma("sp", t0[:], pT[R_G + h:R_G + h + 1, :], writes=[t0_r])
            bd.dma("sp", t1[:], pT[R_G + 2 + h:R_G + 3 + h, :], writes=[t1_r])
            bd.op("dve", lambda e, cib=cib: e.tensor_scalar(out=t0[:], in0=t0[:], scalar1=cst[0:1, cib:cib + 1],
                                                            scalar2=None, op0=ALU.add), reads=[t0_r, cst_r], writes=[t0_r])
            bd.op("act", lambda e, cfb=cfb: e.activation(out=t1[:], in_=t1[:], func=AF.Sigmoid,
                                                         bias=cst[0:1, cfb:cfb + 1]), reads=[t1_r, cst_r], writes=[t1_r])
            bd.op("act", lambda e: e.activation(out=t1[:], in_=t1[:], func=AF.Ln), reads=[t1_r], writes=[t1_r])
            bd.op("dve", lambda e: e.tensor_tensor_scan(out=t2[:], data0=onesrow[:], data1=t1[:], initial=0.0,
                                                        op0=ALU.mult, op1=ALU.add), reads=[or_r, t1_r], writes=[t2_r])
            bd.op("dve", lambda e: e.tensor_tensor(out=t0[:], in0=t0[:], in1=t2[:], op=ALU.subtract),
                  reads=[t0_r, t2_r], writes=[t0_r])
            bd.op("dve", lambda e: e.tensor_tensor_scan(out=t1[:], data0=onesrow[:], data1=t0[:], initial=0.0,
                                                        op0=ALU.mult, op1=ALU.max), reads=[or_r, t0_r], writes=[t1_r])
            bd.op("dve", lambda e: e.tensor_tensor(out=t2[:], in0=t2[:], in1=t1[:], op=ALU.add),
                  reads=[t2_r, t1_r], writes=[t2_r])
            for jt in range(32):
                bd.op("pe", lambda e, jt=jt: e.transpose(out=misc[:, jt:jt + 1], in_=t0[0:1, jt * 128:(jt + 1) * 128],
                                                         identity=identf[0:1, 0:1]), reads=[t0_r, K["mats_r"]], writes=[misc_r])
                bd.op("pe", lambda e, jt=jt: e.transpose(out=misc[:, 32 + jt:33 + jt], in_=t2[0:1, jt * 128:(jt + 1) * 128],
                                                         identity=identf[0:1, 0:1]), reads=[t2_r, K["mats_r"]], writes=[misc_r])
            bd.op("dve", lambda e, h=h: e.tensor_copy(out=uT[:, h, :], in_=misc[:, 0:32]), reads=[misc_r], writes=[prep_r])
            bd.op("act", lambda e, h=h: e.activation(out=emT[:, h, :], in_=misc[:, 32:64], func=AF.Exp, scale=-1.0),
                  reads=[misc_r], writes=[prep_r])
            for tb in range(SEQ // 512):
                bd.op("pe", lambda e, tb=tb: e.matmul(misc[:, :], lhsT=K["ones"][0:1, :], rhs=t1[0:1, tb * 512:(tb + 1) * 512],
                                                      start=True, stop=True), reads=[t1_r, K["ones_r"]], writes=[misc_r])
                bd.op("act", lambda e, tb=tb, h=h: e.activation(out=nPb[h][:, tb * 512:(tb + 1) * 512], in_=misc[:, :],
                                                                func=AF.Copy, scale=-1.0), reads=[misc_r], writes=[prep_r])
        bd.dma("sp", vin[:], pT[R_V:R_V + 128, :], writes=[vin_r])
        for jt in range(32):
            bd.op("pe", lambda e, jt=jt: e.transpose(out=misc[:, 0:128], in_=vin[:, jt * 128:(jt + 1) * 128],
                                                     identity=identf), reads=[vin_r, K["mats_r"]], writes=[misc_r])
            bd.op("dve", lambda e, jt=jt: e.tensor_copy(out=Vm[:, jt, :, 0:64],
                                                        in_=misc[:, 0:128].rearrange("p (h d) -> p h d", h=2)),
                  reads=[misc_r], writes=[prep_r])
        bd.barrier()
    with ExitStack() as es3:
        sb3 = lambda nm, shape, dt: es3.enter_context(nc.sbuf_tensor(_u(nm), shape, dt))
        Wt = [sb3(f"l_W{i}", [128, 512], F32) for i in range(2)]
        Wt_r = [Res(), Res()]
        of = sb3("l_of", [128, 64], F32)
        of_r = Res()
        onb = sb3("l_onb", [128, 64], BF16)
        onb_r = Res()
        st_ = sb3("l_stat", [128, 6], F32)
        st_r = Res()
        junk = sb3("l_junk", [128, 64], F32)
        junk_r = Res()
        wi = [0]

        def wfun(h, kt, qb, st, st_r2, p, p_r):
            w, w_r = Wt[wi[0] % 2], Wt_r[wi[0] % 2]
            wi[0] += 1
            bd.op("act", lambda e: e.activation(out=w[:], in_=nPb[h][:, qb * 512:(qb + 1) * 512], func=AF.Exp,
                                                bias=uT[:, h, kt:kt + 1]), reads=[prep_r], writes=[w_r])
            bd.op("dve", lambda e: e.tensor_tensor(out=p[:], in0=st[:, :], in1=w[:], op=ALU.mult),
                  reads=[st_r2, w_r], writes=[p_r])

        def out_fn(h, qt, o, o_r):
            bd.op("act", lambda e: e.activation(out=st_[:, 0:1], in_=o[:, 64:65], func=AF.Abs), reads=[o_r], writes=[st_r])
            bd.op("dve", lambda e: e.tensor_tensor(out=st_[:, 1:2], in0=st_[:, 0:1], in1=emT[:, h, qt:qt + 1], op=ALU.max),
                  reads=[st_r, prep_r], writes=[st_r])
            bd.op("dve", lambda e: e.reciprocal(out=st_[:, 2:3], in_=st_[:, 1:2]), reads=[st_r], writes=[st_r])
            bd.op("dve", lambda e: e.tensor_scalar(out=of[:], in0=o[:, 0:64], scalar1=st_[:, 2:3], scalar2=None,
                                                   op0=ALU.mult), reads=[o_r, st_r], writes=[of_r])
            bd.op("dve", lambda e: e.scalar_tensor_tensor(out=junk[:], in0=of[:], scalar=1.0, in1=of[:],
                                                          op0=ALU.mult, op1=ALU.mult, accum_out=st_[:, 3:4]),
                  reads=[of_r], writes=[junk_r, st_r])
            bd.op("act", lambda e: e.activation(out=st_[:, 4:5], in_=st_[:, 3:4], func=AF.Sqrt, scale=1.0 / 64,
                                                bias=cst[:, C_EPS:C_EPS + 1]), reads=[st_r, cst_r], writes=[st_r])
            bd.op("dve", lambda e: e.reciprocal(out=st_[:, 5:6], in_=st_[:, 4:5]), reads=[st_r], writes=[st_r])
            bd.op("act", lambda e: e.activation(out=onb[:], in_=of[:], func=AF.Copy, scale=st_[:, 5:6]),
                  reads=[of_r, st_r], writes=[onb_r])
            bd.op("pe", lambda e: e.transpose(out=ptb[h * 64:(h + 1) * 64, 0:128], in_=onb[:], identity=K["identb"][:]),
                  reads=[onb_r, K["identb_r"]], writes=[ptb_r])
            bd.op("act", lambda e: e.activation(out=Ym[h * 64:(h + 1) * 64, qt * 128:(qt + 1) * 128],
                                                in_=ptb[h * 64:(h + 1) * 64, 0:128], func=AF.Copy,
                                                scale=cst[h * 64:(h + 1) * 64, C_MLO:C_MLO + 1]),
                  reads=[ptb_r, cst_r], writes=[ym_r])

        PS = dict(st=[(banks[0], banks_r[0]), (banks[1], banks_r[1])],
                  o=[(banks[2 + j], banks_r[2 + j]) for j in range(4)],
                  mask=K["mask"], mask_r=K["mask_r"],
                  kparts=lambda h, kt: [(Kb[h * 32:(h + 1) * 32, kt * 128:(kt + 1) * 128], prep_r)],
                  qparts=lambda h, qb: [(Qb[h * 32:(h + 1) * 32, qb * 512:(qb + 1) * 512], prep_r)],
                  v=lambda h, kt: (Vm[:, kt, h, :], prep_r))
        emit_attention(bd, nc, es3, "mls", [0, 1], 1.0, 64, wfun, out_fn, PS)
        og = sb3("l_og", [128, SEQ], F32)
        og_r = Res()
        bd.dma("sp", og[:], pT[R_O:R_O + 128, :], writes=[og_r])
        bd.op("act", lambda e: e.activation(out=og[:], in_=og[:], func=AF.Sigmoid), reads=[og_r], writes=[og_r])
        bd.op("dve", lambda e: e.tensor_tensor(out=Ym[:], in0=Ym[:], in1=og[:], op=ALU.mult),
              reads=[ym_r, og_r], writes=[ym_r])
        bd.dma("sp", yT[384:512, :], Ym[:], reads=[ym_r])
        bd.barrier()


RW_T = 16


def emit_rwkv(bd, nc, es, D_, cst, cst_r, K):
    sb = lambda nm, shape, dt: es.enter_context(nc.sbuf_tensor(_u(nm), shape, dt))
    pT = D_["pT"].ap()
    yT = D_["yT"].ap()
    scr = D_["scr"]
    banks, banks_r = K["banks"], K["banks_r"]
    bones, identf, mats_r = K["bones"], K["identf"], K["mats_r"]
    R_R, R_K, R_V, R_L = 640, 768, 896, 1024
    vS = sb("r_vS", [128, SEQ], F32)
    gS = sb("r_gS", [128, SEQ], F32)
    boS = sb("r_boS", [128, SEQ], F32)
    yS = sb("r_yS", [128, SEQ], F32)
    vS_r, gS_r, boS_r, yS_r = Res(), Res(), Res(), Res()
    wl = sb("r_wl", [128, 3, 128], F32)
    wl_r = Res()
    bd.dma("sp", wl[:], D_["wl"].ap(), writes=[wl_r])
    c2 = sb("r_c2", [128, 2], F32)
    c2_r = Res()
    bd.op("dve", lambda e: e.tensor_scalar(out=c2[:, 0:1], in0=cst[:, C_KA:C_KA + 1], scalar1=-1.0, scalar2=1.0,
                                           op0=ALU.mult, op1=ALU.add), reads=[cst_r], writes=[c2_r])
    scr_r = Res()
    with ExitStack() as es2:
        sb2 = lambda nm, shape, dt: es2.enter_context(nc.sbuf_tensor(_u(nm), shape, dt))
        rS = sb2("r_rS", [128, SEQ], F32)
        kS = sb2("r_kS", [128, SEQ], F32)
        lS = sb2("r_lS", [128, SEQ], F32)
        dd = sb2("r_dd", [128, SEQ], F32)
        rS_r, kS_r, lS_r, dd_r = Res(), Res(), Res(), Res()
        for (row0, t, t_r, mu) in ((R_R, rS, rS_r, C_MU_R), (R_K, kS, kS_r, C_MU_K), (R_V, vS, vS_r, C_MU_V),
                                   (R_L, lS, lS_r, C_MU_L)):
            bd.dma("sp", t[:], pT[row0:row0 + 128, :], writes=[t_r])
            bd.op("dve", lambda e, t=t: e.tensor_tensor(out=dd[:, 1:SEQ], in0=t[:, 0:SEQ - 1], in1=t[:, 1:SEQ],
                                                        op=ALU.subtract), reads=[t_r], writes=[dd_r])
            bd.op("dve", lambda e, t=t: e.tensor_scalar(out=dd[:, 0:1], in0=t[:, 0:1], scalar1=-1.0, scalar2=None,
                                                        op0=ALU.mult), reads=[t_r], writes=[dd_r])
            bd.op("dve", lambda e, t=t, mu=mu: e.scalar_tensor_tensor(out=t[:], in0=dd[:], scalar=cst[:, mu:mu + 1],
                                                                      in1=t[:], op0=ALU.mult, op1=ALU.add),
                  reads=[dd_r, t_r, cst_r], writes=[t_r])
        names = ["th", "sg", "sgm", "wd", "aT", "kkr", "sq", "nrm", "nkk", "bb", "t1", "km", "prod"]
        T_ = {n: sb2("r_" + n, [128, 512], F32) for n in names}
        T_r = {n: Res() for n in names}
        stg = [sb2(f"r_stg{i}", [128, 5, 128], F32) for i in range(2)]
        stg_r = [Res(), Res()]
        bi = [0]

        def nb():
            b = bi[0] % 7
            bi[0] += 1
            return banks[b], banks_r[b]

        def A(fn, reads, writes):
            bd.op("act", fn, reads=reads, writes=writes)

        def V(fn, reads, writes):
            bd.op("dve", fn, reads=reads, writes=writes)

        ti = 0
        for tb in range(SEQ // 512):
            sl = slice(tb * 512, (tb + 1) * 512)
            A(lambda e: e.activation(out=T_["th"][:], in_=lS[:, sl], func=AF.Tanh), [lS_r], [T_r["th"]])
            A(lambda e: e.activation(out=T_["sg"][:], in_=lS[:, sl], func=AF.Sigmoid), [lS_r], [T_r["sg"]])
            pw, pw_r = nb()
            bd.op("pe", lambda e: e.matmul(pw[:, :], lhsT=wl[:, 0, :], rhs=T_["th"][:], start=True, stop=True),
                  reads=[wl_r, T_r["th"]], writes=[pw_r])
            A(lambda e: e.activation(out=T_["sgm"][:], in_=pw[:, :], func=AF.Sigmoid, bias=cst[:, C_W0:C_W0 + 1]),
              [pw_r, cst_r], [T_r["sgm"]])
            A(lambda e: e.activation(out=T_["wd"][:], in_=T_["sgm"][:], func=AF.Exp, scale=-float(np.exp(-0.5))),
              [T_r["sgm"]], [T_r["wd"]])
            pa, pa_r = nb()
            bd.op("pe", lambda e: e.matmul(pa[:, :], lhsT=wl[:, 1, :], rhs=lS[:, sl], start=True, stop=True),
                  reads=[wl_r, lS_r], writes=[pa_r])
            A(lambda e: e.activation(out=T_["aT"][:], in_=pa[:, :], func=AF.Sigmoid, bias=cst[:, C_A0:C_A0 + 1]),
              [pa_r, cst_r], [T_r["aT"]])
            pg, pg_r = nb()
            bd.op("pe", lambda e: e.matmul(pg[:, :], lhsT=wl[:, 2, :], rhs=T_["sg"][:], start=True, stop=True),
                  reads=[wl_r, T_r["sg"]], writes=[pg_r])
            A(lambda e: e.activation(out=gS[:, sl], in_=pg[:, :], func=AF.Copy), [pg_r], [gS_r])
            V(lambda e: e.tensor_scalar(out=T_["kkr"][:], in0=kS[:, sl], scalar1=cst[:, C_KK:C_KK + 1], scalar2=None,
                                        op0=ALU.mult), [kS_r, cst_r], [T_r["kkr"]])
            A(lambda e: e.activation(out=T_["sq"][:], in_=T_["kkr"][:], func=AF.Square), [T_r["kkr"]], [T_r["sq"]])
            pn, pn_r = nb()
            bd.op("pe", lambda e: e.matmul(pn[:, :], lhsT=bones, rhs=T_["sq"][:], start=True, stop=True),
                  reads=[mats_r, T_r["sq"]], writes=[pn_r])
            A(lambda e: e.activation(out=T_["nrm"][:], in_=pn[:, :], func=AF.Sqrt), [pn_r], [T_r["nrm"]])
            V(lambda e: e.tensor_scalar(out=T_["nrm"][:], in0=T_["nrm"][:], scalar1=1e-12, scalar2=None, op0=ALU.max),
              [T_r["nrm"]], [T_r["nrm"]])
            V(lambda e: e.reciprocal(out=T_["nrm"][:], in_=T_["nrm"][:]), [T_r["nrm"]], [T_r["nrm"]])
            V(lambda e: e.scalar_tensor_tensor(out=T_["nkk"][:], in0=T_["kkr"][:], scalar=-1.0, in1=T_["nrm"][:],
                                               op0=ALU.mult, op1=ALU.mult), [T_r["kkr"], T_r["nrm"]], [T_r["nkk"]])
            V(lambda e: e.scalar_tensor_tensor(out=T_["bb"][:], in0=T_["nkk"][:], scalar=-1.0, in1=T_["aT"][:],
                                               op0=ALU.mult, op1=ALU.mult), [T_r["nkk"], T_r["aT"]], [T_r["bb"]])
            V(lambda e: e.tensor_scalar(out=T_["t1"][:], in0=T_["aT"][:], scalar1=cst[:, C_KA:C_KA + 1],
                                        scalar2=c2[:, 0:1], op0=ALU.mult, op1=ALU.add),
              [T_r["aT"], cst_r, c2_r], [T_r["t1"]])
            V(lambda e: e.tensor_tensor(out=T_["km"][:], in0=kS[:, sl], in1=T_["t1"][:], op=ALU.mult),
              [kS_r, T_r["t1"]], [T_r["km"]])
            V(lambda e: e.scalar_tensor_tensor(out=T_["prod"][:], in0=rS[:, sl], scalar=cst[:, C_RK:C_RK + 1],
                                               in1=T_["km"][:], op0=ALU.mult, op1=ALU.mult),
              [rS_r, cst_r, T_r["km"]], [T_r["prod"]])
            pb_, pb_r = nb()
            bd.op("pe", lambda e: e.matmul(pb_[:, :], lhsT=bones, rhs=T_["prod"][:], start=True, stop=True),
                  reads=[mats_r, T_r["prod"]], writes=[pb_r])
            V(lambda e: e.tensor_tensor(out=boS[:, sl], in0=pb_[:, :], in1=vS[:, sl], op=ALU.mult),
              [pb_r, vS_r], [boS_r])
            for j in range(4):
                t0 = tb * 512 + j * 128
                px, px_r = nb()
                py, py_r = nb()
                srcs = [(T_["nkk"][:, j * 128:(j + 1) * 128], T_r["nkk"]), (T_["wd"][:, j * 128:(j + 1) * 128], T_r["wd"]),
                        (T_["bb"][:, j * 128:(j + 1) * 128], T_r["bb"]), (T_["km"][:, j * 128:(j + 1) * 128], T_r["km"]),
                        (rS[:, t0:t0 + 128], rS_r)]
                for q, (ap_, r_) in enumerate(srcs):
                    if q < 4:
                        bd.op("pe", lambda e, q=q, ap_=ap_: e.transpose(out=px[:, q * 128:(q + 1) * 128], in_=ap_,
                                                                       identity=identf), reads=[r_, mats_r], writes=[px_r])
                    else:
                        bd.op("pe", lambda e, ap_=ap_: e.transpose(out=py[:, 0:128], in_=ap_, identity=identf),
                              reads=[r_, mats_r], writes=[py_r])
                sg_, sg_r = stg[ti % 2], stg_r[ti % 2]
                ti += 1
                A(lambda e, sg_=sg_: e.activation(out=sg_[:, 0:4, :], in_=px[:, :].rearrange("p (q c) -> p q c", q=4),
                                                  func=AF.Copy), [px_r], [sg_r])
                V(lambda e, sg_=sg_: e.tensor_copy(out=sg_[:, 4, :], in_=py[:, 0:128]), [py_r], [sg_r])
                for h in range(2):
                    dst = bass.AP(scr, h * SEQ * 320 + t0 * 320, [[320, 128], [64, 5], [1, 64]])
                    bd.dma("sp" if h == 0 else "act", dst, sg_[:, :, h * 64:(h + 1) * 64], reads=[sg_r], writes=[scr_r])
        bd.barrier()
    with ExitStack() as es3:
        sb3 = lambda nm, shape, dt: es3.enter_context(nc.sbuf_tensor(_u(nm), shape, dt))
        T = RW_T
        NB = 3
        BC = [sb3(f"r_BC{i}", [128, T, 5, 64], F32) for i in range(NB)]
        BC_r = [Res() for _ in range(NB)]
        S = sb3("r_S", [128, 64], F32)
        junk = sb3("r_junk", [128, 64], F32)
        sa = sb3("r_sa", [128, 1], F32)
        S_r, junk_r, sa_r = Res(), Res(), Res()
        bd.op("dve", lambda e: e.memset(S[:], 0.0), writes=[S_r])
        nchunk = SEQ // T

        def load(ci):
            b = ci % NB
            for h in range(2):
                src = bass.AP(scr, h * SEQ * 320 + ci * T * 320, [[0, 64], [1, T * 320]])
                bd.dma("sp" if h == 0 else "act", BC[b][h * 64:(h + 1) * 64, :, :, :].rearrange("p t q j -> p (t q j)"),
                       src, reads=[scr_r], writes=[BC_r[b]])

        load(0)
        load(1)
        for ci in range(nchunk):
            if ci + 2 < nchunk:
                load(ci + 2)
            b = ci % NB
            bc, bc_r = BC[b], BC_r[b]
            for tt in range(T):
                t = ci * T + tt
                bd.op("dve", lambda e, bc=bc, tt=tt: e.scalar_tensor_tensor(
                    out=junk[:], in0=S[:], scalar=1.0, in1=bc[:, tt, 0, :], op0=ALU.mult, op1=ALU.mult,
                    accum_out=sa[:, 0:1]), reads=[S_r, bc_r], writes=[junk_r, sa_r])
                bd.op("dve", lambda e, bc=bc, tt=tt: e.tensor_tensor(out=S[:], in0=S[:], in1=bc[:, tt, 1, :], op=ALU.mult),
                      reads=[S_r, bc_r], writes=[S_r])
                bd.op("dve", lambda e, bc=bc, tt=tt: e.scalar_tensor_tensor(
                    out=S[:], in0=bc[:, tt, 2, :], scalar=sa[:, 0:1], in1=S[:], op0=ALU.mult, op1=ALU.add),
                    reads=[S_r, bc_r, sa_r], writes=[S_r])
                bd.op("dve", lambda e, bc=bc, tt=tt, t=t: e.scalar_tensor_tensor(
                    out=S[:], in0=bc[:, tt, 3, :], scalar=vS[:, t:t + 1], in1=S[:], op0=ALU.mult, op1=ALU.add),
                    reads=[S_r, bc_r, vS_r], writes=[S_r])
                bd.op("dve", lambda e, bc=bc, tt=tt, t=t: e.scalar_tensor_tensor(
                    out=junk[:], in0=S[:], scalar=1.0, in1=bc[:, tt, 4, :], op0=ALU.mult, op1=ALU.mult,
                    accum_out=yS[:, t:t + 1]), reads=[S_r, bc_r], writes=[junk_r, yS_r])
        bd.barrier()
    with ExitStack() as es4:
        sb4 = lambda nm, shape, dt: es4.enter_context(nc.sbuf_tensor(_u(nm), shape, dt))
        yc = sb4("r_yc", [128, 512], F32)
        sq = sb4("r_sq2", [128, 512], F32)
        rs = sb4("r_rs", [128, 512], F32)
        yo = [sb4(f"r_yo{i}", [128, 512], BF16) for i in range(2)]
        yc_r, sq_r, rs_r = Res(), Res(), Res()
        yo_r = [Res(), Res()]
        for tb in range(SEQ // 512):
            sl = slice(tb * 512, (tb + 1) * 512)
            pm, pm_r = banks[tb % 2], banks_r[tb % 2]
            pv, pv_r = banks[2 + tb % 2], banks_r[2 + tb % 2]
            bd.op("pe", lambda e: e.matmul(pm[:, :], lhsT=bones, rhs=yS[:, sl], start=True, stop=True),
                  reads=[mats_r, yS_r], writes=[pm_r])
            bd.op("dve", lambda e: e.scalar_tensor_tensor(out=yc[:], in0=pm[:, :], scalar=-1.0 / 64, in1=yS[:, sl],
                                                          op0=ALU.mult, op1=ALU.add), reads=[pm_r, yS_r], writes=[yc_r])
            bd.op("act", lambda e: e.activation(out=sq[:], in_=yc[:], func=AF.Square), reads=[yc_r], writes=[sq_r])
            bd.op("pe", lambda e: e.matmul(pv[:, :], lhsT=bones, rhs=sq[:], start=True, stop=True),
                  reads=[mats_r, sq_r], writes=[pv_r])
            bd.op("act", lambda e: e.activation(out=rs[:], in_=pv[:, :], func=AF.Sqrt, scale=1.0 / 64,
                                                bias=cst[:, C_LNEPS:C_LNEPS + 1]), reads=[pv_r, cst_r], writes=[rs_r])
            bd.op("dve", lambda e: e.reciprocal(out=rs[:], in_=rs[:]), reads=[rs_r], writes=[rs_r])
            bd.op("dve", lambda e: e.tensor_tensor(out=yc[:], in0=yc[:], in1=rs[:], op=ALU.mult),
                  reads=[yc_r, rs_r], writes=[yc_r])
            bd.op("dve", lambda e: e.tensor_scalar(out=yc[:], in0=yc[:], scalar1=cst[:, C_LNW:C_LNW + 1],
                                                   scalar2=cst[:, C_LNB:C_LNB + 1], op0=ALU.mult, op1=ALU.add),
                  reads=[yc_r, cst_r], writes=[yc_r])
            bd.op("dve", lambda e: e.tensor_tensor(out=yc[:], in0=yc[:], in1=boS[:, sl], op=ALU.add),
                  reads=[yc_r, boS_r], writes=[yc_r])
            o, o_r = yo[tb % 2], yo_r[tb % 2]
            bd.op("dve", lambda e, o=o: e.tensor_tensor(out=o[:], in0=yc[:], in1=gS[:, sl], op=ALU.mult),
                  reads=[yc_r, gS_r], writes=[o_r])
            bd.dma("sp", yT[256:384, sl], o[:], reads=[o_r])
        bd.barrier()


RWKV_CHUNKED = True
import os as _os
_DBG_STAGE = int(_os.environ.get('RW_DBG', '0'))


def rwkv_masks():
    s_ = (np.arange(128) % 64)[:, None]
    t_ = (np.arange(512) % 64)[None, :]
    m = np.zeros((128, 5, 512), np.float32)
    m[:, 0, :] = (s_ < t_)
    m[:, 1, :] = (s_ <= t_)
    m[:, 2, :] = (s_ > t_)
    m[:, 3, :] = (s_ == t_)
    m[:, 4, :] = np.broadcast_to(t_ != 0, (128, 512))
    return m


def emit_rwkv_chunked(bd, nc, es, D_, cst, cst_r, K):
    sb = lambda nm, shape, dt: es.enter_context(nc.sbuf_tensor(_u(nm), shape, dt))
    pT = D_["pT"].ap()
    yT = D_["yT"].ap()
    banks, banks_r = K["banks"], K["banks_r"]
    bones, identf, mats_r = K["bones"], K["identf"], K["mats_r"]
    R_R, R_K, R_V, R_L = 640, 768, 896, 1024
    rS = sb("r_rS", [128, SEQ], F32)
    kS = sb("r_kS", [128, SEQ], F32)
    vS = sb("r_vS", [128, SEQ], F32)
    lS = sb("r_lS", [128, SEQ], F32)
    rS_r, kS_r, vS_r, lS_r = Res(), Res(), Res(), Res()
    wl = sb("r_wl", [128, 3, 128], F32)
    wl_r = Res()
    bd.dma("sp", wl[:], D_["wl"].ap(), writes=[wl_r])
    msk = sb("r_msk", [128, 5, 512], F32)
    msk_r = Res()
    bd.dma("act", msk[:], D_["rwm"].ap(), writes=[msk_r])
    c2 = sb("r_c2", [128, 2], F32)
    c2_r = Res()
    bd.op("dve", lambda e: e.tensor_scalar(out=c2[:, 0:1], in0=cst[:, C_KA:C_KA + 1], scalar1=-1.0, scalar2=1.0,
                                           op0=ALU.mult, op1=ALU.add), reads=[cst_r], writes=[c2_r])
    with ExitStack() as es2:
        sb2 = lambda nm, shape, dt: es2.enter_context(nc.sbuf_tensor(_u(nm), shape, dt))
        dd = sb2("r_dd", [128, SEQ], F32)
        dd_r = Res()
        for (row0, t, t_r, mu) in ((R_R, rS, rS_r, C_MU_R), (R_K, kS, kS_r, C_MU_K), (R_V, vS, vS_r, C_MU_V),
                                   (R_L, lS, lS_r, C_MU_L)):
            bd.dma("sp", t[:], pT[row0:row0 + 128, :], writes=[t_r])
            bd.op("dve", lambda e, t=t: e.tensor_tensor(out=dd[:, 1:SEQ], in0=t[:, 0:SEQ - 1], in1=t[:, 1:SEQ],
                                                        op=ALU.subtract), reads=[t_r], writes=[dd_r])
            bd.op("dve", lambda e, t=t: e.tensor_scalar(out=dd[:, 0:1], in0=t[:, 0:1], scalar1=-1.0, scalar2=None,
                                                        op0=ALU.mult), reads=[t_r], writes=[dd_r])
            bd.op("dve", lambda e, t=t, mu=mu: e.scalar_tensor_tensor(out=t[:], in0=dd[:], scalar=cst[:, mu:mu + 1],
                                                                      in1=t[:], op0=ALU.mult, op1=ALU.add),
                  reads=[dd_r, t_r, cst_r], writes=[t_r])
        bd.barrier()
    names = ["th", "sg", "sgm", "aT", "kkr", "sq", "nrm", "nkk", "bb", "t1", "km", "prod", "gB", "boB",
             "lw", "cl", "e1", "e2", "e3", "At", "Bt", "Kt", "Rt", "Bh", "Kh",
             "Mab", "Lab", "Mkb", "Nbr", "Nkr", "T", "TT", "Mk0", "Mk1", "Lk0", "Lk1",
             "VT", "BhT", "KhT", "yB", "yc", "sq2", "rs"]
    T_ = {n: sb("r_" + n, [128, 512], F32) for n in names}
    T_r = {n: Res() for n in names}
    UT = sb("r_UT", [128, 4, 128], F32)
    UT_r = Res()
    xts = sb("r_xts", [128, 128], F32)
    xts_r = Res()
    ST = sb("r_ST", [128, 64], F32)
    ST_r = Res()
    yo = [sb(f"r_yo{i}", [128, 512], BF16) for i in range(2)]
    yo_r = [Res(), Res()]
    bd.op("dve", lambda e: e.memset(ST[:], 0.0), writes=[ST_r])
    bX, bX_r = banks[3], banks_r[3]
    bU, bU_r = banks[4], banks_r[4]
    bS, bS_r = banks[5], banks_r[5]
    bY, bY_r = banks[6], banks_r[6]
    bi = [0]

    def nb():
        b = bi[0] % 3
        bi[0] += 1
        return banks[b], banks_r[b]

    def A(fn, reads, writes):
        bd.op("act", fn, reads=reads, writes=writes)

    def V(fn, reads, writes):
        bd.op("dve", fn, reads=reads, writes=writes)

    def G(fn, reads, writes):
        bd.op("pool", fn, reads=reads, writes=writes)

    def slot(c, h):
        return ((c // 2) * 2 + h) * 64

    def fam(out_name, lname, rname, mask_k):
        pb_, pb_r = nb()
        for c in range(8):
            P0 = (c % 2) * 64
            for h in range(2):
                H0 = h * 64
                o = slot(c, h)
                bd.op("pe", lambda e, P0=P0, H0=H0, o=o, c=c: e.matmul(
                    pb_[P0:P0 + 64, o:o + 64], lhsT=T_[lname][H0:H0 + 64, c * 64:(c + 1) * 64],
                    rhs=T_[rname][H0:H0 + 64, c * 64:(c + 1) * 64], start=True, stop=True),
                    reads=[T_r[lname], T_r[rname]], writes=[pb_r], ser=True)
        V(lambda e: e.tensor_tensor(out=T_[out_name][:], in0=pb_[:, :], in1=msk[:, mask_k, :], op=ALU.mult),
          [pb_r, msk_r], [T_r[out_name]])

    def sq16(out_name, lname, rname, add_name=None):
        pb_, pb_r = nb()
        for c in range(8):
            P0 = (c % 2) * 64
            for h in range(2):
                o = slot(c, h)
                bd.op("pe", lambda e, P0=P0, o=o: e.matmul(
                    pb_[P0:P0 + 64, o:o + 64], lhsT=T_[lname][P0:P0 + 64, o:o + 64],
                    rhs=T_[rname][P0:P0 + 64, o:o + 64], start=True, stop=True),
                    reads=[T_r[lname], T_r[rname]], writes=[pb_r], ser=True)
        if add_name is None:
            A(lambda e: e.activation(out=T_[out_name][:], in_=pb_[:, :], func=AF.Copy), [pb_r], [T_r[out_name]])
        else:
            V(lambda e: e.tensor_tensor(out=T_[out_name][:], in0=pb_[:, :], in1=T_[add_name][:], op=ALU.add),
              [pb_r, T_r[add_name]], [T_r[out_name]])

    def tr4(out_name, src_ap_fn, src_res):
        pb_, pb_r = nb()
        for c2_ in range(4):
            bd.op("pe", lambda e, c2_=c2_: e.transpose(out=pb_[:, c2_ * 128:(c2_ + 1) * 128], in_=src_ap_fn(c2_),
                                                       identity=identf), reads=[src_res, mats_r], writes=[pb_r], ser=True)
        A(lambda e: e.activation(out=T_[out_name][:], in_=pb_[:, :], func=AF.Copy), [pb_r], [T_r[out_name]])

    for tb in range(SEQ // 512):
        sl = slice(tb * 512, (tb + 1) * 512)
        A(lambda e: e.activation(out=T_["th"][:], in_=lS[:, sl], func=AF.Tanh), [lS_r], [T_r["th"]])
        A(lambda e: e.activation(out=T_["sg"][:], in_=lS[:, sl], func=AF.Sigmoid), [lS_r], [T_r["sg"]])
        pw, pw_r = nb()
        bd.op("pe", lambda e: e.matmul(pw[:, :], lhsT=wl[:, 0, :], rhs=T_["th"][:], start=True, stop=True),
              reads=[wl_r, T_r["th"]], writes=[pw_r])
        A(lambda e: e.activation(out=T_["sgm"][:], in_=pw[:, :], func=AF.Sigmoid, bias=cst[:, C_W0:C_W0 + 1]),
          [pw_r, cst_r], [T_r["sgm"]])
        V(lambda e: e.tensor_scalar(out=T_["lw"][:], in0=T_["sgm"][:], scalar1=-float(np.exp(-0.5)), scalar2=None,
                                    op0=ALU.mult), [T_r["sgm"]], [T_r["lw"]])
        pa, pa_r = nb()
        bd.op("pe", lambda e: e.matmul(pa[:, :], lhsT=wl[:, 1, :], rhs=lS[:, sl], start=True, stop=True),
              reads=[wl_r, lS_r], writes=[pa_r])
        A(lambda e: e.activation(out=T_["aT"][:], in_=pa[:, :], func=AF.Sigmoid, bias=cst[:, C_A0:C_A0 + 1]),
          [pa_r, cst_r], [T_r["aT"]])
        pg, pg_r = nb()
        bd.op("pe", lambda e: e.matmul(pg[:, :], lhsT=wl[:, 2, :], rhs=T_["sg"][:], start=True, stop=True),
              reads=[wl_r, T_r["sg"]], writes=[pg_r])
        A(lambda e: e.activation(out=T_["gB"][:], in_=pg[:, :], func=AF.Copy), [pg_r], [T_r["gB"]])
        V(lambda e: e.tensor_scalar(out=T_["kkr"][:], in0=kS[:, sl], scalar1=cst[:, C_KK:C_KK + 1], scalar2=None,
                                    op0=ALU.mult), [kS_r, cst_r], [T_r["kkr"]])
        A(lambda e: e.activation(out=T_["sq"][:], in_=T_["kkr"][:], func=AF.Square), [T_r["kkr"]], [T_r["sq"]])
        pn, pn_r = nb()
        bd.op("pe", lambda e: e.matmul(pn[:, :], lhsT=bones, rhs=T_["sq"][:], start=True, stop=True),
              reads=[mats_r, T_r["sq"]], writes=[pn_r])
        A(lambda e: e.activation(out=T_["nrm"][:], in_=pn[:, :], func=AF.Sqrt), [pn_r], [T_r["nrm"]])
        V(lambda e: e.tensor_scalar(out=T_["nrm"][:], in0=T_["nrm"][:], scalar1=1e-12, scalar2=None, op0=ALU.max),
          [T_r["nrm"]], [T_r["nrm"]])
        V(lambda e: e.reciprocal(out=T_["nrm"][:], in_=T_["nrm"][:]), [T_r["nrm"]], [T_r["nrm"]])
        V(lambda e: e.scalar_tensor_tensor(out=T_["nkk"][:], in0=T_["kkr"][:], scalar=-1.0, in1=T_["nrm"][:],
                                           op0=ALU.mult, op1=ALU.mult), [T_r["kkr"], T_r["nrm"]], [T_r["nkk"]])
        V(lambda e: e.scalar_tensor_tensor(out=T_["bb"][:], in0=T_["nkk"][:], scalar=-1.0, in1=T_["aT"][:],
                                           op0=ALU.mult, op1=ALU.mult), [T_r["nkk"], T_r["aT"]], [T_r["bb"]])
        V(lambda e: e.tensor_scalar(out=T_["t1"][:], in0=T_["aT"][:], scalar1=cst[:, C_KA:C_KA + 1],
                                    scalar2=c2[:, 0:1], op0=ALU.mult, op1=ALU.add),
          [T_r["aT"], cst_r, c2_r], [T_r["t1"]])
        V(lambda e: e.tensor_tensor(out=T_["km"][:], in0=kS[:, sl], in1=T_["t1"][:], op=ALU.mult),
          [kS_r, T_r["t1"]], [T_r["km"]])
        V(lambda e: e.scalar_tensor_tensor(out=T_["prod"][:], in0=rS[:, sl], scalar=cst[:, C_RK:C_RK + 1],
                                           in1=T_["km"][:], op0=ALU.mult, op1=ALU.mult),
          [rS_r, cst_r, T_r["km"]], [T_r["prod"]])
        pb2, pb2_r = nb()
        bd.op("pe", lambda e: e.matmul(pb2[:, :], lhsT=bones, rhs=T_["prod"][:], start=True, stop=True),
              reads=[mats_r, T_r["prod"]], writes=[pb2_r])
        V(lambda e: e.tensor_tensor(out=T_["boB"][:], in0=pb2[:, :], in1=vS[:, sl], op=ALU.mult),
          [pb2_r, vS_r], [T_r["boB"]])
        V(lambda e: e.tensor_tensor_scan(out=T_["cl"][:], data0=msk[:, 4, :], data1=T_["lw"][:], initial=0.0,
                                         op0=ALU.mult, op1=ALU.add), [msk_r, T_r["lw"]], [T_r["cl"]])
        A(lambda e: e.activation(out=T_["e1"][:], in_=T_["cl"][:], func=AF.Exp), [T_r["cl"]], [T_r["e1"]])
        A(lambda e: e.activation(out=T_["e2"][:], in_=T_["cl"][:], func=AF.Exp, scale=-1.0), [T_r["cl"]], [T_r["e2"]])
        V(lambda e: e.tensor_tensor(out=T_["e3"][:], in0=T_["cl"][:], in1=T_["lw"][:], op=ALU.subtract),
          [T_r["cl"], T_r["lw"]], [T_r["e3"]])
        A(lambda e: e.activation(out=T_["e3"][:], in_=T_["e3"][:], func=AF.Exp), [T_r["e3"]], [T_r["e3"]])
        V(lambda e: e.tensor_tensor(out=T_["At"][:], in0=T_["nkk"][:], in1=T_["e3"][:], op=ALU.mult),
          [T_r["nkk"], T_r["e3"]], [T_r["At"]])
        G(lambda e: e.tensor_tensor(out=T_["Bt"][:], in0=T_["bb"][:], in1=T_["e2"][:], op=ALU.mult),
          [T_r["bb"], T_r["e2"]], [T_r["Bt"]])
        V(lambda e: e.tensor_tensor(out=T_["Kt"][:], in0=T_["km"][:], in1=T_["e2"][:], op=ALU.mult),
          [T_r["km"], T_r["e2"]], [T_r["Kt"]])
        G(lambda e: e.tensor_tensor(out=T_["Rt"][:], in0=rS[:, sl], in1=T_["e1"][:], op=ALU.mult),
          [rS_r, T_r["e1"]], [T_r["Rt"]])
        for c in range(8):
            gam = T_["e1"][:, c * 64 + 63:c * 64 + 64]
            V(lambda e, c=c, gam=gam: e.tensor_scalar(out=T_["Bh"][:, c * 64:(c + 1) * 64], in0=T_["Bt"][:, c * 64:(c + 1) * 64],
                                                      scalar1=gam, scalar2=None, op0=ALU.mult),
              [T_r["Bt"], T_r["e1"]], [T_r["Bh"]])
            G(lambda e, c=c, gam=gam: e.tensor_scalar(out=T_["Kh"][:, c * 64:(c + 1) * 64], in0=T_["Kt"][:, c * 64:(c + 1) * 64],
                                                      scalar1=gam, scalar2=None, op0=ALU.mult),
              [T_r["Kt"], T_r["e1"]], [T_r["Kh"]])
        if _DBG_STAGE == 1:
            break
        fam("Mab", "Bt", "At", 0)
        fam("Lab", "At", "Bt", 2)
        fam("Mkb", "Kt", "At", 0)
        fam("Nbr", "Bt", "Rt", 1)
        fam("Nkr", "Kt", "Rt", 1)
        if _DBG_STAGE == 2:
            break
        V(lambda e: e.tensor_tensor(out=T_["T"][:], in0=T_["Mab"][:], in1=msk[:, 3, :], op=ALU.add),
          [T_r["Mab"], msk_r], [T_r["T"]])
        G(lambda e: e.tensor_tensor(out=T_["TT"][:], in0=T_["Lab"][:], in1=msk[:, 3, :], op=ALU.add),
          [T_r["Lab"], msk_r], [T_r["TT"]])
        Mp, Lp = "Mab", "Lab"
        for lev in range(1, 6):
            Mn, Ln = f"Mk{lev % 2}", f"Lk{lev % 2}"
            sq16(Mn, Lp, Mp)
            if lev < 5:
                sq16(Ln, Mp, Lp)
            sq16("T", "TT", Mn, add_name="T")
            if lev < 5:
                sq16("TT", Mn, "TT", add_name="TT")
            Mp, Lp = Mn, Ln
        if _DBG_STAGE == 3:
            break
        tr4("VT", lambda c2_: vS[:, tb * 512 + c2_ * 128: tb * 512 + (c2_ + 1) * 128], vS_r)
        tr4("BhT", lambda c2_: T_["Bh"][:, c2_ * 128:(c2_ + 1) * 128], T_r["Bh"])
        tr4("KhT", lambda c2_: T_["Kh"][:, c2_ * 128:(c2_ + 1) * 128], T_r["Kh"])
        if _DBG_STAGE == 4:
            break
        for c in range(8):
            P0 = (c % 2) * 64
            c2_ = c // 2
            cs = slice(c * 64, (c + 1) * 64)
            for h in range(2):
                H0 = h * 64
                o = slot(c, h)
                vt = T_["VT"][P0:P0 + 64, c2_ * 128 + H0:c2_ * 128 + H0 + 64]
                bd.op("pe", lambda e, P0=P0, H0=H0, cs=cs: e.matmul(
                    bX[P0:P0 + 64, H0:H0 + 64], lhsT=T_["At"][H0:H0 + 64, cs], rhs=ST[H0:H0 + 64, :],
                    start=True, stop=False), reads=[T_r["At"], ST_r], writes=[bX_r], ser=True)
                bd.op("pe", lambda e, P0=P0, H0=H0, o=o, vt=vt: e.matmul(
                    bX[P0:P0 + 64, H0:H0 + 64], lhsT=T_["Mkb"][P0:P0 + 64, o:o + 64], rhs=vt,
                    start=False, stop=True), reads=[T_r["Mkb"], T_r["VT"]], writes=[bX_r], ser=True)
            A(lambda e, P0=P0: e.activation(out=xts[P0:P0 + 64, :], in_=bX[P0:P0 + 64, 0:128], func=AF.Copy),
              [bX_r], [xts_r])
            for h in range(2):
                H0 = h * 64
                o = slot(c, h)
                bd.op("pe", lambda e, P0=P0, H0=H0, o=o: e.matmul(
                    bU[P0:P0 + 64, H0:H0 + 64], lhsT=T_["T"][P0:P0 + 64, o:o + 64], rhs=xts[P0:P0 + 64, H0:H0 + 64],
                    start=True, stop=True), reads=[T_r["T"], xts_r], writes=[bU_r], ser=True)
            V(lambda e, P0=P0, c2_=c2_: e.tensor_copy(out=UT[P0:P0 + 64, c2_, :], in_=bU[P0:P0 + 64, 0:128]),
              [bU_r], [UT_r])
            for h in range(2):
                H0 = h * 64
                o = slot(c, h)
                ut = UT[P0:P0 + 64, c2_, H0:H0 + 64]
                vt = T_["VT"][P0:P0 + 64, c2_ * 128 + H0:c2_ * 128 + H0 + 64]
                bd.op("pe", lambda e, H0=H0, cs=cs: e.matmul(
                    bY[H0:H0 + 64, cs], lhsT=ST[H0:H0 + 64, :], rhs=T_["Rt"][H0:H0 + 64, cs],
                    start=True, stop=False), reads=[ST_r, T_r["Rt"]], writes=[bY_r], ser=True)
                bd.op("pe", lambda e, H0=H0, cs=cs, P0=P0, o=o, ut=ut: e.matmul(
                    bY[H0:H0 + 64, cs], lhsT=ut, rhs=T_["Nbr"][P0:P0 + 64, o:o + 64],
                    start=False, stop=False), reads=[UT_r, T_r["Nbr"]], writes=[bY_r], ser=True)
                bd.op("pe", lambda e, H0=H0, cs=cs, P0=P0, o=o, vt=vt: e.matmul(
                    bY[H0:H0 + 64, cs], lhsT=vt, rhs=T_["Nkr"][P0:P0 + 64, o:o + 64],
                    start=False, stop=True), reads=[T_r["VT"], T_r["Nkr"]], writes=[bY_r], ser=True)
                bd.op("pe", lambda e, H0=H0, P0=P0, c2_=c2_, ut=ut: e.matmul(
                    bS[H0:H0 + 64, 0:64], lhsT=T_["BhT"][P0:P0 + 64, c2_ * 128 + H0:c2_ * 128 + H0 + 64], rhs=ut,
                    start=True, stop=False), reads=[T_r["BhT"], UT_r], writes=[bS_r], ser=True)
                bd.op("pe", lambda e, H0=H0, P0=P0, c2_=c2_, vt=vt: e.matmul(
                    bS[H0:H0 + 64, 0:64], lhsT=T_["KhT"][P0:P0 + 64, c2_ * 128 + H0:c2_ * 128 + H0 + 64], rhs=vt,
                    start=False, stop=True), reads=[T_r["KhT"], T_r["VT"]], writes=[bS_r], ser=True)
            gam = T_["e1"][:, c * 64 + 63:c * 64 + 64]
            V(lambda e, gam=gam: e.scalar_tensor_tensor(out=ST[:], in0=ST[:], scalar=gam, in1=bS[:, 0:64],
                                                        op0=ALU.mult, op1=ALU.add),
              [ST_r, bS_r, T_r["e1"]], [ST_r])
        if _DBG_STAGE == 5:
            break
        A(lambda e: e.activation(out=T_["yB"][:], in_=bY[:, :], func=AF.Copy), [bY_r], [T_r["yB"]])
        pm, pm_r = nb()
        bd.op("pe", lambda e: e.matmul(pm[:, :], lhsT=bones, rhs=T_["yB"][:], start=True, stop=True),
              reads=[mats_r, T_r["yB"]], writes=[pm_r])
        V(lambda e: e.scalar_tensor_tensor(out=T_["yc"][:], in0=pm[:, :], scalar=-1.0 / 64, in1=T_["yB"][:],
                                           op0=ALU.mult, op1=ALU.add), [pm_r, T_r["yB"]], [T_r["yc"]])
        A(lambda e: e.activation(out=T_["sq2"][:], in_=T_["yc"][:], func=AF.Square), [T_r["yc"]], [T_r["sq2"]])
        pv, pv_r = nb()
        bd.op("pe", lambda e: e.matmul(pv[:, :], lhsT=bones, rhs=T_["sq2"][:], start=True, stop=True),
              reads=[mats_r, T_r["sq2"]], writes=[pv_r])
        A(lambda e: e.activation(out=T_["rs"][:], in_=pv[:, :], func=AF.Sqrt, scale=1.0 / 64,
                                 bias=cst[:, C_LNEPS:C_LNEPS + 1]), [pv_r, cst_r], [T_r["rs"]])
        V(lambda e: e.reciprocal(out=T_["rs"][:], in_=T_["rs"][:]), [T_r["rs"]], [T_r["rs"]])
        V(lambda e: e.tensor_tensor(out=T_["yc"][:], in0=T_["yc"][:], in1=T_["rs"][:], op=ALU.mult),
          [T_r["yc"], T_r["rs"]], [T_r["yc"]])
        V(lambda e: e.tensor_scalar(out=T_["yc"][:], in0=T_["yc"][:], scalar1=cst[:, C_LNW:C_LNW + 1],
                                    scalar2=cst[:, C_LNB:C_LNB + 1], op0=ALU.mult, op1=ALU.add),
          [T_r["yc"], cst_r], [T_r["yc"]])
        V(lambda e: e.tensor_tensor(out=T_["yc"][:], in0=T_["yc"][:], in1=T_["boB"][:], op=ALU.add),
          [T_r["yc"], T_r["boB"]], [T_r["yc"]])
        o_, o_r = yo[tb % 2], yo_r[tb % 2]
        V(lambda e, o_=o_: e.tensor_tensor(out=o_[:], in0=T_["yc"][:], in1=T_["gB"][:], op=ALU.mult),
          [T_r["yc"], T_r["gB"]], [o_r])
        bd.dma("sp", yT[256:384, sl], o_[:], reads=[o_r])
    bd.barrier()


def emit_phaseB(nc, mkbld, es, D_, mixers, tag):
    bd = mkbld(tag + "m")
    sb = lambda nm, shape, dt: es.enter_context(nc.sbuf_tensor(_u(nm), shape, dt))
    ps = lambda nm, shape, dt: es.enter_context(nc.psum_tensor(_u(nm), shape, dt))
    cst = sb("B_cst", [128, NCST], F32)
    cst_r = Res()
    bd.dma("sp", cst[:], D_["cst"].ap(), writes=[cst_r])
    mats = sb("B_mats", [128, 4, 128], F32)
    mats_r = Res()
    bd.dma("sp", mats[:], D_["mats"].ap(), writes=[mats_r])
    identb = sb("B_identb", [128, 128], BF16)
    maskb = sb("B_maskb", [128, 128], BF16)
    identb_r, maskb_r = Res(), Res()
    bd.op("dve", lambda e: e.tensor_copy(out=identb[:], in_=mats[:, 0, :]), reads=[mats_r], writes=[identb_r])
    bd.op("dve", lambda e: e.tensor_copy(out=maskb[:], in_=mats[:, 3, :]), reads=[mats_r], writes=[maskb_r])
    banks = [ps(f"B_bank{i}", [128, 512], F32) for i in range(7)]
    banks_r = [Res() for _ in range(7)]
    ptb = ps("B_ptb", [128, 1024], BF16)
    ptb_r = Res()
    K = dict(banks=banks, banks_r=banks_r, ptb=ptb, ptb_r=ptb_r, identb=identb, identb_r=identb_r,
             mask=maskb, mask_r=maskb_r, ones=mats[:, 1, :], ones_r=mats_r, identf=mats[:, 0, :],
             bones=mats[:, 2, :], mats_r=mats_r)
    if "mla" in mixers:
        with ExitStack() as es1:
            emit_mla(bd, nc, es1, D_["pT"], D_["pos"], D_["wuq"], D_["wukv"], D_["yT"], cst, cst_r, K)
    bd.barrier()
    if "mlstm" in mixers:
        with ExitStack() as es1:
            emit_mlstm(bd, nc, es1, D_["pT"], D_["yT"], cst, cst_r, K)
        bd.barrier()
    if "rwkv" in mixers:
        bd = mkbld(tag + "r")
        with ExitStack() as es1:
            (emit_rwkv_chunked if RWKV_CHUNKED else emit_rwkv)(bd, nc, es1, D_, cst, cst_r, K)
        bd.barrier()


def prep_phaseB_consts(P, l, hc):
    c = np.zeros((128, NCST), np.float32)
    c[:, C_QN0] = P["mla_q_norm"][l][0:128]
    c[:, C_QN1] = P["mla_q_norm"][l][128:256]
    c[:, C_KVN0] = P["mla_kv_norm"][l][0:128]
    c[:, C_KVN1] = P["mla_kv_norm"][l][128:256]
    c[:, C_INVF] = np.tile(INV_FREQ, 4)
    c[:, C_SGN] = np.tile(np.concatenate([np.full(32, -1.0, np.float32), np.full(32, 1.0, np.float32)]), 2)
    for h in range(2):
        hh = 2 * hc + h
        c[:, C_MLAO0 + h] = P["mla_out_norm"][l][hh * 128:(hh + 1) * 128]
    mu = P["rwkv_mu"][l]
    ch = slice(hc * 128, hc * 128 + 128)
    c[:, C_MU_R] = mu[0:256][ch]
    c[:, C_MU_K] = mu[256:512][ch]
    c[:, C_MU_V] = mu[512:768][ch]
    c[:, C_MU_L] = mu[768:896]
    c[:, C_W0] = P["rwkv_w0"][l][ch]
    c[:, C_A0] = P["rwkv_a0"][l][ch]
    c[:, C_KK] = P["rwkv_k_k"][l][ch]
    c[:, C_KA] = P["rwkv_k_a"][l][ch]
    c[:, C_RK] = P["rwkv_r_k"][l][ch]
    c[:, C_LNW] = P["rwkv_ln_w"][l][ch]
    c[:, C_LNB] = P["rwkv_ln_b"][l][ch]
    cw = P["mlstm_conv_w"][l]
    cb = P["mlstm_conv_b"][l]
    qs = slice(hc * 64, hc * 64 + 64)
    ks = slice(128 + hc * 64, 128 + hc * 64 + 64)
    for j in range(4):
        c[0:64, C_CWQ0 + j] = cw[j][qs]
        c[0:64, C_CWK0 + j] = cw[j][ks]
    c[0:64, C_CBQ] = cb[qs]
    c[0:64, C_CBK] = cb[ks]
    c[:, C_IB] = P["mlstm_i_bias"][l][hc * 2]
    c[:, C_FB] = P["mlstm_f_bias"][l][hc * 2]
    c[:, C_IB1] = P["mlstm_i_bias"][l][hc * 2 + 1]
    c[:, C_FB1] = P["mlstm_f_bias"][l][hc * 2 + 1]
    c[:, C_MLO] = P["mlstm_out_norm"][l][ch]
    c[:, C_EPS] = EPS
    c[:, C_LNEPS] = 64e-5
    c[:, C_ONE] = 1.0
    hq = []
    hkv = []
    for h in range(2):
        hh = 2 * hc + h
        wq = P["mla_w_uq"][l][:, hh * 192:(hh + 1) * 192]
        hq += [wq[:, 0:128], wq[:, 128:192], wq[:, 160:192], wq[:, 128:160]]
        wkv = P["mla_w_ukv"][l][:, hh * 256:(hh + 1) * 256]
        hkv += [wkv]
    wuq = np.ascontiguousarray(np.concatenate(hq, axis=1))
    wukv = np.ascontiguousarray(np.concatenate(hkv, axis=1))
    wl = np.zeros((128, 3, 128), np.float32)
    wl[0:32, 0, :] = P["rwkv_w2"][l][:, ch]
    wl[32:64, 1, :] = P["rwkv_a2"][l][:, ch]
    wl[64:128, 2, :] = P["rwkv_g2"][l][:, ch]
    return dict(cst=c, wuq=wuq, wukv=wukv, wl=wl)


INV_FREQ = (10000.0 ** (-np.arange(0, 64, 2, dtype=np.float32) / np.float32(64))).astype(np.float32)


def const_mats():
    m = np.zeros((128, 4, 128), np.float32)
    m[:, 0, :] = np.eye(128, dtype=np.float32)
    m[:, 1, :] = 1.0
    m[0:64, 2, 0:64] = 1.0
    m[64:128, 2, 64:128] = 1.0
    m[:, 3, :] = (np.arange(128)[:, None] <= np.arange(128)[None, :]).astype(np.float32)
    return m


W_OUT_PERM = (list(range(0, 256)) + list(range(512, 640)) + list(range(768, 896)) +
              list(range(256, 512)) + list(range(640, 768)) + list(range(896, 1024)))
TBC = 256


def emit_phaseC(nc, bd, es, D_, final, ntok):
    sb = lambda name, shape, dt: es.enter_context(nc.sbuf_tensor(_u(name), shape, dt))
    ps = lambda name, shape, dt: es.enter_context(nc.psum_tensor(_u(name), shape, dt))
    wo = sb("C_wo", [128, 8, D], BF16)
    wg = sb("C_wg", [128, 8, DFF], BF16)
    wu = sb("C_wu", [128, 8, DFF], BF16)
    wd = sb("C_wd", [128, 22, D], BF16)
    w_r = Res()
    HS = DFF // 2
    stage = [sb(f"C_stage{i}", [128, HS], F32) for i in range(3)]
    stage_r = [Res() for _ in range(3)]
    gcol = sb("C_gcol", [128, 8], F32)
    gcol_r = Res()
    idf = sb("C_idf", [128, 128], F32)
    ident = sb("C_ident", [128, 128], BF16)
    idf_r, ident_r = Res(), Res()
    bd.dma("sp", gcol[:], D_["g"].ap(), writes=[gcol_r])
    bd.dma("sp", idf[:], D_["ident"].ap(), writes=[idf_r])
    bd.op("dve", lambda e: e.tensor_copy(out=ident[:], in_=idf[:]), reads=[idf_r], writes=[ident_r])
    if final:
        gbc = sb("C_gbc", [128, D], F32)
        gbc_r = Res()
        bd.dma("sp", gbc[:], bass.AP(D_["gf_t"], 0, [[0, 128], [1, D]]), writes=[gbc_r])
    si = [0]
    queues = ("sp", "pool", "act")
    engs = ("dve", "pool", "act")

    def cast(dst_ap, src_ap, ncols, scale_ap=None):
        i = si[0] % 3
        si[0] += 1
        bd.dma(queues[i], stage[i][:, 0:ncols], src_ap, writes=[stage_r[i]])
        eng = engs[i] if scale_ap is None else ("dve" if i != 2 else "act")
        if eng == "act":
            if scale_ap is None:
                bd.op("act", lambda e: e.activation(out=dst_ap, in_=stage[i][:, 0:ncols], func=AF.Copy),
                      reads=[stage_r[i]], writes=[w_r])
            else:
                bd.op("act", lambda e: e.activation(out=dst_ap, in_=stage[i][:, 0:ncols], func=AF.Copy, scale=scale_ap),
                      reads=[stage_r[i], gcol_r], writes=[w_r])
        elif scale_ap is None:
            bd.op(eng, lambda e: e.tensor_copy(out=dst_ap, in_=stage[i][:, 0:ncols]), reads=[stage_r[i]], writes=[w_r])
        else:
            bd.op(eng, lambda e: e.tensor_scalar(out=dst_ap, in0=stage[i][:, 0:ncols], scalar1=scale_ap, scalar2=None,
                                                 op0=ALU.mult), reads=[stage_r[i], gcol_r], writes=[w_r])

    for kc in range(8):
        cast(wo[:, kc, :], D_["wo"].ap()[kc * 128:(kc + 1) * 128, :], D)
    for kc in range(8):
        for hh in range(2):
            cast(wg[:, kc, hh * HS:(hh + 1) * HS], D_["wg"].ap()[kc * 128:(kc + 1) * 128, hh * HS:(hh + 1) * HS], HS,
                 gcol[:, kc:kc + 1])
            cast(wu[:, kc, hh * HS:(hh + 1) * HS], D_["wu"].ap()[kc * 128:(kc + 1) * 128, hh * HS:(hh + 1) * HS], HS,
                 gcol[:, kc:kc + 1])
    for f in range(22):
        cast(wd[:, f, :], D_["wd"].ap()[f * 128:(f + 1) * 128, :], D)

    NJ = TBC // 128
    xs = sb("C_xs", [128, NJ, D], F32)
    xs_r = [Res() for _ in range(NJ)]
    yTb = sb("C_yTb", [128, 8, TBC], BF16)
    yTb_r = Res()
    hT = sb("C_hT", [128, 8, TBC], BF16)
    hT_r = Res()
    AT = sb("C_AT", [128, 22, TBC], BF16)
    AT_r = Res()
    sgt = [sb(f"C_sg{i}", [128, TBC], F32) for i in range(2)]
    sgt_r = [Res(), Res()]
    tmp = dict(ss=sb("C_ss", [128, 4], F32), ss_r=Res(), junk=sb("C_junk", [128, D], BF16), junk_r=Res(),
               xn=sb("C_xn", [128, D], BF16), xn_r=Res(),
               pt=ps("C_pt", [128, D], BF16), pt_r=Res(), eps=sb("C_eps", [128, 1], F32), eps_r=Res())
    bd.op("dve", lambda e: e.memset(tmp["eps"][:], EPS), writes=[tmp["eps_r"]])
    pb = [ps(f"C_pb{i}", [128, 512], F32) for i in range(6)]
    pb_r = [Res() for _ in range(6)]
    x_ap = D_["x"].ap()
    yT_ap = D_["yT"].ap()
    out_ap = D_["out"].ap()
    for blk in range(ntok // TBC):
        t0 = blk * TBC
        bd.dma("sp", yTb[:], yT_ap[:, t0:t0 + TBC].rearrange("(k p) t -> p k t", p=128), writes=[yTb_r])
        for j in range(NJ):
            bd.dma("pool", xs[:, j, :], x_ap[t0 + j * 128:t0 + (j + 1) * 128, :], writes=[xs_r[j]])
            for ch in range(2):
                po, po_r = pb[ch], pb_r[ch]
                for k in range(8):
                    bd.op("pe", lambda e, k=k, j=j, ch=ch, po=po: e.matmul(
                        po[:, :], lhsT=yTb[:, k, j * 128:(j + 1) * 128], rhs=wo[:, k, ch * 512:(ch + 1) * 512],
                        start=(k == 0), stop=(k == 7)), reads=[yTb_r, w_r], writes=[po_r])
                bd.op("dve", lambda e, j=j, ch=ch, po=po: e.tensor_tensor(
                    out=xs[:, j, ch * 512:(ch + 1) * 512], in0=xs[:, j, ch * 512:(ch + 1) * 512], in1=po[:, :], op=ALU.add),
                    reads=[po_r, xs_r[j]], writes=[xs_r[j]])
            emit_rmsnorm_T(bd, xs[:, j, :], xs_r[j], hT, hT_r, j, tmp, ident)
        for f in range(22):
            pg, pg_r = pb[2 + (f % 2) * 2], pb_r[2 + (f % 2) * 2]
            pu, pu_r = pb[3 + (f % 2) * 2], pb_r[3 + (f % 2) * 2]
            for (pp, pp_r, ww) in ((pg, pg_r, wg), (pu, pu_r, wu)):
                for k in range(8):
                    bd.op("pe", lambda e, k=k, f=f, pp=pp, ww=ww: e.matmul(
                        pp[:, 0:TBC], lhsT=ww[:, k, f * 128:(f + 1) * 128], rhs=hT[:, k, :],
                        start=(k == 0), stop=(k == 7)), reads=[w_r, hT_r], writes=[pp_r])
            sg, sg_r = sgt[f % 2], sgt_r[f % 2]
            bd.op("act", lambda e, sg=sg, pg=pg: e.activation(out=sg[:], in_=pg[:, 0:TBC], func=AF.Silu),
                  reads=[pg_r], writes=[sg_r])
            bd.op("dve", lambda e, sg=sg, pu=pu, f=f: e.tensor_tensor(out=AT[:, f, :], in0=sg[:], in1=pu[:, 0:TBC],
                                                                      op=ALU.mult),
                  reads=[sg_r, pu_r], writes=[AT_r])
        for j in range(NJ):
            for ch in range(2):
                pd, pd_r = pb[ch], pb_r[ch]
                for f in range(22):
                    bd.op("pe", lambda e, f=f, j=j, ch=ch, pd=pd: e.matmul(
                        pd[:, :], lhsT=AT[:, f, j * 128:(j + 1) * 128], rhs=wd[:, f, ch * 512:(ch + 1) * 512],
                        start=(f == 0), stop=(f == 21)), reads=[AT_r, w_r], writes=[pd_r])
                bd.op("dve", lambda e, j=j, ch=ch, pd=pd: e.tensor_tensor(
                    out=xs[:, j, ch * 512:(ch + 1) * 512], in0=xs[:, j, ch * 512:(ch + 1) * 512], in1=pd[:, :], op=ALU.add),
                    reads=[pd_r, xs_r[j]], writes=[xs_r[j]])
            if final:
                ss, ss_r = tmp["ss"], tmp["ss_r"]
                bd.op("dve", lambda e, j=j: e.scalar_tensor_tensor(
                    out=tmp["junk"][:], in0=xs[:, j, :], scalar=1.0, in1=xs[:, j, :], op0=ALU.mult, op1=ALU.mult,
                    accum_out=ss[:, 0:1]), reads=[xs_r[j]], writes=[tmp["junk_r"], ss_r])
                bd.op("act", lambda e: e.activation(out=ss[:, 1:2], in_=ss[:, 0:1], func=AF.Sqrt, scale=1.0 / D,
                                                    bias=tmp["eps"][:, 0:1]), reads=[ss_r, tmp["eps_r"]], writes=[ss_r])
                bd.op("dve", lambda e: e.reciprocal(out=ss[:, 2:3], in_=ss[:, 1:2]), reads=[ss_r], writes=[ss_r])
                bd.op("dve", lambda e, j=j: e.scalar_tensor_tensor(
                    out=xs[:, j, :], in0=xs[:, j, :], scalar=ss[:, 2:3], in1=gbc[:], op0=ALU.mult, op1=ALU.mult),
                    reads=[xs_r[j], ss_r, gbc_r], writes=[xs_r[j]])
            bd.dma("sp", out_ap[t0 + j * 128:t0 + (j + 1) * 128, :], xs[:, j, :], reads=[xs_r[j]])
    bd.barrier()


def build_fused(phases=None):
    nc = bass.Bass("TRN2", target_bir_lowering=False)
    dt = nc.dram_tensor
    x_d = dt("x", [SEQ, D], F32, kind="ExternalInput")
    pos_d = dt("pos", [1, SEQ], I32, kind="ExternalInput")
    wA_d = dt("wA", [2, D, NCOLA], F32, kind="ExternalInput")
    gA_d = dt("gA", [2, 128, 8], F32, kind="ExternalInput")
    id_d = dt("ident", [128, 128], F32, kind="ExternalInput")
    mats_d = dt("mats", [128, 4, 128], F32, kind="ExternalInput")
    rwm_d = dt("rwm", [128, 5, 512], F32, kind="ExternalInput")
    wuq_d = dt("wuq", [2, 2, 256, 512], F32, kind="ExternalInput")
    wukv_d = dt("wukv", [2, 2, 256, 512], F32, kind="ExternalInput")
    cst_d = dt("cst", [2, 2, 128, NCST], F32, kind="ExternalInput")
    wl_d = dt("wl", [2, 2, 128, 3, 128], F32, kind="ExternalInput")
    wo_d = dt("wo", [2, D, D], F32, kind="ExternalInput")
    wg_d = dt("wg", [2, D, DFF], F32, kind="ExternalInput")
    wu_d = dt("wu", [2, D, DFF], F32, kind="ExternalInput")
    wd_d = dt("wd", [2, DFF, D], F32, kind="ExternalInput")
    gC_d = dt("gC", [2, 128, 8], F32, kind="ExternalInput")
    gf_d = dt("gf", [1, D], F32, kind="ExternalInput")
    out_d = dt("out", [SEQ, D], F32, kind="ExternalOutput")
    pT_d = dt("pT_scr", [NCOLA, SEQ], F32)
    yT_d = dt("yT_scr", [D, SEQ], BF16)
    scr_d = dt("rw_scr", [2 * SEQ * 320], F32)
    x1_d = dt("x1_scr", [SEQ, D], F32)
    shared = {}
    mkbld = lambda tag: Bld(nc, tag, shared)
    for l in range(2):
        x_in = x_d if l == 0 else x1_d
        x_out = x1_d if l == 0 else out_d
        if phases is None or f"A{l}" in phases:
          with ExitStack() as es:
            bd = mkbld(f"A{l}")
            emit_phaseA(nc, bd, es, x_in, View(wA_d.ap()[l]), View(gA_d.ap()[l]), id_d, pT_d, SEQ)
        for hc in range(2):
            if phases is not None and f"B{l}{hc}" not in phases:
                continue
            D_ = dict(pT=View(pT_d.ap()[hc * HALF_ROWS:(hc + 1) * HALF_ROWS, :]), pos=pos_d,
                      wuq=View(wuq_d.ap()[l, hc]), wukv=View(wukv_d.ap()[l, hc]), cst=View(cst_d.ap()[l, hc]),
                      wl=View(wl_d.ap()[l, hc]), mats=mats_d, yT=View(yT_d.ap()[hc * 512:(hc + 1) * 512, :]),
                      scr=scr_d, rwm=rwm_d)
            with ExitStack() as es:
                emit_phaseB(nc, mkbld, es, D_, ("mla", "mlstm", "rwkv"), f"B{l}{hc}")
        D_ = dict(x=x_in, yT=yT_d, wo=View(wo_d.ap()[l]), wg=View(wg_d.ap()[l]), wu=View(wu_d.ap()[l]),
                  wd=View(wd_d.ap()[l]), g=View(gC_d.ap()[l]), gf_t=gf_d, ident=id_d, out=x_out)
        if phases is None or f"C{l}" in phases:
          with ExitStack() as es:
            bd = mkbld(f"C{l}")
            emit_phaseC(nc, bd, es, D_, l == 1, SEQ)
    return nc


_CACHE = {}
_PHASES = None


def kernel(**inputs):
    P = {k: np.asarray(v) for k, v in inputs.items()}
    x = np.asarray(P["x"], np.float32)
    positions = np.asarray(P["positions"]).astype(np.int32)
    if "nc" not in _CACHE:
        _CACHE["nc"] = build_fused(_PHASES)
    nc = _CACHE["nc"]
    f32 = lambda a: np.ascontiguousarray(np.asarray(a, np.float32))
    wA = f32(np.stack([prep_w_inA(P["w_in"][l]) for l in range(2)]))
    gA = f32(np.stack([_gcol(P["mix_norm"][l]) for l in range(2)]))
    gC = f32(np.stack([_gcol(P["ffn_norm"][l]) for l in range(2)]))
    pb = [[prep_phaseB_consts(P, l, hc) for hc in range(2)] for l in range(2)]
    stk = lambda key: f32(np.stack([np.stack([pb[l][hc][key] for hc in range(2)]) for l in range(2)]))
    common = dict(
        wA=wA, gA=gA, ident=np.eye(128, dtype=np.float32), mats=const_mats(), rwm=rwkv_masks(),
        wuq=stk("wuq"), wukv=stk("wukv"), cst=stk("cst"), wl=stk("wl"),
        wo=f32(np.stack([P["w_out"][l][W_OUT_PERM, :] for l in range(2)])),
        wg=f32(P["w_gate"]), wu=f32(P["w_up"]), wd=f32(P["w_down"]), gC=gC,
        gf=f32(np.asarray(P["final_norm"]).reshape(1, D)))
    in_maps = []
    for c in range(NCORES):
        b = c // 2
        m = dict(common)
        m["x"] = f32(x[b])
        m["pos"] = np.ascontiguousarray(positions[b:b + 1])
        in_maps.append(m)
    res = run_bass_kernel_spmd(nc, in_maps, core_ids=list(range(NCORES)))
    out = np.zeros((4, SEQ, D), np.float32)
    for b in range(4):
        out[b] = res.results[2 * b]["out"]
    return out
```

```python
from contextlib import ExitStack
import numpy as np
import concourse.bass as bass
import concourse.mybir as mybir
from concourse.bass_utils import run_bass_kernel_spmd

F32 = mybir.dt.float32
BF16 = mybir.dt.bfloat16
I32 = mybir.dt.int32
ALU = mybir.AluOpType
AF = mybir.ActivationFunctionType

D = 1024
SEQ = 4096
NTOK = 2048
DFF = 2816
NCORES = 8
EPS = 1e-6

NCH = 13
CH_ROWS = [128] * 12 + [4]
HALF_ROWS = 12 * 128 + 4
CH_OFF = [i * 128 for i in range(13)]
NCOLA = 2 * HALF_ROWS


def half_cols(hc):
    cols = []
    cols += list(range(0, 256))
    cols += list(range(256, 512))
    cols += list(range(512, 576))
    cols += list(range(544, 576)) + list(range(512, 544))
    R0 = 576
    for part in range(3):
        cols += list(range(R0 + part * 256 + hc * 128, R0 + part * 256 + hc * 128 + 128))
    cols += list(range(R0 + 768, R0 + 896))
    M0 = 576 + 896
    cols += list(range(M0 + hc * 64, M0 + hc * 64 + 64))
    cols += list(range(M0 + 128 + hc * 64, M0 + 128 + hc * 64 + 64))
    cols += list(range(M0 + 256 + hc * 128, M0 + 256 + hc * 128 + 128))
    cols += list(range(M0 + 520 + hc * 128, M0 + 520 + hc * 128 + 128))
    cols += list(range(M0 + 512 + hc * 2, M0 + 512 + hc * 2 + 2))
    cols += list(range(M0 + 516 + hc * 2, M0 + 516 + hc * 2 + 2))
    assert len(cols) == HALF_ROWS
    return cols


class Res:
    __slots__ = ("w", "r", "name")

    def __init__(self, name=""):
        self.w = None
        self.r = {}
        self.name = name


class Bld:
    NDMA = 8

    def __init__(self, nc, tag="", shared=None):
        self.nc = nc
        self.E = {"pe": nc.tensor, "dve": nc.vector, "act": nc.scalar, "pool": nc.gpsimd, "sp": nc.sync}
        self.sems = {}
        self.cnt = {}
        self.seen = {e: {} for e in self.E}
        self.touched = set()
        for e in self.E:
            self.sems[e] = nc.alloc_semaphore(name=f"s{tag}_{e}")
            self.cnt[e] = 0
        if shared is not None and "sems" in shared:
            self.sems.update(shared["sems"])
            self.cnt.update(shared["cnt"])
            self.dslot = shared["dslot"]
        else:
            dsems, dcnt = {}, {}
            self.dslot = {}
            for q in ("sp", "act", "pool"):
                for i in range(self.NDMA):
                    k = f"d{q}{i}"
                    dsems[k] = nc.alloc_semaphore(name=f"sdma_{k}")
                    dcnt[k] = 0
                self.dslot[q] = 0
            self.sems.update(dsems)
            self.cnt.update(dcnt)
            if shared is not None:
                shared["sems"] = dsems
                shared["dslot"] = self.dslot
                shared["cnt"] = {}
        self.shared = shared

    def _sync_shared(self):
        if self.shared is not None:
            for k in self.shared["sems"]:
                self.shared["cnt"][k] = self.cnt[k]

    def _wait(self, eng, deps):
        best = {}
        for k, v in deps:
            if v > best.get(k, 0):
                best[k] = v
        for k, v in best.items():
            if self.seen[eng].get(k, 0) >= v:
                continue
            self.E[eng].wait_ge(self.sems[k], v)
            self.seen[eng][k] = v

    def _deps(self, eng, reads, writes):
        deps = []
        for r in reads:
            if r.w is not None:
                if not (eng == "pe" and r.w[0] == "pe"):
                    deps.append(r.w)
        for w in writes:
            if w.w is not None and (w.w[0] != eng or eng != "pe"):
                deps.append(w.w)
            for k, v in w.r.items():
                if k != eng or eng != "pe":
                    deps.append((k, v))
        return deps

    def _mark(self, ev, reads, writes):
        self.touched.update(reads)
        self.touched.update(writes)
        for r in reads:
            if ev[1] > r.r.get(ev[0], 0):
                r.r[ev[0]] = ev[1]
        for w in writes:
            w.w = ev
            w.r = {}

    def op(self, eng, fn, reads=(), writes=(), ser=False):
        self._wait(eng, self._deps(eng, reads, writes))
        ins = fn(self.E[eng])
        self.cnt[eng] += 1
        ins.then_inc(self.sems[eng], 1)
        self._mark((eng, self.cnt[eng]), reads, writes)
        if ser:
            self._wait(eng, [(eng, self.cnt[eng])])
        return ins

    def dma(self, q, out, in_, reads=(), writes=()):
        i = self.dslot[q]
        self.dslot[q] = (i + 1) % self.NDMA
        k = f"d{q}{i}"
        deps = self._deps(k, reads, writes)
        deps.append((k, self.cnt[k]))
        self._wait(q, deps)
        ins = self.E[q].dma_start(out=out, in_=in_)
        self.cnt[k] += 16
        ins.then_inc(self.sems[k], 16)
        self._mark((k, self.cnt[k]), reads, writes)
        return ins

    def barrier(self):
        deps = [(k, v) for k, v in self.cnt.items() if v > 0]
        for e in ("sp", "pe", "dve", "act", "pool"):
            self._wait(e, deps)
        for e in ("sp", "pe", "dve", "act", "pool"):
            for k, v in deps:
                assert self.seen[e].get(k, 0) >= v
        for r in self.touched:
            r.w = None
            r.r = {}
        self.touched = set()
        self._sync_shared()

    def wait_all(self, eng, ress):
        deps = []
        for r in ress:
            if r.w is not None:
                deps.append(r.w)
        self._wait(eng, deps)


def dram_ap(t, offset, pattern):
    return bass.AP(t, offset, [list(p) for p in pattern])


def emit_rmsnorm_T(bd, x_tile, x_res, hT, hT_res, j, tmp, ident):
    ss, ss_r = tmp["ss"], tmp["ss_r"]
    junk, junk_r = tmp["junk"], tmp["junk_r"]
    xn, xn_r = tmp["xn"], tmp["xn_r"]
    pt, pt_r = tmp["pt"], tmp["pt_r"]
    bd.op("dve", lambda e: e.scalar_tensor_tensor(out=junk[:], in0=x_tile, scalar=1.0, in1=x_tile,
                                                  op0=ALU.mult, op1=ALU.mult, accum_out=ss[:, 0:1]),
          reads=[x_res], writes=[junk_r, ss_r])
    bd.op("act", lambda e: e.activation(out=ss[:, 1:2], in_=ss[:, 0:1], func=AF.Sqrt, scale=1.0 / D,
                                        bias=tmp["eps"][:, 0:1]), reads=[ss_r, tmp["eps_r"]], writes=[ss_r])
    bd.op("dve", lambda e: e.reciprocal(out=ss[:, 2:3], in_=ss[:, 1:2]), reads=[ss_r], writes=[ss_r])
    bd.op("act", lambda e: e.activation(out=xn[:], in_=x_tile, func=AF.Copy, scale=ss[:, 2:3]),
          reads=[x_res, ss_r], writes=[xn_r])
    for kc in range(8):
        bd.op("pe", lambda e, kc=kc: e.transpose(out=pt[:, kc * 128:(kc + 1) * 128],
                                                 in_=xn[:, kc * 128:(kc + 1) * 128], identity=ident[:]),
              reads=[xn_r], writes=[pt_r])
    bd.op("act", lambda e: e.activation(out=hT[:, :, j * 128:(j + 1) * 128],
                                        in_=pt[:].rearrange("p (k t) -> p k t", k=8), func=AF.Copy),
          reads=[pt_r], writes=[hT_res])


_UC = [0]


def _u(name):
    _UC[0] += 1
    return f"{name}_{_UC[0]}"


class View:
    def __init__(self, ap):
        self._ap = ap

    def ap(self):
        return self._ap


def emit_phaseA(nc, bd, es, x_d, w_d, g_d, id_d, pT_d, ntok):
    sb = lambda name, shape, dt: es.enter_context(nc.sbuf_tensor(_u(name), shape, dt))
    ps = lambda name, shape, dt: es.enter_context(nc.psum_tensor(_u(name), shape, dt))
    wb = sb("A_wb", [128, 8, NCOLA], BF16)
    wb_r = [Res() for _ in range(8)]
    stage = [sb(f"A_stage{i}", [128, NCOLA], F32) for i in range(2)]
    stage_r = [Res(), Res()]
    gcol = sb("A_gcol", [128, 8], F32)
    gcol_r = Res()
    idf = sb("A_idf", [128, 128], F32)
    ident = sb("A_ident", [128, 128], BF16)
    ident_r = Res()
    idf_r = Res()
    xt = [sb(f"A_xt{i}", [128, D], F32) for i in range(2)]
    xt_r = [Res(), Res()]
    hT = [sb(f"A_hT{i}", [128, 8, 512], BF16) for i in range(2)]
    hT_r = [Res(), Res()]
    tmp = dict(ss=sb("A_ss", [128, 4], F32), ss_r=Res(), junk=sb("A_junk", [128, D], BF16), junk_r=Res(),
               xn=sb("A_xn", [128, D], BF16), xn_r=Res(),
               pt=ps("A_pt", [128, D], BF16), pt_r=Res(), eps=sb("A_eps", [128, 1], F32), eps_r=Res())
    bd.op("dve", lambda e: e.memset(tmp["eps"][:], EPS), writes=[tmp["eps_r"]])
    NPB = 4
    pb = [ps(f"A_pb{i}", [128, 512], F32) for i in range(NPB)]
    pb_r = [Res() for _ in range(NPB)]
    ost = [sb(f"A_ost{i}", [128, 512], F32) for i in range(4)]
    ost_r = [Res() for _ in range(4)]

    bd.dma("sp", gcol[:], g_d.ap(), writes=[gcol_r])
    bd.dma("sp", idf[:], id_d.ap(), writes=[idf_r])
    bd.op("dve", lambda e: e.tensor_copy(out=ident[:], in_=idf[:]), reads=[idf_r], writes=[ident_r])
    for kc in range(8):
        s = kc % 2
        bd.dma("pool", stage[s][:], w_d.ap()[kc * 128:(kc + 1) * 128, :], writes=[stage_r[s]])
        bd.op("dve", lambda e, kc=kc, s=s: e.tensor_scalar(out=wb[:, kc, :], in0=stage[s][:],
                                                          scalar1=gcol[:, kc:kc + 1], scalar2=None, op0=ALU.mult),
              reads=[stage_r[s], gcol_r], writes=[wb_r[kc]])
    x_ap = x_d.ap()
    pT_ap = pT_d.ap()
    nblk = ntok // 512
    oi = 0
    for blk in range(nblk):
        hb = blk % 2
        for j in range(4):
            ti = blk * 4 + j
            xb = ti % 2
            bd.dma("sp", xt[xb][:], x_ap[ti * 128:(ti + 1) * 128, :], writes=[xt_r[xb]])
            tmp2 = dict(tmp)
            emit_rmsnorm_T(bd, xt[xb][:], xt_r[xb], hT[hb], hT_r[hb], j, tmp2, ident)
        for half in range(2):
            for c in range(NCH):
                m = CH_ROWS[c]
                col0 = half * HALF_ROWS + CH_OFF[c]
                pbi = oi % NPB
                for kc in range(8):
                    bd.op("pe", lambda e, kc=kc, col0=col0, m=m, pbi=pbi, hb=hb: e.matmul(
                        pb[pbi][0:m, :], lhsT=wb[:, kc, col0:col0 + m], rhs=hT[hb][:, kc, :],
                        start=(kc == 0), stop=(kc == 7)),
                        reads=[wb_r[kc], hT_r[hb]], writes=[pb_r[pbi]])
                osi = oi % 4
                eng = "act" if oi % 2 == 0 else "dve"
                if eng == "act":
                    bd.op("act", lambda e, m=m, pbi=pbi, osi=osi: e.activation(
                        out=ost[osi][0:m, :], in_=pb[pbi][0:m, :], func=AF.Copy),
                        reads=[pb_r[pbi]], writes=[ost_r[osi]])
                else:
                    bd.op("dve", lambda e, m=m, pbi=pbi, osi=osi: e.tensor_copy(
                        out=ost[osi][0:m, :], in_=pb[pbi][0:m, :]),
                        reads=[pb_r[pbi]], writes=[ost_r[osi]])
                bd.dma("sp", pT_ap[col0:col0 + m, blk * 512:(blk + 1) * 512], ost[osi][0:m, :],
                       reads=[ost_r[osi]])
                oi += 1
    bd.barrier()


def _gcol(g):
    return np.ascontiguousarray(np.asarray(g, np.float32).reshape(8, 128).T)


def prep_w_inA(w_in_l):
    cols = half_cols(0) + half_cols(1)
    return np.ascontiguousarray(w_in_l[:, cols])


TWO_PI = 6.283185307179586
(C_QN0, C_QN1, C_KVN0, C_KVN1, C_INVF, C_SGN, C_MLAO0, C_MLAO1,
 C_MU_R, C_MU_K, C_MU_V, C_MU_L, C_W0, C_A0, C_KK, C_KA, C_RK, C_LNW, C_LNB,
 C_CWQ0, C_CWQ1, C_CWQ2, C_CWQ3, C_CBQ, C_CWK0, C_CWK1, C_CWK2, C_CWK3, C_CBK,
 C_IB, C_FB, C_MLO, C_EPS, C_LNEPS, C_ONE, C_ZERO, C_IB1, C_FB1) = range(38)
NCST = 38


def emit_attention(bd, nc, es, name, heads, scale_exp, dv, wfun, out_fn, PS):
    sb = lambda nm, shape, dt: es.enter_context(nc.sbuf_tensor(_u(nm), shape, dt))
    LOOK = 3
    NST = len(PS["st"])
    NPT = LOOK + 2
    pT = [sb(f"{name}_pT{i}", [128, 512], BF16) for i in range(NPT)]
    pT_r = [Res() for _ in range(NPT)]
    mask = PS["mask"]
    mask_r = PS["mask_r"]
    blocks = [(h, qb, kt) for h in heads for qb in range(SEQ // 512) for kt in range(4 * (qb + 1))]
    pending = []

    def stage1(i):
        h, qb, kt = blocks[i]
        st, st_r = PS["st"][i % NST]
        kp = PS["kparts"](h, kt)
        qp = PS["qparts"](h, qb)
        n = len(kp)
        for a in range(n):
            bd.op("pe", lambda e, a=a: e.matmul(st[:, :], lhsT=kp[a][0], rhs=qp[a][0],
                                                start=(a == 0), stop=(a == n - 1)),
                  reads=[kp[a][1], qp[a][1]], writes=[st_r])
        p, p_r = pT[i % NPT], pT_r[i % NPT]
        wfun(h, kt, qb, st, st_r, p, p_r)
        jd = kt - 4 * qb
        if jd >= 0:
            bd.op("pool", lambda e: e.tensor_tensor(
                out=p[:, jd * 128:(jd + 1) * 128], in0=p[:, jd * 128:(jd + 1) * 128], in1=mask[:],
                op=ALU.mult), reads=[p_r, mask_r], writes=[p_r])

    def stage2(i):
        h, qb, kt = blocks[i]
        p, p_r = pT[i % NPT], pT_r[i % NPT]
        v_ap, v_r = PS["v"](h, kt)
        for j in range(4):
            qt = 4 * qb + j
            if qt < kt:
                continue
            o, o_r = PS["o"][j]
            bd.op("pe", lambda e, o=o, j=j, qt=qt: e.matmul(
                o[:, 0:dv + 1], lhsT=p[:, j * 128:(j + 1) * 128], rhs=v_ap,
                start=(kt == 0), stop=(kt == qt)), reads=[p_r, v_r], writes=[o_r])
            if kt == qt:
                th = out_fn(h, qt, o, o_r)
                if th is not None:
                    pending.append([3, th])

    nblk = len(blocks)
    for i in range(nblk + LOOK):
        if i < nblk:
            stage1(i)
        if i - LOOK >= 0:
            stage2(i - LOOK)
        for item in pending:
            item[0] -= 1
        while pending and pending[0][0] <= 0:
            pending.pop(0)[1]()
    while pending:
        pending.pop(0)[1]()


def emit_rope_tables(bd, nc, es, pos_d, cst, cst_r, CS, SN, tab_r):
    with ExitStack() as es2:
        sb = lambda nm, shape, dt: es2.enter_context(nc.sbuf_tensor(_u(nm), shape, dt))
        ti = sb("rt_i", [64, SEQ], I32)
        ta = sb("rt_a", [64, SEQ], F32)
        tb = sb("rt_b", [64, SEQ], F32)
        ti_r, ta_r, tb_r = Res(), Res(), Res()
        src = bass.AP(pos_d, 0, [[0, 64], [1, SEQ]])
        bd.dma("sp", ti[:], src, writes=[ti_r])
        bd.op("dve", lambda e: e.tensor_copy(out=ta[:], in_=ti[:]), reads=[ti_r], writes=[ta_r])
        bd.op("dve", lambda e: e.tensor_scalar(out=ta[:], in0=ta[:], scalar1=cst[0:64, C_INVF:C_INVF + 1],
                                               scalar2=None, op0=ALU.mult), reads=[ta_r, cst_r], writes=[ta_r])
        for which in (0, 1):
            shift = 0.0 if which == 0 else TWO_PI / 4
            bd.op("dve", lambda e: e.tensor_scalar(out=tb[:], in0=ta[:], scalar1=shift, scalar2=1.0 / TWO_PI,
                                                   op0=ALU.add, op1=ALU.mult), reads=[ta_r], writes=[tb_r])
            bd.op("dve", lambda e: e.tensor_copy(out=ti[:], in_=tb[:]), reads=[tb_r], writes=[ti_r])
            bd.op("dve", lambda e: e.tensor_copy(out=tb[:], in_=ti[:]), reads=[ti_r], writes=[tb_r])
            bd.op("dve", lambda e: e.scalar_tensor_tensor(out=tb[:], in0=tb[:], scalar=-TWO_PI, in1=ta[:],
                                                          op0=ALU.mult, op1=ALU.add),
                  reads=[tb_r, ta_r], writes=[tb_r])
            bd.op("dve", lambda e: e.tensor_scalar(out=tb[:], in0=tb[:], scalar1=shift, scalar2=TWO_PI / 2,
                                                   op0=ALU.add, op1=ALU.min), reads=[tb_r], writes=[tb_r])
            bd.op("dve", lambda e: e.tensor_scalar(out=tb[:], in0=tb[:], scalar1=-TWO_PI / 2, scalar2=None,
                                                   op0=ALU.max), reads=[tb_r], writes=[tb_r])
            if which == 0:
                bd.op("act", lambda e: e.activation(out=SN[:], in_=tb[:], func=AF.Sin,
                                                    scale=cst[0:64, C_SGN:C_SGN + 1]),
                      reads=[tb_r, cst_r], writes=[tab_r])
            else:
                bd.op("act", lambda e: e.activation(out=CS[:], in_=tb[:], func=AF.Sin),
                      reads=[tb_r], writes=[tab_r])
        bd.barrier()


def emit_mla(bd, nc, es, pT_d, pos_d, wuq_d, wukv_d, yT_d, cst, cst_r, K):
    sb = lambda nm, shape, dt: es.enter_context(nc.sbuf_tensor(_u(nm), shape, dt))
    pT = pT_d.ap()
    yT = yT_d.ap()
    CS = sb("m_CS", [64, SEQ], F32)
    SN = sb("m_SN", [64, SEQ], F32)
    tab_r = Res()
    emit_rope_tables(bd, nc, es, pos_d, cst, cst_r, CS, SN, tab_r)
    wq = sb("m_wq", [128, 2, 512], BF16)
    wkv = sb("m_wkv", [128, 2, 512], BF16)
    wq_r, wkv_r = Res(), Res()
    wst = sb("m_wst", [128, 2, 512], F32)
    wst_r = Res()
    for (wd, wt, wr) in ((wuq_d, wq, wq_r), (wukv_d, wkv, wkv_r)):
        bd.dma("sp", wst[:], wd.ap().rearrange("(k p) n -> p k n", p=128), writes=[wst_r])
        bd.op("dve", lambda e, wt=wt: e.tensor_copy(out=wt[:], in_=wst[:]), reads=[wst_r], writes=[wr])
    Qn = [sb(f"m_Qn{h}", [128, SEQ], BF16) for h in range(2)]
    Qr = [sb(f"m_Qr{h}", [64, SEQ], BF16) for h in range(2)]
    Kn = [sb(f"m_Kn{h}", [128, SEQ], BF16) for h in range(2)]
    Kr = sb("m_Kr", [64, SEQ], BF16)
    V = [sb(f"m_V{h}", [128, 32, 129], BF16) for h in range(2)]
    qk_r = Res()
    for h in range(2):
        bd.op("pool", lambda e, h=h: e.memset(V[h][:, :, 128:129], 1.0), writes=[qk_r])
    banks, banks_r, ptb, ptb_r = K["banks"], K["banks_r"], K["ptb"], K["ptb_r"]
    ones, ones_r = K["ones"], K["ones_r"]
    with ExitStack() as es2:
        sb2 = lambda nm, shape, dt: es2.enter_context(nc.sbuf_tensor(_u(nm), shape, dt))
        cf = [sb2(f"m_cf{i}", [128, 2, 512], F32) for i in range(2)]
        cf_r = [Res(), Res()]
        sq = sb2("m_sq", [128, 2, 512], F32)
        sq_r = Res()
        rs = sb2("m_rs", [128, 512], F32)
        rs_r = Res()
        cn = [sb2(f"m_cn{i}", [128, 2, 512], BF16) for i in range(2)]
        cn_r = [Res(), Res()]
        kx = sb2("m_kx", [64, 2, 512], F32)
        kx_r = Res()
        t1 = sb2("m_t1", [64, 512], F32)
        t2 = sb2("m_t2", [64, 512], F32)
        t1_r, t2_r = Res(), Res()
        bi = 0

        def nb():
            nonlocal bi
            b = bi % 6
            bi += 1
            return banks[b], banks_r[b]

        def rope(dst, x_ap, xs_ap, src_res, sl):
            bd.op("dve", lambda e: e.tensor_tensor(out=t1[:], in0=x_ap, in1=CS[:, sl], op=ALU.mult),
                  reads=src_res + [tab_r], writes=[t1_r])
            bd.op("dve", lambda e: e.tensor_tensor(out=t2[:], in0=xs_ap, in1=SN[:, sl], op=ALU.mult),
                  reads=src_res + [tab_r], writes=[t2_r])
            bd.op("dve", lambda e: e.tensor_tensor(out=dst, in0=t1[:], in1=t2[:], op=ALU.add),
                  reads=[t1_r, t2_r], writes=[qk_r])

        for tb in range(SEQ // 512):
            sl = slice(tb * 512, (tb + 1) * 512)
            for which in (0, 1):
                ci = which
                row0 = which * 256
                bd.dma("sp", cf[ci][:], pT[row0:row0 + 256, sl].rearrange("(k p) t -> p k t", p=128),
                       writes=[cf_r[ci]])
                bd.op("act", lambda e, ci=ci: e.activation(out=sq[:], in_=cf[ci][:], func=AF.Square),
                      reads=[cf_r[ci]], writes=[sq_r])
                pss, pss_r = nb()
                for c in range(2):
                    bd.op("pe", lambda e, c=c, pss=pss: e.matmul(pss[:, :], lhsT=ones[:], rhs=sq[:, c, :],
                                                                 start=(c == 0), stop=(c == 1)),
                          reads=[ones_r, sq_r], writes=[pss_r])
                bd.op("act", lambda e, pss=pss: e.activation(out=rs[:], in_=pss[:, :], func=AF.Sqrt,
                                                             scale=1.0 / 256, bias=cst[:, C_EPS:C_EPS + 1]),
                      reads=[pss_r, cst_r], writes=[rs_r])
                bd.op("dve", lambda e: e.reciprocal(out=rs[:], in_=rs[:]), reads=[rs_r], writes=[rs_r])
                gcol = C_QN0 if which == 0 else C_KVN0
                for c in range(2):
                    bd.op("dve", lambda e, c=c, ci=ci, gcol=gcol: e.scalar_tensor_tensor(
                        out=cn[ci][:, c, :], in0=cf[ci][:, c, :], scalar=cst[:, gcol + c:gcol + c + 1], in1=rs[:],
                        op0=ALU.mult, op1=ALU.mult), reads=[cf_r[ci], rs_r, cst_r], writes=[cn_r[ci]])
                if which == 0:
                    for h in range(2):
                        pq, pq_r = nb()
                        for c in range(2):
                            bd.op("pe", lambda e, c=c, h=h, pq=pq: e.matmul(
                                pq[:, :], lhsT=wq[:, c, h * 256:h * 256 + 128], rhs=cn[0][:, c, :],
                                start=(c == 0), stop=(c == 1)), reads=[wq_r, cn_r[0]], writes=[pq_r])
                        bd.op("act", lambda e, h=h, pq=pq: e.activation(out=Qn[h][:, sl], in_=pq[:, :], func=AF.Copy),
                              reads=[pq_r], writes=[qk_r])
                        pa, pa_r = nb()
                        pb_, pb_r = nb()
                        for (pp, pp_r, off) in ((pa, pa_r, 128), (pb_, pb_r, 192)):
                            for c in range(2):
                                bd.op("pe", lambda e, c=c, h=h, pp=pp, off=off: e.matmul(
                                    pp[0:64, :], lhsT=wq[:, c, h * 256 + off:h * 256 + off + 64], rhs=cn[0][:, c, :],
                                    start=(c == 0), stop=(c == 1)), reads=[wq_r, cn_r[0]], writes=[pp_r])
                        rope(Qr[h][:, sl], pa[0:64, :], pb_[0:64, :], [pa_r, pb_r], sl)
                else:
                    for h in range(2):
                        pk, pk_r = nb()
                        for c in range(2):
                            bd.op("pe", lambda e, c=c, h=h, pk=pk: e.matmul(
                                pk[:, :], lhsT=wkv[:, c, h * 256:h * 256 + 128], rhs=cn[1][:, c, :],
                                start=(c == 0), stop=(c == 1)), reads=[wkv_r, cn_r[1]], writes=[pk_r])
                        bd.op("act", lambda e, h=h, pk=pk: e.activation(out=Kn[h][:, sl], in_=pk[:, :], func=AF.Copy),
                              reads=[pk_r], writes=[qk_r])
                        pv, pv_r = nb()
                        for j in range(4):
                            for c in range(2):
                                bd.op("pe", lambda e, c=c, h=h, j=j, pv=pv: e.matmul(
                                    pv[:, j * 128:(j + 1) * 128], lhsT=cn[1][:, c, j * 128:(j + 1) * 128],
                                    rhs=wkv[:, c, h * 256 + 128:h * 256 + 256],
                                    start=(c == 0 and j == 0), stop=(c == 1)), reads=[wkv_r, cn_r[1]], writes=[pv_r])
                        bd.op("dve", lambda e, h=h, pv=pv: e.tensor_copy(
                            out=V[h][:, tb * 4:(tb + 1) * 4, 0:128],
                            in_=pv[:, :].rearrange("p (j d) -> p j d", j=4)), reads=[pv_r], writes=[qk_r])
            bd.dma("sp", kx[:], pT[512:640, sl].rearrange("(k p) t -> p k t", p=64), writes=[kx_r])
            rope(Kr[:, sl], kx[:, 0, :], kx[:, 1, :], [kx_r], sl)
        bd.barrier()
    with ExitStack() as es3:
        sb3 = lambda nm, shape, dt: es3.enter_context(nc.sbuf_tensor(_u(nm), shape, dt))
        of = sb3("m_of", [128, 132], F32)
        of_r = Res()
        onb = sb3("m_onb", [128, 128], BF16)
        onb_r = Res()
        st_ = sb3("m_stat", [128, 4], F32)
        st_r = Res()
        junk = sb3("m_junk", [128, 128], F32)
        junk_r = Res()
        yst = [sb3(f"m_yst{i}", [128, 512], BF16) for i in range(2)]
        yst_r = [Res(), Res()]
        scale = (128 + 64) ** -0.5

        def wfun(h, kt, qb, st, st_r2, p, p_r):
            bd.op("act", lambda e: e.activation(out=p[:], in_=st[:, :], func=AF.Exp, scale=scale),
                  reads=[st_r2], writes=[p_r])

        onbs = [sb3(f"m_onb{i}", [128, 128], BF16) for i in range(4)]
        onbs_r = [Res() for _ in range(4)]
        oi = [0]

        def out_fn(h, qt, o, o_r):
            ob, ob_r = onbs[oi[0] % 4], onbs_r[oi[0] % 4]
            oi[0] += 1
            bd.op("dve", lambda e: e.reciprocal(out=st_[:, 0:1], in_=o[:, 128:129]), reads=[o_r], writes=[st_r])
            bd.op("dve", lambda e: e.tensor_scalar(out=of[:, 0:128], in0=o[:, 0:128], scalar1=st_[:, 0:1],
                                                   scalar2=None, op0=ALU.mult), reads=[o_r, st_r], writes=[of_r])
            bd.op("dve", lambda e: e.scalar_tensor_tensor(out=junk[:], in0=of[:, 0:128], scalar=1.0, in1=of[:, 0:128],
                                                          op0=ALU.mult, op1=ALU.mult, accum_out=st_[:, 1:2]),
                  reads=[of_r], writes=[junk_r, st_r])
            bd.op("act", lambda e: e.activation(out=st_[:, 2:3], in_=st_[:, 1:2], func=AF.Sqrt, scale=1.0 / 128,
                                                bias=cst[:, C_EPS:C_EPS + 1]), reads=[st_r, cst_r], writes=[st_r])
            bd.op("dve", lambda e: e.reciprocal(out=st_[:, 3:4], in_=st_[:, 2:3]), reads=[st_r], writes=[st_r])
            bd.op("dve", lambda e: e.tensor_scalar(out=ob[:], in0=of[:, 0:128], scalar1=st_[:, 3:4], scalar2=None,
                                                   op0=ALU.mult), reads=[of_r, st_r], writes=[ob_r])

            def fin():
                bd.op("pe", lambda e: e.transpose(out=ptb[:, 0:128], in_=ob[:], identity=K["identb"][:]),
                      reads=[ob_r, K["identb_r"]], writes=[ptb_r])
                ys, ys_r = yst[(qt // 4) % 2], yst_r[(qt // 4) % 2]
                j = qt % 4
                bd.op("act", lambda e: e.activation(out=ys[:, j * 128:(j + 1) * 128], in_=ptb[:, 0:128], func=AF.Copy,
                                                    scale=cst[:, C_MLAO0 + h:C_MLAO0 + h + 1]),
                      reads=[ptb_r, cst_r], writes=[ys_r])
                if j == 3:
                    qb = qt // 4
                    bd.dma("sp", yT[h * 128:(h + 1) * 128, qb * 512:(qb + 1) * 512], ys[:], reads=[ys_r])
            return fin

        PS = dict(st=[(banks[0], banks_r[0]), (banks[1], banks_r[1]), (banks[6], banks_r[6])],
                  o=[(banks[2 + j], banks_r[2 + j]) for j in range(4)],
                  mask=K["mask"], mask_r=K["mask_r"],
                  kparts=lambda h, kt: [(Kn[h][:, kt * 128:(kt + 1) * 128], qk_r), (Kr[:, kt * 128:(kt + 1) * 128], qk_r)],
                  qparts=lambda h, qb: [(Qn[h][:, qb * 512:(qb + 1) * 512], qk_r), (Qr[h][:, qb * 512:(qb + 1) * 512], qk_r)],
                  v=lambda h, kt: (V[h][:, kt, :], qk_r))
        emit_attention(bd, nc, es3, "mla", [0, 1], scale, 128, wfun, out_fn, PS)
        bd.barrier()


def emit_mlstm(bd, nc, es, pT_d, yT_d, cst, cst_r, K):
    sb = lambda nm, shape, dt: es.enter_context(nc.sbuf_tensor(_u(nm), shape, dt))
    pT = pT_d.ap()
    yT = yT_d.ap()
    banks, banks_r, ptb, ptb_r = K["banks"], K["banks_r"], K["ptb"], K["ptb_r"]
    misc, misc_r = banks[6], banks_r[6]
    R_Q, R_K, R_V, R_O, R_G = 1152, 1216, 1280, 1408, 1536
    Qb = sb("l_Qb", [64, SEQ], BF16)
    Kb = sb("l_Kb", [64, SEQ], BF16)
    Vm = sb("l_Vm", [128, 32, 2, 65], BF16)
    Ym = sb("l_Ym", [128, SEQ], BF16)
    uT = sb("l_uT", [128, 2, 32], F32)
    emT = sb("l_emT", [128, 2, 32], F32)
    nPb = [sb(f"l_nPb{h}", [128, SEQ], F32) for h in range(2)]
    prep_r = Res()
    ym_r = Res()
    bd.op("pool", lambda e: e.memset(Vm[:, :, :, 64:65], 1.0), writes=[prep_r])
    with ExitStack() as es2:
        sb2 = lambda nm, shape, dt: es2.enter_context(nc.sbuf_tensor(_u(nm), shape, dt))
        xin = sb2("l_xin", [64, SEQ], F32)
        A = sb2("l_A", [64, SEQ], F32)
        B = sb2("l_B", [64, SEQ], F32)
        xin_r, A_r, B_r = Res(), Res(), Res()
        for (row0, cw0, cb, dst, scl) in ((R_Q, C_CWQ0, C_CBQ, Qb, 32 ** -0.5), (R_K, C_CWK0, C_CBK, Kb, 1.0)):
            bd.dma("sp", xin[:], pT[row0:row0 + 64, :], writes=[xin_r])
            bd.op("dve", lambda e, cw0=cw0, cb=cb: e.tensor_scalar(
                out=A[:], in0=xin[:], scalar1=cst[0:64, cw0 + 3:cw0 + 4], scalar2=cst[0:64, cb:cb + 1],
                op0=ALU.mult, op1=ALU.add), reads=[xin_r, cst_r], writes=[A_r])
            src, src_r, dstt, dst_r = A, A_r, B, B_r
            for sh in (1, 2, 3):
                bd.op("dve", lambda e, sh=sh, cw0=cw0, src=src, dstt=dstt: e.scalar_tensor_tensor(
                    out=dstt[:, sh:], in0=xin[:, 0:SEQ - sh], scalar=cst[0:64, cw0 + 3 - sh:cw0 + 4 - sh],
                    in1=src[:, sh:], op0=ALU.mult, op1=ALU.add), reads=[xin_r, cst_r, src_r], writes=[dst_r])
                bd.op("dve", lambda e, sh=sh, src=src, dstt=dstt: e.tensor_copy(out=dstt[:, 0:sh], in_=src[:, 0:sh]),
                      reads=[src_r], writes=[dst_r])
                src, src_r, dstt, dst_r = dstt, dst_r, src, src_r
            bd.op("act", lambda e, src=src: e.activation(out=xin[:], in_=src[:], func=AF.Silu),
                  reads=[src_r], writes=[xin_r])
            bd.op("dve", lambda e, dst=dst, scl=scl: e.tensor_scalar(out=dst[:], in0=xin[:], scalar1=scl, scalar2=None,
                                                                     op0=ALU.mult), reads=[xin_r], writes=[prep_r])
        bd.barrier()
    with ExitStack() as es2:
        sb2 = lambda nm, shape, dt: es2.enter_context(nc.sbuf_tensor(_u(nm), shape, dt))
        t0 = sb2("l_t0", [1, SEQ], F32)
        t1 = sb2("l_t1", [1, SEQ], F32)
        t2 = sb2("l_t2", [1, SEQ], F32)
        onesrow = sb2("l_onesrow", [1, SEQ], F32)
        vin = sb2("l_vin", [128, SEQ], F32)
        t0_r, t1_r, t2_r, or_r, vin_r = Res(), Res(), Res(), Res(), Res()
        bd.op("dve", lambda e: e.memset(onesrow[:], 1.0), writes=[or_r])
        identf = K["identf"]
        for h in range(2):
            cib = C_IB if h == 0 else C_IB1
            cfb = C_FB if h == 0 else C_FB1
            bd.dma("sp", t0[:], pT[R_G + h:R_G + h + 1, :], writes=[t0_r])
            bd.dma("sp", t1[:], pT[R_G + 2 + h:R_G + 3 + h, :], writes=[t1_r])
            bd.op("dve", lambda e, cib=cib: e.tensor_scalar(out=t0[:], in0=t0[:], scalar1=cst[0:1, cib:cib + 1],
                                                            scalar2=None, op0=ALU.add), reads=[t0_r, cst_r], writes=[t0_r])
            bd.op("act", lambda e, cfb=cfb: e.activation(out=t1[:], in_=t1[:], func=AF.Sigmoid,
                                                         bias=cst[0:1, cfb:cfb + 1]), reads=[t1_r, cst_r], writes=[t1_r])
            bd.op("act", lambda e: e.activation(out=t1[:], in_=t1[:], func=AF.Ln), reads=[t1_r], writes=[t1_r])
            bd.op("dve", lambda e: e.tensor_tensor_scan(out=t2[:], data0=onesrow[:], data1=t1[:], initial=0.0,
                                                        op0=ALU.mult, op1=ALU.add), reads=[or_r, t1_r], writes=[t2_r])
            bd.op("dve", lambda e: e.tensor_tensor(out=t0[:], in0=t0[:], in1=t2[:], op=ALU.subtract),
                  reads=[t0_r, t2_r], writes=[t0_r])
            bd.op("dve", lambda e: e.tensor_tensor_scan(out=t1[:], data0=onesrow[:], data1=t0[:], initial=0.0,
                                                        op0=ALU.mult, op1=ALU.max), reads=[or_r, t0_r], writes=[t1_r])
            bd.op("dve", lambda e: e.tensor_tensor(out=t2[:], in0=t2[:], in1=t1[:], op=ALU.add),
                  reads=[t2_r, t1_r], writes=[t2_r])
            for jt in range(32):
                bd.op("pe", lambda e, jt=jt: e.transpose(out=misc[:, jt:jt + 1], in_=t0[0:1, jt * 128:(jt + 1) * 128],
                                                         identity=identf[0:1, 0:1]), reads=[t0_r, K["mats_r"]], writes=[misc_r])
                bd.op("pe", lambda e, jt=jt: e.transpose(out=misc[:, 32 + jt:33 + jt], in_=t2[0:1, jt * 128:(jt + 1) * 128],
                                                         identity=identf[0:1, 0:1]), reads=[t2_r, K["mats_r"]], writes=[misc_r])
            bd.op("dve", lambda e, h=h: e.tensor_copy(out=uT[:, h, :], in_=misc[:, 0:32]), reads=[misc_r], writes=[prep_r])
            bd.op("act", lambda e, h=h: e.activation(out=emT[:, h, :], in_=misc[:, 32:64], func=AF.Exp, scale=-1.0),
                  reads=[misc_r], writes=[prep_r])
            for tb in range(SEQ // 512):
                bd.op("pe", lambda e, tb=tb: e.matmul(misc[:, :], lhsT=K["ones"][0:1, :], rhs=t1[0:1, tb * 512:(tb + 1) * 512],
                                                      start=True, stop=True), reads=[t1_r, K["ones_r"]], writes=[misc_r])
                bd.op("act", lambda e, tb=tb, h=h: e.activation(out=nPb[h][:, tb * 512:(tb + 1) * 512], in_=misc[:, :],
                                                                func=AF.Copy, scale=-1.0), reads=[misc_r], writes=[prep_r])
        bd.dma("sp", vin[:], pT[R_V:R_V + 128, :], writes=[vin_r])
        for jt in range(32):
            bd.op("pe", lambda e, jt=jt: e.transpose(out=misc[:, 0:128], in_=vin[:, jt * 128:(jt + 1) * 128],
                                                     identity=identf), reads=[vin_r, K["mats_r"]], writes=[misc_r])
            bd.op("dve", lambda e, jt=jt: e.tensor_copy(out=Vm[:, jt, :, 0:64],
                                                        in_=misc[:, 0:128].rearrange("p (h d) -> p h d", h=2)),
                  reads=[misc_r], writes=[prep_r])
        bd.barrier()
    with ExitStack() as es3:
        sb3 = lambda nm, shape, dt: es3.enter_context(nc.sbuf_tensor(_u(nm), shape, dt))
        Wt = [sb3(f"l_W{i}", [128, 512], F32) for i in range(2)]
        Wt_r = [Res(), Res()]
        of = sb3("l_of", [128, 64], F32)
        of_r = Res()
        onb = sb3("l_onb", [128, 64], BF16)
        onb_r = Res()
        st_ = sb3("l_stat", [128, 6], F32)
        st_r = Res()
        junk = sb3("l_junk", [128, 64], F32)
        junk_r = Res()
        wi = [0]

        def wfun(h, kt, qb, st, st_r2, p, p_r):
            w, w_r = Wt[wi[0] % 2], Wt_r[wi[0] % 2]
            wi[0] += 1
            bd.op("act", lambda e: e.activation(out=w[:], in_=nPb[h][:, qb * 512:(qb + 1) * 512], func=AF.Exp,
                                                bias=uT[:, h, kt:kt + 1]), reads=[prep_r], writes=[w_r])
            bd.op("dve", lambda e: e.tensor_tensor(out=p[:], in0=st[:, :], in1=w[:], op=ALU.mult),
                  reads=[st_r2, w_r], writes=[p_r])

        onbs = [sb3(f"l_onb{i}", [128, 64], BF16) for i in range(4)]
        onbs_r = [Res() for _ in range(4)]
        oi = [0]

        def out_fn(h, qt, o, o_r):
            ob, ob_r = onbs[oi[0] % 4], onbs_r[oi[0] % 4]
            oi[0] += 1
            bd.op("act", lambda e: e.activation(out=st_[:, 0:1], in_=o[:, 64:65], func=AF.Abs), reads=[o_r], writes=[st_r])
            bd.op("dve", lambda e: e.tensor_tensor(out=st_[:, 1:2], in0=st_[:, 0:1], in1=emT[:, h, qt:qt + 1], op=ALU.max),
                  reads=[st_r, prep_r], writes=[st_r])
            bd.op("dve", lambda e: e.reciprocal(out=st_[:, 2:3], in_=st_[:, 1:2]), reads=[st_r], writes=[st_r])
            bd.op("dve", lambda e: e.tensor_scalar(out=of[:], in0=o[:, 0:64], scalar1=st_[:, 2:3], scalar2=None,
                                                   op0=ALU.mult), reads=[o_r, st_r], writes=[of_r])
            bd.op("dve", lambda e: e.scalar_tensor_tensor(out=junk[:], in0=of[:], scalar=1.0, in1=of[:],
                                                          op0=ALU.mult, op1=ALU.mult, accum_out=st_[:, 3:4]),
                  reads=[of_r], writes=[junk_r, st_r])
            bd.op("act", lambda e: e.activation(out=st_[:, 4:5], in_=st_[:, 3:4], func=AF.Sqrt, scale=1.0 / 64,
                                                bias=cst[:, C_EPS:C_EPS + 1]), reads=[st_r, cst_r], writes=[st_r])
            bd.op("dve", lambda e: e.reciprocal(out=st_[:, 5:6], in_=st_[:, 4:5]), reads=[st_r], writes=[st_r])
            bd.op("dve", lambda e: e.tensor_scalar(out=ob[:], in0=of[:], scalar1=st_[:, 5:6], scalar2=None, op0=ALU.mult),
                  reads=[of_r, st_r], writes=[ob_r])

            def fin():
                bd.op("pe", lambda e: e.transpose(out=ptb[h * 64:(h + 1) * 64, 0:128], in_=ob[:], identity=K["identb"][:]),
                      reads=[ob_r, K["identb_r"]], writes=[ptb_r])
                bd.op("act", lambda e: e.activation(out=Ym[h * 64:(h + 1) * 64, qt * 128:(qt + 1) * 128],
                                                    in_=ptb[h * 64:(h + 1) * 64, 0:128], func=AF.Copy,
                                                    scale=cst[h * 64:(h + 1) * 64, C_MLO:C_MLO + 1]),
                      reads=[ptb_r, cst_r], writes=[ym_r])
            return fin

        PS = dict(st=[(banks[0], banks_r[0]), (banks[1], banks_r[1]), (banks[6], banks_r[6])],
                  o=[(banks[2 + j], banks_r[2 + j]) for j in range(4)],
                  mask=K["mask"], mask_r=K["mask_r"],
                  kparts=lambda h, kt: [(Kb[h * 32:(h + 1) * 32, kt * 128:(kt + 1) * 128], prep_r)],
                  qparts=lambda h, qb: [(Qb[h * 32:(h + 1) * 32, qb * 512:(qb + 1) * 512], prep_r)],
                  v=lambda h, kt: (Vm[:, kt, h, :], prep_r))
        emit_attention(bd, nc, es3, "mls", [0, 1], 1.0, 64, wfun, out_fn, PS)
        og = sb3("l_og", [128, SEQ], F32)
        og_r = Res()
        bd.dma("sp", og[:], pT[R_O:R_O + 128, :], writes=[og_r])
        bd.op("act", lambda e: e.activation(out=og[:], in_=og[:], func=AF.Sigmoid), reads=[og_r], writes=[og_r])
        bd.op("dve", lambda e: e.tensor_tensor(out=Ym[:], in0=Ym[:], in1=og[:], op=ALU.mult),
              reads=[ym_r, og_r], writes=[ym_r])
        bd.dma("sp", yT[384:512, :], Ym[:], reads=[ym_r])
        bd.barrier()


RW_T = 16


def emit_rwkv(bd, nc, es, D_, cst, cst_r, K):
    sb = lambda nm, shape, dt: es.enter_context(nc.sbuf_tensor(_u(nm), shape, dt))
    pT = D_["pT"].ap()
    yT = D_["yT"].ap()
    scr = D_["scr"]
    banks, banks_r = K["banks"], K["banks_r"]
    bones, identf, mats_r = K["bones"], K["identf"], K["mats_r"]
    R_R, R_K, R_V, R_L = 640, 768, 896, 1024
    vS = sb("r_vS", [128, SEQ], F32)
    gS = sb("r_gS", [128, SEQ], F32)
    boS = sb("r_boS", [128, SEQ], F32)
    yS = sb("r_yS", [128, SEQ], F32)
    vS_r, gS_r, boS_r, yS_r = Res(), Res(), Res(), Res()
    wl = sb("r_wl", [128, 3, 128], F32)
    wl_r = Res()
    bd.dma("sp", wl[:], D_["wl"].ap(), writes=[wl_r])
    c2 = sb("r_c2", [128, 2], F32)
    c2_r = Res()
    bd.op("dve", lambda e: e.tensor_scalar(out=c2[:, 0:1], in0=cst[:, C_KA:C_KA + 1], scalar1=-1.0, scalar2=1.0,
                                           op0=ALU.mult, op1=ALU.add), reads=[cst_r], writes=[c2_r])
    scr_r = Res()
    with ExitStack() as es2:
        sb2 = lambda nm, shape, dt: es2.enter_context(nc.sbuf_tensor(_u(nm), shape, dt))
        rS = sb2("r_rS", [128, SEQ], F32)
        kS = sb2("r_kS", [128, SEQ], F32)
        lS = sb2("r_lS", [128, SEQ], F32)
        dd = sb2("r_dd", [128, SEQ], F32)
        rS_r, kS_r, lS_r, dd_r = Res(), Res(), Res(), Res()
        for (row0, t, t_r, mu) in ((R_R, rS, rS_r, C_MU_R), (R_K, kS, kS_r, C_MU_K), (R_V, vS, vS_r, C_MU_V),
                                   (R_L, lS, lS_r, C_MU_L)):
            bd.dma("sp", t[:], pT[row0:row0 + 128, :], writes=[t_r])
            bd.op("dve", lambda e, t=t: e.tensor_tensor(out=dd[:, 1:SEQ], in0=t[:, 0:SEQ - 1], in1=t[:, 1:SEQ],
                                                        op=ALU.subtract), reads=[t_r], writes=[dd_r])
            bd.op("dve", lambda e, t=t: e.tensor_scalar(out=dd[:, 0:1], in0=t[:, 0:1], scalar1=-1.0, scalar2=None,
                                                        op0=ALU.mult), reads=[t_r], writes=[dd_r])
            bd.op("dve", lambda e, t=t, mu=mu: e.scalar_tensor_tensor(out=t[:], in0=dd[:], scalar=cst[:, mu:mu + 1],
                                                                      in1=t[:], op0=ALU.mult, op1=ALU.add),
                  reads=[dd_r, t_r, cst_r], writes=[t_r])
        names = ["th", "sg", "sgm", "wd", "aT", "kkr", "sq", "nrm", "nkk", "bb", "t1", "km", "prod"]
        T_ = {n: sb2("r_" + n, [128, 512], F32) for n in names}
        T_r = {n: Res() for n in names}
        stg = [sb2(f"r_stg{i}", [128, 5, 128], F32) for i in range(2)]
        stg_r = [Res(), Res()]
        bi = [0]

        def nb():
            b = bi[0] % 7
            bi[0] += 1
            return banks[b], banks_r[b]

        def A(fn, reads, writes):
            bd.op("act", fn, reads=reads, writes=writes)

        def V(fn, reads, writes):
            bd.op("dve", fn, reads=reads, writes=writes)

        ti = 0
        for tb in range(SEQ // 512):
            sl = slice(tb * 512, (tb + 1) * 512)
            A(lambda e: e.activation(out=T_["th"][:], in_=lS[:, sl], func=AF.Tanh), [lS_r], [T_r["th"]])
            A(lambda e: e.activation(out=T_["sg"][:], in_=lS[:, sl], func=AF.Sigmoid), [lS_r], [T_r["sg"]])
            pw, pw_r = nb()
            bd.op("pe", lambda e: e.matmul(pw[:, :], lhsT=wl[:, 0, :], rhs=T_["th"][:], start=True, stop=True),
                  reads=[wl_r, T_r["th"]], writes=[pw_r])
            A(lambda e: e.activation(out=T_["sgm"][:], in_=pw[:, :], func=AF.Sigmoid, bias=cst[:, C_W0:C_W0 + 1]),
              [pw_r, cst_r], [T_r["sgm"]])
            A(lambda e: e.activation(out=T_["wd"][:], in_=T_["sgm"][:], func=AF.Exp, scale=-float(np.exp(-0.5))),
              [T_r["sgm"]], [T_r["wd"]])
            pa, pa_r = nb()
            bd.op("pe", lambda e: e.matmul(pa[:, :], lhsT=wl[:, 1, :], rhs=lS[:, sl], start=True, stop=True),
                  reads=[wl_r, lS_r], writes=[pa_r])
            A(lambda e: e.activation(out=T_["aT"][:], in_=pa[:, :], func=AF.Sigmoid, bias=cst[:, C_A0:C_A0 + 1]),
              [pa_r, cst_r], [T_r["aT"]])
            pg, pg_r = nb()
            bd.op("pe", lambda e: e.matmul(pg[:, :], lhsT=wl[:, 2, :], rhs=T_["sg"][:], start=True, stop=True),
                  reads=[wl_r, T_r["sg"]], writes=[pg_r])
            A(lambda e: e.activation(out=gS[:, sl], in_=pg[:, :], func=AF.Copy), [pg_r], [gS_r])
            V(lambda e: e.tensor_scalar(out=T_["kkr"][:], in0=kS[:, sl], scalar1=cst[:, C_KK:C_KK + 1], scalar2=None,
                                        op0=ALU.mult), [kS_r, cst_r], [T_r["kkr"]])
            A(lambda e: e.activation(out=T_["sq"][:], in_=T_["kkr"][:], func=AF.Square), [T_r["kkr"]], [T_r["sq"]])
            pn, pn_r = nb()
            bd.op("pe", lambda e: e.matmul(pn[:, :], lhsT=bones, rhs=T_["sq"][:], start=True, stop=True),
                  reads=[mats_r, T_r["sq"]], writes=[pn_r])
            A(lambda e: e.activation(out=T_["nrm"][:], in_=pn[:, :], func=AF.Sqrt), [pn_r], [T_r["nrm"]])
            V(lambda e: e.tensor_scalar(out=T_["nrm"][:], in0=T_["nrm"][:], scalar1=1e-12, scalar2=None, op0=ALU.max),
              [T_r["nrm"]], [T_r["nrm"]])
            V(lambda e: e.reciprocal(out=T_["nrm"][:], in_=T_["nrm"][:]), [T_r["nrm"]], [T_r["nrm"]])
            V(lambda e: e.scalar_tensor_tensor(out=T_["nkk"][:], in0=T_["kkr"][:], scalar=-1.0, in1=T_["nrm"][:],
                                               op0=ALU.mult, op1=ALU.mult), [T_r["kkr"], T_r["nrm"]], [T_r["nkk"]])
            V(lambda e: e.scalar_tensor_tensor(out=T_["bb"][:], in0=T_["nkk"][:], scalar=-1.0, in1=T_["aT"][:],
                                               op0=ALU.mult, op1=ALU.mult), [T_r["nkk"], T_r["aT"]], [T_r["bb"]])
            V(lambda e: e.tensor_scalar(out=T_["t1"][:], in0=T_["aT"][:], scalar1=cst[:, C_KA:C_KA + 1],
                                        scalar2=c2[:, 0:1], op0=ALU.mult, op1=ALU.add),
              [T_r["aT"], cst_r, c2_r], [T_r["t1"]])
            V(lambda e: e.tensor_tensor(out=T_["km"][:], in0=kS[:, sl], in1=T_["t1"][:], op=ALU.mult),
              [kS_r, T_r["t1"]], [T_r["km"]])
            V(lambda e: e.scalar_tensor_tensor(out=T_["prod"][:], in0=rS[:, sl], scalar=cst[:, C_RK:C_RK + 1],
                                               in1=T_["km"][:], op0=ALU.mult, op1=ALU.mult),
              [rS_r, cst_r, T_r["km"]], [T_r["prod"]])
            pb_, pb_r = nb()
            bd.op("pe", lambda e: e.matmul(pb_[:, :], lhsT=bones, rhs=T_["prod"][:], start=True, stop=True),
                  reads=[mats_r, T_r["prod"]], writes=[pb_r])
            V(lambda e: e.tensor_tensor(out=boS[:, sl], in0=pb_[:, :], in1=vS[:, sl], op=ALU.mult),
              [pb_r, vS_r], [boS_r])
            for j in range(4):
                t0 = tb * 512 + j * 128
                px, px_r = nb()
                py, py_r = nb()
                srcs = [(T_["nkk"][:, j * 128:(j + 1) * 128], T_r["nkk"]), (T_["wd"][:, j * 128:(j + 1) * 128], T_r["wd"]),
                        (T_["bb"][:, j * 128:(j + 1) * 128], T_r["bb"]), (T_["km"][:, j * 128:(j + 1) * 128], T_r["km"]),
                        (rS[:, t0:t0 + 128], rS_r)]
                for q, (ap_, r_) in enumerate(srcs):
                    if q < 4:
                        bd.op("pe", lambda e, q=q, ap_=ap_: e.transpose(out=px[:, q * 128:(q + 1) * 128], in_=ap_,
                                                                       identity=identf), reads=[r_, mats_r], writes=[px_r])
                    else:
                        bd.op("pe", lambda e, ap_=ap_: e.transpose(out=py[:, 0:128], in_=ap_, identity=identf),
                              reads=[r_, mats_r], writes=[py_r])
                sg_, sg_r = stg[ti % 2], stg_r[ti % 2]
                ti += 1
                A(lambda e, sg_=sg_: e.activation(out=sg_[:, 0:4, :], in_=px[:, :].rearrange("p (q c) -> p q c", q=4),
                                                  func=AF.Copy), [px_r], [sg_r])
                V(lambda e, sg_=sg_: e.tensor_copy(out=sg_[:, 4, :], in_=py[:, 0:128]), [py_r], [sg_r])
                for h in range(2):
                    dst = bass.AP(scr, h * SEQ * 320 + t0 * 320, [[320, 128], [64, 5], [1, 64]])
                    bd.dma("sp" if h == 0 else "act", dst, sg_[:, :, h * 64:(h + 1) * 64], reads=[sg_r], writes=[scr_r])
        bd.barrier()
    with ExitStack() as es3:
        sb3 = lambda nm, shape, dt: es3.enter_context(nc.sbuf_tensor(_u(nm), shape, dt))
        T = RW_T
        NB = 3
        BC = [sb3(f"r_BC{i}", [128, T, 5, 64], F32) for i in range(NB)]
        BC_r = [Res() for _ in range(NB)]
        S = sb3("r_S", [128, 64], F32)
        junk = sb3("r_junk", [128, 64], F32)
        sa = sb3("r_sa", [128, 1], F32)
        S_r, junk_r, sa_r = Res(), Res(), Res()
        bd.op("dve", lambda e: e.memset(S[:], 0.0), writes=[S_r])
        nchunk = SEQ // T

        def load(ci):
            b = ci % NB
            for h in range(2):
                src = bass.AP(scr, h * SEQ * 320 + ci * T * 320, [[0, 64], [1, T * 320]])
                bd.dma("sp" if h == 0 else "act", BC[b][h * 64:(h + 1) * 64, :, :, :].rearrange("p t q j -> p (t q j)"),
                       src, reads=[scr_r], writes=[BC_r[b]])

        load(0)
        load(1)
        for ci in range(nchunk):
            if ci + 2 < nchunk:
                load(ci + 2)
            b = ci % NB
            bc, bc_r = BC[b], BC_r[b]
            for tt in range(T):
                t = ci * T + tt
                bd.op("dve", lambda e, bc=bc, tt=tt: e.scalar_tensor_tensor(
                    out=junk[:], in0=S[:], scalar=1.0, in1=bc[:, tt, 0, :], op0=ALU.mult, op1=ALU.mult,
                    accum_out=sa[:, 0:1]), reads=[S_r, bc_r], writes=[junk_r, sa_r])
                bd.op("dve", lambda e, bc=bc, tt=tt: e.tensor_tensor(out=S[:], in0=S[:], in1=bc[:, tt, 1, :], op=ALU.mult),
                      reads=[S_r, bc_r], writes=[S_r])
                bd.op("dve", lambda e, bc=bc, tt=tt: e.scalar_tensor_tensor(
                    out=S[:], in0=bc[:, tt, 2, :], scalar=sa[:, 0:1], in1=S[:], op0=ALU.mult, op1=ALU.add),
                    reads=[S_r, bc_r, sa_r], writes=[S_r])
                bd.op("dve", lambda e, bc=bc, tt=tt, t=t: e.scalar_tensor_tensor(
                    out=S[:], in0=bc[:, tt, 3, :], scalar=vS[:, t:t + 1], in1=S[:], op0=ALU.mult, op1=ALU.add),
                    reads=[S_r, bc_r, vS_r], writes=[S_r])
                bd.op("dve", lambda e, bc=bc, tt=tt, t=t: e.scalar_tensor_tensor(
                    out=junk[:], in0=S[:], scalar=1.0, in1=bc[:, tt, 4, :], op0=ALU.mult, op1=ALU.mult,
                    accum_out=yS[:, t:t + 1]), reads=[S_r, bc_r], writes=[junk_r, yS_r])
        bd.barrier()
    with ExitStack() as es4:
        sb4 = lambda nm, shape, dt: es4.enter_context(nc.sbuf_tensor(_u(nm), shape, dt))
        yc = sb4("r_yc", [128, 512], F32)
        sq = sb4("r_sq2", [128, 512], F32)
        rs = sb4("r_rs", [128, 512], F32)
        yo = [sb4(f"r_yo{i}", [128, 512], BF16) for i in range(2)]
        yc_r, sq_r, rs_r = Res(), Res(), Res()
        yo_r = [Res(), Res()]
        for tb in range(SEQ // 512):
            sl = slice(tb * 512, (tb + 1) * 512)
            pm, pm_r = banks[tb % 2], banks_r[tb % 2]
            pv, pv_r = banks[2 + tb % 2], banks_r[2 + tb % 2]
            bd.op("pe", lambda e: e.matmul(pm[:, :], lhsT=bones, rhs=yS[:, sl], start=True, stop=True),
                  reads=[mats_r, yS_r], writes=[pm_r])
            bd.op("dve", lambda e: e.scalar_tensor_tensor(out=yc[:], in0=pm[:, :], scalar=-1.0 / 64, in1=yS[:, sl],
                                                          op0=ALU.mult, op1=ALU.add), reads=[pm_r, yS_r], writes=[yc_r])
            bd.op("act", lambda e: e.activation(out=sq[:], in_=yc[:], func=AF.Square), reads=[yc_r], writes=[sq_r])
            bd.op("pe", lambda e: e.matmul(pv[:, :], lhsT=bones, rhs=sq[:], start=True, stop=True),
                  reads=[mats_r, sq_r], writes=[pv_r])
            bd.op("act", lambda e: e.activation(out=rs[:], in_=pv[:, :], func=AF.Sqrt, scale=1.0 / 64,
                                                bias=cst[:, C_LNEPS:C_LNEPS + 1]), reads=[pv_r, cst_r], writes=[rs_r])
            bd.op("dve", lambda e: e.reciprocal(out=rs[:], in_=rs[:]), reads=[rs_r], writes=[rs_r])
            bd.op("dve", lambda e: e.tensor_tensor(out=yc[:], in0=yc[:], in1=rs[:], op=ALU.mult),
                  reads=[yc_r, rs_r], writes=[yc_r])
            bd.op("dve", lambda e: e.tensor_scalar(out=yc[:], in0=yc[:], scalar1=cst[:, C_LNW:C_LNW + 1],
                                                   scalar2=cst[:, C_LNB:C_LNB + 1], op0=ALU.mult, op1=ALU.add),
                  reads=[yc_r, cst_r], writes=[yc_r])
            bd.op("dve", lambda e: e.tensor_tensor(out=yc[:], in0=yc[:], in1=boS[:, sl], op=ALU.add),
                  reads=[yc_r, boS_r], writes=[yc_r])
            o, o_r = yo[tb % 2], yo_r[tb % 2]
            bd.op("dve", lambda e, o=o: e.tensor_tensor(out=o[:], in0=yc[:], in1=gS[:, sl], op=ALU.mult),
                  reads=[yc_r, gS_r], writes=[o_r])
            bd.dma("sp", yT[256:384, sl], o[:], reads=[o_r])
        bd.barrier()


RWKV_CHUNKED = True
import os as _os
_DBG_STAGE = int(_os.environ.get('RW_DBG', '0'))


def rwkv_masks():
    s_ = (np.arange(128) % 64)[:, None]
    t_ = (np.arange(512) % 64)[None, :]
    m = np.zeros((128, 5, 512), np.float32)
    m[:, 0, :] = (s_ < t_)
    m[:, 1, :] = (s_ <= t_)
    m[:, 2, :] = (s_ > t_)
    m[:, 3, :] = (s_ == t_)
    m[:, 4, :] = np.broadcast_to(t_ != 0, (128, 512))
    return m


def emit_rwkv_chunked(bd, nc, es, D_, cst, cst_r, K):
    sb = lambda nm, shape, dt: es.enter_context(nc.sbuf_tensor(_u(nm), shape, dt))
    pT = D_["pT"].ap()
    yT = D_["yT"].ap()
    banks, banks_r = K["banks"], K["banks_r"]
    bones, identf, mats_r = K["bones"], K["identf"], K["mats_r"]
    R_R, R_K, R_V, R_L = 640, 768, 896, 1024
    rS = sb("r_rS", [128, SEQ], F32)
    kS = sb("r_kS", [128, SEQ], F32)
    vS = sb("r_vS", [128, SEQ], F32)
    lS = sb("r_lS", [128, SEQ], F32)
    rS_r, kS_r, vS_r, lS_r = Res(), Res(), Res(), Res()
    wl = sb("r_wl", [128, 3, 128], F32)
    wl_r = Res()
    bd.dma("sp", wl[:], D_["wl"].ap(), writes=[wl_r])
    msk = sb("r_msk", [128, 5, 512], F32)
    msk_r = Res()
    bd.dma("act", msk[:], D_["rwm"].ap(), writes=[msk_r])
    c2 = sb("r_c2", [128, 2], F32)
    c2_r = Res()
    bd.op("dve", lambda e: e.tensor_scalar(out=c2[:, 0:1], in0=cst[:, C_KA:C_KA + 1], scalar1=-1.0, scalar2=1.0,
                                           op0=ALU.mult, op1=ALU.add), reads=[cst_r], writes=[c2_r])
    with ExitStack() as es2:
        sb2 = lambda nm, shape, dt: es2.enter_context(nc.sbuf_tensor(_u(nm), shape, dt))
        dd = sb2("r_dd", [128, SEQ], F32)
        dd_r = Res()
        for (row0, t, t_r, mu) in ((R_R, rS, rS_r, C_MU_R), (R_K, kS, kS_r, C_MU_K), (R_V, vS, vS_r, C_MU_V),
                                   (R_L, lS, lS_r, C_MU_L)):
            bd.dma("sp", t[:], pT[row0:row0 + 128, :], writes=[t_r])
            bd.op("dve", lambda e, t=t: e.tensor_tensor(out=dd[:, 1:SEQ], in0=t[:, 0:SEQ - 1], in1=t[:, 1:SEQ],
                                                        op=ALU.subtract), reads=[t_r], writes=[dd_r])
            bd.op("dve", lambda e, t=t: e.tensor_scalar(out=dd[:, 0:1], in0=t[:, 0:1], scalar1=-1.0, scalar2=None,
                                                        op0=ALU.mult), reads=[t_r], writes=[dd_r])
            bd.op("dve", lambda e, t=t, mu=mu: e.scalar_tensor_tensor(out=t[:], in0=dd[:], scalar=cst[:, mu:mu + 1],
                                                                      in1=t[:], op0=ALU.mult, op1=ALU.add),
                  reads=[dd_r, t_r, cst_r], writes=[t_r])
        bd.barrier()
    names = ["th", "sg", "sgm", "aT", "kkr", "sq", "nrm", "nkk", "bb", "t1", "km", "prod", "gB", "boB",
             "lw", "cl", "e1", "e2", "e3", "At", "Bt", "Kt", "Rt", "Bh", "Kh",
             "Mab", "Lab", "Mkb", "Nbr", "Nkr", "T", "TT", "Mk0", "Mk1", "Lk0", "Lk1",
             "VT", "BhT", "KhT", "yB", "yc", "sq2", "rs"]
    T_ = {n: sb("r_" + n, [128, 512], F32) for n in names}
    T_r = {n: Res() for n in names}
    UT = sb("r_UT", [128, 4, 128], F32)
    UT_r = Res()
    xts = sb("r_xts", [128, 128], F32)
    xts_r = Res()
    ST = sb("r_ST", [128, 64], F32)
    ST_r = Res()
    yo = [sb(f"r_yo{i}", [128, 512], BF16) for i in range(2)]
    yo_r = [Res(), Res()]
    bd.op("dve", lambda e: e.memset(ST[:], 0.0), writes=[ST_r])
    bX, bX_r = banks[3], banks_r[3]
    bU, bU_r = banks[4], banks_r[4]
    bS, bS_r = banks[5], banks_r[5]
    bY, bY_r = banks[6], banks_r[6]
    bi = [0]

    def nb():
        b = bi[0] % 3
        bi[0] += 1
        return banks[b], banks_r[b]

    def A(fn, reads, writes):
        bd.op("act", fn, reads=reads, writes=writes)

    def V(fn, reads, writes):
        bd.op("dve", fn, reads=reads, writes=writes)

    def G(fn, reads, writes):
        bd.op("pool", fn, reads=reads, writes=writes)

    def slot(c, h):
        return ((c // 2) * 2 + h) * 64

    def fam(out_name, lname, rname, mask_k):
        pb_, pb_r = nb()
        for c in range(8):
            P0 = (c % 2) * 64
            for h in range(2):
                H0 = h * 64
                o = slot(c, h)
                bd.op("pe", lambda e, P0=P0, H0=H0, o=o, c=c: e.matmul(
                    pb_[P0:P0 + 64, o:o + 64], lhsT=T_[lname][H0:H0 + 64, c * 64:(c + 1) * 64],
                    rhs=T_[rname][H0:H0 + 64, c * 64:(c + 1) * 64], start=True, stop=True),
                    reads=[T_r[lname], T_r[rname]], writes=[pb_r], ser=True)
        V(lambda e: e.tensor_tensor(out=T_[out_name][:], in0=pb_[:, :], in1=msk[:, mask_k, :], op=ALU.mult),
          [pb_r, msk_r], [T_r[out_name]])

    def sq16(out_name, lname, rname, add_name=None):
        pb_, pb_r = nb()
        for c in range(8):
            P0 = (c % 2) * 64
            for h in range(2):
                o = slot(c, h)
                bd.op("pe", lambda e, P0=P0, o=o: e.matmul(
                    pb_[P0:P0 + 64, o:o + 64], lhsT=T_[lname][P0:P0 + 64, o:o + 64],
                    rhs=T_[rname][P0:P0 + 64, o:o + 64], start=True, stop=True),
                    reads=[T_r[lname], T_r[rname]], writes=[pb_r], ser=True)
        if add_name is None:
            A(lambda e: e.activation(out=T_[out_name][:], in_=pb_[:, :], func=AF.Copy), [pb_r], [T_r[out_name]])
        else:
            V(lambda e: e.tensor_tensor(out=T_[out_name][:], in0=pb_[:, :], in1=T_[add_name][:], op=ALU.add),
              [pb_r, T_r[add_name]], [T_r[out_name]])

    def tr4(out_name, src_ap_fn, src_res):
        pb_, pb_r = nb()
        for c2_ in range(4):
            bd.op("pe", lambda e, c2_=c2_: e.transpose(out=pb_[:, c2_ * 128:(c2_ + 1) * 128], in_=src_ap_fn(c2_),
                                                       identity=identf), reads=[src_res, mats_r], writes=[pb_r], ser=True)
        A(lambda e: e.activation(out=T_[out_name][:], in_=pb_[:, :], func=AF.Copy), [pb_r], [T_r[out_name]])

    for tb in range(SEQ // 512):
        sl = slice(tb * 512, (tb + 1) * 512)
        A(lambda e: e.activation(out=T_["th"][:], in_=lS[:, sl], func=AF.Tanh), [lS_r], [T_r["th"]])
        A(lambda e: e.activation(out=T_["sg"][:], in_=lS[:, sl], func=AF.Sigmoid), [lS_r], [T_r["sg"]])
        pw, pw_r = nb()
        bd.op("pe", lambda e: e.matmul(pw[:, :], lhsT=wl[:, 0, :], rhs=T_["th"][:], start=True, stop=True),
              reads=[wl_r, T_r["th"]], writes=[pw_r])
        A(lambda e: e.activation(out=T_["sgm"][:], in_=pw[:, :], func=AF.Sigmoid, bias=cst[:, C_W0:C_W0 + 1]),
          [pw_r, cst_r], [T_r["sgm"]])
        V(lambda e: e.tensor_scalar(out=T_["lw"][:], in0=T_["sgm"][:], scalar1=-float(np.exp(-0.5)), scalar2=None,
                                    op0=ALU.mult), [T_r["sgm"]], [T_r["lw"]])
        pa, pa_r = nb()
        bd.op("pe", lambda e: e.matmul(pa[:, :], lhsT=wl[:, 1, :], rhs=lS[:, sl], start=True, stop=True),
              reads=[wl_r, lS_r], writes=[pa_r])
        A(lambda e: e.activation(out=T_["aT"][:], in_=pa[:, :], func=AF.Sigmoid, bias=cst[:, C_A0:C_A0 + 1]),
          [pa_r, cst_r], [T_r["aT"]])
        pg, pg_r = nb()
        bd.op("pe", lambda e: e.matmul(pg[:, :], lhsT=wl[:, 2, :], rhs=T_["sg"][:], start=True, stop=True),
              reads=[wl_r, T_r["sg"]], writes=[pg_r])
        A(lambda e: e.activation(out=T_["gB"][:], in_=pg[:, :], func=AF.Copy), [pg_r], [T_r["gB"]])
        V(lambda e: e.tensor_scalar(out=T_["kkr"][:], in0=kS[:, sl], scalar1=cst[:, C_KK:C_KK + 1], scalar2=None,
                                    op0=ALU.mult), [kS_r, cst_r], [T_r["kkr"]])
        A(lambda e: e.activation(out=T_["sq"][:], in_=T_["kkr"][:], func=AF.Square), [T_r["kkr"]], [T_r["sq"]])
        pn, pn_r = nb()
        bd.op("pe", lambda e: e.matmul(pn[:, :], lhsT=bones, rhs=T_["sq"][:], start=True, stop=True),
              reads=[mats_r, T_r["sq"]], writes=[pn_r])
        A(lambda e: e.activation(out=T_["nrm"][:], in_=pn[:, :], func=AF.Sqrt), [pn_r], [T_r["nrm"]])
        V(lambda e: e.tensor_scalar(out=T_["nrm"][:], in0=T_["nrm"][:], scalar1=1e-12, scalar2=None, op0=ALU.max),
          [T_r["nrm"]], [T_r["nrm"]])
        V(lambda e: e.reciprocal(out=T_["nrm"][:], in_=T_["nrm"][:]), [T_r["nrm"]], [T_r["nrm"]])
        V(lambda e: e.scalar_tensor_tensor(out=T_["nkk"][:], in0=T_["kkr"][:], scalar=-1.0, in1=T_["nrm"][:],
                                           op0=ALU.mult, op1=ALU.mult), [T_r["kkr"], T_r["nrm"]], [T_r["nkk"]])
        V(lambda e: e.scalar_tensor_tensor(out=T_["bb"][:], in0=T_["nkk"][:], scalar=-1.0, in1=T_["aT"][:],
                                           op0=ALU.mult, op1=ALU.mult), [T_r["nkk"], T_r["aT"]], [T_r["bb"]])
        V(lambda e: e.tensor_scalar(out=T_["t1"][:], in0=T_["aT"][:], scalar1=cst[:, C_KA:C_KA + 1],
                                    scalar2=c2[:, 0:1], op0=ALU.mult, op1=ALU.add),
          [T_r["aT"], cst_r, c2_r], [T_r["t1"]])
        V(lambda e: e.tensor_tensor(out=T_["km"][:], in0=kS[:, sl], in1=T_["t1"][:], op=ALU.mult),
          [kS_r, T_r["t1"]], [T_r["km"]])
        V(lambda e: e.scalar_tensor_tensor(out=T_["prod"][:], in0=rS[:, sl], scalar=cst[:, C_RK:C_RK + 1],
                                           in1=T_["km"][:], op0=ALU.mult, op1=ALU.mult),
          [rS_r, cst_r, T_r["km"]], [T_r["prod"]])
        pb2, pb2_r = nb()
        bd.op("pe", lambda e: e.matmul(pb2[:, :], lhsT=bones, rhs=T_["prod"][:], start=True, stop=True),
              reads=[mats_r, T_r["prod"]], writes=[pb2_r])
        V(lambda e: e.tensor_tensor(out=T_["boB"][:], in0=pb2[:, :], in1=vS[:, sl], op=ALU.mult),
          [pb2_r, vS_r], [T_r["boB"]])
        V(lambda e: e.tensor_tensor_scan(out=T_["cl"][:], data0=msk[:, 4, :], data1=T_["lw"][:], initial=0.0,
                                         op0=ALU.mult, op1=ALU.add), [msk_r, T_r["lw"]], [T_r["cl"]])
        A(lambda e: e.activation(out=T_["e1"][:], in_=T_["cl"][:], func=AF.Exp), [T_r["cl"]], [T_r["e1"]])
        A(lambda e: e.activation(out=T_["e2"][:], in_=T_["cl"][:], func=AF.Exp, scale=-1.0), [T_r["cl"]], [T_r["e2"]])
        V(lambda e: e.tensor_tensor(out=T_["e3"][:], in0=T_["cl"][:], in1=T_["lw"][:], op=ALU.subtract),
          [T_r["cl"], T_r["lw"]], [T_r["e3"]])
        A(lambda e: e.activation(out=T_["e3"][:], in_=T_["e3"][:], func=AF.Exp), [T_r["e3"]], [T_r["e3"]])
        V(lambda e: e.tensor_tensor(out=T_["At"][:], in0=T_["nkk"][:], in1=T_["e3"][:], op=ALU.mult),
          [T_r["nkk"], T_r["e3"]], [T_r["At"]])
        G(lambda e: e.tensor_tensor(out=T_["Bt"][:], in0=T_["bb"][:], in1=T_["e2"][:], op=ALU.mult),
          [T_r["bb"], T_r["e2"]], [T_r["Bt"]])
        V(lambda e: e.tensor_tensor(out=T_["Kt"][:], in0=T_["km"][:], in1=T_["e2"][:], op=ALU.mult),
          [T_r["km"], T_r["e2"]], [T_r["Kt"]])
        G(lambda e: e.tensor_tensor(out=T_["Rt"][:], in0=rS[:, sl], in1=T_["e1"][:], op=ALU.mult),
          [rS_r, T_r["e1"]], [T_r["Rt"]])
        for c in range(8):
            gam = T_["e1"][:, c * 64 + 63:c * 64 + 64]
            V(lambda e, c=c, gam=gam: e.tensor_scalar(out=T_["Bh"][:, c * 64:(c + 1) * 64], in0=T_["Bt"][:, c * 64:(c + 1) * 64],
                                                      scalar1=gam, scalar2=None, op0=ALU.mult),
              [T_r["Bt"], T_r["e1"]], [T_r["Bh"]])
            G(lambda e, c=c, gam=gam: e.tensor_scalar(out=T_["Kh"][:, c * 64:(c + 1) * 64], in0=T_["Kt"][:, c * 64:(c + 1) * 64],
                                                      scalar1=gam, scalar2=None, op0=ALU.mult),
              [T_r["Kt"], T_r["e1"]], [T_r["Kh"]])
        if _DBG_STAGE == 1:
            break
        fam("Mab", "Bt", "At", 0)
        fam("Lab", "At", "Bt", 2)
        fam("Mkb", "Kt", "At", 0)
        fam("Nbr", "Bt", "Rt", 1)
        fam("Nkr", "Kt", "Rt", 1)
        if _DBG_STAGE == 2:
            break
        V(lambda e: e.tensor_tensor(out=T_["T"][:], in0=T_["Mab"][:], in1=msk[:, 3, :], op=ALU.add),
          [T_r["Mab"], msk_r], [T_r["T"]])
        G(lambda e: e.tensor_tensor(out=T_["TT"][:], in0=T_["Lab"][:], in1=msk[:, 3, :], op=ALU.add),
          [T_r["Lab"], msk_r], [T_r["TT"]])
        Mp, Lp = "Mab", "Lab"
        for lev in range(1, 6):
            Mn, Ln = f"Mk{lev % 2}", f"Lk{lev % 2}"
            sq16(Mn, Lp, Mp)
            if lev < 5:
                sq16(Ln, Mp, Lp)
            sq16("T", "TT", Mn, add_name="T")
            if lev < 5:
                sq16("TT", Mn, "TT", add_name="TT")
            Mp, Lp = Mn, Ln
        if _DBG_STAGE == 3:
            break
        tr4("VT", lambda c2_: vS[:, tb * 512 + c2_ * 128: tb * 512 + (c2_ + 1) * 128], vS_r)
        tr4("BhT", lambda c2_: T_["Bh"][:, c2_ * 128:(c2_ + 1) * 128], T_r["Bh"])
        tr4("KhT", lambda c2_: T_["Kh"][:, c2_ * 128:(c2_ + 1) * 128], T_r["Kh"])
        if _DBG_STAGE == 4:
            break
        for c in range(8):
            P0 = (c % 2) * 64
            c2_ = c // 2
            cs = slice(c * 64, (c + 1) * 64)
            for h in range(2):
                H0 = h * 64
                o = slot(c, h)
                vt = T_["VT"][P0:P0 + 64, c2_ * 128 + H0:c2_ * 128 + H0 + 64]
                bd.op("pe", lambda e, P0=P0, H0=H0, cs=cs: e.matmul(
                    bX[P0:P0 + 64, H0:H0 + 64], lhsT=T_["At"][H0:H0 + 64, cs], rhs=ST[H0:H0 + 64, :],
                    start=True, stop=False), reads=[T_r["At"], ST_r], writes=[bX_r], ser=True)
                bd.op("pe", lambda e, P0=P0, H0=H0, o=o, vt=vt: e.matmul(
                    bX[P0:P0 + 64, H0:H0 + 64], lhsT=T_["Mkb"][P0:P0 + 64, o:o + 64], rhs=vt,
                    start=False, stop=True), reads=[T_r["Mkb"], T_r["VT"]], writes=[bX_r], ser=True)
            A(lambda e, P0=P0: e.activation(out=xts[P0:P0 + 64, :], in_=bX[P0:P0 + 64, 0:128], func=AF.Copy),
              [bX_r], [xts_r])
            for h in range(2):
                H0 = h * 64
                o = slot(c, h)
                bd.op("pe", lambda e, P0=P0, H0=H0, o=o: e.matmul(
                    bU[P0:P0 + 64, H0:H0 + 64], lhsT=T_["T"][P0:P0 + 64, o:o + 64], rhs=xts[P0:P0 + 64, H0:H0 + 64],
                    start=True, stop=True), reads=[T_r["T"], xts_r], writes=[bU_r], ser=True)
            V(lambda e, P0=P0, c2_=c2_: e.tensor_copy(out=UT[P0:P0 + 64, c2_, :], in_=bU[P0:P0 + 64, 0:128]),
              [bU_r], [UT_r])
            for h in range(2):
                H0 = h * 64
                o = slot(c, h)
                ut = UT[P0:P0 + 64, c2_, H0:H0 + 64]
                vt = T_["VT"][P0:P0 + 64, c2_ * 128 + H0:c2_ * 128 + H0 + 64]
                bd.op("pe", lambda e, H0=H0, cs=cs: e.matmul(
                    bY[H0:H0 + 64, cs], lhsT=ST[H0:H0 + 64, :], rhs=T_["Rt"][H0:H0 + 64, cs],
                    start=True, stop=False), reads=[ST_r, T_r["Rt"]], writes=[bY_r], ser=True)
                bd.op("pe", lambda e, H0=H0, cs=cs, P0=P0, o=o, ut=ut: e.matmul(
                    bY[H0:H0 + 64, cs], lhsT=ut, rhs=T_["Nbr"][P0:P0 + 64, o:o + 64],
                    start=False, stop=False), reads=[UT_r, T_r["Nbr"]], writes=[bY_r], ser=True)
                bd.op("pe", lambda e, H0=H0, cs=cs, P0=P0, o=o, vt=vt: e.matmul(
                    bY[H0:H0 + 64, cs], lhsT=vt, rhs=T_["Nkr"][P0:P0 + 64, o:o + 64],
                    start=False, stop=True), reads=[T_r["VT"], T_r["Nkr"]], writes=[bY_r], ser=True)
                bd.op("pe", lambda e, H0=H0, P0=P0, c2_=c2_, ut=ut: e.matmul(
                    bS[H0:H0 + 64, 0:64], lhsT=T_["BhT"][P0:P0 + 64, c2_ * 128 + H0:c2_ * 128 + H0 + 64], rhs=ut,
                    start=True, stop=False), reads=[T_r["BhT"], UT_r], writes=[bS_r], ser=True)
                bd.op("pe", lambda e, H0=H0, P0=P0, c2_=c2_, vt=vt: e.matmul(
                    bS[H0:H0 + 64, 0:64], lhsT=T_["KhT"][P0:P0 + 64, c2_ * 128 + H0:c2_ * 128 + H0 + 64], rhs=vt,
                    start=False, stop=True), reads=[T_r["KhT"], T_r["VT"]], writes=[bS_r], ser=True)
            gam = T_["e1"][:, c * 64 + 63:c * 64 + 64]
            V(lambda e, gam=gam: e.scalar_tensor_tensor(out=ST[:], in0=ST[:], scalar=gam, in1=bS[:, 0:64],
                                                        op0=ALU.mult, op1=ALU.add),
              [ST_r, bS_r, T_r["e1"]], [ST_r])
        if _DBG_STAGE == 5:
            break
        A(lambda e: e.activation(out=T_["yB"][:], in_=bY[:, :], func=AF.Copy), [bY_r], [T_r["yB"]])
        pm, pm_r = nb()
        bd.op("pe", lambda e: e.matmul(pm[:, :], lhsT=bones, rhs=T_["yB"][:], start=True, stop=True),
              reads=[mats_r, T_r["yB"]], writes=[pm_r])
        V(lambda e: e.scalar_tensor_tensor(out=T_["yc"][:], in0=pm[:, :], scalar=-1.0 / 64, in1=T_["yB"][:],
                                           op0=ALU.mult, op1=ALU.add), [pm_r, T_r["yB"]], [T_r["yc"]])
        A(lambda e: e.activation(out=T_["sq2"][:], in_=T_["yc"][:], func=AF.Square), [T_r["yc"]], [T_r["sq2"]])
        pv, pv_r = nb()
        bd.op("pe", lambda e: e.matmul(pv[:, :], lhsT=bones, rhs=T_["sq2"][:], start=True, stop=True),
              reads=[mats_r, T_r["sq2"]], writes=[pv_r])
        A(lambda e: e.activation(out=T_["rs"][:], in_=pv[:, :], func=AF.Sqrt, scale=1.0 / 64,
                                 bias=cst[:, C_LNEPS:C_LNEPS + 1]), [pv_r, cst_r], [T_r["rs"]])
        V(lambda e: e.reciprocal(out=T_["rs"][:], in_=T_["rs"][:]), [T_r["rs"]], [T_r["rs"]])
        V(lambda e: e.tensor_tensor(out=T_["yc"][:], in0=T_["yc"][:], in1=T_["rs"][:], op=ALU.mult),
          [T_r["yc"], T_r["rs"]], [T_r["yc"]])
        V(lambda e: e.tensor_scalar(out=T_["yc"][:], in0=T_["yc"][:], scalar1=cst[:, C_LNW:C_LNW + 1],
                                    scalar2=cst[:, C_LNB:C_LNB + 1], op0=ALU.mult, op1=ALU.add),
          [T_r["yc"], cst_r], [T_r["yc"]])
        V(lambda e: e.tensor_tensor(out=T_["yc"][:], in0=T_["yc"][:], in1=T_["boB"][:], op=ALU.add),
          [T_r["yc"], T_r["boB"]], [T_r["yc"]])
        o_, o_r = yo[tb % 2], yo_r[tb % 2]
        V(lambda e, o_=o_: e.tensor_tensor(out=o_[:], in0=T_["yc"][:], in1=T_["gB"][:], op=ALU.mult),
          [T_r["yc"], T_r["gB"]], [o_r])
        bd.dma("sp", yT[256:384, sl], o_[:], reads=[o_r])
    bd.barrier()


def emit_phaseB(nc, mkbld, es, D_, mixers, tag):
    bd = mkbld(tag + "m")
    sb = lambda nm, shape, dt: es.enter_context(nc.sbuf_tensor(_u(nm), shape, dt))
    ps = lambda nm, shape, dt: es.enter_context(nc.psum_tensor(_u(nm), shape, dt))
    cst = sb("B_cst", [128, NCST], F32)
    cst_r = Res()
    bd.dma("sp", cst[:], D_["cst"].ap(), writes=[cst_r])
    mats = sb("B_mats", [128, 4, 128], F32)
    mats_r = Res()
    bd.dma("sp", mats[:], D_["mats"].ap(), writes=[mats_r])
    identb = sb("B_identb", [128, 128], BF16)
    maskb = sb("B_maskb", [128, 128], BF16)
    identb_r, maskb_r = Res(), Res()
    bd.op("dve", lambda e: e.tensor_copy(out=identb[:], in_=mats[:, 0, :]), reads=[mats_r], writes=[identb_r])
    bd.op("dve", lambda e: e.tensor_copy(out=maskb[:], in_=mats[:, 3, :]), reads=[mats_r], writes=[maskb_r])
    banks = [ps(f"B_bank{i}", [128, 512], F32) for i in range(7)]
    banks_r = [Res() for _ in range(7)]
    ptb = ps("B_ptb", [128, 1024], BF16)
    ptb_r = Res()
    K = dict(banks=banks, banks_r=banks_r, ptb=ptb, ptb_r=ptb_r, identb=identb, identb_r=identb_r,
             mask=maskb, mask_r=maskb_r, ones=mats[:, 1, :], ones_r=mats_r, identf=mats[:, 0, :],
             bones=mats[:, 2, :], mats_r=mats_r)
    if "mla" in mixers:
        with ExitStack() as es1:
            emit_mla(bd, nc, es1, D_["pT"], D_["pos"], D_["wuq"], D_["wukv"], D_["yT"], cst, cst_r, K)
    bd.barrier()
    if "mlstm" in mixers:
        with ExitStack() as es1:
            emit_mlstm(bd, nc, es1, D_["pT"], D_["yT"], cst, cst_r, K)
        bd.barrier()
    if "rwkv" in mixers:
        bd = mkbld(tag + "r")
        with ExitStack() as es1:
            (emit_rwkv_chunked if RWKV_CHUNKED else emit_rwkv)(bd, nc, es1, D_, cst, cst_r, K)
        bd.barrier()


def prep_phaseB_consts(P, l, hc):
    c = np.zeros((128, NCST), np.float32)
    c[:, C_QN0] = P["mla_q_norm"][l][0:128]
    c[:, C_QN1] = P["mla_q_norm"][l][128:256]
    c[:, C_KVN0] = P["mla_kv_norm"][l][0:128]
    c[:, C_KVN1] = P["mla_kv_norm"][l][128:256]
    c[:, C_INVF] = np.tile(INV_FREQ, 4)
    c[:, C_SGN] = np.tile(np.concatenate([np.full(32, -1.0, np.float32), np.full(32, 1.0, np.float32)]), 2)
    for h in range(2):
        hh = 2 * hc + h
        c[:, C_MLAO0 + h] = P["mla_out_norm"][l][hh * 128:(hh + 1) * 128]
    mu = P["rwkv_mu"][l]
    ch = slice(hc * 128, hc * 128 + 128)
    c[:, C_MU_R] = mu[0:256][ch]
    c[:, C_MU_K] = mu[256:512][ch]
    c[:, C_MU_V] = mu[512:768][ch]
    c[:, C_MU_L] = mu[768:896]
    c[:, C_W0] = P["rwkv_w0"][l][ch]
    c[:, C_A0] = P["rwkv_a0"][l][ch]
    c[:, C_KK] = P["rwkv_k_k"][l][ch]
    c[:, C_KA] = P["rwkv_k_a"][l][ch]
    c[:, C_RK] = P["rwkv_r_k"][l][ch]
    c[:, C_LNW] = P["rwkv_ln_w"][l][ch]
    c[:, C_LNB] = P["rwkv_ln_b"][l][ch]
    cw = P["mlstm_conv_w"][l]
    cb = P["mlstm_conv_b"][l]
    qs = slice(hc * 64, hc * 64 + 64)
    ks = slice(128 + hc * 64, 128 + hc * 64 + 64)
    for j in range(4):
        c[0:64, C_CWQ0 + j] = cw[j][qs]
        c[0:64, C_CWK0 + j] = cw[j][ks]
    c[0:64, C_CBQ] = cb[qs]
    c[0:64, C_CBK] = cb[ks]
    c[:, C_IB] = P["mlstm_i_bias"][l][hc * 2]
    c[:, C_FB] = P["mlstm_f_bias"][l][hc * 2]
    c[:, C_IB1] = P["mlstm_i_bias"][l][hc * 2 + 1]
    c[:, C_FB1] = P["mlstm_f_bias"][l][hc * 2 + 1]
    c[:, C_MLO] = P["mlstm_out_norm"][l][ch]
    c[:, C_EPS] = EPS
    c[:, C_LNEPS] = 64e-5
    c[:, C_ONE] = 1.0
    hq = []
    hkv = []
    for h in range(2):
        hh = 2 * hc + h
        wq = P["mla_w_uq"][l][:, hh * 192:(hh + 1) * 192]
        hq += [wq[:, 0:128], wq[:, 128:192], wq[:, 160:192], wq[:, 128:160]]
        wkv = P["mla_w_ukv"][l][:, hh * 256:(hh + 1) * 256]
        hkv += [wkv]
    wuq = np.ascontiguousarray(np.concatenate(hq, axis=1))
    wukv = np.ascontiguousarray(np.concatenate(hkv, axis=1))
    wl = np.zeros((128, 3, 128), np.float32)
    wl[0:32, 0, :] = P["rwkv_w2"][l][:, ch]
    wl[32:64, 1, :] = P["rwkv_a2"][l][:, ch]
    wl[64:128, 2, :] = P["rwkv_g2"][l][:, ch]
    return dict(cst=c, wuq=wuq, wukv=wukv, wl=wl)


INV_FREQ = (10000.0 ** (-np.arange(0, 64, 2, dtype=np.float32) / np.float32(64))).astype(np.float32)


def const_mats():
    m = np.zeros((128, 4, 128), np.float32)
    m[:, 0, :] = np.eye(128, dtype=np.float32)
    m[:, 1, :] = 1.0
    m[0:64, 2, 0:64] = 1.0
    m[64:128, 2, 64:128] = 1.0
    m[:, 3, :] = (np.arange(128)[:, None] <= np.arange(128)[None, :]).astype(np.float32)
    return m


W_OUT_PERM = (list(range(0, 256)) + list(range(512, 640)) + list(range(768, 896)) +
              list(range(256, 512)) + list(range(640, 768)) + list(range(896, 1024)))
TBC = 256


def emit_phaseC(nc, bd, es, D_, final, ntok):
    sb = lambda name, shape, dt: es.enter_context(nc.sbuf_tensor(_u(name), shape, dt))
    ps = lambda name, shape, dt: es.enter_context(nc.psum_tensor(_u(name), shape, dt))
    wo = sb("C_wo", [128, 8, D], BF16)
    wg = sb("C_wg", [128, 8, DFF], BF16)
    wu = sb("C_wu", [128, 8, DFF], BF16)
    wd = sb("C_wd", [128, 22, D], BF16)
    w_r = Res()
    HS = DFF // 2
    stage = [sb(f"C_stage{i}", [128, HS], F32) for i in range(3)]
    stage_r = [Res() for _ in range(3)]
    gcol = sb("C_gcol", [128, 8], F32)
    gcol_r = Res()
    idf = sb("C_idf", [128, 128], F32)
    ident = sb("C_ident", [128, 128], BF16)
    idf_r, ident_r = Res(), Res()
    bd.dma("sp", gcol[:], D_["g"].ap(), writes=[gcol_r])
    bd.dma("sp", idf[:], D_["ident"].ap(), writes=[idf_r])
    bd.op("dve", lambda e: e.tensor_copy(out=ident[:], in_=idf[:]), reads=[idf_r], writes=[ident_r])
    if final:
        gbc = sb("C_gbc", [128, D], F32)
        gbc_r = Res()
        bd.dma("sp", gbc[:], bass.AP(D_["gf_t"], 0, [[0, 128], [1, D]]), writes=[gbc_r])
    si = [0]
    queues = ("sp", "pool", "act")
    engs = ("dve", "pool", "act")

    def cast(dst_ap, src_ap, ncols, scale_ap=None):
        i = si[0] % 3
        si[0] += 1
        bd.dma(queues[i], stage[i][:, 0:ncols], src_ap, writes=[stage_r[i]])
        eng = engs[i] if scale_ap is None else ("dve" if i != 2 else "act")
        if eng == "act":
            if scale_ap is None:
                bd.op("act", lambda e: e.activation(out=dst_ap, in_=stage[i][:, 0:ncols], func=AF.Copy),
                      reads=[stage_r[i]], writes=[w_r])
            else:
                bd.op("act", lambda e: e.activation(out=dst_ap, in_=stage[i][:, 0:ncols], func=AF.Copy, scale=scale_ap),
                      reads=[stage_r[i], gcol_r], writes=[w_r])
        elif scale_ap is None:
            bd.op(eng, lambda e: e.tensor_copy(out=dst_ap, in_=stage[i][:, 0:ncols]), reads=[stage_r[i]], writes=[w_r])
        else:
            bd.op(eng, lambda e: e.tensor_scalar(out=dst_ap, in0=stage[i][:, 0:ncols], scalar1=scale_ap, scalar2=None,
                                                 op0=ALU.mult), reads=[stage_r[i], gcol_r], writes=[w_r])

    for kc in range(8):
        cast(wo[:, kc, :], D_["wo"].ap()[kc * 128:(kc + 1) * 128, :], D)
    for kc in range(8):
        for hh in range(2):
            cast(wg[:, kc, hh * HS:(hh + 1) * HS], D_["wg"].ap()[kc * 128:(kc + 1) * 128, hh * HS:(hh + 1) * HS], HS,
                 gcol[:, kc:kc + 1])
            cast(wu[:, kc, hh * HS:(hh + 1) * HS], D_["wu"].ap()[kc * 128:(kc + 1) * 128, hh * HS:(hh + 1) * HS], HS,
                 gcol[:, kc:kc + 1])
    for f in range(22):
        cast(wd[:, f, :], D_["wd"].ap()[f * 128:(f + 1) * 128, :], D)

    NJ = TBC // 128
    xs = sb("C_xs", [128, NJ, D], F32)
    xs_r = [Res() for _ in range(NJ)]
    yTb = sb("C_yTb", [128, 8, TBC], BF16)
    yTb_r = Res()
    hT = sb("C_hT", [128, 8, TBC], BF16)
    hT_r = Res()
    AT = sb("C_AT", [128, 22, TBC], BF16)
    AT_r = Res()
    sgt = [sb(f"C_sg{i}", [128, TBC], F32) for i in range(2)]
    sgt_r = [Res(), Res()]
    tmp = dict(ss=sb("C_ss", [128, 4], F32), ss_r=Res(), junk=sb("C_junk", [128, D], BF16), junk_r=Res(),
               xn=sb("C_xn", [128, D], BF16), xn_r=Res(),
               pt=ps("C_pt", [128, D], BF16), pt_r=Res(), eps=sb("C_eps", [128, 1], F32), eps_r=Res())
    bd.op("dve", lambda e: e.memset(tmp["eps"][:], EPS), writes=[tmp["eps_r"]])
    pb = [ps(f"C_pb{i}", [128, 512], F32) for i in range(6)]
    pb_r = [Res() for _ in range(6)]
    x_ap = D_["x"].ap()
    yT_ap = D_["yT"].ap()
    out_ap = D_["out"].ap()
    for blk in range(ntok // TBC):
        t0 = blk * TBC
        bd.dma("sp", yTb[:], yT_ap[:, t0:t0 + TBC].rearrange("(k p) t -> p k t", p=128), writes=[yTb_r])
        for j in range(NJ):
            bd.dma("pool", xs[:, j, :], x_ap[t0 + j * 128:t0 + (j + 1) * 128, :], writes=[xs_r[j]])
            for ch in range(2):
                po, po_r = pb[ch], pb_r[ch]
                for k in range(8):
                    bd.op("pe", lambda e, k=k, j=j, ch=ch, po=po: e.matmul(
                        po[:, :], lhsT=yTb[:, k, j * 128:(j + 1) * 128], rhs=wo[:, k, ch * 512:(ch + 1) * 512],
                        start=(k == 0), stop=(k == 7)), reads=[yTb_r, w_r], writes=[po_r])
                bd.op("dve", lambda e, j=j, ch=ch, po=po: e.tensor_tensor(
                    out=xs[:, j, ch * 512:(ch + 1) * 512], in0=xs[:, j, ch * 512:(ch + 1) * 512], in1=po[:, :], op=ALU.add),
                    reads=[po_r, xs_r[j]], writes=[xs_r[j]])
            emit_rmsnorm_T(bd, xs[:, j, :], xs_r[j], hT, hT_r, j, tmp, ident)
        for f in range(22):
            pg, pg_r = pb[2 + (f % 2) * 2], pb_r[2 + (f % 2) * 2]
            pu, pu_r = pb[3 + (f % 2) * 2], pb_r[3 + (f % 2) * 2]
            for (pp, pp_r, ww) in ((pg, pg_r, wg), (pu, pu_r, wu)):
                for k in range(8):
                    bd.op("pe", lambda e, k=k, f=f, pp=pp, ww=ww: e.matmul(
                        pp[:, 0:TBC], lhsT=ww[:, k, f * 128:(f + 1) * 128], rhs=hT[:, k, :],
                        start=(k == 0), stop=(k == 7)), reads=[w_r, hT_r], writes=[pp_r])
            sg, sg_r = sgt[f % 2], sgt_r[f % 2]
            bd.op("act", lambda e, sg=sg, pg=pg: e.activation(out=sg[:], in_=pg[:, 0:TBC], func=AF.Silu),
                  reads=[pg_r], writes=[sg_r])
            bd.op("dve", lambda e, sg=sg, pu=pu, f=f: e.tensor_tensor(out=AT[:, f, :], in0=sg[:], in1=pu[:, 0:TBC],
                                                                      op=ALU.mult),
                  reads=[sg_r, pu_r], writes=[AT_r])
        for j in range(NJ):
            for ch in range(2):
                pd, pd_r = pb[ch], pb_r[ch]
                for f in range(22):
                    bd.op("pe", lambda e, f=f, j=j, ch=ch, pd=pd: e.matmul(
                        pd[:, :], lhsT=AT[:, f, j * 128:(j + 1) * 128], rhs=wd[:, f, ch * 512:(ch + 1) * 512],
                        start=(f == 0), stop=(f == 21)), reads=[AT_r, w_r], writes=[pd_r])
                bd.op("dve", lambda e, j=j, ch=ch, pd=pd: e.tensor_tensor(
                    out=xs[:, j, ch * 512:(ch + 1) * 512], in0=xs[:, j, ch * 512:(ch + 1) * 512], in1=pd[:, :], op=ALU.add),
                    reads=[pd_r, xs_r[j]], writes=[xs_r[j]])
            if final:
                ss, ss_r = tmp["ss"], tmp["ss_r"]
                bd.op("dve", lambda e, j=j: e.scalar_tensor_tensor(
                    out=tmp["junk"][:], in0=xs[:, j, :], scalar=1.0, in1=xs[:, j, :], op0=ALU.mult, op1=ALU.mult,
                    accum_out=ss[:, 0:1]), reads=[xs_r[j]], writes=[tmp["junk_r"], ss_r])
                bd.op("act", lambda e: e.activation(out=ss[:, 1:2], in_=ss[:, 0:1], func=AF.Sqrt, scale=1.0 / D,
                                                    bias=tmp["eps"][:, 0:1]), reads=[ss_r, tmp["eps_r"]], writes=[ss_r])
                bd.op("dve", lambda e: e.reciprocal(out=ss[:, 2:3], in_=ss[:, 1:2]), reads=[ss_r], writes=[ss_r])
                bd.op("dve", lambda e, j=j: e.scalar_tensor_tensor(
                    out=xs[:, j, :], in0=xs[:, j, :], scalar=ss[:, 2:3], in1=gbc[:], op0=ALU.mult, op1=ALU.mult),
                    reads=[xs_r[j], ss_r, gbc_r], writes=[xs_r[j]])
            bd.dma("sp", out_ap[t0 + j * 128:t0 + (j + 1) * 128, :], xs[:, j, :], reads=[xs_r[j]])
    bd.barrier()


def build_fused(phases=None):
    nc = bass.Bass("TRN2", target_bir_lowering=False)
    dt = nc.dram_tensor
    x_d = dt("x", [SEQ, D], F32, kind="ExternalInput")
    pos_d = dt("pos", [1, SEQ], I32, kind="ExternalInput")
    wA_d = dt("wA", [2, D, NCOLA], F32, kind="ExternalInput")
    gA_d = dt("gA", [2, 128, 8], F32, kind="ExternalInput")
    id_d = dt("ident", [128, 128], F32, kind="ExternalInput")
    mats_d = dt("mats", [128, 4, 128], F32, kind="ExternalInput")
    rwm_d = dt("rwm", [128, 5, 512], F32, kind="ExternalInput")
    wuq_d = dt("wuq", [2, 2, 256, 512], F32, kind="ExternalInput")
    wukv_d = dt("wukv", [2, 2, 256, 512], F32, kind="ExternalInput")
    cst_d = dt("cst", [2, 2, 128, NCST], F32, kind="ExternalInput")
    wl_d = dt("wl", [2, 2, 128, 3, 128], F32, kind="ExternalInput")
    wo_d = dt("wo", [2, D, D], F32, kind="ExternalInput")
    wg_d = dt("wg", [2, D, DFF], F32, kind="ExternalInput")
    wu_d = dt("wu", [2, D, DFF], F32, kind="ExternalInput")
    wd_d = dt("wd", [2, DFF, D], F32, kind="ExternalInput")
    gC_d = dt("gC", [2, 128, 8], F32, kind="ExternalInput")
    gf_d = dt("gf", [1, D], F32, kind="ExternalInput")
    out_d = dt("out", [SEQ, D], F32, kind="ExternalOutput")
    pT_d = dt("pT_scr", [NCOLA, SEQ], F32)
    yT_d = dt("yT_scr", [D, SEQ], BF16)
    scr_d = dt("rw_scr", [2 * SEQ * 320], F32)
    x1_d = dt("x1_scr", [SEQ, D], F32)
    shared = {}
    mkbld = lambda tag: Bld(nc, tag, shared)
    for l in range(2):
        x_in = x_d if l == 0 else x1_d
        x_out = x1_d if l == 0 else out_d
        if phases is None or f"A{l}" in phases:
          with ExitStack() as es:
            bd = mkbld(f"A{l}")
            emit_phaseA(nc, bd, es, x_in, View(wA_d.ap()[l]), View(gA_d.ap()[l]), id_d, pT_d, SEQ)
        for hc in range(2):
            if phases is not None and f"B{l}{hc}" not in phases:
                continue
            D_ = dict(pT=View(pT_d.ap()[hc * HALF_ROWS:(hc + 1) * HALF_ROWS, :]), pos=pos_d,
                      wuq=View(wuq_d.ap()[l, hc]), wukv=View(wukv_d.ap()[l, hc]), cst=View(cst_d.ap()[l, hc]),
                      wl=View(wl_d.ap()[l, hc]), mats=mats_d, yT=View(yT_d.ap()[hc * 512:(hc + 1) * 512, :]),
                      scr=scr_d, rwm=rwm_d)
            with ExitStack() as es:
                emit_phaseB(nc, mkbld, es, D_, ("mla", "mlstm", "rwkv"), f"B{l}{hc}")
        D_ = dict(x=x_in, yT=yT_d, wo=View(wo_d.ap()[l]), wg=View(wg_d.ap()[l]), wu=View(wu_d.ap()[l]),
                  wd=View(wd_d.ap()[l]), g=View(gC_d.ap()[l]), gf_t=gf_d, ident=id_d, out=x_out)
        if phases is None or f"C{l}" in phases:
          with ExitStack() as es:
            bd = mkbld(f"C{l}")
            emit_phaseC(nc, bd, es, D_, l == 1, SEQ)
    return nc


_CACHE = {}
_PHASES = None


def kernel(**inputs):
    P = {k: np.asarray(v) for k, v in inputs.items()}
    x = np.asarray(P["x"], np.float32)
    positions = np.asarray(P["positions"]).astype(np.int32)
    if "nc" not in _CACHE:
        _CACHE["nc"] = build_fused(_PHASES)
    nc = _CACHE["nc"]
    f32 = lambda a: np.ascontiguousarray(np.asarray(a, np.float32))
    wA = f32(np.stack([prep_w_inA(P["w_in"][l]) for l in range(2)]))
    gA = f32(np.stack([_gcol(P["mix_norm"][l]) for l in range(2)]))
    gC = f32(np.stack([_gcol(P["ffn_norm"][l]) for l in range(2)]))
    pb = [[prep_phaseB_consts(P, l, hc) for hc in range(2)] for l in range(2)]
    stk = lambda key: f32(np.stack([np.stack([pb[l][hc][key] for hc in range(2)]) for l in range(2)]))
    common = dict(
        wA=wA, gA=gA, ident=np.eye(128, dtype=np.float32), mats=const_mats(), rwm=rwkv_masks(),
        wuq=stk("wuq"), wukv=stk("wukv"), cst=stk("cst"), wl=stk("wl"),
        wo=f32(np.stack([P["w_out"][l][W_OUT_PERM, :] for l in range(2)])),
        wg=f32(P["w_gate"]), wu=f32(P["w_up"]), wd=f32(P["w_down"]), gC=gC,
        gf=f32(np.asarray(P["final_norm"]).reshape(1, D)))
    in_maps = []
    for c in range(NCORES):
        b = c // 2
        m = dict(common)
        m["x"] = f32(x[b])
        m["pos"] = np.ascontiguousarray(positions[b:b + 1])
        in_maps.append(m)
    res = run_bass_kernel_spmd(nc, in_maps, core_ids=list(range(NCORES)))
    out = np.zeros((4, SEQ, D), np.float32)
    for b in range(4):
        out[b] = res.results[2 * b]["out"]
    return out
```

```python
from contextlib import ExitStack
import numpy as np
import concourse.bass as bass
import concourse.mybir as mybir
from concourse.bass_utils import run_bass_kernel_spmd

F32 = mybir.dt.float32
BF16 = mybir.dt.bfloat16
I32 = mybir.dt.int32
ALU = mybir.AluOpType
AF = mybir.ActivationFunctionType

D = 1024
SEQ = 4096
NTOK = 2048
DFF = 2816
NCORES = 8
EPS = 1e-6

NCH = 13
CH_ROWS = [128] * 12 + [4]
HALF_ROWS = 12 * 128 + 4
CH_OFF = [i * 128 for i in range(13)]
NCOLA = 2 * HALF_ROWS


def half_cols(hc):
    cols = []
    cols += list(range(0, 256))
    cols += list(range(256, 512))
    cols += list(range(512, 576))
    cols += list(range(544, 576)) + list(range(512, 544))
    R0 = 576
    for part in range(3):
        cols += list(range(R0 + part * 256 + hc * 128, R0 + part * 256 + hc * 128 + 128))
    cols += list(range(R0 + 768, R0 + 896))
    M0 = 576 + 896
    cols += list(range(M0 + hc * 64, M0 + hc * 64 + 64))
    cols += list(range(M0 + 128 + hc * 64, M0 + 128 + hc * 64 + 64))
    cols += list(range(M0 + 256 + hc * 128, M0 + 256 + hc * 128 + 128))
    cols += list(range(M0 + 520 + hc * 128, M0 + 520 + hc * 128 + 128))
    cols += list(range(M0 + 512 + hc * 2, M0 + 512 + hc * 2 + 2))
    cols += list(range(M0 + 516 + hc * 2, M0 + 516 + hc * 2 + 2))
    assert len(cols) == HALF_ROWS
    return cols


class Res:
    __slots__ = ("w", "r", "name")

    def __init__(self, name=""):
        self.w = None
        self.r = {}
        self.name = name


class Bld:
    NDMA = 8

    def __init__(self, nc, tag="", shared=None):
        self.nc = nc
        self.E = {"pe": nc.tensor, "dve": nc.vector, "act": nc.scalar, "pool": nc.gpsimd, "sp": nc.sync}
        self.sems = {}
        self.cnt = {}
        self.seen = {e: {} for e in self.E}
        self.touched = set()
        for e in self.E:
            self.sems[e] = nc.alloc_semaphore(name=f"s{tag}_{e}")
            self.cnt[e] = 0
        if shared is not None and "sems" in shared:
            self.sems.update(shared["sems"])
            self.cnt.update(shared["cnt"])
            self.dslot = shared["dslot"]
        else:
            dsems, dcnt = {}, {}
            self.dslot = {}
            for q in ("sp", "act", "pool"):
                for i in range(self.NDMA):
                    k = f"d{q}{i}"
                    dsems[k] = nc.alloc_semaphore(name=f"sdma_{k}")
                    dcnt[k] = 0
                self.dslot[q] = 0
            self.sems.update(dsems)
            self.cnt.update(dcnt)
            if shared is not None:
                shared["sems"] = dsems
                shared["dslot"] = self.dslot
                shared["cnt"] = {}
        self.shared = shared

    def _sync_shared(self):
        if self.shared is not None:
            for k in self.shared["sems"]:
                self.shared["cnt"][k] = self.cnt[k]

    def _wait(self, eng, deps):
        best = {}
        for k, v in deps:
            if v > best.get(k, 0):
                best[k] = v
        for k, v in best.items():
            if self.seen[eng].get(k, 0) >= v:
                continue
            self.E[eng].wait_ge(self.sems[k], v)
            self.seen[eng][k] = v

    def _deps(self, eng, reads, writes):
        deps = []
        for r in reads:
            if r.w is not None:
                if not (eng == "pe" and r.w[0] == "pe"):
                    deps.append(r.w)
        for w in writes:
            if w.w is not None and (w.w[0] != eng or eng != "pe"):
                deps.append(w.w)
            for k, v in w.r.items():
                if k != eng or eng != "pe":
                    deps.append((k, v))
        return deps

    def _mark(self, ev, reads, writes):
        self.touched.update(reads)
        self.touched.update(writes)
        for r in reads:
            if ev[1] > r.r.get(ev[0], 0):
                r.r[ev[0]] = ev[1]
        for w in writes:
            w.w = ev
            w.r = {}

    def op(self, eng, fn, reads=(), writes=(), ser=False):
        self._wait(eng, self._deps(eng, reads, writes))
        ins = fn(self.E[eng])
        self.cnt[eng] += 1
        ins.then_inc(self.sems[eng], 1)
        self._mark((eng, self.cnt[eng]), reads, writes)
        if ser:
            self._wait(eng, [(eng, self.cnt[eng])])
        return ins

    def dma(self, q, out, in_, reads=(), writes=()):
        i = self.dslot[q]
        self.dslot[q] = (i + 1) % self.NDMA
        k = f"d{q}{i}"
        deps = self._deps(k, reads, writes)
        deps.append((k, self.cnt[k]))
        self._wait(q, deps)
        ins = self.E[q].dma_start(out=out, in_=in_)
        self.cnt[k] += 16
        ins.then_inc(self.sems[k], 16)
        self._mark((k, self.cnt[k]), reads, writes)
        return ins

    def barrier(self):
        deps = [(k, v) for k, v in self.cnt.items() if v > 0]
        for e in ("sp", "pe", "dve", "act", "pool"):
            self._wait(e, deps)
        for e in ("sp", "pe", "dve", "act", "pool"):
            for k, v in deps:
                assert self.seen[e].get(k, 0) >= v
        for r in self.touched:
            r.w = None
            r.r = {}
        self.touched = set()
        self._sync_shared()

    def wait_all(self, eng, ress):
        deps = []
        for r in ress:
            if r.w is not None:
                deps.append(r.w)
        self._wait(eng, deps)


def dram_ap(t, offset, pattern):
    return bass.AP(t, offset, [list(p) for p in pattern])


def emit_rmsnorm_T(bd, x_tile, x_res, hT, hT_res, j, tmp, ident):
    ss, ss_r = tmp["ss"], tmp["ss_r"]
    junk, junk_r = tmp["junk"], tmp["junk_r"]
    xn, xn_r = tmp["xn"], tmp["xn_r"]
    pt, pt_r = tmp["pt"], tmp["pt_r"]
    bd.op("dve", lambda e: e.scalar_tensor_tensor(out=junk[:], in0=x_tile, scalar=1.0, in1=x_tile,
                                                  op0=ALU.mult, op1=ALU.mult, accum_out=ss[:, 0:1]),
          reads=[x_res], writes=[junk_r, ss_r])
    bd.op("act", lambda e: e.activation(out=ss[:, 1:2], in_=ss[:, 0:1], func=AF.Sqrt, scale=1.0 / D,
                                        bias=tmp["eps"][:, 0:1]), reads=[ss_r, tmp["eps_r"]], writes=[ss_r])
    bd.op("dve", lambda e: e.reciprocal(out=ss[:, 2:3], in_=ss[:, 1:2]), reads=[ss_r], writes=[ss_r])
    bd.op("act", lambda e: e.activation(out=xn[:], in_=x_tile, func=AF.Copy, scale=ss[:, 2:3]),
          reads=[x_res, ss_r], writes=[xn_r])
    for kc in range(8):
        bd.op("pe", lambda e, kc=kc: e.transpose(out=pt[:, kc * 128:(kc + 1) * 128],
                                                 in_=xn[:, kc * 128:(kc + 1) * 128], identity=ident[:]),
              reads=[xn_r], writes=[pt_r])
    bd.op("act", lambda e: e.activation(out=hT[:, :, j * 128:(j + 1) * 128],
                                        in_=pt[:].rearrange("p (k t) -> p k t", k=8), func=AF.Copy),
          reads=[pt_r], writes=[hT_res])


_UC = [0]


def _u(name):
    _UC[0] += 1
    return f"{name}_{_UC[0]}"


class View:
    def __init__(self, ap):
        self._ap = ap

    def ap(self):
        return self._ap


def emit_phaseA(nc, bd, es, x_d, w_d, g_d, id_d, pT_d, ntok):
    sb = lambda name, shape, dt: es.enter_context(nc.sbuf_tensor(_u(name), shape, dt))
    ps = lambda name, shape, dt: es.enter_context(nc.psum_tensor(_u(name), shape, dt))
    wb = sb("A_wb", [128, 8, NCOLA], BF16)
    wb_r = [Res() for _ in range(8)]
    stage = [sb(f"A_stage{i}", [128, NCOLA], F32) for i in range(2)]
    stage_r = [Res(), Res()]
    gcol = sb("A_gcol", [128, 8], F32)
    gcol_r = Res()
    idf = sb("A_idf", [128, 128], F32)
    ident = sb("A_ident", [128, 128], BF16)
    ident_r = Res()
    idf_r = Res()
    xt = [sb(f"A_xt{i}", [128, D], F32) for i in range(2)]
    xt_r = [Res(), Res()]
    hT = [sb(f"A_hT{i}", [128, 8, 512], BF16) for i in range(2)]
    hT_r = [Res(), Res()]
    tmp = dict(ss=sb("A_ss", [128, 4], F32), ss_r=Res(), junk=sb("A_junk", [128, D], BF16), junk_r=Res(),
               xn=sb("A_xn", [128, D], BF16), xn_r=Res(),
               pt=ps("A_pt", [128, D], BF16), pt_r=Res(), eps=sb("A_eps", [128, 1], F32), eps_r=Res())
    bd.op("dve", lambda e: e.memset(tmp["eps"][:], EPS), writes=[tmp["eps_r"]])
    NPB = 4
    pb = [ps(f"A_pb{i}", [128, 512], F32) for i in range(NPB)]
    pb_r = [Res() for _ in range(NPB)]
    ost = [sb(f"A_ost{i}", [128, 512], F32) for i in range(4)]
    ost_r = [Res() for _ in range(4)]

    bd.dma("sp", gcol[:], g_d.ap(), writes=[gcol_r])
    bd.dma("sp", idf[:], id_d.ap(), writes=[idf_r])
    bd.op("dve", lambda e: e.tensor_copy(out=ident[:], in_=idf[:]), reads=[idf_r], writes=[ident_r])
    for kc in range(8):
        s = kc % 2
        bd.dma("pool", stage[s][:], w_d.ap()[kc * 128:(kc + 1) * 128, :], writes=[stage_r[s]])
        bd.op("dve", lambda e, kc=kc, s=s: e.tensor_scalar(out=wb[:, kc, :], in0=stage[s][:],
                                                          scalar1=gcol[:, kc:kc + 1], scalar2=None, op0=ALU.mult),
              reads=[stage_r[s], gcol_r], writes=[wb_r[kc]])
    x_ap = x_d.ap()
    pT_ap = pT_d.ap()
    nblk = ntok // 512
    oi = 0
    for blk in range(nblk):
        hb = blk % 2
        for j in range(4):
            ti = blk * 4 + j
            xb = ti % 2
            bd.dma("sp", xt[xb][:], x_ap[ti * 128:(ti + 1) * 128, :], writes=[xt_r[xb]])
            tmp2 = dict(tmp)
            emit_rmsnorm_T(bd, xt[xb][:], xt_r[xb], hT[hb], hT_r[hb], j, tmp2, ident)
        for half in range(2):
            for c in range(NCH):
                m = CH_ROWS[c]
                col0 = half * HALF_ROWS + CH_OFF[c]
                pbi = oi % NPB
                for kc in range(8):
                    bd.op("pe", lambda e, kc=kc, col0=col0, m=m, pbi=pbi, hb=hb: e.matmul(
                        pb[pbi][0:m, :], lhsT=wb[:, kc, col0:col0 + m], rhs=hT[hb][:, kc, :],
                        start=(kc == 0), stop=(kc == 7)),
                        reads=[wb_r[kc], hT_r[hb]], writes=[pb_r[pbi]])
                osi = oi % 4
                eng = "act" if oi % 2 == 0 else "dve"
                if eng == "act":
                    bd.op("act", lambda e, m=m, pbi=pbi, osi=osi: e.activation(
                        out=ost[osi][0:m, :], in_=pb[pbi][0:m, :], func=AF.Copy),
                        reads=[pb_r[pbi]], writes=[ost_r[osi]])
                else:
                    bd.op("dve", lambda e, m=m, pbi=pbi, osi=osi: e.tensor_copy(
                        out=ost[osi][0:m, :], in_=pb[pbi][0:m, :]),
                        reads=[pb_r[pbi]], writes=[ost_r[osi]])
                bd.dma("sp", pT_ap[col0:col0 + m, blk * 512:(blk + 1) * 512], ost[osi][0:m, :],
                       reads=[ost_r[osi]])
                oi += 1
    bd.barrier()


def _gcol(g):
    return np.ascontiguousarray(np.asarray(g, np.float32).reshape(8, 128).T)


def prep_w_inA(w_in_l):
    cols = half_cols(0) + half_cols(1)
    return np.ascontiguousarray(w_in_l[:, cols])


TWO_PI = 6.283185307179586
(C_QN0, C_QN1, C_KVN0, C_KVN1, C_INVF, C_SGN, C_MLAO0, C_MLAO1,
 C_MU_R, C_MU_K, C_MU_V, C_MU_L, C_W0, C_A0, C_KK, C_KA, C_RK, C_LNW, C_LNB,
 C_CWQ0, C_CWQ1, C_CWQ2, C_CWQ3, C_CBQ, C_CWK0, C_CWK1, C_CWK2, C_CWK3, C_CBK,
 C_IB, C_FB, C_MLO, C_EPS, C_LNEPS, C_ONE, C_ZERO, C_IB1, C_FB1) = range(38)
NCST = 38


def emit_attention(bd, nc, es, name, heads, scale_exp, dv, wfun, out_fn, PS):
    sb = lambda nm, shape, dt: es.enter_context(nc.sbuf_tensor(_u(nm), shape, dt))
    LOOK = 3
    NST = len(PS["st"])
    NPT = LOOK + 2
    pT = [sb(f"{name}_pT{i}", [128, 512], BF16) for i in range(NPT)]
    pT_r = [Res() for _ in range(NPT)]
    mask = PS["mask"]
    mask_r = PS["mask_r"]
    blocks = [(h, qb, kt) for h in heads for qb in range(SEQ // 512) for kt in range(4 * (qb + 1))]
    pending = []

    def stage1(i):
        h, qb, kt = blocks[i]
        st, st_r = PS["st"][i % NST]
        kp = PS["kparts"](h, kt)
        qp = PS["qparts"](h, qb)
        n = len(kp)
        for a in range(n):
            bd.op("pe", lambda e, a=a: e.matmul(st[:, :], lhsT=kp[a][0], rhs=qp[a][0],
                                                start=(a == 0), stop=(a == n - 1)),
                  reads=[kp[a][1], qp[a][1]], writes=[st_r])
        p, p_r = pT[i % NPT], pT_r[i % NPT]
        wfun(h, kt, qb, st, st_r, p, p_r)
        jd = kt - 4 * qb
        if jd >= 0:
            bd.op("pool", lambda e: e.tensor_tensor(
                out=p[:, jd * 128:(jd + 1) * 128], in0=p[:, jd * 128:(jd + 1) * 128], in1=mask[:],
                op=ALU.mult), reads=[p_r, mask_r], writes=[p_r])

    def stage2(i):
        h, qb, kt = blocks[i]
        p, p_r = pT[i % NPT], pT_r[i % NPT]
        v_ap, v_r = PS["v"](h, kt)
        for j in range(4):
            qt = 4 * qb + j
            if qt < kt:
                continue
            o, o_r = PS["o"][j]
            bd.op("pe", lambda e, o=o, j=j, qt=qt: e.matmul(
                o[:, 0:dv + 1], lhsT=p[:, j * 128:(j + 1) * 128], rhs=v_ap,
                start=(kt == 0), stop=(kt == qt)), reads=[p_r, v_r], writes=[o_r])
            if kt == qt:
                th = out_fn(h, qt, o, o_r)
                if th is not None:
                    pending.append([3, th])

    nblk = len(blocks)
    for i in range(nblk + LOOK):
        if i < nblk:
            stage1(i)
        if i - LOOK >= 0:
            stage2(i - LOOK)
        for item in pending:
            item[0] -= 1
        while pending and pending[0][0] <= 0:
            pending.pop(0)[1]()
    while pending:
        pending.pop(0)[1]()


def emit_rope_tables(bd, nc, es, pos_d, cst, cst_r, CS, SN, tab_r):
    with ExitStack() as es2:
        sb = lambda nm, shape, dt: es2.enter_context(nc.sbuf_tensor(_u(nm), shape, dt))
        ti = sb("rt_i", [64, SEQ], I32)
        ta = sb("rt_a", [64, SEQ], F32)
        tb = sb("rt_b", [64, SEQ], F32)
        ti_r, ta_r, tb_r = Res(), Res(), Res()
        src = bass.AP(pos_d, 0, [[0, 64], [1, SEQ]])
        bd.dma("sp", ti[:], src, writes=[ti_r])
        bd.op("dve", lambda e: e.tensor_copy(out=ta[:], in_=ti[:]), reads=[ti_r], writes=[ta_r])
        bd.op("dve", lambda e: e.tensor_scalar(out=ta[:], in0=ta[:], scalar1=cst[0:64, C_INVF:C_INVF + 1],
                                               scalar2=None, op0=ALU.mult), reads=[ta_r, cst_r], writes=[ta_r])
        for which in (0, 1):
            shift = 0.0 if which == 0 else TWO_PI / 4
            bd.op("dve", lambda e: e.tensor_scalar(out=tb[:], in0=ta[:], scalar1=shift, scalar2=1.0 / TWO_PI,
                                                   op0=ALU.add, op1=ALU.mult), reads=[ta_r], writes=[tb_r])
            bd.op("dve", lambda e: e.tensor_copy(out=ti[:], in_=tb[:]), reads=[tb_r], writes=[ti_r])
            bd.op("dve", lambda e: e.tensor_copy(out=tb[:], in_=ti[:]), reads=[ti_r], writes=[tb_r])
            bd.op("dve", lambda e: e.scalar_tensor_tensor(out=tb[:], in0=tb[:], scalar=-TWO_PI, in1=ta[:],
                                                          op0=ALU.mult, op1=ALU.add),
                  reads=[tb_r, ta_r], writes=[tb_r])
            bd.op("dve", lambda e: e.tensor_scalar(out=tb[:], in0=tb[:], scalar1=shift, scalar2=TWO_PI / 2,
                                                   op0=ALU.add, op1=ALU.min), reads=[tb_r], writes=[tb_r])
            bd.op("dve", lambda e: e.tensor_scalar(out=tb[:], in0=tb[:], scalar1=-TWO_PI / 2, scalar2=None,
                                                   op0=ALU.max), reads=[tb_r], writes=[tb_r])
            if which == 0:
                bd.op("act", lambda e: e.activation(out=SN[:], in_=tb[:], func=AF.Sin,
                                                    scale=cst[0:64, C_SGN:C_SGN + 1]),
                      reads=[tb_r, cst_r], writes=[tab_r])
            else:
                bd.op("act", lambda e: e.activation(out=CS[:], in_=tb[:], func=AF.Sin),
                      reads=[tb_r], writes=[tab_r])
        bd.barrier()


def emit_mla(bd, nc, es, pT_d, pos_d, wuq_d, wukv_d, yT_d, cst, cst_r, K):
    sb = lambda nm, shape, dt: es.enter_context(nc.sbuf_tensor(_u(nm), shape, dt))
    pT = pT_d.ap()
    yT = yT_d.ap()
    CS = sb("m_CS", [64, SEQ], F32)
    SN = sb("m_SN", [64, SEQ], F32)
    tab_r = Res()
    emit_rope_tables(bd, nc, es, pos_d, cst, cst_r, CS, SN, tab_r)
    wq = sb("m_wq", [128, 2, 512], BF16)
    wkv = sb("m_wkv", [128, 2, 512], BF16)
    wq_r, wkv_r = Res(), Res()
    wst = sb("m_wst", [128, 2, 512], F32)
    wst_r = Res()
    for (wd, wt, wr) in ((wuq_d, wq, wq_r), (wukv_d, wkv, wkv_r)):
        bd.dma("sp", wst[:], wd.ap().rearrange("(k p) n -> p k n", p=128), writes=[wst_r])
        bd.op("dve", lambda e, wt=wt: e.tensor_copy(out=wt[:], in_=wst[:]), reads=[wst_r], writes=[wr])
    Qn = [sb(f"m_Qn{h}", [128, SEQ], BF16) for h in range(2)]
    Qr = [sb(f"m_Qr{h}", [64, SEQ], BF16) for h in range(2)]
    Kn = [sb(f"m_Kn{h}", [128, SEQ], BF16) for h in range(2)]
    Kr = sb("m_Kr", [64, SEQ], BF16)
    V = [sb(f"m_V{h}", [128, 32, 129], BF16) for h in range(2)]
    qk_r = Res()
    for h in range(2):
        bd.op("pool", lambda e, h=h: e.memset(V[h][:, :, 128:129], 1.0), writes=[qk_r])
    banks, banks_r, ptb, ptb_r = K["banks"], K["banks_r"], K["ptb"], K["ptb_r"]
    ones, ones_r = K["ones"], K["ones_r"]
    with ExitStack() as es2:
        sb2 = lambda nm, shape, dt: es2.enter_context(nc.sbuf_tensor(_u(nm), shape, dt))
        cf = [sb2(f"m_cf{i}", [128, 2, 512], F32) for i in range(2)]
        cf_r = [Res(), Res()]
        sq = sb2("m_sq", [128, 2, 512], F32)
        sq_r = Res()
        rs = sb2("m_rs", [128, 512], F32)
        rs_r = Res()
        cn = [sb2(f"m_cn{i}", [128, 2, 512], BF16) for i in range(2)]
        cn_r = [Res(), Res()]
        kx = sb2("m_kx", [64, 2, 512], F32)
        kx_r = Res()
        t1 = sb2("m_t1", [64, 512], F32)
        t2 = sb2("m_t2", [64, 512], F32)
        t1_r, t2_r = Res(), Res()
        bi = 0

        def nb():
            nonlocal bi
            b = bi % 6
            bi += 1
            return banks[b], banks_r[b]

        def rope(dst, x_ap, xs_ap, src_res, sl):
            bd.op("dve", lambda e: e.tensor_tensor(out=t1[:], in0=x_ap, in1=CS[:, sl], op=ALU.mult),
                  reads=src_res + [tab_r], writes=[t1_r])
            bd.op("dve", lambda e: e.tensor_tensor(out=t2[:], in0=xs_ap, in1=SN[:, sl], op=ALU.mult),
                  reads=src_res + [tab_r], writes=[t2_r])
            bd.op("dve", lambda e: e.tensor_tensor(out=dst, in0=t1[:], in1=t2[:], op=ALU.add),
                  reads=[t1_r, t2_r], writes=[qk_r])

        for tb in range(SEQ // 512):
            sl = slice(tb * 512, (tb + 1) * 512)
            for which in (0, 1):
                ci = which
                row0 = which * 256
                bd.dma("sp", cf[ci][:], pT[row0:row0 + 256, sl].rearrange("(k p) t -> p k t", p=128),
                       writes=[cf_r[ci]])
                bd.op("act", lambda e, ci=ci: e.activation(out=sq[:], in_=cf[ci][:], func=AF.Square),
                      reads=[cf_r[ci]], writes=[sq_r])
                pss, pss_r = nb()
                for c in range(2):
                    bd.op("pe", lambda e, c=c, pss=pss: e.matmul(pss[:, :], lhsT=ones[:], rhs=sq[:, c, :],
                                                                 start=(c == 0), stop=(c == 1)),
                          reads=[ones_r, sq_r], writes=[pss_r])
                bd.op("act", lambda e, pss=pss: e.activation(out=rs[:], in_=pss[:, :], func=AF.Sqrt,
                                                             scale=1.0 / 256, bias=cst[:, C_EPS:C_EPS + 1]),
                      reads=[pss_r, cst_r], writes=[rs_r])
                bd.op("dve", lambda e: e.reciprocal(out=rs[:], in_=rs[:]), reads=[rs_r], writes=[rs_r])
                gcol = C_QN0 if which == 0 else C_KVN0
                for c in range(2):
                    bd.op("dve", lambda e, c=c, ci=ci, gcol=gcol: e.scalar_tensor_tensor(
                        out=cn[ci][:, c, :], in0=cf[ci][:, c, :], scalar=cst[:, gcol + c:gcol + c + 1], in1=rs[:],
                        op0=ALU.mult, op1=ALU.mult), reads=[cf_r[ci], rs_r, cst_r], writes=[cn_r[ci]])
                if which == 0:
                    for h in range(2):
                        pq, pq_r = nb()
                        for c in range(2):
                            bd.op("pe", lambda e, c=c, h=h, pq=pq: e.matmul(
                                pq[:, :], lhsT=wq[:, c, h * 256:h * 256 + 128], rhs=cn[0][:, c, :],
                                start=(c == 0), stop=(c == 1)), reads=[wq_r, cn_r[0]], writes=[pq_r])
                        bd.op("act", lambda e, h=h, pq=pq: e.activation(out=Qn[h][:, sl], in_=pq[:, :], func=AF.Copy),
                              reads=[pq_r], writes=[qk_r])
                        pa, pa_r = nb()
                        pb_, pb_r = nb()
                        for (pp, pp_r, off) in ((pa, pa_r, 128), (pb_, pb_r, 192)):
                            for c in range(2):
                                bd.op("pe", lambda e, c=c, h=h, pp=pp, off=off: e.matmul(
                                    pp[0:64, :], lhsT=wq[:, c, h * 256 + off:h * 256 + off + 64], rhs=cn[0][:, c, :],
                                    start=(c == 0), stop=(c == 1)), reads=[wq_r, cn_r[0]], writes=[pp_r])
                        rope(Qr[h][:, sl], pa[0:64, :], pb_[0:64, :], [pa_r, pb_r], sl)
                else:
                    for h in range(2):
                        pk, pk_r = nb()
                        for c in range(2):
                            bd.op("pe", lambda e, c=c, h=h, pk=pk: e.matmul(
                                pk[:, :], lhsT=wkv[:, c, h * 256:h * 256 + 128], rhs=cn[1][:, c, :],
                                start=(c == 0), stop=(c == 1)), reads=[wkv_r, cn_r[1]], writes=[pk_r])
                        bd.op("act", lambda e, h=h, pk=pk: e.activation(out=Kn[h][:, sl], in_=pk[:, :], func=AF.Copy),
                              reads=[pk_r], writes=[qk_r])
                        pv, pv_r = nb()
                        for j in range(4):
                            for c in range(2):
                                bd.op("pe", lambda e, c=c, h=h, j=j, pv=pv: e.matmul(
                                    pv[:, j * 128:(j + 1) * 128], lhsT=cn[1][:, c, j * 128:(j + 1) * 128],
                                    rhs=wkv[:, c, h * 256 + 128:h * 256 + 256],
                                    start=(c == 0 and j == 0), stop=(c == 1)), reads=[wkv_r, cn_r[1]], writes=[pv_r])
                        bd.op("dve", lambda e, h=h, pv=pv: e.tensor_copy(
                            out=V[h][:, tb * 4:(tb + 1) * 4, 0:128],
                            in_=pv[:, :].rearrange("p (j d) -> p j d", j=4)), reads=[pv_r], writes=[qk_r])
            bd.dma("sp", kx[:], pT[512:640, sl].rearrange("(k p) t -> p k t", p=64), writes=[kx_r])
            rope(Kr[:, sl], kx[:, 0, :], kx[:, 1, :], [kx_r], sl)
        bd.barrier()
    with ExitStack() as es3:
        sb3 = lambda nm, shape, dt: es3.enter_context(nc.sbuf_tensor(_u(nm), shape, dt))
        of = sb3("m_of", [128, 132], F32)
        of_r = Res()
        onb = sb3("m_onb", [128, 128], BF16)
        onb_r = Res()
        st_ = sb3("m_stat", [128, 4], F32)
        st_r = Res()
        junk = sb3("m_junk", [128, 128], F32)
        junk_r = Res()
        yst = [sb3(f"m_yst{i}", [128, 512], BF16) for i in range(2)]
        yst_r = [Res(), Res()]
        scale = (128 + 64) ** -0.5

        def wfun(h, kt, qb, st, st_r2, p, p_r):
            bd.op("act", lambda e: e.activation(out=p[:], in_=st[:, :], func=AF.Exp, scale=scale),
                  reads=[st_r2], writes=[p_r])

        onbs = [sb3(f"m_onb{i}", [128, 128], BF16) for i in range(4)]
        onbs_r = [Res() for _ in range(4)]
        oi = [0]

        def out_fn(h, qt, o, o_r):
            ob, ob_r = onbs[oi[0] % 4], onbs_r[oi[0] % 4]
            oi[0] += 1
            bd.op("dve", lambda e: e.reciprocal(out=st_[:, 0:1], in_=o[:, 128:129]), reads=[o_r], writes=[st_r])
            bd.op("dve", lambda e: e.tensor_scalar(out=of[:, 0:128], in0=o[:, 0:128], scalar1=st_[:, 0:1],
                                                   scalar2=None, op0=ALU.mult), reads=[o_r, st_r], writes=[of_r])
            bd.op("dve", lambda e: e.scalar_tensor_tensor(out=junk[:], in0=of[:, 0:128], scalar=1.0, in1=of[:, 0:128],
                                                          op0=ALU.mult, op1=ALU.mult, accum_out=st_[:, 1:2]),
                  reads=[of_r], writes=[junk_r, st_r])
            bd.op("act", lambda e: e.activation(out=st_[:, 2:3], in_=st_[:, 1:2], func=AF.Sqrt, scale=1.0 / 128,
                                                bias=cst[:, C_EPS:C_EPS + 1]), reads=[st_r, cst_r], writes=[st_r])
            bd.op("dve", lambda e: e.reciprocal(out=st_[:, 3:4], in_=st_[:, 2:3]), reads=[st_r], writes=[st_r])
            bd.op("dve", lambda e: e.tensor_scalar(out=ob[:], in0=of[:, 0:128], scalar1=st_[:, 3:4], scalar2=None,
                                                   op0=ALU.mult), reads=[of_r, st_r], writes=[ob_r])

            def fin():
                bd.op("pe", lambda e: e.transpose(out=ptb[:, 0:128], in_=ob[:], identity=K["identb"][:]),
                      reads=[ob_r, K["identb_r"]], writes=[ptb_r])
                ys, ys_r = yst[(qt // 4) % 2], yst_r[(qt // 4) % 2]
                j = qt % 4
                bd.op("act", lambda e: e.activation(out=ys[:, j * 128:(j + 1) * 128], in_=ptb[:, 0:128], func=AF.Copy,
                                                    scale=cst[:, C_MLAO0 + h:C_MLAO0 + h + 1]),
                      reads=[ptb_r, cst_r], writes=[ys_r])
                if j == 3:
                    qb = qt // 4
                    bd.dma("sp", yT[h * 128:(h + 1) * 128, qb * 512:(qb + 1) * 512], ys[:], reads=[ys_r])
            return fin

        PS = dict(st=[(banks[0], banks_r[0]), (banks[1], banks_r[1]), (banks[6], banks_r[6])],
                  o=[(banks[2 + j], banks_r[2 + j]) for j in range(4)],
                  mask=K["mask"], mask_r=K["mask_r"],
                  kparts=lambda h, kt: [(Kn[h][:, kt * 128:(kt + 1) * 128], qk_r), (Kr[:, kt * 128:(kt + 1) * 128], qk_r)],
                  qparts=lambda h, qb: [(Qn[h][:, qb * 512:(qb + 1) * 512], qk_r), (Qr[h][:, qb * 512:(qb + 1) * 512], qk_r)],
                  v=lambda h, kt: (V[h][:, kt, :], qk_r))
        emit_attention(bd, nc, es3, "mla", [0, 1], scale, 128, wfun, out_fn, PS)
        bd.barrier()


def emit_mlstm(bd, nc, es, pT_d, yT_d, cst, cst_r, K):
    sb = lambda nm, shape, dt: es.enter_context(nc.sbuf_tensor(_u(nm), shape, dt))
    pT = pT_d.ap()
    yT = yT_d.ap()
    banks, banks_r, ptb, ptb_r = K["banks"], K["banks_r"], K["ptb"], K["ptb_r"]
    misc, misc_r = banks[6], banks_r[6]
    R_Q, R_K, R_V, R_O, R_G = 1152, 1216, 1280, 1408, 1536
    Qb = sb("l_Qb", [64, SEQ], BF16)
    Kb = sb("l_Kb", [64, SEQ], BF16)
    Vm = sb("l_Vm", [128, 32, 2, 65], BF16)
    Ym = sb("l_Ym", [128, SEQ], BF16)
    uT = sb("l_uT", [128, 2, 32], F32)
    emT = sb("l_emT", [128, 2, 32], F32)
    nPb = [sb(f"l_nPb{h}", [128, SEQ], F32) for h in range(2)]
    prep_r = Res()
    ym_r = Res()
    bd.op("pool", lambda e: e.memset(Vm[:, :, :, 64:65], 1.0), writes=[prep_r])
    with ExitStack() as es2:
        sb2 = lambda nm, shape, dt: es2.enter_context(nc.sbuf_tensor(_u(nm), shape, dt))
        xin = sb2("l_xin", [64, SEQ], F32)
        A = sb2("l_A", [64, SEQ], F32)
        B = sb2("l_B", [64, SEQ], F32)
        xin_r, A_r, B_r = Res(), Res(), Res()
        for (row0, cw0, cb, dst, scl) in ((R_Q, C_CWQ0, C_CBQ, Qb, 32 ** -0.5), (R_K, C_CWK0, C_CBK, Kb, 1.0)):
            bd.dma("sp", xin[:], pT[row0:row0 + 64, :], writes=[xin_r])
            bd.op("dve", lambda e, cw0=cw0, cb=cb: e.tensor_scalar(
                out=A[:], in0=xin[:], scalar1=cst[0:64, cw0 + 3:cw0 + 4], scalar2=cst[0:64, cb:cb + 1],
                op0=ALU.mult, op1=ALU.add), reads=[xin_r, cst_r], writes=[A_r])
            src, src_r, dstt, dst_r = A, A_r, B, B_r
            for sh in (1, 2, 3):
                bd.op("dve", lambda e, sh=sh, cw0=cw0, src=src, dstt=dstt: e.scalar_tensor_tensor(
                    out=dstt[:, sh:], in0=xin[:, 0:SEQ - sh], scalar=cst[0:64, cw0 + 3 - sh:cw0 + 4 - sh],
                    in1=src[:, sh:], op0=ALU.mult, op1=ALU.add), reads=[xin_r, cst_r, src_r], writes=[dst_r])
                bd.op("dve", lambda e, sh=sh, src=src, dstt=dstt: e.tensor_copy(out=dstt[:, 0:sh], in_=src[:, 0:sh]),
                      reads=[src_r], writes=[dst_r])
                src, src_r, dstt, dst_r = dstt, dst_r, src, src_r
            bd.op("act", lambda e, src=src: e.activation(out=xin[:], in_=src[:], func=AF.Silu),
                  reads=[src_r], writes=[xin_r])
            bd.op("dve", lambda e, dst=dst, scl=scl: e.tensor_scalar(out=dst[:], in0=xin[:], scalar1=scl, scalar2=None,
                                                                     op0=ALU.mult), reads=[xin_r], writes=[prep_r])
        bd.barrier()
    with ExitStack() as es2:
        sb2 = lambda nm, shape, dt: es2.enter_context(nc.sbuf_tensor(_u(nm), shape, dt))
        t0 = sb2("l_t0", [1, SEQ], F32)
        t1 = sb2("l_t1", [1, SEQ], F32)
        t2 = sb2("l_t2", [1, SEQ], F32)
        onesrow = sb2("l_onesrow", [1, SEQ], F32)
        vin = sb2("l_vin", [128, SEQ], F32)
        t0_r, t1_r, t2_r, or_r, vin_r = Res(), Res(), Res(), Res(), Res()
        bd.op("dve", lambda e: e.memset(onesrow[:], 1.0), writes=[or_r])
        identf = K["identf"]
        for h in range(2):
            cib = C_IB if h == 0 else C_IB1
            cfb = C_FB if h == 0 else C_FB1
            bd.dma("sp", t0[:], pT[R_G + h:R_G + h + 1, :], writes=[t0_r])
            bd.dma("sp", t1[:], pT[R_G + 2 + h:R_G + 3 + h, :], writes=[t1_r])
            bd.op("dve", lambda e, cib=cib: e.tensor_scalar(out=t0[:], in0=t0[:], scalar1=cst[0:1, cib:cib + 1],
                                                            scalar2=None, op0=ALU.add), reads=[t0_r, cst_r], writes=[t0_r])
            bd.op("act", lambda e, cfb=cfb: e.activation(out=t1[:], in_=t1[:], func=AF.Sigmoid,
                                                         bias=cst[0:1, cfb:cfb + 1]), reads=[t1_r, cst_r], writes=[t1_r])
            bd.op("act", lambda e: e.activation(out=t1[:], in_=t1[:], func=AF.Ln), reads=[t1_r], writes=[t1_r])
            bd.op("dve", lambda e: e.tensor_tensor_scan(out=t2[:], data0=onesrow[:], data1=t1[:], initial=0.0,
                                                        op0=ALU.mult, op1=ALU.add), reads=[or_r, t1_r], writes=[t2_r])
            bd.op("dve", lambda e: e.tensor_tensor(out=t0[:], in0=t0[:], in1=t2[:], op=ALU.subtract),
                  reads=[t0_r, t2_r], writes=[t0_r])
            bd.op("dve", lambda e: e.tensor_tensor_scan(out=t1[:], data0=onesrow[:], data1=t0[:], initial=0.0,
                                                        op0=ALU.mult, op1=ALU.max), reads=[or_r, t0_r], writes=[t1_r])
            bd.op("dve", lambda e: e.tensor_tensor(out=t2[:], in0=t2[:], in1=t1[:], op=ALU.add),
                  reads=[t2_r, t1_r], writes=[t2_r])
            for jt in range(32):
                bd.op("pe", lambda e, jt=jt: e.transpose(out=misc[:, jt:jt + 1], in_=t0[0:1, jt * 128:(jt + 1) * 128],
                                                         identity=identf[0:1, 0:1]), reads=[t0_r, K["mats_r"]], writes=[misc_r])
                bd.op("pe", lambda e, jt=jt: e.transpose(out=misc[:, 32 + jt:33 + jt], in_=t2[0:1, jt * 128:(jt + 1) * 128],
                                                         identity=identf[0:1, 0:1]), reads=[t2_r, K["mats_r"]], writes=[misc_r])
            bd.op("dve", lambda e, h=h: e.tensor_copy(out=uT[:, h, :], in_=misc[:, 0:32]), reads=[misc_r], writes=[prep_r])
            bd.op("act", lambda e, h=h: e.activation(out=emT[:, h, :], in_=misc[:, 32:64], func=AF.Exp, scale=-1.0),
                  reads=[misc_r], writes=[prep_r])
            for tb in range(SEQ // 512):
                bd.op("pe", lambda e, tb=tb: e.matmul(misc[:, :], lhsT=K["ones"][0:1, :], rhs=t1[0:1, tb * 512:(tb + 1) * 512],
                                                      start=True, stop=True), reads=[t1_r, K["ones_r"]], writes=[misc_r])
                bd.op("act", lambda e, tb=tb, h=h: e.activation(out=nPb[h][:, tb * 512:(tb + 1) * 512], in_=misc[:, :],
                                                                func=AF.Copy, scale=-1.0), reads=[misc_r], writes=[prep_r])
        bd.dma("sp", vin[:], pT[R_V:R_V + 128, :], writes=[vin_r])
        for jt in range(32):
            bd.op("pe", lambda e, jt=jt: e.transpose(out=misc[:, 0:128], in_=vin[:, jt * 128:(jt + 1) * 128],
                                                     identity=identf), reads=[vin_r, K["mats_r"]], writes=[misc_r])
            bd.op("dve", lambda e, jt=jt: e.tensor_copy(out=Vm[:, jt, :, 0:64],
                                                        in_=misc[:, 0:128].rearrange("p (h d) -> p h d", h=2)),
                  reads=[misc_r], writes=[prep_r])
        bd.barrier()
    with ExitStack() as es3:
        sb3 = lambda nm, shape, dt: es3.enter_context(nc.sbuf_tensor(_u(nm), shape, dt))
        Wt = [sb3(f"l_W{i}", [128, 512], F32) for i in range(2)]
        Wt_r = [Res(), Res()]
        of = sb3("l_of", [128, 64], F32)
        of_r = Res()
        onb = sb3("l_onb", [128, 64], BF16)
        onb_r = Res()
        st_ = sb3("l_stat", [128, 6], F32)
        st_r = Res()
        junk = sb3("l_junk", [128, 64], F32)
        junk_r = Res()
        wi = [0]

        def wfun(h, kt, qb, st, st_r2, p, p_r):
            w, w_r = Wt[wi[0] % 2], Wt_r[wi[0] % 2]
            wi[0] += 1
            bd.op("act", lambda e: e.activation(out=w[:], in_=nPb[h][:, qb * 512:(qb + 1) * 512], func=AF.Exp,
                                                bias=uT[:, h, kt:kt + 1]), reads=[prep_r], writes=[w_r])
            bd.op("dve", lambda e: e.tensor_tensor(out=p[:], in0=st[:, :], in1=w[:], op=ALU.mult),
                  reads=[st_r2, w_r], writes=[p_r])

        onbs = [sb3(f"l_onb{i}", [128, 64], BF16) for i in range(4)]
        onbs_r = [Res() for _ in range(4)]
        oi = [0]

        def out_fn(h, qt, o, o_r):
            ob, ob_r = onbs[oi[0] % 4], onbs_r[oi[0] % 4]
            oi[0] += 1
            bd.op("act", lambda e: e.activation(out=st_[:, 0:1], in_=o[:, 64:65], func=AF.Abs), reads=[o_r], writes=[st_r])
            bd.op("dve", lambda e: e.tensor_tensor(out=st_[:, 1:2], in0=st_[:, 0:1], in1=emT[:, h, qt:qt + 1], op=ALU.max),
                  reads=[st_r, prep_r], writes=[st_r])
            bd.op("dve", lambda e: e.reciprocal(out=st_[:, 2:3], in_=st_[:, 1:2]), reads=[st_r], writes=[st_r])
            bd.op("dve", lambda e: e.tensor_scalar(out=of[:], in0=o[:, 0:64], scalar1=st_[:, 2:3], scalar2=None,
                                                   op0=ALU.mult), reads=[o_r, st_r], writes=[of_r])
            bd.op("dve", lambda e: e.scalar_tensor_tensor(out=junk[:], in0=of[:], scalar=1.0, in1=of[:],
                                                          op0=ALU.mult, op1=ALU.mult, accum_out=st_[:, 3:4]),
                  reads=[of_r], writes=[junk_r, st_r])
            bd.op("act", lambda e: e.activation(out=st_[:, 4:5], in_=st_[:, 3:4], func=AF.Sqrt, scale=1.0 / 64,
                                                bias=cst[:, C_EPS:C_EPS + 1]), reads=[st_r, cst_r], writes=[st_r])
            bd.op("dve", lambda e: e.reciprocal(out=st_[:, 5:6], in_=st_[:, 4:5]), reads=[st_r], writes=[st_r])
            bd.op("dve", lambda e: e.tensor_scalar(out=ob[:], in0=of[:], scalar1=st_[:, 5:6], scalar2=None, op0=ALU.mult),
                  reads=[of_r, st_r], writes=[ob_r])

            def fin():
                bd.op("pe", lambda e: e.transpose(out=ptb[h * 64:(h + 1) * 64, 0:128], in_=ob[:], identity=K["identb"][:]),
                      reads=[ob_r, K["identb_r"]], writes=[ptb_r])
                bd.op("act", lambda e: e.activation(out=Ym[h * 64:(h + 1) * 64, qt * 128:(qt + 1) * 128],
                                                    in_=ptb[h * 64:(h + 1) * 64, 0:128], func=AF.Copy,
                                                    scale=cst[h * 64:(h + 1) * 64, C_MLO:C_MLO + 1]),
                      reads=[ptb_r, cst_r], writes=[ym_r])
            return fin

        PS = dict(st=[(banks[0], banks_r[0]), (banks[1], banks_r[1]), (banks[6], banks_r[6])],
                  o=[(banks[2 + j], banks_r[2 + j]) for j in range(4)],
                  mask=K["mask"], mask_r=K["mask_r"],
                  kparts=lambda h, kt: [(Kb[h * 32:(h + 1) * 32, kt * 128:(kt + 1) * 128], prep_r)],
                  qparts=lambda h, qb: [(Qb[h * 32:(h + 1) * 32, qb * 512:(qb + 1) * 512], prep_r)],
                  v=lambda h, kt: (Vm[:, kt, h, :], prep_r))
        emit_attention(bd, nc, es3, "mls", [0, 1], 1.0, 64, wfun, out_fn, PS)
        og = sb3("l_og", [128, SEQ], F32)
        og_r = Res()
        bd.dma("sp", og[:], pT[R_O:R_O + 128, :], writes=[og_r])
        bd.op("act", lambda e: e.activation(out=og[:], in_=og[:], func=AF.Sigmoid), reads=[og_r], writes=[og_r])
        bd.op("dve", lambda e: e.tensor_tensor(out=Ym[:], in0=Ym[:], in1=og[:], op=ALU.mult),
              reads=[ym_r, og_r], writes=[ym_r])
        bd.dma("sp", yT[384:512, :], Ym[:], reads=[ym_r])
        bd.barrier()


RW_T = 16


def emit_rwkv(bd, nc, es, D_, cst, cst_r, K):
    sb = lambda nm, shape, dt: es.enter_context(nc.sbuf_tensor(_u(nm), shape, dt))
    pT = D_["pT"].ap()
    yT = D_["yT"].ap()
    scr = D_["scr"]
    banks, banks_r = K["banks"], K["banks_r"]
    bones, identf, mats_r = K["bones"], K["identf"], K["mats_r"]
    R_R, R_K, R_V, R_L = 640, 768, 896, 1024
    vS = sb("r_vS", [128, SEQ], F32)
    gS = sb("r_gS", [128, SEQ], F32)
    boS = sb("r_boS", [128, SEQ], F32)
    yS = sb("r_yS", [128, SEQ], F32)
    vS_r, gS_r, boS_r, yS_r = Res(), Res(), Res(), Res()
    wl = sb("r_wl", [128, 3, 128], F32)
    wl_r = Res()
    bd.dma("sp", wl[:], D_["wl"].ap(), writes=[wl_r])
    c2 = sb("r_c2", [128, 2], F32)
    c2_r = Res()
    bd.op("dve", lambda e: e.tensor_scalar(out=c2[:, 0:1], in0=cst[:, C_KA:C_KA + 1], scalar1=-1.0, scalar2=1.0,
                                           op0=ALU.mult, op1=ALU.add), reads=[cst_r], writes=[c2_r])
    scr_r = Res()
    with ExitStack() as es2:
        sb2 = lambda nm, shape, dt: es2.enter_context(nc.sbuf_tensor(_u(nm), shape, dt))
        rS = sb2("r_rS", [128, SEQ], F32)
        kS = sb2("r_kS", [128, SEQ], F32)
        lS = sb2("r_lS", [128, SEQ], F32)
        dd = sb2("r_dd", [128, SEQ], F32)
        rS_r, kS_r, lS_r, dd_r = Res(), Res(), Res(), Res()
        for (row0, t, t_r, mu) in ((R_R, rS, rS_r, C_MU_R), (R_K, kS, kS_r, C_MU_K), (R_V, vS, vS_r, C_MU_V),
                                   (R_L, lS, lS_r, C_MU_L)):
            bd.dma("sp", t[:], pT[row0:row0 + 128, :], writes=[t_r])
            bd.op("dve", lambda e, t=t: e.tensor_tensor(out=dd[:, 1:SEQ], in0=t[:, 0:SEQ - 1], in1=t[:, 1:SEQ],
                                                        op=ALU.subtract), reads=[t_r], writes=[dd_r])
            bd.op("dve", lambda e, t=t: e.tensor_scalar(out=dd[:, 0:1], in0=t[:, 0:1], scalar1=-1.0, scalar2=None,
                                                        op0=ALU.mult), reads=[t_r], writes=[dd_r])
            bd.op("dve", lambda e, t=t, mu=mu: e.scalar_tensor_tensor(out=t[:], in0=dd[:], scalar=cst[:, mu:mu + 1],
                                                                      in1=t[:], op0=ALU.mult, op1=ALU.add),
                  reads=[dd_r, t_r, cst_r], writes=[t_r])
        names = ["th", "sg", "sgm", "wd", "aT", "kkr", "sq", "nrm", "nkk", "bb", "t1", "km", "prod"]
        T_ = {n: sb2("r_" + n, [128, 512], F32) for n in names}
        T_r = {n: Res() for n in names}
        stg = [sb2(f"r_stg{i}", [128, 5, 128], F32) for i in range(2)]
        stg_r = [Res(), Res()]
        bi = [0]

        def nb():
            b = bi[0] % 7
            bi[0] += 1
            return banks[b], banks_r[b]

        def A(fn, reads, writes):
            bd.op("act", fn, reads=reads, writes=writes)

        def V(fn, reads, writes):
            bd.op("dve", fn, reads=reads, writes=writes)

        ti = 0
        for tb in range(SEQ // 512):
            sl = slice(tb * 512, (tb + 1) * 512)
            A(lambda e: e.activation(out=T_["th"][:], in_=lS[:, sl], func=AF.Tanh), [lS_r], [T_r["th"]])
            A(lambda e: e.activation(out=T_["sg"][:], in_=lS[:, sl], func=AF.Sigmoid), [lS_r], [T_r["sg"]])
            pw, pw_r = nb()
            bd.op("pe", lambda e: e.matmul(pw[:, :], lhsT=wl[:, 0, :], rhs=T_["th"][:], start=True, stop=True),
                  reads=[wl_r, T_r["th"]], writes=[pw_r])
            A(lambda e: e.activation(out=T_["sgm"][:], in_=pw[:, :], func=AF.Sigmoid, bias=cst[:, C_W0:C_W0 + 1]),
              [pw_r, cst_r], [T_r["sgm"]])
            A(lambda e: e.activation(out=T_["wd"][:], in_=T_["sgm"][:], func=AF.Exp, scale=-float(np.exp(-0.5))),
              [T_r["sgm"]], [T_r["wd"]])
            pa, pa_r = nb()
            bd.op("pe", lambda e: e.matmul(pa[:, :], lhsT=wl[:, 1, :], rhs=lS[:, sl], start=True, stop=True),
                  reads=[wl_r, lS_r], writes=[pa_r])
            A(lambda e: e.activation(out=T_["aT"][:], in_=pa[:, :], func=AF.Sigmoid, bias=cst[:, C_A0:C_A0 + 1]),
              [pa_r, cst_r], [T_r["aT"]])
            pg, pg_r = nb()
            bd.op("pe", lambda e: e.matmul(pg[:, :], lhsT=wl[:, 2, :], rhs=T_["sg"][:], start=True, stop=True),
                  reads=[wl_r, T_r["sg"]], writes=[pg_r])
            A(lambda e: e.activation(out=gS[:, sl], in_=pg[:, :], func=AF.Copy), [pg_r], [gS_r])
            V(lambda e: e.tensor_scalar(out=T_["kkr"][:], in0=kS[:, sl], scalar1=cst[:, C_KK:C_KK + 1], scalar2=None,
                                        op0=ALU.mult), [kS_r, cst_r], [T_r["kkr"]])
            A(lambda e: e.activation(out=T_["sq"][:], in_=T_["kkr"][:], func=AF.Square), [T_r["kkr"]], [T_r["sq"]])
            pn, pn_r = nb()
            bd.op("pe", lambda e: e.matmul(pn[:, :], lhsT=bones, rhs=T_["sq"][:], start=True, stop=True),
                  reads=[mats_r, T_r["sq"]], writes=[pn_r])
            A(lambda e: e.activation(out=T_["nrm"][:], in_=pn[:, :], func=AF.Sqrt), [pn_r], [T_r["nrm"]])
            V(lambda e: e.tensor_scalar(out=T_["nrm"][:], in0=T_["nrm"][:], scalar1=1e-12, scalar2=None, op0=ALU.max),
              [T_r["nrm"]], [T_r["nrm"]])
            V(lambda e: e.reciprocal(out=T_["nrm"][:], in_=T_["nrm"][:]), [T_r["nrm"]], [T_r["nrm"]])
            V(lambda e: e.scalar_tensor_tensor(out=T_["nkk"][:], in0=T_["kkr"][:], scalar=-1.0, in1=T_["nrm"][:],
                                               op0=ALU.mult, op1=ALU.mult), [T_r["kkr"], T_r["nrm"]], [T_r["nkk"]])
            V(lambda e: e.scalar_tensor_tensor(out=T_["bb"][:], in0=T_["nkk"][:], scalar=-1.0, in1=T_["aT"][:],
                                               op0=ALU.mult, op1=ALU.mult), [T_r["nkk"], T_r["aT"]], [T_r["bb"]])
            V(lambda e: e.tensor_scalar(out=T_["t1"][:], in0=T_["aT"][:], scalar1=cst[:, C_KA:C_KA + 1],
                                        scalar2=c2[:, 0:1], op0=ALU.mult, op1=ALU.add),
              [T_r["aT"], cst_r, c2_r], [T_r["t1"]])
            V(lambda e: e.tensor_tensor(out=T_["km"][:], in0=kS[:, sl], in1=T_["t1"][:], op=ALU.mult),
              [kS_r, T_r["t1"]], [T_r["km"]])
            V(lambda e: e.scalar_tensor_tensor(out=T_["prod"][:], in0=rS[:, sl], scalar=cst[:, C_RK:C_RK + 1],
                                               in1=T_["km"][:], op0=ALU.mult, op1=ALU.mult),
              [rS_r, cst_r, T_r["km"]], [T_r["prod"]])
            pb_, pb_r = nb()
            bd.op("pe", lambda e: e.matmul(pb_[:, :], lhsT=bones, rhs=T_["prod"][:], start=True, stop=True),
                  reads=[mats_r, T_r["prod"]], writes=[pb_r])
            V(lambda e: e.tensor_tensor(out=boS[:, sl], in0=pb_[:, :], in1=vS[:, sl], op=ALU.mult),
              [pb_r, vS_r], [boS_r])
            for j in range(4):
                t0 = tb * 512 + j * 128
                px, px_r = nb()
                py, py_r = nb()
                srcs = [(T_["nkk"][:, j * 128:(j + 1) * 128], T_r["nkk"]), (T_["wd"][:, j * 128:(j + 1) * 128], T_r["wd"]),
                        (T_["bb"][:, j * 128:(j + 1) * 128], T_r["bb"]), (T_["km"][:, j * 128:(j + 1) * 128], T_r["km"]),
                        (rS[:, t0:t0 + 128], rS_r)]
                for q, (ap_, r_) in enumerate(srcs):
                    if q < 4:
                        bd.op("pe", lambda e, q=q, ap_=ap_: e.transpose(out=px[:, q * 128:(q + 1) * 128], in_=ap_,
                                                                       identity=identf), reads=[r_, mats_r], writes=[px_r])
                    else:
                        bd.op("pe", lambda e, ap_=ap_: e.transpose(out=py[:, 0:128], in_=ap_, identity=identf),
                              reads=[r_, mats_r], writes=[py_r])
                sg_, sg_r = stg[ti % 2], stg_r[ti % 2]
                ti += 1
                A(lambda e, sg_=sg_: e.activation(out=sg_[:, 0:4, :], in_=px[:, :].rearrange("p (q c) -> p q c", q=4),
                                                  func=AF.Copy), [px_r], [sg_r])
                V(lambda e, sg_=sg_: e.tensor_copy(out=sg_[:, 4, :], in_=py[:, 0:128]), [py_r], [sg_r])
                for h in range(2):
                    dst = bass.AP(scr, h * SEQ * 320 + t0 * 320, [[320, 128], [64, 5], [1, 64]])
                    bd.dma("sp" if h == 0 else "act", dst, sg_[:, :, h * 64:(h + 1) * 64], reads=[sg_r], writes=[scr_r])
        bd.barrier()
    with ExitStack() as es3:
        sb3 = lambda nm, shape, dt: es3.enter_context(nc.sbuf_tensor(_u(nm), shape, dt))
        T = RW_T
        NB = 3
        BC = [sb3(f"r_BC{i}", [128, T, 5, 64], F32) for i in range(NB)]
        BC_r = [Res() for _ in range(NB)]
        S = sb3("r_S", [128, 64], F32)
        junk = sb3("r_junk", [128, 64], F32)
        sa = sb3("r_sa", [128, 1], F32)
        S_r, junk_r, sa_r = Res(), Res(), Res()
        bd.op("dve", lambda e: e.memset(S[:], 0.0), writes=[S_r])
        nchunk = SEQ // T

        def load(ci):
            b = ci % NB
            for h in range(2):
                src = bass.AP(scr, h * SEQ * 320 + ci * T * 320, [[0, 64], [1, T * 320]])
                bd.dma("sp" if h == 0 else "act", BC[b][h * 64:(h + 1) * 64, :, :, :].rearrange("p t q j -> p (t q j)"),
                       src, reads=[scr_r], writes=[BC_r[b]])

        load(0)
        load(1)
        for ci in range(nchunk):
            if ci + 2 < nchunk:
                load(ci + 2)
            b = ci % NB
            bc, bc_r = BC[b], BC_r[b]
            for tt in range(T):
                t = ci * T + tt
                bd.op("dve", lambda e, bc=bc, tt=tt: e.scalar_tensor_tensor(
                    out=junk[:], in0=S[:], scalar=1.0, in1=bc[:, tt, 0, :], op0=ALU.mult, op1=ALU.mult,
                    accum_out=sa[:, 0:1]), reads=[S_r, bc_r], writes=[junk_r, sa_r])
                bd.op("dve", lambda e, bc=bc, tt=tt: e.tensor_tensor(out=S[:], in0=S[:], in1=bc[:, tt, 1, :], op=ALU.mult),
                      reads=[S_r, bc_r], writes=[S_r])
                bd.op("dve", lambda e, bc=bc, tt=tt: e.scalar_tensor_tensor(
                    out=S[:], in0=bc[:, tt, 2, :], scalar=sa[:, 0:1], in1=S[:], op0=ALU.mult, op1=ALU.add),
                    reads=[S_r, bc_r, sa_r], writes=[S_r])
                bd.op("dve", lambda e, bc=bc, tt=tt, t=t: e.scalar_tensor_tensor(
                    out=S[:], in0=bc[:, tt, 3, :], scalar=vS[:, t:t + 1], in1=S[:], op0=ALU.mult, op1=ALU.add),
                    reads=[S_r, bc_r, vS_r], writes=[S_r])
                bd.op("dve", lambda e, bc=bc, tt=tt, t=t: e.scalar_tensor_tensor(
                    out=junk[:], in0=S[:], scalar=1.0, in1=bc[:, tt, 4, :], op0=ALU.mult, op1=ALU.mult,
                    accum_out=yS[:, t:t + 1]), reads=[S_r, bc_r], writes=[junk_r, yS_r])
        bd.barrier()
    with ExitStack() as es4:
        sb4 = lambda nm, shape, dt: es4.enter_context(nc.sbuf_tensor(_u(nm), shape, dt))
        yc = sb4("r_yc", [128, 512], F32)
        sq = sb4("r_sq2", [128, 512], F32)
        rs = sb4("r_rs", [128, 512], F32)
        yo = [sb4(f"r_yo{i}", [128, 512], BF16) for i in range(2)]
        yc_r, sq_r, rs_r = Res(), Res(), Res()
        yo_r = [Res(), Res()]
        for tb in range(SEQ // 512):
            sl = slice(tb * 512, (tb + 1) * 512)
            pm, pm_r = banks[tb % 2], banks_r[tb % 2]
            pv, pv_r = banks[2 + tb % 2], banks_r[2 + tb % 2]
            bd.op("pe", lambda e: e.matmul(pm[:, :], lhsT=bones, rhs=yS[:, sl], start=True, stop=True),
                  reads=[mats_r, yS_r], writes=[pm_r])
            bd.op("dve", lambda e: e.scalar_tensor_tensor(out=yc[:], in0=pm[:, :], scalar=-1.0 / 64, in1=yS[:, sl],
                                                          op0=ALU.mult, op1=ALU.add), reads=[pm_r, yS_r], writes=[yc_r])
            bd.op("act", lambda e: e.activation(out=sq[:], in_=yc[:], func=AF.Square), reads=[yc_r], writes=[sq_r])
            bd.op("pe", lambda e: e.matmul(pv[:, :], lhsT=bones, rhs=sq[:], start=True, stop=True),
                  reads=[mats_r, sq_r], writes=[pv_r])
            bd.op("act", lambda e: e.activation(out=rs[:], in_=pv[:, :], func=AF.Sqrt, scale=1.0 / 64,
                                                bias=cst[:, C_LNEPS:C_LNEPS + 1]), reads=[pv_r, cst_r], writes=[rs_r])
            bd.op("dve", lambda e: e.reciprocal(out=rs[:], in_=rs[:]), reads=[rs_r], writes=[rs_r])
            bd.op("dve", lambda e: e.tensor_tensor(out=yc[:], in0=yc[:], in1=rs[:], op=ALU.mult),
                  reads=[yc_r, rs_r], writes=[yc_r])
            bd.op("dve", lambda e: e.tensor_scalar(out=yc[:], in0=yc[:], scalar1=cst[:, C_LNW:C_LNW + 1],
                                                   scalar2=cst[:, C_LNB:C_LNB + 1], op0=ALU.mult, op1=ALU.add),
                  reads=[yc_r, cst_r], writes=[yc_r])
            bd.op("dve", lambda e: e.tensor_tensor(out=yc[:], in0=yc[:], in1=boS[:, sl], op=ALU.add),
                  reads=[yc_r, boS_r], writes=[yc_r])
            o, o_r = yo[tb % 2], yo_r[tb % 2]
            bd.op("dve", lambda e, o=o: e.tensor_tensor(out=o[:], in0=yc[:], in1=gS[:, sl], op=ALU.mult),
                  reads=[yc_r, gS_r], writes=[o_r])
            bd.dma("sp", yT[256:384, sl], o[:], reads=[o_r])
        bd.barrier()


RWKV_CHUNKED = True
import os as _os
_DBG_STAGE = int(_os.environ.get('RW_DBG', '0'))


def rwkv_masks():
    s_ = (np.arange(128) % 64)[:, None]
    t_ = (np.arange(512) % 64)[None, :]
    m = np.zeros((128, 5, 512), np.float32)
    m[:, 0, :] = (s_ < t_)
    m[:, 1, :] = (s_ <= t_)
    m[:, 2, :] = (s_ > t_)
    m[:, 3, :] = (s_ == t_)
    m[:, 4, :] = np.broadcast_to(t_ != 0, (128, 512))
    return m


def emit_rwkv_chunked(bd, nc, es, D_, cst, cst_r, K):
    sb = lambda nm, shape, dt: es.enter_context(nc.sbuf_tensor(_u(nm), shape, dt))
    pT = D_["pT"].ap()
    yT = D_["yT"].ap()
    banks, banks_r = K["banks"], K["banks_r"]
    bones, identf, mats_r = K["bones"], K["identf"], K["mats_r"]
    R_R, R_K, R_V, R_L = 640, 768, 896, 1024
    rS = sb("r_rS", [128, SEQ], F32)
    kS = sb("r_kS", [128, SEQ], F32)
    vS = sb("r_vS", [128, SEQ], F32)
    lS = sb("r_lS", [128, SEQ], F32)
    rS_r, kS_r, vS_r, lS_r = Res(), Res(), Res(), Res()
    wl = sb("r_wl", [128, 3, 128], F32)
    wl_r = Res()
    bd.dma("sp", wl[:], D_["wl"].ap(), writes=[wl_r])
    msk = sb("r_msk", [128, 5, 512], F32)
    msk_r = Res()
    bd.dma("act", msk[:], D_["rwm"].ap(), writes=[msk_r])
    c2 = sb("r_c2", [128, 2], F32)
    c2_r = Res()
    bd.op("dve", lambda e: e.tensor_scalar(out=c2[:, 0:1], in0=cst[:, C_KA:C_KA + 1], scalar1=-1.0, scalar2=1.0,
                                           op0=ALU.mult, op1=ALU.add), reads=[cst_r], writes=[c2_r])
    with ExitStack() as es2:
        sb2 = lambda nm, shape, dt: es2.enter_context(nc.sbuf_tensor(_u(nm), shape, dt))
        dd = sb2("r_dd", [128, SEQ], F32)
        dd_r = Res()
        for (row0, t, t_r, mu) in ((R_R, rS, rS_r, C_MU_R), (R_K, kS, kS_r, C_MU_K), (R_V, vS, vS_r, C_MU_V),
                                   (R_L, lS, lS_r, C_MU_L)):
            bd.dma("sp", t[:], pT[row0:row0 + 128, :], writes=[t_r])
            bd.op("dve", lambda e, t=t: e.tensor_tensor(out=dd[:, 1:SEQ], in0=t[:, 0:SEQ - 1], in1=t[:, 1:SEQ],
                                                        op=ALU.subtract), reads=[t_r], writes=[dd_r])
            bd.op("dve", lambda e, t=t: e.tensor_scalar(out=dd[:, 0:1], in0=t[:, 0:1], scalar1=-1.0, scalar2=None,
                                                        op0=ALU.mult), reads=[t_r], writes=[dd_r])
            bd.op("dve", lambda e, t=t, mu=mu: e.scalar_tensor_tensor(out=t[:], in0=dd[:], scalar=cst[:, mu:mu + 1],
                                                                      in1=t[:], op0=ALU.mult, op1=ALU.add),
                  reads=[dd_r, t_r, cst_r], writes=[t_r])
        bd.barrier()
    names = ["th", "sg", "sgm", "aT", "kkr", "sq", "nrm", "nkk", "bb", "t1", "km", "prod", "gB", "boB",
             "lw", "cl", "e1", "e2", "e3", "At", "Bt", "Kt", "Rt", "Bh", "Kh",
             "Mab", "Lab", "Mkb", "Nbr", "Nkr", "T", "TT", "Mk0", "Mk1", "Lk0", "Lk1",
             "VT", "BhT", "KhT", "yB", "yc", "sq2", "rs", "Tb", "TTb"]
    BFN = {"At", "Bt", "Kt", "Rt", "Mab", "Lab", "Mkb", "Nbr", "Nkr", "Mk0", "Mk1", "Lk0", "Lk1",
           "VT", "BhT", "KhT", "Tb", "TTb"}
    T_ = {n: sb("r_" + n, [128, 512], BF16 if n in BFN else F32) for n in names}
    T_r = {n: Res() for n in names}
    UT = sb("r_UT", [128, 4, 128], BF16)
    UT_r = Res()
    xts = sb("r_xts", [128, 128], BF16)
    xts_r = Res()
    ST = sb("r_ST", [128, 64], F32)
    ST_r = Res()
    yo = [sb(f"r_yo{i}", [128, 512], BF16) for i in range(2)]
    yo_r = [Res(), Res()]
    STb = sb("r_STb", [128, 64], BF16)
    STb_r = Res()
    bd.op("dve", lambda e: e.memset(ST[:], 0.0), writes=[ST_r])
    bd.op("dve", lambda e: e.memset(STb[:], 0.0), writes=[STb_r])
    bX, bX_r = banks[3], banks_r[3]
    bU, bU_r = banks[4], banks_r[4]
    bS, bS_r = banks[5], banks_r[5]
    bY, bY_r = banks[6], banks_r[6]
    bi = [0]

    def nb():
        b = bi[0] % 3
        bi[0] += 1
        return banks[b], banks_r[b]

    def A(fn, reads, writes):
        bd.op("act", fn, reads=reads, writes=writes)

    def V(fn, reads, writes):
        bd.op("dve", fn, reads=reads, writes=writes)

    def G(fn, reads, writes):
        bd.op("pool", fn, reads=reads, writes=writes)

    def slot(c, h):
        return ((c // 2) * 2 + h) * 64

    def fam(out_name, lname, rname, mask_k):
        pb_, pb_r = nb()
        for c in range(8):
            P0 = (c % 2) * 64
            for h in range(2):
                H0 = h * 64
                o = slot(c, h)
                bd.op("pe", lambda e, P0=P0, H0=H0, o=o, c=c: e.matmul(
                    pb_[P0:P0 + 64, o:o + 64], lhsT=T_[lname][H0:H0 + 64, c * 64:(c + 1) * 64],
                    rhs=T_[rname][H0:H0 + 64, c * 64:(c + 1) * 64], start=True, stop=True),
                    reads=[T_r[lname], T_r[rname]], writes=[pb_r], ser=True)
        V(lambda e: e.tensor_tensor(out=T_[out_name][:], in0=pb_[:, :], in1=msk[:, mask_k, :], op=ALU.mult),
          [pb_r, msk_r], [T_r[out_name]])

    def sq16(out_name, lname, rname, add_name=None):
        pb_, pb_r = nb()
        for c in range(8):
            P0 = (c % 2) * 64
            for h in range(2):
                o = slot(c, h)
                bd.op("pe", lambda e, P0=P0, o=o: e.matmul(
                    pb_[P0:P0 + 64, o:o + 64], lhsT=T_[lname][P0:P0 + 64, o:o + 64],
                    rhs=T_[rname][P0:P0 + 64, o:o + 64], start=True, stop=True),
                    reads=[T_r[lname], T_r[rname]], writes=[pb_r], ser=True)
        if add_name is None:
            A(lambda e: e.activation(out=T_[out_name][:], in_=pb_[:, :], func=AF.Copy), [pb_r], [T_r[out_name]])
        else:
            V(lambda e: e.tensor_tensor(out=T_[out_name][:], in0=pb_[:, :], in1=T_[add_name][:], op=ALU.add),
              [pb_r, T_r[add_name]], [T_r[out_name]])

    def tr4(out_name, src_ap_fn, src_res):
        pb_, pb_r = nb()
        for c2_ in range(4):
            bd.op("pe", lambda e, c2_=c2_: e.transpose(out=pb_[:, c2_ * 128:(c2_ + 1) * 128], in_=src_ap_fn(c2_),
                                                       identity=identf), reads=[src_res, mats_r], writes=[pb_r], ser=True)
        A(lambda e: e.activation(out=T_[out_name][:], in_=pb_[:, :], func=AF.Copy), [pb_r], [T_r[out_name]])

    for tb in range(SEQ // 512):
        sl = slice(tb * 512, (tb + 1) * 512)
        A(lambda e: e.activation(out=T_["th"][:], in_=lS[:, sl], func=AF.Tanh), [lS_r], [T_r["th"]])
        A(lambda e: e.activation(out=T_["sg"][:], in_=lS[:, sl], func=AF.Sigmoid), [lS_r], [T_r["sg"]])
        pw, pw_r = nb()
        bd.op("pe", lambda e: e.matmul(pw[:, :], lhsT=wl[:, 0, :], rhs=T_["th"][:], start=True, stop=True),
              reads=[wl_r, T_r["th"]], writes=[pw_r])
        A(lambda e: e.activation(out=T_["sgm"][:], in_=pw[:, :], func=AF.Sigmoid, bias=cst[:, C_W0:C_W0 + 1]),
          [pw_r, cst_r], [T_r["sgm"]])
        V(lambda e: e.tensor_scalar(out=T_["lw"][:], in0=T_["sgm"][:], scalar1=-float(np.exp(-0.5)), scalar2=None,
                                    op0=ALU.mult), [T_r["sgm"]], [T_r["lw"]])
        pa, pa_r = nb()
        bd.op("pe", lambda e: e.matmul(pa[:, :], lhsT=wl[:, 1, :], rhs=lS[:, sl], start=True, stop=True),
              reads=[wl_r, lS_r], writes=[pa_r])
        A(lambda e: e.activation(out=T_["aT"][:], in_=pa[:, :], func=AF.Sigmoid, bias=cst[:, C_A0:C_A0 + 1]),
          [pa_r, cst_r], [T_r["aT"]])
        pg, pg_r = nb()
        bd.op("pe", lambda e: e.matmul(pg[:, :], lhsT=wl[:, 2, :], rhs=T_["sg"][:], start=True, stop=True),
              reads=[wl_r, T_r["sg"]], writes=[pg_r])
        A(lambda e: e.activation(out=T_["gB"][:], in_=pg[:, :], func=AF.Copy), [pg_r], [T_r["gB"]])
        V(lambda e: e.tensor_scalar(out=T_["kkr"][:], in0=kS[:, sl], scalar1=cst[:, C_KK:C_KK + 1], scalar2=None,
                                    op0=ALU.mult), [kS_r, cst_r], [T_r["kkr"]])
        A(lambda e: e.activation(out=T_["sq"][:], in_=T_["kkr"][:], func=AF.Square), [T_r["kkr"]], [T_r["sq"]])
        pn, pn_r = nb()
        bd.op("pe", lambda e: e.matmul(pn[:, :], lhsT=bones, rhs=T_["sq"][:], start=True, stop=True),
              reads=[mats_r, T_r["sq"]], writes=[pn_r])
        A(lambda e: e.activation(out=T_["nrm"][:], in_=pn[:, :], func=AF.Sqrt), [pn_r], [T_r["nrm"]])
        V(lambda e: e.tensor_scalar(out=T_["nrm"][:], in0=T_["nrm"][:], scalar1=1e-12, scalar2=None, op0=ALU.max),
          [T_r["nrm"]], [T_r["nrm"]])
        V(lambda e: e.reciprocal(out=T_["nrm"][:], in_=T_["nrm"][:]), [T_r["nrm"]], [T_r["nrm"]])
        V(lambda e: e.scalar_tensor_tensor(out=T_["nkk"][:], in0=T_["kkr"][:], scalar=-1.0, in1=T_["nrm"][:],
                                           op0=ALU.mult, op1=ALU.mult), [T_r["kkr"], T_r["nrm"]], [T_r["nkk"]])
        V(lambda e: e.scalar_tensor_tensor(out=T_["bb"][:], in0=T_["nkk"][:], scalar=-1.0, in1=T_["aT"][:],
                                           op0=ALU.mult, op1=ALU.mult), [T_r["nkk"], T_r["aT"]], [T_r["bb"]])
        V(lambda e: e.tensor_scalar(out=T_["t1"][:], in0=T_["aT"][:], scalar1=cst[:, C_KA:C_KA + 1],
                                    scalar2=c2[:, 0:1], op0=ALU.mult, op1=ALU.add),
          [T_r["aT"], cst_r, c2_r], [T_r["t1"]])
        V(lambda e: e.tensor_tensor(out=T_["km"][:], in0=kS[:, sl], in1=T_["t1"][:], op=ALU.mult),
          [kS_r, T_r["t1"]], [T_r["km"]])
        V(lambda e: e.scalar_tensor_tensor(out=T_["prod"][:], in0=rS[:, sl], scalar=cst[:, C_RK:C_RK + 1],
                                           in1=T_["km"][:], op0=ALU.mult, op1=ALU.mult),
          [rS_r, cst_r, T_r["km"]], [T_r["prod"]])
        pb2, pb2_r = nb()
        bd.op("pe", lambda e: e.matmul(pb2[:, :], lhsT=bones, rhs=T_["prod"][:], start=True, stop=True),
              reads=[mats_r, T_r["prod"]], writes=[pb2_r])
        V(lambda e: e.tensor_tensor(out=T_["boB"][:], in0=pb2[:, :], in1=vS[:, sl], op=ALU.mult),
          [pb2_r, vS_r], [T_r["boB"]])
        V(lambda e: e.tensor_tensor_scan(out=T_["cl"][:], data0=msk[:, 4, :], data1=T_["lw"][:], initial=0.0,
                                         op0=ALU.mult, op1=ALU.add), [msk_r, T_r["lw"]], [T_r["cl"]])
        A(lambda e: e.activation(out=T_["e1"][:], in_=T_["cl"][:], func=AF.Exp), [T_r["cl"]], [T_r["e1"]])
        A(lambda e: e.activation(out=T_["e2"][:], in_=T_["cl"][:], func=AF.Exp, scale=-1.0), [T_r["cl"]], [T_r["e2"]])
        V(lambda e: e.tensor_tensor(out=T_["e3"][:], in0=T_["cl"][:], in1=T_["lw"][:], op=ALU.subtract),
          [T_r["cl"], T_r["lw"]], [T_r["e3"]])
        A(lambda e: e.activation(out=T_["e3"][:], in_=T_["e3"][:], func=AF.Exp), [T_r["e3"]], [T_r["e3"]])
        V(lambda e: e.tensor_tensor(out=T_["At"][:], in0=T_["nkk"][:], in1=T_["e3"][:], op=ALU.mult),
          [T_r["nkk"], T_r["e3"]], [T_r["At"]])
        G(lambda e: e.tensor_tensor(out=T_["Bt"][:], in0=T_["bb"][:], in1=T_["e2"][:], op=ALU.mult),
          [T_r["bb"], T_r["e2"]], [T_r["Bt"]])
        V(lambda e: e.tensor_tensor(out=T_["Kt"][:], in0=T_["km"][:], in1=T_["e2"][:], op=ALU.mult),
          [T_r["km"], T_r["e2"]], [T_r["Kt"]])
        G(lambda e: e.tensor_tensor(out=T_["Rt"][:], in0=rS[:, sl], in1=T_["e1"][:], op=ALU.mult),
          [rS_r, T_r["e1"]], [T_r["Rt"]])
        for c in range(8):
            gam = T_["e1"][:, c * 64 + 63:c * 64 + 64]
            V(lambda e, c=c, gam=gam: e.tensor_scalar(out=T_["Bh"][:, c * 64:(c + 1) * 64], in0=T_["Bt"][:, c * 64:(c + 1) * 64],
                                                      scalar1=gam, scalar2=None, op0=ALU.mult),
              [T_r["Bt"], T_r["e1"]], [T_r["Bh"]])
            G(lambda e, c=c, gam=gam: e.tensor_scalar(out=T_["Kh"][:, c * 64:(c + 1) * 64], in0=T_["Kt"][:, c * 64:(c + 1) * 64],
                                                      scalar1=gam, scalar2=None, op0=ALU.mult),
              [T_r["Kt"], T_r["e1"]], [T_r["Kh"]])
        if _DBG_STAGE == 1:
            break
        fam("Mab", "Bt", "At", 0)
        fam("Lab", "At", "Bt", 2)
        fam("Mkb", "Kt", "At", 0)
        fam("Nbr", "Bt", "Rt", 1)
        fam("Nkr", "Kt", "Rt", 1)
        if _DBG_STAGE == 2:
            break
        V(lambda e: e.tensor_tensor(out=T_["T"][:], in0=T_["Mab"][:], in1=msk[:, 3, :], op=ALU.add),
          [T_r["Mab"], msk_r], [T_r["T"]])
        G(lambda e: e.tensor_tensor(out=T_["TT"][:], in0=T_["Lab"][:], in1=msk[:, 3, :], op=ALU.add),
          [T_r["Lab"], msk_r], [T_r["TT"]])
        A(lambda e: e.activation(out=T_["Tb"][:], in_=T_["T"][:], func=AF.Copy), [T_r["T"]], [T_r["Tb"]])
        G(lambda e: e.tensor_copy(out=T_["TTb"][:], in_=T_["TT"][:]), [T_r["TT"]], [T_r["TTb"]])
        Mp, Lp = "Mab", "Lab"
        for lev in range(1, 6):
            Mn, Ln = f"Mk{lev % 2}", f"Lk{lev % 2}"
            sq16(Mn, Lp, Mp)
            if lev < 5:
                sq16(Ln, Mp, Lp)
            sq16("T", "TTb", Mn, add_name="T")
            if lev < 5:
                sq16("TT", Mn, "TTb", add_name="TT")
                G(lambda e: e.tensor_copy(out=T_["TTb"][:], in_=T_["TT"][:]), [T_r["TT"]], [T_r["TTb"]])
            A(lambda e: e.activation(out=T_["Tb"][:], in_=T_["T"][:], func=AF.Copy), [T_r["T"]], [T_r["Tb"]])
            Mp, Lp = Mn, Ln
        if _DBG_STAGE == 3:
            break
        tr4("VT", lambda c2_: vS[:, tb * 512 + c2_ * 128: tb * 512 + (c2_ + 1) * 128], vS_r)
        tr4("BhT", lambda c2_: T_["Bh"][:, c2_ * 128:(c2_ + 1) * 128], T_r["Bh"])
        tr4("KhT", lambda c2_: T_["Kh"][:, c2_ * 128:(c2_ + 1) * 128], T_r["Kh"])
        if _DBG_STAGE == 4:
            break
        for c in range(8):
            P0 = (c % 2) * 64
            c2_ = c // 2
            cs = slice(c * 64, (c + 1) * 64)
            for h in range(2):
                H0 = h * 64
                o = slot(c, h)
                vt = T_["VT"][P0:P0 + 64, c2_ * 128 + H0:c2_ * 128 + H0 + 64]
                bd.op("pe", lambda e, P0=P0, H0=H0, cs=cs: e.matmul(
                    bX[P0:P0 + 64, H0:H0 + 64], lhsT=T_["At"][H0:H0 + 64, cs], rhs=STb[H0:H0 + 64, :],
                    start=True, stop=False), reads=[T_r["At"], STb_r], writes=[bX_r], ser=True)
                bd.op("pe", lambda e, P0=P0, H0=H0, o=o, vt=vt: e.matmul(
                    bX[P0:P0 + 64, H0:H0 + 64], lhsT=T_["Mkb"][P0:P0 + 64, o:o + 64], rhs=vt,
                    start=False, stop=True), reads=[T_r["Mkb"], T_r["VT"]], writes=[bX_r], ser=True)
            A(lambda e, P0=P0: e.activation(out=xts[P0:P0 + 64, :], in_=bX[P0:P0 + 64, 0:128], func=AF.Copy),
              [bX_r], [xts_r])
            for h in range(2):
                H0 = h * 64
                o = slot(c, h)
                bd.op("pe", lambda e, P0=P0, H0=H0, o=o: e.matmul(
                    bU[P0:P0 + 64, H0:H0 + 64], lhsT=T_["Tb"][P0:P0 + 64, o:o + 64], rhs=xts[P0:P0 + 64, H0:H0 + 64],
                    start=True, stop=True), reads=[T_r["Tb"], xts_r], writes=[bU_r], ser=True)
            V(lambda e, P0=P0, c2_=c2_: e.tensor_copy(out=UT[P0:P0 + 64, c2_, :], in_=bU[P0:P0 + 64, 0:128]),
              [bU_r], [UT_r])
            for h in range(2):
                H0 = h * 64
                o = slot(c, h)
                ut = UT[P0:P0 + 64, c2_, H0:H0 + 64]
                vt = T_["VT"][P0:P0 + 64, c2_ * 128 + H0:c2_ * 128 + H0 + 64]
                bd.op("pe", lambda e, H0=H0, cs=cs: e.matmul(
                    bY[H0:H0 + 64, cs], lhsT=STb[H0:H0 + 64, :], rhs=T_["Rt"][H0:H0 + 64, cs],
                    start=True, stop=False), reads=[STb_r, T_r["Rt"]], writes=[bY_r], ser=True)
                bd.op("pe", lambda e, H0=H0, cs=cs, P0=P0, o=o, ut=ut: e.matmul(
                    bY[H0:H0 + 64, cs], lhsT=ut, rhs=T_["Nbr"][P0:P0 + 64, o:o + 64],
                    start=False, stop=False), reads=[UT_r, T_r["Nbr"]], writes=[bY_r], ser=True)
                bd.op("pe", lambda e, H0=H0, cs=cs, P0=P0, o=o, vt=vt: e.matmul(
                    bY[H0:H0 + 64, cs], lhsT=vt, rhs=T_["Nkr"][P0:P0 + 64, o:o + 64],
                    start=False, stop=True), reads=[T_r["VT"], T_r["Nkr"]], writes=[bY_r], ser=True)
                bd.op("pe", lambda e, H0=H0, P0=P0, c2_=c2_, ut=ut: e.matmul(
                    bS[H0:H0 + 64, 0:64], lhsT=T_["BhT"][P0:P0 + 64, c2_ * 128 + H0:c2_ * 128 + H0 + 64], rhs=ut,
                    start=True, stop=False), reads=[T_r["BhT"], UT_r], writes=[bS_r], ser=True)
                bd.op("pe", lambda e, H0=H0, P0=P0, c2_=c2_, vt=vt: e.matmul(
                    bS[H0:H0 + 64, 0:64], lhsT=T_["KhT"][P0:P0 + 64, c2_ * 128 + H0:c2_ * 128 + H0 + 64], rhs=vt,
                    start=False, stop=True), reads=[T_r["KhT"], T_r["VT"]], writes=[bS_r], ser=True)
            gam = T_["e1"][:, c * 64 + 63:c * 64 + 64]
            V(lambda e, gam=gam: e.scalar_tensor_tensor(out=ST[:], in0=ST[:], scalar=gam, in1=bS[:, 0:64],
                                                        op0=ALU.mult, op1=ALU.add),
              [ST_r, bS_r, T_r["e1"]], [ST_r])
            A(lambda e: e.activation(out=STb[:], in_=ST[:], func=AF.Copy), [ST_r], [STb_r])
        if _DBG_STAGE == 5:
            break
        A(lambda e: e.activation(out=T_["yB"][:], in_=bY[:, :], func=AF.Copy), [bY_r], [T_r["yB"]])
        pm, pm_r = nb()
        bd.op("pe", lambda e: e.matmul(pm[:, :], lhsT=bones, rhs=T_["yB"][:], start=True, stop=True),
              reads=[mats_r, T_r["yB"]], writes=[pm_r])
        V(lambda e: e.scalar_tensor_tensor(out=T_["yc"][:], in0=pm[:, :], scalar=-1.0 / 64, in1=T_["yB"][:],
                                           op0=ALU.mult, op1=ALU.add), [pm_r, T_r["yB"]], [T_r["yc"]])
        A(lambda e: e.activation(out=T_["sq2"][:], in_=T_["yc"][:], func=AF.Square), [T_r["yc"]], [T_r["sq2"]])
        pv, pv_r = nb()
        bd.op("pe", lambda e: e.matmul(pv[:, :], lhsT=bones, rhs=T_["sq2"][:], start=True, stop=True),
              reads=[mats_r, T_r["sq2"]], writes=[pv_r])
        A(lambda e: e.activation(out=T_["rs"][:], in_=pv[:, :], func=AF.Sqrt, scale=1.0 / 64,
                                 bias=cst[:, C_LNEPS:C_LNEPS + 1]), [pv_r, cst_r], [T_r["rs"]])
        V(lambda e: e.reciprocal(out=T_["rs"][:], in_=T_["rs"][:]), [T_r["rs"]], [T_r["rs"]])
        V(lambda e: e.tensor_tensor(out=T_["yc"][:], in0=T_["yc"][:], in1=T_["rs"][:], op=ALU.mult),
          [T_r["yc"], T_r["rs"]], [T_r["yc"]])
        V(lambda e: e.tensor_scalar(out=T_["yc"][:], in0=T_["yc"][:], scalar1=cst[:, C_LNW:C_LNW + 1],
                                    scalar2=cst[:, C_LNB:C_LNB + 1], op0=ALU.mult, op1=ALU.add),
          [T_r["yc"], cst_r], [T_r["yc"]])
        V(lambda e: e.tensor_tensor(out=T_["yc"][:], in0=T_["yc"][:], in1=T_["boB"][:], op=ALU.add),
          [T_r["yc"], T_r["boB"]], [T_r["yc"]])
        o_, o_r = yo[tb % 2], yo_r[tb % 2]
        V(lambda e, o_=o_: e.tensor_tensor(out=o_[:], in0=T_["yc"][:], in1=T_["gB"][:], op=ALU.mult),
          [T_r["yc"], T_r["gB"]], [o_r])
        bd.dma("sp", yT[256:384, sl], o_[:], reads=[o_r])
    bd.barrier()


def emit_phaseB(nc, mkbld, es, D_, mixers, tag):
    bd = mkbld(tag + "m")
    sb = lambda nm, shape, dt: es.enter_context(nc.sbuf_tensor(_u(nm), shape, dt))
    ps = lambda nm, shape, dt: es.enter_context(nc.psum_tensor(_u(nm), shape, dt))
    cst = sb("B_cst", [128, NCST], F32)
    cst_r = Res()
    bd.dma("sp", cst[:], D_["cst"].ap(), writes=[cst_r])
    mats = sb("B_mats", [128, 4, 128], F32)
    mats_r = Res()
    bd.dma("sp", mats[:], D_["mats"].ap(), writes=[mats_r])
    identb = sb("B_identb", [128, 128], BF16)
    maskb = sb("B_maskb", [128, 128], BF16)
    identb_r, maskb_r = Res(), Res()
    bd.op("dve", lambda e: e.tensor_copy(out=identb[:], in_=mats[:, 0, :]), reads=[mats_r], writes=[identb_r])
    bd.op("dve", lambda e: e.tensor_copy(out=maskb[:], in_=mats[:, 3, :]), reads=[mats_r], writes=[maskb_r])
    banks = [ps(f"B_bank{i}", [128, 512], F32) for i in range(7)]
    banks_r = [Res() for _ in range(7)]
    ptb = ps("B_ptb", [128, 1024], BF16)
    ptb_r = Res()
    K = dict(banks=banks, banks_r=banks_r, ptb=ptb, ptb_r=ptb_r, identb=identb, identb_r=identb_r,
             mask=maskb, mask_r=maskb_r, ones=mats[:, 1, :], ones_r=mats_r, identf=mats[:, 0, :],
             bones=mats[:, 2, :], mats_r=mats_r)
    if "mla" in mixers:
        with ExitStack() as es1:
            emit_mla(bd, nc, es1, D_["pT"], D_["pos"], D_["wuq"], D_["wukv"], D_["yT"], cst, cst_r, K)
    bd.barrier()
    if "mlstm" in mixers:
        with ExitStack() as es1:
            emit_mlstm(bd, nc, es1, D_["pT"], D_["yT"], cst, cst_r, K)
        bd.barrier()
    if "rwkv" in mixers:
        bd = mkbld(tag + "r")
        with ExitStack() as es1:
            (emit_rwkv_chunked if RWKV_CHUNKED else emit_rwkv)(bd, nc, es1, D_, cst, cst_r, K)
        bd.barrier()


def prep_phaseB_consts(P, l, hc):
    c = np.zeros((128, NCST), np.float32)
    c[:, C_QN0] = P["mla_q_norm"][l][0:128]
    c[:, C_QN1] = P["mla_q_norm"][l][128:256]
    c[:, C_KVN0] = P["mla_kv_norm"][l][0:128]
    c[:, C_KVN1] = P["mla_kv_norm"][l][128:256]
    c[:, C_INVF] = np.tile(INV_FREQ, 4)
    c[:, C_SGN] = np.tile(np.concatenate([np.full(32, -1.0, np.float32), np.full(32, 1.0, np.float32)]), 2)
    for h in range(2):
        hh = 2 * hc + h
        c[:, C_MLAO0 + h] = P["mla_out_norm"][l][hh * 128:(hh + 1) * 128]
    mu = P["rwkv_mu"][l]
    ch = slice(hc * 128, hc * 128 + 128)
    c[:, C_MU_R] = mu[0:256][ch]
    c[:, C_MU_K] = mu[256:512][ch]
    c[:, C_MU_V] = mu[512:768][ch]
    c[:, C_MU_L] = mu[768:896]
    c[:, C_W0] = P["rwkv_w0"][l][ch]
    c[:, C_A0] = P["rwkv_a0"][l][ch]
    c[:, C_KK] = P["rwkv_k_k"][l][ch]
    c[:, C_KA] = P["rwkv_k_a"][l][ch]
    c[:, C_RK] = P["rwkv_r_k"][l][ch]
    c[:, C_LNW] = P["rwkv_ln_w"][l][ch]
    c[:, C_LNB] = P["rwkv_ln_b"][l][ch]
    cw = P["mlstm_conv_w"][l]
    cb = P["mlstm_conv_b"][l]
    qs = slice(hc * 64, hc * 64 + 64)
    ks = slice(128 + hc * 64, 128 + hc * 64 + 64)
    for j in range(4):
        c[0:64, C_CWQ0 + j] = cw[j][qs]
        c[0:64, C_CWK0 + j] = cw[j][ks]
    c[0:64, C_CBQ] = cb[qs]
    c[0:64, C_CBK] = cb[ks]
    c[:, C_IB] = P["mlstm_i_bias"][l][hc * 2]
    c[:, C_FB] = P["mlstm_f_bias"][l][hc * 2]
    c[:, C_IB1] = P["mlstm_i_bias"][l][hc * 2 + 1]
    c[:, C_FB1] = P["mlstm_f_bias"][l][hc * 2 + 1]
    c[:, C_MLO] = P["mlstm_out_norm"][l][ch]
    c[:, C_EPS] = EPS
    c[:, C_LNEPS] = 64e-5
    c[:, C_ONE] = 1.0
    hq = []
    hkv = []
    for h in range(2):
        hh = 2 * hc + h
        wq = P["mla_w_uq"][l][:, hh * 192:(hh + 1) * 192]
        hq += [wq[:, 0:128], wq[:, 128:192], wq[:, 160:192], wq[:, 128:160]]
        wkv = P["mla_w_ukv"][l][:, hh * 256:(hh + 1) * 256]
        hkv += [wkv]
    wuq = np.ascontiguousarray(np.concatenate(hq, axis=1))
    wukv = np.ascontiguousarray(np.concatenate(hkv, axis=1))
    wl = np.zeros((128, 3, 128), np.float32)
    wl[0:32, 0, :] = P["rwkv_w2"][l][:, ch]
    wl[32:64, 1, :] = P["rwkv_a2"][l][:, ch]
    wl[64:128, 2, :] = P["rwkv_g2"][l][:, ch]
    return dict(cst=c, wuq=wuq, wukv=wukv, wl=wl)


INV_FREQ = (10000.0 ** (-np.arange(0, 64, 2, dtype=np.float32) / np.float32(64))).astype(np.float32)


def const_mats():
    m = np.zeros((128, 4, 128), np.float32)
    m[:, 0, :] = np.eye(128, dtype=np.float32)
    m[:, 1, :] = 1.0
    m[0:64, 2, 0:64] = 1.0
    m[64:128, 2, 64:128] = 1.0
    m[:, 3, :] = (np.arange(128)[:, None] <= np.arange(128)[None, :]).astype(np.float32)
    return m


W_OUT_PERM = (list(range(0, 256)) + list(range(512, 640)) + list(range(768, 896)) +
              list(range(256, 512)) + list(range(640, 768)) + list(range(896, 1024)))
TBC = 256


def emit_phaseC(nc, bd, es, D_, final, ntok):
    sb = lambda name, shape, dt: es.enter_context(nc.sbuf_tensor(_u(name), shape, dt))
    ps = lambda name, shape, dt: es.enter_context(nc.psum_tensor(_u(name), shape, dt))
    wo = sb("C_wo", [128, 8, D], BF16)
    wg = sb("C_wg", [128, 8, DFF], BF16)
    wu = sb("C_wu", [128, 8, DFF], BF16)
    wd = sb("C_wd", [128, 22, D], BF16)
    w_r = Res()
    HS = DFF // 2
    stage = [sb(f"C_stage{i}", [128, HS], F32) for i in range(3)]
    stage_r = [Res() for _ in range(3)]
    gcol = sb("C_gcol", [128, 8], F32)
    gcol_r = Res()
    idf = sb("C_idf", [128, 128], F32)
    ident = sb("C_ident", [128, 128], BF16)
    idf_r, ident_r = Res(), Res()
    bd.dma("sp", gcol[:], D_["g"].ap(), writes=[gcol_r])
    bd.dma("sp", idf[:], D_["ident"].ap(), writes=[idf_r])
    bd.op("dve", lambda e: e.tensor_copy(out=ident[:], in_=idf[:]), reads=[idf_r], writes=[ident_r])
    if final:
        gbc = sb("C_gbc", [128, D], F32)
        gbc_r = Res()
        bd.dma("sp", gbc[:], bass.AP(D_["gf_t"], 0, [[0, 128], [1, D]]), writes=[gbc_r])
    si = [0]
    queues = ("sp", "pool", "act")
    engs = ("dve", "pool", "act")

    def cast(dst_ap, src_ap, ncols, scale_ap=None):
        i = si[0] % 3
        si[0] += 1
        bd.dma(queues[i], stage[i][:, 0:ncols], src_ap, writes=[stage_r[i]])
        eng = engs[i] if scale_ap is None else ("dve" if i != 2 else "act")
        if eng == "act":
            if scale_ap is None:
                bd.op("act", lambda e: e.activation(out=dst_ap, in_=stage[i][:, 0:ncols], func=AF.Copy),
                      reads=[stage_r[i]], writes=[w_r])
            else:
                bd.op("act", lambda e: e.activation(out=dst_ap, in_=stage[i][:, 0:ncols], func=AF.Copy, scale=scale_ap),
                      reads=[stage_r[i], gcol_r], writes=[w_r])
        elif scale_ap is None:
            bd.op(eng, lambda e: e.tensor_copy(out=dst_ap, in_=stage[i][:, 0:ncols]), reads=[stage_r[i]], writes=[w_r])
        else:
            bd.op(eng, lambda e: e.tensor_scalar(out=dst_ap, in0=stage[i][:, 0:ncols], scalar1=scale_ap, scalar2=None,
                                                 op0=ALU.mult), reads=[stage_r[i], gcol_r], writes=[w_r])

    for kc in range(8):
        cast(wo[:, kc, :], D_["wo"].ap()[kc * 128:(kc + 1) * 128, :], D)
    for kc in range(8):
        for hh in range(2):
            cast(wg[:, kc, hh * HS:(hh + 1) * HS], D_["wg"].ap()[kc * 128:(kc + 1) * 128, hh * HS:(hh + 1) * HS], HS,
                 gcol[:, kc:kc + 1])
            cast(wu[:, kc, hh * HS:(hh + 1) * HS], D_["wu"].ap()[kc * 128:(kc + 1) * 128, hh * HS:(hh + 1) * HS], HS,
                 gcol[:, kc:kc + 1])
    for f in range(22):
        cast(wd[:, f, :], D_["wd"].ap()[f * 128:(f + 1) * 128, :], D)

    NJ = TBC // 128
    xs = sb("C_xs", [128, NJ, D], F32)
    xs_r = [Res() for _ in range(NJ)]
    yTb = sb("C_yTb", [128, 8, TBC], BF16)
    yTb_r = Res()
    hT = sb("C_hT", [128, 8, TBC], BF16)
    hT_r = Res()
    AT = sb("C_AT", [128, 22, TBC], BF16)
    AT_r = Res()
    sgt = [sb(f"C_sg{i}", [128, TBC], F32) for i in range(2)]
    sgt_r = [Res(), Res()]
    tmp = dict(ss=sb("C_ss", [128, 4], F32), ss_r=Res(), junk=sb("C_junk", [128, D], BF16), junk_r=Res(),
               xn=sb("C_xn", [128, D], BF16), xn_r=Res(),
               pt=ps("C_pt", [128, D], BF16), pt_r=Res(), eps=sb("C_eps", [128, 1], F32), eps_r=Res())
    bd.op("dve", lambda e: e.memset(tmp["eps"][:], EPS), writes=[tmp["eps_r"]])
    pb = [ps(f"C_pb{i}", [128, 512], F32) for i in range(6)]
    pb_r = [Res() for _ in range(6)]
    x_ap = D_["x"].ap()
    yT_ap = D_["yT"].ap()
    out_ap = D_["out"].ap()
    for blk in range(ntok // TBC):
        t0 = blk * TBC
        bd.dma("sp", yTb[:], yT_ap[:, t0:t0 + TBC].rearrange("(k p) t -> p k t", p=128), writes=[yTb_r])
        for j in range(NJ):
            bd.dma("pool", xs[:, j, :], x_ap[t0 + j * 128:t0 + (j + 1) * 128, :], writes=[xs_r[j]])
            for ch in range(2):
                po, po_r = pb[ch], pb_r[ch]
                for k in range(8):
                    bd.op("pe", lambda e, k=k, j=j, ch=ch, po=po: e.matmul(
                        po[:, :], lhsT=yTb[:, k, j * 128:(j + 1) * 128], rhs=wo[:, k, ch * 512:(ch + 1) * 512],
                        start=(k == 0), stop=(k == 7)), reads=[yTb_r, w_r], writes=[po_r])
                bd.op("dve", lambda e, j=j, ch=ch, po=po: e.tensor_tensor(
                    out=xs[:, j, ch * 512:(ch + 1) * 512], in0=xs[:, j, ch * 512:(ch + 1) * 512], in1=po[:, :], op=ALU.add),
                    reads=[po_r, xs_r[j]], writes=[xs_r[j]])
            emit_rmsnorm_T(bd, xs[:, j, :], xs_r[j], hT, hT_r, j, tmp, ident)
        for f in range(22):
            pg, pg_r = pb[2 + (f % 2) * 2], pb_r[2 + (f % 2) * 2]
            pu, pu_r = pb[3 + (f % 2) * 2], pb_r[3 + (f % 2) * 2]
            for (pp, pp_r, ww) in ((pg, pg_r, wg), (pu, pu_r, wu)):
                for k in range(8):
                    bd.op("pe", lambda e, k=k, f=f, pp=pp, ww=ww: e.matmul(
                        pp[:, 0:TBC], lhsT=ww[:, k, f * 128:(f + 1) * 128], rhs=hT[:, k, :],
                        start=(k == 0), stop=(k == 7)), reads=[w_r, hT_r], writes=[pp_r])
            sg, sg_r = sgt[f % 2], sgt_r[f % 2]
            bd.op("act", lambda e, sg=sg, pg=pg: e.activation(out=sg[:], in_=pg[:, 0:TBC], func=AF.Silu),
                  reads=[pg_r], writes=[sg_r])
            bd.op("dve", lambda e, sg=sg, pu=pu, f=f: e.tensor_tensor(out=AT[:, f, :], in0=sg[:], in1=pu[:, 0:TBC],
                                                                      op=ALU.mult),
                  reads=[sg_r, pu_r], writes=[AT_r])
        for j in range(NJ):
            for ch in range(2):
                pd, pd_r = pb[ch], pb_r[ch]
                for f in range(22):
                    bd.op("pe", lambda e, f=f, j=j, ch=ch, pd=pd: e.matmul(
                        pd[:, :], lhsT=AT[:, f, j * 128:(j + 1) * 128], rhs=wd[:, f, ch * 512:(ch + 1) * 512],
                        start=(f == 0), stop=(f == 21)), reads=[AT_r, w_r], writes=[pd_r])
                bd.op("dve", lambda e, j=j, ch=ch, pd=pd: e.tensor_tensor(
                    out=xs[:, j, ch * 512:(ch + 1) * 512], in0=xs[:, j, ch * 512:(ch + 1) * 512], in1=pd[:, :], op=ALU.add),
                    reads=[pd_r, xs_r[j]], writes=[xs_r[j]])
            if final:
                ss, ss_r = tmp["ss"], tmp["ss_r"]
                bd.op("dve", lambda e, j=j: e.scalar_tensor_tensor(
                    out=tmp["junk"][:], in0=xs[:, j, :], scalar=1.0, in1=xs[:, j, :], op0=ALU.mult, op1=ALU.mult,
                    accum_out=ss[:, 0:1]), reads=[xs_r[j]], writes=[tmp["junk_r"], ss_r])
                bd.op("act", lambda e: e.activation(out=ss[:, 1:2], in_=ss[:, 0:1], func=AF.Sqrt, scale=1.0 / D,
                                                    bias=tmp["eps"][:, 0:1]), reads=[ss_r, tmp["eps_r"]], writes=[ss_r])
                bd.op("dve", lambda e: e.reciprocal(out=ss[:, 2:3], in_=ss[:, 1:2]), reads=[ss_r], writes=[ss_r])
                bd.op("dve", lambda e, j=j: e.scalar_tensor_tensor(
                    out=xs[:, j, :], in0=xs[:, j, :], scalar=ss[:, 2:3], in1=gbc[:], op0=ALU.mult, op1=ALU.mult),
                    reads=[xs_r[j], ss_r, gbc_r], writes=[xs_r[j]])
            bd.dma("sp", out_ap[t0 + j * 128:t0 + (j + 1) * 128, :], xs[:, j, :], reads=[xs_r[j]])
    bd.barrier()


def build_fused(phases=None):
    nc = bass.Bass("TRN2", target_bir_lowering=False)
    dt = nc.dram_tensor
    x_d = dt("x", [SEQ, D], F32, kind="ExternalInput")
    pos_d = dt("pos", [1, SEQ], I32, kind="ExternalInput")
    wA_d = dt("wA", [2, D, NCOLA], F32, kind="ExternalInput")
    gA_d = dt("gA", [2, 128, 8], F32, kind="ExternalInput")
    id_d = dt("ident", [128, 128], F32, kind="ExternalInput")
    mats_d = dt("mats", [128, 4, 128], F32, kind="ExternalInput")
    rwm_d = dt("rwm", [128, 5, 512], F32, kind="ExternalInput")
    wuq_d = dt("wuq", [2, 2, 256, 512], F32, kind="ExternalInput")
    wukv_d = dt("wukv", [2, 2, 256, 512], F32, kind="ExternalInput")
    cst_d = dt("cst", [2, 2, 128, NCST], F32, kind="ExternalInput")
    wl_d = dt("wl", [2, 2, 128, 3, 128], F32, kind="ExternalInput")
    wo_d = dt("wo", [2, D, D], F32, kind="ExternalInput")
    wg_d = dt("wg", [2, D, DFF], F32, kind="ExternalInput")
    wu_d = dt("wu", [2, D, DFF], F32, kind="ExternalInput")
    wd_d = dt("wd", [2, DFF, D], F32, kind="ExternalInput")
    gC_d = dt("gC", [2, 128, 8], F32, kind="ExternalInput")
    gf_d = dt("gf", [1, D], F32, kind="ExternalInput")
    out_d = dt("out", [SEQ, D], F32, kind="ExternalOutput")
    pT_d = dt("pT_scr", [NCOLA, SEQ], F32)
    yT_d = dt("yT_scr", [D, SEQ], BF16)
    scr_d = dt("rw_scr", [2 * SEQ * 320], F32)
    x1_d = dt("x1_scr", [SEQ, D], F32)
    shared = {}
    mkbld = lambda tag: Bld(nc, tag, shared)
    for l in range(2):
        x_in = x_d if l == 0 else x1_d
        x_out = x1_d if l == 0 else out_d
        if phases is None or f"A{l}" in phases:
          with ExitStack() as es:
            bd = mkbld(f"A{l}")
            emit_phaseA(nc, bd, es, x_in, View(wA_d.ap()[l]), View(gA_d.ap()[l]), id_d, pT_d, SEQ)
        for hc in range(2):
            if phases is not None and f"B{l}{hc}" not in phases:
                continue
            D_ = dict(pT=View(pT_d.ap()[hc * HALF_ROWS:(hc + 1) * HALF_ROWS, :]), pos=pos_d,
                      wuq=View(wuq_d.ap()[l, hc]), wukv=View(wukv_d.ap()[l, hc]), cst=View(cst_d.ap()[l, hc]),
                      wl=View(wl_d.ap()[l, hc]), mats=mats_d, yT=View(yT_d.ap()[hc * 512:(hc + 1) * 512, :]),
                      scr=scr_d, rwm=rwm_d)
            with ExitStack() as es:
                emit_phaseB(nc, mkbld, es, D_, ("mla", "mlstm", "rwkv"), f"B{l}{hc}")
        D_ = dict(x=x_in, yT=yT_d, wo=View(wo_d.ap()[l]), wg=View(wg_d.ap()[l]), wu=View(wu_d.ap()[l]),
                  wd=View(wd_d.ap()[l]), g=View(gC_d.ap()[l]), gf_t=gf_d, ident=id_d, out=x_out)
        if phases is None or f"C{l}" in phases:
          with ExitStack() as es:
            bd = mkbld(f"C{l}")
            emit_phaseC(nc, bd, es, D_, l == 1, SEQ)
    return nc


_CACHE = {}
_PHASES = None


def kernel(**inputs):
    P = {k: np.asarray(v) for k, v in inputs.items()}
    x = np.asarray(P["x"], np.float32)
    positions = np.asarray(P["positions"]).astype(np.int32)
    if "nc" not in _CACHE:
        _CACHE["nc"] = build_fused(_PHASES)
    nc = _CACHE["nc"]
    f32 = lambda a: np.ascontiguousarray(np.asarray(a, np.float32))
    wA = f32(np.stack([prep_w_inA(P["w_in"][l]) for l in range(2)]))
    gA = f32(np.stack([_gcol(P["mix_norm"][l]) for l in range(2)]))
    gC = f32(np.stack([_gcol(P["ffn_norm"][l]) for l in range(2)]))
    pb = [[prep_phaseB_consts(P, l, hc) for hc in range(2)] for l in range(2)]
    stk = lambda key: f32(np.stack([np.stack([pb[l][hc][key] for hc in range(2)]) for l in range(2)]))
    common = dict(
        wA=wA, gA=gA, ident=np.eye(128, dtype=np.float32), mats=const_mats(), rwm=rwkv_masks(),
        wuq=stk("wuq"), wukv=stk("wukv"), cst=stk("cst"), wl=stk("wl"),
        wo=f32(np.stack([P["w_out"][l][W_OUT_PERM, :] for l in range(2)])),
        wg=f32(P["w_gate"]), wu=f32(P["w_up"]), wd=f32(P["w_down"]), gC=gC,
        gf=f32(np.asarray(P["final_norm"]).reshape(1, D)))
    in_maps = []
    for c in range(NCORES):
        b = c // 2
        m = dict(common)
        m["x"] = f32(x[b])
        m["pos"] = np.ascontiguousarray(positions[b:b + 1])
        in_maps.append(m)
    res = run_bass_kernel_spmd(nc, in_maps, core_ids=list(range(NCORES)))
    out = np.zeros((4, SEQ, D), np.float32)
    for b in range(4):
        out[b] = res.results[2 * b]["out"]
    return out
```

```python
from contextlib import ExitStack
import numpy as np
import concourse.bass as bass
import concourse.mybir as mybir
from concourse.bass_utils import run_bass_kernel_spmd

F32 = mybir.dt.float32
BF16 = mybir.dt.bfloat16
I32 = mybir.dt.int32
ALU = mybir.AluOpType
AF = mybir.ActivationFunctionType

D = 1024
SEQ = 4096
NTOK = 2048
DFF = 2816
NCORES = 8
EPS = 1e-6

NCH = 13
CH_ROWS = [128] * 12 + [4]
HALF_ROWS = 12 * 128 + 4
CH_OFF = [i * 128 for i in range(13)]
NCOLA = 2 * HALF_ROWS


def half_cols(hc):
    cols = []
    cols += list(range(0, 256))
    cols += list(range(256, 512))
    cols += list(range(512, 576))
    cols += list(range(544, 576)) + list(range(512, 544))
    R0 = 576
    for part in range(3):
        cols += list(range(R0 + part * 256 + hc * 128, R0 + part * 256 + hc * 128 + 128))
    cols += list(range(R0 + 768, R0 + 896))
    M0 = 576 + 896
    cols += list(range(M0 + hc * 64, M0 + hc * 64 + 64))
    cols += list(range(M0 + 128 + hc * 64, M0 + 128 + hc * 64 + 64))
    cols += list(range(M0 + 256 + hc * 128, M0 + 256 + hc * 128 + 128))
    cols += list(range(M0 + 520 + hc * 128, M0 + 520 + hc * 128 + 128))
    cols += list(range(M0 + 512 + hc * 2, M0 + 512 + hc * 2 + 2))
    cols += list(range(M0 + 516 + hc * 2, M0 + 516 + hc * 2 + 2))
    assert len(cols) == HALF_ROWS
    return cols


class Res:
    __slots__ = ("w", "r", "name")

    def __init__(self, name=""):
        self.w = None
        self.r = {}
        self.name = name


class Bld:
    NDMA = 8

    def __init__(self, nc, tag="", shared=None):
        self.nc = nc
        self.E = {"pe": nc.tensor, "dve": nc.vector, "act": nc.scalar, "pool": nc.gpsimd, "sp": nc.sync}
        self.sems = {}
        self.cnt = {}
        self.seen = {e: {} for e in self.E}
        self.touched = set()
        for e in self.E:
            self.sems[e] = nc.alloc_semaphore(name=f"s{tag}_{e}")
            self.cnt[e] = 0
        if shared is not None and "sems" in shared:
            self.sems.update(shared["sems"])
            self.cnt.update(shared["cnt"])
            self.dslot = shared["dslot"]
        else:
            dsems, dcnt = {}, {}
            self.dslot = {}
            for q in ("sp", "act", "pool"):
                for i in range(self.NDMA):
                    k = f"d{q}{i}"
                    dsems[k] = nc.alloc_semaphore(name=f"sdma_{k}")
                    dcnt[k] = 0
                self.dslot[q] = 0
            self.sems.update(dsems)
            self.cnt.update(dcnt)
            if shared is not None:
                shared["sems"] = dsems
                shared["dslot"] = self.dslot
                shared["cnt"] = {}
        self.shared = shared

    def _sync_shared(self):
        if self.shared is not None:
            for k in self.shared["sems"]:
                self.shared["cnt"][k] = self.cnt[k]

    def _wait(self, eng, deps):
        best = {}
        for k, v in deps:
            if v > best.get(k, 0):
                best[k] = v
        for k, v in best.items():
            if self.seen[eng].get(k, 0) >= v:
                continue
            self.E[eng].wait_ge(self.sems[k], v)
            self.seen[eng][k] = v

    def _deps(self, eng, reads, writes):
        deps = []
        for r in reads:
            if r.w is not None:
                if not (eng == "pe" and r.w[0] == "pe"):
                    deps.append(r.w)
        for w in writes:
            if w.w is not None and (w.w[0] != eng or eng != "pe"):
                deps.append(w.w)
            for k, v in w.r.items():
                if k != eng or eng != "pe":
                    deps.append((k, v))
        return deps

    def _mark(self, ev, reads, writes):
        self.touched.update(reads)
        self.touched.update(writes)
        for r in reads:
            if ev[1] > r.r.get(ev[0], 0):
                r.r[ev[0]] = ev[1]
        for w in writes:
            w.w = ev
            w.r = {}

    def op(self, eng, fn, reads=(), writes=(), ser=False, rt=None):
        if eng == "pe":
            last = getattr(self, "last_rt", None)
            if rt != last and self.cnt["pe"] > 0:
                self._wait("pe", [("pe", self.cnt["pe"])])
            self.last_rt = rt
        self._wait(eng, self._deps(eng, reads, writes))
        ins = fn(self.E[eng])
        self.cnt[eng] += 1
        ins.then_inc(self.sems[eng], 1)
        self._mark((eng, self.cnt[eng]), reads, writes)
        if ser:
            self._wait(eng, [(eng, self.cnt[eng])])
        return ins

    def dma(self, q, out, in_, reads=(), writes=()):
        i = self.dslot[q]
        self.dslot[q] = (i + 1) % self.NDMA
        k = f"d{q}{i}"
        deps = self._deps(k, reads, writes)
        deps.append((k, self.cnt[k]))
        self._wait(q, deps)
        ins = self.E[q].dma_start(out=out, in_=in_)
        self.cnt[k] += 16
        ins.then_inc(self.sems[k], 16)
        self._mark((k, self.cnt[k]), reads, writes)
        return ins

    def barrier(self):
        deps = [(k, v) for k, v in self.cnt.items() if v > 0]
        for e in ("sp", "pe", "dve", "act", "pool"):
            self._wait(e, deps)
        for e in ("sp", "pe", "dve", "act", "pool"):
            for k, v in deps:
                assert self.seen[e].get(k, 0) >= v
        for r in self.touched:
            r.w = None
            r.r = {}
        self.touched = set()
        self._sync_shared()

    def wait_all(self, eng, ress):
        deps = []
        for r in ress:
            if r.w is not None:
                deps.append(r.w)
        self._wait(eng, deps)


def dram_ap(t, offset, pattern):
    return bass.AP(t, offset, [list(p) for p in pattern])


def emit_rmsnorm_T(bd, x_tile, x_res, hT, hT_res, j, tmp, ident):
    ss, ss_r = tmp["ss"], tmp["ss_r"]
    junk, junk_r = tmp["junk"], tmp["junk_r"]
    xn, xn_r = tmp["xn"], tmp["xn_r"]
    pt, pt_r = tmp["pt"], tmp["pt_r"]
    bd.op("dve", lambda e: e.scalar_tensor_tensor(out=junk[:], in0=x_tile, scalar=1.0, in1=x_tile,
                                                  op0=ALU.mult, op1=ALU.mult, accum_out=ss[:, 0:1]),
          reads=[x_res], writes=[junk_r, ss_r])
    bd.op("act", lambda e: e.activation(out=ss[:, 1:2], in_=ss[:, 0:1], func=AF.Sqrt, scale=1.0 / D,
                                        bias=tmp["eps"][:, 0:1]), reads=[ss_r, tmp["eps_r"]], writes=[ss_r])
    bd.op("dve", lambda e: e.reciprocal(out=ss[:, 2:3], in_=ss[:, 1:2]), reads=[ss_r], writes=[ss_r])
    bd.op("act", lambda e: e.activation(out=xn[:], in_=x_tile, func=AF.Copy, scale=ss[:, 2:3]),
          reads=[x_res, ss_r], writes=[xn_r])
    for kc in range(8):
        bd.op("pe", lambda e, kc=kc: e.transpose(out=pt[:, kc * 128:(kc + 1) * 128],
                                                 in_=xn[:, kc * 128:(kc + 1) * 128], identity=ident[:]),
              reads=[xn_r], writes=[pt_r])
    bd.op("act", lambda e: e.activation(out=hT[:, :, j * 128:(j + 1) * 128],
                                        in_=pt[:].rearrange("p (k t) -> p k t", k=8), func=AF.Copy),
          reads=[pt_r], writes=[hT_res])


_UC = [0]


def _u(name):
    _UC[0] += 1
    return f"{name}_{_UC[0]}"


class View:
    def __init__(self, ap):
        self._ap = ap

    def ap(self):
        return self._ap


def emit_phaseA(nc, bd, es, x_d, w_d, g_d, id_d, pT_d, ntok):
    sb = lambda name, shape, dt: es.enter_context(nc.sbuf_tensor(_u(name), shape, dt))
    ps = lambda name, shape, dt: es.enter_context(nc.psum_tensor(_u(name), shape, dt))
    wb = sb("A_wb", [128, 8, NCOLA], BF16)
    wb_r = [Res() for _ in range(8)]
    stage = [sb(f"A_stage{i}", [128, NCOLA], F32) for i in range(2)]
    stage_r = [Res(), Res()]
    gcol = sb("A_gcol", [128, 8], F32)
    gcol_r = Res()
    idf = sb("A_idf", [128, 128], F32)
    ident = sb("A_ident", [128, 128], BF16)
    ident_r = Res()
    idf_r = Res()
    xt = [sb(f"A_xt{i}", [128, D], F32) for i in range(2)]
    xt_r = [Res(), Res()]
    hT = [sb(f"A_hT{i}", [128, 8, 512], BF16) for i in range(2)]
    hT_r = [Res(), Res()]
    tmp = dict(ss=sb("A_ss", [128, 4], F32), ss_r=Res(), junk=sb("A_junk", [128, D], BF16), junk_r=Res(),
               xn=sb("A_xn", [128, D], BF16), xn_r=Res(),
               pt=ps("A_pt", [128, D], BF16), pt_r=Res(), eps=sb("A_eps", [128, 1], F32), eps_r=Res())
    bd.op("dve", lambda e: e.memset(tmp["eps"][:], EPS), writes=[tmp["eps_r"]])
    NPB = 4
    pb = [ps(f"A_pb{i}", [128, 512], F32) for i in range(NPB)]
    pb_r = [Res() for _ in range(NPB)]
    ost = [sb(f"A_ost{i}", [128, 512], F32) for i in range(4)]
    ost_r = [Res() for _ in range(4)]

    bd.dma("sp", gcol[:], g_d.ap(), writes=[gcol_r])
    bd.dma("sp", idf[:], id_d.ap(), writes=[idf_r])
    bd.op("dve", lambda e: e.tensor_copy(out=ident[:], in_=idf[:]), reads=[idf_r], writes=[ident_r])
    for kc in range(8):
        s = kc % 2
        bd.dma("pool", stage[s][:], w_d.ap()[kc * 128:(kc + 1) * 128, :], writes=[stage_r[s]])
        bd.op("dve", lambda e, kc=kc, s=s: e.tensor_scalar(out=wb[:, kc, :], in0=stage[s][:],
                                                          scalar1=gcol[:, kc:kc + 1], scalar2=None, op0=ALU.mult),
              reads=[stage_r[s], gcol_r], writes=[wb_r[kc]])
    x_ap = x_d.ap()
    pT_ap = pT_d.ap()
    nblk = ntok // 512
    oi = 0
    for blk in range(nblk):
        hb = blk % 2
        for j in range(4):
            ti = blk * 4 + j
            xb = ti % 2
            bd.dma("sp", xt[xb][:], x_ap[ti * 128:(ti + 1) * 128, :], writes=[xt_r[xb]])
            tmp2 = dict(tmp)
            emit_rmsnorm_T(bd, xt[xb][:], xt_r[xb], hT[hb], hT_r[hb], j, tmp2, ident)
        for half in range(2):
            for c in range(NCH):
                m = CH_ROWS[c]
                col0 = half * HALF_ROWS + CH_OFF[c]
                pbi = oi % NPB
                for kc in range(8):
                    bd.op("pe", lambda e, kc=kc, col0=col0, m=m, pbi=pbi, hb=hb: e.matmul(
                        pb[pbi][0:m, :], lhsT=wb[:, kc, col0:col0 + m], rhs=hT[hb][:, kc, :],
                        start=(kc == 0), stop=(kc == 7)),
                        reads=[wb_r[kc], hT_r[hb]], writes=[pb_r[pbi]])
                osi = oi % 4
                eng = "act" if oi % 2 == 0 else "dve"
                if eng == "act":
                    bd.op("act", lambda e, m=m, pbi=pbi, osi=osi: e.activation(
                        out=ost[osi][0:m, :], in_=pb[pbi][0:m, :], func=AF.Copy),
                        reads=[pb_r[pbi]], writes=[ost_r[osi]])
                else:
                    bd.op("dve", lambda e, m=m, pbi=pbi, osi=osi: e.tensor_copy(
                        out=ost[osi][0:m, :], in_=pb[pbi][0:m, :]),
                        reads=[pb_r[pbi]], writes=[ost_r[osi]])
                bd.dma("sp", pT_ap[col0:col0 + m, blk * 512:(blk + 1) * 512], ost[osi][0:m, :],
                       reads=[ost_r[osi]])
                oi += 1
    bd.barrier()


def _gcol(g):
    return np.ascontiguousarray(np.asarray(g, np.float32).reshape(8, 128).T)


def prep_w_inA(w_in_l):
    cols = half_cols(0) + half_cols(1)
    return np.ascontiguousarray(w_in_l[:, cols])


TWO_PI = 6.283185307179586
(C_QN0, C_QN1, C_KVN0, C_KVN1, C_INVF, C_SGN, C_MLAO0, C_MLAO1,
 C_MU_R, C_MU_K, C_MU_V, C_MU_L, C_W0, C_A0, C_KK, C_KA, C_RK, C_LNW, C_LNB,
 C_CWQ0, C_CWQ1, C_CWQ2, C_CWQ3, C_CBQ, C_CWK0, C_CWK1, C_CWK2, C_CWK3, C_CBK,
 C_IB, C_FB, C_MLO, C_EPS, C_LNEPS, C_ONE, C_ZERO, C_IB1, C_FB1) = range(38)
NCST = 38


def emit_attention(bd, nc, es, name, heads, scale_exp, dv, wfun, out_fn, PS):
    sb = lambda nm, shape, dt: es.enter_context(nc.sbuf_tensor(_u(nm), shape, dt))
    LOOK = 3
    NST = len(PS["st"])
    NPT = LOOK + 2
    pT = [sb(f"{name}_pT{i}", [128, 512], BF16) for i in range(NPT)]
    pT_r = [Res() for _ in range(NPT)]
    mask = PS["mask"]
    mask_r = PS["mask_r"]
    blocks = [(h, qb, kt) for h in heads for qb in range(SEQ // 512) for kt in range(4 * (qb + 1))]
    pending = []

    def stage1(i):
        h, qb, kt = blocks[i]
        st, st_r = PS["st"][i % NST]
        kp = PS["kparts"](h, kt)
        qp = PS["qparts"](h, qb)
        n = len(kp)
        for a in range(n):
            bd.op("pe", lambda e, a=a: e.matmul(st[:, :], lhsT=kp[a][0], rhs=qp[a][0],
                                                start=(a == 0), stop=(a == n - 1)),
                  reads=[kp[a][1], qp[a][1]], writes=[st_r])
        p, p_r = pT[i % NPT], pT_r[i % NPT]
        wfun(h, kt, qb, st, st_r, p, p_r)
        jd = kt - 4 * qb
        if jd >= 0:
            bd.op("pool", lambda e: e.tensor_tensor(
                out=p[:, jd * 128:(jd + 1) * 128], in0=p[:, jd * 128:(jd + 1) * 128], in1=mask[:],
                op=ALU.mult), reads=[p_r, mask_r], writes=[p_r])

    def stage2(i):
        h, qb, kt = blocks[i]
        p, p_r = pT[i % NPT], pT_r[i % NPT]
        v_ap, v_r = PS["v"](h, kt)
        for j in range(4):
            qt = 4 * qb + j
            if qt < kt:
                continue
            o, o_r = PS["o"][j]
            bd.op("pe", lambda e, o=o, j=j, qt=qt: e.matmul(
                o[:, 0:dv + 1], lhsT=p[:, j * 128:(j + 1) * 128], rhs=v_ap,
                start=(kt == 0), stop=(kt == qt)), reads=[p_r, v_r], writes=[o_r])
            if kt == qt:
                th = out_fn(h, qt, o, o_r)
                if th is not None:
                    pending.append([3, th])

    nblk = len(blocks)
    for i in range(nblk + LOOK):
        if i < nblk:
            stage1(i)
        if i - LOOK >= 0:
            stage2(i - LOOK)
        for item in pending:
            item[0] -= 1
        while pending and pending[0][0] <= 0:
            pending.pop(0)[1]()
    while pending:
        pending.pop(0)[1]()


def emit_rope_tables(bd, nc, es, pos_d, cst, cst_r, CS, SN, tab_r):
    with ExitStack() as es2:
        sb = lambda nm, shape, dt: es2.enter_context(nc.sbuf_tensor(_u(nm), shape, dt))
        ti = sb("rt_i", [64, SEQ], I32)
        ta = sb("rt_a", [64, SEQ], F32)
        tb = sb("rt_b", [64, SEQ], F32)
        ti_r, ta_r, tb_r = Res(), Res(), Res()
        src = bass.AP(pos_d, 0, [[0, 64], [1, SEQ]])
        bd.dma("sp", ti[:], src, writes=[ti_r])
        bd.op("dve", lambda e: e.tensor_copy(out=ta[:], in_=ti[:]), reads=[ti_r], writes=[ta_r])
        bd.op("dve", lambda e: e.tensor_scalar(out=ta[:], in0=ta[:], scalar1=cst[0:64, C_INVF:C_INVF + 1],
                                               scalar2=None, op0=ALU.mult), reads=[ta_r, cst_r], writes=[ta_r])
        for which in (0, 1):
            shift = 0.0 if which == 0 else TWO_PI / 4
            bd.op("dve", lambda e: e.tensor_scalar(out=tb[:], in0=ta[:], scalar1=shift, scalar2=1.0 / TWO_PI,
                                                   op0=ALU.add, op1=ALU.mult), reads=[ta_r], writes=[tb_r])
            bd.op("dve", lambda e: e.tensor_copy(out=ti[:], in_=tb[:]), reads=[tb_r], writes=[ti_r])
            bd.op("dve", lambda e: e.tensor_copy(out=tb[:], in_=ti[:]), reads=[ti_r], writes=[tb_r])
            bd.op("dve", lambda e: e.scalar_tensor_tensor(out=tb[:], in0=tb[:], scalar=-TWO_PI, in1=ta[:],
                                                          op0=ALU.mult, op1=ALU.add),
                  reads=[tb_r, ta_r], writes=[tb_r])
            bd.op("dve", lambda e: e.tensor_scalar(out=tb[:], in0=tb[:], scalar1=shift, scalar2=TWO_PI / 2,
                                                   op0=ALU.add, op1=ALU.min), reads=[tb_r], writes=[tb_r])
            bd.op("dve", lambda e: e.tensor_scalar(out=tb[:], in0=tb[:], scalar1=-TWO_PI / 2, scalar2=None,
                                                   op0=ALU.max), reads=[tb_r], writes=[tb_r])
            if which == 0:
                bd.op("act", lambda e: e.activation(out=SN[:], in_=tb[:], func=AF.Sin,
                                                    scale=cst[0:64, C_SGN:C_SGN + 1]),
                      reads=[tb_r, cst_r], writes=[tab_r])
            else:
                bd.op("act", lambda e: e.activation(out=CS[:], in_=tb[:], func=AF.Sin),
                      reads=[tb_r], writes=[tab_r])
        bd.barrier()


def emit_mla(bd, nc, es, pT_d, pos_d, wuq_d, wukv_d, yT_d, cst, cst_r, K):
    sb = lambda nm, shape, dt: es.enter_context(nc.sbuf_tensor(_u(nm), shape, dt))
    pT = pT_d.ap()
    yT = yT_d.ap()
    CS = sb("m_CS", [64, SEQ], F32)
    SN = sb("m_SN", [64, SEQ], F32)
    tab_r = Res()
    emit_rope_tables(bd, nc, es, pos_d, cst, cst_r, CS, SN, tab_r)
    wq = sb("m_wq", [128, 2, 512], BF16)
    wkv = sb("m_wkv", [128, 2, 512], BF16)
    wq_r, wkv_r = Res(), Res()
    wst = sb("m_wst", [128, 2, 512], F32)
    wst_r = Res()
    for (wd, wt, wr) in ((wuq_d, wq, wq_r), (wukv_d, wkv, wkv_r)):
        bd.dma("sp", wst[:], wd.ap().rearrange("(k p) n -> p k n", p=128), writes=[wst_r])
        bd.op("dve", lambda e, wt=wt: e.tensor_copy(out=wt[:], in_=wst[:]), reads=[wst_r], writes=[wr])
    Qn = [sb(f"m_Qn{h}", [128, SEQ], BF16) for h in range(2)]
    Qr = [sb(f"m_Qr{h}", [64, SEQ], BF16) for h in range(2)]
    Kn = [sb(f"m_Kn{h}", [128, SEQ], BF16) for h in range(2)]
    Kr = sb("m_Kr", [64, SEQ], BF16)
    V = [sb(f"m_V{h}", [128, 32, 129], BF16) for h in range(2)]
    qk_r = Res()
    for h in range(2):
        bd.op("pool", lambda e, h=h: e.memset(V[h][:, :, 128:129], 1.0), writes=[qk_r])
    banks, banks_r, ptb, ptb_r = K["banks"], K["banks_r"], K["ptb"], K["ptb_r"]
    ones, ones_r = K["ones"], K["ones_r"]
    with ExitStack() as es2:
        sb2 = lambda nm, shape, dt: es2.enter_context(nc.sbuf_tensor(_u(nm), shape, dt))
        cf = [sb2(f"m_cf{i}", [128, 2, 512], F32) for i in range(2)]
        cf_r = [Res(), Res()]
        sq = sb2("m_sq", [128, 2, 512], F32)
        sq_r = Res()
        rs = sb2("m_rs", [128, 512], F32)
        rs_r = Res()
        cn = [sb2(f"m_cn{i}", [128, 2, 512], BF16) for i in range(2)]
        cn_r = [Res(), Res()]
        kx = sb2("m_kx", [64, 2, 512], F32)
        kx_r = Res()
        t1 = sb2("m_t1", [64, 512], F32)
        t2 = sb2("m_t2", [64, 512], F32)
        t1_r, t2_r = Res(), Res()
        bi = 0

        def nb():
            nonlocal bi
            b = bi % 6
            bi += 1
            return banks[b], banks_r[b]

        def rope(dst, x_ap, xs_ap, src_res, sl):
            bd.op("dve", lambda e: e.tensor_tensor(out=t1[:], in0=x_ap, in1=CS[:, sl], op=ALU.mult),
                  reads=src_res + [tab_r], writes=[t1_r])
            bd.op("dve", lambda e: e.tensor_tensor(out=t2[:], in0=xs_ap, in1=SN[:, sl], op=ALU.mult),
                  reads=src_res + [tab_r], writes=[t2_r])
            bd.op("dve", lambda e: e.tensor_tensor(out=dst, in0=t1[:], in1=t2[:], op=ALU.add),
                  reads=[t1_r, t2_r], writes=[qk_r])

        for tb in range(SEQ // 512):
            sl = slice(tb * 512, (tb + 1) * 512)
            for which in (0, 1):
                ci = which
                row0 = which * 256
                bd.dma("sp", cf[ci][:], pT[row0:row0 + 256, sl].rearrange("(k p) t -> p k t", p=128),
                       writes=[cf_r[ci]])
                bd.op("act", lambda e, ci=ci: e.activation(out=sq[:], in_=cf[ci][:], func=AF.Square),
                      reads=[cf_r[ci]], writes=[sq_r])
                pss, pss_r = nb()
                for c in range(2):
                    bd.op("pe", lambda e, c=c, pss=pss: e.matmul(pss[:, :], lhsT=ones[:], rhs=sq[:, c, :],
                                                                 start=(c == 0), stop=(c == 1)),
                          reads=[ones_r, sq_r], writes=[pss_r])
                bd.op("act", lambda e, pss=pss: e.activation(out=rs[:], in_=pss[:, :], func=AF.Sqrt,
                                                             scale=1.0 / 256, bias=cst[:, C_EPS:C_EPS + 1]),
                      reads=[pss_r, cst_r], writes=[rs_r])
                bd.op("dve", lambda e: e.reciprocal(out=rs[:], in_=rs[:]), reads=[rs_r], writes=[rs_r])
                gcol = C_QN0 if which == 0 else C_KVN0
                for c in range(2):
                    bd.op("dve", lambda e, c=c, ci=ci, gcol=gcol: e.scalar_tensor_tensor(
                        out=cn[ci][:, c, :], in0=cf[ci][:, c, :], scalar=cst[:, gcol + c:gcol + c + 1], in1=rs[:],
                        op0=ALU.mult, op1=ALU.mult), reads=[cf_r[ci], rs_r, cst_r], writes=[cn_r[ci]])
                if which == 0:
                    for h in range(2):
                        pq, pq_r = nb()
                        for c in range(2):
                            bd.op("pe", lambda e, c=c, h=h, pq=pq: e.matmul(
                                pq[:, :], lhsT=wq[:, c, h * 256:h * 256 + 128], rhs=cn[0][:, c, :],
                                start=(c == 0), stop=(c == 1)), reads=[wq_r, cn_r[0]], writes=[pq_r])
                        bd.op("act", lambda e, h=h, pq=pq: e.activation(out=Qn[h][:, sl], in_=pq[:, :], func=AF.Copy),
                              reads=[pq_r], writes=[qk_r])
                        pa, pa_r = nb()
                        pb_, pb_r = nb()
                        for (pp, pp_r, off) in ((pa, pa_r, 128), (pb_, pb_r, 192)):
                            for c in range(2):
                                bd.op("pe", lambda e, c=c, h=h, pp=pp, off=off: e.matmul(
                                    pp[0:64, :], lhsT=wq[:, c, h * 256 + off:h * 256 + off + 64], rhs=cn[0][:, c, :],
                                    start=(c == 0), stop=(c == 1)), reads=[wq_r, cn_r[0]], writes=[pp_r])
                        rope(Qr[h][:, sl], pa[0:64, :], pb_[0:64, :], [pa_r, pb_r], sl)
                else:
                    for h in range(2):
                        pk, pk_r = nb()
                        for c in range(2):
                            bd.op("pe", lambda e, c=c, h=h, pk=pk: e.matmul(
                                pk[:, :], lhsT=wkv[:, c, h * 256:h * 256 + 128], rhs=cn[1][:, c, :],
                                start=(c == 0), stop=(c == 1)), reads=[wkv_r, cn_r[1]], writes=[pk_r])
                        bd.op("act", lambda e, h=h, pk=pk: e.activation(out=Kn[h][:, sl], in_=pk[:, :], func=AF.Copy),
                              reads=[pk_r], writes=[qk_r])
                        pv, pv_r = nb()
                        for j in range(4):
                            for c in range(2):
                                bd.op("pe", lambda e, c=c, h=h, j=j, pv=pv: e.matmul(
                                    pv[:, j * 128:(j + 1) * 128], lhsT=cn[1][:, c, j * 128:(j + 1) * 128],
                                    rhs=wkv[:, c, h * 256 + 128:h * 256 + 256],
                                    start=(c == 0 and j == 0), stop=(c == 1)), reads=[wkv_r, cn_r[1]], writes=[pv_r])
                        bd.op("dve", lambda e, h=h, pv=pv: e.tensor_copy(
                            out=V[h][:, tb * 4:(tb + 1) * 4, 0:128],
                            in_=pv[:, :].rearrange("p (j d) -> p j d", j=4)), reads=[pv_r], writes=[qk_r])
            bd.dma("sp", kx[:], pT[512:640, sl].rearrange("(k p) t -> p k t", p=64), writes=[kx_r])
            rope(Kr[:, sl], kx[:, 0, :], kx[:, 1, :], [kx_r], sl)
        bd.barrier()
    with ExitStack() as es3:
        sb3 = lambda nm, shape, dt: es3.enter_context(nc.sbuf_tensor(_u(nm), shape, dt))
        of = sb3("m_of", [128, 132], F32)
        of_r = Res()
        onb = sb3("m_onb", [128, 128], BF16)
        onb_r = Res()
        st_ = sb3("m_stat", [128, 4], F32)
        st_r = Res()
        junk = sb3("m_junk", [128, 128], F32)
        junk_r = Res()
        yst = [sb3(f"m_yst{i}", [128, 512], BF16) for i in range(2)]
        yst_r = [Res(), Res()]
        scale = (128 + 64) ** -0.5

        def wfun(h, kt, qb, st, st_r2, p, p_r):
            bd.op("act", lambda e: e.activation(out=p[:], in_=st[:, :], func=AF.Exp, scale=scale),
                  reads=[st_r2], writes=[p_r])

        onbs = [sb3(f"m_onb{i}", [128, 128], BF16) for i in range(4)]
        onbs_r = [Res() for _ in range(4)]
        oi = [0]

        def out_fn(h, qt, o, o_r):
            ob, ob_r = onbs[oi[0] % 4], onbs_r[oi[0] % 4]
            oi[0] += 1
            bd.op("dve", lambda e: e.reciprocal(out=st_[:, 0:1], in_=o[:, 128:129]), reads=[o_r], writes=[st_r])
            bd.op("dve", lambda e: e.tensor_scalar(out=of[:, 0:128], in0=o[:, 0:128], scalar1=st_[:, 0:1],
                                                   scalar2=None, op0=ALU.mult), reads=[o_r, st_r], writes=[of_r])
            bd.op("dve", lambda e: e.scalar_tensor_tensor(out=junk[:], in0=of[:, 0:128], scalar=1.0, in1=of[:, 0:128],
                                                          op0=ALU.mult, op1=ALU.mult, accum_out=st_[:, 1:2]),
                  reads=[of_r], writes=[junk_r, st_r])
            bd.op("act", lambda e: e.activation(out=st_[:, 2:3], in_=st_[:, 1:2], func=AF.Sqrt, scale=1.0 / 128,
                                                bias=cst[:, C_EPS:C_EPS + 1]), reads=[st_r, cst_r], writes=[st_r])
            bd.op("dve", lambda e: e.reciprocal(out=st_[:, 3:4], in_=st_[:, 2:3]), reads=[st_r], writes=[st_r])
            bd.op("dve", lambda e: e.tensor_scalar(out=ob[:], in0=of[:, 0:128], scalar1=st_[:, 3:4], scalar2=None,
                                                   op0=ALU.mult), reads=[of_r, st_r], writes=[ob_r])

            def fin():
                bd.op("pe", lambda e: e.transpose(out=ptb[:, 0:128], in_=ob[:], identity=K["identb"][:]),
                      reads=[ob_r, K["identb_r"]], writes=[ptb_r])
                ys, ys_r = yst[(qt // 4) % 2], yst_r[(qt // 4) % 2]
                j = qt % 4
                bd.op("act", lambda e: e.activation(out=ys[:, j * 128:(j + 1) * 128], in_=ptb[:, 0:128], func=AF.Copy,
                                                    scale=cst[:, C_MLAO0 + h:C_MLAO0 + h + 1]),
                      reads=[ptb_r, cst_r], writes=[ys_r])
                if j == 3:
                    qb = qt // 4
                    bd.dma("sp", yT[h * 128:(h + 1) * 128, qb * 512:(qb + 1) * 512], ys[:], reads=[ys_r])
            return fin

        PS = dict(st=[(banks[0], banks_r[0]), (banks[1], banks_r[1]), (banks[6], banks_r[6])],
                  o=[(banks[2 + j], banks_r[2 + j]) for j in range(4)],
                  mask=K["mask"], mask_r=K["mask_r"],
                  kparts=lambda h, kt: [(Kn[h][:, kt * 128:(kt + 1) * 128], qk_r), (Kr[:, kt * 128:(kt + 1) * 128], qk_r)],
                  qparts=lambda h, qb: [(Qn[h][:, qb * 512:(qb + 1) * 512], qk_r), (Qr[h][:, qb * 512:(qb + 1) * 512], qk_r)],
                  v=lambda h, kt: (V[h][:, kt, :], qk_r))
        emit_attention(bd, nc, es3, "mla", [0, 1], scale, 128, wfun, out_fn, PS)
        bd.barrier()


def emit_mlstm(bd, nc, es, pT_d, yT_d, cst, cst_r, K):
    sb = lambda nm, shape, dt: es.enter_context(nc.sbuf_tensor(_u(nm), shape, dt))
    pT = pT_d.ap()
    yT = yT_d.ap()
    banks, banks_r, ptb, ptb_r = K["banks"], K["banks_r"], K["ptb"], K["ptb_r"]
    misc, misc_r = banks[6], banks_r[6]
    R_Q, R_K, R_V, R_O, R_G = 1152, 1216, 1280, 1408, 1536
    Qb = sb("l_Qb", [64, SEQ], BF16)
    Kb = sb("l_Kb", [64, SEQ], BF16)
    Vm = sb("l_Vm", [128, 32, 2, 65], BF16)
    Ym = sb("l_Ym", [128, SEQ], BF16)
    uT = sb("l_uT", [128, 2, 32], F32)
    emT = sb("l_emT", [128, 2, 32], F32)
    nPb = [sb(f"l_nPb{h}", [128, SEQ], F32) for h in range(2)]
    prep_r = Res()
    ym_r = Res()
    bd.op("pool", lambda e: e.memset(Vm[:, :, :, 64:65], 1.0), writes=[prep_r])
    with ExitStack() as es2:
        sb2 = lambda nm, shape, dt: es2.enter_context(nc.sbuf_tensor(_u(nm), shape, dt))
        xin = sb2("l_xin", [64, SEQ], F32)
        A = sb2("l_A", [64, SEQ], F32)
        B = sb2("l_B", [64, SEQ], F32)
        xin_r, A_r, B_r = Res(), Res(), Res()
        for (row0, cw0, cb, dst, scl) in ((R_Q, C_CWQ0, C_CBQ, Qb, 32 ** -0.5), (R_K, C_CWK0, C_CBK, Kb, 1.0)):
            bd.dma("sp", xin[:], pT[row0:row0 + 64, :], writes=[xin_r])
            bd.op("dve", lambda e, cw0=cw0, cb=cb: e.tensor_scalar(
                out=A[:], in0=xin[:], scalar1=cst[0:64, cw0 + 3:cw0 + 4], scalar2=cst[0:64, cb:cb + 1],
                op0=ALU.mult, op1=ALU.add), reads=[xin_r, cst_r], writes=[A_r])
            src, src_r, dstt, dst_r = A, A_r, B, B_r
            for sh in (1, 2, 3):
                bd.op("dve", lambda e, sh=sh, cw0=cw0, src=src, dstt=dstt: e.scalar_tensor_tensor(
                    out=dstt[:, sh:], in0=xin[:, 0:SEQ - sh], scalar=cst[0:64, cw0 + 3 - sh:cw0 + 4 - sh],
                    in1=src[:, sh:], op0=ALU.mult, op1=ALU.add), reads=[xin_r, cst_r, src_r], writes=[dst_r])
                bd.op("dve", lambda e, sh=sh, src=src, dstt=dstt: e.tensor_copy(out=dstt[:, 0:sh], in_=src[:, 0:sh]),
                      reads=[src_r], writes=[dst_r])
                src, src_r, dstt, dst_r = dstt, dst_r, src, src_r
            bd.op("act", lambda e, src=src: e.activation(out=xin[:], in_=src[:], func=AF.Silu),
                  reads=[src_r], writes=[xin_r])
            bd.op("dve", lambda e, dst=dst, scl=scl: e.tensor_scalar(out=dst[:], in0=xin[:], scalar1=scl, scalar2=None,
                                                                     op0=ALU.mult), reads=[xin_r], writes=[prep_r])
        bd.barrier()
    with ExitStack() as es2:
        sb2 = lambda nm, shape, dt: es2.enter_context(nc.sbuf_tensor(_u(nm), shape, dt))
        t0 = sb2("l_t0", [1, SEQ], F32)
        t1 = sb2("l_t1", [1, SEQ], F32)
        t2 = sb2("l_t2", [1, SEQ], F32)
        onesrow = sb2("l_onesrow", [1, SEQ], F32)
        vin = sb2("l_vin", [128, SEQ], F32)
        t0_r, t1_r, t2_r, or_r, vin_r = Res(), Res(), Res(), Res(), Res()
        bd.op("dve", lambda e: e.memset(onesrow[:], 1.0), writes=[or_r])
        identf = K["identf"]
        for h in range(2):
            cib = C_IB if h == 0 else C_IB1
            cfb = C_FB if h == 0 else C_FB1
            bd.dma("sp", t0[:], pT[R_G + h:R_G + h + 1, :], writes=[t0_r])
            bd.dma("sp", t1[:], pT[R_G + 2 + h:R_G + 3 + h, :], writes=[t1_r])
            bd.op("dve", lambda e, cib=cib: e.tensor_scalar(out=t0[:], in0=t0[:], scalar1=cst[0:1, cib:cib + 1],
                                                            scalar2=None, op0=ALU.add), reads=[t0_r, cst_r], writes=[t0_r])
            bd.op("act", lambda e, cfb=cfb: e.activation(out=t1[:], in_=t1[:], func=AF.Sigmoid,
                                                         bias=cst[0:1, cfb:cfb + 1]), reads=[t1_r, cst_r], writes=[t1_r])
            bd.op("act", lambda e: e.activation(out=t1[:], in_=t1[:], func=AF.Ln), reads=[t1_r], writes=[t1_r])
            bd.op("dve", lambda e: e.tensor_tensor_scan(out=t2[:], data0=onesrow[:], data1=t1[:], initial=0.0,
                                                        op0=ALU.mult, op1=ALU.add), reads=[or_r, t1_r], writes=[t2_r])
            bd.op("dve", lambda e: e.tensor_tensor(out=t0[:], in0=t0[:], in1=t2[:], op=ALU.subtract),
                  reads=[t0_r, t2_r], writes=[t0_r])
            bd.op("dve", lambda e: e.tensor_tensor_scan(out=t1[:], data0=onesrow[:], data1=t0[:], initial=0.0,
                                                        op0=ALU.mult, op1=ALU.max), reads=[or_r, t0_r], writes=[t1_r])
            bd.op("dve", lambda e: e.tensor_tensor(out=t2[:], in0=t2[:], in1=t1[:], op=ALU.add),
                  reads=[t2_r, t1_r], writes=[t2_r])
            for jt in range(32):
                bd.op("pe", lambda e, jt=jt: e.transpose(out=misc[:, jt:jt + 1], in_=t0[0:1, jt * 128:(jt + 1) * 128],
                                                         identity=identf[0:1, 0:1]), reads=[t0_r, K["mats_r"]], writes=[misc_r])
                bd.op("pe", lambda e, jt=jt: e.transpose(out=misc[:, 32 + jt:33 + jt], in_=t2[0:1, jt * 128:(jt + 1) * 128],
                                                         identity=identf[0:1, 0:1]), reads=[t2_r, K["mats_r"]], writes=[misc_r])
            bd.op("dve", lambda e, h=h: e.tensor_copy(out=uT[:, h, :], in_=misc[:, 0:32]), reads=[misc_r], writes=[prep_r])
            bd.op("act", lambda e, h=h: e.activation(out=emT[:, h, :], in_=misc[:, 32:64], func=AF.Exp, scale=-1.0),
                  reads=[misc_r], writes=[prep_r])
            for tb in range(SEQ // 512):
                bd.op("pe", lambda e, tb=tb: e.matmul(misc[:, :], lhsT=K["ones"][0:1, :], rhs=t1[0:1, tb * 512:(tb + 1) * 512],
                                                      start=True, stop=True), reads=[t1_r, K["ones_r"]], writes=[misc_r])
                bd.op("act", lambda e, tb=tb, h=h: e.activation(out=nPb[h][:, tb * 512:(tb + 1) * 512], in_=misc[:, :],
                                                                func=AF.Copy, scale=-1.0), reads=[misc_r], writes=[prep_r])
        bd.dma("sp", vin[:], pT[R_V:R_V + 128, :], writes=[vin_r])
        for jt in range(32):
            bd.op("pe", lambda e, jt=jt: e.transpose(out=misc[:, 0:128], in_=vin[:, jt * 128:(jt + 1) * 128],
                                                     identity=identf), reads=[vin_r, K["mats_r"]], writes=[misc_r])
            bd.op("dve", lambda e, jt=jt: e.tensor_copy(out=Vm[:, jt, :, 0:64],
                                                        in_=misc[:, 0:128].rearrange("p (h d) -> p h d", h=2)),
                  reads=[misc_r], writes=[prep_r])
        bd.barrier()
    with ExitStack() as es3:
        sb3 = lambda nm, shape, dt: es3.enter_context(nc.sbuf_tensor(_u(nm), shape, dt))
        Wt = [sb3(f"l_W{i}", [128, 512], F32) for i in range(2)]
        Wt_r = [Res(), Res()]
        of = sb3("l_of", [128, 64], F32)
        of_r = Res()
        onb = sb3("l_onb", [128, 64], BF16)
        onb_r = Res()
        st_ = sb3("l_stat", [128, 6], F32)
        st_r = Res()
        junk = sb3("l_junk", [128, 64], F32)
        junk_r = Res()
        wi = [0]

        def wfun(h, kt, qb, st, st_r2, p, p_r):
            w, w_r = Wt[wi[0] % 2], Wt_r[wi[0] % 2]
            wi[0] += 1
            bd.op("act", lambda e: e.activation(out=w[:], in_=nPb[h][:, qb * 512:(qb + 1) * 512], func=AF.Exp,
                                                bias=uT[:, h, kt:kt + 1]), reads=[prep_r], writes=[w_r])
            bd.op("dve", lambda e: e.tensor_tensor(out=p[:], in0=st[:, :], in1=w[:], op=ALU.mult),
                  reads=[st_r2, w_r], writes=[p_r])

        onbs = [sb3(f"l_onb{i}", [128, 64], BF16) for i in range(4)]
        onbs_r = [Res() for _ in range(4)]
        oi = [0]

        def out_fn(h, qt, o, o_r):
            ob, ob_r = onbs[oi[0] % 4], onbs_r[oi[0] % 4]
            oi[0] += 1
            bd.op("act", lambda e: e.activation(out=st_[:, 0:1], in_=o[:, 64:65], func=AF.Abs), reads=[o_r], writes=[st_r])
            bd.op("dve", lambda e: e.tensor_tensor(out=st_[:, 1:2], in0=st_[:, 0:1], in1=emT[:, h, qt:qt + 1], op=ALU.max),
                  reads=[st_r, prep_r], writes=[st_r])
            bd.op("dve", lambda e: e.reciprocal(out=st_[:, 2:3], in_=st_[:, 1:2]), reads=[st_r], writes=[st_r])
            bd.op("dve", lambda e: e.tensor_scalar(out=of[:], in0=o[:, 0:64], scalar1=st_[:, 2:3], scalar2=None,
                                                   op0=ALU.mult), reads=[o_r, st_r], writes=[of_r])
            bd.op("dve", lambda e: e.scalar_tensor_tensor(out=junk[:], in0=of[:], scalar=1.0, in1=of[:],
                                                          op0=ALU.mult, op1=ALU.mult, accum_out=st_[:, 3:4]),
                  reads=[of_r], writes=[junk_r, st_r])
            bd.op("act", lambda e: e.activation(out=st_[:, 4:5], in_=st_[:, 3:4], func=AF.Sqrt, scale=1.0 / 64,
                                                bias=cst[:, C_EPS:C_EPS + 1]), reads=[st_r, cst_r], writes=[st_r])
            bd.op("dve", lambda e: e.reciprocal(out=st_[:, 5:6], in_=st_[:, 4:5]), reads=[st_r], writes=[st_r])
            bd.op("dve", lambda e: e.tensor_scalar(out=ob[:], in0=of[:], scalar1=st_[:, 5:6], scalar2=None, op0=ALU.mult),
                  reads=[of_r, st_r], writes=[ob_r])

            def fin():
                bd.op("pe", lambda e: e.transpose(out=ptb[h * 64:(h + 1) * 64, 0:128], in_=ob[:], identity=K["identb"][:]),
                      reads=[ob_r, K["identb_r"]], writes=[ptb_r])
                bd.op("act", lambda e: e.activation(out=Ym[h * 64:(h + 1) * 64, qt * 128:(qt + 1) * 128],
                                                    in_=ptb[h * 64:(h + 1) * 64, 0:128], func=AF.Copy,
                                                    scale=cst[h * 64:(h + 1) * 64, C_MLO:C_MLO + 1]),
                      reads=[ptb_r, cst_r], writes=[ym_r])
            return fin

        PS = dict(st=[(banks[0], banks_r[0]), (banks[1], banks_r[1]), (banks[6], banks_r[6])],
                  o=[(banks[2 + j], banks_r[2 + j]) for j in range(4)],
                  mask=K["mask"], mask_r=K["mask_r"],
                  kparts=lambda h, kt: [(Kb[h * 32:(h + 1) * 32, kt * 128:(kt + 1) * 128], prep_r)],
                  qparts=lambda h, qb: [(Qb[h * 32:(h + 1) * 32, qb * 512:(qb + 1) * 512], prep_r)],
                  v=lambda h, kt: (Vm[:, kt, h, :], prep_r))
        emit_attention(bd, nc, es3, "mls", [0, 1], 1.0, 64, wfun, out_fn, PS)
        og = sb3("l_og", [128, SEQ], F32)
        og_r = Res()
        bd.dma("sp", og[:], pT[R_O:R_O + 128, :], writes=[og_r])
        bd.op("act", lambda e: e.activation(out=og[:], in_=og[:], func=AF.Sigmoid), reads=[og_r], writes=[og_r])
        bd.op("dve", lambda e: e.tensor_tensor(out=Ym[:], in0=Ym[:], in1=og[:], op=ALU.mult),
              reads=[ym_r, og_r], writes=[ym_r])
        bd.dma("sp", yT[384:512, :], Ym[:], reads=[ym_r])
        bd.barrier()


RW_T = 16


def emit_rwkv(bd, nc, es, D_, cst, cst_r, K):
    sb = lambda nm, shape, dt: es.enter_context(nc.sbuf_tensor(_u(nm), shape, dt))
    pT = D_["pT"].ap()
    yT = D_["yT"].ap()
    scr = D_["scr"]
    banks, banks_r = K["banks"], K["banks_r"]
    bones, identf, mats_r = K["bones"], K["identf"], K["mats_r"]
    R_R, R_K, R_V, R_L = 640, 768, 896, 1024
    vS = sb("r_vS", [128, SEQ], F32)
    gS = sb("r_gS", [128, SEQ], F32)
    boS = sb("r_boS", [128, SEQ], F32)
    yS = sb("r_yS", [128, SEQ], F32)
    vS_r, gS_r, boS_r, yS_r = Res(), Res(), Res(), Res()
    wl = sb("r_wl", [128, 3, 128], F32)
    wl_r = Res()
    bd.dma("sp", wl[:], D_["wl"].ap(), writes=[wl_r])
    c2 = sb("r_c2", [128, 2], F32)
    c2_r = Res()
    bd.op("dve", lambda e: e.tensor_scalar(out=c2[:, 0:1], in0=cst[:, C_KA:C_KA + 1], scalar1=-1.0, scalar2=1.0,
                                           op0=ALU.mult, op1=ALU.add), reads=[cst_r], writes=[c2_r])
    scr_r = Res()
    with ExitStack() as es2:
        sb2 = lambda nm, shape, dt: es2.enter_context(nc.sbuf_tensor(_u(nm), shape, dt))
        rS = sb2("r_rS", [128, SEQ], F32)
        kS = sb2("r_kS", [128, SEQ], F32)
        lS = sb2("r_lS", [128, SEQ], F32)
        dd = sb2("r_dd", [128, SEQ], F32)
        rS_r, kS_r, lS_r, dd_r = Res(), Res(), Res(), Res()
        for (row0, t, t_r, mu) in ((R_R, rS, rS_r, C_MU_R), (R_K, kS, kS_r, C_MU_K), (R_V, vS, vS_r, C_MU_V),
                                   (R_L, lS, lS_r, C_MU_L)):
            bd.dma("sp", t[:], pT[row0:row0 + 128, :], writes=[t_r])
            bd.op("dve", lambda e, t=t: e.tensor_tensor(out=dd[:, 1:SEQ], in0=t[:, 0:SEQ - 1], in1=t[:, 1:SEQ],
                                                        op=ALU.subtract), reads=[t_r], writes=[dd_r])
            bd.op("dve", lambda e, t=t: e.tensor_scalar(out=dd[:, 0:1], in0=t[:, 0:1], scalar1=-1.0, scalar2=None,
                                                        op0=ALU.mult), reads=[t_r], writes=[dd_r])
            bd.op("dve", lambda e, t=t, mu=mu: e.scalar_tensor_tensor(out=t[:], in0=dd[:], scalar=cst[:, mu:mu + 1],
                                                                      in1=t[:], op0=ALU.mult, op1=ALU.add),
                  reads=[dd_r, t_r, cst_r], writes=[t_r])
        names = ["th", "sg", "sgm", "wd", "aT", "kkr", "sq", "nrm", "nkk", "bb", "t1", "km", "prod"]
        T_ = {n: sb2("r_" + n, [128, 512], F32) for n in names}
        T_r = {n: Res() for n in names}
        stg = [sb2(f"r_stg{i}", [128, 5, 128], F32) for i in range(2)]
        stg_r = [Res(), Res()]
        bi = [0]

        def nb():
            b = bi[0] % 7
            bi[0] += 1
            return banks[b], banks_r[b]

        def A(fn, reads, writes):
            bd.op("act", fn, reads=reads, writes=writes)

        def V(fn, reads, writes):
            bd.op("dve", fn, reads=reads, writes=writes)

        ti = 0
        for tb in range(SEQ // 512):
            sl = slice(tb * 512, (tb + 1) * 512)
            A(lambda e: e.activation(out=T_["th"][:], in_=lS[:, sl], func=AF.Tanh), [lS_r], [T_r["th"]])
            A(lambda e: e.activation(out=T_["sg"][:], in_=lS[:, sl], func=AF.Sigmoid), [lS_r], [T_r["sg"]])
            pw, pw_r = nb()
            bd.op("pe", lambda e: e.matmul(pw[:, :], lhsT=wl[:, 0, :], rhs=T_["th"][:], start=True, stop=True),
                  reads=[wl_r, T_r["th"]], writes=[pw_r])
            A(lambda e: e.activation(out=T_["sgm"][:], in_=pw[:, :], func=AF.Sigmoid, bias=cst[:, C_W0:C_W0 + 1]),
              [pw_r, cst_r], [T_r["sgm"]])
            A(lambda e: e.activation(out=T_["wd"][:], in_=T_["sgm"][:], func=AF.Exp, scale=-float(np.exp(-0.5))),
              [T_r["sgm"]], [T_r["wd"]])
            pa, pa_r = nb()
            bd.op("pe", lambda e: e.matmul(pa[:, :], lhsT=wl[:, 1, :], rhs=lS[:, sl], start=True, stop=True),
                  reads=[wl_r, lS_r], writes=[pa_r])
            A(lambda e: e.activation(out=T_["aT"][:], in_=pa[:, :], func=AF.Sigmoid, bias=cst[:, C_A0:C_A0 + 1]),
              [pa_r, cst_r], [T_r["aT"]])
            pg, pg_r = nb()
            bd.op("pe", lambda e: e.matmul(pg[:, :], lhsT=wl[:, 2, :], rhs=T_["sg"][:], start=True, stop=True),
                  reads=[wl_r, T_r["sg"]], writes=[pg_r])
            A(lambda e: e.activation(out=gS[:, sl], in_=pg[:, :], func=AF.Copy), [pg_r], [gS_r])
            V(lambda e: e.tensor_scalar(out=T_["kkr"][:], in0=kS[:, sl], scalar1=cst[:, C_KK:C_KK + 1], scalar2=None,
                                        op0=ALU.mult), [kS_r, cst_r], [T_r["kkr"]])
            A(lambda e: e.activation(out=T_["sq"][:], in_=T_["kkr"][:], func=AF.Square), [T_r["kkr"]], [T_r["sq"]])
            pn, pn_r = nb()
            bd.op("pe", lambda e: e.matmul(pn[:, :], lhsT=bones, rhs=T_["sq"][:], start=True, stop=True),
                  reads=[mats_r, T_r["sq"]], writes=[pn_r])
            A(lambda e: e.activation(out=T_["nrm"][:], in_=pn[:, :], func=AF.Sqrt), [pn_r], [T_r["nrm"]])
            V(lambda e: e.tensor_scalar(out=T_["nrm"][:], in0=T_["nrm"][:], scalar1=1e-12, scalar2=None, op0=ALU.max),
              [T_r["nrm"]], [T_r["nrm"]])
            V(lambda e: e.reciprocal(out=T_["nrm"][:], in_=T_["nrm"][:]), [T_r["nrm"]], [T_r["nrm"]])
            V(lambda e: e.scalar_tensor_tensor(out=T_["nkk"][:], in0=T_["kkr"][:], scalar=-1.0, in1=T_["nrm"][:],
                                               op0=ALU.mult, op1=ALU.mult), [T_r["kkr"], T_r["nrm"]], [T_r["nkk"]])
            V(lambda e: e.scalar_tensor_tensor(out=T_["bb"][:], in0=T_["nkk"][:], scalar=-1.0, in1=T_["aT"][:],
                                               op0=ALU.mult, op1=ALU.mult), [T_r["nkk"], T_r["aT"]], [T_r["bb"]])
            V(lambda e: e.tensor_scalar(out=T_["t1"][:], in0=T_["aT"][:], scalar1=cst[:, C_KA:C_KA + 1],
                                        scalar2=c2[:, 0:1], op0=ALU.mult, op1=ALU.add),
              [T_r["aT"], cst_r, c2_r], [T_r["t1"]])
            V(lambda e: e.tensor_tensor(out=T_["km"][:], in0=kS[:, sl], in1=T_["t1"][:], op=ALU.mult),
              [kS_r, T_r["t1"]], [T_r["km"]])
            V(lambda e: e.scalar_tensor_tensor(out=T_["prod"][:], in0=rS[:, sl], scalar=cst[:, C_RK:C_RK + 1],
                                               in1=T_["km"][:], op0=ALU.mult, op1=ALU.mult),
              [rS_r, cst_r, T_r["km"]], [T_r["prod"]])
            pb_, pb_r = nb()
            bd.op("pe", lambda e: e.matmul(pb_[:, :], lhsT=bones, rhs=T_["prod"][:], start=True, stop=True),
                  reads=[mats_r, T_r["prod"]], writes=[pb_r])
            V(lambda e: e.tensor_tensor(out=boS[:, sl], in0=pb_[:, :], in1=vS[:, sl], op=ALU.mult),
              [pb_r, vS_r], [boS_r])
            for j in range(4):
                t0 = tb * 512 + j * 128
                px, px_r = nb()
                py, py_r = nb()
                srcs = [(T_["nkk"][:, j * 128:(j + 1) * 128], T_r["nkk"]), (T_["wd"][:, j * 128:(j + 1) * 128], T_r["wd"]),
                        (T_["bb"][:, j * 128:(j + 1) * 128], T_r["bb"]), (T_["km"][:, j * 128:(j + 1) * 128], T_r["km"]),
                        (rS[:, t0:t0 + 128], rS_r)]
                for q, (ap_, r_) in enumerate(srcs):
                    if q < 4:
                        bd.op("pe", lambda e, q=q, ap_=ap_: e.transpose(out=px[:, q * 128:(q + 1) * 128], in_=ap_,
                                                                       identity=identf), reads=[r_, mats_r], writes=[px_r])
                    else:
                        bd.op("pe", lambda e, ap_=ap_: e.transpose(out=py[:, 0:128], in_=ap_, identity=identf),
                              reads=[r_, mats_r], writes=[py_r])
                sg_, sg_r = stg[ti % 2], stg_r[ti % 2]
                ti += 1
                A(lambda e, sg_=sg_: e.activation(out=sg_[:, 0:4, :], in_=px[:, :].rearrange("p (q c) -> p q c", q=4),
                                                  func=AF.Copy), [px_r], [sg_r])
                V(lambda e, sg_=sg_: e.tensor_copy(out=sg_[:, 4, :], in_=py[:, 0:128]), [py_r], [sg_r])
                for h in range(2):
                    dst = bass.AP(scr, h * SEQ * 320 + t0 * 320, [[320, 128], [64, 5], [1, 64]])
                    bd.dma("sp" if h == 0 else "act", dst, sg_[:, :, h * 64:(h + 1) * 64], reads=[sg_r], writes=[scr_r])
        bd.barrier()
    with ExitStack() as es3:
        sb3 = lambda nm, shape, dt: es3.enter_context(nc.sbuf_tensor(_u(nm), shape, dt))
        T = RW_T
        NB = 3
        BC = [sb3(f"r_BC{i}", [128, T, 5, 64], F32) for i in range(NB)]
        BC_r = [Res() for _ in range(NB)]
        S = sb3("r_S", [128, 64], F32)
        junk = sb3("r_junk", [128, 64], F32)
        sa = sb3("r_sa", [128, 1], F32)
        S_r, junk_r, sa_r = Res(), Res(), Res()
        bd.op("dve", lambda e: e.memset(S[:], 0.0), writes=[S_r])
        nchunk = SEQ // T

        def load(ci):
            b = ci % NB
            for h in range(2):
                src = bass.AP(scr, h * SEQ * 320 + ci * T * 320, [[0, 64], [1, T * 320]])
                bd.dma("sp" if h == 0 else "act", BC[b][h * 64:(h + 1) * 64, :, :, :].rearrange("p t q j -> p (t q j)"),
                       src, reads=[scr_r], writes=[BC_r[b]])

        load(0)
        load(1)
        for ci in range(nchunk):
            if ci + 2 < nchunk:
                load(ci + 2)
            b = ci % NB
            bc, bc_r = BC[b], BC_r[b]
            for tt in range(T):
                t = ci * T + tt
                bd.op("dve", lambda e, bc=bc, tt=tt: e.scalar_tensor_tensor(
                    out=junk[:], in0=S[:], scalar=1.0, in1=bc[:, tt, 0, :], op0=ALU.mult, op1=ALU.mult,
                    accum_out=sa[:, 0:1]), reads=[S_r, bc_r], writes=[junk_r, sa_r])
                bd.op("dve", lambda e, bc=bc, tt=tt: e.tensor_tensor(out=S[:], in0=S[:], in1=bc[:, tt, 1, :], op=ALU.mult),
                      reads=[S_r, bc_r], writes=[S_r])
                bd.op("dve", lambda e, bc=bc, tt=tt: e.scalar_tensor_tensor(
                    out=S[:], in0=bc[:, tt, 2, :], scalar=sa[:, 0:1], in1=S[:], op0=ALU.mult, op1=ALU.add),
                    reads=[S_r, bc_r, sa_r], writes=[S_r])
                bd.op("dve", lambda e, bc=bc, tt=tt, t=t: e.scalar_tensor_tensor(
                    out=S[:], in0=bc[:, tt, 3, :], scalar=vS[:, t:t + 1], in1=S[:], op0=ALU.mult, op1=ALU.add),
                    reads=[S_r, bc_r, vS_r], writes=[S_r])
                bd.op("dve", lambda e, bc=bc, tt=tt, t=t: e.scalar_tensor_tensor(
                    out=junk[:], in0=S[:], scalar=1.0, in1=bc[:, tt, 4, :], op0=ALU.mult, op1=ALU.mult,
                    accum_out=yS[:, t:t + 1]), reads=[S_r, bc_r], writes=[junk_r, yS_r])
        bd.barrier()
    with ExitStack() as es4:
        sb4 = lambda nm, shape, dt: es4.enter_context(nc.sbuf_tensor(_u(nm), shape, dt))
        yc = sb4("r_yc", [128, 512], F32)
        sq = sb4("r_sq2", [128, 512], F32)
        rs = sb4("r_rs", [128, 512], F32)
        yo = [sb4(f"r_yo{i}", [128, 512], BF16) for i in range(2)]
        yc_r, sq_r, rs_r = Res(), Res(), Res()
        yo_r = [Res(), Res()]
        for tb in range(SEQ // 512):
            sl = slice(tb * 512, (tb + 1) * 512)
            pm, pm_r = banks[tb % 2], banks_r[tb % 2]
            pv, pv_r = banks[2 + tb % 2], banks_r[2 + tb % 2]
            bd.op("pe", lambda e: e.matmul(pm[:, :], lhsT=bones, rhs=yS[:, sl], start=True, stop=True),
                  reads=[mats_r, yS_r], writes=[pm_r])
            bd.op("dve", lambda e: e.scalar_tensor_tensor(out=yc[:], in0=pm[:, :], scalar=-1.0 / 64, in1=yS[:, sl],
                                                          op0=ALU.mult, op1=ALU.add), reads=[pm_r, yS_r], writes=[yc_r])
            bd.op("act", lambda e: e.activation(out=sq[:], in_=yc[:], func=AF.Square), reads=[yc_r], writes=[sq_r])
            bd.op("pe", lambda e: e.matmul(pv[:, :], lhsT=bones, rhs=sq[:], start=True, stop=True),
                  reads=[mats_r, sq_r], writes=[pv_r])
            bd.op("act", lambda e: e.activation(out=rs[:], in_=pv[:, :], func=AF.Sqrt, scale=1.0 / 64,
                                                bias=cst[:, C_LNEPS:C_LNEPS + 1]), reads=[pv_r, cst_r], writes=[rs_r])
            bd.op("dve", lambda e: e.reciprocal(out=rs[:], in_=rs[:]), reads=[rs_r], writes=[rs_r])
            bd.op("dve", lambda e: e.tensor_tensor(out=yc[:], in0=yc[:], in1=rs[:], op=ALU.mult),
                  reads=[yc_r, rs_r], writes=[yc_r])
            bd.op("dve", lambda e: e.tensor_scalar(out=yc[:], in0=yc[:], scalar1=cst[:, C_LNW:C_LNW + 1],
                                                   scalar2=cst[:, C_LNB:C_LNB + 1], op0=ALU.mult, op1=ALU.add),
                  reads=[yc_r, cst_r], writes=[yc_r])
            bd.op("dve", lambda e: e.tensor_tensor(out=yc[:], in0=yc[:], in1=boS[:, sl], op=ALU.add),
                  reads=[yc_r, boS_r], writes=[yc_r])
            o, o_r = yo[tb % 2], yo_r[tb % 2]
            bd.op("dve", lambda e, o=o: e.tensor_tensor(out=o[:], in0=yc[:], in1=gS[:, sl], op=ALU.mult),
                  reads=[yc_r, gS_r], writes=[o_r])
            bd.dma("sp", yT[256:384, sl], o[:], reads=[o_r])
        bd.barrier()


RWKV_CHUNKED = True
import os as _os
_DBG_STAGE = int(_os.environ.get('RW_DBG', '0'))


def rwkv_masks():
    s_ = (np.arange(128) % 64)[:, None]
    t_ = (np.arange(512) % 64)[None, :]
    m = np.zeros((128, 5, 512), np.float32)
    m[:, 0, :] = (s_ < t_)
    m[:, 1, :] = (s_ <= t_)
    m[:, 2, :] = (s_ > t_)
    m[:, 3, :] = (s_ == t_)
    m[:, 4, :] = np.broadcast_to(t_ != 0, (128, 512))
    return m


def emit_rwkv_chunked(bd, nc, es, D_, cst, cst_r, K):
    sb = lambda nm, shape, dt: es.enter_context(nc.sbuf_tensor(_u(nm), shape, dt))
    pT = D_["pT"].ap()
    yT = D_["yT"].ap()
    banks, banks_r = K["banks"], K["banks_r"]
    bones, identf, mats_r = K["bones"], K["identf"], K["mats_r"]
    R_R, R_K, R_V, R_L = 640, 768, 896, 1024
    rS = sb("r_rS", [128, SEQ], F32)
    kS = sb("r_kS", [128, SEQ], F32)
    vS = sb("r_vS", [128, SEQ], F32)
    lS = sb("r_lS", [128, SEQ], F32)
    rS_r, kS_r, vS_r, lS_r = Res(), Res(), Res(), Res()
    wl = sb("r_wl", [128, 3, 128], F32)
    wl_r = Res()
    bd.dma("sp", wl[:], D_["wl"].ap(), writes=[wl_r])
    msk = sb("r_msk", [128, 5, 512], F32)
    msk_r = Res()
    bd.dma("act", msk[:], D_["rwm"].ap(), writes=[msk_r])
    c2 = sb("r_c2", [128, 2], F32)
    c2_r = Res()
    bd.op("dve", lambda e: e.tensor_scalar(out=c2[:, 0:1], in0=cst[:, C_KA:C_KA + 1], scalar1=-1.0, scalar2=1.0,
                                           op0=ALU.mult, op1=ALU.add), reads=[cst_r], writes=[c2_r])
    with ExitStack() as es2:
        sb2 = lambda nm, shape, dt: es2.enter_context(nc.sbuf_tensor(_u(nm), shape, dt))
        dd = sb2("r_dd", [128, SEQ], F32)
        dd_r = Res()
        for (row0, t, t_r, mu) in ((R_R, rS, rS_r, C_MU_R), (R_K, kS, kS_r, C_MU_K), (R_V, vS, vS_r, C_MU_V),
                                   (R_L, lS, lS_r, C_MU_L)):
            bd.dma("sp", t[:], pT[row0:row0 + 128, :], writes=[t_r])
            bd.op("dve", lambda e, t=t: e.tensor_tensor(out=dd[:, 1:SEQ], in0=t[:, 0:SEQ - 1], in1=t[:, 1:SEQ],
                                                        op=ALU.subtract), reads=[t_r], writes=[dd_r])
            bd.op("dve", lambda e, t=t: e.tensor_scalar(out=dd[:, 0:1], in0=t[:, 0:1], scalar1=-1.0, scalar2=None,
                                                        op0=ALU.mult), reads=[t_r], writes=[dd_r])
            bd.op("dve", lambda e, t=t, mu=mu: e.scalar_tensor_tensor(out=t[:], in0=dd[:], scalar=cst[:, mu:mu + 1],
                                                                      in1=t[:], op0=ALU.mult, op1=ALU.add),
                  reads=[dd_r, t_r, cst_r], writes=[t_r])
        bd.barrier()
    names = ["th", "sg", "sgm", "aT", "kkr", "sq", "nrm", "nkk", "bb", "t1", "km", "prod", "gB", "boB",
             "lw", "cl", "e1", "e2", "e3", "At", "Bt", "Kt", "Rt", "Bh", "Kh",
             "Mab", "Lab", "Mkb", "Nbr", "Nkr", "T", "TT", "Mk0", "Mk1", "Lk0", "Lk1",
             "VT", "BhT", "KhT", "yB", "yc", "sq2", "rs", "Tb", "TTb"]
    BFN = {"At", "Bt", "Kt", "Rt", "Mab", "Lab", "Mkb", "Nbr", "Nkr", "Mk0", "Mk1", "Lk0", "Lk1",
           "VT", "BhT", "KhT", "Tb", "TTb"}
    T_ = {n: sb("r_" + n, [128, 512], BF16 if n in BFN else F32) for n in names}
    T_r = {n: Res() for n in names}
    UT = sb("r_UT", [128, 4, 128], BF16)
    UT_r = Res()
    xts = sb("r_xts", [128, 128], BF16)
    xts_r = Res()
    ST = sb("r_ST", [128, 64], F32)
    ST_r = Res()
    yo = [sb(f"r_yo{i}", [128, 512], BF16) for i in range(2)]
    yo_r = [Res(), Res()]
    STb = sb("r_STb", [128, 64], BF16)
    STb_r = Res()
    bd.op("dve", lambda e: e.memset(ST[:], 0.0), writes=[ST_r])
    bd.op("dve", lambda e: e.memset(STb[:], 0.0), writes=[STb_r])
    bX, bX_r = banks[3], banks_r[3]
    bU, bU_r = banks[4], banks_r[4]
    bS, bS_r = banks[5], banks_r[5]
    bY, bY_r = banks[6], banks_r[6]
    bi = [0]

    def nb():
        b = bi[0] % 3
        bi[0] += 1
        return banks[b], banks_r[b]

    def A(fn, reads, writes):
        bd.op("act", fn, reads=reads, writes=writes)

    def V(fn, reads, writes):
        bd.op("dve", fn, reads=reads, writes=writes)

    def G(fn, reads, writes):
        bd.op("pool", fn, reads=reads, writes=writes)

    def slot(c, h):
        return ((c // 2) * 2 + h) * 64

    def fam(out_name, lname, rname, mask_k):
        pb_, pb_r = nb()
        for h in range(2):
            H0 = h * 64
            for c in range(8):
                P0 = (c % 2) * 64
                o = slot(c, h)
                bd.op("pe", lambda e, P0=P0, H0=H0, o=o, c=c: e.matmul(
                    pb_[P0:P0 + 64, o:o + 64], lhsT=T_[lname][H0:H0 + 64, c * 64:(c + 1) * 64],
                    rhs=T_[rname][H0:H0 + 64, c * 64:(c + 1) * 64], start=True, stop=True),
                    reads=[T_r[lname], T_r[rname]], writes=[pb_r], rt=H0)
        V(lambda e: e.tensor_tensor(out=T_[out_name][:], in0=pb_[:, :], in1=msk[:, mask_k, :], op=ALU.mult),
          [pb_r, msk_r], [T_r[out_name]])

    def sq16(out_name, lname, rname, add_name=None):
        pb_, pb_r = nb()
        for cpar in range(2):
            P0 = cpar * 64
            grp = [(c, h) for c in range(cpar, 8, 2) for h in range(2)]
            for gi, (c, h) in enumerate(grp):
                o = slot(c, h)
                bd.op("pe", lambda e, P0=P0, o=o: e.matmul(
                    pb_[P0:P0 + 64, o:o + 64], lhsT=T_[lname][P0:P0 + 64, o:o + 64],
                    rhs=T_[rname][P0:P0 + 64, o:o + 64], start=True, stop=True),
                    reads=[T_r[lname], T_r[rname]], writes=[pb_r], rt=P0)
        if add_name is None:
            A(lambda e: e.activation(out=T_[out_name][:], in_=pb_[:, :], func=AF.Copy), [pb_r], [T_r[out_name]])
        else:
            V(lambda e: e.tensor_tensor(out=T_[out_name][:], in0=pb_[:, :], in1=T_[add_name][:], op=ALU.add),
              [pb_r, T_r[add_name]], [T_r[out_name]])

    def tr4(out_name, src_ap_fn, src_res):
        pb_, pb_r = nb()
        for c2_ in range(4):
            bd.op("pe", lambda e, c2_=c2_: e.transpose(out=pb_[:, c2_ * 128:(c2_ + 1) * 128], in_=src_ap_fn(c2_),
                                                       identity=identf), reads=[src_res, mats_r], writes=[pb_r])
        A(lambda e: e.activation(out=T_[out_name][:], in_=pb_[:, :], func=AF.Copy), [pb_r], [T_r[out_name]])

    for tb in range(SEQ // 512):
        sl = slice(tb * 512, (tb + 1) * 512)
        A(lambda e: e.activation(out=T_["th"][:], in_=lS[:, sl], func=AF.Tanh), [lS_r], [T_r["th"]])
        A(lambda e: e.activation(out=T_["sg"][:], in_=lS[:, sl], func=AF.Sigmoid), [lS_r], [T_r["sg"]])
        pw, pw_r = nb()
        bd.op("pe", lambda e: e.matmul(pw[:, :], lhsT=wl[:, 0, :], rhs=T_["th"][:], start=True, stop=True),
              reads=[wl_r, T_r["th"]], writes=[pw_r])
        A(lambda e: e.activation(out=T_["sgm"][:], in_=pw[:, :], func=AF.Sigmoid, bias=cst[:, C_W0:C_W0 + 1]),
          [pw_r, cst_r], [T_r["sgm"]])
        V(lambda e: e.tensor_scalar(out=T_["lw"][:], in0=T_["sgm"][:], scalar1=-float(np.exp(-0.5)), scalar2=None,
                                    op0=ALU.mult), [T_r["sgm"]], [T_r["lw"]])
        pa, pa_r = nb()
        bd.op("pe", lambda e: e.matmul(pa[:, :], lhsT=wl[:, 1, :], rhs=lS[:, sl], start=True, stop=True),
              reads=[wl_r, lS_r], writes=[pa_r])
        A(lambda e: e.activation(out=T_["aT"][:], in_=pa[:, :], func=AF.Sigmoid, bias=cst[:, C_A0:C_A0 + 1]),
          [pa_r, cst_r], [T_r["aT"]])
        pg, pg_r = nb()
        bd.op("pe", lambda e: e.matmul(pg[:, :], lhsT=wl[:, 2, :], rhs=T_["sg"][:], start=True, stop=True),
              reads=[wl_r, T_r["sg"]], writes=[pg_r])
        A(lambda e: e.activation(out=T_["gB"][:], in_=pg[:, :], func=AF.Copy), [pg_r], [T_r["gB"]])
        V(lambda e: e.tensor_scalar(out=T_["kkr"][:], in0=kS[:, sl], scalar1=cst[:, C_KK:C_KK + 1], scalar2=None,
                                    op0=ALU.mult), [kS_r, cst_r], [T_r["kkr"]])
        A(lambda e: e.activation(out=T_["sq"][:], in_=T_["kkr"][:], func=AF.Square), [T_r["kkr"]], [T_r["sq"]])
        pn, pn_r = nb()
        bd.op("pe", lambda e: e.matmul(pn[:, :], lhsT=bones, rhs=T_["sq"][:], start=True, stop=True),
              reads=[mats_r, T_r["sq"]], writes=[pn_r])
        A(lambda e: e.activation(out=T_["nrm"][:], in_=pn[:, :], func=AF.Sqrt), [pn_r], [T_r["nrm"]])
        V(lambda e: e.tensor_scalar(out=T_["nrm"][:], in0=T_["nrm"][:], scalar1=1e-12, scalar2=None, op0=ALU.max),
          [T_r["nrm"]], [T_r["nrm"]])
        V(lambda e: e.reciprocal(out=T_["nrm"][:], in_=T_["nrm"][:]), [T_r["nrm"]], [T_r["nrm"]])
        V(lambda e: e.scalar_tensor_tensor(out=T_["nkk"][:], in0=T_["kkr"][:], scalar=-1.0, in1=T_["nrm"][:],
                                           op0=ALU.mult, op1=ALU.mult), [T_r["kkr"], T_r["nrm"]], [T_r["nkk"]])
        V(lambda e: e.scalar_tensor_tensor(out=T_["bb"][:], in0=T_["nkk"][:], scalar=-1.0, in1=T_["aT"][:],
                                           op0=ALU.mult, op1=ALU.mult), [T_r["nkk"], T_r["aT"]], [T_r["bb"]])
        V(lambda e: e.tensor_scalar(out=T_["t1"][:], in0=T_["aT"][:], scalar1=cst[:, C_KA:C_KA + 1],
                                    scalar2=c2[:, 0:1], op0=ALU.mult, op1=ALU.add),
          [T_r["aT"], cst_r, c2_r], [T_r["t1"]])
        V(lambda e: e.tensor_tensor(out=T_["km"][:], in0=kS[:, sl], in1=T_["t1"][:], op=ALU.mult),
          [kS_r, T_r["t1"]], [T_r["km"]])
        V(lambda e: e.scalar_tensor_tensor(out=T_["prod"][:], in0=rS[:, sl], scalar=cst[:, C_RK:C_RK + 1],
                                           in1=T_["km"][:], op0=ALU.mult, op1=ALU.mult),
          [rS_r, cst_r, T_r["km"]], [T_r["prod"]])
        pb2, pb2_r = nb()
        bd.op("pe", lambda e: e.matmul(pb2[:, :], lhsT=bones, rhs=T_["prod"][:], start=True, stop=True),
              reads=[mats_r, T_r["prod"]], writes=[pb2_r])
        V(lambda e: e.tensor_tensor(out=T_["boB"][:], in0=pb2[:, :], in1=vS[:, sl], op=ALU.mult),
          [pb2_r, vS_r], [T_r["boB"]])
        V(lambda e: e.tensor_tensor_scan(out=T_["cl"][:], data0=msk[:, 4, :], data1=T_["lw"][:], initial=0.0,
                                         op0=ALU.mult, op1=ALU.add), [msk_r, T_r["lw"]], [T_r["cl"]])
        A(lambda e: e.activation(out=T_["e1"][:], in_=T_["cl"][:], func=AF.Exp), [T_r["cl"]], [T_r["e1"]])
        A(lambda e: e.activation(out=T_["e2"][:], in_=T_["cl"][:], func=AF.Exp, scale=-1.0), [T_r["cl"]], [T_r["e2"]])
        V(lambda e: e.tensor_tensor(out=T_["e3"][:], in0=T_["cl"][:], in1=T_["lw"][:], op=ALU.subtract),
          [T_r["cl"], T_r["lw"]], [T_r["e3"]])
        A(lambda e: e.activation(out=T_["e3"][:], in_=T_["e3"][:], func=AF.Exp), [T_r["e3"]], [T_r["e3"]])
        V(lambda e: e.tensor_tensor(out=T_["At"][:], in0=T_["nkk"][:], in1=T_["e3"][:], op=ALU.mult),
          [T_r["nkk"], T_r["e3"]], [T_r["At"]])
        G(lambda e: e.tensor_tensor(out=T_["Bt"][:], in0=T_["bb"][:], in1=T_["e2"][:], op=ALU.mult),
          [T_r["bb"], T_r["e2"]], [T_r["Bt"]])
        V(lambda e: e.tensor_tensor(out=T_["Kt"][:], in0=T_["km"][:], in1=T_["e2"][:], op=ALU.mult),
          [T_r["km"], T_r["e2"]], [T_r["Kt"]])
        G(lambda e: e.tensor_tensor(out=T_["Rt"][:], in0=rS[:, sl], in1=T_["e1"][:], op=ALU.mult),
          [rS_r, T_r["e1"]], [T_r["Rt"]])
        for c in range(8):
            gam = T_["e1"][:, c * 64 + 63:c * 64 + 64]
            V(lambda e, c=c, gam=gam: e.tensor_scalar(out=T_["Bh"][:, c * 64:(c + 1) * 64], in0=T_["Bt"][:, c * 64:(c + 1) * 64],
                                                      scalar1=gam, scalar2=None, op0=ALU.mult),
              [T_r["Bt"], T_r["e1"]], [T_r["Bh"]])
            G(lambda e, c=c, gam=gam: e.tensor_scalar(out=T_["Kh"][:, c * 64:(c + 1) * 64], in0=T_["Kt"][:, c * 64:(c + 1) * 64],
                                                      scalar1=gam, scalar2=None, op0=ALU.mult),
              [T_r["Kt"], T_r["e1"]], [T_r["Kh"]])
        if _DBG_STAGE == 1:
            break
        fam("Mab", "Bt", "At", 0)
        fam("Lab", "At", "Bt", 2)
        fam("Mkb", "Kt", "At", 0)
        fam("Nbr", "Bt", "Rt", 1)
        fam("Nkr", "Kt", "Rt", 1)
        if _DBG_STAGE == 2:
            break
        V(lambda e: e.tensor_tensor(out=T_["T"][:], in0=T_["Mab"][:], in1=msk[:, 3, :], op=ALU.add),
          [T_r["Mab"], msk_r], [T_r["T"]])
        G(lambda e: e.tensor_tensor(out=T_["TT"][:], in0=T_["Lab"][:], in1=msk[:, 3, :], op=ALU.add),
          [T_r["Lab"], msk_r], [T_r["TT"]])
        A(lambda e: e.activation(out=T_["Tb"][:], in_=T_["T"][:], func=AF.Copy), [T_r["T"]], [T_r["Tb"]])
        G(lambda e: e.tensor_copy(out=T_["TTb"][:], in_=T_["TT"][:]), [T_r["TT"]], [T_r["TTb"]])
        Mp, Lp = "Mab", "Lab"
        for lev in range(1, 6):
            Mn, Ln = f"Mk{lev % 2}", f"Lk{lev % 2}"
            sq16(Mn, Lp, Mp)
            if lev < 5:
                sq16(Ln, Mp, Lp)
            sq16("T", "TTb", Mn, add_name="T")
            if lev < 5:
                sq16("TT", Mn, "TTb", add_name="TT")
                G(lambda e: e.tensor_copy(out=T_["TTb"][:], in_=T_["TT"][:]), [T_r["TT"]], [T_r["TTb"]])
            A(lambda e: e.activation(out=T_["Tb"][:], in_=T_["T"][:], func=AF.Copy), [T_r["T"]], [T_r["Tb"]])
            Mp, Lp = Mn, Ln
        if _DBG_STAGE == 3:
            break
        tr4("VT", lambda c2_: vS[:, tb * 512 + c2_ * 128: tb * 512 + (c2_ + 1) * 128], vS_r)
        tr4("BhT", lambda c2_: T_["Bh"][:, c2_ * 128:(c2_ + 1) * 128], T_r["Bh"])
        tr4("KhT", lambda c2_: T_["Kh"][:, c2_ * 128:(c2_ + 1) * 128], T_r["Kh"])
        if _DBG_STAGE == 4:
            break
        for c in range(8):
            P0 = (c % 2) * 64
            c2_ = c // 2
            cs = slice(c * 64, (c + 1) * 64)
            for h in range(2):
                H0 = h * 64
                o = slot(c, h)
                vt = T_["VT"][P0:P0 + 64, c2_ * 128 + H0:c2_ * 128 + H0 + 64]
                bd.op("pe", lambda e, P0=P0, H0=H0, cs=cs: e.matmul(
                    bX[P0:P0 + 64, H0:H0 + 64], lhsT=T_["At"][H0:H0 + 64, cs], rhs=STb[H0:H0 + 64, :],
                    start=True, stop=False), reads=[T_r["At"], STb_r], writes=[bX_r], rt=H0)
                bd.op("pe", lambda e, P0=P0, H0=H0, o=o, vt=vt: e.matmul(
                    bX[P0:P0 + 64, H0:H0 + 64], lhsT=T_["Mkb"][P0:P0 + 64, o:o + 64], rhs=vt,
                    start=False, stop=True), reads=[T_r["Mkb"], T_r["VT"]], writes=[bX_r], rt=P0)
            A(lambda e, P0=P0: e.activation(out=xts[P0:P0 + 64, :], in_=bX[P0:P0 + 64, 0:128], func=AF.Copy),
              [bX_r], [xts_r])
            for h in range(2):
                H0 = h * 64
                o = slot(c, h)
                bd.op("pe", lambda e, P0=P0, H0=H0, o=o: e.matmul(
                    bU[P0:P0 + 64, H0:H0 + 64], lhsT=T_["Tb"][P0:P0 + 64, o:o + 64], rhs=xts[P0:P0 + 64, H0:H0 + 64],
                    start=True, stop=True), reads=[T_r["Tb"], xts_r], writes=[bU_r], rt=P0)
            V(lambda e, P0=P0, c2_=c2_: e.tensor_copy(out=UT[P0:P0 + 64, c2_, :], in_=bU[P0:P0 + 64, 0:128]),
              [bU_r], [UT_r])
            for h in range(2):
                H0 = h * 64
                o = slot(c, h)
                ut = UT[P0:P0 + 64, c2_, H0:H0 + 64]
                vt = T_["VT"][P0:P0 + 64, c2_ * 128 + H0:c2_ * 128 + H0 + 64]
                bd.op("pe", lambda e, H0=H0, cs=cs: e.matmul(
                    bY[H0:H0 + 64, cs], lhsT=STb[H0:H0 + 64, :], rhs=T_["Rt"][H0:H0 + 64, cs],
                    start=True, stop=False), reads=[STb_r, T_r["Rt"]], writes=[bY_r], rt=H0)
                bd.op("pe", lambda e, H0=H0, cs=cs, P0=P0, o=o, ut=ut: e.matmul(
                    bY[H0:H0 + 64, cs], lhsT=ut, rhs=T_["Nbr"][P0:P0 + 64, o:o + 64],
                    start=False, stop=False), reads=[UT_r, T_r["Nbr"]], writes=[bY_r], rt=P0)
                bd.op("pe", lambda e, H0=H0, cs=cs, P0=P0, o=o, vt=vt: e.matmul(
                    bY[H0:H0 + 64, cs], lhsT=vt, rhs=T_["Nkr"][P0:P0 + 64, o:o + 64],
                    start=False, stop=True), reads=[T_r["VT"], T_r["Nkr"]], writes=[bY_r], rt=P0)
                bd.op("pe", lambda e, H0=H0, P0=P0, c2_=c2_, ut=ut: e.matmul(
                    bS[H0:H0 + 64, 0:64], lhsT=T_["BhT"][P0:P0 + 64, c2_ * 128 + H0:c2_ * 128 + H0 + 64], rhs=ut,
                    start=True, stop=False), reads=[T_r["BhT"], UT_r], writes=[bS_r], rt=P0)
                bd.op("pe", lambda e, H0=H0, P0=P0, c2_=c2_, vt=vt: e.matmul(
                    bS[H0:H0 + 64, 0:64], lhsT=T_["KhT"][P0:P0 + 64, c2_ * 128 + H0:c2_ * 128 + H0 + 64], rhs=vt,
                    start=False, stop=True), reads=[T_r["KhT"], T_r["VT"]], writes=[bS_r], rt=P0)
            gam = T_["e1"][:, c * 64 + 63:c * 64 + 64]
            V(lambda e, gam=gam: e.scalar_tensor_tensor(out=ST[:], in0=ST[:], scalar=gam, in1=bS[:, 0:64],
                                                        op0=ALU.mult, op1=ALU.add),
              [ST_r, bS_r, T_r["e1"]], [ST_r])
            A(lambda e: e.activation(out=STb[:], in_=ST[:], func=AF.Copy), [ST_r], [STb_r])
        if _DBG_STAGE == 5:
            break
        A(lambda e: e.activation(out=T_["yB"][:], in_=bY[:, :], func=AF.Copy), [bY_r], [T_r["yB"]])
        pm, pm_r = nb()
        bd.op("pe", lambda e: e.matmul(pm[:, :], lhsT=bones, rhs=T_["yB"][:], start=True, stop=True),
              reads=[mats_r, T_r["yB"]], writes=[pm_r])
        V(lambda e: e.scalar_tensor_tensor(out=T_["yc"][:], in0=pm[:, :], scalar=-1.0 / 64, in1=T_["yB"][:],
                                           op0=ALU.mult, op1=ALU.add), [pm_r, T_r["yB"]], [T_r["yc"]])
        A(lambda e: e.activation(out=T_["sq2"][:], in_=T_["yc"][:], func=AF.Square), [T_r["yc"]], [T_r["sq2"]])
        pv, pv_r = nb()
        bd.op("pe", lambda e: e.matmul(pv[:, :], lhsT=bones, rhs=T_["sq2"][:], start=True, stop=True),
              reads=[mats_r, T_r["sq2"]], writes=[pv_r])
        A(lambda e: e.activation(out=T_["rs"][:], in_=pv[:, :], func=AF.Sqrt, scale=1.0 / 64,
                                 bias=cst[:, C_LNEPS:C_LNEPS + 1]), [pv_r, cst_r], [T_r["rs"]])
        V(lambda e: e.reciprocal(out=T_["rs"][:], in_=T_["rs"][:]), [T_r["rs"]], [T_r["rs"]])
        V(lambda e: e.tensor_tensor(out=T_["yc"][:], in0=T_["yc"][:], in1=T_["rs"][:], op=ALU.mult),
          [T_r["yc"], T_r["rs"]], [T_r["yc"]])
        V(lambda e: e.tensor_scalar(out=T_["yc"][:], in0=T_["yc"][:], scalar1=cst[:, C_LNW:C_LNW + 1],
                                    scalar2=cst[:, C_LNB:C_LNB + 1], op0=ALU.mult, op1=ALU.add),
          [T_r["yc"], cst_r], [T_r["yc"]])
        V(lambda e: e.tensor_tensor(out=T_["yc"][:], in0=T_["yc"][:], in1=T_["boB"][:], op=ALU.add),
          [T_r["yc"], T_r["boB"]], [T_r["yc"]])
        o_, o_r = yo[tb % 2], yo_r[tb % 2]
        V(lambda e, o_=o_: e.tensor_tensor(out=o_[:], in0=T_["yc"][:], in1=T_["gB"][:], op=ALU.mult),
          [T_r["yc"], T_r["gB"]], [o_r])
        bd.dma("sp", yT[256:384, sl], o_[:], reads=[o_r])
    bd.barrier()


def emit_phaseB(nc, mkbld, es, D_, mixers, tag):
    bd = mkbld(tag + "m")
    sb = lambda nm, shape, dt: es.enter_context(nc.sbuf_tensor(_u(nm), shape, dt))
    ps = lambda nm, shape, dt: es.enter_context(nc.psum_tensor(_u(nm), shape, dt))
    cst = sb("B_cst", [128, NCST], F32)
    cst_r = Res()
    bd.dma("sp", cst[:], D_["cst"].ap(), writes=[cst_r])
    mats = sb("B_mats", [128, 4, 128], F32)
    mats_r = Res()
    bd.dma("sp", mats[:], D_["mats"].ap(), writes=[mats_r])
    identb = sb("B_identb", [128, 128], BF16)
    maskb = sb("B_maskb", [128, 128], BF16)
    identb_r, maskb_r = Res(), Res()
    bd.op("dve", lambda e: e.tensor_copy(out=identb[:], in_=mats[:, 0, :]), reads=[mats_r], writes=[identb_r])
    bd.op("dve", lambda e: e.tensor_copy(out=maskb[:], in_=mats[:, 3, :]), reads=[mats_r], writes=[maskb_r])
    banks = [ps(f"B_bank{i}", [128, 512], F32) for i in range(7)]
    banks_r = [Res() for _ in range(7)]
    ptb = ps("B_ptb", [128, 1024], BF16)
    ptb_r = Res()
    K = dict(banks=banks, banks_r=banks_r, ptb=ptb, ptb_r=ptb_r, identb=identb, identb_r=identb_r,
             mask=maskb, mask_r=maskb_r, ones=mats[:, 1, :], ones_r=mats_r, identf=mats[:, 0, :],
             bones=mats[:, 2, :], mats_r=mats_r)
    if "mla" in mixers:
        with ExitStack() as es1:
            emit_mla(bd, nc, es1, D_["pT"], D_["pos"], D_["wuq"], D_["wukv"], D_["yT"], cst, cst_r, K)
    bd.barrier()
    if "mlstm" in mixers:
        with ExitStack() as es1:
            emit_mlstm(bd, nc, es1, D_["pT"], D_["yT"], cst, cst_r, K)
        bd.barrier()
    if "rwkv" in mixers:
        bd = mkbld(tag + "r")
        with ExitStack() as es1:
            (emit_rwkv_chunked if RWKV_CHUNKED else emit_rwkv)(bd, nc, es1, D_, cst, cst_r, K)
        bd.barrier()


def prep_phaseB_consts(P, l, hc):
    c = np.zeros((128, NCST), np.float32)
    c[:, C_QN0] = P["mla_q_norm"][l][0:128]
    c[:, C_QN1] = P["mla_q_norm"][l][128:256]
    c[:, C_KVN0] = P["mla_kv_norm"][l][0:128]
    c[:, C_KVN1] = P["mla_kv_norm"][l][128:256]
    c[:, C_INVF] = np.tile(INV_FREQ, 4)
    c[:, C_SGN] = np.tile(np.concatenate([np.full(32, -1.0, np.float32), np.full(32, 1.0, np.float32)]), 2)
    for h in range(2):
        hh = 2 * hc + h
        c[:, C_MLAO0 + h] = P["mla_out_norm"][l][hh * 128:(hh + 1) * 128]
    mu = P["rwkv_mu"][l]
    ch = slice(hc * 128, hc * 128 + 128)
    c[:, C_MU_R] = mu[0:256][ch]
    c[:, C_MU_K] = mu[256:512][ch]
    c[:, C_MU_V] = mu[512:768][ch]
    c[:, C_MU_L] = mu[768:896]
    c[:, C_W0] = P["rwkv_w0"][l][ch]
    c[:, C_A0] = P["rwkv_a0"][l][ch]
    c[:, C_KK] = P["rwkv_k_k"][l][ch]
    c[:, C_KA] = P["rwkv_k_a"][l][ch]
    c[:, C_RK] = P["rwkv_r_k"][l][ch]
    c[:, C_LNW] = P["rwkv_ln_w"][l][ch]
    c[:, C_LNB] = P["rwkv_ln_b"][l][ch]
    cw = P["mlstm_conv_w"][l]
    cb = P["mlstm_conv_b"][l]
    qs = slice(hc * 64, hc * 64 + 64)
    ks = slice(128 + hc * 64, 128 + hc * 64 + 64)
    for j in range(4):
        c[0:64, C_CWQ0 + j] = cw[j][qs]
        c[0:64, C_CWK0 + j] = cw[j][ks]
    c[0:64, C_CBQ] = cb[qs]
    c[0:64, C_CBK] = cb[ks]
    c[:, C_IB] = P["mlstm_i_bias"][l][hc * 2]
    c[:, C_FB] = P["mlstm_f_bias"][l][hc * 2]
    c[:, C_IB1] = P["mlstm_i_bias"][l][hc * 2 + 1]
    c[:, C_FB1] = P["mlstm_f_bias"][l][hc * 2 + 1]
    c[:, C_MLO] = P["mlstm_out_norm"][l][ch]
    c[:, C_EPS] = EPS
    c[:, C_LNEPS] = 64e-5
    c[:, C_ONE] = 1.0
    hq = []
    hkv = []
    for h in range(2):
        hh = 2 * hc + h
        wq = P["mla_w_uq"][l][:, hh * 192:(hh + 1) * 192]
        hq += [wq[:, 0:128], wq[:, 128:192], wq[:, 160:192], wq[:, 128:160]]
        wkv = P["mla_w_ukv"][l][:, hh * 256:(hh + 1) * 256]
        hkv += [wkv]
    wuq = np.ascontiguousarray(np.concatenate(hq, axis=1))
    wukv = np.ascontiguousarray(np.concatenate(hkv, axis=1))
    wl = np.zeros((128, 3, 128), np.float32)
    wl[0:32, 0, :] = P["rwkv_w2"][l][:, ch]
    wl[32:64, 1, :] = P["rwkv_a2"][l][:, ch]
    wl[64:128, 2, :] = P["rwkv_g2"][l][:, ch]
    return dict(cst=c, wuq=wuq, wukv=wukv, wl=wl)


INV_FREQ = (10000.0 ** (-np.arange(0, 64, 2, dtype=np.float32) / np.float32(64))).astype(np.float32)


def const_mats():
    m = np.zeros((128, 4, 128), np.float32)
    m[:, 0, :] = np.eye(128, dtype=np.float32)
    m[:, 1, :] = 1.0
    m[0:64, 2, 0:64] = 1.0
    m[64:128, 2, 64:128] = 1.0
    m[:, 3, :] = (np.arange(128)[:, None] <= np.arange(128)[None, :]).astype(np.float32)
    return m


W_OUT_PERM = (list(range(0, 256)) + list(range(512, 640)) + list(range(768, 896)) +
              list(range(256, 512)) + list(range(640, 768)) + list(range(896, 1024)))
TBC = 256


def emit_phaseC(nc, bd, es, D_, final, ntok):
    sb = lambda name, shape, dt: es.enter_context(nc.sbuf_tensor(_u(name), shape, dt))
    ps = lambda name, shape, dt: es.enter_context(nc.psum_tensor(_u(name), shape, dt))
    wo = sb("C_wo", [128, 8, D], BF16)
    wg = sb("C_wg", [128, 8, DFF], BF16)
    wu = sb("C_wu", [128, 8, DFF], BF16)
    wd = sb("C_wd", [128, 22, D], BF16)
    w_r = Res()
    HS = DFF // 2
    stage = [sb(f"C_stage{i}", [128, HS], F32) for i in range(3)]
    stage_r = [Res() for _ in range(3)]
    gcol = sb("C_gcol", [128, 8], F32)
    gcol_r = Res()
    idf = sb("C_idf", [128, 128], F32)
    ident = sb("C_ident", [128, 128], BF16)
    idf_r, ident_r = Res(), Res()
    bd.dma("sp", gcol[:], D_["g"].ap(), writes=[gcol_r])
    bd.dma("sp", idf[:], D_["ident"].ap(), writes=[idf_r])
    bd.op("dve", lambda e: e.tensor_copy(out=ident[:], in_=idf[:]), reads=[idf_r], writes=[ident_r])
    if final:
        gbc = sb("C_gbc", [128, D], F32)
        gbc_r = Res()
        bd.dma("sp", gbc[:], bass.AP(D_["gf_t"], 0, [[0, 128], [1, D]]), writes=[gbc_r])
    si = [0]
    queues = ("sp", "pool", "act")
    engs = ("dve", "pool", "act")

    def cast(dst_ap, src_ap, ncols, scale_ap=None):
        i = si[0] % 3
        si[0] += 1
        bd.dma(queues[i], stage[i][:, 0:ncols], src_ap, writes=[stage_r[i]])
        eng = engs[i] if scale_ap is None else ("dve" if i != 2 else "act")
        if eng == "act":
            if scale_ap is None:
                bd.op("act", lambda e: e.activation(out=dst_ap, in_=stage[i][:, 0:ncols], func=AF.Copy),
                      reads=[stage_r[i]], writes=[w_r])
            else:
                bd.op("act", lambda e: e.activation(out=dst_ap, in_=stage[i][:, 0:ncols], func=AF.Copy, scale=scale_ap),
                      reads=[stage_r[i], gcol_r], writes=[w_r])
        elif scale_ap is None:
            bd.op(eng, lambda e: e.tensor_copy(out=dst_ap, in_=stage[i][:, 0:ncols]), reads=[stage_r[i]], writes=[w_r])
        else:
            bd.op(eng, lambda e: e.tensor_scalar(out=dst_ap, in0=stage[i][:, 0:ncols], scalar1=scale_ap, scalar2=None,
                                                 op0=ALU.mult), reads=[stage_r[i], gcol_r], writes=[w_r])

    for kc in range(8):
        cast(wo[:, kc, :], D_["wo"].ap()[kc * 128:(kc + 1) * 128, :], D)
    for kc in range(8):
        for hh in range(2):
            cast(wg[:, kc, hh * HS:(hh + 1) * HS], D_["wg"].ap()[kc * 128:(kc + 1) * 128, hh * HS:(hh + 1) * HS], HS,
                 gcol[:, kc:kc + 1])
            cast(wu[:, kc, hh * HS:(hh + 1) * HS], D_["wu"].ap()[kc * 128:(kc + 1) * 128, hh * HS:(hh + 1) * HS], HS,
                 gcol[:, kc:kc + 1])
    for f in range(22):
        cast(wd[:, f, :], D_["wd"].ap()[f * 128:(f + 1) * 128, :], D)

    NJ = TBC // 128
    xs = sb("C_xs", [128, NJ, D], F32)
    xs_r = [Res() for _ in range(NJ)]
    yTb = sb("C_yTb", [128, 8, TBC], BF16)
    yTb_r = Res()
    hT = sb("C_hT", [128, 8, TBC], BF16)
    hT_r = Res()
    AT = sb("C_AT", [128, 22, TBC], BF16)
    AT_r = Res()
    sgt = [sb(f"C_sg{i}", [128, TBC], F32) for i in range(2)]
    sgt_r = [Res(), Res()]
    tmp = dict(ss=sb("C_ss", [128, 4], F32), ss_r=Res(), junk=sb("C_junk", [128, D], BF16), junk_r=Res(),
               xn=sb("C_xn", [128, D], BF16), xn_r=Res(),
               pt=ps("C_pt", [128, D], BF16), pt_r=Res(), eps=sb("C_eps", [128, 1], F32), eps_r=Res())
    bd.op("dve", lambda e: e.memset(tmp["eps"][:], EPS), writes=[tmp["eps_r"]])
    pb = [ps(f"C_pb{i}", [128, 512], F32) for i in range(6)]
    pb_r = [Res() for _ in range(6)]
    x_ap = D_["x"].ap()
    yT_ap = D_["yT"].ap()
    out_ap = D_["out"].ap()
    for blk in range(ntok // TBC):
        t0 = blk * TBC
        bd.dma("sp", yTb[:], yT_ap[:, t0:t0 + TBC].rearrange("(k p) t -> p k t", p=128), writes=[yTb_r])
        for j in range(NJ):
            bd.dma("pool", xs[:, j, :], x_ap[t0 + j * 128:t0 + (j + 1) * 128, :], writes=[xs_r[j]])
            for ch in range(2):
                po, po_r = pb[ch], pb_r[ch]
                for k in range(8):
                    bd.op("pe", lambda e, k=k, j=j, ch=ch, po=po: e.matmul(
                        po[:, :], lhsT=yTb[:, k, j * 128:(j + 1) * 128], rhs=wo[:, k, ch * 512:(ch + 1) * 512],
                        start=(k == 0), stop=(k == 7)), reads=[yTb_r, w_r], writes=[po_r])
                bd.op("dve", lambda e, j=j, ch=ch, po=po: e.tensor_tensor(
                    out=xs[:, j, ch * 512:(ch + 1) * 512], in0=xs[:, j, ch * 512:(ch + 1) * 512], in1=po[:, :], op=ALU.add),
                    reads=[po_r, xs_r[j]], writes=[xs_r[j]])
            emit_rmsnorm_T(bd, xs[:, j, :], xs_r[j], hT, hT_r, j, tmp, ident)
        for f in range(22):
            pg, pg_r = pb[2 + (f % 2) * 2], pb_r[2 + (f % 2) * 2]
            pu, pu_r = pb[3 + (f % 2) * 2], pb_r[3 + (f % 2) * 2]
            for (pp, pp_r, ww) in ((pg, pg_r, wg), (pu, pu_r, wu)):
                for k in range(8):
                    bd.op("pe", lambda e, k=k, f=f, pp=pp, ww=ww: e.matmul(
                        pp[:, 0:TBC], lhsT=ww[:, k, f * 128:(f + 1) * 128], rhs=hT[:, k, :],
                        start=(k == 0), stop=(k == 7)), reads=[w_r, hT_r], writes=[pp_r])
            sg, sg_r = sgt[f % 2], sgt_r[f % 2]
            bd.op("act", lambda e, sg=sg, pg=pg: e.activation(out=sg[:], in_=pg[:, 0:TBC], func=AF.Silu),
                  reads=[pg_r], writes=[sg_r])
            bd.op("dve", lambda e, sg=sg, pu=pu, f=f: e.tensor_tensor(out=AT[:, f, :], in0=sg[:], in1=pu[:, 0:TBC],
                                                                      op=ALU.mult),
                  reads=[sg_r, pu_r], writes=[AT_r])
        for j in range(NJ):
            for ch in range(2):
                pd, pd_r = pb[ch], pb_r[ch]
                for f in range(22):
                    bd.op("pe", lambda e, f=f, j=j, ch=ch, pd=pd: e.matmul(
                        pd[:, :], lhsT=AT[:, f, j * 128:(j + 1) * 128], rhs=wd[:, f, ch * 512:(ch + 1) * 512],
                        start=(f == 0), stop=(f == 21)), reads=[AT_r, w_r], writes=[pd_r])
                bd.op("dve", lambda e, j=j, ch=ch, pd=pd: e.tensor_tensor(
                    out=xs[:, j, ch * 512:(ch + 1) * 512], in0=xs[:, j, ch * 512:(ch + 1) * 512], in1=pd[:, :], op=ALU.add),
                    reads=[pd_r, xs_r[j]], writes=[xs_r[j]])
            if final:
                ss, ss_r = tmp["ss"], tmp["ss_r"]
                bd.op("dve", lambda e, j=j: e.scalar_tensor_tensor(
                    out=tmp["junk"][:], in0=xs[:, j, :], scalar=1.0, in1=xs[:, j, :], op0=ALU.mult, op1=ALU.mult,
                    accum_out=ss[:, 0:1]), reads=[xs_r[j]], writes=[tmp["junk_r"], ss_r])
                bd.op("act", lambda e: e.activation(out=ss[:, 1:2], in_=ss[:, 0:1], func=AF.Sqrt, scale=1.0 / D,
                                                    bias=tmp["eps"][:, 0:1]), reads=[ss_r, tmp["eps_r"]], writes=[ss_r])
                bd.op("dve", lambda e: e.reciprocal(out=ss[:, 2:3], in_=ss[:, 1:2]), reads=[ss_r], writes=[ss_r])
                bd.op("dve", lambda e, j=j: e.scalar_tensor_tensor(
                    out=xs[:, j, :], in0=xs[:, j, :], scalar=ss[:, 2:3], in1=gbc[:], op0=ALU.mult, op1=ALU.mult),
                    reads=[xs_r[j], ss_r, gbc_r], writes=[xs_r[j]])
            bd.dma("sp", out_ap[t0 + j * 128:t0 + (j + 1) * 128, :], xs[:, j, :], reads=[xs_r[j]])
    bd.barrier()


def build_fused(phases=None):
    nc = bass.Bass("TRN2", target_bir_lowering=False)
    dt = nc.dram_tensor
    x_d = dt("x", [SEQ, D], F32, kind="ExternalInput")
    pos_d = dt("pos", [1, SEQ], I32, kind="ExternalInput")
    wA_d = dt("wA", [2, D, NCOLA], F32, kind="ExternalInput")
    gA_d = dt("gA", [2, 128, 8], F32, kind="ExternalInput")
    id_d = dt("ident", [128, 128], F32, kind="ExternalInput")
    mats_d = dt("mats", [128, 4, 128], F32, kind="ExternalInput")
    rwm_d = dt("rwm", [128, 5, 512], F32, kind="ExternalInput")
    wuq_d = dt("wuq", [2, 2, 256, 512], F32, kind="ExternalInput")
    wukv_d = dt("wukv", [2, 2, 256, 512], F32, kind="ExternalInput")
    cst_d = dt("cst", [2, 2, 128, NCST], F32, kind="ExternalInput")
    wl_d = dt("wl", [2, 2, 128, 3, 128], F32, kind="ExternalInput")
    wo_d = dt("wo", [2, D, D], F32, kind="ExternalInput")
    wg_d = dt("wg", [2, D, DFF], F32, kind="ExternalInput")
    wu_d = dt("wu", [2, D, DFF], F32, kind="ExternalInput")
    wd_d = dt("wd", [2, DFF, D], F32, kind="ExternalInput")
    gC_d = dt("gC", [2, 128, 8], F32, kind="ExternalInput")
    gf_d = dt("gf", [1, D], F32, kind="ExternalInput")
    out_d = dt("out", [SEQ, D], F32, kind="ExternalOutput")
    pT_d = dt("pT_scr", [NCOLA, SEQ], F32)
    yT_d = dt("yT_scr", [D, SEQ], BF16)
    scr_d = dt("rw_scr", [2 * SEQ * 320], F32)
    x1_d = dt("x1_scr", [SEQ, D], F32)
    shared = {}
    mkbld = lambda tag: Bld(nc, tag, shared)
    for l in range(2):
        x_in = x_d if l == 0 else x1_d
        x_out = x1_d if l == 0 else out_d
        if phases is None or f"A{l}" in phases:
          with ExitStack() as es:
            bd = mkbld(f"A{l}")
            emit_phaseA(nc, bd, es, x_in, View(wA_d.ap()[l]), View(gA_d.ap()[l]), id_d, pT_d, SEQ)
        for hc in range(2):
            if phases is not None and f"B{l}{hc}" not in phases:
                continue
            D_ = dict(pT=View(pT_d.ap()[hc * HALF_ROWS:(hc + 1) * HALF_ROWS, :]), pos=pos_d,
                      wuq=View(wuq_d.ap()[l, hc]), wukv=View(wukv_d.ap()[l, hc]), cst=View(cst_d.ap()[l, hc]),
                      wl=View(wl_d.ap()[l, hc]), mats=mats_d, yT=View(yT_d.ap()[hc * 512:(hc + 1) * 512, :]),
                      scr=scr_d, rwm=rwm_d)
            with ExitStack() as es:
                emit_phaseB(nc, mkbld, es, D_, ("mla", "mlstm", "rwkv"), f"B{l}{hc}")
        D_ = dict(x=x_in, yT=yT_d, wo=View(wo_d.ap()[l]), wg=View(wg_d.ap()[l]), wu=View(wu_d.ap()[l]),
                  wd=View(wd_d.ap()[l]), g=View(gC_d.ap()[l]), gf_t=gf_d, ident=id_d, out=x_out)
        if phases is None or f"C{l}" in phases:
          with ExitStack() as es:
            bd = mkbld(f"C{l}")
            emit_phaseC(nc, bd, es, D_, l == 1, SEQ)
    return nc


_CACHE = {}
_PHASES = None


def kernel(**inputs):
    P = {k: np.asarray(v) for k, v in inputs.items()}
    x = np.asarray(P["x"], np.float32)
    positions = np.asarray(P["positions"]).astype(np.int32)
    if "nc" not in _CACHE:
        _CACHE["nc"] = build_fused(_PHASES)
    nc = _CACHE["nc"]
    f32 = lambda a: np.ascontiguousarray(np.asarray(a, np.float32))
    wA = f32(np.stack([prep_w_inA(P["w_in"][l]) for l in range(2)]))
    gA = f32(np.stack([_gcol(P["mix_norm"][l]) for l in range(2)]))
    gC = f32(np.stack([_gcol(P["ffn_norm"][l]) for l in range(2)]))
    pb = [[prep_phaseB_consts(P, l, hc) for hc in range(2)] for l in range(2)]
    stk = lambda key: f32(np.stack([np.stack([pb[l][hc][key] for hc in range(2)]) for l in range(2)]))
    common = dict(
        wA=wA, gA=gA, ident=np.eye(128, dtype=np.float32), mats=const_mats(), rwm=rwkv_masks(),
        wuq=stk("wuq"), wukv=stk("wukv"), cst=stk("cst"), wl=stk("wl"),
        wo=f32(np.stack([P["w_out"][l][W_OUT_PERM, :] for l in range(2)])),
        wg=f32(P["w_gate"]), wu=f32(P["w_up"]), wd=f32(P["w_down"]), gC=gC,
        gf=f32(np.asarray(P["final_norm"]).reshape(1, D)))
    in_maps = []
    for c in range(NCORES):
        b = c // 2
        m = dict(common)
        m["x"] = f32(x[b])
        m["pos"] = np.ascontiguousarray(positions[b:b + 1])
        in_maps.append(m)
    res = run_bass_kernel_spmd(nc, in_maps, core_ids=list(range(NCORES)))
    out = np.zeros((4, SEQ, D), np.float32)
    for b in range(4):
        out[b] = res.results[2 * b]["out"]
    return out
```

```python
from contextlib import ExitStack
import numpy as np
import concourse.bass as bass
import concourse.mybir as mybir
from concourse.bass_utils import run_bass_kernel_spmd

F32 = mybir.dt.float32
BF16 = mybir.dt.bfloat16
I32 = mybir.dt.int32
ALU = mybir.AluOpType
AF = mybir.ActivationFunctionType

D = 1024
SEQ = 4096
NTOK = 2048
DFF = 2816
NCORES = 8
EPS = 1e-6

NCH = 13
CH_ROWS = [128] * 12 + [4]
HALF_ROWS = 12 * 128 + 4
CH_OFF = [i * 128 for i in range(13)]
NCOLA = 2 * HALF_ROWS


def half_cols(hc):
    cols = []
    cols += list(range(0, 256))
    cols += list(range(256, 512))
    cols += list(range(512, 576))
    cols += list(range(544, 576)) + list(range(512, 544))
    R0 = 576
    for part in range(3):
        cols += list(range(R0 + part * 256 + hc * 128, R0 + part * 256 + hc * 128 + 128))
    cols += list(range(R0 + 768, R0 + 896))
    M0 = 576 + 896
    cols += list(range(M0 + hc * 64, M0 + hc * 64 + 64))
    cols += list(range(M0 + 128 + hc * 64, M0 + 128 + hc * 64 + 64))
    cols += list(range(M0 + 256 + hc * 128, M0 + 256 + hc * 128 + 128))
    cols += list(range(M0 + 520 + hc * 128, M0 + 520 + hc * 128 + 128))
    cols += list(range(M0 + 512 + hc * 2, M0 + 512 + hc * 2 + 2))
    cols += list(range(M0 + 516 + hc * 2, M0 + 516 + hc * 2 + 2))
    assert len(cols) == HALF_ROWS
    return cols


class Res:
    __slots__ = ("w", "r", "name")

    def __init__(self, name=""):
        self.w = None
        self.r = {}
        self.name = name


class Bld:
    NDMA = 8

    def __init__(self, nc, tag="", shared=None):
        self.nc = nc
        self.E = {"pe": nc.tensor, "dve": nc.vector, "act": nc.scalar, "pool": nc.gpsimd, "sp": nc.sync}
        self.sems = {}
        self.cnt = {}
        self.seen = {e: {} for e in self.E}
        self.touched = set()
        for e in self.E:
            self.sems[e] = nc.alloc_semaphore(name=f"s{tag}_{e}")
            self.cnt[e] = 0
        if shared is not None and "sems" in shared:
            self.sems.update(shared["sems"])
            self.cnt.update(shared["cnt"])
            self.dslot = shared["dslot"]
        else:
            dsems, dcnt = {}, {}
            self.dslot = {}
            for q in ("sp", "act", "pool"):
                for i in range(self.NDMA):
                    k = f"d{q}{i}"
                    dsems[k] = nc.alloc_semaphore(name=f"sdma_{k}")
                    dcnt[k] = 0
                self.dslot[q] = 0
            self.sems.update(dsems)
            self.cnt.update(dcnt)
            if shared is not None:
                shared["sems"] = dsems
                shared["dslot"] = self.dslot
                shared["cnt"] = {}
        self.shared = shared

    def _sync_shared(self):
        if self.shared is not None:
            for k in self.shared["sems"]:
                self.shared["cnt"][k] = self.cnt[k]

    def _wait(self, eng, deps):
        best = {}
        for k, v in deps:
            if v > best.get(k, 0):
                best[k] = v
        for k, v in best.items():
            if self.seen[eng].get(k, 0) >= v:
                continue
            self.E[eng].wait_ge(self.sems[k], v)
            self.seen[eng][k] = v

    def _deps(self, eng, reads, writes):
        deps = []
        for r in reads:
            if r.w is not None:
                if not (eng == "pe" and r.w[0] == "pe"):
                    deps.append(r.w)
        for w in writes:
            if w.w is not None and (w.w[0] != eng or eng != "pe"):
                deps.append(w.w)
            for k, v in w.r.items():
                if k != eng or eng != "pe":
                    deps.append((k, v))
        return deps

    def _mark(self, ev, reads, writes):
        self.touched.update(reads)
        self.touched.update(writes)
        for r in reads:
            if ev[1] > r.r.get(ev[0], 0):
                r.r[ev[0]] = ev[1]
        for w in writes:
            w.w = ev
            w.r = {}

    def op(self, eng, fn, reads=(), writes=(), ser=False, rt=None):
        if eng == "pe":
            last = getattr(self, "last_rt", None)
            if rt != last and self.cnt["pe"] > 0:
                self._wait("pe", [("pe", self.cnt["pe"])])
            self.last_rt = rt
        self._wait(eng, self._deps(eng, reads, writes))
        ins = fn(self.E[eng])
        self.cnt[eng] += 1
        ins.then_inc(self.sems[eng], 1)
        self._mark((eng, self.cnt[eng]), reads, writes)
        if ser:
            self._wait(eng, [(eng, self.cnt[eng])])
        return ins

    def dma(self, q, out, in_, reads=(), writes=()):
        i = self.dslot[q]
        self.dslot[q] = (i + 1) % self.NDMA
        k = f"d{q}{i}"
        deps = self._deps(k, reads, writes)
        deps.append((k, self.cnt[k]))
        self._wait(q, deps)
        ins = self.E[q].dma_start(out=out, in_=in_)
        self.cnt[k] += 16
        ins.then_inc(self.sems[k], 16)
        self._mark((k, self.cnt[k]), reads, writes)
        return ins

    def barrier(self):
        deps = [(k, v) for k, v in self.cnt.items() if v > 0]
        for e in ("sp", "pe", "dve", "act", "pool"):
            self._wait(e, deps)
        for e in ("sp", "pe", "dve", "act", "pool"):
            for k, v in deps:
                assert self.seen[e].get(k, 0) >= v
        for r in self.touched:
            r.w = None
            r.r = {}
        self.touched = set()
        self._sync_shared()

    def wait_all(self, eng, ress):
        deps = []
        for r in ress:
            if r.w is not None:
                deps.append(r.w)
        self._wait(eng, deps)


def dram_ap(t, offset, pattern):
    return bass.AP(t, offset, [list(p) for p in pattern])


def emit_rmsnorm_T(bd, x_tile, x_res, hT, hT_res, j, tmp, ident):
    ss, ss_r = tmp["ss"], tmp["ss_r"]
    junk, junk_r = tmp["junk"], tmp["junk_r"]
    xn, xn_r = tmp["xn"], tmp["xn_r"]
    pt, pt_r = tmp["pt"], tmp["pt_r"]
    bd.op("dve", lambda e: e.scalar_tensor_tensor(out=junk[:], in0=x_tile, scalar=1.0, in1=x_tile,
                                                  op0=ALU.mult, op1=ALU.mult, accum_out=ss[:, 0:1]),
          reads=[x_res], writes=[junk_r, ss_r])
    bd.op("act", lambda e: e.activation(out=ss[:, 1:2], in_=ss[:, 0:1], func=AF.Sqrt, scale=1.0 / D,
                                        bias=tmp["eps"][:, 0:1]), reads=[ss_r, tmp["eps_r"]], writes=[ss_r])
    bd.op("dve", lambda e: e.reciprocal(out=ss[:, 2:3], in_=ss[:, 1:2]), reads=[ss_r], writes=[ss_r])
    bd.op("act", lambda e: e.activation(out=xn[:], in_=x_tile, func=AF.Copy, scale=ss[:, 2:3]),
          reads=[x_res, ss_r], writes=[xn_r])
    for kc in range(8):
        bd.op("pe", lambda e, kc=kc: e.transpose(out=pt[:, kc * 128:(kc + 1) * 128],
                                                 in_=xn[:, kc * 128:(kc + 1) * 128], identity=ident[:]),
              reads=[xn_r], writes=[pt_r])
    bd.op("act", lambda e: e.activation(out=hT[:, :, j * 128:(j + 1) * 128],
                                        in_=pt[:].rearrange("p (k t) -> p k t", k=8), func=AF.Copy),
          reads=[pt_r], writes=[hT_res])


_UC = [0]


def _u(name):
    _UC[0] += 1
    return f"{name}_{_UC[0]}"


class View:
    def __init__(self, ap):
        self._ap = ap

    def ap(self):
        return self._ap


def emit_phaseA(nc, bd, es, x_d, w_d, g_d, id_d, pT_d, ntok):
    sb = lambda name, shape, dt: es.enter_context(nc.sbuf_tensor(_u(name), shape, dt))
    ps = lambda name, shape, dt: es.enter_context(nc.psum_tensor(_u(name), shape, dt))
    wb = sb("A_wb", [128, 8, NCOLA], BF16)
    wb_r = [Res() for _ in range(8)]
    stage = [sb(f"A_stage{i}", [128, NCOLA], F32) for i in range(2)]
    stage_r = [Res(), Res()]
    gcol = sb("A_gcol", [128, 8], F32)
    gcol_r = Res()
    idf = sb("A_idf", [128, 128], F32)
    ident = sb("A_ident", [128, 128], BF16)
    ident_r = Res()
    idf_r = Res()
    xt = [sb(f"A_xt{i}", [128, D], F32) for i in range(2)]
    xt_r = [Res(), Res()]
    hT = [sb(f"A_hT{i}", [128, 8, 512], BF16) for i in range(2)]
    hT_r = [Res(), Res()]
    tmp = dict(ss=sb("A_ss", [128, 4], F32), ss_r=Res(), junk=sb("A_junk", [128, D], BF16), junk_r=Res(),
               xn=sb("A_xn", [128, D], BF16), xn_r=Res(),
               pt=ps("A_pt", [128, D], BF16), pt_r=Res(), eps=sb("A_eps", [128, 1], F32), eps_r=Res())
    bd.op("dve", lambda e: e.memset(tmp["eps"][:], EPS), writes=[tmp["eps_r"]])
    NPB = 4
    pb = [ps(f"A_pb{i}", [128, 512], F32) for i in range(NPB)]
    pb_r = [Res() for _ in range(NPB)]
    ost = [sb(f"A_ost{i}", [128, 512], F32) for i in range(4)]
    ost_r = [Res() for _ in range(4)]

    bd.dma("sp", gcol[:], g_d.ap(), writes=[gcol_r])
    bd.dma("sp", idf[:], id_d.ap(), writes=[idf_r])
    bd.op("dve", lambda e: e.tensor_copy(out=ident[:], in_=idf[:]), reads=[idf_r], writes=[ident_r])
    for kc in range(8):
        s = kc % 2
        bd.dma("pool", stage[s][:], w_d.ap()[kc * 128:(kc + 1) * 128, :], writes=[stage_r[s]])
        bd.op("dve", lambda e, kc=kc, s=s: e.tensor_scalar(out=wb[:, kc, :], in0=stage[s][:],
                                                          scalar1=gcol[:, kc:kc + 1], scalar2=None, op0=ALU.mult),
              reads=[stage_r[s], gcol_r], writes=[wb_r[kc]])
    x_ap = x_d.ap()
    pT_ap = pT_d.ap()
    nblk = ntok // 512
    oi = 0
    for blk in range(nblk):
        hb = blk % 2
        for j in range(4):
            ti = blk * 4 + j
            xb = ti % 2
            bd.dma("sp", xt[xb][:], x_ap[ti * 128:(ti + 1) * 128, :], writes=[xt_r[xb]])
            tmp2 = dict(tmp)
            emit_rmsnorm_T(bd, xt[xb][:], xt_r[xb], hT[hb], hT_r[hb], j, tmp2, ident)
        for half in range(2):
            for c in range(NCH):
                m = CH_ROWS[c]
                col0 = half * HALF_ROWS + CH_OFF[c]
                pbi = oi % NPB
                for kc in range(8):
                    bd.op("pe", lambda e, kc=kc, col0=col0, m=m, pbi=pbi, hb=hb: e.matmul(
                        pb[pbi][0:m, :], lhsT=wb[:, kc, col0:col0 + m], rhs=hT[hb][:, kc, :],
                        start=(kc == 0), stop=(kc == 7)),
                        reads=[wb_r[kc], hT_r[hb]], writes=[pb_r[pbi]])
                osi = oi % 4
                eng = "act" if oi % 2 == 0 else "dve"
                if eng == "act":
                    bd.op("act", lambda e, m=m, pbi=pbi, osi=osi: e.activation(
                        out=ost[osi][0:m, :], in_=pb[pbi][0:m, :], func=AF.Copy),
                        reads=[pb_r[pbi]], writes=[ost_r[osi]])
                else:
                    bd.op("dve", lambda e, m=m, pbi=pbi, osi=osi: e.tensor_copy(
                        out=ost[osi][0:m, :], in_=pb[pbi][0:m, :]),
                        reads=[pb_r[pbi]], writes=[ost_r[osi]])
                bd.dma("sp", pT_ap[col0:col0 + m, blk * 512:(blk + 1) * 512], ost[osi][0:m, :],
                       reads=[ost_r[osi]])
                oi += 1
    bd.barrier()


def _gcol(g):
    return np.ascontiguousarray(np.asarray(g, np.float32).reshape(8, 128).T)


def prep_w_inA(w_in_l):
    cols = half_cols(0) + half_cols(1)
    return np.ascontiguousarray(w_in_l[:, cols])


TWO_PI = 6.283185307179586
(C_QN0, C_QN1, C_KVN0, C_KVN1, C_INVF, C_SGN, C_MLAO0, C_MLAO1,
 C_MU_R, C_MU_K, C_MU_V, C_MU_L, C_W0, C_A0, C_KK, C_KA, C_RK, C_LNW, C_LNB,
 C_CWQ0, C_CWQ1, C_CWQ2, C_CWQ3, C_CBQ, C_CWK0, C_CWK1, C_CWK2, C_CWK3, C_CBK,
 C_IB, C_FB, C_MLO, C_EPS, C_LNEPS, C_ONE, C_ZERO, C_IB1, C_FB1) = range(38)
NCST = 38


def emit_attention(bd, nc, es, name, heads, scale_exp, dv, wfun, out_fn, PS):
    sb = lambda nm, shape, dt: es.enter_context(nc.sbuf_tensor(_u(nm), shape, dt))
    LOOK = 3
    NST = len(PS["st"])
    NPT = LOOK + 2
    pT = [sb(f"{name}_pT{i}", [128, 512], BF16) for i in range(NPT)]
    pT_r = [Res() for _ in range(NPT)]
    mask = PS["mask"]
    mask_r = PS["mask_r"]
    blocks = [(h, qb, kt) for h in heads for qb in range(SEQ // 512) for kt in range(4 * (qb + 1))]
    pending = []

    def stage1(i):
        h, qb, kt = blocks[i]
        st, st_r = PS["st"][i % NST]
        kp = PS["kparts"](h, kt)
        qp = PS["qparts"](h, qb)
        n = len(kp)
        jd = kt - 4 * qb
        c0 = max(jd, 0) * 128
        for a in range(n):
            bd.op("pe", lambda e, a=a: e.matmul(st[:, c0:512], lhsT=kp[a][0], rhs=qp[a][0][:, c0:512],
                                                start=(a == 0), stop=(a == n - 1)),
                  reads=[kp[a][1], qp[a][1]], writes=[st_r])
        p, p_r = pT[i % NPT], pT_r[i % NPT]
        wfun(h, kt, qb, st, st_r, p, p_r, c0)
        if jd >= 0:
            bd.op("pool", lambda e: e.tensor_tensor(
                out=p[:, jd * 128:(jd + 1) * 128], in0=p[:, jd * 128:(jd + 1) * 128], in1=mask[:],
                op=ALU.mult), reads=[p_r, mask_r], writes=[p_r])

    def stage2(i):
        h, qb, kt = blocks[i]
        p, p_r = pT[i % NPT], pT_r[i % NPT]
        v_ap, v_r = PS["v"](h, kt)
        for j in range(4):
            qt = 4 * qb + j
            if qt < kt:
                continue
            o, o_r = PS["o"][j]
            bd.op("pe", lambda e, o=o, j=j, qt=qt: e.matmul(
                o[:, 0:dv + 1], lhsT=p[:, j * 128:(j + 1) * 128], rhs=v_ap,
                start=(kt == 0), stop=(kt == qt)), reads=[p_r, v_r], writes=[o_r])
            if kt == qt:
                th = out_fn(h, qt, o, o_r)
                if th is not None:
                    pending.append([3, th])

    nblk = len(blocks)
    for i in range(nblk + LOOK):
        if i < nblk:
            stage1(i)
        if i - LOOK >= 0:
            stage2(i - LOOK)
        for item in pending:
            item[0] -= 1
        while pending and pending[0][0] <= 0:
            pending.pop(0)[1]()
    while pending:
        pending.pop(0)[1]()


def emit_rope_tables(bd, nc, es, pos_d, cst, cst_r, CS, SN, tab_r):
    with ExitStack() as es2:
        sb = lambda nm, shape, dt: es2.enter_context(nc.sbuf_tensor(_u(nm), shape, dt))
        ti = sb("rt_i", [64, SEQ], I32)
        ta = sb("rt_a", [64, SEQ], F32)
        tb = sb("rt_b", [64, SEQ], F32)
        ti_r, ta_r, tb_r = Res(), Res(), Res()
        src = bass.AP(pos_d, 0, [[0, 64], [1, SEQ]])
        bd.dma("sp", ti[:], src, writes=[ti_r])
        bd.op("dve", lambda e: e.tensor_copy(out=ta[:], in_=ti[:]), reads=[ti_r], writes=[ta_r])
        bd.op("dve", lambda e: e.tensor_scalar(out=ta[:], in0=ta[:], scalar1=cst[0:64, C_INVF:C_INVF + 1],
                                               scalar2=None, op0=ALU.mult), reads=[ta_r, cst_r], writes=[ta_r])
        for which in (0, 1):
            shift = 0.0 if which == 0 else TWO_PI / 4
            bd.op("dve", lambda e: e.tensor_scalar(out=tb[:], in0=ta[:], scalar1=shift, scalar2=1.0 / TWO_PI,
                                                   op0=ALU.add, op1=ALU.mult), reads=[ta_r], writes=[tb_r])
            bd.op("dve", lambda e: e.tensor_copy(out=ti[:], in_=tb[:]), reads=[tb_r], writes=[ti_r])
            bd.op("dve", lambda e: e.tensor_copy(out=tb[:], in_=ti[:]), reads=[ti_r], writes=[tb_r])
            bd.op("dve", lambda e: e.scalar_tensor_tensor(out=tb[:], in0=tb[:], scalar=-TWO_PI, in1=ta[:],
                                                          op0=ALU.mult, op1=ALU.add),
                  reads=[tb_r, ta_r], writes=[tb_r])
            bd.op("dve", lambda e: e.tensor_scalar(out=tb[:], in0=tb[:], scalar1=shift, scalar2=TWO_PI / 2,
                                                   op0=ALU.add, op1=ALU.min), reads=[tb_r], writes=[tb_r])
            bd.op("dve", lambda e: e.tensor_scalar(out=tb[:], in0=tb[:], scalar1=-TWO_PI / 2, scalar2=None,
                                                   op0=ALU.max), reads=[tb_r], writes=[tb_r])
            if which == 0:
                bd.op("act", lambda e: e.activation(out=SN[:], in_=tb[:], func=AF.Sin,
                                                    scale=cst[0:64, C_SGN:C_SGN + 1]),
                      reads=[tb_r, cst_r], writes=[tab_r])
            else:
                bd.op("act", lambda e: e.activation(out=CS[:], in_=tb[:], func=AF.Sin),
                      reads=[tb_r], writes=[tab_r])
        bd.barrier()


def emit_mla(bd, nc, es, pT_d, pos_d, wuq_d, wukv_d, yT_d, cst, cst_r, K):
    sb = lambda nm, shape, dt: es.enter_context(nc.sbuf_tensor(_u(nm), shape, dt))
    pT = pT_d.ap()
    yT = yT_d.ap()
    CS = sb("m_CS", [64, SEQ], F32)
    SN = sb("m_SN", [64, SEQ], F32)
    tab_r = Res()
    emit_rope_tables(bd, nc, es, pos_d, cst, cst_r, CS, SN, tab_r)
    wq = sb("m_wq", [128, 2, 512], BF16)
    wkv = sb("m_wkv", [128, 2, 512], BF16)
    wq_r, wkv_r = Res(), Res()
    wst = sb("m_wst", [128, 2, 512], F32)
    wst_r = Res()
    for (wd, wt, wr) in ((wuq_d, wq, wq_r), (wukv_d, wkv, wkv_r)):
        bd.dma("sp", wst[:], wd.ap().rearrange("(k p) n -> p k n", p=128), writes=[wst_r])
        bd.op("dve", lambda e, wt=wt: e.tensor_copy(out=wt[:], in_=wst[:]), reads=[wst_r], writes=[wr])
    Qn = [sb(f"m_Qn{h}", [128, SEQ], BF16) for h in range(2)]
    Qr = [sb(f"m_Qr{h}", [64, SEQ], BF16) for h in range(2)]
    Kn = [sb(f"m_Kn{h}", [128, SEQ], BF16) for h in range(2)]
    Kr = sb("m_Kr", [64, SEQ], BF16)
    V = [sb(f"m_V{h}", [128, 32, 129], BF16) for h in range(2)]
    qk_r = Res()
    for h in range(2):
        bd.op("pool", lambda e, h=h: e.memset(V[h][:, :, 128:129], 1.0), writes=[qk_r])
    banks, banks_r, ptb, ptb_r = K["banks"], K["banks_r"], K["ptb"], K["ptb_r"]
    ones, ones_r = K["ones"], K["ones_r"]
    with ExitStack() as es2:
        sb2 = lambda nm, shape, dt: es2.enter_context(nc.sbuf_tensor(_u(nm), shape, dt))
        cf = [sb2(f"m_cf{i}", [128, 2, 512], F32) for i in range(2)]
        cf_r = [Res(), Res()]
        sq = sb2("m_sq", [128, 2, 512], F32)
        sq_r = Res()
        rs = sb2("m_rs", [128, 512], F32)
        rs_r = Res()
        cn = [sb2(f"m_cn{i}", [128, 2, 512], BF16) for i in range(2)]
        cn_r = [Res(), Res()]
        kx = sb2("m_kx", [64, 2, 512], F32)
        kx_r = Res()
        t1 = sb2("m_t1", [64, 512], F32)
        t2 = sb2("m_t2", [64, 512], F32)
        t1_r, t2_r = Res(), Res()
        bi = 0

        def nb():
            nonlocal bi
            b = bi % 6
            bi += 1
            return banks[b], banks_r[b]

        def rope(dst, x_ap, xs_ap, src_res, sl):
            bd.op("dve", lambda e: e.tensor_tensor(out=t1[:], in0=x_ap, in1=CS[:, sl], op=ALU.mult),
                  reads=src_res + [tab_r], writes=[t1_r])
            bd.op("dve", lambda e: e.tensor_tensor(out=t2[:], in0=xs_ap, in1=SN[:, sl], op=ALU.mult),
                  reads=src_res + [tab_r], writes=[t2_r])
            bd.op("dve", lambda e: e.tensor_tensor(out=dst, in0=t1[:], in1=t2[:], op=ALU.add),
                  reads=[t1_r, t2_r], writes=[qk_r])

        for tb in range(SEQ // 512):
            sl = slice(tb * 512, (tb + 1) * 512)
            for which in (0, 1):
                ci = which
                row0 = which * 256
                bd.dma("sp", cf[ci][:], pT[row0:row0 + 256, sl].rearrange("(k p) t -> p k t", p=128),
                       writes=[cf_r[ci]])
                bd.op("act", lambda e, ci=ci: e.activation(out=sq[:], in_=cf[ci][:], func=AF.Square),
                      reads=[cf_r[ci]], writes=[sq_r])
                pss, pss_r = nb()
                for c in range(2):
                    bd.op("pe", lambda e, c=c, pss=pss: e.matmul(pss[:, :], lhsT=ones[:], rhs=sq[:, c, :],
                                                                 start=(c == 0), stop=(c == 1)),
                          reads=[ones_r, sq_r], writes=[pss_r])
                bd.op("act", lambda e, pss=pss: e.activation(out=rs[:], in_=pss[:, :], func=AF.Sqrt,
                                                             scale=1.0 / 256, bias=cst[:, C_EPS:C_EPS + 1]),
                      reads=[pss_r, cst_r], writes=[rs_r])
                bd.op("dve", lambda e: e.reciprocal(out=rs[:], in_=rs[:]), reads=[rs_r], writes=[rs_r])
                gcol = C_QN0 if which == 0 else C_KVN0
                for c in range(2):
                    bd.op("dve", lambda e, c=c, ci=ci, gcol=gcol: e.scalar_tensor_tensor(
                        out=cn[ci][:, c, :], in0=cf[ci][:, c, :], scalar=cst[:, gcol + c:gcol + c + 1], in1=rs[:],
                        op0=ALU.mult, op1=ALU.mult), reads=[cf_r[ci], rs_r, cst_r], writes=[cn_r[ci]])
                if which == 0:
                    for h in range(2):
                        pq, pq_r = nb()
                        for c in range(2):
                            bd.op("pe", lambda e, c=c, h=h, pq=pq: e.matmul(
                                pq[:, :], lhsT=wq[:, c, h * 256:h * 256 + 128], rhs=cn[0][:, c, :],
                                start=(c == 0), stop=(c == 1)), reads=[wq_r, cn_r[0]], writes=[pq_r])
                        bd.op("act", lambda e, h=h, pq=pq: e.activation(out=Qn[h][:, sl], in_=pq[:, :], func=AF.Copy),
                              reads=[pq_r], writes=[qk_r])
                        pa, pa_r = nb()
                        pb_, pb_r = nb()
                        for (pp, pp_r, off) in ((pa, pa_r, 128), (pb_, pb_r, 192)):
                            for c in range(2):
                                bd.op("pe", lambda e, c=c, h=h, pp=pp, off=off: e.matmul(
                                    pp[0:64, :], lhsT=wq[:, c, h * 256 + off:h * 256 + off + 64], rhs=cn[0][:, c, :],
                                    start=(c == 0), stop=(c == 1)), reads=[wq_r, cn_r[0]], writes=[pp_r])
                        rope(Qr[h][:, sl], pa[0:64, :], pb_[0:64, :], [pa_r, pb_r], sl)
                else:
                    for h in range(2):
                        pk, pk_r = nb()
                        for c in range(2):
                            bd.op("pe", lambda e, c=c, h=h, pk=pk: e.matmul(
                                pk[:, :], lhsT=wkv[:, c, h * 256:h * 256 + 128], rhs=cn[1][:, c, :],
                                start=(c == 0), stop=(c == 1)), reads=[wkv_r, cn_r[1]], writes=[pk_r])
                        bd.op("act", lambda e, h=h, pk=pk: e.activation(out=Kn[h][:, sl], in_=pk[:, :], func=AF.Copy),
                              reads=[pk_r], writes=[qk_r])
                        pv, pv_r = nb()
                        for j in range(4):
                            for c in range(2):
                                bd.op("pe", lambda e, c=c, h=h, j=j, pv=pv: e.matmul(
                                    pv[:, j * 128:(j + 1) * 128], lhsT=cn[1][:, c, j * 128:(j + 1) * 128],
                                    rhs=wkv[:, c, h * 256 + 128:h * 256 + 256],
                                    start=(c == 0 and j == 0), stop=(c == 1)), reads=[wkv_r, cn_r[1]], writes=[pv_r])
                        bd.op("dve", lambda e, h=h, pv=pv: e.tensor_copy(
                            out=V[h][:, tb * 4:(tb + 1) * 4, 0:128],
                            in_=pv[:, :].rearrange("p (j d) -> p j d", j=4)), reads=[pv_r], writes=[qk_r])
            bd.dma("sp", kx[:], pT[512:640, sl].rearrange("(k p) t -> p k t", p=64), writes=[kx_r])
            rope(Kr[:, sl], kx[:, 0, :], kx[:, 1, :], [kx_r], sl)
        bd.barrier()
    with ExitStack() as es3:
        sb3 = lambda nm, shape, dt: es3.enter_context(nc.sbuf_tensor(_u(nm), shape, dt))
        of = sb3("m_of", [128, 132], F32)
        of_r = Res()
        onb = sb3("m_onb", [128, 128], BF16)
        onb_r = Res()
        st_ = sb3("m_stat", [128, 4], F32)
        st_r = Res()
        junk = sb3("m_junk", [128, 128], F32)
        junk_r = Res()
        yst = [sb3(f"m_yst{i}", [128, 512], BF16) for i in range(2)]
        yst_r = [Res(), Res()]
        scale = (128 + 64) ** -0.5

        def wfun(h, kt, qb, st, st_r2, p, p_r, c0):
            bd.op("act", lambda e: e.activation(out=p[:, c0:512], in_=st[:, c0:512], func=AF.Exp, scale=scale),
                  reads=[st_r2], writes=[p_r])

        onbs = [sb3(f"m_onb{i}", [128, 128], BF16) for i in range(4)]
        onbs_r = [Res() for _ in range(4)]
        oi = [0]

        def out_fn(h, qt, o, o_r):
            ob, ob_r = onbs[oi[0] % 4], onbs_r[oi[0] % 4]
            oi[0] += 1
            bd.op("dve", lambda e: e.reciprocal(out=st_[:, 0:1], in_=o[:, 128:129]), reads=[o_r], writes=[st_r])
            bd.op("dve", lambda e: e.tensor_scalar(out=of[:, 0:128], in0=o[:, 0:128], scalar1=st_[:, 0:1],
                                                   scalar2=None, op0=ALU.mult), reads=[o_r, st_r], writes=[of_r])
            bd.op("dve", lambda e: e.scalar_tensor_tensor(out=junk[:], in0=of[:, 0:128], scalar=1.0, in1=of[:, 0:128],
                                                          op0=ALU.mult, op1=ALU.mult, accum_out=st_[:, 1:2]),
                  reads=[of_r], writes=[junk_r, st_r])
            bd.op("act", lambda e: e.activation(out=st_[:, 2:3], in_=st_[:, 1:2], func=AF.Sqrt, scale=1.0 / 128,
                                                bias=cst[:, C_EPS:C_EPS + 1]), reads=[st_r, cst_r], writes=[st_r])
            bd.op("dve", lambda e: e.reciprocal(out=st_[:, 3:4], in_=st_[:, 2:3]), reads=[st_r], writes=[st_r])
            bd.op("dve", lambda e: e.tensor_scalar(out=ob[:], in0=of[:, 0:128], scalar1=st_[:, 3:4], scalar2=None,
                                                   op0=ALU.mult), reads=[of_r, st_r], writes=[ob_r])

            def fin():
                bd.op("pe", lambda e: e.transpose(out=ptb[:, 0:128], in_=ob[:], identity=K["identb"][:]),
                      reads=[ob_r, K["identb_r"]], writes=[ptb_r])
                ys, ys_r = yst[(qt // 4) % 2], yst_r[(qt // 4) % 2]
                j = qt % 4
                bd.op("act", lambda e: e.activation(out=ys[:, j * 128:(j + 1) * 128], in_=ptb[:, 0:128], func=AF.Copy,
                                                    scale=cst[:, C_MLAO0 + h:C_MLAO0 + h + 1]),
                      reads=[ptb_r, cst_r], writes=[ys_r])
                if j == 3:
                    qb = qt // 4
                    bd.dma("sp", yT[h * 128:(h + 1) * 128, qb * 512:(qb + 1) * 512], ys[:], reads=[ys_r])
            return fin

        PS = dict(st=[(banks[0], banks_r[0]), (banks[1], banks_r[1]), (banks[6], banks_r[6])],
                  o=[(banks[2 + j], banks_r[2 + j]) for j in range(4)],
                  mask=K["mask"], mask_r=K["mask_r"],
                  kparts=lambda h, kt: [(Kn[h][:, kt * 128:(kt + 1) * 128], qk_r), (Kr[:, kt * 128:(kt + 1) * 128], qk_r)],
                  qparts=lambda h, qb: [(Qn[h][:, qb * 512:(qb + 1) * 512], qk_r), (Qr[h][:, qb * 512:(qb + 1) * 512], qk_r)],
                  v=lambda h, kt: (V[h][:, kt, :], qk_r))
        emit_attention(bd, nc, es3, "mla", [0, 1], scale, 128, wfun, out_fn, PS)
        bd.barrier()


def emit_mlstm(bd, nc, es, pT_d, yT_d, cst, cst_r, K):
    sb = lambda nm, shape, dt: es.enter_context(nc.sbuf_tensor(_u(nm), shape, dt))
    pT = pT_d.ap()
    yT = yT_d.ap()
    banks, banks_r, ptb, ptb_r = K["banks"], K["banks_r"], K["ptb"], K["ptb_r"]
    misc, misc_r = banks[6], banks_r[6]
    R_Q, R_K, R_V, R_O, R_G = 1152, 1216, 1280, 1408, 1536
    Qb = sb("l_Qb", [64, SEQ], BF16)
    Kb = sb("l_Kb", [64, SEQ], BF16)
    Vm = sb("l_Vm", [128, 32, 2, 65], BF16)
    Ym = sb("l_Ym", [128, SEQ], BF16)
    uT = sb("l_uT", [128, 2, 32], F32)
    emT = sb("l_emT", [128, 2, 32], F32)
    nPb = [sb(f"l_nPb{h}", [128, SEQ], F32) for h in range(2)]
    prep_r = Res()
    ym_r = Res()
    bd.op("pool", lambda e: e.memset(Vm[:, :, :, 64:65], 1.0), writes=[prep_r])
    with ExitStack() as es2:
        sb2 = lambda nm, shape, dt: es2.enter_context(nc.sbuf_tensor(_u(nm), shape, dt))
        xin = sb2("l_xin", [64, SEQ], F32)
        A = sb2("l_A", [64, SEQ], F32)
        B = sb2("l_B", [64, SEQ], F32)
        xin_r, A_r, B_r = Res(), Res(), Res()
        for (row0, cw0, cb, dst, scl) in ((R_Q, C_CWQ0, C_CBQ, Qb, 32 ** -0.5), (R_K, C_CWK0, C_CBK, Kb, 1.0)):
            bd.dma("sp", xin[:], pT[row0:row0 + 64, :], writes=[xin_r])
            bd.op("dve", lambda e, cw0=cw0, cb=cb: e.tensor_scalar(
                out=A[:], in0=xin[:], scalar1=cst[0:64, cw0 + 3:cw0 + 4], scalar2=cst[0:64, cb:cb + 1],
                op0=ALU.mult, op1=ALU.add), reads=[xin_r, cst_r], writes=[A_r])
            src, src_r, dstt, dst_r = A, A_r, B, B_r
            for sh in (1, 2, 3):
                bd.op("dve", lambda e, sh=sh, cw0=cw0, src=src, dstt=dstt: e.scalar_tensor_tensor(
                    out=dstt[:, sh:], in0=xin[:, 0:SEQ - sh], scalar=cst[0:64, cw0 + 3 - sh:cw0 + 4 - sh],
                    in1=src[:, sh:], op0=ALU.mult, op1=ALU.add), reads=[xin_r, cst_r, src_r], writes=[dst_r])
                bd.op("dve", lambda e, sh=sh, src=src, dstt=dstt: e.tensor_copy(out=dstt[:, 0:sh], in_=src[:, 0:sh]),
                      reads=[src_r], writes=[dst_r])
                src, src_r, dstt, dst_r = dstt, dst_r, src, src_r
            bd.op("act", lambda e, src=src: e.activation(out=xin[:], in_=src[:], func=AF.Silu),
                  reads=[src_r], writes=[xin_r])
            bd.op("dve", lambda e, dst=dst, scl=scl: e.tensor_scalar(out=dst[:], in0=xin[:], scalar1=scl, scalar2=None,
                                                                     op0=ALU.mult), reads=[xin_r], writes=[prep_r])
        bd.barrier()
    with ExitStack() as es2:
        sb2 = lambda nm, shape, dt: es2.enter_context(nc.sbuf_tensor(_u(nm), shape, dt))
        t0 = sb2("l_t0", [1, SEQ], F32)
        t1 = sb2("l_t1", [1, SEQ], F32)
        t2 = sb2("l_t2", [1, SEQ], F32)
        onesrow = sb2("l_onesrow", [1, SEQ], F32)
        vin = sb2("l_vin", [128, SEQ], F32)
        t0_r, t1_r, t2_r, or_r, vin_r = Res(), Res(), Res(), Res(), Res()
        bd.op("dve", lambda e: e.memset(onesrow[:], 1.0), writes=[or_r])
        identf = K["identf"]
        for h in range(2):
            cib = C_IB if h == 0 else C_IB1
            cfb = C_FB if h == 0 else C_FB1
            bd.dma("sp", t0[:], pT[R_G + h:R_G + h + 1, :], writes=[t0_r])
            bd.dma("sp", t1[:], pT[R_G + 2 + h:R_G + 3 + h, :], writes=[t1_r])
            bd.op("dve", lambda e, cib=cib: e.tensor_scalar(out=t0[:], in0=t0[:], scalar1=cst[0:1, cib:cib + 1],
                                                            scalar2=None, op0=ALU.add), reads=[t0_r, cst_r], writes=[t0_r])
            bd.op("act", lambda e, cfb=cfb: e.activation(out=t1[:], in_=t1[:], func=AF.Sigmoid,
                                                         bias=cst[0:1, cfb:cfb + 1]), reads=[t1_r, cst_r], writes=[t1_r])
            bd.op("act", lambda e: e.activation(out=t1[:], in_=t1[:], func=AF.Ln), reads=[t1_r], writes=[t1_r])
            bd.op("dve", lambda e: e.tensor_tensor_scan(out=t2[:], data0=onesrow[:], data1=t1[:], initial=0.0,
                                                        op0=ALU.mult, op1=ALU.add), reads=[or_r, t1_r], writes=[t2_r])
            bd.op("dve", lambda e: e.tensor_tensor(out=t0[:], in0=t0[:], in1=t2[:], op=ALU.subtract),
                  reads=[t0_r, t2_r], writes=[t0_r])
            bd.op("dve", lambda e: e.tensor_tensor_scan(out=t1[:], data0=onesrow[:], data1=t0[:], initial=0.0,
                                                        op0=ALU.mult, op1=ALU.max), reads=[or_r, t0_r], writes=[t1_r])
            bd.op("dve", lambda e: e.tensor_tensor(out=t2[:], in0=t2[:], in1=t1[:], op=ALU.add),
                  reads=[t2_r, t1_r], writes=[t2_r])
            for jt in range(32):
                bd.op("pe", lambda e, jt=jt: e.transpose(out=misc[:, jt:jt + 1], in_=t0[0:1, jt * 128:(jt + 1) * 128],
                                                         identity=identf[0:1, 0:1]), reads=[t0_r, K["mats_r"]], writes=[misc_r])
                bd.op("pe", lambda e, jt=jt: e.transpose(out=misc[:, 32 + jt:33 + jt], in_=t2[0:1, jt * 128:(jt + 1) * 128],
                                                         identity=identf[0:1, 0:1]), reads=[t2_r, K["mats_r"]], writes=[misc_r])
            bd.op("dve", lambda e, h=h: e.tensor_copy(out=uT[:, h, :], in_=misc[:, 0:32]), reads=[misc_r], writes=[prep_r])
            bd.op("act", lambda e, h=h: e.activation(out=emT[:, h, :], in_=misc[:, 32:64], func=AF.Exp, scale=-1.0),
                  reads=[misc_r], writes=[prep_r])
            for tb in range(SEQ // 512):
                bd.op("pe", lambda e, tb=tb: e.matmul(misc[:, :], lhsT=K["ones"][0:1, :], rhs=t1[0:1, tb * 512:(tb + 1) * 512],
                                                      start=True, stop=True), reads=[t1_r, K["ones_r"]], writes=[misc_r])
                bd.op("act", lambda e, tb=tb, h=h: e.activation(out=nPb[h][:, tb * 512:(tb + 1) * 512], in_=misc[:, :],
                                                                func=AF.Copy, scale=-1.0), reads=[misc_r], writes=[prep_r])
        bd.dma("sp", vin[:], pT[R_V:R_V + 128, :], writes=[vin_r])
        for jt in range(32):
            bd.op("pe", lambda e, jt=jt: e.transpose(out=misc[:, 0:128], in_=vin[:, jt * 128:(jt + 1) * 128],
                                                     identity=identf), reads=[vin_r, K["mats_r"]], writes=[misc_r])
            bd.op("dve", lambda e, jt=jt: e.tensor_copy(out=Vm[:, jt, :, 0:64],
                                                        in_=misc[:, 0:128].rearrange("p (h d) -> p h d", h=2)),
                  reads=[misc_r], writes=[prep_r])
        bd.barrier()
    with ExitStack() as es3:
        sb3 = lambda nm, shape, dt: es3.enter_context(nc.sbuf_tensor(_u(nm), shape, dt))
        Wt = [sb3(f"l_W{i}", [128, 512], F32) for i in range(2)]
        Wt_r = [Res(), Res()]
        of = sb3("l_of", [128, 64], F32)
        of_r = Res()
        onb = sb3("l_onb", [128, 64], BF16)
        onb_r = Res()
        st_ = sb3("l_stat", [128, 6], F32)
        st_r = Res()
        junk = sb3("l_junk", [128, 64], F32)
        junk_r = Res()
        wi = [0]

        def wfun(h, kt, qb, st, st_r2, p, p_r, c0):
            w, w_r = Wt[wi[0] % 2], Wt_r[wi[0] % 2]
            wi[0] += 1
            bd.op("act", lambda e: e.activation(out=w[:, c0:512], in_=nPb[h][:, qb * 512 + c0:(qb + 1) * 512], func=AF.Exp,
                                                bias=uT[:, h, kt:kt + 1]), reads=[prep_r], writes=[w_r])
            bd.op("dve", lambda e: e.tensor_tensor(out=p[:, c0:512], in0=st[:, c0:512], in1=w[:, c0:512], op=ALU.mult),
                  reads=[st_r2, w_r], writes=[p_r])

        onbs = [sb3(f"l_onb{i}", [128, 64], BF16) for i in range(4)]
        onbs_r = [Res() for _ in range(4)]
        oi = [0]

        def out_fn(h, qt, o, o_r):
            ob, ob_r = onbs[oi[0] % 4], onbs_r[oi[0] % 4]
            oi[0] += 1
            bd.op("act", lambda e: e.activation(out=st_[:, 0:1], in_=o[:, 64:65], func=AF.Abs), reads=[o_r], writes=[st_r])
            bd.op("dve", lambda e: e.tensor_tensor(out=st_[:, 1:2], in0=st_[:, 0:1], in1=emT[:, h, qt:qt + 1], op=ALU.max),
                  reads=[st_r, prep_r], writes=[st_r])
            bd.op("dve", lambda e: e.reciprocal(out=st_[:, 2:3], in_=st_[:, 1:2]), reads=[st_r], writes=[st_r])
            bd.op("dve", lambda e: e.tensor_scalar(out=of[:], in0=o[:, 0:64], scalar1=st_[:, 2:3], scalar2=None,
                                                   op0=ALU.mult), reads=[o_r, st_r], writes=[of_r])
            bd.op("dve", lambda e: e.scalar_tensor_tensor(out=junk[:], in0=of[:], scalar=1.0, in1=of[:],
                                                          op0=ALU.mult, op1=ALU.mult, accum_out=st_[:, 3:4]),
                  reads=[of_r], writes=[junk_r, st_r])
            bd.op("act", lambda e: e.activation(out=st_[:, 4:5], in_=st_[:, 3:4], func=AF.Sqrt, scale=1.0 / 64,
                                                bias=cst[:, C_EPS:C_EPS + 1]), reads=[st_r, cst_r], writes=[st_r])
            bd.op("dve", lambda e: e.reciprocal(out=st_[:, 5:6], in_=st_[:, 4:5]), reads=[st_r], writes=[st_r])
            bd.op("dve", lambda e: e.tensor_scalar(out=ob[:], in0=of[:], scalar1=st_[:, 5:6], scalar2=None, op0=ALU.mult),
                  reads=[of_r, st_r], writes=[ob_r])

            def fin():
                bd.op("pe", lambda e: e.transpose(out=ptb[h * 64:(h + 1) * 64, 0:128], in_=ob[:], identity=K["identb"][:]),
                      reads=[ob_r, K["identb_r"]], writes=[ptb_r])
                bd.op("act", lambda e: e.activation(out=Ym[h * 64:(h + 1) * 64, qt * 128:(qt + 1) * 128],
                                                    in_=ptb[h * 64:(h + 1) * 64, 0:128], func=AF.Copy,
                                                    scale=cst[h * 64:(h + 1) * 64, C_MLO:C_MLO + 1]),
                      reads=[ptb_r, cst_r], writes=[ym_r])
            return fin

        PS = dict(st=[(banks[0], banks_r[0]), (banks[1], banks_r[1]), (banks[6], banks_r[6])],
                  o=[(banks[2 + j], banks_r[2 + j]) for j in range(4)],
                  mask=K["mask"], mask_r=K["mask_r"],
                  kparts=lambda h, kt: [(Kb[h * 32:(h + 1) * 32, kt * 128:(kt + 1) * 128], prep_r)],
                  qparts=lambda h, qb: [(Qb[h * 32:(h + 1) * 32, qb * 512:(qb + 1) * 512], prep_r)],
                  v=lambda h, kt: (Vm[:, kt, h, :], prep_r))
        emit_attention(bd, nc, es3, "mls", [0, 1], 1.0, 64, wfun, out_fn, PS)
        og = sb3("l_og", [128, SEQ], F32)
        og_r = Res()
        bd.dma("sp", og[:], pT[R_O:R_O + 128, :], writes=[og_r])
        bd.op("act", lambda e: e.activation(out=og[:], in_=og[:], func=AF.Sigmoid), reads=[og_r], writes=[og_r])
        bd.op("dve", lambda e: e.tensor_tensor(out=Ym[:], in0=Ym[:], in1=og[:], op=ALU.mult),
              reads=[ym_r, og_r], writes=[ym_r])
        bd.dma("sp", yT[384:512, :], Ym[:], reads=[ym_r])
        bd.barrier()


RW_T = 16


def emit_rwkv(bd, nc, es, D_, cst, cst_r, K):
    sb = lambda nm, shape, dt: es.enter_context(nc.sbuf_tensor(_u(nm), shape, dt))
    pT = D_["pT"].ap()
    yT = D_["yT"].ap()
    scr = D_["scr"]
    banks, banks_r = K["banks"], K["banks_r"]
    bones, identf, mats_r = K["bones"], K["identf"], K["mats_r"]
    R_R, R_K, R_V, R_L = 640, 768, 896, 1024
    vS = sb("r_vS", [128, SEQ], F32)
    gS = sb("r_gS", [128, SEQ], F32)
    boS = sb("r_boS", [128, SEQ], F32)
    yS = sb("r_yS", [128, SEQ], F32)
    vS_r, gS_r, boS_r, yS_r = Res(), Res(), Res(), Res()
    wl = sb("r_wl", [128, 3, 128], F32)
    wl_r = Res()
    bd.dma("sp", wl[:], D_["wl"].ap(), writes=[wl_r])
    c2 = sb("r_c2", [128, 2], F32)
    c2_r = Res()
    bd.op("dve", lambda e: e.tensor_scalar(out=c2[:, 0:1], in0=cst[:, C_KA:C_KA + 1], scalar1=-1.0, scalar2=1.0,
                                           op0=ALU.mult, op1=ALU.add), reads=[cst_r], writes=[c2_r])
    scr_r = Res()
    with ExitStack() as es2:
        sb2 = lambda nm, shape, dt: es2.enter_context(nc.sbuf_tensor(_u(nm), shape, dt))
        rS = sb2("r_rS", [128, SEQ], F32)
        kS = sb2("r_kS", [128, SEQ], F32)
        lS = sb2("r_lS", [128, SEQ], F32)
        dd = sb2("r_dd", [128, SEQ], F32)
        rS_r, kS_r, lS_r, dd_r = Res(), Res(), Res(), Res()
        for (row0, t, t_r, mu) in ((R_R, rS, rS_r, C_MU_R), (R_K, kS, kS_r, C_MU_K), (R_V, vS, vS_r, C_MU_V),
                                   (R_L, lS, lS_r, C_MU_L)):
            bd.dma("sp", t[:], pT[row0:row0 + 128, :], writes=[t_r])
            bd.op("dve", lambda e, t=t: e.tensor_tensor(out=dd[:, 1:SEQ], in0=t[:, 0:SEQ - 1], in1=t[:, 1:SEQ],
                                                        op=ALU.subtract), reads=[t_r], writes=[dd_r])
            bd.op("dve", lambda e, t=t: e.tensor_scalar(out=dd[:, 0:1], in0=t[:, 0:1], scalar1=-1.0, scalar2=None,
                                                        op0=ALU.mult), reads=[t_r], writes=[dd_r])
            bd.op("dve", lambda e, t=t, mu=mu: e.scalar_tensor_tensor(out=t[:], in0=dd[:], scalar=cst[:, mu:mu + 1],
                                                                      in1=t[:], op0=ALU.mult, op1=ALU.add),
                  reads=[dd_r, t_r, cst_r], writes=[t_r])
        names = ["th", "sg", "sgm", "wd", "aT", "kkr", "sq", "nrm", "nkk", "bb", "t1", "km", "prod"]
        T_ = {n: sb2("r_" + n, [128, 512], F32) for n in names}
        T_r = {n: Res() for n in names}
        stg = [sb2(f"r_stg{i}", [128, 5, 128], F32) for i in range(2)]
        stg_r = [Res(), Res()]
        bi = [0]

        def nb():
            b = bi[0] % 7
            bi[0] += 1
            return banks[b], banks_r[b]

        def A(fn, reads, writes):
            bd.op("act", fn, reads=reads, writes=writes)

        def V(fn, reads, writes):
            bd.op("dve", fn, reads=reads, writes=writes)

        ti = 0
        for tb in range(SEQ // 512):
            sl = slice(tb * 512, (tb + 1) * 512)
            A(lambda e: e.activation(out=T_["th"][:], in_=lS[:, sl], func=AF.Tanh), [lS_r], [T_r["th"]])
            A(lambda e: e.activation(out=T_["sg"][:], in_=lS[:, sl], func=AF.Sigmoid), [lS_r], [T_r["sg"]])
            pw, pw_r = nb()
            bd.op("pe", lambda e: e.matmul(pw[:, :], lhsT=wl[:, 0, :], rhs=T_["th"][:], start=True, stop=True),
                  reads=[wl_r, T_r["th"]], writes=[pw_r])
            A(lambda e: e.activation(out=T_["sgm"][:], in_=pw[:, :], func=AF.Sigmoid, bias=cst[:, C_W0:C_W0 + 1]),
              [pw_r, cst_r], [T_r["sgm"]])
            A(lambda e: e.activation(out=T_["wd"][:], in_=T_["sgm"][:], func=AF.Exp, scale=-float(np.exp(-0.5))),
              [T_r["sgm"]], [T_r["wd"]])
            pa, pa_r = nb()
            bd.op("pe", lambda e: e.matmul(pa[:, :], lhsT=wl[:, 1, :], rhs=lS[:, sl], start=True, stop=True),
                  reads=[wl_r, lS_r], writes=[pa_r])
            A(lambda e: e.activation(out=T_["aT"][:], in_=pa[:, :], func=AF.Sigmoid, bias=cst[:, C_A0:C_A0 + 1]),
              [pa_r, cst_r], [T_r["aT"]])
            pg, pg_r = nb()
            bd.op("pe", lambda e: e.matmul(pg[:, :], lhsT=wl[:, 2, :], rhs=T_["sg"][:], start=True, stop=True),
                  reads=[wl_r, T_r["sg"]], writes=[pg_r])
            A(lambda e: e.activation(out=gS[:, sl], in_=pg[:, :], func=AF.Copy), [pg_r], [gS_r])
            V(lambda e: e.tensor_scalar(out=T_["kkr"][:], in0=kS[:, sl], scalar1=cst[:, C_KK:C_KK + 1], scalar2=None,
                                        op0=ALU.mult), [kS_r, cst_r], [T_r["kkr"]])
            A(lambda e: e.activation(out=T_["sq"][:], in_=T_["kkr"][:], func=AF.Square), [T_r["kkr"]], [T_r["sq"]])
            pn, pn_r = nb()
            bd.op("pe", lambda e: e.matmul(pn[:, :], lhsT=bones, rhs=T_["sq"][:], start=True, stop=True),
                  reads=[mats_r, T_r["sq"]], writes=[pn_r])
            A(lambda e: e.activation(out=T_["nrm"][:], in_=pn[:, :], func=AF.Sqrt), [pn_r], [T_r["nrm"]])
            V(lambda e: e.tensor_scalar(out=T_["nrm"][:], in0=T_["nrm"][:], scalar1=1e-12, scalar2=None, op0=ALU.max),
              [T_r["nrm"]], [T_r["nrm"]])
            V(lambda e: e.reciprocal(out=T_["nrm"][:], in_=T_["nrm"][:]), [T_r["nrm"]], [T_r["nrm"]])
            V(lambda e: e.scalar_tensor_tensor(out=T_["nkk"][:], in0=T_["kkr"][:], scalar=-1.0, in1=T_["nrm"][:],
                                               op0=ALU.mult, op1=ALU.mult), [T_r["kkr"], T_r["nrm"]], [T_r["nkk"]])
            V(lambda e: e.scalar_tensor_tensor(out=T_["bb"][:], in0=T_["nkk"][:], scalar=-1.0, in1=T_["aT"][:],
                                               op0=ALU.mult, op1=ALU.mult), [T_r["nkk"], T_r["aT"]], [T_r["bb"]])
            V(lambda e: e.tensor_scalar(out=T_["t1"][:], in0=T_["aT"][:], scalar1=cst[:, C_KA:C_KA + 1],
                                        scalar2=c2[:, 0:1], op0=ALU.mult, op1=ALU.add),
              [T_r["aT"], cst_r, c2_r], [T_r["t1"]])
            V(lambda e: e.tensor_tensor(out=T_["km"][:], in0=kS[:, sl], in1=T_["t1"][:], op=ALU.mult),
              [kS_r, T_r["t1"]], [T_r["km"]])
            V(lambda e: e.scalar_tensor_tensor(out=T_["prod"][:], in0=rS[:, sl], scalar=cst[:, C_RK:C_RK + 1],
                                               in1=T_["km"][:], op0=ALU.mult, op1=ALU.mult),
              [rS_r, cst_r, T_r["km"]], [T_r["prod"]])
            pb_, pb_r = nb()
            bd.op("pe", lambda e: e.matmul(pb_[:, :], lhsT=bones, rhs=T_["prod"][:], start=True, stop=True),
                  reads=[mats_r, T_r["prod"]], writes=[pb_r])
            V(lambda e: e.tensor_tensor(out=boS[:, sl], in0=pb_[:, :], in1=vS[:, sl], op=ALU.mult),
              [pb_r, vS_r], [boS_r])
            for j in range(4):
                t0 = tb * 512 + j * 128
                px, px_r = nb()
                py, py_r = nb()
                srcs = [(T_["nkk"][:, j * 128:(j + 1) * 128], T_r["nkk"]), (T_["wd"][:, j * 128:(j + 1) * 128], T_r["wd"]),
                        (T_["bb"][:, j * 128:(j + 1) * 128], T_r["bb"]), (T_["km"][:, j * 128:(j + 1) * 128], T_r["km"]),
                        (rS[:, t0:t0 + 128], rS_r)]
                for q, (ap_, r_) in enumerate(srcs):
                    if q < 4:
                        bd.op("pe", lambda e, q=q, ap_=ap_: e.transpose(out=px[:, q * 128:(q + 1) * 128], in_=ap_,
                                                                       identity=identf), reads=[r_, mats_r], writes=[px_r])
                    else:
                        bd.op("pe", lambda e, ap_=ap_: e.transpose(out=py[:, 0:128], in_=ap_, identity=identf),
                              reads=[r_, mats_r], writes=[py_r])
                sg_, sg_r = stg[ti % 2], stg_r[ti % 2]
                ti += 1
                A(lambda e, sg_=sg_: e.activation(out=sg_[:, 0:4, :], in_=px[:, :].rearrange("p (q c) -> p q c", q=4),
                                                  func=AF.Copy), [px_r], [sg_r])
                V(lambda e, sg_=sg_: e.tensor_copy(out=sg_[:, 4, :], in_=py[:, 0:128]), [py_r], [sg_r])
                for h in range(2):
                    dst = bass.AP(scr, h * SEQ * 320 + t0 * 320, [[320, 128], [64, 5], [1, 64]])
                    bd.dma("sp" if h == 0 else "act", dst, sg_[:, :, h * 64:(h + 1) * 64], reads=[sg_r], writes=[scr_r])
        bd.barrier()
    with ExitStack() as es3:
        sb3 = lambda nm, shape, dt: es3.enter_context(nc.sbuf_tensor(_u(nm), shape, dt))
        T = RW_T
        NB = 3
        BC = [sb3(f"r_BC{i}", [128, T, 5, 64], F32) for i in range(NB)]
        BC_r = [Res() for _ in range(NB)]
        S = sb3("r_S", [128, 64], F32)
        junk = sb3("r_junk", [128, 64], F32)
        sa = sb3("r_sa", [128, 1], F32)
        S_r, junk_r, sa_r = Res(), Res(), Res()
        bd.op("dve", lambda e: e.memset(S[:], 0.0), writes=[S_r])
        nchunk = SEQ // T

        def load(ci):
            b = ci % NB
            for h in range(2):
                src = bass.AP(scr, h * SEQ * 320 + ci * T * 320, [[0, 64], [1, T * 320]])
                bd.dma("sp" if h == 0 else "act", BC[b][h * 64:(h + 1) * 64, :, :, :].rearrange("p t q j -> p (t q j)"),
                       src, reads=[scr_r], writes=[BC_r[b]])

        load(0)
        load(1)
        for ci in range(nchunk):
            if ci + 2 < nchunk:
                load(ci + 2)
            b = ci % NB
            bc, bc_r = BC[b], BC_r[b]
            for tt in range(T):
                t = ci * T + tt
                bd.op("dve", lambda e, bc=bc, tt=tt: e.scalar_tensor_tensor(
                    out=junk[:], in0=S[:], scalar=1.0, in1=bc[:, tt, 0, :], op0=ALU.mult, op1=ALU.mult,
                    accum_out=sa[:, 0:1]), reads=[S_r, bc_r], writes=[junk_r, sa_r])
                bd.op("dve", lambda e, bc=bc, tt=tt: e.tensor_tensor(out=S[:], in0=S[:], in1=bc[:, tt, 1, :], op=ALU.mult),
                      reads=[S_r, bc_r], writes=[S_r])
                bd.op("dve", lambda e, bc=bc, tt=tt: e.scalar_tensor_tensor(
                    out=S[:], in0=bc[:, tt, 2, :], scalar=sa[:, 0:1], in1=S[:], op0=ALU.mult, op1=ALU.add),
                    reads=[S_r, bc_r, sa_r], writes=[S_r])
                bd.op("dve", lambda e, bc=bc, tt=tt, t=t: e.scalar_tensor_tensor(
                    out=S[:], in0=bc[:, tt, 3, :], scalar=vS[:, t:t + 1], in1=S[:], op0=ALU.mult, op1=ALU.add),
                    reads=[S_r, bc_r, vS_r], writes=[S_r])
                bd.op("dve", lambda e, bc=bc, tt=tt, t=t: e.scalar_tensor_tensor(
                    out=junk[:], in0=S[:], scalar=1.0, in1=bc[:, tt, 4, :], op0=ALU.mult, op1=ALU.mult,
                    accum_out=yS[:, t:t + 1]), reads=[S_r, bc_r], writes=[junk_r, yS_r])
        bd.barrier()
    with ExitStack() as es4:
        sb4 = lambda nm, shape, dt: es4.enter_context(nc.sbuf_tensor(_u(nm), shape, dt))
        yc = sb4("r_yc", [128, 512], F32)
        sq = sb4("r_sq2", [128, 512], F32)
        rs = sb4("r_rs", [128, 512], F32)
        yo = [sb4(f"r_yo{i}", [128, 512], BF16) for i in range(2)]
        yc_r, sq_r, rs_r = Res(), Res(), Res()
        yo_r = [Res(), Res()]
        for tb in range(SEQ // 512):
            sl = slice(tb * 512, (tb + 1) * 512)
            pm, pm_r = banks[tb % 2], banks_r[tb % 2]
            pv, pv_r = banks[2 + tb % 2], banks_r[2 + tb % 2]
            bd.op("pe", lambda e: e.matmul(pm[:, :], lhsT=bones, rhs=yS[:, sl], start=True, stop=True),
                  reads=[mats_r, yS_r], writes=[pm_r])
            bd.op("dve", lambda e: e.scalar_tensor_tensor(out=yc[:], in0=pm[:, :], scalar=-1.0 / 64, in1=yS[:, sl],
                                                          op0=ALU.mult, op1=ALU.add), reads=[pm_r, yS_r], writes=[yc_r])
            bd.op("act", lambda e: e.activation(out=sq[:], in_=yc[:], func=AF.Square), reads=[yc_r], writes=[sq_r])
            bd.op("pe", lambda e: e.matmul(pv[:, :], lhsT=bones, rhs=sq[:], start=True, stop=True),
                  reads=[mats_r, sq_r], writes=[pv_r])
            bd.op("act", lambda e: e.activation(out=rs[:], in_=pv[:, :], func=AF.Sqrt, scale=1.0 / 64,
                                                bias=cst[:, C_LNEPS:C_LNEPS + 1]), reads=[pv_r, cst_r], writes=[rs_r])
            bd.op("dve", lambda e: e.reciprocal(out=rs[:], in_=rs[:]), reads=[rs_r], writes=[rs_r])
            bd.op("dve", lambda e: e.tensor_tensor(out=yc[:], in0=yc[:], in1=rs[:], op=ALU.mult),
                  reads=[yc_r, rs_r], writes=[yc_r])
            bd.op("dve", lambda e: e.tensor_scalar(out=yc[:], in0=yc[:], scalar1=cst[:, C_LNW:C_LNW + 1],
                                                   scalar2=cst[:, C_LNB:C_LNB + 1], op0=ALU.mult, op1=ALU.add),
                  reads=[yc_r, cst_r], writes=[yc_r])
            bd.op("dve", lambda e: e.tensor_tensor(out=yc[:], in0=yc[:], in1=boS[:, sl], op=ALU.add),
                  reads=[yc_r, boS_r], writes=[yc_r])
            o, o_r = yo[tb % 2], yo_r[tb % 2]
            bd.op("dve", lambda e, o=o: e.tensor_tensor(out=o[:], in0=yc[:], in1=gS[:, sl], op=ALU.mult),
                  reads=[yc_r, gS_r], writes=[o_r])
            bd.dma("sp", yT[256:384, sl], o[:], reads=[o_r])
        bd.barrier()


RWKV_CHUNKED = True
import os as _os
_DBG_STAGE = int(_os.environ.get('RW_DBG', '0'))


def rwkv_masks():
    s_ = (np.arange(128) % 64)[:, None]
    t_ = (np.arange(512) % 64)[None, :]
    m = np.zeros((128, 5, 512), np.float32)
    m[:, 0, :] = (s_ < t_)
    m[:, 1, :] = (s_ <= t_)
    m[:, 2, :] = (s_ > t_)
    m[:, 3, :] = (s_ == t_)
    m[:, 4, :] = np.broadcast_to(t_ != 0, (128, 512))
    return m


def emit_rwkv_chunked(bd, nc, es, D_, cst, cst_r, K):
    sb = lambda nm, shape, dt: es.enter_context(nc.sbuf_tensor(_u(nm), shape, dt))
    pT = D_["pT"].ap()
    yT = D_["yT"].ap()
    banks, banks_r = K["banks"], K["banks_r"]
    bones, identf, mats_r = K["bones"], K["identf"], K["mats_r"]
    R_R, R_K, R_V, R_L = 640, 768, 896, 1024
    rS = sb("r_rS", [128, SEQ], F32)
    kS = sb("r_kS", [128, SEQ], F32)
    vS = sb("r_vS", [128, SEQ], F32)
    lS = sb("r_lS", [128, SEQ], F32)
    rS_r, kS_r, vS_r, lS_r = Res(), Res(), Res(), Res()
    wl = sb("r_wl", [128, 3, 128], F32)
    wl_r = Res()
    bd.dma("sp", wl[:], D_["wl"].ap(), writes=[wl_r])
    msk = sb("r_msk", [128, 5, 512], F32)
    msk_r = Res()
    bd.dma("act", msk[:], D_["rwm"].ap(), writes=[msk_r])
    c2 = sb("r_c2", [128, 2], F32)
    c2_r = Res()
    bd.op("dve", lambda e: e.tensor_scalar(out=c2[:, 0:1], in0=cst[:, C_KA:C_KA + 1], scalar1=-1.0, scalar2=1.0,
                                           op0=ALU.mult, op1=ALU.add), reads=[cst_r], writes=[c2_r])
    with ExitStack() as es2:
        sb2 = lambda nm, shape, dt: es2.enter_context(nc.sbuf_tensor(_u(nm), shape, dt))
        dd = sb2("r_dd", [128, SEQ], F32)
        dd_r = Res()
        for (row0, t, t_r, mu) in ((R_R, rS, rS_r, C_MU_R), (R_K, kS, kS_r, C_MU_K), (R_V, vS, vS_r, C_MU_V),
                                   (R_L, lS, lS_r, C_MU_L)):
            bd.dma("sp", t[:], pT[row0:row0 + 128, :], writes=[t_r])
            bd.op("dve", lambda e, t=t: e.tensor_tensor(out=dd[:, 1:SEQ], in0=t[:, 0:SEQ - 1], in1=t[:, 1:SEQ],
                                                        op=ALU.subtract), reads=[t_r], writes=[dd_r])
            bd.op("dve", lambda e, t=t: e.tensor_scalar(out=dd[:, 0:1], in0=t[:, 0:1], scalar1=-1.0, scalar2=None,
                                                        op0=ALU.mult), reads=[t_r], writes=[dd_r])
            bd.op("dve", lambda e, t=t, mu=mu: e.scalar_tensor_tensor(out=t[:], in0=dd[:], scalar=cst[:, mu:mu + 1],
                                                                      in1=t[:], op0=ALU.mult, op1=ALU.add),
                  reads=[dd_r, t_r, cst_r], writes=[t_r])
        bd.barrier()
    names = ["th", "sg", "sgm", "aT", "kkr", "sq", "nrm", "nkk", "bb", "t1", "km", "prod", "gB", "boB",
             "lw", "cl", "e1", "e2", "e3", "At", "Bt", "Kt", "Rt", "Bh", "Kh",
             "Mab", "Lab", "Mkb", "Nbr", "Nkr", "T", "TT", "Mk0", "Mk1", "Lk0", "Lk1",
             "VT", "BhT", "KhT", "yB", "yc", "sq2", "rs", "Tb", "TTb"]
    BFN = {"At", "Bt", "Kt", "Rt", "Mab", "Lab", "Mkb", "Nbr", "Nkr", "Mk0", "Mk1", "Lk0", "Lk1",
           "VT", "BhT", "KhT", "Tb", "TTb"}
    T_ = {n: sb("r_" + n, [128, 512], BF16 if n in BFN else F32) for n in names}
    T_r = {n: Res() for n in names}
    UT = sb("r_UT", [128, 4, 128], BF16)
    UT_r = Res()
    xts = sb("r_xts", [128, 128], BF16)
    xts_r = Res()
    ST = sb("r_ST", [128, 64], F32)
    ST_r = Res()
    yo = [sb(f"r_yo{i}", [128, 512], BF16) for i in range(2)]
    yo_r = [Res(), Res()]
    STb = sb("r_STb", [128, 64], BF16)
    STb_r = Res()
    bd.op("dve", lambda e: e.memset(ST[:], 0.0), writes=[ST_r])
    bd.op("dve", lambda e: e.memset(STb[:], 0.0), writes=[STb_r])
    bX, bX_r = banks[3], banks_r[3]
    bU, bU_r = banks[4], banks_r[4]
    bS, bS_r = banks[5], banks_r[5]
    bY, bY_r = banks[6], banks_r[6]
    bi = [0]

    def nb():
        b = bi[0] % 3
        bi[0] += 1
        return banks[b], banks_r[b]

    def A(fn, reads, writes):
        bd.op("act", fn, reads=reads, writes=writes)

    def V(fn, reads, writes):
        bd.op("dve", fn, reads=reads, writes=writes)

    def G(fn, reads, writes):
        bd.op("pool", fn, reads=reads, writes=writes)

    def slot(c, h):
        return ((c // 2) * 2 + h) * 64

    def fam(out_name, lname, rname, mask_k):
        pb_, pb_r = nb()
        for h in range(2):
            H0 = h * 64
            for c in range(8):
                P0 = (c % 2) * 64
                o = slot(c, h)
                bd.op("pe", lambda e, P0=P0, H0=H0, o=o, c=c: e.matmul(
                    pb_[P0:P0 + 64, o:o + 64], lhsT=T_[lname][H0:H0 + 64, c * 64:(c + 1) * 64],
                    rhs=T_[rname][H0:H0 + 64, c * 64:(c + 1) * 64], start=True, stop=True),
                    reads=[T_r[lname], T_r[rname]], writes=[pb_r], rt=H0)
        V(lambda e: e.tensor_tensor(out=T_[out_name][:], in0=pb_[:, :], in1=msk[:, mask_k, :], op=ALU.mult),
          [pb_r, msk_r], [T_r[out_name]])

    def sq16(out_name, lname, rname, add_name=None):
        pb_, pb_r = nb()
        for cpar in range(2):
            P0 = cpar * 64
            grp = [(c, h) for c in range(cpar, 8, 2) for h in range(2)]
            for gi, (c, h) in enumerate(grp):
                o = slot(c, h)
                bd.op("pe", lambda e, P0=P0, o=o: e.matmul(
                    pb_[P0:P0 + 64, o:o + 64], lhsT=T_[lname][P0:P0 + 64, o:o + 64],
                    rhs=T_[rname][P0:P0 + 64, o:o + 64], start=True, stop=True),
                    reads=[T_r[lname], T_r[rname]], writes=[pb_r], rt=P0)
        if add_name is None:
            A(lambda e: e.activation(out=T_[out_name][:], in_=pb_[:, :], func=AF.Copy), [pb_r], [T_r[out_name]])
        else:
            V(lambda e: e.tensor_tensor(out=T_[out_name][:], in0=pb_[:, :], in1=T_[add_name][:], op=ALU.add),
              [pb_r, T_r[add_name]], [T_r[out_name]])

    def tr4(out_name, src_ap_fn, src_res):
        pb_, pb_r = nb()
        for c2_ in range(4):
            bd.op("pe", lambda e, c2_=c2_: e.transpose(out=pb_[:, c2_ * 128:(c2_ + 1) * 128], in_=src_ap_fn(c2_),
                                                       identity=identf), reads=[src_res, mats_r], writes=[pb_r])
        A(lambda e: e.activation(out=T_[out_name][:], in_=pb_[:, :], func=AF.Copy), [pb_r], [T_r[out_name]])

    for tb in range(SEQ // 512):
        sl = slice(tb * 512, (tb + 1) * 512)
        A(lambda e: e.activation(out=T_["th"][:], in_=lS[:, sl], func=AF.Tanh), [lS_r], [T_r["th"]])
        A(lambda e: e.activation(out=T_["sg"][:], in_=lS[:, sl], func=AF.Sigmoid), [lS_r], [T_r["sg"]])
        pw, pw_r = nb()
        bd.op("pe", lambda e: e.matmul(pw[:, :], lhsT=wl[:, 0, :], rhs=T_["th"][:], start=True, stop=True),
              reads=[wl_r, T_r["th"]], writes=[pw_r])
        A(lambda e: e.activation(out=T_["sgm"][:], in_=pw[:, :], func=AF.Sigmoid, bias=cst[:, C_W0:C_W0 + 1]),
          [pw_r, cst_r], [T_r["sgm"]])
        V(lambda e: e.tensor_scalar(out=T_["lw"][:], in0=T_["sgm"][:], scalar1=-float(np.exp(-0.5)), scalar2=None,
                                    op0=ALU.mult), [T_r["sgm"]], [T_r["lw"]])
        pa, pa_r = nb()
        bd.op("pe", lambda e: e.matmul(pa[:, :], lhsT=wl[:, 1, :], rhs=lS[:, sl], start=True, stop=True),
              reads=[wl_r, lS_r], writes=[pa_r])
        A(lambda e: e.activation(out=T_["aT"][:], in_=pa[:, :], func=AF.Sigmoid, bias=cst[:, C_A0:C_A0 + 1]),
          [pa_r, cst_r], [T_r["aT"]])
        pg, pg_r = nb()
        bd.op("pe", lambda e: e.matmul(pg[:, :], lhsT=wl[:, 2, :], rhs=T_["sg"][:], start=True, stop=True),
              reads=[wl_r, T_r["sg"]], writes=[pg_r])
        A(lambda e: e.activation(out=T_["gB"][:], in_=pg[:, :], func=AF.Copy), [pg_r], [T_r["gB"]])
        V(lambda e: e.tensor_scalar(out=T_["kkr"][:], in0=kS[:, sl], scalar1=cst[:, C_KK:C_KK + 1], scalar2=None,
                                    op0=ALU.mult), [kS_r, cst_r], [T_r["kkr"]])
        A(lambda e: e.activation(out=T_["sq"][:], in_=T_["kkr"][:], func=AF.Square), [T_r["kkr"]], [T_r["sq"]])
        pn, pn_r = nb()
        bd.op("pe", lambda e: e.matmul(pn[:, :], lhsT=bones, rhs=T_["sq"][:], start=True, stop=True),
              reads=[mats_r, T_r["sq"]], writes=[pn_r])
        A(lambda e: e.activation(out=T_["nrm"][:], in_=pn[:, :], func=AF.Sqrt), [pn_r], [T_r["nrm"]])
        V(lambda e: e.tensor_scalar(out=T_["nrm"][:], in0=T_["nrm"][:], scalar1=1e-12, scalar2=None, op0=ALU.max),
          [T_r["nrm"]], [T_r["nrm"]])
        V(lambda e: e.reciprocal(out=T_["nrm"][:], in_=T_["nrm"][:]), [T_r["nrm"]], [T_r["nrm"]])
        V(lambda e: e.scalar_tensor_tensor(out=T_["nkk"][:], in0=T_["kkr"][:], scalar=-1.0, in1=T_["nrm"][:],
                                           op0=ALU.mult, op1=ALU.mult), [T_r["kkr"], T_r["nrm"]], [T_r["nkk"]])
        V(lambda e: e.scalar_tensor_tensor(out=T_["bb"][:], in0=T_["nkk"][:], scalar=-1.0, in1=T_["aT"][:],
                                           op0=ALU.mult, op1=ALU.mult), [T_r["nkk"], T_r["aT"]], [T_r["bb"]])
        V(lambda e: e.tensor_scalar(out=T_["t1"][:], in0=T_["aT"][:], scalar1=cst[:, C_KA:C_KA + 1],
                                    scalar2=c2[:, 0:1], op0=ALU.mult, op1=ALU.add),
          [T_r["aT"], cst_r, c2_r], [T_r["t1"]])
        V(lambda e: e.tensor_tensor(out=T_["km"][:], in0=kS[:, sl], in1=T_["t1"][:], op=ALU.mult),
          [kS_r, T_r["t1"]], [T_r["km"]])
        V(lambda e: e.scalar_tensor_tensor(out=T_["prod"][:], in0=rS[:, sl], scalar=cst[:, C_RK:C_RK + 1],
                                           in1=T_["km"][:], op0=ALU.mult, op1=ALU.mult),
          [rS_r, cst_r, T_r["km"]], [T_r["prod"]])
        pb2, pb2_r = nb()
        bd.op("pe", lambda e: e.matmul(pb2[:, :], lhsT=bones, rhs=T_["prod"][:], start=True, stop=True),
              reads=[mats_r, T_r["prod"]], writes=[pb2_r])
        V(lambda e: e.tensor_tensor(out=T_["boB"][:], in0=pb2[:, :], in1=vS[:, sl], op=ALU.mult),
          [pb2_r, vS_r], [T_r["boB"]])
        V(lambda e: e.tensor_tensor_scan(out=T_["cl"][:], data0=msk[:, 4, :], data1=T_["lw"][:], initial=0.0,
                                         op0=ALU.mult, op1=ALU.add), [msk_r, T_r["lw"]], [T_r["cl"]])
        A(lambda e: e.activation(out=T_["e1"][:], in_=T_["cl"][:], func=AF.Exp), [T_r["cl"]], [T_r["e1"]])
        A(lambda e: e.activation(out=T_["e2"][:], in_=T_["cl"][:], func=AF.Exp, scale=-1.0), [T_r["cl"]], [T_r["e2"]])
        V(lambda e: e.tensor_tensor(out=T_["e3"][:], in0=T_["cl"][:], in1=T_["lw"][:], op=ALU.subtract),
          [T_r["cl"], T_r["lw"]], [T_r["e3"]])
        A(lambda e: e.activation(out=T_["e3"][:], in_=T_["e3"][:], func=AF.Exp), [T_r["e3"]], [T_r["e3"]])
        V(lambda e: e.tensor_tensor(out=T_["At"][:], in0=T_["nkk"][:], in1=T_["e3"][:], op=ALU.mult),
          [T_r["nkk"], T_r["e3"]], [T_r["At"]])
        G(lambda e: e.tensor_tensor(out=T_["Bt"][:], in0=T_["bb"][:], in1=T_["e2"][:], op=ALU.mult),
          [T_r["bb"], T_r["e2"]], [T_r["Bt"]])
        V(lambda e: e.tensor_tensor(out=T_["Kt"][:], in0=T_["km"][:], in1=T_["e2"][:], op=ALU.mult),
          [T_r["km"], T_r["e2"]], [T_r["Kt"]])
        G(lambda e: e.tensor_tensor(out=T_["Rt"][:], in0=rS[:, sl], in1=T_["e1"][:], op=ALU.mult),
          [rS_r, T_r["e1"]], [T_r["Rt"]])
        for c in range(8):
            gam = T_["e1"][:, c * 64 + 63:c * 64 + 64]
            V(lambda e, c=c, gam=gam: e.tensor_scalar(out=T_["Bh"][:, c * 64:(c + 1) * 64], in0=T_["Bt"][:, c * 64:(c + 1) * 64],
                                                      scalar1=gam, scalar2=None, op0=ALU.mult),
              [T_r["Bt"], T_r["e1"]], [T_r["Bh"]])
            G(lambda e, c=c, gam=gam: e.tensor_scalar(out=T_["Kh"][:, c * 64:(c + 1) * 64], in0=T_["Kt"][:, c * 64:(c + 1) * 64],
                                                      scalar1=gam, scalar2=None, op0=ALU.mult),
              [T_r["Kt"], T_r["e1"]], [T_r["Kh"]])
        if _DBG_STAGE == 1:
            break
        fam("Mab", "Bt", "At", 0)
        fam("Lab", "At", "Bt", 2)
        fam("Mkb", "Kt", "At", 0)
        fam("Nbr", "Bt", "Rt", 1)
        fam("Nkr", "Kt", "Rt", 1)
        if _DBG_STAGE == 2:
            break
        V(lambda e: e.tensor_tensor(out=T_["T"][:], in0=T_["Mab"][:], in1=msk[:, 3, :], op=ALU.add),
          [T_r["Mab"], msk_r], [T_r["T"]])
        G(lambda e: e.tensor_tensor(out=T_["TT"][:], in0=T_["Lab"][:], in1=msk[:, 3, :], op=ALU.add),
          [T_r["Lab"], msk_r], [T_r["TT"]])
        A(lambda e: e.activation(out=T_["Tb"][:], in_=T_["T"][:], func=AF.Copy), [T_r["T"]], [T_r["Tb"]])
        G(lambda e: e.tensor_copy(out=T_["TTb"][:], in_=T_["TT"][:]), [T_r["TT"]], [T_r["TTb"]])
        Mp, Lp = "Mab", "Lab"
        for lev in range(1, 6):
            Mn, Ln = f"Mk{lev % 2}", f"Lk{lev % 2}"
            sq16(Mn, Lp, Mp)
            if lev < 5:
                sq16(Ln, Mp, Lp)
            sq16("T", "TTb", Mn, add_name="T")
            if lev < 5:
                sq16("TT", Mn, "TTb", add_name="TT")
                G(lambda e: e.tensor_copy(out=T_["TTb"][:], in_=T_["TT"][:]), [T_r["TT"]], [T_r["TTb"]])
            A(lambda e: e.activation(out=T_["Tb"][:], in_=T_["T"][:], func=AF.Copy), [T_r["T"]], [T_r["Tb"]])
            Mp, Lp = Mn, Ln
        if _DBG_STAGE == 3:
            break
        tr4("VT", lambda c2_: vS[:, tb * 512 + c2_ * 128: tb * 512 + (c2_ + 1) * 128], vS_r)
        tr4("BhT", lambda c2_: T_["Bh"][:, c2_ * 128:(c2_ + 1) * 128], T_r["Bh"])
        tr4("KhT", lambda c2_: T_["Kh"][:, c2_ * 128:(c2_ + 1) * 128], T_r["Kh"])
        if _DBG_STAGE == 4:
            break
        for c in range(8):
            P0 = (c % 2) * 64
            c2_ = c // 2
            cs = slice(c * 64, (c + 1) * 64)
            for h in range(2):
                H0 = h * 64
                o = slot(c, h)
                vt = T_["VT"][P0:P0 + 64, c2_ * 128 + H0:c2_ * 128 + H0 + 64]
                bd.op("pe", lambda e, P0=P0, H0=H0, cs=cs: e.matmul(
                    bX[P0:P0 + 64, H0:H0 + 64], lhsT=T_["At"][H0:H0 + 64, cs], rhs=STb[H0:H0 + 64, :],
                    start=True, stop=False), reads=[T_r["At"], STb_r], writes=[bX_r], rt=H0)
                bd.op("pe", lambda e, P0=P0, H0=H0, o=o, vt=vt: e.matmul(
                    bX[P0:P0 + 64, H0:H0 + 64], lhsT=T_["Mkb"][P0:P0 + 64, o:o + 64], rhs=vt,
                    start=False, stop=True), reads=[T_r["Mkb"], T_r["VT"]], writes=[bX_r], rt=P0)
            A(lambda e, P0=P0: e.activation(out=xts[P0:P0 + 64, :], in_=bX[P0:P0 + 64, 0:128], func=AF.Copy),
              [bX_r], [xts_r])
            for h in range(2):
                H0 = h * 64
                o = slot(c, h)
                bd.op("pe", lambda e, P0=P0, H0=H0, o=o: e.matmul(
                    bU[P0:P0 + 64, H0:H0 + 64], lhsT=T_["Tb"][P0:P0 + 64, o:o + 64], rhs=xts[P0:P0 + 64, H0:H0 + 64],
                    start=True, stop=True), reads=[T_r["Tb"], xts_r], writes=[bU_r], rt=P0)
            V(lambda e, P0=P0, c2_=c2_: e.tensor_copy(out=UT[P0:P0 + 64, c2_, :], in_=bU[P0:P0 + 64, 0:128]),
              [bU_r], [UT_r])
            for h in range(2):
                H0 = h * 64
                o = slot(c, h)
                ut = UT[P0:P0 + 64, c2_, H0:H0 + 64]
                vt = T_["VT"][P0:P0 + 64, c2_ * 128 + H0:c2_ * 128 + H0 + 64]
                bd.op("pe", lambda e, H0=H0, cs=cs: e.matmul(
                    bY[H0:H0 + 64, cs], lhsT=STb[H0:H0 + 64, :], rhs=T_["Rt"][H0:H0 + 64, cs],
                    start=True, stop=False), reads=[STb_r, T_r["Rt"]], writes=[bY_r], rt=H0)
                bd.op("pe", lambda e, H0=H0, cs=cs, P0=P0, o=o, ut=ut: e.matmul(
                    bY[H0:H0 + 64, cs], lhsT=ut, rhs=T_["Nbr"][P0:P0 + 64, o:o + 64],
                    start=False, stop=False), reads=[UT_r, T_r["Nbr"]], writes=[bY_r], rt=P0)
                bd.op("pe", lambda e, H0=H0, cs=cs, P0=P0, o=o, vt=vt: e.matmul(
                    bY[H0:H0 + 64, cs], lhsT=vt, rhs=T_["Nkr"][P0:P0 + 64, o:o + 64],
                    start=False, stop=True), reads=[T_r["VT"], T_r["Nkr"]], writes=[bY_r], rt=P0)
                bd.op("pe", lambda e, H0=H0, P0=P0, c2_=c2_, ut=ut: e.matmul(
                    bS[H0:H0 + 64, 0:64], lhsT=T_["BhT"][P0:P0 + 64, c2_ * 128 + H0:c2_ * 128 + H0 + 64], rhs=ut,
                    start=True, stop=False), reads=[T_r["BhT"], UT_r], writes=[bS_r], rt=P0)
                bd.op("pe", lambda e, H0=H0, P0=P0, c2_=c2_, vt=vt: e.matmul(
                    bS[H0:H0 + 64, 0:64], lhsT=T_["KhT"][P0:P0 + 64, c2_ * 128 + H0:c2_ * 128 + H0 + 64], rhs=vt,
                    start=False, stop=True), reads=[T_r["KhT"], T_r["VT"]], writes=[bS_r], rt=P0)
            gam = T_["e1"][:, c * 64 + 63:c * 64 + 64]
            V(lambda e, gam=gam: e.scalar_tensor_tensor(out=ST[:], in0=ST[:], scalar=gam, in1=bS[:, 0:64],
                                                        op0=ALU.mult, op1=ALU.add),
              [ST_r, bS_r, T_r["e1"]], [ST_r])
            A(lambda e: e.activation(out=STb[:], in_=ST[:], func=AF.Copy), [ST_r], [STb_r])
        if _DBG_STAGE == 5:
            break
        A(lambda e: e.activation(out=T_["yB"][:], in_=bY[:, :], func=AF.Copy), [bY_r], [T_r["yB"]])
        pm, pm_r = nb()
        bd.op("pe", lambda e: e.matmul(pm[:, :], lhsT=bones, rhs=T_["yB"][:], start=True, stop=True),
              reads=[mats_r, T_r["yB"]], writes=[pm_r])
        V(lambda e: e.scalar_tensor_tensor(out=T_["yc"][:], in0=pm[:, :], scalar=-1.0 / 64, in1=T_["yB"][:],
                                           op0=ALU.mult, op1=ALU.add), [pm_r, T_r["yB"]], [T_r["yc"]])
        A(lambda e: e.activation(out=T_["sq2"][:], in_=T_["yc"][:], func=AF.Square), [T_r["yc"]], [T_r["sq2"]])
        pv, pv_r = nb()
        bd.op("pe", lambda e: e.matmul(pv[:, :], lhsT=bones, rhs=T_["sq2"][:], start=True, stop=True),
              reads=[mats_r, T_r["sq2"]], writes=[pv_r])
        A(lambda e: e.activation(out=T_["rs"][:], in_=pv[:, :], func=AF.Sqrt, scale=1.0 / 64,
                                 bias=cst[:, C_LNEPS:C_LNEPS + 1]), [pv_r, cst_r], [T_r["rs"]])
        V(lambda e: e.reciprocal(out=T_["rs"][:], in_=T_["rs"][:]), [T_r["rs"]], [T_r["rs"]])
        V(lambda e: e.tensor_tensor(out=T_["yc"][:], in0=T_["yc"][:], in1=T_["rs"][:], op=ALU.mult),
          [T_r["yc"], T_r["rs"]], [T_r["yc"]])
        V(lambda e: e.tensor_scalar(out=T_["yc"][:], in0=T_["yc"][:], scalar1=cst[:, C_LNW:C_LNW + 1],
                                    scalar2=cst[:, C_LNB:C_LNB + 1], op0=ALU.mult, op1=ALU.add),
          [T_r["yc"], cst_r], [T_r["yc"]])
        V(lambda e: e.tensor_tensor(out=T_["yc"][:], in0=T_["yc"][:], in1=T_["boB"][:], op=ALU.add),
          [T_r["yc"], T_r["boB"]], [T_r["yc"]])
        o_, o_r = yo[tb % 2], yo_r[tb % 2]
        V(lambda e, o_=o_: e.tensor_tensor(out=o_[:], in0=T_["yc"][:], in1=T_["gB"][:], op=ALU.mult),
          [T_r["yc"], T_r["gB"]], [o_r])
        bd.dma("sp", yT[256:384, sl], o_[:], reads=[o_r])
    bd.barrier()


def emit_phaseB(nc, mkbld, es, D_, mixers, tag):
    bd = mkbld(tag + "m")
    sb = lambda nm, shape, dt: es.enter_context(nc.sbuf_tensor(_u(nm), shape, dt))
    ps = lambda nm, shape, dt: es.enter_context(nc.psum_tensor(_u(nm), shape, dt))
    cst = sb("B_cst", [128, NCST], F32)
    cst_r = Res()
    bd.dma("sp", cst[:], D_["cst"].ap(), writes=[cst_r])
    mats = sb("B_mats", [128, 4, 128], F32)
    mats_r = Res()
    bd.dma("sp", mats[:], D_["mats"].ap(), writes=[mats_r])
    identb = sb("B_identb", [128, 128], BF16)
    maskb = sb("B_maskb", [128, 128], BF16)
    identb_r, maskb_r = Res(), Res()
    bd.op("dve", lambda e: e.tensor_copy(out=identb[:], in_=mats[:, 0, :]), reads=[mats_r], writes=[identb_r])
    bd.op("dve", lambda e: e.tensor_copy(out=maskb[:], in_=mats[:, 3, :]), reads=[mats_r], writes=[maskb_r])
    banks = [ps(f"B_bank{i}", [128, 512], F32) for i in range(7)]
    banks_r = [Res() for _ in range(7)]
    ptb = ps("B_ptb", [128, 1024], BF16)
    ptb_r = Res()
    K = dict(banks=banks, banks_r=banks_r, ptb=ptb, ptb_r=ptb_r, identb=identb, identb_r=identb_r,
             mask=maskb, mask_r=maskb_r, ones=mats[:, 1, :], ones_r=mats_r, identf=mats[:, 0, :],
             bones=mats[:, 2, :], mats_r=mats_r)
    if "mla" in mixers:
        with ExitStack() as es1:
            emit_mla(bd, nc, es1, D_["pT"], D_["pos"], D_["wuq"], D_["wukv"], D_["yT"], cst, cst_r, K)
    bd.barrier()
    if "mlstm" in mixers:
        with ExitStack() as es1:
            emit_mlstm(bd, nc, es1, D_["pT"], D_["yT"], cst, cst_r, K)
        bd.barrier()
    if "rwkv" in mixers:
        bd = mkbld(tag + "r")
        with ExitStack() as es1:
            (emit_rwkv_chunked if RWKV_CHUNKED else emit_rwkv)(bd, nc, es1, D_, cst, cst_r, K)
        bd.barrier()


def prep_phaseB_consts(P, l, hc):
    c = np.zeros((128, NCST), np.float32)
    c[:, C_QN0] = P["mla_q_norm"][l][0:128]
    c[:, C_QN1] = P["mla_q_norm"][l][128:256]
    c[:, C_KVN0] = P["mla_kv_norm"][l][0:128]
    c[:, C_KVN1] = P["mla_kv_norm"][l][128:256]
    c[:, C_INVF] = np.tile(INV_FREQ, 4)
    c[:, C_SGN] = np.tile(np.concatenate([np.full(32, -1.0, np.float32), np.full(32, 1.0, np.float32)]), 2)
    for h in range(2):
        hh = 2 * hc + h
        c[:, C_MLAO0 + h] = P["mla_out_norm"][l][hh * 128:(hh + 1) * 128]
    mu = P["rwkv_mu"][l]
    ch = slice(hc * 128, hc * 128 + 128)
    c[:, C_MU_R] = mu[0:256][ch]
    c[:, C_MU_K] = mu[256:512][ch]
    c[:, C_MU_V] = mu[512:768][ch]
    c[:, C_MU_L] = mu[768:896]
    c[:, C_W0] = P["rwkv_w0"][l][ch]
    c[:, C_A0] = P["rwkv_a0"][l][ch]
    c[:, C_KK] = P["rwkv_k_k"][l][ch]
    c[:, C_KA] = P["rwkv_k_a"][l][ch]
    c[:, C_RK] = P["rwkv_r_k"][l][ch]
    c[:, C_LNW] = P["rwkv_ln_w"][l][ch]
    c[:, C_LNB] = P["rwkv_ln_b"][l][ch]
    cw = P["mlstm_conv_w"][l]
    cb = P["mlstm_conv_b"][l]
    qs = slice(hc * 64, hc * 64 + 64)
    ks = slice(128 + hc * 64, 128 + hc * 64 + 64)
    for j in range(4):
        c[0:64, C_CWQ0 + j] = cw[j][qs]
        c[0:64, C_CWK0 + j] = cw[j][ks]
    c[0:64, C_CBQ] = cb[qs]
    c[0:64, C_CBK] = cb[ks]
    c[:, C_IB] = P["mlstm_i_bias"][l][hc * 2]
    c[:, C_FB] = P["mlstm_f_bias"][l][hc * 2]
    c[:, C_IB1] = P["mlstm_i_bias"][l][hc * 2 + 1]
    c[:, C_FB1] = P["mlstm_f_bias"][l][hc * 2 + 1]
    c[:, C_MLO] = P["mlstm_out_norm"][l][ch]
    c[:, C_EPS] = EPS
    c[:, C_LNEPS] = 64e-5
    c[:, C_ONE] = 1.0
    hq = []
    hkv = []
    for h in range(2):
        hh = 2 * hc + h
        wq = P["mla_w_uq"][l][:, hh * 192:(hh + 1) * 192]
        hq += [wq[:, 0:128], wq[:, 128:192], wq[:, 160:192], wq[:, 128:160]]
        wkv = P["mla_w_ukv"][l][:, hh * 256:(hh + 1) * 256]
        hkv += [wkv]
    wuq = np.ascontiguousarray(np.concatenate(hq, axis=1))
    wukv = np.ascontiguousarray(np.concatenate(hkv, axis=1))
    wl = np.zeros((128, 3, 128), np.float32)
    wl[0:32, 0, :] = P["rwkv_w2"][l][:, ch]
    wl[32:64, 1, :] = P["rwkv_a2"][l][:, ch]
    wl[64:128, 2, :] = P["rwkv_g2"][l][:, ch]
    return dict(cst=c, wuq=wuq, wukv=wukv, wl=wl)


INV_FREQ = (10000.0 ** (-np.arange(0, 64, 2, dtype=np.float32) / np.float32(64))).astype(np.float32)


def const_mats():
    m = np.zeros((128, 4, 128), np.float32)
    m[:, 0, :] = np.eye(128, dtype=np.float32)
    m[:, 1, :] = 1.0
    m[0:64, 2, 0:64] = 1.0
    m[64:128, 2, 64:128] = 1.0
    m[:, 3, :] = (np.arange(128)[:, None] <= np.arange(128)[None, :]).astype(np.float32)
    return m


W_OUT_PERM = (list(range(0, 256)) + list(range(512, 640)) + list(range(768, 896)) +
              list(range(256, 512)) + list(range(640, 768)) + list(range(896, 1024)))
TBC = 512


def emit_phaseC(nc, bd, es, D_, final, ntok):
    sb = lambda name, shape, dt: es.enter_context(nc.sbuf_tensor(_u(name), shape, dt))
    ps = lambda name, shape, dt: es.enter_context(nc.psum_tensor(_u(name), shape, dt))
    wo = sb("C_wo", [128, 8, D], BF16)
    wg = sb("C_wg", [128, 8, DFF], BF16)
    wu = sb("C_wu", [128, 8, DFF], BF16)
    wd = sb("C_wd", [128, 22, D], BF16)
    w_r = Res()
    HS = DFF // 2
    gcol = sb("C_gcol", [128, 8], F32)
    gcol_r = Res()
    idf = sb("C_idf", [128, 128], F32)
    ident = sb("C_ident", [128, 128], BF16)
    idf_r, ident_r = Res(), Res()
    bd.dma("sp", gcol[:], D_["g"].ap(), writes=[gcol_r])
    bd.dma("sp", idf[:], D_["ident"].ap(), writes=[idf_r])
    bd.op("dve", lambda e: e.tensor_copy(out=ident[:], in_=idf[:]), reads=[idf_r], writes=[ident_r])
    if final:
        gbc = sb("C_gbc", [128, D], F32)
        gbc_r = Res()
        bd.dma("sp", gbc[:], bass.AP(D_["gf_t"], 0, [[0, 128], [1, D]]), writes=[gbc_r])
    es_stage = ExitStack()
    stage = [es_stage.enter_context(nc.sbuf_tensor(_u(f"C_stage{i}"), [128, HS], F32)) for i in range(3)]
    stage_r = [Res() for _ in range(3)]
    si = [0]
    queues = ("sp", "pool", "act")
    engs = ("dve", "pool", "act")

    def cast(dst_ap, src_ap, ncols, scale_ap=None):
        i = si[0] % 3
        si[0] += 1
        bd.dma(queues[i], stage[i][:, 0:ncols], src_ap, writes=[stage_r[i]])
        eng = engs[i] if scale_ap is None else ("dve" if i != 2 else "act")
        if eng == "act":
            if scale_ap is None:
                bd.op("act", lambda e: e.activation(out=dst_ap, in_=stage[i][:, 0:ncols], func=AF.Copy),
                      reads=[stage_r[i]], writes=[w_r])
            else:
                bd.op("act", lambda e: e.activation(out=dst_ap, in_=stage[i][:, 0:ncols], func=AF.Copy, scale=scale_ap),
                      reads=[stage_r[i], gcol_r], writes=[w_r])
        elif scale_ap is None:
            bd.op(eng, lambda e: e.tensor_copy(out=dst_ap, in_=stage[i][:, 0:ncols]), reads=[stage_r[i]], writes=[w_r])
        else:
            bd.op(eng, lambda e: e.tensor_scalar(out=dst_ap, in0=stage[i][:, 0:ncols], scalar1=scale_ap, scalar2=None,
                                                 op0=ALU.mult), reads=[stage_r[i], gcol_r], writes=[w_r])

    for kc in range(8):
        cast(wo[:, kc, :], D_["wo"].ap()[kc * 128:(kc + 1) * 128, :], D)
    for kc in range(8):
        for hh in range(2):
            cast(wg[:, kc, hh * HS:(hh + 1) * HS], D_["wg"].ap()[kc * 128:(kc + 1) * 128, hh * HS:(hh + 1) * HS], HS,
                 gcol[:, kc:kc + 1])
            cast(wu[:, kc, hh * HS:(hh + 1) * HS], D_["wu"].ap()[kc * 128:(kc + 1) * 128, hh * HS:(hh + 1) * HS], HS,
                 gcol[:, kc:kc + 1])
    for f in range(22):
        cast(wd[:, f, :], D_["wd"].ap()[f * 128:(f + 1) * 128, :], D)
    bd.barrier()
    es_stage.close()

    NJ = TBC // 128
    xs = sb("C_xs", [128, NJ, D], F32)
    xs_r = [Res() for _ in range(NJ)]
    yTb = sb("C_yTb", [128, 8, 128], BF16)
    yTb_r = Res()
    hT = sb("C_hT", [128, 8, TBC], BF16)
    hT_r = Res()
    AT = sb("C_AT", [128, 22, TBC], BF16)
    AT_r = Res()
    sgt = [sb("C_sg0", [128, TBC], BF16)] * 2
    sgt_r = [Res()] * 2
    c_xn = sb("C_xn", [128, D], BF16)
    c_xn_r = Res()
    tmp = dict(ss=sb("C_ss", [128, 4], F32), ss_r=Res(), junk=c_xn, junk_r=c_xn_r,
               xn=c_xn, xn_r=c_xn_r,
               pt=ps("C_pt", [128, D], BF16), pt_r=Res(), eps=sb("C_eps", [128, 1], F32), eps_r=Res())
    bd.op("dve", lambda e: e.memset(tmp["eps"][:], EPS), writes=[tmp["eps_r"]])
    pb = [ps(f"C_pb{i}", [128, 512], F32) for i in range(6)]
    pb_r = [Res() for _ in range(6)]
    x_ap = D_["x"].ap()
    yT_ap = D_["yT"].ap()
    out_ap = D_["out"].ap()
    for blk in range(ntok // TBC):
        t0 = blk * TBC
        for j in range(NJ):
            bd.dma("sp", yTb[:], yT_ap[:, t0 + j * 128:t0 + (j + 1) * 128].rearrange("(k p) t -> p k t", p=128),
                   writes=[yTb_r])
            bd.dma("pool", xs[:, j, :], x_ap[t0 + j * 128:t0 + (j + 1) * 128, :], writes=[xs_r[j]])
            for ch in range(2):
                po, po_r = pb[ch], pb_r[ch]
                for k in range(8):
                    bd.op("pe", lambda e, k=k, j=j, ch=ch, po=po: e.matmul(
                        po[:, :], lhsT=yTb[:, k, :], rhs=wo[:, k, ch * 512:(ch + 1) * 512],
                        start=(k == 0), stop=(k == 7)), reads=[yTb_r, w_r], writes=[po_r])
                bd.op("dve", lambda e, j=j, ch=ch, po=po: e.tensor_tensor(
                    out=xs[:, j, ch * 512:(ch + 1) * 512], in0=xs[:, j, ch * 512:(ch + 1) * 512], in1=po[:, :], op=ALU.add),
                    reads=[po_r, xs_r[j]], writes=[xs_r[j]])
            emit_rmsnorm_T(bd, xs[:, j, :], xs_r[j], hT, hT_r, j, tmp, ident)
        for f in range(22):
            pg, pg_r = pb[2 + (f % 2) * 2], pb_r[2 + (f % 2) * 2]
            pu, pu_r = pb[3 + (f % 2) * 2], pb_r[3 + (f % 2) * 2]
            for (pp, pp_r, ww) in ((pg, pg_r, wg), (pu, pu_r, wu)):
                for k in range(8):
                    bd.op("pe", lambda e, k=k, f=f, pp=pp, ww=ww: e.matmul(
                        pp[:, 0:TBC], lhsT=ww[:, k, f * 128:(f + 1) * 128], rhs=hT[:, k, :],
                        start=(k == 0), stop=(k == 7)), reads=[w_r, hT_r], writes=[pp_r])
            sg, sg_r = sgt[f % 2], sgt_r[f % 2]
            bd.op("act", lambda e, sg=sg, pg=pg: e.activation(out=sg[:], in_=pg[:, 0:TBC], func=AF.Silu),
                  reads=[pg_r], writes=[sg_r])
            bd.op("dve", lambda e, sg=sg, pu=pu, f=f: e.tensor_tensor(out=AT[:, f, :], in0=sg[:], in1=pu[:, 0:TBC],
                                                                      op=ALU.mult),
                  reads=[sg_r, pu_r], writes=[AT_r])
        for j in range(NJ):
            for ch in range(2):
                pd, pd_r = pb[ch], pb_r[ch]
                for f in range(22):
                    bd.op("pe", lambda e, f=f, j=j, ch=ch, pd=pd: e.matmul(
                        pd[:, :], lhsT=AT[:, f, j * 128:(j + 1) * 128], rhs=wd[:, f, ch * 512:(ch + 1) * 512],
                        start=(f == 0), stop=(f == 21)), reads=[AT_r, w_r], writes=[pd_r])
                bd.op("dve", lambda e, j=j, ch=ch, pd=pd: e.tensor_tensor(
                    out=xs[:, j, ch * 512:(ch + 1) * 512], in0=xs[:, j, ch * 512:(ch + 1) * 512], in1=pd[:, :], op=ALU.add),
                    reads=[pd_r, xs_r[j]], writes=[xs_r[j]])
            if final:
                ss, ss_r = tmp["ss"], tmp["ss_r"]
                bd.op("dve", lambda e, j=j: e.scalar_tensor_tensor(
                    out=tmp["junk"][:], in0=xs[:, j, :], scalar=1.0, in1=xs[:, j, :], op0=ALU.mult, op1=ALU.mult,
                    accum_out=ss[:, 0:1]), reads=[xs_r[j]], writes=[tmp["junk_r"], ss_r])
                bd.op("act", lambda e: e.activation(out=ss[:, 1:2], in_=ss[:, 0:1], func=AF.Sqrt, scale=1.0 / D,
                                                    bias=tmp["eps"][:, 0:1]), reads=[ss_r, tmp["eps_r"]], writes=[ss_r])
                bd.op("dve", lambda e: e.reciprocal(out=ss[:, 2:3], in_=ss[:, 1:2]), reads=[ss_r], writes=[ss_r])
                bd.op("dve", lambda e, j=j: e.scalar_tensor_tensor(
                    out=xs[:, j, :], in0=xs[:, j, :], scalar=ss[:, 2:3], in1=gbc[:], op0=ALU.mult, op1=ALU.mult),
                    reads=[xs_r[j], ss_r, gbc_r], writes=[xs_r[j]])
            bd.dma("sp", out_ap[t0 + j * 128:t0 + (j + 1) * 128, :], xs[:, j, :], reads=[xs_r[j]])
    bd.barrier()


def build_fused(phases=None):
    nc = bass.Bass("TRN2", target_bir_lowering=False)
    dt = nc.dram_tensor
    x_d = dt("x", [SEQ, D], F32, kind="ExternalInput")
    pos_d = dt("pos", [1, SEQ], I32, kind="ExternalInput")
    wA_d = dt("wA", [2, D, NCOLA], F32, kind="ExternalInput")
    gA_d = dt("gA", [2, 128, 8], F32, kind="ExternalInput")
    id_d = dt("ident", [128, 128], F32, kind="ExternalInput")
    mats_d = dt("mats", [128, 4, 128], F32, kind="ExternalInput")
    rwm_d = dt("rwm", [128, 5, 512], F32, kind="ExternalInput")
    wuq_d = dt("wuq", [2, 2, 256, 512], F32, kind="ExternalInput")
    wukv_d = dt("wukv", [2, 2, 256, 512], F32, kind="ExternalInput")
    cst_d = dt("cst", [2, 2, 128, NCST], F32, kind="ExternalInput")
    wl_d = dt("wl", [2, 2, 128, 3, 128], F32, kind="ExternalInput")
    wo_d = dt("wo", [2, D, D], F32, kind="ExternalInput")
    wg_d = dt("wg", [2, D, DFF], F32, kind="ExternalInput")
    wu_d = dt("wu", [2, D, DFF], F32, kind="ExternalInput")
    wd_d = dt("wd", [2, DFF, D], F32, kind="ExternalInput")
    gC_d = dt("gC", [2, 128, 8], F32, kind="ExternalInput")
    gf_d = dt("gf", [1, D], F32, kind="ExternalInput")
    out_d = dt("out", [SEQ, D], F32, kind="ExternalOutput")
    pT_d = dt("pT_scr", [NCOLA, SEQ], F32)
    yT_d = dt("yT_scr", [D, SEQ], BF16)
    scr_d = dt("rw_scr", [2 * SEQ * 320], F32)
    x1_d = dt("x1_scr", [SEQ, D], F32)
    shared = {}
    mkbld = lambda tag: Bld(nc, tag, shared)
    for l in range(2):
        x_in = x_d if l == 0 else x1_d
        x_out = x1_d if l == 0 else out_d
        if phases is None or f"A{l}" in phases:
          with ExitStack() as es:
            bd = mkbld(f"A{l}")
            emit_phaseA(nc, bd, es, x_in, View(wA_d.ap()[l]), View(gA_d.ap()[l]), id_d, pT_d, SEQ)
        for hc in range(2):
            if phases is not None and f"B{l}{hc}" not in phases:
                continue
            D_ = dict(pT=View(pT_d.ap()[hc * HALF_ROWS:(hc + 1) * HALF_ROWS, :]), pos=pos_d,
                      wuq=View(wuq_d.ap()[l, hc]), wukv=View(wukv_d.ap()[l, hc]), cst=View(cst_d.ap()[l, hc]),
                      wl=View(wl_d.ap()[l, hc]), mats=mats_d, yT=View(yT_d.ap()[hc * 512:(hc + 1) * 512, :]),
                      scr=scr_d, rwm=rwm_d)
            with ExitStack() as es:
                emit_phaseB(nc, mkbld, es, D_, ("mla", "mlstm", "rwkv"), f"B{l}{hc}")
        D_ = dict(x=x_in, yT=yT_d, wo=View(wo_d.ap()[l]), wg=View(wg_d.ap()[l]), wu=View(wu_d.ap()[l]),
                  wd=View(wd_d.ap()[l]), g=View(gC_d.ap()[l]), gf_t=gf_d, ident=id_d, out=x_out)
        if phases is None or f"C{l}" in phases:
          with ExitStack() as es:
            bd = mkbld(f"C{l}")
            emit_phaseC(nc, bd, es, D_, l == 1, SEQ)
    return nc


_CACHE = {}
_PHASES = None


def kernel(**inputs):
    P = {k: np.asarray(v) for k, v in inputs.items()}
    x = np.asarray(P["x"], np.float32)
    positions = np.asarray(P["positions"]).astype(np.int32)
    if "nc" not in _CACHE:
        _CACHE["nc"] = build_fused(_PHASES)
    nc = _CACHE["nc"]
    f32 = lambda a: np.ascontiguousarray(np.asarray(a, np.float32))
    wA = f32(np.stack([prep_w_inA(P["w_in"][l]) for l in range(2)]))
    gA = f32(np.stack([_gcol(P["mix_norm"][l]) for l in range(2)]))
    gC = f32(np.stack([_gcol(P["ffn_norm"][l]) for l in range(2)]))
    pb = [[prep_phaseB_consts(P, l, hc) for hc in range(2)] for l in range(2)]
    stk = lambda key: f32(np.stack([np.stack([pb[l][hc][key] for hc in range(2)]) for l in range(2)]))
    common = dict(
        wA=wA, gA=gA, ident=np.eye(128, dtype=np.float32), mats=const_mats(), rwm=rwkv_masks(),
        wuq=stk("wuq"), wukv=stk("wukv"), cst=stk("cst"), wl=stk("wl"),
        wo=f32(np.stack([P["w_out"][l][W_OUT_PERM, :] for l in range(2)])),
        wg=f32(P["w_gate"]), wu=f32(P["w_up"]), wd=f32(P["w_down"]), gC=gC,
        gf=f32(np.asarray(P["final_norm"]).reshape(1, D)))
    in_maps = []
    for c in range(NCORES):
        b = c // 2
        m = dict(common)
        m["x"] = f32(x[b])
        m["pos"] = np.ascontiguousarray(positions[b:b + 1])
        in_maps.append(m)
    res = run_bass_kernel_spmd(nc, in_maps, core_ids=list(range(NCORES)))
    out = np.zeros((4, SEQ, D), np.float32)
    for b in range(4):
        out[b] = res.results[2 * b]["out"]
    return out
```

```python
from contextlib import ExitStack
import numpy as np
import concourse.bass as bass
import concourse.mybir as mybir
from concourse.bass_utils import run_bass_kernel_spmd

F32 = mybir.dt.float32
BF16 = mybir.dt.bfloat16
I32 = mybir.dt.int32
ALU = mybir.AluOpType
AF = mybir.ActivationFunctionType

D = 1024
SEQ = 4096
NTOK = 2048
DFF = 2816
NCORES = 8
EPS = 1e-6

NCH = 13
CH_ROWS = [128] * 12 + [4]
HALF_ROWS = 12 * 128 + 4
CH_OFF = [i * 128 for i in range(13)]
NCOLA = 2 * HALF_ROWS


def half_cols(hc):
    cols = []
    cols += list(range(0, 256))
    cols += list(range(256, 512))
    cols += list(range(512, 576))
    cols += list(range(544, 576)) + list(range(512, 544))
    R0 = 576
    for part in range(3):
        cols += list(range(R0 + part * 256 + hc * 128, R0 + part * 256 + hc * 128 + 128))
    cols += list(range(R0 + 768, R0 + 896))
    M0 = 576 + 896
    cols += list(range(M0 + hc * 64, M0 + hc * 64 + 64))
    cols += list(range(M0 + 128 + hc * 64, M0 + 128 + hc * 64 + 64))
    cols += list(range(M0 + 256 + hc * 128, M0 + 256 + hc * 128 + 128))
    cols += list(range(M0 + 520 + hc * 128, M0 + 520 + hc * 128 + 128))
    cols += list(range(M0 + 512 + hc * 2, M0 + 512 + hc * 2 + 2))
    cols += list(range(M0 + 516 + hc * 2, M0 + 516 + hc * 2 + 2))
    assert len(cols) == HALF_ROWS
    return cols


class Res:
    __slots__ = ("w", "r", "name")

    def __init__(self, name=""):
        self.w = None
        self.r = {}
        self.name = name


class Bld:
    NDMA = 8

    def __init__(self, nc, tag="", shared=None):
        self.nc = nc
        self.E = {"pe": nc.tensor, "dve": nc.vector, "act": nc.scalar, "pool": nc.gpsimd, "sp": nc.sync}
        self.sems = {}
        self.cnt = {}
        self.seen = {e: {} for e in self.E}
        self.touched = set()
        for e in self.E:
            self.sems[e] = nc.alloc_semaphore(name=f"s{tag}_{e}")
            self.cnt[e] = 0
        if shared is not None and "sems" in shared:
            self.sems.update(shared["sems"])
            self.cnt.update(shared["cnt"])
            self.dslot = shared["dslot"]
        else:
            dsems, dcnt = {}, {}
            self.dslot = {}
            for q in ("sp", "act", "pool"):
                for i in range(self.NDMA):
                    k = f"d{q}{i}"
                    dsems[k] = nc.alloc_semaphore(name=f"sdma_{k}")
                    dcnt[k] = 0
                self.dslot[q] = 0
            self.sems.update(dsems)
            self.cnt.update(dcnt)
            if shared is not None:
                shared["sems"] = dsems
                shared["dslot"] = self.dslot
                shared["cnt"] = {}
        self.shared = shared

    def _sync_shared(self):
        if self.shared is not None:
            for k in self.shared["sems"]:
                self.shared["cnt"][k] = self.cnt[k]

    def _wait(self, eng, deps):
        best = {}
        for k, v in deps:
            if v > best.get(k, 0):
                best[k] = v
        for k, v in best.items():
            if self.seen[eng].get(k, 0) >= v:
                continue
            self.E[eng].wait_ge(self.sems[k], v)
            self.seen[eng][k] = v

    def _deps(self, eng, reads, writes):
        deps = []
        for r in reads:
            if r.w is not None:
                if not (eng == "pe" and r.w[0] == "pe"):
                    deps.append(r.w)
        for w in writes:
            if w.w is not None and (w.w[0] != eng or eng != "pe"):
                deps.append(w.w)
            for k, v in w.r.items():
                if k != eng or eng != "pe":
                    deps.append((k, v))
        return deps

    def _mark(self, ev, reads, writes):
        self.touched.update(reads)
        self.touched.update(writes)
        for r in reads:
            if ev[1] > r.r.get(ev[0], 0):
                r.r[ev[0]] = ev[1]
        for w in writes:
            w.w = ev
            w.r = {}

    def op(self, eng, fn, reads=(), writes=(), ser=False, rt=None):
        if eng == "pe":
            last = getattr(self, "last_rt", None)
            if rt != last and self.cnt["pe"] > 0:
                self._wait("pe", [("pe", self.cnt["pe"])])
            self.last_rt = rt
        self._wait(eng, self._deps(eng, reads, writes))
        ins = fn(self.E[eng])
        self.cnt[eng] += 1
        ins.then_inc(self.sems[eng], 1)
        self._mark((eng, self.cnt[eng]), reads, writes)
        if ser:
            self._wait(eng, [(eng, self.cnt[eng])])
        return ins

    def dma(self, q, out, in_, reads=(), writes=()):
        i = self.dslot[q]
        self.dslot[q] = (i + 1) % self.NDMA
        k = f"d{q}{i}"
        deps = self._deps(k, reads, writes)
        deps.append((k, self.cnt[k]))
        self._wait(q, deps)
        ins = self.E[q].dma_start(out=out, in_=in_)
        self.cnt[k] += 16
        ins.then_inc(self.sems[k], 16)
        self._mark((k, self.cnt[k]), reads, writes)
        return ins

    def barrier(self):
        deps = [(k, v) for k, v in self.cnt.items() if v > 0]
        for e in ("sp", "pe", "dve", "act", "pool"):
            self._wait(e, deps)
        for e in ("sp", "pe", "dve", "act", "pool"):
            for k, v in deps:
                assert self.seen[e].get(k, 0) >= v
        for r in self.touched:
            r.w = None
            r.r = {}
        self.touched = set()
        self._sync_shared()

    def wait_all(self, eng, ress):
        deps = []
        for r in ress:
            if r.w is not None:
                deps.append(r.w)
        self._wait(eng, deps)


def dram_ap(t, offset, pattern):
    return bass.AP(t, offset, [list(p) for p in pattern])


def emit_rmsnorm_T(bd, x_tile, x_res, hT, hT_res, j, tmp, ident):
    ss, ss_r = tmp["ss"], tmp["ss_r"]
    junk, junk_r = tmp["junk"], tmp["junk_r"]
    xn, xn_r = tmp["xn"], tmp["xn_r"]
    pt, pt_r = tmp["pt"], tmp["pt_r"]
    bd.op("dve", lambda e: e.scalar_tensor_tensor(out=junk[:], in0=x_tile, scalar=1.0, in1=x_tile,
                                                  op0=ALU.mult, op1=ALU.mult, accum_out=ss[:, 0:1]),
          reads=[x_res], writes=[junk_r, ss_r])
    bd.op("act", lambda e: e.activation(out=ss[:, 1:2], in_=ss[:, 0:1], func=AF.Sqrt, scale=1.0 / D,
                                        bias=tmp["eps"][:, 0:1]), reads=[ss_r, tmp["eps_r"]], writes=[ss_r])
    bd.op("dve", lambda e: e.reciprocal(out=ss[:, 2:3], in_=ss[:, 1:2]), reads=[ss_r], writes=[ss_r])
    bd.op("act", lambda e: e.activation(out=xn[:], in_=x_tile, func=AF.Copy, scale=ss[:, 2:3]),
          reads=[x_res, ss_r], writes=[xn_r])
    for kc in range(8):
        bd.op("pe", lambda e, kc=kc: e.transpose(out=pt[:, kc * 128:(kc + 1) * 128],
                                                 in_=xn[:, kc * 128:(kc + 1) * 128], identity=ident[:]),
              reads=[xn_r], writes=[pt_r])
    bd.op("act", lambda e: e.activation(out=hT[:, :, j * 128:(j + 1) * 128],
                                        in_=pt[:].rearrange("p (k t) -> p k t", k=8), func=AF.Copy),
          reads=[pt_r], writes=[hT_res])


_UC = [0]


def _u(name):
    _UC[0] += 1
    return f"{name}_{_UC[0]}"


class View:
    def __init__(self, ap):
        self._ap = ap

    def ap(self):
        return self._ap


def emit_phaseA(nc, bd, es, x_d, w_d, g_d, id_d, pT_d, ntok):
    sb = lambda name, shape, dt: es.enter_context(nc.sbuf_tensor(_u(name), shape, dt))
    ps = lambda name, shape, dt: es.enter_context(nc.psum_tensor(_u(name), shape, dt))
    wb = sb("A_wb", [128, 8, NCOLA], BF16)
    wb_r = [Res() for _ in range(8)]
    stage = [sb(f"A_stage{i}", [128, NCOLA], F32) for i in range(2)]
    stage_r = [Res(), Res()]
    gcol = sb("A_gcol", [128, 8], F32)
    gcol_r = Res()
    idf = sb("A_idf", [128, 128], F32)
    ident = sb("A_ident", [128, 128], BF16)
    ident_r = Res()
    idf_r = Res()
    xt = [sb(f"A_xt{i}", [128, D], F32) for i in range(2)]
    xt_r = [Res(), Res()]
    hT = [sb(f"A_hT{i}", [128, 8, 512], BF16) for i in range(2)]
    hT_r = [Res(), Res()]
    tmp = dict(ss=sb("A_ss", [128, 4], F32), ss_r=Res(), junk=sb("A_junk", [128, D], BF16), junk_r=Res(),
               xn=sb("A_xn", [128, D], BF16), xn_r=Res(),
               pt=ps("A_pt", [128, D], BF16), pt_r=Res(), eps=sb("A_eps", [128, 1], F32), eps_r=Res())
    bd.op("dve", lambda e: e.memset(tmp["eps"][:], EPS), writes=[tmp["eps_r"]])
    NPB = 4
    pb = [ps(f"A_pb{i}", [128, 512], F32) for i in range(NPB)]
    pb_r = [Res() for _ in range(NPB)]
    ost = [sb(f"A_ost{i}", [128, 512], F32) for i in range(4)]
    ost_r = [Res() for _ in range(4)]

    bd.dma("sp", gcol[:], g_d.ap(), writes=[gcol_r])
    bd.dma("sp", idf[:], id_d.ap(), writes=[idf_r])
    bd.op("dve", lambda e: e.tensor_copy(out=ident[:], in_=idf[:]), reads=[idf_r], writes=[ident_r])
    for kc in range(8):
        s = kc % 2
        bd.dma("pool", stage[s][:], w_d.ap()[kc * 128:(kc + 1) * 128, :], writes=[stage_r[s]])
        bd.op("dve", lambda e, kc=kc, s=s: e.tensor_scalar(out=wb[:, kc, :], in0=stage[s][:],
                                                          scalar1=gcol[:, kc:kc + 1], scalar2=None, op0=ALU.mult),
              reads=[stage_r[s], gcol_r], writes=[wb_r[kc]])
    x_ap = x_d.ap()
    pT_ap = pT_d.ap()
    nblk = ntok // 512
    oi = 0
    for blk in range(nblk):
        hb = blk % 2
        for j in range(4):
            ti = blk * 4 + j
            xb = ti % 2
            bd.dma("sp", xt[xb][:], x_ap[ti * 128:(ti + 1) * 128, :], writes=[xt_r[xb]])
            tmp2 = dict(tmp)
            emit_rmsnorm_T(bd, xt[xb][:], xt_r[xb], hT[hb], hT_r[hb], j, tmp2, ident)
        for half in range(2):
            for c in range(NCH):
                m = CH_ROWS[c]
                col0 = half * HALF_ROWS + CH_OFF[c]
                pbi = oi % NPB
                for kc in range(8):
                    bd.op("pe", lambda e, kc=kc, col0=col0, m=m, pbi=pbi, hb=hb: e.matmul(
                        pb[pbi][0:m, :], lhsT=wb[:, kc, col0:col0 + m], rhs=hT[hb][:, kc, :],
                        start=(kc == 0), stop=(kc == 7)),
                        reads=[wb_r[kc], hT_r[hb]], writes=[pb_r[pbi]])
                osi = oi % 4
                eng = "act" if oi % 2 == 0 else "dve"
                if eng == "act":
                    bd.op("act", lambda e, m=m, pbi=pbi, osi=osi: e.activation(
                        out=ost[osi][0:m, :], in_=pb[pbi][0:m, :], func=AF.Copy),
                        reads=[pb_r[pbi]], writes=[ost_r[osi]])
                else:
                    bd.op("dve", lambda e, m=m, pbi=pbi, osi=osi: e.tensor_copy(
                        out=ost[osi][0:m, :], in_=pb[pbi][0:m, :]),
                        reads=[pb_r[pbi]], writes=[ost_r[osi]])
                bd.dma("sp", pT_ap[col0:col0 + m, blk * 512:(blk + 1) * 512], ost[osi][0:m, :],
                       reads=[ost_r[osi]])
                oi += 1
    bd.barrier()


def _gcol(g):
    return np.ascontiguousarray(np.asarray(g, np.float32).reshape(8, 128).T)


def prep_w_inA(w_in_l):
    cols = half_cols(0) + half_cols(1)
    return np.ascontiguousarray(w_in_l[:, cols])


TWO_PI = 6.283185307179586
(C_QN0, C_QN1, C_KVN0, C_KVN1, C_INVF, C_SGN, C_MLAO0, C_MLAO1,
 C_MU_R, C_MU_K, C_MU_V, C_MU_L, C_W0, C_A0, C_KK, C_KA, C_RK, C_LNW, C_LNB,
 C_CWQ0, C_CWQ1, C_CWQ2, C_CWQ3, C_CBQ, C_CWK0, C_CWK1, C_CWK2, C_CWK3, C_CBK,
 C_IB, C_FB, C_MLO, C_EPS, C_LNEPS, C_ONE, C_ZERO, C_IB1, C_FB1) = range(38)
NCST = 38


def emit_attention(bd, nc, es, name, heads, scale_exp, dv, wfun, out_fn, PS):
    sb = lambda nm, shape, dt: es.enter_context(nc.sbuf_tensor(_u(nm), shape, dt))
    LOOK = 3
    NST = len(PS["st"])
    NPT = LOOK + 2
    pT = [sb(f"{name}_pT{i}", [128, 512], BF16) for i in range(NPT)]
    pT_r = [Res() for _ in range(NPT)]
    mask = PS["mask"]
    mask_r = PS["mask_r"]
    blocks = [(h, qb, kt) for h in heads for qb in range(SEQ // 512) for kt in range(4 * (qb + 1))]
    pending = []

    def stage1(i):
        h, qb, kt = blocks[i]
        st, st_r = PS["st"][i % NST]
        kp = PS["kparts"](h, kt)
        qp = PS["qparts"](h, qb)
        n = len(kp)
        jd = kt - 4 * qb
        c0 = max(jd, 0) * 128
        for a in range(n):
            bd.op("pe", lambda e, a=a: e.matmul(st[:, c0:512], lhsT=kp[a][0], rhs=qp[a][0][:, c0:512],
                                                start=(a == 0), stop=(a == n - 1)),
                  reads=[kp[a][1], qp[a][1]], writes=[st_r])
        p, p_r = pT[i % NPT], pT_r[i % NPT]
        wfun(h, kt, qb, st, st_r, p, p_r, c0)
        if jd >= 0:
            bd.op("pool", lambda e: e.tensor_tensor(
                out=p[:, jd * 128:(jd + 1) * 128], in0=p[:, jd * 128:(jd + 1) * 128], in1=mask[:],
                op=ALU.mult), reads=[p_r, mask_r], writes=[p_r])

    def stage2(i):
        h, qb, kt = blocks[i]
        p, p_r = pT[i % NPT], pT_r[i % NPT]
        v_ap, v_r = PS["v"](h, kt)
        for j in range(4):
            qt = 4 * qb + j
            if qt < kt:
                continue
            o, o_r = PS["o"][j]
            bd.op("pe", lambda e, o=o, j=j, qt=qt: e.matmul(
                o[:, 0:dv + 1], lhsT=p[:, j * 128:(j + 1) * 128], rhs=v_ap,
                start=(kt == 0), stop=(kt == qt)), reads=[p_r, v_r], writes=[o_r])
            if kt == qt:
                th = out_fn(h, qt, o, o_r)
                if th is not None:
                    pending.append([3, th])

    nblk = len(blocks)
    for i in range(nblk + LOOK):
        if i < nblk:
            stage1(i)
        if i - LOOK >= 0:
            stage2(i - LOOK)
        for item in pending:
            item[0] -= 1
        while pending and pending[0][0] <= 0:
            pending.pop(0)[1]()
    while pending:
        pending.pop(0)[1]()


_ROPE_CACHE = {}


def emit_rope_tables(bd, nc, es, pos_d, cst, cst_r, CS, SN, tab_r):
    key = id(nc)
    if key in _ROPE_CACHE:
        scr = _ROPE_CACHE[key]
        bd.dma("sp", CS[:], scr.ap()[0], writes=[tab_r])
        bd.dma("act", SN[:], scr.ap()[1], writes=[tab_r])
        return
    with ExitStack() as es2:
        sb = lambda nm, shape, dt: es2.enter_context(nc.sbuf_tensor(_u(nm), shape, dt))
        ti = sb("rt_i", [64, SEQ], I32)
        ta = sb("rt_a", [64, SEQ], F32)
        tb = sb("rt_b", [64, SEQ], F32)
        ti_r, ta_r, tb_r = Res(), Res(), Res()
        src = bass.AP(pos_d, 0, [[0, 64], [1, SEQ]])
        bd.dma("sp", ti[:], src, writes=[ti_r])
        bd.op("dve", lambda e: e.tensor_copy(out=ta[:], in_=ti[:]), reads=[ti_r], writes=[ta_r])
        bd.op("dve", lambda e: e.tensor_scalar(out=ta[:], in0=ta[:], scalar1=cst[0:64, C_INVF:C_INVF + 1],
                                               scalar2=None, op0=ALU.mult), reads=[ta_r, cst_r], writes=[ta_r])
        for which in (0, 1):
            shift = 0.0 if which == 0 else TWO_PI / 4
            bd.op("dve", lambda e: e.tensor_scalar(out=tb[:], in0=ta[:], scalar1=shift, scalar2=1.0 / TWO_PI,
                                                   op0=ALU.add, op1=ALU.mult), reads=[ta_r], writes=[tb_r])
            bd.op("dve", lambda e: e.tensor_copy(out=ti[:], in_=tb[:]), reads=[tb_r], writes=[ti_r])
            bd.op("dve", lambda e: e.tensor_copy(out=tb[:], in_=ti[:]), reads=[ti_r], writes=[tb_r])
            bd.op("dve", lambda e: e.scalar_tensor_tensor(out=tb[:], in0=tb[:], scalar=-TWO_PI, in1=ta[:],
                                                          op0=ALU.mult, op1=ALU.add),
                  reads=[tb_r, ta_r], writes=[tb_r])
            bd.op("dve", lambda e: e.tensor_scalar(out=tb[:], in0=tb[:], scalar1=shift, scalar2=TWO_PI / 2,
                                                   op0=ALU.add, op1=ALU.min), reads=[tb_r], writes=[tb_r])
            bd.op("dve", lambda e: e.tensor_scalar(out=tb[:], in0=tb[:], scalar1=-TWO_PI / 2, scalar2=None,
                                                   op0=ALU.max), reads=[tb_r], writes=[tb_r])
            if which == 0:
                bd.op("act", lambda e: e.activation(out=SN[:], in_=tb[:], func=AF.Sin,
                                                    scale=cst[0:64, C_SGN:C_SGN + 1]),
                      reads=[tb_r, cst_r], writes=[tab_r])
            else:
                bd.op("act", lambda e: e.activation(out=CS[:], in_=tb[:], func=AF.Sin),
                      reads=[tb_r], writes=[tab_r])
        scr = nc.dram_tensor(_u("rope_scr"), [2, 64, SEQ], F32)
        bd.dma("sp", scr.ap()[0], CS[:], reads=[tab_r])
        bd.dma("act", scr.ap()[1], SN[:], reads=[tab_r])
        _ROPE_CACHE[key] = scr
        bd.barrier()


def emit_mla(bd, nc, es, pT_d, pos_d, wuq_d, wukv_d, yT_d, cst, cst_r, K):
    sb = lambda nm, shape, dt: es.enter_context(nc.sbuf_tensor(_u(nm), shape, dt))
    pT = pT_d.ap()
    yT = yT_d.ap()
    CS = sb("m_CS", [64, SEQ], F32)
    SN = sb("m_SN", [64, SEQ], F32)
    tab_r = Res()
    emit_rope_tables(bd, nc, es, pos_d, cst, cst_r, CS, SN, tab_r)
    wq = sb("m_wq", [128, 2, 512], BF16)
    wkv = sb("m_wkv", [128, 2, 512], BF16)
    wq_r, wkv_r = Res(), Res()
    wst = sb("m_wst", [128, 2, 512], F32)
    wst_r = Res()
    for (wd, wt, wr) in ((wuq_d, wq, wq_r), (wukv_d, wkv, wkv_r)):
        bd.dma("sp", wst[:], wd.ap().rearrange("(k p) n -> p k n", p=128), writes=[wst_r])
        bd.op("dve", lambda e, wt=wt: e.tensor_copy(out=wt[:], in_=wst[:]), reads=[wst_r], writes=[wr])
    Qn = [sb(f"m_Qn{h}", [128, SEQ], BF16) for h in range(2)]
    Qr = [sb(f"m_Qr{h}", [64, SEQ], BF16) for h in range(2)]
    Kn = [sb(f"m_Kn{h}", [128, SEQ], BF16) for h in range(2)]
    Kr = sb("m_Kr", [64, SEQ], BF16)
    V = [sb(f"m_V{h}", [128, 32, 129], BF16) for h in range(2)]
    qk_r = Res()
    for h in range(2):
        bd.op("pool", lambda e, h=h: e.memset(V[h][:, :, 128:129], 1.0), writes=[qk_r])
    banks, banks_r, ptb, ptb_r = K["banks"], K["banks_r"], K["ptb"], K["ptb_r"]
    ones, ones_r = K["ones"], K["ones_r"]
    with ExitStack() as es2:
        sb2 = lambda nm, shape, dt: es2.enter_context(nc.sbuf_tensor(_u(nm), shape, dt))
        cf = [sb2(f"m_cf{i}", [128, 2, 512], F32) for i in range(2)]
        cf_r = [Res(), Res()]
        sq = sb2("m_sq", [128, 2, 512], F32)
        sq_r = Res()
        rs = sb2("m_rs", [128, 512], F32)
        rs_r = Res()
        cn = [sb2(f"m_cn{i}", [128, 2, 512], BF16) for i in range(2)]
        cn_r = [Res(), Res()]
        kx = sb2("m_kx", [64, 2, 512], F32)
        kx_r = Res()
        t1 = sb2("m_t1", [64, 512], F32)
        t2 = sb2("m_t2", [64, 512], F32)
        t1_r, t2_r = Res(), Res()
        bi = 0

        def nb():
            nonlocal bi
            b = bi % 6
            bi += 1
            return banks[b], banks_r[b]

        def rope(dst, x_ap, xs_ap, src_res, sl):
            bd.op("dve", lambda e: e.tensor_tensor(out=t1[:], in0=x_ap, in1=CS[:, sl], op=ALU.mult),
                  reads=src_res + [tab_r], writes=[t1_r])
            bd.op("dve", lambda e: e.tensor_tensor(out=t2[:], in0=xs_ap, in1=SN[:, sl], op=ALU.mult),
                  reads=src_res + [tab_r], writes=[t2_r])
            bd.op("dve", lambda e: e.tensor_tensor(out=dst, in0=t1[:], in1=t2[:], op=ALU.add),
                  reads=[t1_r, t2_r], writes=[qk_r])

        for tb in range(SEQ // 512):
            sl = slice(tb * 512, (tb + 1) * 512)
            for which in (0, 1):
                ci = which
                row0 = which * 256
                bd.dma("sp", cf[ci][:], pT[row0:row0 + 256, sl].rearrange("(k p) t -> p k t", p=128),
                       writes=[cf_r[ci]])
                bd.op("act", lambda e, ci=ci: e.activation(out=sq[:], in_=cf[ci][:], func=AF.Square),
                      reads=[cf_r[ci]], writes=[sq_r])
                pss, pss_r = nb()
                for c in range(2):
                    bd.op("pe", lambda e, c=c, pss=pss: e.matmul(pss[:, :], lhsT=ones[:], rhs=sq[:, c, :],
                                                                 start=(c == 0), stop=(c == 1)),
                          reads=[ones_r, sq_r], writes=[pss_r])
                bd.op("act", lambda e, pss=pss: e.activation(out=rs[:], in_=pss[:, :], func=AF.Sqrt,
                                                             scale=1.0 / 256, bias=cst[:, C_EPS:C_EPS + 1]),
                      reads=[pss_r, cst_r], writes=[rs_r])
                bd.op("dve", lambda e: e.reciprocal(out=rs[:], in_=rs[:]), reads=[rs_r], writes=[rs_r])
                gcol = C_QN0 if which == 0 else C_KVN0
                for c in range(2):
                    bd.op("dve", lambda e, c=c, ci=ci, gcol=gcol: e.scalar_tensor_tensor(
                        out=cn[ci][:, c, :], in0=cf[ci][:, c, :], scalar=cst[:, gcol + c:gcol + c + 1], in1=rs[:],
                        op0=ALU.mult, op1=ALU.mult), reads=[cf_r[ci], rs_r, cst_r], writes=[cn_r[ci]])
                if which == 0:
                    for h in range(2):
                        pq, pq_r = nb()
                        for c in range(2):
                            bd.op("pe", lambda e, c=c, h=h, pq=pq: e.matmul(
                                pq[:, :], lhsT=wq[:, c, h * 256:h * 256 + 128], rhs=cn[0][:, c, :],
                                start=(c == 0), stop=(c == 1)), reads=[wq_r, cn_r[0]], writes=[pq_r])
                        bd.op("act", lambda e, h=h, pq=pq: e.activation(out=Qn[h][:, sl], in_=pq[:, :], func=AF.Copy),
                              reads=[pq_r], writes=[qk_r])
                        pa, pa_r = nb()
                        pb_, pb_r = nb()
                        for (pp, pp_r, off) in ((pa, pa_r, 128), (pb_, pb_r, 192)):
                            for c in range(2):
                                bd.op("pe", lambda e, c=c, h=h, pp=pp, off=off: e.matmul(
                                    pp[0:64, :], lhsT=wq[:, c, h * 256 + off:h * 256 + off + 64], rhs=cn[0][:, c, :],
                                    start=(c == 0), stop=(c == 1)), reads=[wq_r, cn_r[0]], writes=[pp_r])
                        rope(Qr[h][:, sl], pa[0:64, :], pb_[0:64, :], [pa_r, pb_r], sl)
                else:
                    for h in range(2):
                        pk, pk_r = nb()
                        for c in range(2):
                            bd.op("pe", lambda e, c=c, h=h, pk=pk: e.matmul(
                                pk[:, :], lhsT=wkv[:, c, h * 256:h * 256 + 128], rhs=cn[1][:, c, :],
                                start=(c == 0), stop=(c == 1)), reads=[wkv_r, cn_r[1]], writes=[pk_r])
                        bd.op("act", lambda e, h=h, pk=pk: e.activation(out=Kn[h][:, sl], in_=pk[:, :], func=AF.Copy),
                              reads=[pk_r], writes=[qk_r])
                        pv, pv_r = nb()
                        for j in range(4):
                            for c in range(2):
                                bd.op("pe", lambda e, c=c, h=h, j=j, pv=pv: e.matmul(
                                    pv[:, j * 128:(j + 1) * 128], lhsT=cn[1][:, c, j * 128:(j + 1) * 128],
                                    rhs=wkv[:, c, h * 256 + 128:h * 256 + 256],
                                    start=(c == 0 and j == 0), stop=(c == 1)), reads=[wkv_r, cn_r[1]], writes=[pv_r])
                        bd.op("dve", lambda e, h=h, pv=pv: e.tensor_copy(
                            out=V[h][:, tb * 4:(tb + 1) * 4, 0:128],
                            in_=pv[:, :].rearrange("p (j d) -> p j d", j=4)), reads=[pv_r], writes=[qk_r])
            bd.dma("sp", kx[:], pT[512:640, sl].rearrange("(k p) t -> p k t", p=64), writes=[kx_r])
            rope(Kr[:, sl], kx[:, 0, :], kx[:, 1, :], [kx_r], sl)
        bd.barrier()
    with ExitStack() as es3:
        sb3 = lambda nm, shape, dt: es3.enter_context(nc.sbuf_tensor(_u(nm), shape, dt))
        of = sb3("m_of", [128, 132], F32)
        of_r = Res()
        onb = sb3("m_onb", [128, 128], BF16)
        onb_r = Res()
        st_ = sb3("m_stat", [128, 4], F32)
        st_r = Res()
        junk = sb3("m_junk", [128, 128], F32)
        junk_r = Res()
        yst = [sb3(f"m_yst{i}", [128, 512], BF16) for i in range(2)]
        yst_r = [Res(), Res()]
        scale = (128 + 64) ** -0.5

        def wfun(h, kt, qb, st, st_r2, p, p_r, c0):
            bd.op("act", lambda e: e.activation(out=p[:, c0:512], in_=st[:, c0:512], func=AF.Exp, scale=scale),
                  reads=[st_r2], writes=[p_r])

        onbs = [sb3(f"m_onb{i}", [128, 128], BF16) for i in range(4)]
        onbs_r = [Res() for _ in range(4)]
        oi = [0]

        def out_fn(h, qt, o, o_r):
            ob, ob_r = onbs[oi[0] % 4], onbs_r[oi[0] % 4]
            oi[0] += 1
            bd.op("dve", lambda e: e.reciprocal(out=st_[:, 0:1], in_=o[:, 128:129]), reads=[o_r], writes=[st_r])
            bd.op("dve", lambda e: e.tensor_scalar(out=of[:, 0:128], in0=o[:, 0:128], scalar1=st_[:, 0:1],
                                                   scalar2=None, op0=ALU.mult), reads=[o_r, st_r], writes=[of_r])
            bd.op("dve", lambda e: e.scalar_tensor_tensor(out=junk[:], in0=of[:, 0:128], scalar=1.0, in1=of[:, 0:128],
                                                          op0=ALU.mult, op1=ALU.mult, accum_out=st_[:, 1:2]),
                  reads=[of_r], writes=[junk_r, st_r])
            bd.op("act", lambda e: e.activation(out=st_[:, 2:3], in_=st_[:, 1:2], func=AF.Sqrt, scale=1.0 / 128,
                                                bias=cst[:, C_EPS:C_EPS + 1]), reads=[st_r, cst_r], writes=[st_r])
            bd.op("dve", lambda e: e.reciprocal(out=st_[:, 3:4], in_=st_[:, 2:3]), reads=[st_r], writes=[st_r])
            bd.op("dve", lambda e: e.tensor_scalar(out=ob[:], in0=of[:, 0:128], scalar1=st_[:, 3:4], scalar2=None,
                                                   op0=ALU.mult), reads=[of_r, st_r], writes=[ob_r])

            def fin():
                bd.op("pe", lambda e: e.transpose(out=ptb[:, 0:128], in_=ob[:], identity=K["identb"][:]),
                      reads=[ob_r, K["identb_r"]], writes=[ptb_r])
                ys, ys_r = yst[(qt // 4) % 2], yst_r[(qt // 4) % 2]
                j = qt % 4
                bd.op("act", lambda e: e.activation(out=ys[:, j * 128:(j + 1) * 128], in_=ptb[:, 0:128], func=AF.Copy,
                                                    scale=cst[:, C_MLAO0 + h:C_MLAO0 + h + 1]),
                      reads=[ptb_r, cst_r], writes=[ys_r])
                if j == 3:
                    qb = qt // 4
                    bd.dma("sp", yT[h * 128:(h + 1) * 128, qb * 512:(qb + 1) * 512], ys[:], reads=[ys_r])
            return fin

        PS = dict(st=[(banks[0], banks_r[0]), (banks[1], banks_r[1]), (banks[6], banks_r[6])],
                  o=[(banks[2 + j], banks_r[2 + j]) for j in range(4)],
                  mask=K["mask"], mask_r=K["mask_r"],
                  kparts=lambda h, kt: [(Kn[h][:, kt * 128:(kt + 1) * 128], qk_r), (Kr[:, kt * 128:(kt + 1) * 128], qk_r)],
                  qparts=lambda h, qb: [(Qn[h][:, qb * 512:(qb + 1) * 512], qk_r), (Qr[h][:, qb * 512:(qb + 1) * 512], qk_r)],
                  v=lambda h, kt: (V[h][:, kt, :], qk_r))
        emit_attention(bd, nc, es3, "mla", [0, 1], scale, 128, wfun, out_fn, PS)
        bd.barrier()


def emit_mlstm(bd, nc, es, pT_d, yT_d, cst, cst_r, K):
    sb = lambda nm, shape, dt: es.enter_context(nc.sbuf_tensor(_u(nm), shape, dt))
    pT = pT_d.ap()
    yT = yT_d.ap()
    banks, banks_r, ptb, ptb_r = K["banks"], K["banks_r"], K["ptb"], K["ptb_r"]
    misc, misc_r = banks[6], banks_r[6]
    R_Q, R_K, R_V, R_O, R_G = 1152, 1216, 1280, 1408, 1536
    Qb = sb("l_Qb", [64, SEQ], BF16)
    Kb = sb("l_Kb", [64, SEQ], BF16)
    Vm = sb("l_Vm", [128, 32, 2, 65], BF16)
    Ym = sb("l_Ym", [128, SEQ], BF16)
    uT = sb("l_uT", [128, 2, 32], F32)
    emT = sb("l_emT", [128, 2, 32], F32)
    nPb = [sb(f"l_nPb{h}", [128, SEQ], F32) for h in range(2)]
    prep_r = Res()
    ym_r = Res()
    bd.op("pool", lambda e: e.memset(Vm[:, :, :, 64:65], 1.0), writes=[prep_r])
    with ExitStack() as es2:
        sb2 = lambda nm, shape, dt: es2.enter_context(nc.sbuf_tensor(_u(nm), shape, dt))
        xin = sb2("l_xin", [64, SEQ], F32)
        A = sb2("l_A", [64, SEQ], F32)
        B = sb2("l_B", [64, SEQ], F32)
        xin_r, A_r, B_r = Res(), Res(), Res()
        for (row0, cw0, cb, dst, scl) in ((R_Q, C_CWQ0, C_CBQ, Qb, 32 ** -0.5), (R_K, C_CWK0, C_CBK, Kb, 1.0)):
            bd.dma("sp", xin[:], pT[row0:row0 + 64, :], writes=[xin_r])
            bd.op("dve", lambda e, cw0=cw0, cb=cb: e.tensor_scalar(
                out=A[:], in0=xin[:], scalar1=cst[0:64, cw0 + 3:cw0 + 4], scalar2=cst[0:64, cb:cb + 1],
                op0=ALU.mult, op1=ALU.add), reads=[xin_r, cst_r], writes=[A_r])
            src, src_r, dstt, dst_r = A, A_r, B, B_r
            for sh in (1, 2, 3):
                bd.op("dve", lambda e, sh=sh, cw0=cw0, src=src, dstt=dstt: e.scalar_tensor_tensor(
                    out=dstt[:, sh:], in0=xin[:, 0:SEQ - sh], scalar=cst[0:64, cw0 + 3 - sh:cw0 + 4 - sh],
                    in1=src[:, sh:], op0=ALU.mult, op1=ALU.add), reads=[xin_r, cst_r, src_r], writes=[dst_r])
                bd.op("dve", lambda e, sh=sh, src=src, dstt=dstt: e.tensor_copy(out=dstt[:, 0:sh], in_=src[:, 0:sh]),
                      reads=[src_r], writes=[dst_r])
                src, src_r, dstt, dst_r = dstt, dst_r, src, src_r
            bd.op("act", lambda e, src=src: e.activation(out=xin[:], in_=src[:], func=AF.Silu),
                  reads=[src_r], writes=[xin_r])
            bd.op("dve", lambda e, dst=dst, scl=scl: e.tensor_scalar(out=dst[:], in0=xin[:], scalar1=scl, scalar2=None,
                                                                     op0=ALU.mult), reads=[xin_r], writes=[prep_r])
        bd.barrier()
    with ExitStack() as es2:
        sb2 = lambda nm, shape, dt: es2.enter_context(nc.sbuf_tensor(_u(nm), shape, dt))
        t0 = sb2("l_t0", [1, SEQ], F32)
        t1 = sb2("l_t1", [1, SEQ], F32)
        t2 = sb2("l_t2", [1, SEQ], F32)
        onesrow = sb2("l_onesrow", [1, SEQ], F32)
        vin = sb2("l_vin", [128, SEQ], F32)
        t0_r, t1_r, t2_r, or_r, vin_r = Res(), Res(), Res(), Res(), Res()
        bd.op("dve", lambda e: e.memset(onesrow[:], 1.0), writes=[or_r])
        identf = K["identf"]
        for h in range(2):
            cib = C_IB if h == 0 else C_IB1
            cfb = C_FB if h == 0 else C_FB1
            bd.dma("sp", t0[:], pT[R_G + h:R_G + h + 1, :], writes=[t0_r])
            bd.dma("sp", t1[:], pT[R_G + 2 + h:R_G + 3 + h, :], writes=[t1_r])
            bd.op("dve", lambda e, cib=cib: e.tensor_scalar(out=t0[:], in0=t0[:], scalar1=cst[0:1, cib:cib + 1],
                                                            scalar2=None, op0=ALU.add), reads=[t0_r, cst_r], writes=[t0_r])
            bd.op("act", lambda e, cfb=cfb: e.activation(out=t1[:], in_=t1[:], func=AF.Sigmoid,
                                                         bias=cst[0:1, cfb:cfb + 1]), reads=[t1_r, cst_r], writes=[t1_r])
            bd.op("act", lambda e: e.activation(out=t1[:], in_=t1[:], func=AF.Ln), reads=[t1_r], writes=[t1_r])
            bd.op("dve", lambda e: e.tensor_tensor_scan(out=t2[:], data0=onesrow[:], data1=t1[:], initial=0.0,
                                                        op0=ALU.mult, op1=ALU.add), reads=[or_r, t1_r], writes=[t2_r])
            bd.op("dve", lambda e: e.tensor_tensor(out=t0[:], in0=t0[:], in1=t2[:], op=ALU.subtract),
                  reads=[t0_r, t2_r], writes=[t0_r])
            bd.op("dve", lambda e: e.tensor_tensor_scan(out=t1[:], data0=onesrow[:], data1=t0[:], initial=0.0,
                                                        op0=ALU.mult, op1=ALU.max), reads=[or_r, t0_r], writes=[t1_r])
            bd.op("dve", lambda e: e.tensor_tensor(out=t2[:], in0=t2[:], in1=t1[:], op=ALU.add),
                  reads=[t2_r, t1_r], writes=[t2_r])
            for jt in range(32):
                bd.op("pe", lambda e, jt=jt: e.transpose(out=misc[:, jt:jt + 1], in_=t0[0:1, jt * 128:(jt + 1) * 128],
                                                         identity=identf[0:1, 0:1]), reads=[t0_r, K["mats_r"]], writes=[misc_r])
                bd.op("pe", lambda e, jt=jt: e.transpose(out=misc[:, 32 + jt:33 + jt], in_=t2[0:1, jt * 128:(jt + 1) * 128],
                                                         identity=identf[0:1, 0:1]), reads=[t2_r, K["mats_r"]], writes=[misc_r])
            bd.op("dve", lambda e, h=h: e.tensor_copy(out=uT[:, h, :], in_=misc[:, 0:32]), reads=[misc_r], writes=[prep_r])
            bd.op("act", lambda e, h=h: e.activation(out=emT[:, h, :], in_=misc[:, 32:64], func=AF.Exp, scale=-1.0),
                  reads=[misc_r], writes=[prep_r])
            for tb in range(SEQ // 512):
                bd.op("pe", lambda e, tb=tb: e.matmul(misc[:, :], lhsT=K["ones"][0:1, :], rhs=t1[0:1, tb * 512:(tb + 1) * 512],
                                                      start=True, stop=True), reads=[t1_r, K["ones_r"]], writes=[misc_r])
                bd.op("act", lambda e, tb=tb, h=h: e.activation(out=nPb[h][:, tb * 512:(tb + 1) * 512], in_=misc[:, :],
                                                                func=AF.Copy, scale=-1.0), reads=[misc_r], writes=[prep_r])
        bd.dma("sp", vin[:], pT[R_V:R_V + 128, :], writes=[vin_r])
        for jt in range(32):
            bd.op("pe", lambda e, jt=jt: e.transpose(out=misc[:, 0:128], in_=vin[:, jt * 128:(jt + 1) * 128],
                                                     identity=identf), reads=[vin_r, K["mats_r"]], writes=[misc_r])
            bd.op("dve", lambda e, jt=jt: e.tensor_copy(out=Vm[:, jt, :, 0:64],
                                                        in_=misc[:, 0:128].rearrange("p (h d) -> p h d", h=2)),
                  reads=[misc_r], writes=[prep_r])
        bd.barrier()
    with ExitStack() as es3:
        sb3 = lambda nm, shape, dt: es3.enter_context(nc.sbuf_tensor(_u(nm), shape, dt))
        Wt = [sb3(f"l_W{i}", [128, 512], F32) for i in range(2)]
        Wt_r = [Res(), Res()]
        of = sb3("l_of", [128, 64], F32)
        of_r = Res()
        onb = sb3("l_onb", [128, 64], BF16)
        onb_r = Res()
        st_ = sb3("l_stat", [128, 6], F32)
        st_r = Res()
        junk = sb3("l_junk", [128, 64], F32)
        junk_r = Res()
        wi = [0]

        def wfun(h, kt, qb, st, st_r2, p, p_r, c0):
            w, w_r = Wt[wi[0] % 2], Wt_r[wi[0] % 2]
            wi[0] += 1
            bd.op("act", lambda e: e.activation(out=w[:, c0:512], in_=nPb[h][:, qb * 512 + c0:(qb + 1) * 512], func=AF.Exp,
                                                bias=uT[:, h, kt:kt + 1]), reads=[prep_r], writes=[w_r])
            bd.op("dve", lambda e: e.tensor_tensor(out=p[:, c0:512], in0=st[:, c0:512], in1=w[:, c0:512], op=ALU.mult),
                  reads=[st_r2, w_r], writes=[p_r])

        onbs = [sb3(f"l_onb{i}", [128, 64], BF16) for i in range(4)]
        onbs_r = [Res() for _ in range(4)]
        oi = [0]

        def out_fn(h, qt, o, o_r):
            ob, ob_r = onbs[oi[0] % 4], onbs_r[oi[0] % 4]
            oi[0] += 1
            bd.op("act", lambda e: e.activation(out=st_[:, 0:1], in_=o[:, 64:65], func=AF.Abs), reads=[o_r], writes=[st_r])
            bd.op("dve", lambda e: e.tensor_tensor(out=st_[:, 1:2], in0=st_[:, 0:1], in1=emT[:, h, qt:qt + 1], op=ALU.max),
                  reads=[st_r, prep_r], writes=[st_r])
            bd.op("dve", lambda e: e.reciprocal(out=st_[:, 2:3], in_=st_[:, 1:2]), reads=[st_r], writes=[st_r])
            bd.op("dve", lambda e: e.tensor_scalar(out=of[:], in0=o[:, 0:64], scalar1=st_[:, 2:3], scalar2=None,
                                                   op0=ALU.mult), reads=[o_r, st_r], writes=[of_r])
            bd.op("dve", lambda e: e.scalar_tensor_tensor(out=junk[:], in0=of[:], scalar=1.0, in1=of[:],
                                                          op0=ALU.mult, op1=ALU.mult, accum_out=st_[:, 3:4]),
                  reads=[of_r], writes=[junk_r, st_r])
            bd.op("act", lambda e: e.activation(out=st_[:, 4:5], in_=st_[:, 3:4], func=AF.Sqrt, scale=1.0 / 64,
                                                bias=cst[:, C_EPS:C_EPS + 1]), reads=[st_r, cst_r], writes=[st_r])
            bd.op("dve", lambda e: e.reciprocal(out=st_[:, 5:6], in_=st_[:, 4:5]), reads=[st_r], writes=[st_r])
            bd.op("dve", lambda e: e.tensor_scalar(out=ob[:], in0=of[:], scalar1=st_[:, 5:6], scalar2=None, op0=ALU.mult),
                  reads=[of_r, st_r], writes=[ob_r])

            def fin():
                bd.op("pe", lambda e: e.transpose(out=ptb[h * 64:(h + 1) * 64, 0:128], in_=ob[:], identity=K["identb"][:]),
                      reads=[ob_r, K["identb_r"]], writes=[ptb_r])
                bd.op("act", lambda e: e.activation(out=Ym[h * 64:(h + 1) * 64, qt * 128:(qt + 1) * 128],
                                                    in_=ptb[h * 64:(h + 1) * 64, 0:128], func=AF.Copy,
                                                    scale=cst[h * 64:(h + 1) * 64, C_MLO:C_MLO + 1]),
                      reads=[ptb_r, cst_r], writes=[ym_r])
            return fin

        PS = dict(st=[(banks[0], banks_r[0]), (banks[1], banks_r[1]), (banks[6], banks_r[6])],
                  o=[(banks[2 + j], banks_r[2 + j]) for j in range(4)],
                  mask=K["mask"], mask_r=K["mask_r"],
                  kparts=lambda h, kt: [(Kb[h * 32:(h + 1) * 32, kt * 128:(kt + 1) * 128], prep_r)],
                  qparts=lambda h, qb: [(Qb[h * 32:(h + 1) * 32, qb * 512:(qb + 1) * 512], prep_r)],
                  v=lambda h, kt: (Vm[:, kt, h, :], prep_r))
        emit_attention(bd, nc, es3, "mls", [0, 1], 1.0, 64, wfun, out_fn, PS)
        og = sb3("l_og", [128, SEQ], F32)
        og_r = Res()
        bd.dma("sp", og[:], pT[R_O:R_O + 128, :], writes=[og_r])
        bd.op("act", lambda e: e.activation(out=og[:], in_=og[:], func=AF.Sigmoid), reads=[og_r], writes=[og_r])
        bd.op("dve", lambda e: e.tensor_tensor(out=Ym[:], in0=Ym[:], in1=og[:], op=ALU.mult),
              reads=[ym_r, og_r], writes=[ym_r])
        bd.dma("sp", yT[384:512, :], Ym[:], reads=[ym_r])
        bd.barrier()


RW_T = 16


def emit_rwkv(bd, nc, es, D_, cst, cst_r, K):
    sb = lambda nm, shape, dt: es.enter_context(nc.sbuf_tensor(_u(nm), shape, dt))
    pT = D_["pT"].ap()
    yT = D_["yT"].ap()
    scr = D_["scr"]
    banks, banks_r = K["banks"], K["banks_r"]
    bones, identf, mats_r = K["bones"], K["identf"], K["mats_r"]
    R_R, R_K, R_V, R_L = 640, 768, 896, 1024
    vS = sb("r_vS", [128, SEQ], F32)
    gS = sb("r_gS", [128, SEQ], F32)
    boS = sb("r_boS", [128, SEQ], F32)
    yS = sb("r_yS", [128, SEQ], F32)
    vS_r, gS_r, boS_r, yS_r = Res(), Res(), Res(), Res()
    wl = sb("r_wl", [128, 3, 128], F32)
    wl_r = Res()
    bd.dma("sp", wl[:], D_["wl"].ap(), writes=[wl_r])
    c2 = sb("r_c2", [128, 2], F32)
    c2_r = Res()
    bd.op("dve", lambda e: e.tensor_scalar(out=c2[:, 0:1], in0=cst[:, C_KA:C_KA + 1], scalar1=-1.0, scalar2=1.0,
                                           op0=ALU.mult, op1=ALU.add), reads=[cst_r], writes=[c2_r])
    scr_r = Res()
    with ExitStack() as es2:
        sb2 = lambda nm, shape, dt: es2.enter_context(nc.sbuf_tensor(_u(nm), shape, dt))
        rS = sb2("r_rS", [128, SEQ], F32)
        kS = sb2("r_kS", [128, SEQ], F32)
        lS = sb2("r_lS", [128, SEQ], F32)
        dd = sb2("r_dd", [128, SEQ], F32)
        rS_r, kS_r, lS_r, dd_r = Res(), Res(), Res(), Res()
        for (row0, t, t_r, mu) in ((R_R, rS, rS_r, C_MU_R), (R_K, kS, kS_r, C_MU_K), (R_V, vS, vS_r, C_MU_V),
                                   (R_L, lS, lS_r, C_MU_L)):
            bd.dma("sp", t[:], pT[row0:row0 + 128, :], writes=[t_r])
            bd.op("dve", lambda e, t=t: e.tensor_tensor(out=dd[:, 1:SEQ], in0=t[:, 0:SEQ - 1], in1=t[:, 1:SEQ],
                                                        op=ALU.subtract), reads=[t_r], writes=[dd_r])
            bd.op("dve", lambda e, t=t: e.tensor_scalar(out=dd[:, 0:1], in0=t[:, 0:1], scalar1=-1.0, scalar2=None,
                                                        op0=ALU.mult), reads=[t_r], writes=[dd_r])
            bd.op("dve", lambda e, t=t, mu=mu: e.scalar_tensor_tensor(out=t[:], in0=dd[:], scalar=cst[:, mu:mu + 1],
                                                                      in1=t[:], op0=ALU.mult, op1=ALU.add),
                  reads=[dd_r, t_r, cst_r], writes=[t_r])
        names = ["th", "sg", "sgm", "wd", "aT", "kkr", "sq", "nrm", "nkk", "bb", "t1", "km", "prod"]
        T_ = {n: sb2("r_" + n, [128, 512], F32) for n in names}
        T_r = {n: Res() for n in names}
        stg = [sb2(f"r_stg{i}", [128, 5, 128], F32) for i in range(2)]
        stg_r = [Res(), Res()]
        bi = [0]

        def nb():
            b = bi[0] % 7
            bi[0] += 1
            return banks[b], banks_r[b]

        def A(fn, reads, writes):
            bd.op("act", fn, reads=reads, writes=writes)

        def V(fn, reads, writes):
            bd.op("dve", fn, reads=reads, writes=writes)

        ti = 0
        for tb in range(SEQ // 512):
            sl = slice(tb * 512, (tb + 1) * 512)
            A(lambda e: e.activation(out=T_["th"][:], in_=lS[:, sl], func=AF.Tanh), [lS_r], [T_r["th"]])
            A(lambda e: e.activation(out=T_["sg"][:], in_=lS[:, sl], func=AF.Sigmoid), [lS_r], [T_r["sg"]])
            pw, pw_r = nb()
            bd.op("pe", lambda e: e.matmul(pw[:, :], lhsT=wl[:, 0, :], rhs=T_["th"][:], start=True, stop=True),
                  reads=[wl_r, T_r["th"]], writes=[pw_r])
            A(lambda e: e.activation(out=T_["sgm"][:], in_=pw[:, :], func=AF.Sigmoid, bias=cst[:, C_W0:C_W0 + 1]),
              [pw_r, cst_r], [T_r["sgm"]])
            A(lambda e: e.activation(out=T_["wd"][:], in_=T_["sgm"][:], func=AF.Exp, scale=-float(np.exp(-0.5))),
              [T_r["sgm"]], [T_r["wd"]])
            pa, pa_r = nb()
            bd.op("pe", lambda e: e.matmul(pa[:, :], lhsT=wl[:, 1, :], rhs=lS[:, sl], start=True, stop=True),
                  reads=[wl_r, lS_r], writes=[pa_r])
            A(lambda e: e.activation(out=T_["aT"][:], in_=pa[:, :], func=AF.Sigmoid, bias=cst[:, C_A0:C_A0 + 1]),
              [pa_r, cst_r], [T_r["aT"]])
            pg, pg_r = nb()
            bd.op("pe", lambda e: e.matmul(pg[:, :], lhsT=wl[:, 2, :], rhs=T_["sg"][:], start=True, stop=True),
                  reads=[wl_r, T_r["sg"]], writes=[pg_r])
            A(lambda e: e.activation(out=gS[:, sl], in_=pg[:, :], func=AF.Copy), [pg_r], [gS_r])
            V(lambda e: e.tensor_scalar(out=T_["kkr"][:], in0=kS[:, sl], scalar1=cst[:, C_KK:C_KK + 1], scalar2=None,
                                        op0=ALU.mult), [kS_r, cst_r], [T_r["kkr"]])
            A(lambda e: e.activation(out=T_["sq"][:], in_=T_["kkr"][:], func=AF.Square), [T_r["kkr"]], [T_r["sq"]])
            pn, pn_r = nb()
            bd.op("pe", lambda e: e.matmul(pn[:, :], lhsT=bones, rhs=T_["sq"][:], start=True, stop=True),
                  reads=[mats_r, T_r["sq"]], writes=[pn_r])
            A(lambda e: e.activation(out=T_["nrm"][:], in_=pn[:, :], func=AF.Sqrt), [pn_r], [T_r["nrm"]])
            V(lambda e: e.tensor_scalar(out=T_["nrm"][:], in0=T_["nrm"][:], scalar1=1e-12, scalar2=None, op0=ALU.max),
              [T_r["nrm"]], [T_r["nrm"]])
            V(lambda e: e.reciprocal(out=T_["nrm"][:], in_=T_["nrm"][:]), [T_r["nrm"]], [T_r["nrm"]])
            V(lambda e: e.scalar_tensor_tensor(out=T_["nkk"][:], in0=T_["kkr"][:], scalar=-1.0, in1=T_["nrm"][:],
                                               op0=ALU.mult, op1=ALU.mult), [T_r["kkr"], T_r["nrm"]], [T_r["nkk"]])
            V(lambda e: e.scalar_tensor_tensor(out=T_["bb"][:], in0=T_["nkk"][:], scalar=-1.0, in1=T_["aT"][:],
                                               op0=ALU.mult, op1=ALU.mult), [T_r["nkk"], T_r["aT"]], [T_r["bb"]])
            V(lambda e: e.tensor_scalar(out=T_["t1"][:], in0=T_["aT"][:], scalar1=cst[:, C_KA:C_KA + 1],
                                        scalar2=c2[:, 0:1], op0=ALU.mult, op1=ALU.add),
              [T_r["aT"], cst_r, c2_r], [T_r["t1"]])
            V(lambda e: e.tensor_tensor(out=T_["km"][:], in0=kS[:, sl], in1=T_["t1"][:], op=ALU.mult),
              [kS_r, T_r["t1"]], [T_r["km"]])
            V(lambda e: e.scalar_tensor_tensor(out=T_["prod"][:], in0=rS[:, sl], scalar=cst[:, C_RK:C_RK + 1],
                                               in1=T_["km"][:], op0=ALU.mult, op1=ALU.mult),
              [rS_r, cst_r, T_r["km"]], [T_r["prod"]])
            pb_, pb_r = nb()
            bd.op("pe", lambda e: e.matmul(pb_[:, :], lhsT=bones, rhs=T_["prod"][:], start=True, stop=True),
                  reads=[mats_r, T_r["prod"]], writes=[pb_r])
            V(lambda e: e.tensor_tensor(out=boS[:, sl], in0=pb_[:, :], in1=vS[:, sl], op=ALU.mult),
              [pb_r, vS_r], [boS_r])
            for j in range(4):
                t0 = tb * 512 + j * 128
                px, px_r = nb()
                py, py_r = nb()
                srcs = [(T_["nkk"][:, j * 128:(j + 1) * 128], T_r["nkk"]), (T_["wd"][:, j * 128:(j + 1) * 128], T_r["wd"]),
                        (T_["bb"][:, j * 128:(j + 1) * 128], T_r["bb"]), (T_["km"][:, j * 128:(j + 1) * 128], T_r["km"]),
                        (rS[:, t0:t0 + 128], rS_r)]
                for q, (ap_, r_) in enumerate(srcs):
                    if q < 4:
                        bd.op("pe", lambda e, q=q, ap_=ap_: e.transpose(out=px[:, q * 128:(q + 1) * 128], in_=ap_,
                                                                       identity=identf), reads=[r_, mats_r], writes=[px_r])
                    else:
                        bd.op("pe", lambda e, ap_=ap_: e.transpose(out=py[:, 0:128], in_=ap_, identity=identf),
                              reads=[r_, mats_r], writes=[py_r])
                sg_, sg_r = stg[ti % 2], stg_r[ti % 2]
                ti += 1
                A(lambda e, sg_=sg_: e.activation(out=sg_[:, 0:4, :], in_=px[:, :].rearrange("p (q c) -> p q c", q=4),
                                                  func=AF.Copy), [px_r], [sg_r])
                V(lambda e, sg_=sg_: e.tensor_copy(out=sg_[:, 4, :], in_=py[:, 0:128]), [py_r], [sg_r])
                for h in range(2):
                    dst = bass.AP(scr, h * SEQ * 320 + t0 * 320, [[320, 128], [64, 5], [1, 64]])
                    bd.dma("sp" if h == 0 else "act", dst, sg_[:, :, h * 64:(h + 1) * 64], reads=[sg_r], writes=[scr_r])
        bd.barrier()
    with ExitStack() as es3:
        sb3 = lambda nm, shape, dt: es3.enter_context(nc.sbuf_tensor(_u(nm), shape, dt))
        T = RW_T
        NB = 3
        BC = [sb3(f"r_BC{i}", [128, T, 5, 64], F32) for i in range(NB)]
        BC_r = [Res() for _ in range(NB)]
        S = sb3("r_S", [128, 64], F32)
        junk = sb3("r_junk", [128, 64], F32)
        sa = sb3("r_sa", [128, 1], F32)
        S_r, junk_r, sa_r = Res(), Res(), Res()
        bd.op("dve", lambda e: e.memset(S[:], 0.0), writes=[S_r])
        nchunk = SEQ // T

        def load(ci):
            b = ci % NB
            for h in range(2):
                src = bass.AP(scr, h * SEQ * 320 + ci * T * 320, [[0, 64], [1, T * 320]])
                bd.dma("sp" if h == 0 else "act", BC[b][h * 64:(h + 1) * 64, :, :, :].rearrange("p t q j -> p (t q j)"),
                       src, reads=[scr_r], writes=[BC_r[b]])

        load(0)
        load(1)
        for ci in range(nchunk):
            if ci + 2 < nchunk:
                load(ci + 2)
            b = ci % NB
            bc, bc_r = BC[b], BC_r[b]
            for tt in range(T):
                t = ci * T + tt
                bd.op("dve", lambda e, bc=bc, tt=tt: e.scalar_tensor_tensor(
                    out=junk[:], in0=S[:], scalar=1.0, in1=bc[:, tt, 0, :], op0=ALU.mult, op1=ALU.mult,
                    accum_out=sa[:, 0:1]), reads=[S_r, bc_r], writes=[junk_r, sa_r])
                bd.op("dve", lambda e, bc=bc, tt=tt: e.tensor_tensor(out=S[:], in0=S[:], in1=bc[:, tt, 1, :], op=ALU.mult),
                      reads=[S_r, bc_r], writes=[S_r])
                bd.op("dve", lambda e, bc=bc, tt=tt: e.scalar_tensor_tensor(
                    out=S[:], in0=bc[:, tt, 2, :], scalar=sa[:, 0:1], in1=S[:], op0=ALU.mult, op1=ALU.add),
                    reads=[S_r, bc_r, sa_r], writes=[S_r])
                bd.op("dve", lambda e, bc=bc, tt=tt, t=t: e.scalar_tensor_tensor(
                    out=S[:], in0=bc[:, tt, 3, :], scalar=vS[:, t:t + 1], in1=S[:], op0=ALU.mult, op1=ALU.add),
                    reads=[S_r, bc_r, vS_r], writes=[S_r])
                bd.op("dve", lambda e, bc=bc, tt=tt, t=t: e.scalar_tensor_tensor(
                    out=junk[:], in0=S[:], scalar=1.0, in1=bc[:, tt, 4, :], op0=ALU.mult, op1=ALU.mult,
                    accum_out=yS[:, t:t + 1]), reads=[S_r, bc_r], writes=[junk_r, yS_r])
        bd.barrier()
    with ExitStack() as es4:
        sb4 = lambda nm, shape, dt: es4.enter_context(nc.sbuf_tensor(_u(nm), shape, dt))
        yc = sb4("r_yc", [128, 512], F32)
        sq = sb4("r_sq2", [128, 512], F32)
        rs = sb4("r_rs", [128, 512], F32)
        yo = [sb4(f"r_yo{i}", [128, 512], BF16) for i in range(2)]
        yc_r, sq_r, rs_r = Res(), Res(), Res()
        yo_r = [Res(), Res()]
        for tb in range(SEQ // 512):
            sl = slice(tb * 512, (tb + 1) * 512)
            pm, pm_r = banks[tb % 2], banks_r[tb % 2]
            pv, pv_r = banks[2 + tb % 2], banks_r[2 + tb % 2]
            bd.op("pe", lambda e: e.matmul(pm[:, :], lhsT=bones, rhs=yS[:, sl], start=True, stop=True),
                  reads=[mats_r, yS_r], writes=[pm_r])
            bd.op("dve", lambda e: e.scalar_tensor_tensor(out=yc[:], in0=pm[:, :], scalar=-1.0 / 64, in1=yS[:, sl],
                                                          op0=ALU.mult, op1=ALU.add), reads=[pm_r, yS_r], writes=[yc_r])
            bd.op("act", lambda e: e.activation(out=sq[:], in_=yc[:], func=AF.Square), reads=[yc_r], writes=[sq_r])
            bd.op("pe", lambda e: e.matmul(pv[:, :], lhsT=bones, rhs=sq[:], start=True, stop=True),
                  reads=[mats_r, sq_r], writes=[pv_r])
            bd.op("act", lambda e: e.activation(out=rs[:], in_=pv[:, :], func=AF.Sqrt, scale=1.0 / 64,
                                                bias=cst[:, C_LNEPS:C_LNEPS + 1]), reads=[pv_r, cst_r], writes=[rs_r])
            bd.op("dve", lambda e: e.reciprocal(out=rs[:], in_=rs[:]), reads=[rs_r], writes=[rs_r])
            bd.op("dve", lambda e: e.tensor_tensor(out=yc[:], in0=yc[:], in1=rs[:], op=ALU.mult),
                  reads=[yc_r, rs_r], writes=[yc_r])
            bd.op("dve", lambda e: e.tensor_scalar(out=yc[:], in0=yc[:], scalar1=cst[:, C_LNW:C_LNW + 1],
                                                   scalar2=cst[:, C_LNB:C_LNB + 1], op0=ALU.mult, op1=ALU.add),
                  reads=[yc_r, cst_r], writes=[yc_r])
            bd.op("dve", lambda e: e.tensor_tensor(out=yc[:], in0=yc[:], in1=boS[:, sl], op=ALU.add),
                  reads=[yc_r, boS_r], writes=[yc_r])
            o, o_r = yo[tb % 2], yo_r[tb % 2]
            bd.op("dve", lambda e, o=o: e.tensor_tensor(out=o[:], in0=yc[:], in1=gS[:, sl], op=ALU.mult),
                  reads=[yc_r, gS_r], writes=[o_r])
            bd.dma("sp", yT[256:384, sl], o[:], reads=[o_r])
        bd.barrier()


RWKV_CHUNKED = True
import os as _os
_DBG_STAGE = int(_os.environ.get('RW_DBG', '0'))


def rwkv_masks():
    s_ = (np.arange(128) % 64)[:, None]
    t_ = (np.arange(512) % 64)[None, :]
    m = np.zeros((128, 5, 512), np.float32)
    m[:, 0, :] = (s_ < t_)
    m[:, 1, :] = (s_ <= t_)
    m[:, 2, :] = (s_ > t_)
    m[:, 3, :] = (s_ == t_)
    m[:, 4, :] = np.broadcast_to(t_ != 0, (128, 512))
    return m


def emit_rwkv_chunked(bd, nc, es, D_, cst, cst_r, K):
    sb = lambda nm, shape, dt: es.enter_context(nc.sbuf_tensor(_u(nm), shape, dt))
    pT = D_["pT"].ap()
    yT = D_["yT"].ap()
    banks, banks_r = K["banks"], K["banks_r"]
    bones, identf, mats_r = K["bones"], K["identf"], K["mats_r"]
    R_R, R_K, R_V, R_L = 640, 768, 896, 1024
    rS = sb("r_rS", [128, SEQ], F32)
    kS = sb("r_kS", [128, SEQ], F32)
    vS = sb("r_vS", [128, SEQ], F32)
    lS = sb("r_lS", [128, SEQ], F32)
    rS_r, kS_r, vS_r, lS_r = Res(), Res(), Res(), Res()
    wl = sb("r_wl", [128, 3, 128], F32)
    wl_r = Res()
    bd.dma("sp", wl[:], D_["wl"].ap(), writes=[wl_r])
    msk = sb("r_msk", [128, 5, 512], F32)
    msk_r = Res()
    bd.dma("act", msk[:], D_["rwm"].ap(), writes=[msk_r])
    c2 = sb("r_c2", [128, 2], F32)
    c2_r = Res()
    bd.op("dve", lambda e: e.tensor_scalar(out=c2[:, 0:1], in0=cst[:, C_KA:C_KA + 1], scalar1=-1.0, scalar2=1.0,
                                           op0=ALU.mult, op1=ALU.add), reads=[cst_r], writes=[c2_r])
    with ExitStack() as es2:
        sb2 = lambda nm, shape, dt: es2.enter_context(nc.sbuf_tensor(_u(nm), shape, dt))
        dd = sb2("r_dd", [128, SEQ], F32)
        dd_r = Res()
        for (row0, t, t_r, mu) in ((R_R, rS, rS_r, C_MU_R), (R_K, kS, kS_r, C_MU_K), (R_V, vS, vS_r, C_MU_V),
                                   (R_L, lS, lS_r, C_MU_L)):
            bd.dma("sp", t[:], pT[row0:row0 + 128, :], writes=[t_r])
            bd.op("dve", lambda e, t=t: e.tensor_tensor(out=dd[:, 1:SEQ], in0=t[:, 0:SEQ - 1], in1=t[:, 1:SEQ],
                                                        op=ALU.subtract), reads=[t_r], writes=[dd_r])
            bd.op("dve", lambda e, t=t: e.tensor_scalar(out=dd[:, 0:1], in0=t[:, 0:1], scalar1=-1.0, scalar2=None,
                                                        op0=ALU.mult), reads=[t_r], writes=[dd_r])
            bd.op("dve", lambda e, t=t, mu=mu: e.scalar_tensor_tensor(out=t[:], in0=dd[:], scalar=cst[:, mu:mu + 1],
                                                                      in1=t[:], op0=ALU.mult, op1=ALU.add),
                  reads=[dd_r, t_r, cst_r], writes=[t_r])
        bd.barrier()
    names = ["th", "sg", "sgm", "aT", "kkr", "sq", "nrm", "nkk", "bb", "t1", "km", "prod", "gB", "boB",
             "lw", "cl", "e1", "e2", "e3", "At", "Bt", "Kt", "Rt", "Bh", "Kh",
             "Mab", "Lab", "Mkb", "Nbr", "Nkr", "T", "TT", "Mk0", "Mk1", "Lk0", "Lk1",
             "VT", "BhT", "KhT", "yB", "yc", "sq2", "rs", "Tb", "TTb"]
    BFN = {"At", "Bt", "Kt", "Rt", "Mab", "Lab", "Mkb", "Nbr", "Nkr", "Mk0", "Mk1", "Lk0", "Lk1",
           "VT", "BhT", "KhT", "Tb", "TTb"}
    T_ = {n: sb("r_" + n, [128, 512], BF16 if n in BFN else F32) for n in names}
    T_r = {n: Res() for n in names}
    UT = sb("r_UT", [128, 4, 128], BF16)
    UT_r = Res()
    xts = sb("r_xts", [128, 128], BF16)
    xts_r = Res()
    ST = sb("r_ST", [128, 64], F32)
    ST_r = Res()
    yo = [sb(f"r_yo{i}", [128, 512], BF16) for i in range(2)]
    yo_r = [Res(), Res()]
    STb = sb("r_STb", [128, 64], BF16)
    STb_r = Res()
    bd.op("dve", lambda e: e.memset(ST[:], 0.0), writes=[ST_r])
    bd.op("dve", lambda e: e.memset(STb[:], 0.0), writes=[STb_r])
    bX, bX_r = banks[3], banks_r[3]
    bU, bU_r = banks[4], banks_r[4]
    bS, bS_r = banks[5], banks_r[5]
    bY, bY_r = banks[6], banks_r[6]
    bi = [0]

    def nb():
        b = bi[0] % 3
        bi[0] += 1
        return banks[b], banks_r[b]

    def A(fn, reads, writes):
        bd.op("act", fn, reads=reads, writes=writes)

    def V(fn, reads, writes):
        bd.op("dve", fn, reads=reads, writes=writes)

    def G(fn, reads, writes):
        bd.op("pool", fn, reads=reads, writes=writes)

    def slot(c, h):
        return ((c // 2) * 2 + h) * 64

    def fam(out_name, lname, rname, mask_k):
        pb_, pb_r = nb()
        for h in range(2):
            H0 = h * 64
            for c in range(8):
                P0 = (c % 2) * 64
                o = slot(c, h)
                bd.op("pe", lambda e, P0=P0, H0=H0, o=o, c=c: e.matmul(
                    pb_[P0:P0 + 64, o:o + 64], lhsT=T_[lname][H0:H0 + 64, c * 64:(c + 1) * 64],
                    rhs=T_[rname][H0:H0 + 64, c * 64:(c + 1) * 64], start=True, stop=True),
                    reads=[T_r[lname], T_r[rname]], writes=[pb_r], rt=H0)
        V(lambda e: e.tensor_tensor(out=T_[out_name][:], in0=pb_[:, :], in1=msk[:, mask_k, :], op=ALU.mult),
          [pb_r, msk_r], [T_r[out_name]])

    def sq16(out_name, lname, rname, add_name=None):
        pb_, pb_r = nb()
        for cpar in range(2):
            P0 = cpar * 64
            grp = [(c, h) for c in range(cpar, 8, 2) for h in range(2)]
            for gi, (c, h) in enumerate(grp):
                o = slot(c, h)
                bd.op("pe", lambda e, P0=P0, o=o: e.matmul(
                    pb_[P0:P0 + 64, o:o + 64], lhsT=T_[lname][P0:P0 + 64, o:o + 64],
                    rhs=T_[rname][P0:P0 + 64, o:o + 64], start=True, stop=True),
                    reads=[T_r[lname], T_r[rname]], writes=[pb_r], rt=P0)
        if add_name is None:
            A(lambda e: e.activation(out=T_[out_name][:], in_=pb_[:, :], func=AF.Copy), [pb_r], [T_r[out_name]])
        else:
            V(lambda e: e.tensor_tensor(out=T_[out_name][:], in0=pb_[:, :], in1=T_[add_name][:], op=ALU.add),
              [pb_r, T_r[add_name]], [T_r[out_name]])

    def tr4(out_name, src_ap_fn, src_res):
        pb_, pb_r = nb()
        for c2_ in range(4):
            bd.op("pe", lambda e, c2_=c2_: e.transpose(out=pb_[:, c2_ * 128:(c2_ + 1) * 128], in_=src_ap_fn(c2_),
                                                       identity=identf), reads=[src_res, mats_r], writes=[pb_r])
        A(lambda e: e.activation(out=T_[out_name][:], in_=pb_[:, :], func=AF.Copy), [pb_r], [T_r[out_name]])

    for tb in range(SEQ // 512):
        sl = slice(tb * 512, (tb + 1) * 512)
        A(lambda e: e.activation(out=T_["th"][:], in_=lS[:, sl], func=AF.Tanh), [lS_r], [T_r["th"]])
        A(lambda e: e.activation(out=T_["sg"][:], in_=lS[:, sl], func=AF.Sigmoid), [lS_r], [T_r["sg"]])
        pw, pw_r = nb()
        bd.op("pe", lambda e: e.matmul(pw[:, :], lhsT=wl[:, 0, :], rhs=T_["th"][:], start=True, stop=True),
              reads=[wl_r, T_r["th"]], writes=[pw_r])
        A(lambda e: e.activation(out=T_["sgm"][:], in_=pw[:, :], func=AF.Sigmoid, bias=cst[:, C_W0:C_W0 + 1]),
          [pw_r, cst_r], [T_r["sgm"]])
        V(lambda e: e.tensor_scalar(out=T_["lw"][:], in0=T_["sgm"][:], scalar1=-float(np.exp(-0.5)), scalar2=None,
                                    op0=ALU.mult), [T_r["sgm"]], [T_r["lw"]])
        pa, pa_r = nb()
        bd.op("pe", lambda e: e.matmul(pa[:, :], lhsT=wl[:, 1, :], rhs=lS[:, sl], start=True, stop=True),
              reads=[wl_r, lS_r], writes=[pa_r])
        A(lambda e: e.activation(out=T_["aT"][:], in_=pa[:, :], func=AF.Sigmoid, bias=cst[:, C_A0:C_A0 + 1]),
          [pa_r, cst_r], [T_r["aT"]])
        pg, pg_r = nb()
        bd.op("pe", lambda e: e.matmul(pg[:, :], lhsT=wl[:, 2, :], rhs=T_["sg"][:], start=True, stop=True),
              reads=[wl_r, T_r["sg"]], writes=[pg_r])
        A(lambda e: e.activation(out=T_["gB"][:], in_=pg[:, :], func=AF.Copy), [pg_r], [T_r["gB"]])
        V(lambda e: e.tensor_scalar(out=T_["kkr"][:], in0=kS[:, sl], scalar1=cst[:, C_KK:C_KK + 1], scalar2=None,
                                    op0=ALU.mult), [kS_r, cst_r], [T_r["kkr"]])
        A(lambda e: e.activation(out=T_["sq"][:], in_=T_["kkr"][:], func=AF.Square), [T_r["kkr"]], [T_r["sq"]])
        pn, pn_r = nb()
        bd.op("pe", lambda e: e.matmul(pn[:, :], lhsT=bones, rhs=T_["sq"][:], start=True, stop=True),
              reads=[mats_r, T_r["sq"]], writes=[pn_r])
        A(lambda e: e.activation(out=T_["nrm"][:], in_=pn[:, :], func=AF.Sqrt), [pn_r], [T_r["nrm"]])
        V(lambda e: e.tensor_scalar(out=T_["nrm"][:], in0=T_["nrm"][:], scalar1=1e-12, scalar2=None, op0=ALU.max),
          [T_r["nrm"]], [T_r["nrm"]])
        V(lambda e: e.reciprocal(out=T_["nrm"][:], in_=T_["nrm"][:]), [T_r["nrm"]], [T_r["nrm"]])
        V(lambda e: e.scalar_tensor_tensor(out=T_["nkk"][:], in0=T_["kkr"][:], scalar=-1.0, in1=T_["nrm"][:],
                                           op0=ALU.mult, op1=ALU.mult), [T_r["kkr"], T_r["nrm"]], [T_r["nkk"]])
        V(lambda e: e.scalar_tensor_tensor(out=T_["bb"][:], in0=T_["nkk"][:], scalar=-1.0, in1=T_["aT"][:],
                                           op0=ALU.mult, op1=ALU.mult), [T_r["nkk"], T_r["aT"]], [T_r["bb"]])
        V(lambda e: e.tensor_scalar(out=T_["t1"][:], in0=T_["aT"][:], scalar1=cst[:, C_KA:C_KA + 1],
                                    scalar2=c2[:, 0:1], op0=ALU.mult, op1=ALU.add),
          [T_r["aT"], cst_r, c2_r], [T_r["t1"]])
        V(lambda e: e.tensor_tensor(out=T_["km"][:], in0=kS[:, sl], in1=T_["t1"][:], op=ALU.mult),
          [kS_r, T_r["t1"]], [T_r["km"]])
        V(lambda e: e.scalar_tensor_tensor(out=T_["prod"][:], in0=rS[:, sl], scalar=cst[:, C_RK:C_RK + 1],
                                           in1=T_["km"][:], op0=ALU.mult, op1=ALU.mult),
          [rS_r, cst_r, T_r["km"]], [T_r["prod"]])
        pb2, pb2_r = nb()
        bd.op("pe", lambda e: e.matmul(pb2[:, :], lhsT=bones, rhs=T_["prod"][:], start=True, stop=True),
              reads=[mats_r, T_r["prod"]], writes=[pb2_r])
        V(lambda e: e.tensor_tensor(out=T_["boB"][:], in0=pb2[:, :], in1=vS[:, sl], op=ALU.mult),
          [pb2_r, vS_r], [T_r["boB"]])
        V(lambda e: e.tensor_tensor_scan(out=T_["cl"][:], data0=msk[:, 4, :], data1=T_["lw"][:], initial=0.0,
                                         op0=ALU.mult, op1=ALU.add), [msk_r, T_r["lw"]], [T_r["cl"]])
        A(lambda e: e.activation(out=T_["e1"][:], in_=T_["cl"][:], func=AF.Exp), [T_r["cl"]], [T_r["e1"]])
        A(lambda e: e.activation(out=T_["e2"][:], in_=T_["cl"][:], func=AF.Exp, scale=-1.0), [T_r["cl"]], [T_r["e2"]])
        V(lambda e: e.tensor_tensor(out=T_["e3"][:], in0=T_["cl"][:], in1=T_["lw"][:], op=ALU.subtract),
          [T_r["cl"], T_r["lw"]], [T_r["e3"]])
        A(lambda e: e.activation(out=T_["e3"][:], in_=T_["e3"][:], func=AF.Exp), [T_r["e3"]], [T_r["e3"]])
        V(lambda e: e.tensor_tensor(out=T_["At"][:], in0=T_["nkk"][:], in1=T_["e3"][:], op=ALU.mult),
          [T_r["nkk"], T_r["e3"]], [T_r["At"]])
        G(lambda e: e.tensor_tensor(out=T_["Bt"][:], in0=T_["bb"][:], in1=T_["e2"][:], op=ALU.mult),
          [T_r["bb"], T_r["e2"]], [T_r["Bt"]])
        V(lambda e: e.tensor_tensor(out=T_["Kt"][:], in0=T_["km"][:], in1=T_["e2"][:], op=ALU.mult),
          [T_r["km"], T_r["e2"]], [T_r["Kt"]])
        G(lambda e: e.tensor_tensor(out=T_["Rt"][:], in0=rS[:, sl], in1=T_["e1"][:], op=ALU.mult),
          [rS_r, T_r["e1"]], [T_r["Rt"]])
        for c in range(8):
            gam = T_["e1"][:, c * 64 + 63:c * 64 + 64]
            V(lambda e, c=c, gam=gam: e.tensor_scalar(out=T_["Bh"][:, c * 64:(c + 1) * 64], in0=T_["Bt"][:, c * 64:(c + 1) * 64],
                                                      scalar1=gam, scalar2=None, op0=ALU.mult),
              [T_r["Bt"], T_r["e1"]], [T_r["Bh"]])
            G(lambda e, c=c, gam=gam: e.tensor_scalar(out=T_["Kh"][:, c * 64:(c + 1) * 64], in0=T_["Kt"][:, c * 64:(c + 1) * 64],
                                                      scalar1=gam, scalar2=None, op0=ALU.mult),
              [T_r["Kt"], T_r["e1"]], [T_r["Kh"]])
        if _DBG_STAGE == 1:
            break
        fam("Mab", "Bt", "At", 0)
        fam("Lab", "At", "Bt", 2)
        fam("Mkb", "Kt", "At", 0)
        fam("Nbr", "Bt", "Rt", 1)
        fam("Nkr", "Kt", "Rt", 1)
        if _DBG_STAGE == 2:
            break
        V(lambda e: e.tensor_tensor(out=T_["T"][:], in0=T_["Mab"][:], in1=msk[:, 3, :], op=ALU.add),
          [T_r["Mab"], msk_r], [T_r["T"]])
        G(lambda e: e.tensor_tensor(out=T_["TT"][:], in0=T_["Lab"][:], in1=msk[:, 3, :], op=ALU.add),
          [T_r["Lab"], msk_r], [T_r["TT"]])
        A(lambda e: e.activation(out=T_["Tb"][:], in_=T_["T"][:], func=AF.Copy), [T_r["T"]], [T_r["Tb"]])
        G(lambda e: e.tensor_copy(out=T_["TTb"][:], in_=T_["TT"][:]), [T_r["TT"]], [T_r["TTb"]])
        Mp, Lp = "Mab", "Lab"
        for lev in range(1, 6):
            Mn, Ln = f"Mk{lev % 2}", f"Lk{lev % 2}"
            sq16(Mn, Lp, Mp)
            if lev < 5:
                sq16(Ln, Mp, Lp)
            sq16("T", "TTb", Mn, add_name="T")
            if lev < 5:
                sq16("TT", Mn, "TTb", add_name="TT")
                G(lambda e: e.tensor_copy(out=T_["TTb"][:], in_=T_["TT"][:]), [T_r["TT"]], [T_r["TTb"]])
            A(lambda e: e.activation(out=T_["Tb"][:], in_=T_["T"][:], func=AF.Copy), [T_r["T"]], [T_r["Tb"]])
            Mp, Lp = Mn, Ln
        if _DBG_STAGE == 3:
            break
        tr4("VT", lambda c2_: vS[:, tb * 512 + c2_ * 128: tb * 512 + (c2_ + 1) * 128], vS_r)
        tr4("BhT", lambda c2_: T_["Bh"][:, c2_ * 128:(c2_ + 1) * 128], T_r["Bh"])
        tr4("KhT", lambda c2_: T_["Kh"][:, c2_ * 128:(c2_ + 1) * 128], T_r["Kh"])
        if _DBG_STAGE == 4:
            break
        for c in range(8):
            P0 = (c % 2) * 64
            c2_ = c // 2
            cs = slice(c * 64, (c + 1) * 64)
            for h in range(2):
                H0 = h * 64
                o = slot(c, h)
                vt = T_["VT"][P0:P0 + 64, c2_ * 128 + H0:c2_ * 128 + H0 + 64]
                bd.op("pe", lambda e, P0=P0, H0=H0, cs=cs: e.matmul(
                    bX[P0:P0 + 64, H0:H0 + 64], lhsT=T_["At"][H0:H0 + 64, cs], rhs=STb[H0:H0 + 64, :],
                    start=True, stop=False), reads=[T_r["At"], STb_r], writes=[bX_r], rt=H0)
                bd.op("pe", lambda e, P0=P0, H0=H0, o=o, vt=vt: e.matmul(
                    bX[P0:P0 + 64, H0:H0 + 64], lhsT=T_["Mkb"][P0:P0 + 64, o:o + 64], rhs=vt,
                    start=False, stop=True), reads=[T_r["Mkb"], T_r["VT"]], writes=[bX_r], rt=P0)
            A(lambda e, P0=P0: e.activation(out=xts[P0:P0 + 64, :], in_=bX[P0:P0 + 64, 0:128], func=AF.Copy),
              [bX_r], [xts_r])
            for h in range(2):
                H0 = h * 64
                o = slot(c, h)
                bd.op("pe", lambda e, P0=P0, H0=H0, o=o: e.matmul(
                    bU[P0:P0 + 64, H0:H0 + 64], lhsT=T_["Tb"][P0:P0 + 64, o:o + 64], rhs=xts[P0:P0 + 64, H0:H0 + 64],
                    start=True, stop=True), reads=[T_r["Tb"], xts_r], writes=[bU_r], rt=P0)
            V(lambda e, P0=P0, c2_=c2_: e.tensor_copy(out=UT[P0:P0 + 64, c2_, :], in_=bU[P0:P0 + 64, 0:128]),
              [bU_r], [UT_r])
            for h in range(2):
                H0 = h * 64
                o = slot(c, h)
                ut = UT[P0:P0 + 64, c2_, H0:H0 + 64]
                vt = T_["VT"][P0:P0 + 64, c2_ * 128 + H0:c2_ * 128 + H0 + 64]
                bd.op("pe", lambda e, H0=H0, cs=cs: e.matmul(
                    bY[H0:H0 + 64, cs], lhsT=STb[H0:H0 + 64, :], rhs=T_["Rt"][H0:H0 + 64, cs],
                    start=True, stop=False), reads=[STb_r, T_r["Rt"]], writes=[bY_r], rt=H0)
                bd.op("pe", lambda e, H0=H0, cs=cs, P0=P0, o=o, ut=ut: e.matmul(
                    bY[H0:H0 + 64, cs], lhsT=ut, rhs=T_["Nbr"][P0:P0 + 64, o:o + 64],
                    start=False, stop=False), reads=[UT_r, T_r["Nbr"]], writes=[bY_r], rt=P0)
                bd.op("pe", lambda e, H0=H0, cs=cs, P0=P0, o=o, vt=vt: e.matmul(
                    bY[H0:H0 + 64, cs], lhsT=vt, rhs=T_["Nkr"][P0:P0 + 64, o:o + 64],
                    start=False, stop=True), reads=[T_r["VT"], T_r["Nkr"]], writes=[bY_r], rt=P0)
                bd.op("pe", lambda e, H0=H0, P0=P0, c2_=c2_, ut=ut: e.matmul(
                    bS[H0:H0 + 64, 0:64], lhsT=T_["BhT"][P0:P0 + 64, c2_ * 128 + H0:c2_ * 128 + H0 + 64], rhs=ut,
                    start=True, stop=False), reads=[T_r["BhT"], UT_r], writes=[bS_r], rt=P0)
                bd.op("pe", lambda e, H0=H0, P0=P0, c2_=c2_, vt=vt: e.matmul(
                    bS[H0:H0 + 64, 0:64], lhsT=T_["KhT"][P0:P0 + 64, c2_ * 128 + H0:c2_ * 128 + H0 + 64], rhs=vt,
                    start=False, stop=True), reads=[T_r["KhT"], T_r["VT"]], writes=[bS_r], rt=P0)
            gam = T_["e1"][:, c * 64 + 63:c * 64 + 64]
            V(lambda e, gam=gam: e.scalar_tensor_tensor(out=ST[:], in0=ST[:], scalar=gam, in1=bS[:, 0:64],
                                                        op0=ALU.mult, op1=ALU.add),
              [ST_r, bS_r, T_r["e1"]], [ST_r])
            A(lambda e: e.activation(out=STb[:], in_=ST[:], func=AF.Copy), [ST_r], [STb_r])
        if _DBG_STAGE == 5:
            break
        A(lambda e: e.activation(out=T_["yB"][:], in_=bY[:, :], func=AF.Copy), [bY_r], [T_r["yB"]])
        pm, pm_r = nb()
        bd.op("pe", lambda e: e.matmul(pm[:, :], lhsT=bones, rhs=T_["yB"][:], start=True, stop=True),
              reads=[mats_r, T_r["yB"]], writes=[pm_r])
        V(lambda e: e.scalar_tensor_tensor(out=T_["yc"][:], in0=pm[:, :], scalar=-1.0 / 64, in1=T_["yB"][:],
                                           op0=ALU.mult, op1=ALU.add), [pm_r, T_r["yB"]], [T_r["yc"]])
        A(lambda e: e.activation(out=T_["sq2"][:], in_=T_["yc"][:], func=AF.Square), [T_r["yc"]], [T_r["sq2"]])
        pv, pv_r = nb()
        bd.op("pe", lambda e: e.matmul(pv[:, :], lhsT=bones, rhs=T_["sq2"][:], start=True, stop=True),
              reads=[mats_r, T_r["sq2"]], writes=[pv_r])
        A(lambda e: e.activation(out=T_["rs"][:], in_=pv[:, :], func=AF.Sqrt, scale=1.0 / 64,
                                 bias=cst[:, C_LNEPS:C_LNEPS + 1]), [pv_r, cst_r], [T_r["rs"]])
        V(lambda e: e.reciprocal(out=T_["rs"][:], in_=T_["rs"][:]), [T_r["rs"]], [T_r["rs"]])
        V(lambda e: e.tensor_tensor(out=T_["yc"][:], in0=T_["yc"][:], in1=T_["rs"][:], op=ALU.mult),
          [T_r["yc"], T_r["rs"]], [T_r["yc"]])
        V(lambda e: e.tensor_scalar(out=T_["yc"][:], in0=T_["yc"][:], scalar1=cst[:, C_LNW:C_LNW + 1],
                                    scalar2=cst[:, C_LNB:C_LNB + 1], op0=ALU.mult, op1=ALU.add),
          [T_r["yc"], cst_r], [T_r["yc"]])
        V(lambda e: e.tensor_tensor(out=T_["yc"][:], in0=T_["yc"][:], in1=T_["boB"][:], op=ALU.add),
          [T_r["yc"], T_r["boB"]], [T_r["yc"]])
        o_, o_r = yo[tb % 2], yo_r[tb % 2]
        V(lambda e, o_=o_: e.tensor_tensor(out=o_[:], in0=T_["yc"][:], in1=T_["gB"][:], op=ALU.mult),
          [T_r["yc"], T_r["gB"]], [o_r])
        bd.dma("sp", yT[256:384, sl], o_[:], reads=[o_r])
    bd.barrier()


def emit_phaseB(nc, mkbld, es, D_, mixers, tag):
    bd = mkbld(tag + "m")
    sb = lambda nm, shape, dt: es.enter_context(nc.sbuf_tensor(_u(nm), shape, dt))
    ps = lambda nm, shape, dt: es.enter_context(nc.psum_tensor(_u(nm), shape, dt))
    cst = sb("B_cst", [128, NCST], F32)
    cst_r = Res()
    bd.dma("sp", cst[:], D_["cst"].ap(), writes=[cst_r])
    mats = sb("B_mats", [128, 4, 128], F32)
    mats_r = Res()
    bd.dma("sp", mats[:], D_["mats"].ap(), writes=[mats_r])
    identb = sb("B_identb", [128, 128], BF16)
    maskb = sb("B_maskb", [128, 128], BF16)
    identb_r, maskb_r = Res(), Res()
    bd.op("dve", lambda e: e.tensor_copy(out=identb[:], in_=mats[:, 0, :]), reads=[mats_r], writes=[identb_r])
    bd.op("dve", lambda e: e.tensor_copy(out=maskb[:], in_=mats[:, 3, :]), reads=[mats_r], writes=[maskb_r])
    banks = [ps(f"B_bank{i}", [128, 512], F32) for i in range(7)]
    banks_r = [Res() for _ in range(7)]
    ptb = ps("B_ptb", [128, 1024], BF16)
    ptb_r = Res()
    K = dict(banks=banks, banks_r=banks_r, ptb=ptb, ptb_r=ptb_r, identb=identb, identb_r=identb_r,
             mask=maskb, mask_r=maskb_r, ones=mats[:, 1, :], ones_r=mats_r, identf=mats[:, 0, :],
             bones=mats[:, 2, :], mats_r=mats_r)
    if "mla" in mixers:
        with ExitStack() as es1:
            emit_mla(bd, nc, es1, D_["pT"], D_["pos"], D_["wuq"], D_["wukv"], D_["yT"], cst, cst_r, K)
    bd.barrier()
    if "mlstm" in mixers:
        with ExitStack() as es1:
            emit_mlstm(bd, nc, es1, D_["pT"], D_["yT"], cst, cst_r, K)
        bd.barrier()
    if "rwkv" in mixers:
        bd = mkbld(tag + "r")
        with ExitStack() as es1:
            (emit_rwkv_chunked if RWKV_CHUNKED else emit_rwkv)(bd, nc, es1, D_, cst, cst_r, K)
        bd.barrier()


def prep_phaseB_consts(P, l, hc):
    c = np.zeros((128, NCST), np.float32)
    c[:, C_QN0] = P["mla_q_norm"][l][0:128]
    c[:, C_QN1] = P["mla_q_norm"][l][128:256]
    c[:, C_KVN0] = P["mla_kv_norm"][l][0:128]
    c[:, C_KVN1] = P["mla_kv_norm"][l][128:256]
    c[:, C_INVF] = np.tile(INV_FREQ, 4)
    c[:, C_SGN] = np.tile(np.concatenate([np.full(32, -1.0, np.float32), np.full(32, 1.0, np.float32)]), 2)
    for h in range(2):
        hh = 2 * hc + h
        c[:, C_MLAO0 + h] = P["mla_out_norm"][l][hh * 128:(hh + 1) * 128]
    mu = P["rwkv_mu"][l]
    ch = slice(hc * 128, hc * 128 + 128)
    c[:, C_MU_R] = mu[0:256][ch]
    c[:, C_MU_K] = mu[256:512][ch]
    c[:, C_MU_V] = mu[512:768][ch]
    c[:, C_MU_L] = mu[768:896]
    c[:, C_W0] = P["rwkv_w0"][l][ch]
    c[:, C_A0] = P["rwkv_a0"][l][ch]
    c[:, C_KK] = P["rwkv_k_k"][l][ch]
    c[:, C_KA] = P["rwkv_k_a"][l][ch]
    c[:, C_RK] = P["rwkv_r_k"][l][ch]
    c[:, C_LNW] = P["rwkv_ln_w"][l][ch]
    c[:, C_LNB] = P["rwkv_ln_b"][l][ch]
    cw = P["mlstm_conv_w"][l]
    cb = P["mlstm_conv_b"][l]
    qs = slice(hc * 64, hc * 64 + 64)
    ks = slice(128 + hc * 64, 128 + hc * 64 + 64)
    for j in range(4):
        c[0:64, C_CWQ0 + j] = cw[j][qs]
        c[0:64, C_CWK0 + j] = cw[j][ks]
    c[0:64, C_CBQ] = cb[qs]
    c[0:64, C_CBK] = cb[ks]
    c[:, C_IB] = P["mlstm_i_bias"][l][hc * 2]
    c[:, C_FB] = P["mlstm_f_bias"][l][hc * 2]
    c[:, C_IB1] = P["mlstm_i_bias"][l][hc * 2 + 1]
    c[:, C_FB1] = P["mlstm_f_bias"][l][hc * 2 + 1]
    c[:, C_MLO] = P["mlstm_out_norm"][l][ch]
    c[:, C_EPS] = EPS
    c[:, C_LNEPS] = 64e-5
    c[:, C_ONE] = 1.0
    hq = []
    hkv = []
    for h in range(2):
        hh = 2 * hc + h
        wq = P["mla_w_uq"][l][:, hh * 192:(hh + 1) * 192]
        hq += [wq[:, 0:128], wq[:, 128:192], wq[:, 160:192], wq[:, 128:160]]
        wkv = P["mla_w_ukv"][l][:, hh * 256:(hh + 1) * 256]
        hkv += [wkv]
    wuq = np.ascontiguousarray(np.concatenate(hq, axis=1))
    wukv = np.ascontiguousarray(np.concatenate(hkv, axis=1))
    wl = np.zeros((128, 3, 128), np.float32)
    wl[0:32, 0, :] = P["rwkv_w2"][l][:, ch]
    wl[32:64, 1, :] = P["rwkv_a2"][l][:, ch]
    wl[64:128, 2, :] = P["rwkv_g2"][l][:, ch]
    return dict(cst=c, wuq=wuq, wukv=wukv, wl=wl)


INV_FREQ = (10000.0 ** (-np.arange(0, 64, 2, dtype=np.float32) / np.float32(64))).astype(np.float32)


def const_mats():
    m = np.zeros((128, 4, 128), np.float32)
    m[:, 0, :] = np.eye(128, dtype=np.float32)
    m[:, 1, :] = 1.0
    m[0:64, 2, 0:64] = 1.0
    m[64:128, 2, 64:128] = 1.0
    m[:, 3, :] = (np.arange(128)[:, None] <= np.arange(128)[None, :]).astype(np.float32)
    return m


W_OUT_PERM = (list(range(0, 256)) + list(range(512, 640)) + list(range(768, 896)) +
              list(range(256, 512)) + list(range(640, 768)) + list(range(896, 1024)))
TBC = 512


def emit_phaseC(nc, bd, es, D_, final, ntok):
    sb = lambda name, shape, dt: es.enter_context(nc.sbuf_tensor(_u(name), shape, dt))
    ps = lambda name, shape, dt: es.enter_context(nc.psum_tensor(_u(name), shape, dt))
    wo = sb("C_wo", [128, 8, D], BF16)
    wg = sb("C_wg", [128, 8, DFF], BF16)
    wu = sb("C_wu", [128, 8, DFF], BF16)
    wd = sb("C_wd", [128, 22, D], BF16)
    w_r = Res()
    HS = DFF // 2
    gcol = sb("C_gcol", [128, 8], F32)
    gcol_r = Res()
    idf = sb("C_idf", [128, 128], F32)
    ident = sb("C_ident", [128, 128], BF16)
    idf_r, ident_r = Res(), Res()
    bd.dma("sp", gcol[:], D_["g"].ap(), writes=[gcol_r])
    bd.dma("sp", idf[:], D_["ident"].ap(), writes=[idf_r])
    bd.op("dve", lambda e: e.tensor_copy(out=ident[:], in_=idf[:]), reads=[idf_r], writes=[ident_r])
    if final:
        gbc = sb("C_gbc", [128, D], F32)
        gbc_r = Res()
        bd.dma("sp", gbc[:], bass.AP(D_["gf_t"], 0, [[0, 128], [1, D]]), writes=[gbc_r])
    es_stage = ExitStack()
    stage = [es_stage.enter_context(nc.sbuf_tensor(_u(f"C_stage{i}"), [128, HS], F32)) for i in range(3)]
    stage_r = [Res() for _ in range(3)]
    si = [0]
    queues = ("sp", "pool", "act")
    engs = ("dve", "pool", "act")

    def cast(dst_ap, src_ap, ncols, scale_ap=None):
        i = si[0] % 3
        si[0] += 1
        bd.dma(queues[i], stage[i][:, 0:ncols], src_ap, writes=[stage_r[i]])
        eng = engs[i] if scale_ap is None else ("dve" if i != 2 else "act")
        if eng == "act":
            if scale_ap is None:
                bd.op("act", lambda e: e.activation(out=dst_ap, in_=stage[i][:, 0:ncols], func=AF.Copy),
                      reads=[stage_r[i]], writes=[w_r])
            else:
                bd.op("act", lambda e: e.activation(out=dst_ap, in_=stage[i][:, 0:ncols], func=AF.Copy, scale=scale_ap),
                      reads=[stage_r[i], gcol_r], writes=[w_r])
        elif scale_ap is None:
            bd.op(eng, lambda e: e.tensor_copy(out=dst_ap, in_=stage[i][:, 0:ncols]), reads=[stage_r[i]], writes=[w_r])
        else:
            bd.op(eng, lambda e: e.tensor_scalar(out=dst_ap, in0=stage[i][:, 0:ncols], scalar1=scale_ap, scalar2=None,
                                                 op0=ALU.mult), reads=[stage_r[i], gcol_r], writes=[w_r])

    for kc in range(8):
        cast(wo[:, kc, :], D_["wo"].ap()[kc * 128:(kc + 1) * 128, :], D)
    for kc in range(8):
        for hh in range(2):
            cast(wg[:, kc, hh * HS:(hh + 1) * HS], D_["wg"].ap()[kc * 128:(kc + 1) * 128, hh * HS:(hh + 1) * HS], HS,
                 gcol[:, kc:kc + 1])
            cast(wu[:, kc, hh * HS:(hh + 1) * HS], D_["wu"].ap()[kc * 128:(kc + 1) * 128, hh * HS:(hh + 1) * HS], HS,
                 gcol[:, kc:kc + 1])
    for f in range(22):
        cast(wd[:, f, :], D_["wd"].ap()[f * 128:(f + 1) * 128, :], D)
    bd.barrier()
    es_stage.close()

    NJ = TBC // 128
    xs = sb("C_xs", [128, NJ, D], F32)
    xs_r = [Res() for _ in range(NJ)]
    yTb = sb("C_yTb", [128, 8, 128], BF16)
    yTb_r = Res()
    hT = sb("C_hT", [128, 8, TBC], BF16)
    hT_r = Res()
    AT = sb("C_AT", [128, 22, TBC], BF16)
    AT_r = Res()
    sgt = [sb("C_sg0", [128, TBC], BF16)] * 2
    sgt_r = [Res()] * 2
    c_xn = sb("C_xn", [128, D], BF16)
    c_xn_r = Res()
    tmp = dict(ss=sb("C_ss", [128, 4], F32), ss_r=Res(), junk=c_xn, junk_r=c_xn_r,
               xn=c_xn, xn_r=c_xn_r,
               pt=ps("C_pt", [128, D], BF16), pt_r=Res(), eps=sb("C_eps", [128, 1], F32), eps_r=Res())
    bd.op("dve", lambda e: e.memset(tmp["eps"][:], EPS), writes=[tmp["eps_r"]])
    pb = [ps(f"C_pb{i}", [128, 512], F32) for i in range(6)]
    pb_r = [Res() for _ in range(6)]
    x_ap = D_["x"].ap()
    yT_ap = D_["yT"].ap()
    out_ap = D_["out"].ap()
    for blk in range(ntok // TBC):
        t0 = blk * TBC
        for j in range(NJ):
            bd.dma("sp", yTb[:], yT_ap[:, t0 + j * 128:t0 + (j + 1) * 128].rearrange("(k p) t -> p k t", p=128),
                   writes=[yTb_r])
            bd.dma("pool", xs[:, j, :], x_ap[t0 + j * 128:t0 + (j + 1) * 128, :], writes=[xs_r[j]])
            for ch in range(2):
                po, po_r = pb[ch], pb_r[ch]
                for k in range(8):
                    bd.op("pe", lambda e, k=k, j=j, ch=ch, po=po: e.matmul(
                        po[:, :], lhsT=yTb[:, k, :], rhs=wo[:, k, ch * 512:(ch + 1) * 512],
                        start=(k == 0), stop=(k == 7)), reads=[yTb_r, w_r], writes=[po_r])
                bd.op("dve", lambda e, j=j, ch=ch, po=po: e.tensor_tensor(
                    out=xs[:, j, ch * 512:(ch + 1) * 512], in0=xs[:, j, ch * 512:(ch + 1) * 512], in1=po[:, :], op=ALU.add),
                    reads=[po_r, xs_r[j]], writes=[xs_r[j]])
            emit_rmsnorm_T(bd, xs[:, j, :], xs_r[j], hT, hT_r, j, tmp, ident)
        for f in range(22):
            pg, pg_r = pb[2 + (f % 2) * 2], pb_r[2 + (f % 2) * 2]
            pu, pu_r = pb[3 + (f % 2) * 2], pb_r[3 + (f % 2) * 2]
            for (pp, pp_r, ww) in ((pg, pg_r, wg), (pu, pu_r, wu)):
                for k in range(8):
                    bd.op("pe", lambda e, k=k, f=f, pp=pp, ww=ww: e.matmul(
                        pp[:, 0:TBC], lhsT=ww[:, k, f * 128:(f + 1) * 128], rhs=hT[:, k, :],
                        start=(k == 0), stop=(k == 7)), reads=[w_r, hT_r], writes=[pp_r])
            sg, sg_r = sgt[f % 2], sgt_r[f % 2]
            bd.op("act", lambda e, sg=sg, pg=pg: e.activation(out=sg[:], in_=pg[:, 0:TBC], func=AF.Silu),
                  reads=[pg_r], writes=[sg_r])
            bd.op("dve", lambda e, sg=sg, pu=pu, f=f: e.tensor_tensor(out=AT[:, f, :], in0=sg[:], in1=pu[:, 0:TBC],
                                                                      op=ALU.mult),
                  reads=[sg_r, pu_r], writes=[AT_r])
        for j in range(NJ):
            for ch in range(2):
                pd, pd_r = pb[ch], pb_r[ch]
                for f in range(22):
                    bd.op("pe", lambda e, f=f, j=j, ch=ch, pd=pd: e.matmul(
                        pd[:, :], lhsT=AT[:, f, j * 128:(j + 1) * 128], rhs=wd[:, f, ch * 512:(ch + 1) * 512],
                        start=(f == 0), stop=(f == 21)), reads=[AT_r, w_r], writes=[pd_r])
                bd.op("dve", lambda e, j=j, ch=ch, pd=pd: e.tensor_tensor(
                    out=xs[:, j, ch * 512:(ch + 1) * 512], in0=xs[:, j, ch * 512:(ch + 1) * 512], in1=pd[:, :], op=ALU.add),
                    reads=[pd_r, xs_r[j]], writes=[xs_r[j]])
            if final:
                ss, ss_r = tmp["ss"], tmp["ss_r"]
                bd.op("dve", lambda e, j=j: e.scalar_tensor_tensor(
                    out=tmp["junk"][:], in0=xs[:, j, :], scalar=1.0, in1=xs[:, j, :], op0=ALU.mult, op1=ALU.mult,
                    accum_out=ss[:, 0:1]), reads=[xs_r[j]], writes=[tmp["junk_r"], ss_r])
                bd.op("act", lambda e: e.activation(out=ss[:, 1:2], in_=ss[:, 0:1], func=AF.Sqrt, scale=1.0 / D,
                                                    bias=tmp["eps"][:, 0:1]), reads=[ss_r, tmp["eps_r"]], writes=[ss_r])
                bd.op("dve", lambda e: e.reciprocal(out=ss[:, 2:3], in_=ss[:, 1:2]), reads=[ss_r], writes=[ss_r])
                bd.op("dve", lambda e, j=j: e.scalar_tensor_tensor(
                    out=xs[:, j, :], in0=xs[:, j, :], scalar=ss[:, 2:3], in1=gbc[:], op0=ALU.mult, op1=ALU.mult),
                    reads=[xs_r[j], ss_r, gbc_r], writes=[xs_r[j]])
            bd.dma("sp", out_ap[t0 + j * 128:t0 + (j + 1) * 128, :], xs[:, j, :], reads=[xs_r[j]])
    bd.barrier()


def build_fused(phases=None):
    _ROPE_CACHE.clear()
    nc = bass.Bass("TRN2", target_bir_lowering=False)
    dt = nc.dram_tensor
    x_d = dt("x", [SEQ, D], F32, kind="ExternalInput")
    pos_d = dt("pos", [1, SEQ], I32, kind="ExternalInput")
    wA_d = dt("wA", [2, D, NCOLA], F32, kind="ExternalInput")
    gA_d = dt("gA", [2, 128, 8], F32, kind="ExternalInput")
    id_d = dt("ident", [128, 128], F32, kind="ExternalInput")
    mats_d = dt("mats", [128, 4, 128], F32, kind="ExternalInput")
    rwm_d = dt("rwm", [128, 5, 512], F32, kind="ExternalInput")
    wuq_d = dt("wuq", [2, 2, 256, 512], F32, kind="ExternalInput")
    wukv_d = dt("wukv", [2, 2, 256, 512], F32, kind="ExternalInput")
    cst_d = dt("cst", [2, 2, 128, NCST], F32, kind="ExternalInput")
    wl_d = dt("wl", [2, 2, 128, 3, 128], F32, kind="ExternalInput")
    wo_d = dt("wo", [2, D, D], F32, kind="ExternalInput")
    wg_d = dt("wg", [2, D, DFF], F32, kind="ExternalInput")
    wu_d = dt("wu", [2, D, DFF], F32, kind="ExternalInput")
    wd_d = dt("wd", [2, DFF, D], F32, kind="ExternalInput")
    gC_d = dt("gC", [2, 128, 8], F32, kind="ExternalInput")
    gf_d = dt("gf", [1, D], F32, kind="ExternalInput")
    out_d = dt("out", [SEQ, D], F32, kind="ExternalOutput")
    pT_d = dt("pT_scr", [NCOLA, SEQ], F32)
    yT_d = dt("yT_scr", [D, SEQ], BF16)
    scr_d = dt("rw_scr", [2 * SEQ * 320], F32)
    x1_d = dt("x1_scr", [SEQ, D], F32)
    shared = {}
    mkbld = lambda tag: Bld(nc, tag, shared)
    for l in range(2):
        x_in = x_d if l == 0 else x1_d
        x_out = x1_d if l == 0 else out_d
        if phases is None or f"A{l}" in phases:
          with ExitStack() as es:
            bd = mkbld(f"A{l}")
            emit_phaseA(nc, bd, es, x_in, View(wA_d.ap()[l]), View(gA_d.ap()[l]), id_d, pT_d, SEQ)
        for hc in range(2):
            if phases is not None and f"B{l}{hc}" not in phases:
                continue
            D_ = dict(pT=View(pT_d.ap()[hc * HALF_ROWS:(hc + 1) * HALF_ROWS, :]), pos=pos_d,
                      wuq=View(wuq_d.ap()[l, hc]), wukv=View(wukv_d.ap()[l, hc]), cst=View(cst_d.ap()[l, hc]),
                      wl=View(wl_d.ap()[l, hc]), mats=mats_d, yT=View(yT_d.ap()[hc * 512:(hc + 1) * 512, :]),
                      scr=scr_d, rwm=rwm_d)
            with ExitStack() as es:
                emit_phaseB(nc, mkbld, es, D_, ("mla", "mlstm", "rwkv"), f"B{l}{hc}")
        D_ = dict(x=x_in, yT=yT_d, wo=View(wo_d.ap()[l]), wg=View(wg_d.ap()[l]), wu=View(wu_d.ap()[l]),
                  wd=View(wd_d.ap()[l]), g=View(gC_d.ap()[l]), gf_t=gf_d, ident=id_d, out=x_out)
        if phases is None or f"C{l}" in phases:
          with ExitStack() as es:
            bd = mkbld(f"C{l}")
            emit_phaseC(nc, bd, es, D_, l == 1, SEQ)
    return nc


_CACHE = {}
_PHASES = None


def kernel(**inputs):
    P = {k: np.asarray(v) for k, v in inputs.items()}
    x = np.asarray(P["x"], np.float32)
    positions = np.asarray(P["positions"]).astype(np.int32)
    if "nc" not in _CACHE:
        _CACHE["nc"] = build_fused(_PHASES)
    nc = _CACHE["nc"]
    f32 = lambda a: np.ascontiguousarray(np.asarray(a, np.float32))
    wA = f32(np.stack([prep_w_inA(P["w_in"][l]) for l in range(2)]))
    gA = f32(np.stack([_gcol(P["mix_norm"][l]) for l in range(2)]))
    gC = f32(np.stack([_gcol(P["ffn_norm"][l]) for l in range(2)]))
    pb = [[prep_phaseB_consts(P, l, hc) for hc in range(2)] for l in range(2)]
    stk = lambda key: f32(np.stack([np.stack([pb[l][hc][key] for hc in range(2)]) for l in range(2)]))
    common = dict(
        wA=wA, gA=gA, ident=np.eye(128, dtype=np.float32), mats=const_mats(), rwm=rwkv_masks(),
        wuq=stk("wuq"), wukv=stk("wukv"), cst=stk("cst"), wl=stk("wl"),
        wo=f32(np.stack([P["w_out"][l][W_OUT_PERM, :] for l in range(2)])),
        wg=f32(P["w_gate"]), wu=f32(P["w_up"]), wd=f32(P["w_down"]), gC=gC,
        gf=f32(np.asarray(P["final_norm"]).reshape(1, D)))
    in_maps = []
    for c in range(NCORES):
        b = c // 2
        m = dict(common)
        m["x"] = f32(x[b])
        m["pos"] = np.ascontiguousarray(positions[b:b + 1])
        in_maps.append(m)
    res = run_bass_kernel_spmd(nc, in_maps, core_ids=list(range(NCORES)))
    out = np.zeros((4, SEQ, D), np.float32)
    for b in range(4):
        out[b] = res.results[2 * b]["out"]
    return out
```

```python
from contextlib import ExitStack
import numpy as np
import concourse.bass as bass
import concourse.mybir as mybir
from concourse.bass_utils import run_bass_kernel_spmd

F32 = mybir.dt.float32
BF16 = mybir.dt.bfloat16
I32 = mybir.dt.int32
ALU = mybir.AluOpType
AF = mybir.ActivationFunctionType

D = 1024
SEQ = 4096
NTOK = 2048
DFF = 2816
NCORES = 8
EPS = 1e-6

NCH = 13
CH_ROWS = [128] * 12 + [4]
HALF_ROWS = 12 * 128 + 4
CH_OFF = [i * 128 for i in range(13)]
NCOLA = 2 * HALF_ROWS


def half_cols(hc):
    cols = []
    cols += list(range(0, 256))
    cols += list(range(256, 512))
    cols += list(range(512, 576))
    cols += list(range(544, 576)) + list(range(512, 544))
    R0 = 576
    for part in range(3):
        cols += list(range(R0 + part * 256 + hc * 128, R0 + part * 256 + hc * 128 + 128))
    cols += list(range(R0 + 768, R0 + 896))
    M0 = 576 + 896
    cols += list(range(M0 + hc * 64, M0 + hc * 64 + 64))
    cols += list(range(M0 + 128 + hc * 64, M0 + 128 + hc * 64 + 64))
    cols += list(range(M0 + 256 + hc * 128, M0 + 256 + hc * 128 + 128))
    cols += list(range(M0 + 520 + hc * 128, M0 + 520 + hc * 128 + 128))
    cols += list(range(M0 + 512 + hc * 2, M0 + 512 + hc * 2 + 2))
    cols += list(range(M0 + 516 + hc * 2, M0 + 516 + hc * 2 + 2))
    assert len(cols) == HALF_ROWS
    return cols


class Res:
    __slots__ = ("w", "r", "name")

    def __init__(self, name=""):
        self.w = None
        self.r = {}
        self.name = name


class Bld:
    NDMA = 8

    def __init__(self, nc, tag="", shared=None):
        self.nc = nc
        self.E = {"pe": nc.tensor, "dve": nc.vector, "act": nc.scalar, "pool": nc.gpsimd, "sp": nc.sync}
        self.sems = {}
        self.cnt = {}
        self.seen = {e: {} for e in self.E}
        self.touched = set()
        for e in self.E:
            self.sems[e] = nc.alloc_semaphore(name=f"s{tag}_{e}")
            self.cnt[e] = 0
        if shared is not None and "sems" in shared:
            self.sems.update(shared["sems"])
            self.cnt.update(shared["cnt"])
            self.dslot = shared["dslot"]
        else:
            dsems, dcnt = {}, {}
            self.dslot = {}
            for q in ("sp", "act", "pool"):
                for i in range(self.NDMA):
                    k = f"d{q}{i}"
                    dsems[k] = nc.alloc_semaphore(name=f"sdma_{k}")
                    dcnt[k] = 0
                self.dslot[q] = 0
            self.sems.update(dsems)
            self.cnt.update(dcnt)
            if shared is not None:
                shared["sems"] = dsems
                shared["dslot"] = self.dslot
                shared["cnt"] = {}
        self.shared = shared

    def _sync_shared(self):
        if self.shared is not None:
            for k in self.shared["sems"]:
                self.shared["cnt"][k] = self.cnt[k]

    def _wait(self, eng, deps):
        best = {}
        for k, v in deps:
            if v > best.get(k, 0):
                best[k] = v
        for k, v in best.items():
            if self.seen[eng].get(k, 0) >= v:
                continue
            self.E[eng].wait_ge(self.sems[k], v)
            self.seen[eng][k] = v

    def _deps(self, eng, reads, writes):
        deps = []
        for r in reads:
            if r.w is not None:
                if not (eng == "pe" and r.w[0] == "pe"):
                    deps.append(r.w)
        for w in writes:
            if w.w is not None and (w.w[0] != eng or eng != "pe"):
                deps.append(w.w)
            for k, v in w.r.items():
                if k != eng or eng != "pe":
                    deps.append((k, v))
        return deps

    def _mark(self, ev, reads, writes):
        self.touched.update(reads)
        self.touched.update(writes)
        for r in reads:
            if ev[1] > r.r.get(ev[0], 0):
                r.r[ev[0]] = ev[1]
        for w in writes:
            w.w = ev
            w.r = {}

    def op(self, eng, fn, reads=(), writes=(), ser=False, rt=None):
        if eng == "pe":
            last = getattr(self, "last_rt", None)
            if rt != last and self.cnt["pe"] > 0:
                self._wait("pe", [("pe", self.cnt["pe"])])
            self.last_rt = rt
        self._wait(eng, self._deps(eng, reads, writes))
        ins = fn(self.E[eng])
        self.cnt[eng] += 1
        ins.then_inc(self.sems[eng], 1)
        self._mark((eng, self.cnt[eng]), reads, writes)
        if ser:
            self._wait(eng, [(eng, self.cnt[eng])])
        return ins

    def dma(self, q, out, in_, reads=(), writes=()):
        i = self.dslot[q]
        self.dslot[q] = (i + 1) % self.NDMA
        k = f"d{q}{i}"
        deps = self._deps(k, reads, writes)
        deps.append((k, self.cnt[k]))
        self._wait(q, deps)
        ins = self.E[q].dma_start(out=out, in_=in_)
        self.cnt[k] += 16
        ins.then_inc(self.sems[k], 16)
        self._mark((k, self.cnt[k]), reads, writes)
        return ins

    def barrier(self):
        deps = [(k, v) for k, v in self.cnt.items() if v > 0]
        for e in ("sp", "pe", "dve", "act", "pool"):
            self._wait(e, deps)
        for e in ("sp", "pe", "dve", "act", "pool"):
            for k, v in deps:
                assert self.seen[e].get(k, 0) >= v
        for r in self.touched:
            r.w = None
            r.r = {}
        self.touched = set()
        self._sync_shared()

    def wait_all(self, eng, ress):
        deps = []
        for r in ress:
            if r.w is not None:
                deps.append(r.w)
        self._wait(eng, deps)


def dram_ap(t, offset, pattern):
    return bass.AP(t, offset, [list(p) for p in pattern])


def emit_rmsnorm_T(bd, x_tile, x_res, hT, hT_res, j, tmp, ident):
    ss, ss_r = tmp["ss"], tmp["ss_r"]
    junk, junk_r = tmp["junk"], tmp["junk_r"]
    xn, xn_r = tmp["xn"], tmp["xn_r"]
    pt, pt_r = tmp["pt"], tmp["pt_r"]
    bd.op("dve", lambda e: e.scalar_tensor_tensor(out=junk[:], in0=x_tile, scalar=1.0, in1=x_tile,
                                                  op0=ALU.mult, op1=ALU.mult, accum_out=ss[:, 0:1]),
          reads=[x_res], writes=[junk_r, ss_r])
    bd.op("act", lambda e: e.activation(out=ss[:, 1:2], in_=ss[:, 0:1], func=AF.Sqrt, scale=1.0 / D,
                                        bias=tmp["eps"][:, 0:1]), reads=[ss_r, tmp["eps_r"]], writes=[ss_r])
    bd.op("dve", lambda e: e.reciprocal(out=ss[:, 2:3], in_=ss[:, 1:2]), reads=[ss_r], writes=[ss_r])
    bd.op("act", lambda e: e.activation(out=xn[:], in_=x_tile, func=AF.Copy, scale=ss[:, 2:3]),
          reads=[x_res, ss_r], writes=[xn_r])
    for kc in range(8):
        bd.op("pe", lambda e, kc=kc: e.transpose(out=pt[:, kc * 128:(kc + 1) * 128],
                                                 in_=xn[:, kc * 128:(kc + 1) * 128], identity=ident[:]),
              reads=[xn_r], writes=[pt_r])
    bd.op("act", lambda e: e.activation(out=hT[:, :, j * 128:(j + 1) * 128],
                                        in_=pt[:].rearrange("p (k t) -> p k t", k=8), func=AF.Copy),
          reads=[pt_r], writes=[hT_res])


_UC = [0]


def _u(name):
    _UC[0] += 1
    return f"{name}_{_UC[0]}"


class View:
    def __init__(self, ap):
        self._ap = ap

    def ap(self):
        return self._ap


def emit_phaseA(nc, bd, es, x_d, w_d, g_d, id_d, pT_d, ntok):
    sb = lambda name, shape, dt: es.enter_context(nc.sbuf_tensor(_u(name), shape, dt))
    ps = lambda name, shape, dt: es.enter_context(nc.psum_tensor(_u(name), shape, dt))
    wb = sb("A_wb", [128, 8, NCOLA], BF16)
    wb_r = [Res() for _ in range(8)]
    stage = [sb(f"A_stage{i}", [128, NCOLA], F32) for i in range(2)]
    stage_r = [Res(), Res()]
    gcol = sb("A_gcol", [128, 8], F32)
    gcol_r = Res()
    idf = sb("A_idf", [128, 128], F32)
    ident = sb("A_ident", [128, 128], BF16)
    ident_r = Res()
    idf_r = Res()
    xt = [sb(f"A_xt{i}", [128, D], F32) for i in range(2)]
    xt_r = [Res(), Res()]
    hT = [sb(f"A_hT{i}", [128, 8, 512], BF16) for i in range(2)]
    hT_r = [Res(), Res()]
    tmp = dict(ss=sb("A_ss", [128, 4], F32), ss_r=Res(), junk=sb("A_junk", [128, D], BF16), junk_r=Res(),
               xn=sb("A_xn", [128, D], BF16), xn_r=Res(),
               pt=ps("A_pt", [128, D], BF16), pt_r=Res(), eps=sb("A_eps", [128, 1], F32), eps_r=Res())
    bd.op("dve", lambda e: e.memset(tmp["eps"][:], EPS), writes=[tmp["eps_r"]])
    NPB = 4
    pb = [ps(f"A_pb{i}", [128, 512], F32) for i in range(NPB)]
    pb_r = [Res() for _ in range(NPB)]
    ost = [sb(f"A_ost{i}", [128, 512], F32) for i in range(4)]
    ost_r = [Res() for _ in range(4)]

    bd.dma("sp", gcol[:], g_d.ap(), writes=[gcol_r])
    bd.dma("sp", idf[:], id_d.ap(), writes=[idf_r])
    bd.op("dve", lambda e: e.tensor_copy(out=ident[:], in_=idf[:]), reads=[idf_r], writes=[ident_r])
    for kc in range(8):
        s = kc % 2
        bd.dma("pool", stage[s][:], w_d.ap()[kc * 128:(kc + 1) * 128, :], writes=[stage_r[s]])
        bd.op("dve", lambda e, kc=kc, s=s: e.tensor_scalar(out=wb[:, kc, :], in0=stage[s][:],
                                                          scalar1=gcol[:, kc:kc + 1], scalar2=None, op0=ALU.mult),
              reads=[stage_r[s], gcol_r], writes=[wb_r[kc]])
    x_ap = x_d.ap()
    pT_ap = pT_d.ap()
    nblk = ntok // 512
    oi = 0
    for blk in range(nblk):
        hb = blk % 2
        for j in range(4):
            ti = blk * 4 + j
            xb = ti % 2
            bd.dma("sp", xt[xb][:], x_ap[ti * 128:(ti + 1) * 128, :], writes=[xt_r[xb]])
            tmp2 = dict(tmp)
            emit_rmsnorm_T(bd, xt[xb][:], xt_r[xb], hT[hb], hT_r[hb], j, tmp2, ident)
        for half in range(2):
            for c in range(NCH):
                m = CH_ROWS[c]
                col0 = half * HALF_ROWS + CH_OFF[c]
                pbi = oi % NPB
                for kc in range(8):
                    bd.op("pe", lambda e, kc=kc, col0=col0, m=m, pbi=pbi, hb=hb: e.matmul(
                        pb[pbi][0:m, :], lhsT=wb[:, kc, col0:col0 + m], rhs=hT[hb][:, kc, :],
                        start=(kc == 0), stop=(kc == 7)),
                        reads=[wb_r[kc], hT_r[hb]], writes=[pb_r[pbi]])
                osi = oi % 4
                eng = "act" if oi % 2 == 0 else "dve"
                if eng == "act":
                    bd.op("act", lambda e, m=m, pbi=pbi, osi=osi: e.activation(
                        out=ost[osi][0:m, :], in_=pb[pbi][0:m, :], func=AF.Copy),
                        reads=[pb_r[pbi]], writes=[ost_r[osi]])
                else:
                    bd.op("dve", lambda e, m=m, pbi=pbi, osi=osi: e.tensor_copy(
                        out=ost[osi][0:m, :], in_=pb[pbi][0:m, :]),
                        reads=[pb_r[pbi]], writes=[ost_r[osi]])
                bd.dma("sp", pT_ap[col0:col0 + m, blk * 512:(blk + 1) * 512], ost[osi][0:m, :],
                       reads=[ost_r[osi]])
                oi += 1
    bd.barrier()


def _gcol(g):
    return np.ascontiguousarray(np.asarray(g, np.float32).reshape(8, 128).T)


def prep_w_inA(w_in_l):
    cols = half_cols(0) + half_cols(1)
    return np.ascontiguousarray(w_in_l[:, cols])


TWO_PI = 6.283185307179586
(C_QN0, C_QN1, C_KVN0, C_KVN1, C_INVF, C_SGN, C_MLAO0, C_MLAO1,
 C_MU_R, C_MU_K, C_MU_V, C_MU_L, C_W0, C_A0, C_KK, C_KA, C_RK, C_LNW, C_LNB,
 C_CWQ0, C_CWQ1, C_CWQ2, C_CWQ3, C_CBQ, C_CWK0, C_CWK1, C_CWK2, C_CWK3, C_CBK,
 C_IB, C_FB, C_MLO, C_EPS, C_LNEPS, C_ONE, C_ZERO, C_IB1, C_FB1) = range(38)
NCST = 38


def emit_attention(bd, nc, es, name, heads, scale_exp, dv, wfun, out_fn, PS):
    sb = lambda nm, shape, dt: es.enter_context(nc.sbuf_tensor(_u(nm), shape, dt))
    LOOK = 3
    NST = len(PS["st"])
    NPT = LOOK + 2
    pT = [sb(f"{name}_pT{i}", [128, 512], BF16) for i in range(NPT)]
    pT_r = [Res() for _ in range(NPT)]
    mask = PS["mask"]
    mask_r = PS["mask_r"]
    blocks = [(h, qb, kt) for h in heads for qb in range(SEQ // 512) for kt in range(4 * (qb + 1))]
    pending = []

    def stage1(i):
        h, qb, kt = blocks[i]
        st, st_r = PS["st"][i % NST]
        kp = PS["kparts"](h, kt)
        qp = PS["qparts"](h, qb)
        n = len(kp)
        jd = kt - 4 * qb
        c0 = max(jd, 0) * 128
        for a in range(n):
            bd.op("pe", lambda e, a=a: e.matmul(st[:, c0:512], lhsT=kp[a][0], rhs=qp[a][0][:, c0:512],
                                                start=(a == 0), stop=(a == n - 1)),
                  reads=[kp[a][1], qp[a][1]], writes=[st_r])
        p, p_r = pT[i % NPT], pT_r[i % NPT]
        wfun(h, kt, qb, st, st_r, p, p_r, c0)
        if jd >= 0:
            bd.op("pool", lambda e: e.tensor_tensor(
                out=p[:, jd * 128:(jd + 1) * 128], in0=p[:, jd * 128:(jd + 1) * 128], in1=mask[:],
                op=ALU.mult), reads=[p_r, mask_r], writes=[p_r])

    def stage2(i):
        h, qb, kt = blocks[i]
        p, p_r = pT[i % NPT], pT_r[i % NPT]
        v_ap, v_r = PS["v"](h, kt)
        for j in range(4):
            qt = 4 * qb + j
            if qt < kt:
                continue
            o, o_r = PS["o"][j]
            bd.op("pe", lambda e, o=o, j=j, qt=qt: e.matmul(
                o[:, 0:dv + 1], lhsT=p[:, j * 128:(j + 1) * 128], rhs=v_ap,
                start=(kt == 0), stop=(kt == qt)), reads=[p_r, v_r], writes=[o_r])
            if kt == qt:
                th = out_fn(h, qt, o, o_r)
                if th is not None:
                    pending.append([3, th])

    nblk = len(blocks)
    for i in range(nblk + LOOK):
        if i < nblk:
            stage1(i)
        if i - LOOK >= 0:
            stage2(i - LOOK)
        for item in pending:
            item[0] -= 1
        while pending and pending[0][0] <= 0:
            pending.pop(0)[1]()
    while pending:
        pending.pop(0)[1]()


_ROPE_CACHE = {}


def emit_rope_tables(bd, nc, es, pos_d, cst, cst_r, CS, SN, tab_r):
    key = id(nc)
    if key in _ROPE_CACHE:
        scr = _ROPE_CACHE[key]
        bd.dma("sp", CS[:], scr.ap()[0], writes=[tab_r])
        bd.dma("act", SN[:], scr.ap()[1], writes=[tab_r])
        return
    with ExitStack() as es2:
        sb = lambda nm, shape, dt: es2.enter_context(nc.sbuf_tensor(_u(nm), shape, dt))
        ti = sb("rt_i", [64, SEQ], I32)
        ta = sb("rt_a", [64, SEQ], F32)
        tb = sb("rt_b", [64, SEQ], F32)
        ti_r, ta_r, tb_r = Res(), Res(), Res()
        src = bass.AP(pos_d, 0, [[0, 64], [1, SEQ]])
        bd.dma("sp", ti[:], src, writes=[ti_r])
        bd.op("dve", lambda e: e.tensor_copy(out=ta[:], in_=ti[:]), reads=[ti_r], writes=[ta_r])
        bd.op("dve", lambda e: e.tensor_scalar(out=ta[:], in0=ta[:], scalar1=cst[0:64, C_INVF:C_INVF + 1],
                                               scalar2=None, op0=ALU.mult), reads=[ta_r, cst_r], writes=[ta_r])
        for which in (0, 1):
            shift = 0.0 if which == 0 else TWO_PI / 4
            bd.op("dve", lambda e: e.tensor_scalar(out=tb[:], in0=ta[:], scalar1=shift, scalar2=1.0 / TWO_PI,
                                                   op0=ALU.add, op1=ALU.mult), reads=[ta_r], writes=[tb_r])
            bd.op("dve", lambda e: e.tensor_copy(out=ti[:], in_=tb[:]), reads=[tb_r], writes=[ti_r])
            bd.op("dve", lambda e: e.tensor_copy(out=tb[:], in_=ti[:]), reads=[ti_r], writes=[tb_r])
            bd.op("dve", lambda e: e.scalar_tensor_tensor(out=tb[:], in0=tb[:], scalar=-TWO_PI, in1=ta[:],
                                                          op0=ALU.mult, op1=ALU.add),
                  reads=[tb_r, ta_r], writes=[tb_r])
            bd.op("dve", lambda e: e.tensor_scalar(out=tb[:], in0=tb[:], scalar1=shift, scalar2=TWO_PI / 2,
                                                   op0=ALU.add, op1=ALU.min), reads=[tb_r], writes=[tb_r])
            bd.op("dve", lambda e: e.tensor_scalar(out=tb[:], in0=tb[:], scalar1=-TWO_PI / 2, scalar2=None,
                                                   op0=ALU.max), reads=[tb_r], writes=[tb_r])
            if which == 0:
                bd.op("act", lambda e: e.activation(out=SN[:], in_=tb[:], func=AF.Sin,
                                                    scale=cst[0:64, C_SGN:C_SGN + 1]),
                      reads=[tb_r, cst_r], writes=[tab_r])
            else:
                bd.op("act", lambda e: e.activation(out=CS[:], in_=tb[:], func=AF.Sin),
                      reads=[tb_r], writes=[tab_r])
        scr = nc.dram_tensor(_u("rope_scr"), [2, 64, SEQ], F32)
        bd.dma("sp", scr.ap()[0], CS[:], reads=[tab_r])
        bd.dma("act", scr.ap()[1], SN[:], reads=[tab_r])
        _ROPE_CACHE[key] = scr
        bd.barrier()


def emit_mla(bd, nc, es, pT_d, pos_d, wuq_d, wukv_d, yT_d, cst, cst_r, K):
    sb = lambda nm, shape, dt: es.enter_context(nc.sbuf_tensor(_u(nm), shape, dt))
    pT = pT_d.ap()
    yT = yT_d.ap()
    CS = sb("m_CS", [64, SEQ], F32)
    SN = sb("m_SN", [64, SEQ], F32)
    tab_r = Res()
    emit_rope_tables(bd, nc, es, pos_d, cst, cst_r, CS, SN, tab_r)
    wq = sb("m_wq", [128, 2, 512], BF16)
    wkv = sb("m_wkv", [128, 2, 512], BF16)
    wq_r, wkv_r = Res(), Res()
    wst = sb("m_wst", [128, 2, 512], F32)
    wst_r = Res()
    for (wd, wt, wr) in ((wuq_d, wq, wq_r), (wukv_d, wkv, wkv_r)):
        bd.dma("sp", wst[:], wd.ap().rearrange("(k p) n -> p k n", p=128), writes=[wst_r])
        bd.op("dve", lambda e, wt=wt: e.tensor_copy(out=wt[:], in_=wst[:]), reads=[wst_r], writes=[wr])
    Qn = [sb(f"m_Qn{h}", [128, SEQ], BF16) for h in range(2)]
    Qr = [sb(f"m_Qr{h}", [64, SEQ], BF16) for h in range(2)]
    Kn = [sb(f"m_Kn{h}", [128, SEQ], BF16) for h in range(2)]
    Kr = sb("m_Kr", [64, SEQ], BF16)
    V = [sb(f"m_V{h}", [128, 32, 129], BF16) for h in range(2)]
    qk_r = Res()
    for h in range(2):
        bd.op("pool", lambda e, h=h: e.memset(V[h][:, :, 128:129], 1.0), writes=[qk_r])
    banks, banks_r, ptb, ptb_r = K["banks"], K["banks_r"], K["ptb"], K["ptb_r"]
    ones, ones_r = K["ones"], K["ones_r"]
    with ExitStack() as es2:
        sb2 = lambda nm, shape, dt: es2.enter_context(nc.sbuf_tensor(_u(nm), shape, dt))
        cf = [sb2(f"m_cf{i}", [128, 2, 512], F32) for i in range(2)]
        cf_r = [Res(), Res()]
        sq = sb2("m_sq", [128, 2, 512], F32)
        sq_r = Res()
        rs = sb2("m_rs", [128, 512], F32)
        rs_r = Res()
        cn = [sb2(f"m_cn{i}", [128, 2, 512], BF16) for i in range(2)]
        cn_r = [Res(), Res()]
        kx = sb2("m_kx", [64, 2, 512], F32)
        kx_r = Res()
        t1 = sb2("m_t1", [64, 512], F32)
        t2 = sb2("m_t2", [64, 512], F32)
        t1_r, t2_r = Res(), Res()
        bi = 0

        def nb():
            nonlocal bi
            b = bi % 6
            bi += 1
            return banks[b], banks_r[b]

        def rope(dst, x_ap, xs_ap, src_res, sl):
            bd.op("dve", lambda e: e.tensor_tensor(out=t1[:], in0=x_ap, in1=CS[:, sl], op=ALU.mult),
                  reads=src_res + [tab_r], writes=[t1_r])
            bd.op("dve", lambda e: e.tensor_tensor(out=t2[:], in0=xs_ap, in1=SN[:, sl], op=ALU.mult),
                  reads=src_res + [tab_r], writes=[t2_r])
            bd.op("dve", lambda e: e.tensor_tensor(out=dst, in0=t1[:], in1=t2[:], op=ALU.add),
                  reads=[t1_r, t2_r], writes=[qk_r])

        for tb in range(SEQ // 512):
            sl = slice(tb * 512, (tb + 1) * 512)
            for which in (0, 1):
                ci = which
                row0 = which * 256
                bd.dma("sp", cf[ci][:], pT[row0:row0 + 256, sl].rearrange("(k p) t -> p k t", p=128),
                       writes=[cf_r[ci]])
                bd.op("act", lambda e, ci=ci: e.activation(out=sq[:], in_=cf[ci][:], func=AF.Square),
                      reads=[cf_r[ci]], writes=[sq_r])
                pss, pss_r = nb()
                for c in range(2):
                    bd.op("pe", lambda e, c=c, pss=pss: e.matmul(pss[:, :], lhsT=ones[:], rhs=sq[:, c, :],
                                                                 start=(c == 0), stop=(c == 1)),
                          reads=[ones_r, sq_r], writes=[pss_r])
                bd.op("act", lambda e, pss=pss: e.activation(out=rs[:], in_=pss[:, :], func=AF.Sqrt,
                                                             scale=1.0 / 256, bias=cst[:, C_EPS:C_EPS + 1]),
                      reads=[pss_r, cst_r], writes=[rs_r])
                bd.op("dve", lambda e: e.reciprocal(out=rs[:], in_=rs[:]), reads=[rs_r], writes=[rs_r])
                gcol = C_QN0 if which == 0 else C_KVN0
                for c in range(2):
                    bd.op("dve", lambda e, c=c, ci=ci, gcol=gcol: e.scalar_tensor_tensor(
                        out=cn[ci][:, c, :], in0=cf[ci][:, c, :], scalar=cst[:, gcol + c:gcol + c + 1], in1=rs[:],
                        op0=ALU.mult, op1=ALU.mult), reads=[cf_r[ci], rs_r, cst_r], writes=[cn_r[ci]])
                if which == 0:
                    for h in range(2):
                        pq, pq_r = nb()
                        for c in range(2):
                            bd.op("pe", lambda e, c=c, h=h, pq=pq: e.matmul(
                                pq[:, :], lhsT=wq[:, c, h * 256:h * 256 + 128], rhs=cn[0][:, c, :],
                                start=(c == 0), stop=(c == 1)), reads=[wq_r, cn_r[0]], writes=[pq_r])
                        bd.op("act", lambda e, h=h, pq=pq: e.activation(out=Qn[h][:, sl], in_=pq[:, :], func=AF.Copy),
                              reads=[pq_r], writes=[qk_r])
                        pa, pa_r = nb()
                        pb_, pb_r = nb()
                        for (pp, pp_r, off) in ((pa, pa_r, 128), (pb_, pb_r, 192)):
                            for c in range(2):
                                bd.op("pe", lambda e, c=c, h=h, pp=pp, off=off: e.matmul(
                                    pp[0:64, :], lhsT=wq[:, c, h * 256 + off:h * 256 + off + 64], rhs=cn[0][:, c, :],
                                    start=(c == 0), stop=(c == 1)), reads=[wq_r, cn_r[0]], writes=[pp_r])
                        rope(Qr[h][:, sl], pa[0:64, :], pb_[0:64, :], [pa_r, pb_r], sl)
                else:
                    for h in range(2):
                        pk, pk_r = nb()
                        for c in range(2):
                            bd.op("pe", lambda e, c=c, h=h, pk=pk: e.matmul(
                                pk[:, :], lhsT=wkv[:, c, h * 256:h * 256 + 128], rhs=cn[1][:, c, :],
                                start=(c == 0), stop=(c == 1)), reads=[wkv_r, cn_r[1]], writes=[pk_r])
                        bd.op("act", lambda e, h=h, pk=pk: e.activation(out=Kn[h][:, sl], in_=pk[:, :], func=AF.Copy),
                              reads=[pk_r], writes=[qk_r])
                        pv, pv_r = nb()
                        for j in range(4):
                            for c in range(2):
                                bd.op("pe", lambda e, c=c, h=h, j=j, pv=pv: e.matmul(
                                    pv[:, j * 128:(j + 1) * 128], lhsT=cn[1][:, c, j * 128:(j + 1) * 128],
                                    rhs=wkv[:, c, h * 256 + 128:h * 256 + 256],
                                    start=(c == 0), stop=(c == 1)), reads=[wkv_r, cn_r[1]], writes=[pv_r])
                        bd.op("dve", lambda e, h=h, pv=pv: e.tensor_copy(
                            out=V[h][:, tb * 4:(tb + 1) * 4, 0:128],
                            in_=pv[:, :].rearrange("p (j d) -> p j d", j=4)), reads=[pv_r], writes=[qk_r])
            bd.dma("sp", kx[:], pT[512:640, sl].rearrange("(k p) t -> p k t", p=64), writes=[kx_r])
            rope(Kr[:, sl], kx[:, 0, :], kx[:, 1, :], [kx_r], sl)
        bd.barrier()
    with ExitStack() as es3:
        sb3 = lambda nm, shape, dt: es3.enter_context(nc.sbuf_tensor(_u(nm), shape, dt))
        of = sb3("m_of", [128, 132], F32)
        of_r = Res()
        onb = sb3("m_onb", [128, 128], BF16)
        onb_r = Res()
        st_ = sb3("m_stat", [128, 4], F32)
        st_r = Res()
        junk = sb3("m_junk", [128, 128], F32)
        junk_r = Res()
        yst = [sb3(f"m_yst{i}", [128, 512], BF16) for i in range(2)]
        yst_r = [Res(), Res()]
        scale = (128 + 64) ** -0.5

        def wfun(h, kt, qb, st, st_r2, p, p_r, c0):
            bd.op("act", lambda e: e.activation(out=p[:, c0:512], in_=st[:, c0:512], func=AF.Exp, scale=scale),
                  reads=[st_r2], writes=[p_r])

        onbs = [sb3(f"m_onb{i}", [128, 128], BF16) for i in range(4)]
        onbs_r = [Res() for _ in range(4)]
        oi = [0]

        def out_fn(h, qt, o, o_r):
            ob, ob_r = onbs[oi[0] % 4], onbs_r[oi[0] % 4]
            oi[0] += 1
            bd.op("dve", lambda e: e.reciprocal(out=st_[:, 0:1], in_=o[:, 128:129]), reads=[o_r], writes=[st_r])
            bd.op("dve", lambda e: e.tensor_scalar(out=of[:, 0:128], in0=o[:, 0:128], scalar1=st_[:, 0:1],
                                                   scalar2=None, op0=ALU.mult), reads=[o_r, st_r], writes=[of_r])
            bd.op("dve", lambda e: e.scalar_tensor_tensor(out=junk[:], in0=of[:, 0:128], scalar=1.0, in1=of[:, 0:128],
                                                          op0=ALU.mult, op1=ALU.mult, accum_out=st_[:, 1:2]),
                  reads=[of_r], writes=[junk_r, st_r])
            bd.op("act", lambda e: e.activation(out=st_[:, 2:3], in_=st_[:, 1:2], func=AF.Sqrt, scale=1.0 / 128,
                                                bias=cst[:, C_EPS:C_EPS + 1]), reads=[st_r, cst_r], writes=[st_r])
            bd.op("dve", lambda e: e.reciprocal(out=st_[:, 3:4], in_=st_[:, 2:3]), reads=[st_r], writes=[st_r])
            bd.op("dve", lambda e: e.tensor_scalar(out=ob[:], in0=of[:, 0:128], scalar1=st_[:, 3:4], scalar2=None,
                                                   op0=ALU.mult), reads=[of_r, st_r], writes=[ob_r])

            def fin():
                bd.op("pe", lambda e: e.transpose(out=ptb[:, 0:128], in_=ob[:], identity=K["identb"][:]),
                      reads=[ob_r, K["identb_r"]], writes=[ptb_r])
                ys, ys_r = yst[(qt // 4) % 2], yst_r[(qt // 4) % 2]
                j = qt % 4
                bd.op("act", lambda e: e.activation(out=ys[:, j * 128:(j + 1) * 128], in_=ptb[:, 0:128], func=AF.Copy,
                                                    scale=cst[:, C_MLAO0 + h:C_MLAO0 + h + 1]),
                      reads=[ptb_r, cst_r], writes=[ys_r])
                if j == 3:
                    qb = qt // 4
                    bd.dma("sp", yT[h * 128:(h + 1) * 128, qb * 512:(qb + 1) * 512], ys[:], reads=[ys_r])
            return fin

        PS = dict(st=[(banks[0], banks_r[0]), (banks[1], banks_r[1]), (banks[6], banks_r[6])],
                  o=[(banks[2 + j], banks_r[2 + j]) for j in range(4)],
                  mask=K["mask"], mask_r=K["mask_r"],
                  kparts=lambda h, kt: [(Kn[h][:, kt * 128:(kt + 1) * 128], qk_r), (Kr[:, kt * 128:(kt + 1) * 128], qk_r)],
                  qparts=lambda h, qb: [(Qn[h][:, qb * 512:(qb + 1) * 512], qk_r), (Qr[h][:, qb * 512:(qb + 1) * 512], qk_r)],
                  v=lambda h, kt: (V[h][:, kt, :], qk_r))
        emit_attention(bd, nc, es3, "mla", [0, 1], scale, 128, wfun, out_fn, PS)
        bd.barrier()


def emit_mlstm(bd, nc, es, pT_d, yT_d, cst, cst_r, K):
    sb = lambda nm, shape, dt: es.enter_context(nc.sbuf_tensor(_u(nm), shape, dt))
    pT = pT_d.ap()
    yT = yT_d.ap()
    banks, banks_r, ptb, ptb_r = K["banks"], K["banks_r"], K["ptb"], K["ptb_r"]
    misc, misc_r = banks[6], banks_r[6]
    R_Q, R_K, R_V, R_O, R_G = 1152, 1216, 1280, 1408, 1536
    Qb = sb("l_Qb", [64, SEQ], BF16)
    Kb = sb("l_Kb", [64, SEQ], BF16)
    Vm = sb("l_Vm", [128, 32, 2, 65], BF16)
    Ym = sb("l_Ym", [128, SEQ], BF16)
    uT = sb("l_uT", [128, 2, 32], F32)
    emT = sb("l_emT", [128, 2, 32], F32)
    nPb = [sb(f"l_nPb{h}", [128, SEQ], F32) for h in range(2)]
    prep_r = Res()
    ym_r = Res()
    bd.op("pool", lambda e: e.memset(Vm[:, :, :, 64:65], 1.0), writes=[prep_r])
    with ExitStack() as es2:
        sb2 = lambda nm, shape, dt: es2.enter_context(nc.sbuf_tensor(_u(nm), shape, dt))
        xin = sb2("l_xin", [64, SEQ], F32)
        A = sb2("l_A", [64, SEQ], F32)
        B = sb2("l_B", [64, SEQ], F32)
        xin_r, A_r, B_r = Res(), Res(), Res()
        for (row0, cw0, cb, dst, scl) in ((R_Q, C_CWQ0, C_CBQ, Qb, 32 ** -0.5), (R_K, C_CWK0, C_CBK, Kb, 1.0)):
            bd.dma("sp", xin[:], pT[row0:row0 + 64, :], writes=[xin_r])
            bd.op("dve", lambda e, cw0=cw0, cb=cb: e.tensor_scalar(
                out=A[:], in0=xin[:], scalar1=cst[0:64, cw0 + 3:cw0 + 4], scalar2=cst[0:64, cb:cb + 1],
                op0=ALU.mult, op1=ALU.add), reads=[xin_r, cst_r], writes=[A_r])
            src, src_r, dstt, dst_r = A, A_r, B, B_r
            for sh in (1, 2, 3):
                bd.op("dve", lambda e, sh=sh, cw0=cw0, src=src, dstt=dstt: e.scalar_tensor_tensor(
                    out=dstt[:, sh:], in0=xin[:, 0:SEQ - sh], scalar=cst[0:64, cw0 + 3 - sh:cw0 + 4 - sh],
                    in1=src[:, sh:], op0=ALU.mult, op1=ALU.add), reads=[xin_r, cst_r, src_r], writes=[dst_r])
                bd.op("dve", lambda e, sh=sh, src=src, dstt=dstt: e.tensor_copy(out=dstt[:, 0:sh], in_=src[:, 0:sh]),
                      reads=[src_r], writes=[dst_r])
                src, src_r, dstt, dst_r = dstt, dst_r, src, src_r
            bd.op("act", lambda e, src=src: e.activation(out=xin[:], in_=src[:], func=AF.Silu),
                  reads=[src_r], writes=[xin_r])
            bd.op("dve", lambda e, dst=dst, scl=scl: e.tensor_scalar(out=dst[:], in0=xin[:], scalar1=scl, scalar2=None,
                                                                     op0=ALU.mult), reads=[xin_r], writes=[prep_r])
        bd.barrier()
    with ExitStack() as es2:
        sb2 = lambda nm, shape, dt: es2.enter_context(nc.sbuf_tensor(_u(nm), shape, dt))
        t0 = sb2("l_t0", [1, SEQ], F32)
        t1 = sb2("l_t1", [1, SEQ], F32)
        t2 = sb2("l_t2", [1, SEQ], F32)
        onesrow = sb2("l_onesrow", [1, SEQ], F32)
        vin = sb2("l_vin", [128, SEQ], F32)
        t0_r, t1_r, t2_r, or_r, vin_r = Res(), Res(), Res(), Res(), Res()
        bd.op("dve", lambda e: e.memset(onesrow[:], 1.0), writes=[or_r])
        identf = K["identf"]
        for h in range(2):
            cib = C_IB if h == 0 else C_IB1
            cfb = C_FB if h == 0 else C_FB1
            bd.dma("sp", t0[:], pT[R_G + h:R_G + h + 1, :], writes=[t0_r])
            bd.dma("sp", t1[:], pT[R_G + 2 + h:R_G + 3 + h, :], writes=[t1_r])
            bd.op("dve", lambda e, cib=cib: e.tensor_scalar(out=t0[:], in0=t0[:], scalar1=cst[0:1, cib:cib + 1],
                                                            scalar2=None, op0=ALU.add), reads=[t0_r, cst_r], writes=[t0_r])
            bd.op("act", lambda e, cfb=cfb: e.activation(out=t1[:], in_=t1[:], func=AF.Sigmoid,
                                                         bias=cst[0:1, cfb:cfb + 1]), reads=[t1_r, cst_r], writes=[t1_r])
            bd.op("act", lambda e: e.activation(out=t1[:], in_=t1[:], func=AF.Ln), reads=[t1_r], writes=[t1_r])
            bd.op("dve", lambda e: e.tensor_tensor_scan(out=t2[:], data0=onesrow[:], data1=t1[:], initial=0.0,
                                                        op0=ALU.mult, op1=ALU.add), reads=[or_r, t1_r], writes=[t2_r])
            bd.op("dve", lambda e: e.tensor_tensor(out=t0[:], in0=t0[:], in1=t2[:], op=ALU.subtract),
                  reads=[t0_r, t2_r], writes=[t0_r])
            bd.op("dve", lambda e: e.tensor_tensor_scan(out=t1[:], data0=onesrow[:], data1=t0[:], initial=0.0,
                                                        op0=ALU.mult, op1=ALU.max), reads=[or_r, t0_r], writes=[t1_r])
            bd.op("dve", lambda e: e.tensor_tensor(out=t2[:], in0=t2[:], in1=t1[:], op=ALU.add),
                  reads=[t2_r, t1_r], writes=[t2_r])
            for jt in range(32):
                bd.op("pe", lambda e, jt=jt: e.transpose(out=misc[:, jt:jt + 1], in_=t0[0:1, jt * 128:(jt + 1) * 128],
                                                         identity=identf[0:1, 0:1]), reads=[t0_r, K["mats_r"]], writes=[misc_r])
                bd.op("pe", lambda e, jt=jt: e.transpose(out=misc[:, 32 + jt:33 + jt], in_=t2[0:1, jt * 128:(jt + 1) * 128],
                                                         identity=identf[0:1, 0:1]), reads=[t2_r, K["mats_r"]], writes=[misc_r])
            bd.op("dve", lambda e, h=h: e.tensor_copy(out=uT[:, h, :], in_=misc[:, 0:32]), reads=[misc_r], writes=[prep_r])
            bd.op("act", lambda e, h=h: e.activation(out=emT[:, h, :], in_=misc[:, 32:64], func=AF.Exp, scale=-1.0),
                  reads=[misc_r], writes=[prep_r])
            for tb in range(SEQ // 512):
                bd.op("pe", lambda e, tb=tb: e.matmul(misc[:, :], lhsT=K["ones"][0:1, :], rhs=t1[0:1, tb * 512:(tb + 1) * 512],
                                                      start=True, stop=True), reads=[t1_r, K["ones_r"]], writes=[misc_r])
                bd.op("act", lambda e, tb=tb, h=h: e.activation(out=nPb[h][:, tb * 512:(tb + 1) * 512], in_=misc[:, :],
                                                                func=AF.Copy, scale=-1.0), reads=[misc_r], writes=[prep_r])
        bd.dma("sp", vin[:], pT[R_V:R_V + 128, :], writes=[vin_r])
        for jt in range(32):
            bd.op("pe", lambda e, jt=jt: e.transpose(out=misc[:, 0:128], in_=vin[:, jt * 128:(jt + 1) * 128],
                                                     identity=identf), reads=[vin_r, K["mats_r"]], writes=[misc_r])
            bd.op("dve", lambda e, jt=jt: e.tensor_copy(out=Vm[:, jt, :, 0:64],
                                                        in_=misc[:, 0:128].rearrange("p (h d) -> p h d", h=2)),
                  reads=[misc_r], writes=[prep_r])
        bd.barrier()
    with ExitStack() as es3:
        sb3 = lambda nm, shape, dt: es3.enter_context(nc.sbuf_tensor(_u(nm), shape, dt))
        Wt = [sb3(f"l_W{i}", [128, 512], F32) for i in range(2)]
        Wt_r = [Res(), Res()]
        of = sb3("l_of", [128, 64], F32)
        of_r = Res()
        onb = sb3("l_onb", [128, 64], BF16)
        onb_r = Res()
        st_ = sb3("l_stat", [128, 6], F32)
        st_r = Res()
        junk = sb3("l_junk", [128, 64], F32)
        junk_r = Res()
        wi = [0]

        def wfun(h, kt, qb, st, st_r2, p, p_r, c0):
            w, w_r = Wt[wi[0] % 2], Wt_r[wi[0] % 2]
            wi[0] += 1
            bd.op("act", lambda e: e.activation(out=w[:, c0:512], in_=nPb[h][:, qb * 512 + c0:(qb + 1) * 512], func=AF.Exp,
                                                bias=uT[:, h, kt:kt + 1]), reads=[prep_r], writes=[w_r])
            bd.op("dve", lambda e: e.tensor_tensor(out=p[:, c0:512], in0=st[:, c0:512], in1=w[:, c0:512], op=ALU.mult),
                  reads=[st_r2, w_r], writes=[p_r])

        onbs = [sb3(f"l_onb{i}", [128, 64], BF16) for i in range(4)]
        onbs_r = [Res() for _ in range(4)]
        oi = [0]

        def out_fn(h, qt, o, o_r):
            ob, ob_r = onbs[oi[0] % 4], onbs_r[oi[0] % 4]
            oi[0] += 1
            bd.op("act", lambda e: e.activation(out=st_[:, 0:1], in_=o[:, 64:65], func=AF.Abs), reads=[o_r], writes=[st_r])
            bd.op("dve", lambda e: e.tensor_tensor(out=st_[:, 1:2], in0=st_[:, 0:1], in1=emT[:, h, qt:qt + 1], op=ALU.max),
                  reads=[st_r, prep_r], writes=[st_r])
            bd.op("dve", lambda e: e.reciprocal(out=st_[:, 2:3], in_=st_[:, 1:2]), reads=[st_r], writes=[st_r])
            bd.op("dve", lambda e: e.tensor_scalar(out=of[:], in0=o[:, 0:64], scalar1=st_[:, 2:3], scalar2=None,
                                                   op0=ALU.mult), reads=[o_r, st_r], writes=[of_r])
            bd.op("dve", lambda e: e.scalar_tensor_tensor(out=junk[:], in0=of[:], scalar=1.0, in1=of[:],
                                                          op0=ALU.mult, op1=ALU.mult, accum_out=st_[:, 3:4]),
                  reads=[of_r], writes=[junk_r, st_r])
            bd.op("act", lambda e: e.activation(out=st_[:, 4:5], in_=st_[:, 3:4], func=AF.Sqrt, scale=1.0 / 64,
                                                bias=cst[:, C_EPS:C_EPS + 1]), reads=[st_r, cst_r], writes=[st_r])
            bd.op("dve", lambda e: e.reciprocal(out=st_[:, 5:6], in_=st_[:, 4:5]), reads=[st_r], writes=[st_r])
            bd.op("dve", lambda e: e.tensor_scalar(out=ob[:], in0=of[:], scalar1=st_[:, 5:6], scalar2=None, op0=ALU.mult),
                  reads=[of_r, st_r], writes=[ob_r])

            def fin():
                bd.op("pe", lambda e: e.transpose(out=ptb[h * 64:(h + 1) * 64, 0:128], in_=ob[:], identity=K["identb"][:]),
                      reads=[ob_r, K["identb_r"]], writes=[ptb_r])
                bd.op("act", lambda e: e.activation(out=Ym[h * 64:(h + 1) * 64, qt * 128:(qt + 1) * 128],
                                                    in_=ptb[h * 64:(h + 1) * 64, 0:128], func=AF.Copy,
                                                    scale=cst[h * 64:(h + 1) * 64, C_MLO:C_MLO + 1]),
                      reads=[ptb_r, cst_r], writes=[ym_r])
            return fin

        PS = dict(st=[(banks[0], banks_r[0]), (banks[1], banks_r[1]), (banks[6], banks_r[6])],
                  o=[(banks[2 + j], banks_r[2 + j]) for j in range(4)],
                  mask=K["mask"], mask_r=K["mask_r"],
                  kparts=lambda h, kt: [(Kb[h * 32:(h + 1) * 32, kt * 128:(kt + 1) * 128], prep_r)],
                  qparts=lambda h, qb: [(Qb[h * 32:(h + 1) * 32, qb * 512:(qb + 1) * 512], prep_r)],
                  v=lambda h, kt: (Vm[:, kt, h, :], prep_r))
        emit_attention(bd, nc, es3, "mls", [0, 1], 1.0, 64, wfun, out_fn, PS)
        og = sb3("l_og", [128, SEQ], F32)
        og_r = Res()
        bd.dma("sp", og[:], pT[R_O:R_O + 128, :], writes=[og_r])
        bd.op("act", lambda e: e.activation(out=og[:], in_=og[:], func=AF.Sigmoid), reads=[og_r], writes=[og_r])
        bd.op("dve", lambda e: e.tensor_tensor(out=Ym[:], in0=Ym[:], in1=og[:], op=ALU.mult),
              reads=[ym_r, og_r], writes=[ym_r])
        bd.dma("sp", yT[384:512, :], Ym[:], reads=[ym_r])
        bd.barrier()


RW_T = 16


def emit_rwkv(bd, nc, es, D_, cst, cst_r, K):
    sb = lambda nm, shape, dt: es.enter_context(nc.sbuf_tensor(_u(nm), shape, dt))
    pT = D_["pT"].ap()
    yT = D_["yT"].ap()
    scr = D_["scr"]
    banks, banks_r = K["banks"], K["banks_r"]
    bones, identf, mats_r = K["bones"], K["identf"], K["mats_r"]
    R_R, R_K, R_V, R_L = 640, 768, 896, 1024
    vS = sb("r_vS", [128, SEQ], F32)
    gS = sb("r_gS", [128, SEQ], F32)
    boS = sb("r_boS", [128, SEQ], F32)
    yS = sb("r_yS", [128, SEQ], F32)
    vS_r, gS_r, boS_r, yS_r = Res(), Res(), Res(), Res()
    wl = sb("r_wl", [128, 3, 128], F32)
    wl_r = Res()
    bd.dma("sp", wl[:], D_["wl"].ap(), writes=[wl_r])
    c2 = sb("r_c2", [128, 2], F32)
    c2_r = Res()
    bd.op("dve", lambda e: e.tensor_scalar(out=c2[:, 0:1], in0=cst[:, C_KA:C_KA + 1], scalar1=-1.0, scalar2=1.0,
                                           op0=ALU.mult, op1=ALU.add), reads=[cst_r], writes=[c2_r])
    scr_r = Res()
    with ExitStack() as es2:
        sb2 = lambda nm, shape, dt: es2.enter_context(nc.sbuf_tensor(_u(nm), shape, dt))
        rS = sb2("r_rS", [128, SEQ], F32)
        kS = sb2("r_kS", [128, SEQ], F32)
        lS = sb2("r_lS", [128, SEQ], F32)
        dd = sb2("r_dd", [128, SEQ], F32)
        rS_r, kS_r, lS_r, dd_r = Res(), Res(), Res(), Res()
        for (row0, t, t_r, mu) in ((R_R, rS, rS_r, C_MU_R), (R_K, kS, kS_r, C_MU_K), (R_V, vS, vS_r, C_MU_V),
                                   (R_L, lS, lS_r, C_MU_L)):
            bd.dma("sp", t[:], pT[row0:row0 + 128, :], writes=[t_r])
            bd.op("dve", lambda e, t=t: e.tensor_tensor(out=dd[:, 1:SEQ], in0=t[:, 0:SEQ - 1], in1=t[:, 1:SEQ],
                                                        op=ALU.subtract), reads=[t_r], writes=[dd_r])
            bd.op("dve", lambda e, t=t: e.tensor_scalar(out=dd[:, 0:1], in0=t[:, 0:1], scalar1=-1.0, scalar2=None,
                                                        op0=ALU.mult), reads=[t_r], writes=[dd_r])
            bd.op("dve", lambda e, t=t, mu=mu: e.scalar_tensor_tensor(out=t[:], in0=dd[:], scalar=cst[:, mu:mu + 1],
                                                                      in1=t[:], op0=ALU.mult, op1=ALU.add),
                  reads=[dd_r, t_r, cst_r], writes=[t_r])
        names = ["th", "sg", "sgm", "wd", "aT", "kkr", "sq", "nrm", "nkk", "bb", "t1", "km", "prod"]
        T_ = {n: sb2("r_" + n, [128, 512], F32) for n in names}
        T_r = {n: Res() for n in names}
        stg = [sb2(f"r_stg{i}", [128, 5, 128], F32) for i in range(2)]
        stg_r = [Res(), Res()]
        bi = [0]

        def nb():
            b = bi[0] % 7
            bi[0] += 1
            return banks[b], banks_r[b]

        def A(fn, reads, writes):
            bd.op("act", fn, reads=reads, writes=writes)

        def V(fn, reads, writes):
            bd.op("dve", fn, reads=reads, writes=writes)

        ti = 0
        for tb in range(SEQ // 512):
            sl = slice(tb * 512, (tb + 1) * 512)
            A(lambda e: e.activation(out=T_["th"][:], in_=lS[:, sl], func=AF.Tanh), [lS_r], [T_r["th"]])
            A(lambda e: e.activation(out=T_["sg"][:], in_=lS[:, sl], func=AF.Sigmoid), [lS_r], [T_r["sg"]])
            pw, pw_r = nb()
            bd.op("pe", lambda e: e.matmul(pw[:, :], lhsT=wl[:, 0, :], rhs=T_["th"][:], start=True, stop=True),
                  reads=[wl_r, T_r["th"]], writes=[pw_r])
            A(lambda e: e.activation(out=T_["sgm"][:], in_=pw[:, :], func=AF.Sigmoid, bias=cst[:, C_W0:C_W0 + 1]),
              [pw_r, cst_r], [T_r["sgm"]])
            A(lambda e: e.activation(out=T_["wd"][:], in_=T_["sgm"][:], func=AF.Exp, scale=-float(np.exp(-0.5))),
              [T_r["sgm"]], [T_r["wd"]])
            pa, pa_r = nb()
            bd.op("pe", lambda e: e.matmul(pa[:, :], lhsT=wl[:, 1, :], rhs=lS[:, sl], start=True, stop=True),
                  reads=[wl_r, lS_r], writes=[pa_r])
            A(lambda e: e.activation(out=T_["aT"][:], in_=pa[:, :], func=AF.Sigmoid, bias=cst[:, C_A0:C_A0 + 1]),
              [pa_r, cst_r], [T_r["aT"]])
            pg, pg_r = nb()
            bd.op("pe", lambda e: e.matmul(pg[:, :], lhsT=wl[:, 2, :], rhs=T_["sg"][:], start=True, stop=True),
                  reads=[wl_r, T_r["sg"]], writes=[pg_r])
            A(lambda e: e.activation(out=gS[:, sl], in_=pg[:, :], func=AF.Copy), [pg_r], [gS_r])
            V(lambda e: e.tensor_scalar(out=T_["kkr"][:], in0=kS[:, sl], scalar1=cst[:, C_KK:C_KK + 1], scalar2=None,
                                        op0=ALU.mult), [kS_r, cst_r], [T_r["kkr"]])
            A(lambda e: e.activation(out=T_["sq"][:], in_=T_["kkr"][:], func=AF.Square), [T_r["kkr"]], [T_r["sq"]])
            pn, pn_r = nb()
            bd.op("pe", lambda e: e.matmul(pn[:, :], lhsT=bones, rhs=T_["sq"][:], start=True, stop=True),
                  reads=[mats_r, T_r["sq"]], writes=[pn_r])
            A(lambda e: e.activation(out=T_["nrm"][:], in_=pn[:, :], func=AF.Sqrt), [pn_r], [T_r["nrm"]])
            V(lambda e: e.tensor_scalar(out=T_["nrm"][:], in0=T_["nrm"][:], scalar1=1e-12, scalar2=None, op0=ALU.max),
              [T_r["nrm"]], [T_r["nrm"]])
            V(lambda e: e.reciprocal(out=T_["nrm"][:], in_=T_["nrm"][:]), [T_r["nrm"]], [T_r["nrm"]])
            V(lambda e: e.scalar_tensor_tensor(out=T_["nkk"][:], in0=T_["kkr"][:], scalar=-1.0, in1=T_["nrm"][:],
                                               op0=ALU.mult, op1=ALU.mult), [T_r["kkr"], T_r["nrm"]], [T_r["nkk"]])
            V(lambda e: e.scalar_tensor_tensor(out=T_["bb"][:], in0=T_["nkk"][:], scalar=-1.0, in1=T_["aT"][:],
                                               op0=ALU.mult, op1=ALU.mult), [T_r["nkk"], T_r["aT"]], [T_r["bb"]])
            V(lambda e: e.tensor_scalar(out=T_["t1"][:], in0=T_["aT"][:], scalar1=cst[:, C_KA:C_KA + 1],
                                        scalar2=c2[:, 0:1], op0=ALU.mult, op1=ALU.add),
              [T_r["aT"], cst_r, c2_r], [T_r["t1"]])
            V(lambda e: e.tensor_tensor(out=T_["km"][:], in0=kS[:, sl], in1=T_["t1"][:], op=ALU.mult),
              [kS_r, T_r["t1"]], [T_r["km"]])
            V(lambda e: e.scalar_tensor_tensor(out=T_["prod"][:], in0=rS[:, sl], scalar=cst[:, C_RK:C_RK + 1],
                                               in1=T_["km"][:], op0=ALU.mult, op1=ALU.mult),
              [rS_r, cst_r, T_r["km"]], [T_r["prod"]])
            pb_, pb_r = nb()
            bd.op("pe", lambda e: e.matmul(pb_[:, :], lhsT=bones, rhs=T_["prod"][:], start=True, stop=True),
                  reads=[mats_r, T_r["prod"]], writes=[pb_r])
            V(lambda e: e.tensor_tensor(out=boS[:, sl], in0=pb_[:, :], in1=vS[:, sl], op=ALU.mult),
              [pb_r, vS_r], [boS_r])
            for j in range(4):
                t0 = tb * 512 + j * 128
                px, px_r = nb()
                py, py_r = nb()
                srcs = [(T_["nkk"][:, j * 128:(j + 1) * 128], T_r["nkk"]), (T_["wd"][:, j * 128:(j + 1) * 128], T_r["wd"]),
                        (T_["bb"][:, j * 128:(j + 1) * 128], T_r["bb"]), (T_["km"][:, j * 128:(j + 1) * 128], T_r["km"]),
                        (rS[:, t0:t0 + 128], rS_r)]
                for q, (ap_, r_) in enumerate(srcs):
                    if q < 4:
                        bd.op("pe", lambda e, q=q, ap_=ap_: e.transpose(out=px[:, q * 128:(q + 1) * 128], in_=ap_,
                                                                       identity=identf), reads=[r_, mats_r], writes=[px_r])
                    else:
                        bd.op("pe", lambda e, ap_=ap_: e.transpose(out=py[:, 0:128], in_=ap_, identity=identf),
                              reads=[r_, mats_r], writes=[py_r])
                sg_, sg_r = stg[ti % 2], stg_r[ti % 2]
                ti += 1
                A(lambda e, sg_=sg_: e.activation(out=sg_[:, 0:4, :], in_=px[:, :].rearrange("p (q c) -> p q c", q=4),
                                                  func=AF.Copy), [px_r], [sg_r])
                V(lambda e, sg_=sg_: e.tensor_copy(out=sg_[:, 4, :], in_=py[:, 0:128]), [py_r], [sg_r])
                for h in range(2):
                    dst = bass.AP(scr, h * SEQ * 320 + t0 * 320, [[320, 128], [64, 5], [1, 64]])
                    bd.dma("sp" if h == 0 else "act", dst, sg_[:, :, h * 64:(h + 1) * 64], reads=[sg_r], writes=[scr_r])
        bd.barrier()
    with ExitStack() as es3:
        sb3 = lambda nm, shape, dt: es3.enter_context(nc.sbuf_tensor(_u(nm), shape, dt))
        T = RW_T
        NB = 3
        BC = [sb3(f"r_BC{i}", [128, T, 5, 64], F32) for i in range(NB)]
        BC_r = [Res() for _ in range(NB)]
        S = sb3("r_S", [128, 64], F32)
        junk = sb3("r_junk", [128, 64], F32)
        sa = sb3("r_sa", [128, 1], F32)
        S_r, junk_r, sa_r = Res(), Res(), Res()
        bd.op("dve", lambda e: e.memset(S[:], 0.0), writes=[S_r])
        nchunk = SEQ // T

        def load(ci):
            b = ci % NB
            for h in range(2):
                src = bass.AP(scr, h * SEQ * 320 + ci * T * 320, [[0, 64], [1, T * 320]])
                bd.dma("sp" if h == 0 else "act", BC[b][h * 64:(h + 1) * 64, :, :, :].rearrange("p t q j -> p (t q j)"),
                       src, reads=[scr_r], writes=[BC_r[b]])

        load(0)
        load(1)
        for ci in range(nchunk):
            if ci + 2 < nchunk:
                load(ci + 2)
            b = ci % NB
            bc, bc_r = BC[b], BC_r[b]
            for tt in range(T):
                t = ci * T + tt
                bd.op("dve", lambda e, bc=bc, tt=tt: e.scalar_tensor_tensor(
                    out=junk[:], in0=S[:], scalar=1.0, in1=bc[:, tt, 0, :], op0=ALU.mult, op1=ALU.mult,
                    accum_out=sa[:, 0:1]), reads=[S_r, bc_r], writes=[junk_r, sa_r])
                bd.op("dve", lambda e, bc=bc, tt=tt: e.tensor_tensor(out=S[:], in0=S[:], in1=bc[:, tt, 1, :], op=ALU.mult),
                      reads=[S_r, bc_r], writes=[S_r])
                bd.op("dve", lambda e, bc=bc, tt=tt: e.scalar_tensor_tensor(
                    out=S[:], in0=bc[:, tt, 2, :], scalar=sa[:, 0:1], in1=S[:], op0=ALU.mult, op1=ALU.add),
                    reads=[S_r, bc_r, sa_r], writes=[S_r])
                bd.op("dve", lambda e, bc=bc, tt=tt, t=t: e.scalar_tensor_tensor(
                    out=S[:], in0=bc[:, tt, 3, :], scalar=vS[:, t:t + 1], in1=S[:], op0=ALU.mult, op1=ALU.add),
                    reads=[S_r, bc_r, vS_r], writes=[S_r])
                bd.op("dve", lambda e, bc=bc, tt=tt, t=t: e.scalar_tensor_tensor(
                    out=junk[:], in0=S[:], scalar=1.0, in1=bc[:, tt, 4, :], op0=ALU.mult, op1=ALU.mult,
                    accum_out=yS[:, t:t + 1]), reads=[S_r, bc_r], writes=[junk_r, yS_r])
        bd.barrier()
    with ExitStack() as es4:
        sb4 = lambda nm, shape, dt: es4.enter_context(nc.sbuf_tensor(_u(nm), shape, dt))
        yc = sb4("r_yc", [128, 512], F32)
        sq = sb4("r_sq2", [128, 512], F32)
        rs = sb4("r_rs", [128, 512], F32)
        yo = [sb4(f"r_yo{i}", [128, 512], BF16) for i in range(2)]
        yc_r, sq_r, rs_r = Res(), Res(), Res()
        yo_r = [Res(), Res()]
        for tb in range(SEQ // 512):
            sl = slice(tb * 512, (tb + 1) * 512)
            pm, pm_r = banks[tb % 2], banks_r[tb % 2]
            pv, pv_r = banks[2 + tb % 2], banks_r[2 + tb % 2]
            bd.op("pe", lambda e: e.matmul(pm[:, :], lhsT=bones, rhs=yS[:, sl], start=True, stop=True),
                  reads=[mats_r, yS_r], writes=[pm_r])
            bd.op("dve", lambda e: e.scalar_tensor_tensor(out=yc[:], in0=pm[:, :], scalar=-1.0 / 64, in1=yS[:, sl],
                                                          op0=ALU.mult, op1=ALU.add), reads=[pm_r, yS_r], writes=[yc_r])
            bd.op("act", lambda e: e.activation(out=sq[:], in_=yc[:], func=AF.Square), reads=[yc_r], writes=[sq_r])
            bd.op("pe", lambda e: e.matmul(pv[:, :], lhsT=bones, rhs=sq[:], start=True, stop=True),
                  reads=[mats_r, sq_r], writes=[pv_r])
            bd.op("act", lambda e: e.activation(out=rs[:], in_=pv[:, :], func=AF.Sqrt, scale=1.0 / 64,
                                                bias=cst[:, C_LNEPS:C_LNEPS + 1]), reads=[pv_r, cst_r], writes=[rs_r])
            bd.op("dve", lambda e: e.reciprocal(out=rs[:], in_=rs[:]), reads=[rs_r], writes=[rs_r])
            bd.op("dve", lambda e: e.tensor_tensor(out=yc[:], in0=yc[:], in1=rs[:], op=ALU.mult),
                  reads=[yc_r, rs_r], writes=[yc_r])
            bd.op("dve", lambda e: e.tensor_scalar(out=yc[:], in0=yc[:], scalar1=cst[:, C_LNW:C_LNW + 1],
                                                   scalar2=cst[:, C_LNB:C_LNB + 1], op0=ALU.mult, op1=ALU.add),
                  reads=[yc_r, cst_r], writes=[yc_r])
            bd.op("dve", lambda e: e.tensor_tensor(out=yc[:], in0=yc[:], in1=boS[:, sl], op=ALU.add),
                  reads=[yc_r, boS_r], writes=[yc_r])
            o, o_r = yo[tb % 2], yo_r[tb % 2]
            bd.op("dve", lambda e, o=o: e.tensor_tensor(out=o[:], in0=yc[:], in1=gS[:, sl], op=ALU.mult),
                  reads=[yc_r, gS_r], writes=[o_r])
            bd.dma("sp", yT[256:384, sl], o[:], reads=[o_r])
        bd.barrier()


RWKV_CHUNKED = True
import os as _os
_DBG_STAGE = int(_os.environ.get('RW_DBG', '0'))


def rwkv_masks():
    s_ = (np.arange(128) % 64)[:, None]
    t_ = (np.arange(512) % 64)[None, :]
    m = np.zeros((128, 5, 512), np.float32)
    m[:, 0, :] = (s_ < t_)
    m[:, 1, :] = (s_ <= t_)
    m[:, 2, :] = (s_ > t_)
    m[:, 3, :] = (s_ == t_)
    m[:, 4, :] = np.broadcast_to(t_ != 0, (128, 512))
    return m


def emit_rwkv_chunked(bd, nc, es, D_, cst, cst_r, K):
    sb = lambda nm, shape, dt: es.enter_context(nc.sbuf_tensor(_u(nm), shape, dt))
    pT = D_["pT"].ap()
    yT = D_["yT"].ap()
    banks, banks_r = K["banks"], K["banks_r"]
    bones, identf, mats_r = K["bones"], K["identf"], K["mats_r"]
    R_R, R_K, R_V, R_L = 640, 768, 896, 1024
    rS = sb("r_rS", [128, SEQ], F32)
    kS = sb("r_kS", [128, SEQ], F32)
    vS = sb("r_vS", [128, SEQ], F32)
    lS = sb("r_lS", [128, SEQ], F32)
    rS_r, kS_r, vS_r, lS_r = Res(), Res(), Res(), Res()
    wl = sb("r_wl", [128, 3, 128], F32)
    wl_r = Res()
    bd.dma("sp", wl[:], D_["wl"].ap(), writes=[wl_r])
    msk = sb("r_msk", [128, 5, 512], F32)
    msk_r = Res()
    bd.dma("act", msk[:], D_["rwm"].ap(), writes=[msk_r])
    c2 = sb("r_c2", [128, 2], F32)
    c2_r = Res()
    bd.op("dve", lambda e: e.tensor_scalar(out=c2[:, 0:1], in0=cst[:, C_KA:C_KA + 1], scalar1=-1.0, scalar2=1.0,
                                           op0=ALU.mult, op1=ALU.add), reads=[cst_r], writes=[c2_r])
    with ExitStack() as es2:
        sb2 = lambda nm, shape, dt: es2.enter_context(nc.sbuf_tensor(_u(nm), shape, dt))
        dd = sb2("r_dd", [128, SEQ], F32)
        dd_r = Res()
        for (row0, t, t_r, mu) in ((R_R, rS, rS_r, C_MU_R), (R_K, kS, kS_r, C_MU_K), (R_V, vS, vS_r, C_MU_V),
                                   (R_L, lS, lS_r, C_MU_L)):
            bd.dma("sp", t[:], pT[row0:row0 + 128, :], writes=[t_r])
            bd.op("dve", lambda e, t=t: e.tensor_tensor(out=dd[:, 1:SEQ], in0=t[:, 0:SEQ - 1], in1=t[:, 1:SEQ],
                                                        op=ALU.subtract), reads=[t_r], writes=[dd_r])
            bd.op("dve", lambda e, t=t: e.tensor_scalar(out=dd[:, 0:1], in0=t[:, 0:1], scalar1=-1.0, scalar2=None,
                                                        op0=ALU.mult), reads=[t_r], writes=[dd_r])
            bd.op("dve", lambda e, t=t, mu=mu: e.scalar_tensor_tensor(out=t[:], in0=dd[:], scalar=cst[:, mu:mu + 1],
                                                                      in1=t[:], op0=ALU.mult, op1=ALU.add),
                  reads=[dd_r, t_r, cst_r], writes=[t_r])
        bd.barrier()
    names = ["th", "sg", "sgm", "aT", "kkr", "sq", "nrm", "nkk", "bb", "t1", "km", "prod", "gB", "boB",
             "lw", "cl", "e1", "e2", "e3", "At", "Bt", "Kt", "Rt", "Bh", "Kh",
             "Mab", "Lab", "Mkb", "Nbr", "Nkr", "T", "TT", "Mk0", "Mk1", "Lk0", "Lk1",
             "VT", "BhT", "KhT", "yB", "yc", "sq2", "rs", "Tb", "TTb"]
    BFN = {"At", "Bt", "Kt", "Rt", "Mab", "Lab", "Mkb", "Nbr", "Nkr", "Mk0", "Mk1", "Lk0", "Lk1",
           "VT", "BhT", "KhT", "Tb", "TTb"}
    T_ = {n: sb("r_" + n, [128, 512], BF16 if n in BFN else F32) for n in names}
    T_r = {n: Res() for n in names}
    UT = sb("r_UT", [128, 4, 128], BF16)
    UT_r = Res()
    xts = sb("r_xts", [128, 128], BF16)
    xts_r = Res()
    ST = sb("r_ST", [128, 64], F32)
    ST_r = Res()
    yo = [sb(f"r_yo{i}", [128, 512], BF16) for i in range(2)]
    yo_r = [Res(), Res()]
    STb = sb("r_STb", [128, 64], BF16)
    STb_r = Res()
    bd.op("dve", lambda e: e.memset(ST[:], 0.0), writes=[ST_r])
    bd.op("dve", lambda e: e.memset(STb[:], 0.0), writes=[STb_r])
    bX, bX_r = banks[3], banks_r[3]
    bU, bU_r = banks[4], banks_r[4]
    bS, bS_r = banks[5], banks_r[5]
    bY, bY_r = banks[6], banks_r[6]
    bi = [0]

    def nb():
        b = bi[0] % 3
        bi[0] += 1
        return banks[b], banks_r[b]

    def A(fn, reads, writes):
        bd.op("act", fn, reads=reads, writes=writes)

    def V(fn, reads, writes):
        bd.op("dve", fn, reads=reads, writes=writes)

    def G(fn, reads, writes):
        bd.op("pool", fn, reads=reads, writes=writes)

    def slot(c, h):
        return ((c // 2) * 2 + h) * 64

    def fam(out_name, lname, rname, mask_k):
        pb_, pb_r = nb()
        for h in range(2):
            H0 = h * 64
            for c in range(8):
                P0 = (c % 2) * 64
                o = slot(c, h)
                bd.op("pe", lambda e, P0=P0, H0=H0, o=o, c=c: e.matmul(
                    pb_[P0:P0 + 64, o:o + 64], lhsT=T_[lname][H0:H0 + 64, c * 64:(c + 1) * 64],
                    rhs=T_[rname][H0:H0 + 64, c * 64:(c + 1) * 64], start=True, stop=True),
                    reads=[T_r[lname], T_r[rname]], writes=[pb_r], rt=H0)
        V(lambda e: e.tensor_tensor(out=T_[out_name][:], in0=pb_[:, :], in1=msk[:, mask_k, :], op=ALU.mult),
          [pb_r, msk_r], [T_r[out_name]])

    def sq16(out_name, lname, rname, add_name=None):
        pb_, pb_r = nb()
        for cpar in range(2):
            P0 = cpar * 64
            grp = [(c, h) for c in range(cpar, 8, 2) for h in range(2)]
            for gi, (c, h) in enumerate(grp):
                o = slot(c, h)
                bd.op("pe", lambda e, P0=P0, o=o: e.matmul(
                    pb_[P0:P0 + 64, o:o + 64], lhsT=T_[lname][P0:P0 + 64, o:o + 64],
                    rhs=T_[rname][P0:P0 + 64, o:o + 64], start=True, stop=True),
                    reads=[T_r[lname], T_r[rname]], writes=[pb_r], rt=P0)
        if add_name is None:
            A(lambda e: e.activation(out=T_[out_name][:], in_=pb_[:, :], func=AF.Copy), [pb_r], [T_r[out_name]])
        else:
            V(lambda e: e.tensor_tensor(out=T_[out_name][:], in0=pb_[:, :], in1=T_[add_name][:], op=ALU.add),
              [pb_r, T_r[add_name]], [T_r[out_name]])

    def tr4(out_name, src_ap_fn, src_res):
        pb_, pb_r = nb()
        for c2_ in range(4):
            bd.op("pe", lambda e, c2_=c2_: e.transpose(out=pb_[:, c2_ * 128:(c2_ + 1) * 128], in_=src_ap_fn(c2_),
                                                       identity=identf), reads=[src_res, mats_r], writes=[pb_r])
        A(lambda e: e.activation(out=T_[out_name][:], in_=pb_[:, :], func=AF.Copy), [pb_r], [T_r[out_name]])

    for tb in range(SEQ // 512):
        sl = slice(tb * 512, (tb + 1) * 512)
        A(lambda e: e.activation(out=T_["th"][:], in_=lS[:, sl], func=AF.Tanh), [lS_r], [T_r["th"]])
        A(lambda e: e.activation(out=T_["sg"][:], in_=lS[:, sl], func=AF.Sigmoid), [lS_r], [T_r["sg"]])
        pw, pw_r = nb()
        bd.op("pe", lambda e: e.matmul(pw[:, :], lhsT=wl[:, 0, :], rhs=T_["th"][:], start=True, stop=True),
              reads=[wl_r, T_r["th"]], writes=[pw_r])
        A(lambda e: e.activation(out=T_["sgm"][:], in_=pw[:, :], func=AF.Sigmoid, bias=cst[:, C_W0:C_W0 + 1]),
          [pw_r, cst_r], [T_r["sgm"]])
        V(lambda e: e.tensor_scalar(out=T_["lw"][:], in0=T_["sgm"][:], scalar1=-float(np.exp(-0.5)), scalar2=None,
                                    op0=ALU.mult), [T_r["sgm"]], [T_r["lw"]])
        pa, pa_r = nb()
        bd.op("pe", lambda e: e.matmul(pa[:, :], lhsT=wl[:, 1, :], rhs=lS[:, sl], start=True, stop=True),
              reads=[wl_r, lS_r], writes=[pa_r])
        A(lambda e: e.activation(out=T_["aT"][:], in_=pa[:, :], func=AF.Sigmoid, bias=cst[:, C_A0:C_A0 + 1]),
          [pa_r, cst_r], [T_r["aT"]])
        pg, pg_r = nb()
        bd.op("pe", lambda e: e.matmul(pg[:, :], lhsT=wl[:, 2, :], rhs=T_["sg"][:], start=True, stop=True),
              reads=[wl_r, T_r["sg"]], writes=[pg_r])
        A(lambda e: e.activation(out=T_["gB"][:], in_=pg[:, :], func=AF.Copy), [pg_r], [T_r["gB"]])
        V(lambda e: e.tensor_scalar(out=T_["kkr"][:], in0=kS[:, sl], scalar1=cst[:, C_KK:C_KK + 1], scalar2=None,
                                    op0=ALU.mult), [kS_r, cst_r], [T_r["kkr"]])
        A(lambda e: e.activation(out=T_["sq"][:], in_=T_["kkr"][:], func=AF.Square), [T_r["kkr"]], [T_r["sq"]])
        pn, pn_r = nb()
        bd.op("pe", lambda e: e.matmul(pn[:, :], lhsT=bones, rhs=T_["sq"][:], start=True, stop=True),
              reads=[mats_r, T_r["sq"]], writes=[pn_r])
        A(lambda e: e.activation(out=T_["nrm"][:], in_=pn[:, :], func=AF.Sqrt), [pn_r], [T_r["nrm"]])
        V(lambda e: e.tensor_scalar(out=T_["nrm"][:], in0=T_["nrm"][:], scalar1=1e-12, scalar2=None, op0=ALU.max),
          [T_r["nrm"]], [T_r["nrm"]])
        V(lambda e: e.reciprocal(out=T_["nrm"][:], in_=T_["nrm"][:]), [T_r["nrm"]], [T_r["nrm"]])
        V(lambda e: e.scalar_tensor_tensor(out=T_["nkk"][:], in0=T_["kkr"][:], scalar=-1.0, in1=T_["nrm"][:],
                                           op0=ALU.mult, op1=ALU.mult), [T_r["kkr"], T_r["nrm"]], [T_r["nkk"]])
        V(lambda e: e.scalar_tensor_tensor(out=T_["bb"][:], in0=T_["nkk"][:], scalar=-1.0, in1=T_["aT"][:],
                                           op0=ALU.mult, op1=ALU.mult), [T_r["nkk"], T_r["aT"]], [T_r["bb"]])
        V(lambda e: e.tensor_scalar(out=T_["t1"][:], in0=T_["aT"][:], scalar1=cst[:, C_KA:C_KA + 1],
                                    scalar2=c2[:, 0:1], op0=ALU.mult, op1=ALU.add),
          [T_r["aT"], cst_r, c2_r], [T_r["t1"]])
        V(lambda e: e.tensor_tensor(out=T_["km"][:], in0=kS[:, sl], in1=T_["t1"][:], op=ALU.mult),
          [kS_r, T_r["t1"]], [T_r["km"]])
        V(lambda e: e.scalar_tensor_tensor(out=T_["prod"][:], in0=rS[:, sl], scalar=cst[:, C_RK:C_RK + 1],
                                           in1=T_["km"][:], op0=ALU.mult, op1=ALU.mult),
          [rS_r, cst_r, T_r["km"]], [T_r["prod"]])
        pb2, pb2_r = nb()
        bd.op("pe", lambda e: e.matmul(pb2[:, :], lhsT=bones, rhs=T_["prod"][:], start=True, stop=True),
              reads=[mats_r, T_r["prod"]], writes=[pb2_r])
        V(lambda e: e.tensor_tensor(out=T_["boB"][:], in0=pb2[:, :], in1=vS[:, sl], op=ALU.mult),
          [pb2_r, vS_r], [T_r["boB"]])
        V(lambda e: e.tensor_tensor_scan(out=T_["cl"][:], data0=msk[:, 4, :], data1=T_["lw"][:], initial=0.0,
                                         op0=ALU.mult, op1=ALU.add), [msk_r, T_r["lw"]], [T_r["cl"]])
        A(lambda e: e.activation(out=T_["e1"][:], in_=T_["cl"][:], func=AF.Exp), [T_r["cl"]], [T_r["e1"]])
        A(lambda e: e.activation(out=T_["e2"][:], in_=T_["cl"][:], func=AF.Exp, scale=-1.0), [T_r["cl"]], [T_r["e2"]])
        V(lambda e: e.tensor_tensor(out=T_["e3"][:], in0=T_["cl"][:], in1=T_["lw"][:], op=ALU.subtract),
          [T_r["cl"], T_r["lw"]], [T_r["e3"]])
        A(lambda e: e.activation(out=T_["e3"][:], in_=T_["e3"][:], func=AF.Exp), [T_r["e3"]], [T_r["e3"]])
        V(lambda e: e.tensor_tensor(out=T_["At"][:], in0=T_["nkk"][:], in1=T_["e3"][:], op=ALU.mult),
          [T_r["nkk"], T_r["e3"]], [T_r["At"]])
        G(lambda e: e.tensor_tensor(out=T_["Bt"][:], in0=T_["bb"][:], in1=T_["e2"][:], op=ALU.mult),
          [T_r["bb"], T_r["e2"]], [T_r["Bt"]])
        V(lambda e: e.tensor_tensor(out=T_["Kt"][:], in0=T_["km"][:], in1=T_["e2"][:], op=ALU.mult),
          [T_r["km"], T_r["e2"]], [T_r["Kt"]])
        G(lambda e: e.tensor_tensor(out=T_["Rt"][:], in0=rS[:, sl], in1=T_["e1"][:], op=ALU.mult),
          [rS_r, T_r["e1"]], [T_r["Rt"]])
        for c in range(8):
            gam = T_["e1"][:, c * 64 + 63:c * 64 + 64]
            V(lambda e, c=c, gam=gam: e.tensor_scalar(out=T_["Bh"][:, c * 64:(c + 1) * 64], in0=T_["Bt"][:, c * 64:(c + 1) * 64],
                                                      scalar1=gam, scalar2=None, op0=ALU.mult),
              [T_r["Bt"], T_r["e1"]], [T_r["Bh"]])
            G(lambda e, c=c, gam=gam: e.tensor_scalar(out=T_["Kh"][:, c * 64:(c + 1) * 64], in0=T_["Kt"][:, c * 64:(c + 1) * 64],
                                                      scalar1=gam, scalar2=None, op0=ALU.mult),
              [T_r["Kt"], T_r["e1"]], [T_r["Kh"]])
        if _DBG_STAGE == 1:
            break
        fam("Mab", "Bt", "At", 0)
        fam("Lab", "At", "Bt", 2)
        fam("Mkb", "Kt", "At", 0)
        fam("Nbr", "Bt", "Rt", 1)
        fam("Nkr", "Kt", "Rt", 1)
        if _DBG_STAGE == 2:
            break
        V(lambda e: e.tensor_tensor(out=T_["T"][:], in0=T_["Mab"][:], in1=msk[:, 3, :], op=ALU.add),
          [T_r["Mab"], msk_r], [T_r["T"]])
        G(lambda e: e.tensor_tensor(out=T_["TT"][:], in0=T_["Lab"][:], in1=msk[:, 3, :], op=ALU.add),
          [T_r["Lab"], msk_r], [T_r["TT"]])
        A(lambda e: e.activation(out=T_["Tb"][:], in_=T_["T"][:], func=AF.Copy), [T_r["T"]], [T_r["Tb"]])
        G(lambda e: e.tensor_copy(out=T_["TTb"][:], in_=T_["TT"][:]), [T_r["TT"]], [T_r["TTb"]])
        Mp, Lp = "Mab", "Lab"
        for lev in range(1, 6):
            Mn, Ln = f"Mk{lev % 2}", f"Lk{lev % 2}"
            sq16(Mn, Lp, Mp)
            if lev < 5:
                sq16(Ln, Mp, Lp)
            sq16("T", "TTb", Mn, add_name="T")
            if lev < 5:
                sq16("TT", Mn, "TTb", add_name="TT")
                G(lambda e: e.tensor_copy(out=T_["TTb"][:], in_=T_["TT"][:]), [T_r["TT"]], [T_r["TTb"]])
            A(lambda e: e.activation(out=T_["Tb"][:], in_=T_["T"][:], func=AF.Copy), [T_r["T"]], [T_r["Tb"]])
            Mp, Lp = Mn, Ln
        if _DBG_STAGE == 3:
            break
        tr4("VT", lambda c2_: vS[:, tb * 512 + c2_ * 128: tb * 512 + (c2_ + 1) * 128], vS_r)
        tr4("BhT", lambda c2_: T_["Bh"][:, c2_ * 128:(c2_ + 1) * 128], T_r["Bh"])
        tr4("KhT", lambda c2_: T_["Kh"][:, c2_ * 128:(c2_ + 1) * 128], T_r["Kh"])
        if _DBG_STAGE == 4:
            break
        for c in range(8):
            P0 = (c % 2) * 64
            c2_ = c // 2
            cs = slice(c * 64, (c + 1) * 64)
            for h in range(2):
                H0 = h * 64
                o = slot(c, h)
                vt = T_["VT"][P0:P0 + 64, c2_ * 128 + H0:c2_ * 128 + H0 + 64]
                bd.op("pe", lambda e, P0=P0, H0=H0, cs=cs: e.matmul(
                    bX[P0:P0 + 64, H0:H0 + 64], lhsT=T_["At"][H0:H0 + 64, cs], rhs=STb[H0:H0 + 64, :],
                    start=True, stop=False), reads=[T_r["At"], STb_r], writes=[bX_r], rt=H0)
                bd.op("pe", lambda e, P0=P0, H0=H0, o=o, vt=vt: e.matmul(
                    bX[P0:P0 + 64, H0:H0 + 64], lhsT=T_["Mkb"][P0:P0 + 64, o:o + 64], rhs=vt,
                    start=False, stop=True), reads=[T_r["Mkb"], T_r["VT"]], writes=[bX_r], rt=P0)
            A(lambda e, P0=P0: e.activation(out=xts[P0:P0 + 64, :], in_=bX[P0:P0 + 64, 0:128], func=AF.Copy),
              [bX_r], [xts_r])
            for h in range(2):
                H0 = h * 64
                o = slot(c, h)
                bd.op("pe", lambda e, P0=P0, H0=H0, o=o: e.matmul(
                    bU[P0:P0 + 64, H0:H0 + 64], lhsT=T_["Tb"][P0:P0 + 64, o:o + 64], rhs=xts[P0:P0 + 64, H0:H0 + 64],
                    start=True, stop=True), reads=[T_r["Tb"], xts_r], writes=[bU_r], rt=P0)
            V(lambda e, P0=P0, c2_=c2_: e.tensor_copy(out=UT[P0:P0 + 64, c2_, :], in_=bU[P0:P0 + 64, 0:128]),
              [bU_r], [UT_r])
            for h in range(2):
                H0 = h * 64
                o = slot(c, h)
                ut = UT[P0:P0 + 64, c2_, H0:H0 + 64]
                vt = T_["VT"][P0:P0 + 64, c2_ * 128 + H0:c2_ * 128 + H0 + 64]
                bd.op("pe", lambda e, H0=H0, cs=cs: e.matmul(
                    bY[H0:H0 + 64, cs], lhsT=STb[H0:H0 + 64, :], rhs=T_["Rt"][H0:H0 + 64, cs],
                    start=True, stop=False), reads=[STb_r, T_r["Rt"]], writes=[bY_r], rt=H0)
                bd.op("pe", lambda e, H0=H0, cs=cs, P0=P0, o=o, ut=ut: e.matmul(
                    bY[H0:H0 + 64, cs], lhsT=ut, rhs=T_["Nbr"][P0:P0 + 64, o:o + 64],
                    start=False, stop=False), reads=[UT_r, T_r["Nbr"]], writes=[bY_r], rt=P0)
                bd.op("pe", lambda e, H0=H0, cs=cs, P0=P0, o=o, vt=vt: e.matmul(
                    bY[H0:H0 + 64, cs], lhsT=vt, rhs=T_["Nkr"][P0:P0 + 64, o:o + 64],
                    start=False, stop=True), reads=[T_r["VT"], T_r["Nkr"]], writes=[bY_r], rt=P0)
                bd.op("pe", lambda e, H0=H0, P0=P0, c2_=c2_, ut=ut: e.matmul(
                    bS[H0:H0 + 64, 0:64], lhsT=T_["BhT"][P0:P0 + 64, c2_ * 128 + H0:c2_ * 128 + H0 + 64], rhs=ut,
                    start=True, stop=False), reads=[T_r["BhT"], UT_r], writes=[bS_r], rt=P0)
                bd.op("pe", lambda e, H0=H0, P0=P0, c2_=c2_, vt=vt: e.matmul(
                    bS[H0:H0 + 64, 0:64], lhsT=T_["KhT"][P0:P0 + 64, c2_ * 128 + H0:c2_ * 128 + H0 + 64], rhs=vt,
                    start=False, stop=True), reads=[T_r["KhT"], T_r["VT"]], writes=[bS_r], rt=P0)
            gam = T_["e1"][:, c * 64 + 63:c * 64 + 64]
            V(lambda e, gam=gam: e.scalar_tensor_tensor(out=ST[:], in0=ST[:], scalar=gam, in1=bS[:, 0:64],
                                                        op0=ALU.mult, op1=ALU.add),
              [ST_r, bS_r, T_r["e1"]], [ST_r])
            A(lambda e: e.activation(out=STb[:], in_=ST[:], func=AF.Copy), [ST_r], [STb_r])
        if _DBG_STAGE == 5:
            break
        A(lambda e: e.activation(out=T_["yB"][:], in_=bY[:, :], func=AF.Copy), [bY_r], [T_r["yB"]])
        pm, pm_r = nb()
        bd.op("pe", lambda e: e.matmul(pm[:, :], lhsT=bones, rhs=T_["yB"][:], start=True, stop=True),
              reads=[mats_r, T_r["yB"]], writes=[pm_r])
        V(lambda e: e.scalar_tensor_tensor(out=T_["yc"][:], in0=pm[:, :], scalar=-1.0 / 64, in1=T_["yB"][:],
                                           op0=ALU.mult, op1=ALU.add), [pm_r, T_r["yB"]], [T_r["yc"]])
        A(lambda e: e.activation(out=T_["sq2"][:], in_=T_["yc"][:], func=AF.Square), [T_r["yc"]], [T_r["sq2"]])
        pv, pv_r = nb()
        bd.op("pe", lambda e: e.matmul(pv[:, :], lhsT=bones, rhs=T_["sq2"][:], start=True, stop=True),
              reads=[mats_r, T_r["sq2"]], writes=[pv_r])
        A(lambda e: e.activation(out=T_["rs"][:], in_=pv[:, :], func=AF.Sqrt, scale=1.0 / 64,
                                 bias=cst[:, C_LNEPS:C_LNEPS + 1]), [pv_r, cst_r], [T_r["rs"]])
        V(lambda e: e.reciprocal(out=T_["rs"][:], in_=T_["rs"][:]), [T_r["rs"]], [T_r["rs"]])
        V(lambda e: e.tensor_tensor(out=T_["yc"][:], in0=T_["yc"][:], in1=T_["rs"][:], op=ALU.mult),
          [T_r["yc"], T_r["rs"]], [T_r["yc"]])
        V(lambda e: e.tensor_scalar(out=T_["yc"][:], in0=T_["yc"][:], scalar1=cst[:, C_LNW:C_LNW + 1],
                                    scalar2=cst[:, C_LNB:C_LNB + 1], op0=ALU.mult, op1=ALU.add),
          [T_r["yc"], cst_r], [T_r["yc"]])
        V(lambda e: e.tensor_tensor(out=T_["yc"][:], in0=T_["yc"][:], in1=T_["boB"][:], op=ALU.add),
          [T_r["yc"], T_r["boB"]], [T_r["yc"]])
        o_, o_r = yo[tb % 2], yo_r[tb % 2]
        V(lambda e, o_=o_: e.tensor_tensor(out=o_[:], in0=T_["yc"][:], in1=T_["gB"][:], op=ALU.mult),
          [T_r["yc"], T_r["gB"]], [o_r])
        bd.dma("sp", yT[256:384, sl], o_[:], reads=[o_r])
    bd.barrier()


def emit_phaseB(nc, mkbld, es, D_, mixers, tag):
    bd = mkbld(tag + "m")
    sb = lambda nm, shape, dt: es.enter_context(nc.sbuf_tensor(_u(nm), shape, dt))
    ps = lambda nm, shape, dt: es.enter_context(nc.psum_tensor(_u(nm), shape, dt))
    cst = sb("B_cst", [128, NCST], F32)
    cst_r = Res()
    bd.dma("sp", cst[:], D_["cst"].ap(), writes=[cst_r])
    mats = sb("B_mats", [128, 4, 128], F32)
    mats_r = Res()
    bd.dma("sp", mats[:], D_["mats"].ap(), writes=[mats_r])
    identb = sb("B_identb", [128, 128], BF16)
    maskb = sb("B_maskb", [128, 128], BF16)
    identb_r, maskb_r = Res(), Res()
    bd.op("dve", lambda e: e.tensor_copy(out=identb[:], in_=mats[:, 0, :]), reads=[mats_r], writes=[identb_r])
    bd.op("dve", lambda e: e.tensor_copy(out=maskb[:], in_=mats[:, 3, :]), reads=[mats_r], writes=[maskb_r])
    banks = [ps(f"B_bank{i}", [128, 512], F32) for i in range(7)]
    banks_r = [Res() for _ in range(7)]
    ptb = ps("B_ptb", [128, 1024], BF16)
    ptb_r = Res()
    K = dict(banks=banks, banks_r=banks_r, ptb=ptb, ptb_r=ptb_r, identb=identb, identb_r=identb_r,
             mask=maskb, mask_r=maskb_r, ones=mats[:, 1, :], ones_r=mats_r, identf=mats[:, 0, :],
             bones=mats[:, 2, :], mats_r=mats_r)
    if "mla" in mixers:
        with ExitStack() as es1:
            emit_mla(bd, nc, es1, D_["pT"], D_["pos"], D_["wuq"], D_["wukv"], D_["yT"], cst, cst_r, K)
    bd.barrier()
    if "mlstm" in mixers:
        with ExitStack() as es1:
            emit_mlstm(bd, nc, es1, D_["pT"], D_["yT"], cst, cst_r, K)
        bd.barrier()
    if "rwkv" in mixers:
        bd = mkbld(tag + "r")
        with ExitStack() as es1:
            (emit_rwkv_chunked if RWKV_CHUNKED else emit_rwkv)(bd, nc, es1, D_, cst, cst_r, K)
        bd.barrier()


def prep_phaseB_consts(P, l, hc):
    c = np.zeros((128, NCST), np.float32)
    c[:, C_QN0] = P["mla_q_norm"][l][0:128]
    c[:, C_QN1] = P["mla_q_norm"][l][128:256]
    c[:, C_KVN0] = P["mla_kv_norm"][l][0:128]
    c[:, C_KVN1] = P["mla_kv_norm"][l][128:256]
    c[:, C_INVF] = np.tile(INV_FREQ, 4)
    c[:, C_SGN] = np.tile(np.concatenate([np.full(32, -1.0, np.float32), np.full(32, 1.0, np.float32)]), 2)
    for h in range(2):
        hh = 2 * hc + h
        c[:, C_MLAO0 + h] = P["mla_out_norm"][l][hh * 128:(hh + 1) * 128]
    mu = P["rwkv_mu"][l]
    ch = slice(hc * 128, hc * 128 + 128)
    c[:, C_MU_R] = mu[0:256][ch]
    c[:, C_MU_K] = mu[256:512][ch]
    c[:, C_MU_V] = mu[512:768][ch]
    c[:, C_MU_L] = mu[768:896]
    c[:, C_W0] = P["rwkv_w0"][l][ch]
    c[:, C_A0] = P["rwkv_a0"][l][ch]
    c[:, C_KK] = P["rwkv_k_k"][l][ch]
    c[:, C_KA] = P["rwkv_k_a"][l][ch]
    c[:, C_RK] = P["rwkv_r_k"][l][ch]
    c[:, C_LNW] = P["rwkv_ln_w"][l][ch]
    c[:, C_LNB] = P["rwkv_ln_b"][l][ch]
    cw = P["mlstm_conv_w"][l]
    cb = P["mlstm_conv_b"][l]
    qs = slice(hc * 64, hc * 64 + 64)
    ks = slice(128 + hc * 64, 128 + hc * 64 + 64)
    for j in range(4):
        c[0:64, C_CWQ0 + j] = cw[j][qs]
        c[0:64, C_CWK0 + j] = cw[j][ks]
    c[0:64, C_CBQ] = cb[qs]
    c[0:64, C_CBK] = cb[ks]
    c[:, C_IB] = P["mlstm_i_bias"][l][hc * 2]
    c[:, C_FB] = P["mlstm_f_bias"][l][hc * 2]
    c[:, C_IB1] = P["mlstm_i_bias"][l][hc * 2 + 1]
    c[:, C_FB1] = P["mlstm_f_bias"][l][hc * 2 + 1]
    c[:, C_MLO] = P["mlstm_out_norm"][l][ch]
    c[:, C_EPS] = EPS
    c[:, C_LNEPS] = 64e-5
    c[:, C_ONE] = 1.0
    hq = []
    hkv = []
    for h in range(2):
        hh = 2 * hc + h
        wq = P["mla_w_uq"][l][:, hh * 192:(hh + 1) * 192]
        hq += [wq[:, 0:128], wq[:, 128:192], wq[:, 160:192], wq[:, 128:160]]
        wkv = P["mla_w_ukv"][l][:, hh * 256:(hh + 1) * 256]
        hkv += [wkv]
    wuq = np.ascontiguousarray(np.concatenate(hq, axis=1))
    wukv = np.ascontiguousarray(np.concatenate(hkv, axis=1))
    wl = np.zeros((128, 3, 128), np.float32)
    wl[0:32, 0, :] = P["rwkv_w2"][l][:, ch]
    wl[32:64, 1, :] = P["rwkv_a2"][l][:, ch]
    wl[64:128, 2, :] = P["rwkv_g2"][l][:, ch]
    return dict(cst=c, wuq=wuq, wukv=wukv, wl=wl)


INV_FREQ = (10000.0 ** (-np.arange(0, 64, 2, dtype=np.float32) / np.float32(64))).astype(np.float32)


def const_mats():
    m = np.zeros((128, 4, 128), np.float32)
    m[:, 0, :] = np.eye(128, dtype=np.float32)
    m[:, 1, :] = 1.0
    m[0:64, 2, 0:64] = 1.0
    m[64:128, 2, 64:128] = 1.0
    m[:, 3, :] = (np.arange(128)[:, None] <= np.arange(128)[None, :]).astype(np.float32)
    return m


W_OUT_PERM = (list(range(0, 256)) + list(range(512, 640)) + list(range(768, 896)) +
              list(range(256, 512)) + list(range(640, 768)) + list(range(896, 1024)))
TBC = 512


def emit_phaseC(nc, bd, es, D_, final, ntok):
    sb = lambda name, shape, dt: es.enter_context(nc.sbuf_tensor(_u(name), shape, dt))
    ps = lambda name, shape, dt: es.enter_context(nc.psum_tensor(_u(name), shape, dt))
    wo = sb("C_wo", [128, 8, D], BF16)
    wg = sb("C_wg", [128, 8, DFF], BF16)
    wu = sb("C_wu", [128, 8, DFF], BF16)
    wd = sb("C_wd", [128, 22, D], BF16)
    w_r = Res()
    HS = DFF // 2
    gcol = sb("C_gcol", [128, 8], F32)
    gcol_r = Res()
    idf = sb("C_idf", [128, 128], F32)
    ident = sb("C_ident", [128, 128], BF16)
    idf_r, ident_r = Res(), Res()
    bd.dma("sp", gcol[:], D_["g"].ap(), writes=[gcol_r])
    bd.dma("sp", idf[:], D_["ident"].ap(), writes=[idf_r])
    bd.op("dve", lambda e: e.tensor_copy(out=ident[:], in_=idf[:]), reads=[idf_r], writes=[ident_r])
    if final:
        gbc = sb("C_gbc", [128, D], F32)
        gbc_r = Res()
        bd.dma("sp", gbc[:], bass.AP(D_["gf_t"], 0, [[0, 128], [1, D]]), writes=[gbc_r])
    es_stage = ExitStack()
    stage = [es_stage.enter_context(nc.sbuf_tensor(_u(f"C_stage{i}"), [128, HS], F32)) for i in range(3)]
    stage_r = [Res() for _ in range(3)]
    si = [0]
    queues = ("sp", "pool", "act")
    engs = ("dve", "pool", "act")

    def cast(dst_ap, src_ap, ncols, scale_ap=None):
        i = si[0] % 3
        si[0] += 1
        bd.dma(queues[i], stage[i][:, 0:ncols], src_ap, writes=[stage_r[i]])
        eng = engs[i] if scale_ap is None else ("dve" if i != 2 else "act")
        if eng == "act":
            if scale_ap is None:
                bd.op("act", lambda e: e.activation(out=dst_ap, in_=stage[i][:, 0:ncols], func=AF.Copy),
                      reads=[stage_r[i]], writes=[w_r])
            else:
                bd.op("act", lambda e: e.activation(out=dst_ap, in_=stage[i][:, 0:ncols], func=AF.Copy, scale=scale_ap),
                      reads=[stage_r[i], gcol_r], writes=[w_r])
        elif scale_ap is None:
            bd.op(eng, lambda e: e.tensor_copy(out=dst_ap, in_=stage[i][:, 0:ncols]), reads=[stage_r[i]], writes=[w_r])
        else:
            bd.op(eng, lambda e: e.tensor_scalar(out=dst_ap, in0=stage[i][:, 0:ncols], scalar1=scale_ap, scalar2=None,
                                                 op0=ALU.mult), reads=[stage_r[i], gcol_r], writes=[w_r])

    for kc in range(8):
        cast(wo[:, kc, :], D_["wo"].ap()[kc * 128:(kc + 1) * 128, :], D)
    for kc in range(8):
        for hh in range(2):
            cast(wg[:, kc, hh * HS:(hh + 1) * HS], D_["wg"].ap()[kc * 128:(kc + 1) * 128, hh * HS:(hh + 1) * HS], HS,
                 gcol[:, kc:kc + 1])
            cast(wu[:, kc, hh * HS:(hh + 1) * HS], D_["wu"].ap()[kc * 128:(kc + 1) * 128, hh * HS:(hh + 1) * HS], HS,
                 gcol[:, kc:kc + 1])
    for f in range(22):
        cast(wd[:, f, :], D_["wd"].ap()[f * 128:(f + 1) * 128, :], D)
    bd.barrier()
    es_stage.close()

    NJ = TBC // 128
    xs = sb("C_xs", [128, NJ, D], F32)
    xs_r = [Res() for _ in range(NJ)]
    yTb = sb("C_yTb", [128, 8, 128], BF16)
    yTb_r = Res()
    hT = sb("C_hT", [128, 8, TBC], BF16)
    hT_r = Res()
    AT = sb("C_AT", [128, 22, TBC], BF16)
    AT_r = Res()
    sgt = [sb("C_sg0", [128, TBC], BF16)] * 2
    sgt_r = [Res()] * 2
    c_xn = sb("C_xn", [128, D], BF16)
    c_xn_r = Res()
    tmp = dict(ss=sb("C_ss", [128, 4], F32), ss_r=Res(), junk=c_xn, junk_r=c_xn_r,
               xn=c_xn, xn_r=c_xn_r,
               pt=ps("C_pt", [128, D], BF16), pt_r=Res(), eps=sb("C_eps", [128, 1], F32), eps_r=Res())
    bd.op("dve", lambda e: e.memset(tmp["eps"][:], EPS), writes=[tmp["eps_r"]])
    pb = [ps(f"C_pb{i}", [128, 512], F32) for i in range(6)]
    pb_r = [Res() for _ in range(6)]
    x_ap = D_["x"].ap()
    yT_ap = D_["yT"].ap()
    out_ap = D_["out"].ap()
    for blk in range(ntok // TBC):
        t0 = blk * TBC
        for j in range(NJ):
            bd.dma("sp", yTb[:], yT_ap[:, t0 + j * 128:t0 + (j + 1) * 128].rearrange("(k p) t -> p k t", p=128),
                   writes=[yTb_r])
            bd.dma("pool", xs[:, j, :], x_ap[t0 + j * 128:t0 + (j + 1) * 128, :], writes=[xs_r[j]])
            for ch in range(2):
                po, po_r = pb[ch], pb_r[ch]
                for k in range(8):
                    bd.op("pe", lambda e, k=k, j=j, ch=ch, po=po: e.matmul(
                        po[:, :], lhsT=yTb[:, k, :], rhs=wo[:, k, ch * 512:(ch + 1) * 512],
                        start=(k == 0), stop=(k == 7)), reads=[yTb_r, w_r], writes=[po_r])
                bd.op("dve", lambda e, j=j, ch=ch, po=po: e.tensor_tensor(
                    out=xs[:, j, ch * 512:(ch + 1) * 512], in0=xs[:, j, ch * 512:(ch + 1) * 512], in1=po[:, :], op=ALU.add),
                    reads=[po_r, xs_r[j]], writes=[xs_r[j]])
            emit_rmsnorm_T(bd, xs[:, j, :], xs_r[j], hT, hT_r, j, tmp, ident)
        for f in range(22):
            pg, pg_r = pb[2 + (f % 2) * 2], pb_r[2 + (f % 2) * 2]
            pu, pu_r = pb[3 + (f % 2) * 2], pb_r[3 + (f % 2) * 2]
            for (pp, pp_r, ww) in ((pg, pg_r, wg), (pu, pu_r, wu)):
                for k in range(8):
                    bd.op("pe", lambda e, k=k, f=f, pp=pp, ww=ww: e.matmul(
                        pp[:, 0:TBC], lhsT=ww[:, k, f * 128:(f + 1) * 128], rhs=hT[:, k, :],
                        start=(k == 0), stop=(k == 7)), reads=[w_r, hT_r], writes=[pp_r])
            sg, sg_r = sgt[f % 2], sgt_r[f % 2]
            bd.op("act", lambda e, sg=sg, pg=pg: e.activation(out=sg[:], in_=pg[:, 0:TBC], func=AF.Silu),
                  reads=[pg_r], writes=[sg_r])
            bd.op("dve", lambda e, sg=sg, pu=pu, f=f: e.tensor_tensor(out=AT[:, f, :], in0=sg[:], in1=pu[:, 0:TBC],
                                                                      op=ALU.mult),
                  reads=[sg_r, pu_r], writes=[AT_r])
        for j in range(NJ):
            for ch in range(2):
                pd, pd_r = pb[ch], pb_r[ch]
                for f in range(22):
                    bd.op("pe", lambda e, f=f, j=j, ch=ch, pd=pd: e.matmul(
                        pd[:, :], lhsT=AT[:, f, j * 128:(j + 1) * 128], rhs=wd[:, f, ch * 512:(ch + 1) * 512],
                        start=(f == 0), stop=(f == 21)), reads=[AT_r, w_r], writes=[pd_r])
                bd.op("dve", lambda e, j=j, ch=ch, pd=pd: e.tensor_tensor(
                    out=xs[:, j, ch * 512:(ch + 1) * 512], in0=xs[:, j, ch * 512:(ch + 1) * 512], in1=pd[:, :], op=ALU.add),
                    reads=[pd_r, xs_r[j]], writes=[xs_r[j]])
            if final:
                ss, ss_r = tmp["ss"], tmp["ss_r"]
                bd.op("dve", lambda e, j=j: e.scalar_tensor_tensor(
                    out=tmp["junk"][:], in0=xs[:, j, :], scalar=1.0, in1=xs[:, j, :], op0=ALU.mult, op1=ALU.mult,
                    accum_out=ss[:, 0:1]), reads=[xs_r[j]], writes=[tmp["junk_r"], ss_r])
                bd.op("act", lambda e: e.activation(out=ss[:, 1:2], in_=ss[:, 0:1], func=AF.Sqrt, scale=1.0 / D,
                                                    bias=tmp["eps"][:, 0:1]), reads=[ss_r, tmp["eps_r"]], writes=[ss_r])
                bd.op("dve", lambda e: e.reciprocal(out=ss[:, 2:3], in_=ss[:, 1:2]), reads=[ss_r], writes=[ss_r])
                bd.op("dve", lambda e, j=j: e.scalar_tensor_tensor(
                    out=xs[:, j, :], in0=xs[:, j, :], scalar=ss[:, 2:3], in1=gbc[:], op0=ALU.mult, op1=ALU.mult),
                    reads=[xs_r[j], ss_r, gbc_r], writes=[xs_r[j]])
            bd.dma("sp", out_ap[t0 + j * 128:t0 + (j + 1) * 128, :], xs[:, j, :], reads=[xs_r[j]])
    bd.barrier()


def build_fused(phases=None):
    _ROPE_CACHE.clear()
    nc = bass.Bass("TRN2", target_bir_lowering=False)
    dt = nc.dram_tensor
    x_d = dt("x", [SEQ, D], F32, kind="ExternalInput")
    pos_d = dt("pos", [1, SEQ], I32, kind="ExternalInput")
    wA_d = dt("wA", [2, D, NCOLA], F32, kind="ExternalInput")
    gA_d = dt("gA", [2, 128, 8], F32, kind="ExternalInput")
    id_d = dt("ident", [128, 128], F32, kind="ExternalInput")
    mats_d = dt("mats", [128, 4, 128], F32, kind="ExternalInput")
    rwm_d = dt("rwm", [128, 5, 512], F32, kind="ExternalInput")
    wuq_d = dt("wuq", [2, 2, 256, 512], F32, kind="ExternalInput")
    wukv_d = dt("wukv", [2, 2, 256, 512], F32, kind="ExternalInput")
    cst_d = dt("cst", [2, 2, 128, NCST], F32, kind="ExternalInput")
    wl_d = dt("wl", [2, 2, 128, 3, 128], F32, kind="ExternalInput")
    wo_d = dt("wo", [2, D, D], F32, kind="ExternalInput")
    wg_d = dt("wg", [2, D, DFF], F32, kind="ExternalInput")
    wu_d = dt("wu", [2, D, DFF], F32, kind="ExternalInput")
    wd_d = dt("wd", [2, DFF, D], F32, kind="ExternalInput")
    gC_d = dt("gC", [2, 128, 8], F32, kind="ExternalInput")
    gf_d = dt("gf", [1, D], F32, kind="ExternalInput")
    out_d = dt("out", [SEQ, D], F32, kind="ExternalOutput")
    pT_d = dt("pT_scr", [NCOLA, SEQ], F32)
    yT_d = dt("yT_scr", [D, SEQ], BF16)
    scr_d = dt("rw_scr", [2 * SEQ * 320], F32)
    x1_d = dt("x1_scr", [SEQ, D], F32)
    shared = {}
    mkbld = lambda tag: Bld(nc, tag, shared)
    for l in range(2):
        x_in = x_d if l == 0 else x1_d
        x_out = x1_d if l == 0 else out_d
        if phases is None or f"A{l}" in phases:
          with ExitStack() as es:
            bd = mkbld(f"A{l}")
            emit_phaseA(nc, bd, es, x_in, View(wA_d.ap()[l]), View(gA_d.ap()[l]), id_d, pT_d, SEQ)
        for hc in range(2):
            if phases is not None and f"B{l}{hc}" not in phases:
                continue
            D_ = dict(pT=View(pT_d.ap()[hc * HALF_ROWS:(hc + 1) * HALF_ROWS, :]), pos=pos_d,
                      wuq=View(wuq_d.ap()[l, hc]), wukv=View(wukv_d.ap()[l, hc]), cst=View(cst_d.ap()[l, hc]),
                      wl=View(wl_d.ap()[l, hc]), mats=mats_d, yT=View(yT_d.ap()[hc * 512:(hc + 1) * 512, :]),
                      scr=scr_d, rwm=rwm_d)
            with ExitStack() as es:
                emit_phaseB(nc, mkbld, es, D_, ("mla", "mlstm", "rwkv"), f"B{l}{hc}")
        D_ = dict(x=x_in, yT=yT_d, wo=View(wo_d.ap()[l]), wg=View(wg_d.ap()[l]), wu=View(wu_d.ap()[l]),
                  wd=View(wd_d.ap()[l]), g=View(gC_d.ap()[l]), gf_t=gf_d, ident=id_d, out=x_out)
        if phases is None or f"C{l}" in phases:
          with ExitStack() as es:
            bd = mkbld(f"C{l}")
            emit_phaseC(nc, bd, es, D_, l == 1, SEQ)
    return nc


_CACHE = {}
_PHASES = None


def kernel(**inputs):
    P = {k: np.asarray(v) for k, v in inputs.items()}
    x = np.asarray(P["x"], np.float32)
    positions = np.asarray(P["positions"]).astype(np.int32)
    if "nc" not in _CACHE:
        _CACHE["nc"] = build_fused(_PHASES)
    nc = _CACHE["nc"]
    f32 = lambda a: np.ascontiguousarray(np.asarray(a, np.float32))
    wA = f32(np.stack([prep_w_inA(P["w_in"][l]) for l in range(2)]))
    gA = f32(np.stack([_gcol(P["mix_norm"][l]) for l in range(2)]))
    gC = f32(np.stack([_gcol(P["ffn_norm"][l]) for l in range(2)]))
    pb = [[prep_phaseB_consts(P, l, hc) for hc in range(2)] for l in range(2)]
    stk = lambda key: f32(np.stack([np.stack([pb[l][hc][key] for hc in range(2)]) for l in range(2)]))
    common = dict(
        wA=wA, gA=gA, ident=np.eye(128, dtype=np.float32), mats=const_mats(), rwm=rwkv_masks(),
        wuq=stk("wuq"), wukv=stk("wukv"), cst=stk("cst"), wl=stk("wl"),
        wo=f32(np.stack([P["w_out"][l][W_OUT_PERM, :] for l in range(2)])),
        wg=f32(P["w_gate"]), wu=f32(P["w_up"]), wd=f32(P["w_down"]), gC=gC,
        gf=f32(np.asarray(P["final_norm"]).reshape(1, D)))
    in_maps = []
    for c in range(NCORES):
        b = c // 2
        m = dict(common)
        m["x"] = f32(x[b])
        m["pos"] = np.ascontiguousarray(positions[b:b + 1])
        in_maps.append(m)
    res = run_bass_kernel_spmd(nc, in_maps, core_ids=list(range(NCORES)))
    out = np.zeros((4, SEQ, D), np.float32)
    for b in range(4):
        out[b] = res.results[2 * b]["out"]
    return out
```
